# Optimizing a Trainium2 kernel written in Bass

```python
import math
import jax
import jax.numpy as jnp
from jax import lax
import numpy as np

D_MODEL = 2048
BATCH = 4
SEQ = 4096
DEPTH = 4

N_MIXERS = 3
ROPE_THETA = 500000.0
ROPE_FRACTION = 4
Q_BLOCK = 128
NORM_EPS = 1e-6
ADA_CHUNKS = 6
FFN_HIDDEN = -(-8 * D_MODEL // (3 * 256)) * 256

DA_HEADS = D_MODEL // 128
DA_QK_DIM = 64
DA_V_DIM = 2 * DA_QK_DIM
DA_Q_COLS = DA_HEADS * 2 * DA_QK_DIM
DA_IN = 2 * DA_Q_COLS + DA_HEADS * DA_V_DIM

SSD_INNER = 2 * D_MODEL
SSD_HEAD_DIM = 64
SSD_HEADS = SSD_INNER // SSD_HEAD_DIM
SSD_GROUPS = 8
SSD_HPG = SSD_HEADS // SSD_GROUPS
SSD_STATE = 128
SSD_CONV = 4
SSD_CHUNK = 128
SSD_CONV_DIM = SSD_INNER + 2 * SSD_GROUPS * SSD_STATE
SSD_IN = SSD_INNER + SSD_CONV_DIM + SSD_HEADS

SA_HEADS = D_MODEL // 128
SA_KV_HEADS = 4
SA_HEAD_DIM = 128
SA_REP = SA_HEADS // SA_KV_HEADS
IDX_HEADS = 16
IDX_DIM = 64
TOPK_MAX = 256
SA_IN = SA_HEADS * SA_HEAD_DIM + 2 * SA_KV_HEADS * SA_HEAD_DIM + IDX_HEADS * IDX_DIM + IDX_DIM + IDX_HEADS

kernel_name = 'hybrid_diffattn_ssd_dsa_block'


def rms_norm(x, g):
    xf = x.astype(jnp.float32)
    y = xf * lax.rsqrt(jnp.mean(xf * xf, axis=-1, keepdims=True) + NORM_EPS)
    return (y * g.astype(jnp.float32)).astype(x.dtype)


def partial_rope(x, positions):
    head_dim = x.shape[-1]
    rot = head_dim // ROPE_FRACTION
    half = rot // 2
    inv_freq = jnp.power(jnp.float32(ROPE_THETA), -jnp.arange(half, dtype=jnp.float32) / half)
    ang = positions.astype(jnp.float32)[..., None] * inv_freq
    ang = ang.reshape(ang.shape[:2] + (1,) * (x.ndim - 3) + (half,))
    cos = jnp.cos(ang).astype(x.dtype)
    sin = jnp.sin(ang).astype(x.dtype)
    x1, x2, rest = x[..., :half], x[..., half:rot], x[..., rot:]
    return jnp.concatenate([x1 * cos - x2 * sin, x2 * cos + x1 * sin, rest], axis=-1)


def to_blocks(t, n_blocks, block):
    return jnp.moveaxis(t.reshape((t.shape[0], n_blocks, block) + t.shape[2:]), 1, 0)


def from_blocks(t):
    t = jnp.moveaxis(t, 0, 1)
    return t.reshape((t.shape[0], t.shape[1] * t.shape[2]) + t.shape[3:])


def diff_attention(h, positions, w_in, w_out, q_norm_g, k_norm_g, lq1, lk1, lq2, lk2, subln_g, lambda_init):
    bsz, seq, _ = h.shape
    proj = h @ w_in
    q, k, v = jnp.split(proj, [DA_Q_COLS, 2 * DA_Q_COLS], axis=-1)
    q = partial_rope(rms_norm(q.reshape(bsz, seq, DA_HEADS, 2, DA_QK_DIM), q_norm_g), positions)
    k = partial_rope(rms_norm(k.reshape(bsz, seq, DA_HEADS, 2, DA_QK_DIM), k_norm_g), positions)
    v = v.reshape(bsz, seq, DA_HEADS, DA_V_DIM)
    lam = (jnp.exp(jnp.sum(lq1.astype(jnp.float32) * lk1.astype(jnp.float32)))
           - jnp.exp(jnp.sum(lq2.astype(jnp.float32) * lk2.astype(jnp.float32))) + lambda_init)
    scale = DA_QK_DIM ** -0.5
    nb = seq // Q_BLOCK
    k_pos = jnp.arange(seq)

    def block(args):
        qb, start = args
        q_pos = start + jnp.arange(Q_BLOCK)
        s = jnp.einsum('bqhcd,bkhcd->bhcqk', qb, k).astype(jnp.float32) * scale
        s = jnp.where(k_pos[None, :] <= q_pos[:, None], s, -jnp.inf)
        p = jax.nn.softmax(s, axis=-1)
        a = p[:, :, 0] - lam * p[:, :, 1]
        return jnp.einsum('bhqk,bkhd->bqhd', a.astype(v.dtype), v)

    o = from_blocks(lax.map(block, (to_blocks(q, nb, Q_BLOCK), jnp.arange(nb) * Q_BLOCK)))
    o = rms_norm(o, subln_g) * (1.0 - lambda_init)
    return o.reshape(bsz, seq, DA_HEADS * DA_V_DIM) @ w_out


def ssd_chunked_scan(xs, dt, a, bm, cm):
    bsz, seq = xs.shape[:2]
    nc = seq // SSD_CHUNK
    causal = jnp.tril(jnp.ones((SSD_CHUNK, SSD_CHUNK), dtype=bool))

    def step(state, inp):
        xc, dtc, bc, cc = inp
        acum = jnp.cumsum(dtc * a, axis=1)
        seg = acum[:, :, None] - acum[:, None, :]
        decay = jnp.exp(jnp.where(causal[None, :, :, None, None], seg, -jnp.inf))
        xdt = xc * dtc[..., None]
        cb = jnp.einsum('btgn,bsgn->btsg', cc, bc)
        y_intra = jnp.einsum('btsg,btsgh,bsghp->btghp', cb, decay, xdt)
        y_inter = jnp.einsum('btgn,bghpn->btghp', cc, state) * jnp.exp(acum)[..., None]
        to_end = jnp.exp(acum[:, -1:] - acum)
        new_state = (state * jnp.exp(acum[:, -1])[..., None, None]
                     + jnp.einsum('bsgh,bsghp,bsgn->bghpn', to_end, xdt, bc))
        return new_state, y_intra + y_inter

    state0 = jnp.zeros((bsz, SSD_GROUPS, SSD_HPG, SSD_HEAD_DIM, SSD_STATE), jnp.float32)
    inputs = (to_blocks(xs, nc, SSD_CHUNK), to_blocks(dt, nc, SSD_CHUNK),
              to_blocks(bm, nc, SSD_CHUNK), to_blocks(cm, nc, SSD_CHUNK))
    _, ys = lax.scan(step, state0, inputs)
    return from_blocks(ys)


def mamba2_ssd(h, w_in, conv_w, conv_b, dt_bias, a_log, d_skip, norm_g, w_out):
    bsz, seq, _ = h.shape
    proj = h @ w_in
    z, xbc, dt = jnp.split(proj, [SSD_INNER, SSD_INNER + SSD_CONV_DIM], axis=-1)
    xbc = lax.conv_general_dilated(xbc, conv_w[:, None, :].astype(xbc.dtype), window_strides=(1,),
                                   padding=[(SSD_CONV - 1, 0)], dimension_numbers=('NWC', 'WIO', 'NWC'),
                                   feature_group_count=SSD_CONV_DIM) + conv_b
    xbc = jax.nn.silu(xbc)
    xs, bm, cm = jnp.split(xbc, [SSD_INNER, SSD_INNER + SSD_GROUPS * SSD_STATE], axis=-1)
    xs = xs.reshape(bsz, seq, SSD_GROUPS, SSD_HPG, SSD_HEAD_DIM).astype(jnp.float32)
    bm = bm.reshape(bsz, seq, SSD_GROUPS, SSD_STATE).astype(jnp.float32)
    cm = cm.reshape(bsz, seq, SSD_GROUPS, SSD_STATE).astype(jnp.float32)
    dt = jax.nn.softplus(dt.astype(jnp.float32) + dt_bias.astype(jnp.float32))
    dt = dt.reshape(bsz, seq, SSD_GROUPS, SSD_HPG)
    a = -jnp.exp(a_log.astype(jnp.float32)).reshape(SSD_GROUPS, SSD_HPG)
    y = ssd_chunked_scan(xs, dt, a, bm, cm)
    y = y + d_skip.astype(jnp.float32).reshape(SSD_GROUPS, SSD_HPG)[:, :, None] * xs
    y = y.reshape(bsz, seq, SSD_GROUPS, SSD_HPG * SSD_HEAD_DIM).astype(h.dtype)
    y = y * jax.nn.silu(z).reshape(bsz, seq, SSD_GROUPS, SSD_HPG * SSD_HEAD_DIM)
    y = rms_norm(y, norm_g.reshape(SSD_GROUPS, SSD_HPG * SSD_HEAD_DIM))
    return y.reshape(bsz, seq, SSD_INNER) @ w_out


def dsa_attention(h, positions, w_in, w_out, q_norm_g, k_norm_g, idx_k_norm_g):
    bsz, seq, _ = h.shape
    proj = h @ w_in
    o1 = SA_HEADS * SA_HEAD_DIM
    o2 = o1 + SA_KV_HEADS * SA_HEAD_DIM
    o3 = o2 + SA_KV_HEADS * SA_HEAD_DIM
    o4 = o3 + IDX_HEADS * IDX_DIM
    o5 = o4 + IDX_DIM
    q, k, v, qi, ki, wi = jnp.split(proj, [o1, o2, o3, o4, o5], axis=-1)
    q = partial_rope(rms_norm(q.reshape(bsz, seq, SA_HEADS, SA_HEAD_DIM), q_norm_g), positions)
    q = q.reshape(bsz, seq, SA_KV_HEADS, SA_REP, SA_HEAD_DIM)
    k = partial_rope(rms_norm(k.reshape(bsz, seq, SA_KV_HEADS, SA_HEAD_DIM), k_norm_g), positions)
    v = v.reshape(bsz, seq, SA_KV_HEADS, SA_HEAD_DIM)
    qi = partial_rope(qi.reshape(bsz, seq, IDX_HEADS, IDX_DIM), positions)
    ki = partial_rope(rms_norm(ki, idx_k_norm_g), positions)
    wi = wi * (IDX_HEADS ** -0.5)
    k_sel = min(TOPK_MAX, seq // 4)
    nb = seq // Q_BLOCK
    k_pos = jnp.arange(seq)
    b_idx = jnp.arange(bsz)[:, None, None]

    def block(args):
        qb, qib, wib, start = args
        q_pos = start + jnp.arange(Q_BLOCK)
        rel = jax.nn.relu(jnp.einsum('bqhd,bkd->bqhk', qib, ki).astype(jnp.float32) * IDX_DIM ** -0.5)
        score = jnp.einsum('bqh,bqhk->bqk', wib.astype(jnp.float32), rel)
        score = jnp.where((k_pos[None, :] <= q_pos[:, None])[None], score, -jnp.inf)
        _, idx = lax.top_k(score, k_sel)
        ks = k[b_idx, idx]
        vs = v[b_idx, idx]
        s = jnp.einsum('bqgrd,bqkgd->bqgrk', qb, ks).astype(jnp.float32) * SA_HEAD_DIM ** -0.5
        valid = idx <= q_pos[None, :, None]
        s = jnp.where(valid[:, :, None, None, :], s, -jnp.inf)
        p = jax.nn.softmax(s, axis=-1)
        return jnp.einsum('bqgrk,bqkgd->bqgrd', p.astype(vs.dtype), vs)

    o = lax.map(block, (to_blocks(q, nb, Q_BLOCK), to_blocks(qi, nb, Q_BLOCK),
                        to_blocks(wi, nb, Q_BLOCK), jnp.arange(nb) * Q_BLOCK))
    o = from_blocks(o).reshape(bsz, seq, SA_HEADS * SA_HEAD_DIM)
    return o @ w_out


def swiglu(h, w_gate_up, w_down):
    g, u = jnp.split(h @ w_gate_up, 2, axis=-1)
    return (jax.nn.silu(g) * u) @ w_down


def setup_inputs(seed: int = 0) -> dict:
    key = jax.random.key(seed)
    ks = list(jax.random.split(key, 40))
    n_a, n_b, n_c = (len(range(m, DEPTH, N_MIXERS)) for m in range(N_MIXERS))

    def nrm(k, shape, scale):
        return jax.random.normal(k, shape, jnp.float32) * scale

    def gain(k, shape):
        return 1.0 + nrm(k, shape, 0.1)

    dt0 = jnp.exp(jax.random.uniform(ks[24], (n_b, SSD_HEADS), jnp.float32,
                                     minval=math.log(1e-3), maxval=math.log(1e-1)))
    return {
        'x': nrm(ks[0], (BATCH, SEQ, D_MODEL), 1.0),
        'c': nrm(ks[1], (BATCH, D_MODEL), 1.0),
        'positions': (jax.random.randint(ks[2], (BATCH, 1), 0, 1024, dtype=jnp.int32)
                      + jnp.arange(SEQ, dtype=jnp.int32)[None, :]),
        'norm1_g': gain(ks[3], (DEPTH, D_MODEL)),
        'norm2_g': gain(ks[4], (DEPTH, D_MODEL)),
        'ada_w': nrm(ks[5], (DEPTH, D_MODEL, ADA_CHUNKS * D_MODEL), 0.3 * D_MODEL ** -0.5),
        'ada_b': nrm(ks[6], (DEPTH, ADA_CHUNKS * D_MODEL), 0.02),
        'ffn_w_gate_up': nrm(ks[7], (DEPTH, D_MODEL, 2 * FFN_HIDDEN), D_MODEL ** -0.5),
        'ffn_w_down': nrm(ks[8], (DEPTH, FFN_HIDDEN, D_MODEL), FFN_HIDDEN ** -0.5),
        'da_w_in': nrm(ks[9], (n_a, D_MODEL, DA_IN), D_MODEL ** -0.5),
        'da_w_out': nrm(ks[10], (n_a, DA_HEADS * DA_V_DIM, D_MODEL), (DA_HEADS * DA_V_DIM) ** -0.5),
        'da_q_norm_g': gain(ks[11], (n_a, DA_QK_DIM)),
        'da_k_norm_g': gain(ks[12], (n_a, DA_QK_DIM)),
        'da_lambda_q1': nrm(ks[13], (n_a, DA_QK_DIM), 0.1),
        'da_lambda_k1': nrm(ks[14], (n_a, DA_QK_DIM), 0.1),
        'da_lambda_q2': nrm(ks[15], (n_a, DA_QK_DIM), 0.1),
        'da_lambda_k2': nrm(ks[16], (n_a, DA_QK_DIM), 0.1),
        'da_subln_g': gain(ks[17], (n_a, DA_V_DIM)),
        'ssd_w_in': nrm(ks[18], (n_b, D_MODEL, SSD_IN), D_MODEL ** -0.5),
        'ssd_conv_w': nrm(ks[19], (n_b, SSD_CONV, SSD_CONV_DIM), SSD_CONV ** -0.5),
        'ssd_conv_b': nrm(ks[20], (n_b, SSD_CONV_DIM), 0.02),
        'ssd_dt_bias': dt0 + jnp.log(-jnp.expm1(-dt0)),
        'ssd_a_log': jnp.log(jax.random.uniform(ks[21], (n_b, SSD_HEADS), jnp.float32, minval=1.0, maxval=16.0)),
        'ssd_d_skip': gain(ks[22], (n_b, SSD_HEADS)),
        'ssd_norm_g': gain(ks[23], (n_b, SSD_INNER)),
        'ssd_w_out': nrm(ks[25], (n_b, SSD_INNER, D_MODEL), SSD_INNER ** -0.5),
        'sa_w_in': nrm(ks[26], (n_c, D_MODEL, SA_IN), D_MODEL ** -0.5),
        'sa_w_out': nrm(ks[27], (n_c, SA_HEADS * SA_HEAD_DIM, D_MODEL), (SA_HEADS * SA_HEAD_DIM) ** -0.5),
        'sa_q_norm_g': gain(ks[28], (n_c, SA_HEAD_DIM)),
        'sa_k_norm_g': gain(ks[29], (n_c, SA_HEAD_DIM)),
        'sa_idx_k_norm_g': gain(ks[30], (n_c, IDX_DIM)),
    }


def reference(x, c, positions, norm1_g, norm2_g, ada_w, ada_b, ffn_w_gate_up, ffn_w_down,
              da_w_in, da_w_out, da_q_norm_g, da_k_norm_g, da_lambda_q1, da_lambda_k1,
              da_lambda_q2, da_lambda_k2, da_subln_g,
              ssd_w_in, ssd_conv_w, ssd_conv_b, ssd_dt_bias, ssd_a_log, ssd_d_skip, ssd_norm_g, ssd_w_out,
              sa_w_in, sa_w_out, sa_q_norm_g, sa_k_norm_g, sa_idx_k_norm_g):
    c_act = jax.nn.silu(c)
    for i in range(DEPTH):
        kind, j = i % N_MIXERS, i // N_MIXERS
        ada = c_act @ ada_w[i] + ada_b[i]
        shift1, scale1, gate1, shift2, scale2, gate2 = jnp.split(ada[:, None, :], ADA_CHUNKS, axis=-1)
        h = rms_norm(x, norm1_g[i]) * (1.0 + scale1) + shift1
        if kind == 0:
            lambda_init = 0.8 - 0.6 * math.exp(-0.3 * i)
            m = diff_attention(h, positions, da_w_in[j], da_w_out[j], da_q_norm_g[j], da_k_norm_g[j],
                               da_lambda_q1[j], da_lambda_k1[j], da_lambda_q2[j], da_lambda_k2[j],
                               da_subln_g[j], lambda_init)
        elif kind == 1:
            m = mamba2_ssd(h, ssd_w_in[j], ssd_conv_w[j], ssd_conv_b[j], ssd_dt_bias[j], ssd_a_log[j],
                           ssd_d_skip[j], ssd_norm_g[j], ssd_w_out[j])
        else:
            m = dsa_attention(h, positions, sa_w_in[j], sa_w_out[j], sa_q_norm_g[j], sa_k_norm_g[j],
                              sa_idx_k_norm_g[j])
        x = x + gate1 * m
        h = rms_norm(x, norm2_g[i]) * (1.0 + scale2) + shift2
        x = x + gate2 * swiglu(h, ffn_w_gate_up[i], ffn_w_down[i])
    return x
```

```python
import math
import numpy as np
from contextlib import ExitStack
import ml_dtypes
import concourse.bass as bass
import concourse.mybir as mybir
from concourse.bass_utils import run_bass_kernel_spmd

F32 = mybir.dt.float32
BF16 = mybir.dt.bfloat16
I32 = mybir.dt.int32
ALU = mybir.AluOpType
AF = mybir.ActivationFunctionType
AX = mybir.AxisListType

D = 2048
S = 4096
NB = 4
DEPTH = 4
KC = D // 128
TOK = 2048
NT = TOK // 128
FFN = 5632
EPS = 1e-6
ROPE_THETA = 500000.0
NEG = -30000.0

SAME_ENGINE_SYNC = True


class Dep:
    __slots__ = ("name", "w", "r", "dsem", "dq")

    def __init__(self, name=""):
        self.name = name
        self.w = {}
        self.r = {}
        self.dsem = None
        self.dq = None


class X:
    def __init__(self, nc, stack):
        self.nc = nc
        self.stack = stack
        self.root = stack
        self.eng = {"pe": nc.tensor, "act": nc.scalar, "dve": nc.vector,
                    "pool": nc.gpsimd, "sp": nc.sync}
        self.sem = {}
        self.cnt = {}
        self.seen = {}
        for k in self.eng:
            self.sem[k] = stack.enter_context(nc.semaphore("es_" + k))
            self.cnt[k] = 0
            self.seen[k] = {}
        self.nsem = 5
        self.semcnt = {}
        self.free_dsems = {"sp": [], "pool": [], "act": []}
        self.alltok = {}
        self.uid = 0

    def name(self, p):
        self.uid += 1
        return "%s_%d" % (p, self.uid)

    def sb(self, shape, dt, name="sb", stack=None):
        st = stack or self.stack
        return st.enter_context(self.nc.sbuf_tensor(self.name(name), list(shape), dt))

    def ps(self, shape, dt=F32, name="ps", stack=None):
        st = stack or self.stack
        return st.enter_context(self.nc.psum_tensor(self.name(name), list(shape), dt))

    def _wait(self, e, toks):
        en = self.eng[e]
        seen = self.seen[e]
        for s, v in toks.items():
            cv = self.semcnt.get(s)
            if cv is not None and cv > v:
                v = cv
            if seen.get(s, 0) >= v:
                continue
            if s is self.sem[e]:
                if e == "pe" or not SAME_ENGINE_SYNC:
                    continue
            en.wait_ge(s, v)
            seen[s] = v

    def _pre(self, e, r, w, mw=()):
        for d in r:
            self._wait(e, d.w)
        for d in w:
            self._wait(e, d.w)
            self._wait(e, d.r)
        for d in mw:
            self._wait(e, d.r)

    def _post(self, tok, r, w, mw=()):
        s, v = tok
        if self.alltok.get(s, 0) < v:
            self.alltok[s] = v
        for d in r:
            if d.r.get(s, 0) < v:
                d.r[s] = v
        for d in w:
            d.w = {s: v}
            d.r = {}
        for d in mw:
            if d.w.get(s, 0) < v:
                d.w[s] = v

    def op(self, e, fn, r=(), w=(), mw=(), inc=True):
        self._pre(e, r, w, mw)
        ins = fn(self.eng[e])
        if inc:
            self.cnt[e] += 1
            ins.then_inc(self.sem[e], 1)
            self._post((self.sem[e], self.cnt[e]), r, w, mw)
        else:
            self._post((self.sem[e], self.cnt[e] + 1), r, w, mw)
        return ins

    def dma(self, e, out, in_, r=(), w=(), mw=(), **kw):
        host = (list(w) + list(mw))[0]
        assert host.dq in (None, e), (host.name, host.dq, e)
        if host.dsem is None:
            host.dq = e
            if self.free_dsems[e]:
                host.dsem = self.free_dsems[e].pop()
            else:
                host.dsem = self.root.enter_context(self.nc.semaphore(self.name("ds")))
                self.semcnt[host.dsem] = 0
                self.nsem += 1
        self._pre(e, r, w, mw)
        ins = self.eng[e].dma_start(out=out, in_=in_, **kw)
        self.semcnt[host.dsem] += 16
        ins.then_inc(host.dsem, 16)
        self._post((host.dsem, self.semcnt[host.dsem]), r, w, mw)
        return ins

    def mkdep(self, name=""):
        d = Dep(name)
        if isinstance(self.stack, Scope):
            self.stack.deps.append(d)
        return d

    def global_barrier(self):
        for e in self.eng:
            self._wait(e, dict(self.alltok))

    def barrier(self, deps):
        toks = {}
        for d in deps:
            for src in (d.w, d.r):
                for s_, v in src.items():
                    if toks.get(s_, 0) < v:
                        toks[s_] = v
        for e in self.eng:
            self._wait(e, toks)

    def finish(self, deps, e="sp"):
        for d in deps:
            self._wait(e, d.w)


class Scope:
    def __init__(self, x):
        self.x = x
        self.st = ExitStack()
        self.deps = []

    def __enter__(self):
        self.st.__enter__()
        self.prev = self.x.stack
        self.x.stack = self
        return self

    def __exit__(self, *a):
        self.x.stack = self.prev
        if a[0] is None:
            self.x.barrier(self.deps)
            for d in self.deps:
                if d.dsem is not None:
                    self.x.free_dsems[d.dq].append(d.dsem)
                    d.dsem = None
                    d.dq = None
        return self.st.__exit__(*a)

    def enter_context(self, cm):
        return self.st.enter_context(cm)


class T:
    def __init__(self, x, shape, dt, name, psum=False, stack=None):
        stack = stack or x.stack
        self.t = x.ps(shape, dt, name, stack) if psum else x.sb(shape, dt, name, stack)
        self.d = Dep(name)
        if isinstance(stack, Scope):
            stack.deps.append(self.d)

    def __getitem__(self, k):
        return self.t[k]


def sbt(x, shape, dt, name, stack=None):
    return T(x, shape, dt, name, False, stack)


def pst(x, shape, dt, name, stack=None):
    return T(x, shape, dt, name, True, stack)


class Consts:
    pass


def make_consts(x):
    c = Consts()
    idf = sbt(x, [128, 128], F32, "idf")
    c.ident = sbt(x, [128, 128], BF16, "ident")
    x.op("pool", lambda e: e.memset(idf[:], 1.0), w=[idf.d])
    x.op("pool", lambda e: e.affine_select(out=idf[:], in_=idf[:], pattern=[[-1, 128]],
                                           compare_op=ALU.is_equal, fill=0.0, base=0,
                                           channel_multiplier=1), r=[idf.d], w=[idf.d])
    x.op("dve", lambda e: e.tensor_copy(c.ident[:], idf[:]), r=[idf.d], w=[c.ident.d])
    c.identf = idf
    ngf = sbt(x, [128, 128], F32, "ngf")
    c.negT = sbt(x, [128, 128], BF16, "negT")
    x.op("pool", lambda e: e.memset(ngf[:], 0.0), w=[ngf.d])
    x.op("pool", lambda e: e.affine_select(out=ngf[:], in_=ngf[:], pattern=[[1, 128]],
                                           compare_op=ALU.is_ge, fill=NEG, base=0,
                                           channel_multiplier=-1), r=[ngf.d], w=[ngf.d])
    x.op("dve", lambda e: e.tensor_copy(c.negT[:], ngf[:]), r=[ngf.d], w=[c.negT.d])
    c.negQ = sbt(x, [128, 128], F32, "negQ")
    x.op("pool", lambda e: e.memset(c.negQ[:], 0.0), w=[c.negQ.d])
    x.op("pool", lambda e: e.affine_select(out=c.negQ[:], in_=c.negQ[:], pattern=[[-1, 128]],
                                           compare_op=ALU.is_ge, fill=-1e30, base=0,
                                           channel_multiplier=1), r=[c.negQ.d], w=[c.negQ.d])
    trf = sbt(x, [128, 128], F32, "trf")
    c.tri = sbt(x, [128, 128], BF16, "tri")
    x.op("pool", lambda e: e.memset(trf[:], 1.0), w=[trf.d])
    x.op("pool", lambda e: e.affine_select(out=trf[:], in_=trf[:], pattern=[[1, 128]],
                                           compare_op=ALU.is_ge, fill=0.0, base=0,
                                           channel_multiplier=-1), r=[trf.d], w=[trf.d])
    x.op("dve", lambda e: e.tensor_copy(c.tri[:], trf[:]), r=[trf.d], w=[c.tri.d])
    c.trif = trf
    c.ones = sbt(x, [128, 128], BF16, "ones")
    x.op("pool", lambda e: e.memset(c.ones[:], 1.0), w=[c.ones.d])
    c.onesf = sbt(x, [128, 128], F32, "onesf")
    x.op("pool", lambda e: e.memset(c.onesf[:], 1.0), w=[c.onesf.d])
    c.nhalf = sbt(x, [128, 64], F32, "nhalf")
    x.op("pool", lambda e: e.memset(c.nhalf[:], -0.5), w=[c.nhalf.d])
    return c


def rsqrt_mean(x, c, out, ss, n, width):
    x.op("dve", lambda e: e.tensor_scalar(out=out[:, 0:width], in0=ss[:, 0:width], scalar1=1.0 / n,
                                          scalar2=EPS, op0=ALU.mult, op1=ALU.add),
         r=[ss.d], w=[out.d])
    x.op("pool", lambda e: e.tensor_tensor(out=out[:, 0:width], in0=out[:, 0:width],
                                           in1=c.nhalf[:, 0:width], op=ALU.pow),
         r=[out.d, c.nhalf.d], w=[out.d])


def bc(ap, shape):
    return ap.to_broadcast(list(shape))


def emit_ada(x, c_ap, adaw_ap, adab_ap, ada_ap, d_ada):
    with Scope(x) as ls:
        cs = sbt(x, [128, 16], F32, "c_sb", ls)
        ca = sbt(x, [128, 16], F32, "c_act", ls)
        brow = sbt(x, [1, 6 * D], F32, "brow", ls)
        arow = sbt(x, [1, 6 * D], F32, "arow", ls)
        wt = [sbt(x, [128, 16, 512], F32, "adaw%d" % i, ls) for i in range(2)]
        pa = [pst(x, [1, 512], F32, "adap%d" % i, ls) for i in range(2)]
        x.dma("sp", cs[:], c_ap.rearrange("(p k) -> p k", k=16), w=[cs.d])
        x.dma("sp", brow[:], adab_ap.rearrange("(o n) -> o n", o=1), w=[brow.d])
        x.op("act", lambda e: e.activation(out=ca[:], in_=cs[:], func=AF.Silu), r=[cs.d], w=[ca.d])
        for blk in range(24):
            i = blk % 2
            x.dma("sp", wt[i][:], adaw_ap[:, blk * 512:(blk + 1) * 512].rearrange("(p k) f -> p k f", k=16),
                  w=[wt[i].d])
            for k in range(16):
                x.op("pe", lambda e: e.matmul(pa[i][:], ca[:, k:k + 1], wt[i][:, k, :],
                                              start=(k == 0), stop=(k == 15)),
                     r=[ca.d, wt[i].d], w=[pa[i].d], inc=(k == 15))
            x.op("dve", lambda e: e.tensor_tensor(out=arow[0:1, blk * 512:(blk + 1) * 512], in0=pa[i][:],
                                                  in1=brow[0:1, blk * 512:(blk + 1) * 512], op=ALU.add),
                 r=[pa[i].d, brow.d], mw=[arow.d])
        x.dma("sp", ada_ap.rearrange("(o n) -> o n", o=1), arow[:], r=[arow.d], w=[d_ada])


def load_cols(x, ls, vec_ap, name, d_src=None):
    t = sbt(x, [128, 16], F32, name, ls)
    x.dma("sp", t[:], vec_ap.rearrange("(k p) -> p k", p=128), r=([d_src] if d_src else []), w=[t.d],
          allow_slow_non_contiguous=True)
    return t


def load_bcast(x, ls, vec_ap, n, name, d_src=None, dt=F32):
    t = sbt(x, [128, n], dt, name, ls)
    x.dma("sp", t[:], vec_ap.rearrange("(o n) -> o n", o=1).partition_broadcast(128),
          r=([d_src] if d_src else []), w=[t.d])
    return t


def emit_mod_cols(x, ls, g_ap, ada_ap, d_ada, scale_idx, shift_idx):
    g = load_cols(x, ls, g_ap, "gcol")
    sc = load_cols(x, ls, ada_ap[scale_idx * D:(scale_idx + 1) * D], "sccol", d_ada)
    sh = load_cols(x, ls, ada_ap[shift_idx * D:(shift_idx + 1) * D], "shcol", d_ada)
    sT = sbt(x, [128, 16], F32, "sT", ls)
    x.op("dve", lambda e: e.scalar_tensor_tensor(out=sT[:], in0=sc[:], scalar=1.0, in1=g[:],
                                                 op0=ALU.add, op1=ALU.mult),
         r=[sc.d, g.d], w=[sT.d])
    return sT, sh


def emit_norm_hT(x, c, x_ap, d_x, ntiles, sT, shT, hT, hT_d, tile_off=0):
    with Scope(x) as ls:
        xt = [sbt(x, [128, D], F32, "xt%d" % i, ls) for i in range(2)]
        xn = [sbt(x, [128, D], BF16, "xn%d" % i, ls) for i in range(2)]
        junk = sbt(x, [128, D], BF16, "junk", ls)
        ss = [sbt(x, [128, 1], F32, "ss%d" % i, ls) for i in range(2)]
        rstd = [sbt(x, [128, 1], F32, "rstd%d" % i, ls) for i in range(2)]
        pt = [pst(x, [128, 512], BF16, "ptn%d" % i, ls) for i in range(2)]
        for t in range(ntiles):
            i = t % 2
            x.dma("sp", xt[i][:], x_ap[t * 128:(t + 1) * 128, :], r=[d_x], w=[xt[i].d])
            x.op("act", lambda e: e.activation(out=junk[:], in_=xt[i][:], func=AF.Square,
                                               accum_out=ss[i][:, 0:1]),
                 r=[xt[i].d], w=[junk.d, ss[i].d])
            rsqrt_mean(x, c, rstd[i], ss[i], D, 1)
            x.op("act", lambda e: e.activation(out=xn[i][:], in_=xt[i][:], func=AF.Copy,
                                               scale=rstd[i][:, 0:1]),
                 r=[xt[i].d, rstd[i].d], w=[xn[i].d])
            for g in range(4):
                p = pt[g % 2]
                for j in range(4):
                    kc = g * 4 + j
                    x.op("pe", lambda e: e.transpose(p[:, j * 128:(j + 1) * 128],
                                                     xn[i][:, kc * 128:(kc + 1) * 128], c.ident[:]),
                         r=[xn[i].d, c.ident.d], w=[p.d], inc=(j == 3))
                for j in range(4):
                    kc = g * 4 + j
                    dst = hT[:, kc, (tile_off + t) * 128:(tile_off + t + 1) * 128]
                    if True:
                        x.op("dve", lambda e: e.tensor_scalar(out=dst, in0=p[:, j * 128:(j + 1) * 128],
                                                              scalar1=sT[:, kc:kc + 1], scalar2=shT[:, kc:kc + 1],
                                                              op0=ALU.mult, op1=ALU.add),
                             r=[p.d, sT.d, shT.d], mw=[hT_d[tile_off + t]])
                    else:
                        x.op("act", lambda e: e.activation(out=dst, in_=p[:, j * 128:(j + 1) * 128],
                                                           func=AF.Identity, scale=sT[:, kc:kc + 1],
                                                           bias=shT[:, kc:kc + 1]),
                             r=[p.d, sT.d, shT.d], mw=[hT_d[tile_off + t]])


def emit_rope_tables(x, ls, pos_ap, invf_ap, half, ntiles):
    n = ntiles * half
    posi = sbt(x, [128, ntiles], I32, "posi", ls)
    posf = sbt(x, [128, ntiles], F32, "posf", ls)
    invf = sbt(x, [128, half], F32, "invf", ls)
    ang = sbt(x, [128, ntiles, half], F32, "ang", ls)
    x.dma("sp", posi[:], pos_ap.rearrange("(t p) -> p t", p=128), w=[posi.d], allow_slow_non_contiguous=True)
    x.dma("sp", invf[:], invf_ap, w=[invf.d])
    x.op("dve", lambda e: e.tensor_copy(posf[:], posi[:]), r=[posi.d], w=[posf.d])
    x.op("dve", lambda e: e.tensor_tensor(out=ang[:], in0=bc(posf[:].unsqueeze(2), [128, ntiles, half]),
                                          in1=bc(invf[:].unsqueeze(1), [128, ntiles, half]), op=ALU.mult),
         r=[posf.d, invf.d], w=[ang.d])
    outs = []
    C1 = 6.28125
    C2 = 2.0 * math.pi - C1
    for nm, shift in (("cos", math.pi / 2), ("sin", 0.0)):
        a = sbt(x, [128, n], F32, nm + "_a", ls)
        ki = sbt(x, [128, n], I32, nm + "_ki", ls)
        kf = sbt(x, [128, n], F32, nm + "_kf", ls)
        m = sbt(x, [128, n], F32, nm + "_m", ls)
        res = sbt(x, [128, ntiles, half], F32, nm + "_t", ls)
        af = ang[:].rearrange("p t h -> p (t h)")
        x.op("dve", lambda e: e.tensor_scalar(out=a[:], in0=af, scalar1=shift, scalar2=None, op0=ALU.add),
             r=[ang.d], w=[a.d])
        x.op("dve", lambda e: e.tensor_scalar(out=kf[:], in0=a[:], scalar1=1.0 / (2 * math.pi), scalar2=None,
                                              op0=ALU.mult), r=[a.d], w=[kf.d])
        x.op("dve", lambda e: e.tensor_copy(ki[:], kf[:]), r=[kf.d], w=[ki.d])
        x.op("dve", lambda e: e.tensor_copy(kf[:], ki[:]), r=[ki.d], w=[kf.d])
        x.op("dve", lambda e: e.scalar_tensor_tensor(out=a[:], in0=kf[:], scalar=-C1, in1=a[:],
                                                     op0=ALU.mult, op1=ALU.add), r=[kf.d, a.d], w=[a.d])
        x.op("dve", lambda e: e.scalar_tensor_tensor(out=a[:], in0=kf[:], scalar=-C2, in1=a[:],
                                                     op0=ALU.mult, op1=ALU.add), r=[kf.d, a.d], w=[a.d])
        x.op("dve", lambda e: e.tensor_scalar(out=m[:], in0=a[:], scalar1=math.pi, scalar2=2 * math.pi,
                                              op0=ALU.is_gt, op1=ALU.mult), r=[a.d], w=[m.d])
        x.op("dve", lambda e: e.tensor_tensor(out=a[:], in0=a[:], in1=m[:], op=ALU.subtract),
             r=[a.d, m.d], w=[a.d])
        x.op("dve", lambda e: e.tensor_scalar(out=m[:], in0=a[:], scalar1=-math.pi, scalar2=2 * math.pi,
                                              op0=ALU.is_lt, op1=ALU.mult), r=[a.d], w=[m.d])
        x.op("dve", lambda e: e.tensor_tensor(out=a[:], in0=a[:], in1=m[:], op=ALU.add),
             r=[a.d, m.d], w=[a.d])
        x.op("dve", lambda e: e.tensor_scalar(out=a[:], in0=a[:], scalar1=-3.1415925, scalar2=3.1415925,
                                              op0=ALU.max, op1=ALU.min), r=[a.d], w=[a.d])
        x.op("act", lambda e: e.activation(out=res[:].rearrange("p t h -> p (t h)"), in_=a[:], func=AF.Sin),
             r=[a.d], w=[res.d])
        outs.append(res)
    return outs[0], outs[1]


def emit_qk_post(x, c, ls_tiles, ps, ncols, gdim, gain, half, cos, sin, t, out_bf):
    ng = ncols // gdim
    qn, sq, ssq, rs, t1, t2, t3, t4 = ls_tiles
    pv = ps[:, 0:ncols].rearrange("p (g d) -> p g d", d=gdim)
    qv = qn[:, 0:ncols].rearrange("p (g d) -> p g d", d=gdim)
    if gain is not None:
        x.op("act", lambda e: e.activation(out=sq[:, 0:ncols], in_=ps[:, 0:ncols], func=AF.Square),
             r=[ps.d], w=[sq.d])
        x.op("dve", lambda e: e.tensor_reduce(out=ssq[:, 0:ng], in_=sq[:, 0:ncols].rearrange("p (g d) -> p g d", d=gdim),
                                              axis=AX.X, op=ALU.add), r=[sq.d], w=[ssq.d])
        rsqrt_mean(x, c, rs, ssq, gdim, ng)
        x.op("dve", lambda e: e.tensor_tensor(out=qv, in0=pv, in1=bc(rs[:, 0:ng].unsqueeze(2), [128, ng, gdim]),
                                              op=ALU.mult), r=[ps.d, rs.d], w=[qn.d])
        x.op("pool", lambda e: e.tensor_tensor(out=qv, in0=qv, in1=bc(gain[:, 0:gdim].unsqueeze(1), [128, ng, gdim]),
                                               op=ALU.mult), r=[qn.d, gain.d], w=[qn.d])
    else:
        x.op("act", lambda e: e.activation(out=qn[:, 0:ncols], in_=ps[:, 0:ncols], func=AF.Copy),
             r=[ps.d], w=[qn.d])
    if half:
        x1 = qv[:, :, 0:half]
        x2 = qv[:, :, half:2 * half]
        cb = bc(cos[:, t, :].unsqueeze(1), [128, ng, half])
        sb_ = bc(sin[:, t, :].unsqueeze(1), [128, ng, half])
        tv = [tt[:, 0:ng * half].rearrange("p (g h) -> p g h", h=half) for tt in (t1, t2, t3, t4)]
        x.op("dve", lambda e: e.tensor_tensor(out=tv[0], in0=x1, in1=cb, op=ALU.mult), r=[qn.d, cos.d], w=[t1.d])
        x.op("dve", lambda e: e.tensor_tensor(out=tv[1], in0=x2, in1=sb_, op=ALU.mult), r=[qn.d, sin.d], w=[t2.d])
        x.op("pool", lambda e: e.tensor_tensor(out=tv[2], in0=x2, in1=cb, op=ALU.mult), r=[qn.d, cos.d], w=[t3.d])
        x.op("pool", lambda e: e.tensor_tensor(out=tv[3], in0=x1, in1=sb_, op=ALU.mult), r=[qn.d, sin.d], w=[t4.d])
        x.op("dve", lambda e: e.tensor_tensor(out=x1, in0=tv[0], in1=tv[1], op=ALU.subtract),
             r=[t1.d, t2.d], w=[qn.d])
        x.op("pool", lambda e: e.tensor_tensor(out=x2, in0=tv[2], in1=tv[3], op=ALU.add),
             r=[t3.d, t4.d, qn.d], w=[qn.d])
    x.op("act", lambda e: e.activation(out=out_bf[:, 0:ncols], in_=qn[:, 0:ncols], func=AF.Copy),
         r=[qn.d], w=[out_bf.d])


def alloc_qk_tiles(x, ls):
    qn = sbt(x, [128, 512], F32, "qn", ls)
    sq = sbt(x, [128, 512], F32, "sq", ls)
    ssq = sbt(x, [128, 8], F32, "ssq", ls)
    rs = sbt(x, [128, 8], F32, "rs", ls)
    ts = [sbt(x, [128, 128], F32, "rt%d" % i, ls) for i in range(4)]
    return (qn, sq, ssq, rs, ts[0], ts[1], ts[2], ts[3])


class ProjCtx:
    pass


def emit_proj(x, c, w_ap, hT, hT_d, ntiles, blocks):
    with Scope(x) as ls:
        wb = [sbt(x, [128, 16, 512], BF16, "wb%d" % i, ls) for i in range(2)]
        pp = [pst(x, [128, 512], F32, "pp%d" % i, ls) for i in range(2)]
        n = 0
        for bi, (col0, ncols, mode, handler) in enumerate(blocks):
            wt = wb[bi % 2]
            x.dma("pool", wt[:, :, 0:ncols], w_ap[:, col0:col0 + ncols].rearrange("(k p) f -> p k f", p=128),
                  w=[wt.d])
            if mode == "tok":
                for t in range(ntiles):
                    ps = pp[n % 2]
                    n += 1
                    for kc in range(16):
                        x.op("pe", lambda e: e.matmul(ps[:, 0:ncols], hT[:, kc, t * 128:(t + 1) * 128],
                                                      wt[:, kc, 0:ncols], start=(kc == 0), stop=(kc == 15)),
                             r=[hT_d[t], wt.d], w=[ps.d], inc=(kc == 15))
                    handler(t, ps)
            else:
                for fc in range(ncols // 128):
                    for tb in range(ntiles // 4):
                        ps = pp[n % 2]
                        n += 1
                        for kc in range(16):
                            x.op("pe", lambda e: e.matmul(ps[:, :], wt[:, kc, fc * 128:(fc + 1) * 128],
                                                          hT[:, kc, tb * 512:(tb + 1) * 512],
                                                          start=(kc == 0), stop=(kc == 15)),
                                 r=hT_d[tb * 4:tb * 4 + 4] + [wt.d], w=[ps.d], inc=(kc == 15))
                        handler(fc, tb, ps)


def emit_LA(x, c, dram, kind, dm=False):
    x_in = dram("x_in", [TOK, D], F32, "ExternalInput")
    c_in = dram("c_in", [D], F32, "ExternalInput")
    pos = dram("pos", [TOK], I32, "ExternalInput")
    adaw = dram("ada_w", [D, 6 * D], F32, "ExternalInput")
    adab = dram("ada_b", [6 * D], F32, "ExternalInput")
    g1 = dram("g1", [D], F32, "ExternalInput")
    ada = dram("ada", [6 * D], F32, "ExternalOutput")
    FIN = {0: 6144, 1: 10304, 2: 4176}[kind]
    w_in = dram("w_in", [D, FIN], F32, "ExternalInput")
    outs = []
    with Scope(x) as st:
        d_ada = x.mkdep("ada")
        d_x = x.mkdep("x")
        emit_ada(x, c_in, adaw, adab, ada, d_ada)
        hT = x.sb([128, 16, TOK], BF16, "hT")
        hT_d = [x.mkdep("hT%d" % t) for t in range(NT)]
        with Scope(x) as ls:
            sT, shT = emit_mod_cols(x, ls, g1, ada, d_ada, 1, 0)
            emit_norm_hT(x, c, x_in, d_x, NT, sT, shT, hT, hT_d)
        ls = st
        if kind == 0:
            invf = dram("invf", [128, 8], F32, "ExternalInput")
            gq = dram("gq", [64], F32, "ExternalInput")
            gk = dram("gk", [64], F32, "ExternalInput")
            qT = dram("qT", [16, 128, TOK], BF16, "ExternalOutput")
            kT = dram("kT", [16, 128, TOK], BF16, "ExternalOutput")
            v = dram("v", [2, TOK, 1024] if dm else [TOK, 2048], BF16, "ExternalOutput")
            outs = [x.mkdep("qT"), x.mkdep("kT"), x.mkdep("v")]
            cos, sin = emit_rope_tables(x, ls, pos, invf, 8, NT)
            gqb = load_bcast(x, ls, gq, 64, "gqb")
            gkb = load_bcast(x, ls, gk, 64, "gkb")
            qk_tiles = alloc_qk_tiles(x, ls)
            qbf = sbt(x, [128, 512], BF16, "qbf", ls)
            stage = [sbt(x, [128, 4, TOK], BF16, "stage%d" % i, ls) for i in range(2)]
            ptr = [pst(x, [128, 512], BF16, "ptr%d" % i, ls) for i in range(2)]
            vst = [sbt(x, [128, 512], BF16, "vst%d" % i, ls) for i in range(2)]
            blocks = []
            cnt = [0]
            for which in range(2):
                for hb in range(4):
                    def handler(t, ps, which=which, hb=hb):
                        sg = stage[(which * 4 + hb) % 2]
                        emit_qk_post(x, c, qk_tiles, ps, 512, 64, gqb if which == 0 else gkb, 8, cos, sin, t, qbf)
                        p = ptr[cnt[0] % 2]
                        cnt[0] += 1
                        for j in range(4):
                            x.op("pe", lambda e: e.transpose(p[:, j * 128:(j + 1) * 128], qbf[:, j * 128:(j + 1) * 128],
                                                             c.ident[:]),
                                 r=[qbf.d, c.ident.d], w=[p.d], inc=(j == 3))
                        x.op("dve", lambda e: e.tensor_copy(sg[:, :, t * 128:(t + 1) * 128],
                                                            p[:, :].rearrange("p (a b) -> p a b", a=4)),
                             r=[p.d], mw=[sg.d])
                        if t == NT - 1:
                            dst = (qT if which == 0 else kT)[hb * 4:(hb + 1) * 4, :, :].rearrange("h p t -> p h t")
                            x.dma("sp", dst, sg[:], r=[sg.d], mw=[outs[which]])
                    blocks.append((which * 2048 + hb * 512, 512, "tok", handler))
            for vb in range(4):
                def vhandler(t, ps, vb=vb):
                    vs = vst[cnt[0] % 2]
                    cnt[0] += 1
                    x.op("act", lambda e: e.activation(out=vs[:], in_=ps[:, :], func=AF.Copy), r=[ps.d], w=[vs.d])
                    vdst = (v[vb // 2, t * 128:(t + 1) * 128, (vb % 2) * 512:(vb % 2 + 1) * 512] if dm
                            else v[t * 128:(t + 1) * 128, vb * 512:(vb + 1) * 512])
                    x.dma("sp", vdst, vs[:], r=[vs.d], mw=[outs[2]])
                blocks.append((4096 + vb * 512, 512, "tok", vhandler))
            emit_proj(x, c, w_in, hT, hT_d, NT, blocks)
        elif kind == 1:
            blocks = la_handlers_ssd(x, c, ls, dram, outs, dm)
            emit_proj(x, c, w_in, hT, hT_d, NT, blocks)
        else:
            blocks = la_handlers_dsa(x, c, ls, dram, outs, pos, dm)
            emit_proj(x, c, w_in, hT, hT_d, NT, blocks)


def emit_LB_DA(x, c, dram, lambda_init):
    NH = 8
    qT = dram("qT", [NH, 128, S], BF16, "ExternalInput")
    kT = dram("kT", [NH, 128, S], BF16, "ExternalInput")
    v = dram("v", [S, NH * 128], BF16, "ExternalInput")
    lam4 = dram("lam4", [4, 64], F32, "ExternalInput")
    gqk = dram("gqk", [2, 64], F32, "ExternalInput")
    subg = dram("subg", [128], F32, "ExternalInput")
    o = dram("o", [S, NH * 128], BF16, "ExternalOutput")
    with Scope(x) as st:
        d_o = x.mkdep("o")
        ls = st
        lt = [load_bcast(x, ls, lam4[i], 64, "lam%d" % i) for i in range(4)]
        gt = [load_bcast(x, ls, gqk[i], 64, "gqk%d" % i) for i in range(2)]
        gs = load_bcast(x, ls, subg, 128, "gs")
        x.op("dve", lambda e: e.tensor_scalar(out=gs[:], in0=gs[:], scalar1=1.0 - lambda_init, scalar2=None,
                                              op0=ALU.mult), r=[gs.d], w=[gs.d])
        pr = sbt(x, [128, 64], F32, "pr")
        s12 = sbt(x, [128, 2], F32, "s12")
        e12 = sbt(x, [128, 2], F32, "e12")
        neglam = sbt(x, [128, 1], F32, "neglam")
        for i in range(2):
            x.op("dve", lambda e: e.tensor_tensor(out=pr[:], in0=lt[2 * i][:], in1=lt[2 * i + 1][:], op=ALU.mult),
                 r=[lt[2 * i].d, lt[2 * i + 1].d], w=[pr.d])
            x.op("dve", lambda e: e.tensor_reduce(out=s12[:, i:i + 1], in_=pr[:], axis=AX.X, op=ALU.add),
                 r=[pr.d], w=[s12.d])
        x.op("act", lambda e: e.activation(out=e12[:], in_=s12[:], func=AF.Exp), r=[s12.d], w=[e12.d])
        x.op("dve", lambda e: e.scalar_tensor_tensor(out=neglam[:], in0=e12[:, 1:2], scalar=-lambda_init,
                                                     in1=e12[:, 0:1], op0=ALU.add, op1=ALU.subtract),
             r=[e12.d], w=[neglam.d])
        gm = sbt(x, [128, 2], F32, "gm")
        negC = sbt(x, [128, 1], F32, "negC")
        for i in range(2):
            x.op("dve", lambda e: e.tensor_reduce(out=gm[:, i:i + 1], in_=gt[i][:], axis=AX.X, op=ALU.max,
                                                  apply_absolute_value=True), r=[gt[i].d], w=[gm.d])
        x.op("dve", lambda e: e.scalar_tensor_tensor(out=negC[:], in0=gm[:, 0:1], scalar=-8.0, in1=gm[:, 1:2],
                                                     op0=ALU.mult, op1=ALU.mult), r=[gm.d], w=[negC.d])
        kt = [sbt(x, [128, S], BF16, "kt%d" % i) for i in range(2)]
        qt = [sbt(x, [128, S], BF16, "qt%d" % i) for i in range(2)]
        va = [sbt(x, [128, 32, 129], BF16, "va%d" % i) for i in range(2)]
        for i in range(2):
            x.op("pool", lambda e: e.memset(va[i][:, :, 128:129], 1.0), w=[va[i].d])
        pss = [pst(x, [128, 512], F32, "pss%d" % i) for i in range(4)]
        pso = pst(x, [128, 8, 256], F32, "pso")
        pt = [sbt(x, [128, 512], BF16, "pt%d" % i) for i in range(4)]
        R = sbt(x, [128, 8], F32, "R")
        tmp = [sbt(x, [128, 128], F32, "tmp%d" % i) for i in range(2)]
        of = sbt(x, [128, 4, 128], F32, "of")
        sq = sbt(x, [128, 512], F32, "sqo")
        ssq = sbt(x, [128, 4], F32, "ssqo")
        rs = sbt(x, [128, 4], F32, "rso")
        ob = [sbt(x, [128, 4, 128], BF16, "ob%d" % i) for i in range(2)]
        zr = sbt(x, [128, 512], BF16, "zr")
        x.op("pool", lambda e: e.memset(zr[:], 0.0), w=[zr.d])
        psob = pso[:].rearrange("p a b -> p (a b)")
        n = 0
        nq = 0
        for h in range(NH):
            hi = h % 2
            x.dma("sp", kt[hi][:], kT[h], w=[kt[hi].d])
            x.dma("sp", qt[hi][:], qT[h], w=[qt[hi].d])
            x.dma("sp", va[hi][:, :, 0:128], v[:, h * 128:(h + 1) * 128].rearrange("(kb p) d -> p kb d", p=128),
                  mw=[va[hi].d])
            steps = [(qc, kb, comp) for qc in range(8) for kb in range(4 * qc + 4) for comp in range(2)]

            def scores(idx):
                qc, kb, comp = steps[idx]
                jmin = max(0, kb - 4 * qc)
                diag = kb >= 4 * qc
                N = 512 - 128 * jmin
                q0 = qc * 512 + 128 * jmin
                ps = pss[idx % 4]
                ptt = pt[idx % 4]
                pr_ = slice(comp * 64, (comp + 1) * 64)
                lhs = kt[hi][pr_, kb * 128:(kb + 1) * 128]
                if diag:
                    x.op("pe", lambda e: e.matmul(ps[:, 0:128], lhs, qt[hi][pr_, q0:q0 + 128],
                                                  start=True, stop=False),
                         r=[kt[hi].d, qt[hi].d], w=[ps.d], inc=False)
                    x.op("pe", lambda e: e.matmul(ps[:, 0:128], c.ident[:], c.negT[:],
                                                  start=False, stop=True),
                         r=[c.ident.d, c.negT.d], w=[ps.d], inc=(N == 128))
                    if N > 128:
                        x.op("pe", lambda e: e.matmul(ps[:, 128:N], lhs, qt[hi][pr_, q0 + 128:q0 + N],
                                                      start=True, stop=True),
                             r=[kt[hi].d, qt[hi].d], w=[ps.d], inc=True)
                else:
                    x.op("pe", lambda e: e.matmul(ps[:, 0:N], lhs, qt[hi][pr_, q0:q0 + N],
                                                  start=True, stop=True),
                         r=[kt[hi].d, qt[hi].d], w=[ps.d], inc=True)
                x.op("act", lambda e: e.activation(out=ptt[:, 0:N], in_=ps[:, 0:N], func=AF.Exp,
                                                   scale=0.125, bias=negC[:, 0:1]),
                     r=[ps.d, negC.d], w=[ptt.d])

            def pv(idx):
                qc, kb, comp = steps[idx]
                jmin = max(0, kb - 4 * qc)
                ptt = pt[idx % 4]
                for j in range(jmin, 4):
                    last = (kb == 4 * qc + j)
                    x.op("pe", lambda e: e.matmul(pso[:, comp * 4 + j, 0:129],
                                                  ptt[:, (j - jmin) * 128:(j - jmin + 1) * 128],
                                                  va[hi][:, kb, :], start=False,
                                                  stop=(last and j % 2 == 1)),
                         r=[ptt.d, va[hi].d], w=[pso.d], inc=(j == 3))

            def epilogue(qc):
                nonlocal nq
                x.op("dve", lambda e: e.reciprocal(out=R[:], in_=pso[:, :, 128:129].rearrange("p a b -> p (a b)")),
                     r=[pso.d], w=[R.d])
                x.op("dve", lambda e: e.tensor_scalar(out=R[:, 4:8], in0=R[:, 4:8], scalar1=neglam[:, 0:1],
                                                      scalar2=None, op0=ALU.mult), r=[R.d, neglam.d], w=[R.d])
                for j in range(4):
                    tm = tmp[j % 2]
                    x.op("act", lambda e: e.activation(out=tm[:], in_=pso[:, 4 + j, 0:128], func=AF.Copy,
                                                       scale=R[:, 4 + j:5 + j]), r=[pso.d, R.d], w=[tm.d])
                    x.op("dve", lambda e: e.scalar_tensor_tensor(out=of[:, j, :], in0=pso[:, j, 0:128],
                                                                 scalar=R[:, j:j + 1], in1=tm[:],
                                                                 op0=ALU.mult, op1=ALU.add),
                         r=[pso.d, R.d, tm.d], mw=[of.d])
                ofl = of[:].rearrange("p a b -> p (a b)")
                x.op("act", lambda e: e.activation(out=sq[:], in_=ofl, func=AF.Square), r=[of.d], w=[sq.d])
                x.op("dve", lambda e: e.tensor_reduce(out=ssq[:], in_=sq[:].rearrange("p (a b) -> p a b", a=4),
                                                      axis=AX.X, op=ALU.add), r=[sq.d], w=[ssq.d])
                rsqrt_mean(x, c, rs, ssq, 128, 4)
                x.op("dve", lambda e: e.tensor_tensor(out=of[:], in0=of[:], in1=bc(rs[:].unsqueeze(2), [128, 4, 128]),
                                                      op=ALU.mult), r=[of.d, rs.d], w=[of.d])
                obb = ob[nq % 2]
                nq += 1
                x.op("pool", lambda e: e.tensor_tensor(out=obb[:], in0=of[:], in1=bc(gs[:].unsqueeze(1), [128, 4, 128]),
                                                       op=ALU.mult), r=[of.d, gs.d], w=[obb.d])
                x.dma("sp", o[qc * 512:(qc + 1) * 512, h * 128:(h + 1) * 128].rearrange("(j p) d -> p j d", p=128),
                      obb[:], r=[obb.d], mw=[d_o])

            scores(0)
            for idx, (qc, kb, comp) in enumerate(steps):
                if kb == 0 and comp == 0:
                    for bnk in range(4):
                        x.op("pe", lambda e: e.matmul(psob[:, bnk * 512:(bnk + 1) * 512], zr[:, 0:128], zr[:],
                                                      start=True, stop=False), r=[zr.d], w=[pso.d], inc=False)
                if idx + 1 < len(steps):
                    scores(idx + 1)
                pv(idx)
                if kb == 4 * qc + 3 and comp == 1:
                    epilogue(qc)


def emit_outproj(x, c, o_ap, d_o, FO, wout_ap, x_ap, d_x, gate_b, xmid_ap, d_xmid):
    KO = FO // 128
    TB = 1024
    with Scope(x) as ls:
        oT = sbt(x, [128, KO, TB], BF16, "oT", ls)
        oT_d = [x.mkdep("oT%d" % t) for t in range(TB // 128)]
        ls.deps.extend(oT_d)
        ot = [sbt(x, [128, FO], BF16, "ot%d" % i, ls) for i in range(2)]
        pt = [pst(x, [128, 512], BF16, "pto%d" % i, ls) for i in range(2)]
        wb = [sbt(x, [128, KO, 512], BF16, "wo%d" % i, ls) for i in range(2)]
        pp = [pst(x, [128, 512], F32, "ppo%d" % i, ls) for i in range(2)]
        tm = [sbt(x, [128, 512], F32, "tmo%d" % i, ls) for i in range(2)]
        xt = [sbt(x, [128, 512], F32, "xto%d" % i, ls) for i in range(2)]
        n = 0
        nw = 0
        for tb in range(TOK // TB):
            for t in range(TB // 128):
                i = t % 2
                tok0 = tb * TB + t * 128
                x.dma("sp", ot[i][:], o_ap[tok0:tok0 + 128, :], r=[d_o], w=[ot[i].d])
                for g in range(KO // 4):
                    p = pt[g % 2]
                    for j in range(4):
                        kc = g * 4 + j
                        x.op("pe", lambda e: e.transpose(p[:, j * 128:(j + 1) * 128], ot[i][:, kc * 128:(kc + 1) * 128],
                                                         c.ident[:]), r=[ot[i].d, c.ident.d], w=[p.d], inc=(j == 3))
                    dst = oT[:, g * 4:(g + 1) * 4, t * 128:(t + 1) * 128]
                    src = p[:, :].rearrange("p (a b) -> p a b", a=4)
                    if g % 2 == 0:
                        x.op("dve", lambda e: e.tensor_copy(dst, src), r=[p.d], mw=[oT_d[t]])
                    else:
                        x.op("act", lambda e: e.activation(out=dst, in_=src, func=AF.Copy), r=[p.d], mw=[oT_d[t]])
            for cb in range(4):
                wt = wb[nw % 2]
                nw += 1
                x.dma("pool", wt[:], wout_ap[:, cb * 512:(cb + 1) * 512].rearrange("(k p) f -> p k f", p=128), w=[wt.d])
                for t in range(TB // 128):
                    tok0 = tb * TB + t * 128
                    ps = pp[n % 2]
                    tmm = tm[n % 2]
                    xtt = xt[n % 2]
                    n += 1
                    x.dma("sp", xtt[:], x_ap[tok0:tok0 + 128, cb * 512:(cb + 1) * 512], r=[d_x], w=[xtt.d])
                    for kc in range(KO):
                        x.op("pe", lambda e: e.matmul(ps[:], oT[:, kc, t * 128:(t + 1) * 128], wt[:, kc, :],
                                                      start=(kc == 0), stop=(kc == KO - 1)),
                             r=[oT_d[t], wt.d], w=[ps.d], inc=(kc == KO - 1))
                    x.op("dve", lambda e: e.tensor_tensor(out=tmm[:], in0=ps[:], in1=gate_b[:, cb * 512:(cb + 1) * 512],
                                                          op=ALU.mult), r=[ps.d, gate_b.d], w=[tmm.d])
                    x.op("pool", lambda e: e.tensor_tensor(out=tmm[:], in0=tmm[:], in1=xtt[:], op=ALU.add),
                         r=[tmm.d, xtt.d], w=[tmm.d])
                    x.dma("sp", xmid_ap[tok0:tok0 + 128, cb * 512:(cb + 1) * 512], tmm[:], r=[tmm.d], mw=[d_xmid])


def emit_ffn(x, c, xmid_ap, d_xmid, sT, shT, gate_b, wgu_ap, wd_ap, xout_ap, d_xout):
    TB = 512
    NFC = FFN // 128
    with Scope(x) as ls:
        h2T = sbt(x, [128, 16, TB], BF16, "h2T", ls)
        h2_d = [x.mkdep("h2T%d" % t) for t in range(4)]
        ls.deps.extend(h2_d)
        actT = sbt(x, [128, NFC, TB], BF16, "actT", ls)
        act_d = [x.mkdep("act%d" % f) for f in range(NFC)]
        ls.deps.extend(act_d)
        wg = [sbt(x, [128, 16, 256], BF16, "wg%d" % i, ls) for i in range(2)]
        wu = [sbt(x, [128, 16, 256], BF16, "wu%d" % i, ls) for i in range(2)]
        wdA = sbt(x, [128, 22, 512], BF16, "wdA", ls)
        wdB = sbt(x, [128, 22, 512], BF16, "wdB", ls)
        psg = [pst(x, [128, 512], F32, "psg%d" % i, ls) for i in range(2)]
        psu = [pst(x, [128, 512], F32, "psu%d" % i, ls) for i in range(2)]
        psd = [pst(x, [128, 512], F32, "psd%d" % i, ls) for i in range(2)]
        sg = [sbt(x, [128, 512], F32, "sg%d" % i, ls) for i in range(2)]
        tm = [sbt(x, [128, 512], F32, "tmf%d" % i, ls) for i in range(2)]
        xt = [sbt(x, [128, 512], F32, "xtf%d" % i, ls) for i in range(2)]
        nblk = 0
        n = 0
        nd = 0
        for tb in range(TOK // TB):
            emit_norm_hT(x, c, xmid_ap[tb * TB:(tb + 1) * TB, :], d_xmid, 4, sT, shT, h2T, h2_d)
            for blk in range(FFN // 256):
                wgt = wg[nblk % 2]
                wut = wu[nblk % 2]
                nblk += 1
                x.dma("pool", wgt[:], wgu_ap[:, blk * 256:(blk + 1) * 256].rearrange("(k p) f -> p k f", p=128),
                      w=[wgt.d])
                x.dma("pool", wut[:], wgu_ap[:, FFN + blk * 256:FFN + (blk + 1) * 256].rearrange("(k p) f -> p k f", p=128),
                      w=[wut.d])
                for fl in range(2):
                    fc = blk * 2 + fl
                    pg = psg[n % 2]
                    pu = psu[n % 2]
                    sgg = sg[n % 2]
                    n += 1
                    for kc in range(16):
                        x.op("pe", lambda e: e.matmul(pg[:], wgt[:, kc, fl * 128:(fl + 1) * 128], h2T[:, kc, :],
                                                      start=(kc == 0), stop=(kc == 15)),
                             r=h2_d + [wgt.d], w=[pg.d], inc=(kc == 15))
                    for kc in range(16):
                        x.op("pe", lambda e: e.matmul(pu[:], wut[:, kc, fl * 128:(fl + 1) * 128], h2T[:, kc, :],
                                                      start=(kc == 0), stop=(kc == 15)),
                             r=h2_d + [wut.d], w=[pu.d], inc=(kc == 15))
                    x.op("act", lambda e: e.activation(out=sgg[:], in_=pg[:], func=AF.Silu), r=[pg.d], w=[sgg.d])
                    x.op("dve", lambda e: e.tensor_tensor(out=actT[:, fc, :], in0=pu[:], in1=sgg[:], op=ALU.mult),
                         r=[pu.d, sgg.d], w=[act_d[fc]])
            for cb in range(4):
                x.dma("pool", wdA[:], wd_ap[0:22 * 128, cb * 512:(cb + 1) * 512].rearrange("(k p) f -> p k f", p=128),
                      w=[wdA.d])
                x.dma("pool", wdB[:], wd_ap[22 * 128:44 * 128, cb * 512:(cb + 1) * 512].rearrange("(k p) f -> p k f", p=128),
                      w=[wdB.d])
                for t in range(4):
                    tok0 = tb * TB + t * 128
                    ps = psd[nd % 2]
                    tmm = tm[nd % 2]
                    xtt = xt[nd % 2]
                    nd += 1
                    x.dma("sp", xtt[:], xmid_ap[tok0:tok0 + 128, cb * 512:(cb + 1) * 512], r=[d_xmid], w=[xtt.d])
                    for fc in range(NFC):
                        wt = wdA if fc < 22 else wdB
                        x.op("pe", lambda e: e.matmul(ps[:], actT[:, fc, t * 128:(t + 1) * 128], wt[:, fc % 22, :],
                                                      start=(fc == 0), stop=(fc == NFC - 1)),
                             r=[act_d[fc], wt.d], w=[ps.d], inc=(fc == NFC - 1 or fc == 21))
                    x.op("dve", lambda e: e.tensor_tensor(out=tmm[:], in0=ps[:], in1=gate_b[:, cb * 512:(cb + 1) * 512],
                                                          op=ALU.mult), r=[ps.d, gate_b.d], w=[tmm.d])
                    x.op("pool", lambda e: e.tensor_tensor(out=tmm[:], in0=tmm[:], in1=xtt[:], op=ALU.add),
                         r=[tmm.d, xtt.d], w=[tmm.d])
                    x.dma("sp", xout_ap[tok0:tok0 + 128, cb * 512:(cb + 1) * 512], tmm[:], r=[tmm.d], mw=[d_xout])


def emit_LC(x, c, dram, FO):
    x_in = dram("x_in", [TOK, D], F32, "ExternalInput")
    o_in = dram("o_in", [TOK, FO], BF16, "ExternalInput")
    ada = dram("ada", [6 * D], F32, "ExternalInput")
    g2 = dram("g2", [D], F32, "ExternalInput")
    w_out = dram("w_out", [FO, D], F32, "ExternalInput")
    wgu = dram("wgu", [D, 2 * FFN], F32, "ExternalInput")
    wd = dram("wd", [FFN, D], F32, "ExternalInput")
    x_mid = dram("x_mid", [TOK, D], F32, "Internal")
    x_out = dram("x_out", [TOK, D], F32, "ExternalOutput")
    with Scope(x) as st:
        d_none = x.mkdep("in")
        d_xmid = x.mkdep("xmid")
        d_xout = x.mkdep("xout")
        with Scope(x) as ls:
            g1b = load_bcast(x, ls, ada[2 * D:3 * D], D, "g1b")
            emit_outproj(x, c, o_in, d_none, FO, w_out, x_in, d_none, g1b, x_mid, d_xmid)
        with Scope(x) as ls:
            g2b = load_bcast(x, ls, ada[5 * D:6 * D], D, "g2b")
            sT, shT = emit_mod_cols(x, ls, g2, ada, d_none, 4, 3)
            emit_ffn(x, c, x_mid, d_xmid, sT, shT, g2b, wgu, wd, x_out, d_xout)


def la_handlers_ssd(x, c, ls, dram, outs_holder, dm=False):
    z = dram("z", [2, TOK, 2048] if dm else [TOK, 4096], BF16, "ExternalOutput")
    xbcT = dram("xbcT", [2, 3072, TOK] if dm else [6144, TOK], BF16, "ExternalOutput")
    dtr = dram("dtr", [2, TOK, 32] if dm else [TOK, 64], F32, "ExternalOutput")
    outs = [x.mkdep("z"), x.mkdep("xbcT"), x.mkdep("dtr")]
    outs_holder.extend(outs)
    zst = [sbt(x, [128, 512], BF16, "zst%d" % i, ls) for i in range(2)]
    fst = [sbt(x, [128, 512], BF16, "fst%d" % i, ls) for i in range(2)]
    dst_ = [sbt(x, [128, 64], F32, "dst%d" % i, ls) for i in range(2)]
    cnt = [0]
    blocks = []
    for zb in range(8):
        def zh(t, ps, zb=zb):
            s_ = zst[cnt[0] % 2]
            cnt[0] += 1
            x.op("act", lambda e: e.activation(out=s_[:], in_=ps[:, :], func=AF.Copy), r=[ps.d], w=[s_.d])
            zdst = (z[zb // 4, t * 128:(t + 1) * 128, (zb % 4) * 512:(zb % 4 + 1) * 512] if dm
                    else z[t * 128:(t + 1) * 128, zb * 512:(zb + 1) * 512])
            x.dma("sp", zdst, s_[:], r=[s_.d], mw=[outs[0]])
        blocks.append((zb * 512, 512, "tok", zh))
    for xb in range(12):
        def xh(fc, tb, ps, xb=xb):
            s_ = fst[cnt[0] % 2]
            cnt[0] += 1
            x.op("act", lambda e: e.activation(out=s_[:], in_=ps[:, :], func=AF.Copy), r=[ps.d], w=[s_.d])
            if dm:
                dd, rb = ((xb // 4, (xb % 4) * 512) if xb < 8 else ((xb - 8) % 2, 2048 + ((xb - 8) // 2) * 512))
                xdst = xbcT[dd, rb + fc * 128:rb + fc * 128 + 128, tb * 512:(tb + 1) * 512]
            else:
                r0 = xb * 512 + fc * 128
                xdst = xbcT[r0:r0 + 128, tb * 512:(tb + 1) * 512]
            x.dma("sp", xdst, s_[:], r=[s_.d], mw=[outs[1]])
        blocks.append((4096 + xb * 512, 512, "feat", xh))

    def dh(t, ps):
        s_ = dst_[cnt[0] % 2]
        cnt[0] += 1
        x.op("act", lambda e: e.activation(out=s_[:], in_=ps[:, 0:64], func=AF.Copy), r=[ps.d], w=[s_.d])
        if dm:
            for dd in range(2):
                x.dma("sp", dtr[dd, t * 128:(t + 1) * 128, :], s_[:, dd * 32:(dd + 1) * 32], r=[s_.d], mw=[outs[2]])
        else:
            x.dma("sp", dtr[t * 128:(t + 1) * 128, :], s_[:], r=[s_.d], mw=[outs[2]])
    blocks.append((10240, 64, "tok", dh))
    return blocks


def emit_LB_SSD(x, c, dram):
    NHh = 32
    NCH = 24
    raw = dram("raw", [NCH * 128, S], F32, "ExternalInput")
    convw = dram("convw", [4, NCH * 128], F32, "ExternalInput")
    convb = dram("convb", [NCH * 128], F32, "ExternalInput")
    dtr = dram("dtr", [S, NHh], F32, "ExternalInput")
    hp = dram("hp", [3, NHh], F32, "ExternalInput")
    z = dram("z", [S, 2048], BF16, "ExternalInput")
    ng = dram("ng", [2048], F32, "ExternalInput")
    tokd = dram("tokd", [S, 2560], BF16, "Internal")
    featd = dram("featd", [1024, S], BF16, "Internal")
    y = dram("y", [S, 2048], BF16, "ExternalOutput")
    with Scope(x) as st:
        d_in = x.mkdep("in")
        d_tok = x.mkdep("tokd")
        d_feat = x.mkdep("featd")
        d_y = x.mkdep("y")
        with Scope(x) as ls:
            cw = sbt(x, [128, 4, NCH], F32, "cw", ls)
            cb_ = sbt(x, [128, NCH], F32, "cb", ls)
            for j in range(4):
                x.dma("sp", cw[:, j, :], convw[j].rearrange("(k p) -> p k", p=128), mw=[cw.d],
                      allow_slow_non_contiguous=True)
            x.dma("sp", cb_[:], convb.rearrange("(k p) -> p k", p=128), w=[cb_.d], allow_slow_non_contiguous=True)
            rw = [sbt(x, [128, S + 3], F32, "rw%d" % i, ls) for i in range(2)]
            for i in range(2):
                x.op("pool", lambda e: e.memset(rw[i][:, 0:3], 0.0), w=[rw[i].d])
            acc = sbt(x, [128, S], F32, "acc", ls)
            sil = [sbt(x, [128, S], BF16, "sil%d" % i, ls) for i in range(2)]
            ptc = [pst(x, [128, 512], BF16, "ptc%d" % i, ls) for i in range(2)]
            stg = [sbt(x, [128, 4, 128], BF16, "stg%d" % i, ls) for i in range(2)]
            n = 0
            for cc in range(NCH):
                r_ = rw[cc % 2]
                sl_ = sil[cc % 2]
                x.dma("sp", r_[:, 3:3 + S], raw[cc * 128:(cc + 1) * 128, :], r=[d_in], mw=[r_.d])
                x.op("dve", lambda e: e.tensor_scalar(out=acc[:], in0=r_[:, 3:3 + S], scalar1=cw[:, 3, cc:cc + 1],
                                                      scalar2=cb_[:, cc:cc + 1], op0=ALU.mult, op1=ALU.add),
                     r=[r_.d, cw.d, cb_.d], w=[acc.d])
                for j in range(3):
                    x.op("dve", lambda e: e.scalar_tensor_tensor(out=acc[:], in0=r_[:, j:j + S],
                                                                 scalar=cw[:, j, cc:cc + 1], in1=acc[:],
                                                                 op0=ALU.mult, op1=ALU.add),
                         r=[r_.d, cw.d, acc.d], w=[acc.d])
                x.op("act", lambda e: e.activation(out=sl_[:], in_=acc[:], func=AF.Silu), r=[acc.d], w=[sl_.d])
                if cc >= 16:
                    x.dma("sp", featd[(cc - 16) * 128:(cc - 15) * 128, :], sl_[:], r=[sl_.d], mw=[d_feat])
                if cc < 20:
                    for tg in range(8):
                        p = ptc[n % 2]
                        sg_ = stg[n % 2]
                        n += 1
                        for j in range(4):
                            tt = tg * 4 + j
                            x.op("pe", lambda e: e.transpose(p[:, j * 128:(j + 1) * 128], sl_[:, tt * 128:(tt + 1) * 128],
                                                             c.ident[:]), r=[sl_.d, c.ident.d], w=[p.d], inc=(j == 3))
                        if n % 2 == 0:
                            x.op("dve", lambda e: e.tensor_copy(sg_[:], p[:, :].rearrange("p (a b) -> p a b", a=4)),
                                 r=[p.d], w=[sg_.d])
                        else:
                            x.op("act", lambda e: e.activation(out=sg_[:], in_=p[:, :].rearrange("p (a b) -> p a b", a=4),
                                                               func=AF.Copy), r=[p.d], w=[sg_.d])
                        x.dma("sp", tokd[tg * 512:(tg + 1) * 512, cc * 128:(cc + 1) * 128].rearrange("(j p) c -> p j c", p=128),
                              sg_[:], r=[sg_.d], mw=[d_tok])
        with Scope(x) as ls:
            hb = [load_bcast(x, ls, hp[i], NHh, "hp%d" % i) for i in range(3)]
            dtb_b, alog_b, dsk_b = hb
            a_b = sbt(x, [128, NHh], F32, "a_b", ls)
            x.op("act", lambda e: e.activation(out=a_b[:], in_=alog_b[:], func=AF.Exp), r=[alog_b.d], w=[a_b.d])
            x.op("dve", lambda e: e.tensor_scalar(out=a_b[:], in0=a_b[:], scalar1=-1.0, scalar2=None, op0=ALU.mult),
                 r=[a_b.d], w=[a_b.d])
            ngb = load_bcast(x, ls, ng, 2048, "ngb")
            sel = sbt(x, [32, NHh, 128], F32, "sel", ls)
            x.op("pool", lambda e: e.memset(sel[:], 1.0), w=[sel.d])
            x.op("pool", lambda e: e.affine_select(out=sel[:], in_=sel[:], pattern=[[-1, NHh], [0, 128]],
                                                   compare_op=ALU.is_equal, fill=0.0, base=0, channel_multiplier=1),
                 r=[sel.d], w=[sel.d])
            St = [sbt(x, [128, 512], F32, "St%d" % g, ls) for g in range(4)]
            Sb = [sbt(x, [128, 512], BF16, "Sb%d" % g, ls) for g in range(4)]
            for g in range(4):
                x.op("pool", lambda e: e.memset(St[g][:], 0.0), w=[St[g].d])
                x.op("pool", lambda e: e.memset(Sb[g][:], 0.0), w=[Sb[g].d])
            xs_t = [sbt(x, [128, 2560], BF16, "xs_t%d" % i, ls) for i in range(2)]
            bct = [sbt(x, [128, 8, 128], BF16, "bct%d" % i, ls) for i in range(2)]
            dtt = [sbt(x, [128, NHh], F32, "dtt%d" % i, ls) for i in range(2)]
            zt = [sbt(x, [128, 2048], BF16, "zt%d" % i, ls) for i in range(2)]
            f = lambda nm, w_: sbt(x, [128, w_], F32, nm, ls)
            dtb, ab, ee, dt_, dta, acum, nacum, alast, eac, wend, decay = [f(nm, NHh) for nm in
                ("dtb", "ab", "ee", "dt_", "dta", "acum", "nacum", "alast", "eac", "wend", "decay")]
            acT = sbt(x, [32, 128], F32, "acT", ls)
            xdt = sbt(x, [128, 2048], BF16, "xdt", ls)
            xde = sbt(x, [128, 2048], BF16, "xde", ls)
            cbm = [sbt(x, [128, 128], F32, "cbm%d" % g, ls) for g in range(4)]
            Eh = [sbt(x, [128, 128], F32, "Eh%d" % i, ls) for i in range(4)]
            Mh = [sbt(x, [128, 128], BF16, "Mh%d" % i, ls) for i in range(4)]
            yf = sbt(x, [128, 2048], F32, "yf", ls)
            t1 = sbt(x, [128, 2048], F32, "t1", ls)
            sqy = sbt(x, [128, 2048], F32, "sqy", ls)
            ssy = sbt(x, [128, 4], F32, "ssy", ls)
            rsy = sbt(x, [128, 4], F32, "rsy", ls)
            yb = [sbt(x, [128, 2048], BF16, "yb%d" % i, ls) for i in range(2)]
            p_small = pst(x, [128, 512], F32, "p_small", ls)
            p_cb = pst(x, [128, 512], F32, "p_cb", ls)
            p_G = [pst(x, [128, 512], F32, "p_G%d" % i, ls) for i in range(2)]
            p_y = [pst(x, [128, 512], F32, "p_y%d" % i, ls) for i in range(2)]
            p_i = pst(x, [128, 512], F32, "p_i", ls)
            p_s = pst(x, [128, 512], F32, "p_s", ls)
            for ck in range(S // 128):
                i = ck % 2
                t0 = ck * 128
                xt_ = xs_t[i]
                bc_ = bct[i]
                x.dma("sp", xt_[:], tokd[t0:t0 + 128, :], r=[d_tok], w=[xt_.d])
                x.dma("sp", bc_[:], featd[:, t0:t0 + 128].rearrange("(g p) t -> p g t", p=128), r=[d_feat], w=[bc_.d])
                x.dma("sp", dtt[i][:], dtr[t0:t0 + 128, :], r=[d_in], w=[dtt[i].d])
                x.dma("sp", zt[i][:], z[t0:t0 + 128, :], r=[d_in], w=[zt[i].d])
                x.op("dve", lambda e: e.tensor_tensor(out=dtb[:], in0=dtt[i][:], in1=dtb_b[:], op=ALU.add),
                     r=[dtt[i].d, dtb_b.d], w=[dtb.d])
                x.op("dve", lambda e: e.scalar_tensor_tensor(out=ab[:], in0=dtb[:], scalar=-1.0, in1=dtb[:],
                                                             op0=ALU.mult, op1=ALU.max), r=[dtb.d], w=[ab.d])
                x.op("act", lambda e: e.activation(out=ee[:], in_=ab[:], func=AF.Exp, scale=-1.0), r=[ab.d], w=[ee.d])
                x.op("act", lambda e: e.activation(out=ee[:], in_=ee[:], func=AF.Ln, bias=1.0), r=[ee.d], w=[ee.d])
                x.op("dve", lambda e: e.scalar_tensor_tensor(out=dt_[:], in0=dtb[:], scalar=0.0, in1=ee[:],
                                                             op0=ALU.max, op1=ALU.add), r=[dtb.d, ee.d], w=[dt_.d])
                x.op("dve", lambda e: e.tensor_tensor(out=dta[:], in0=dt_[:], in1=a_b[:], op=ALU.mult),
                     r=[dt_.d, a_b.d], w=[dta.d])
                x.op("pe", lambda e: e.matmul(p_small[:, 0:32], c.trif[:], dta[:], start=True, stop=True),
                     r=[c.trif.d, dta.d], w=[p_small.d])
                x.op("pe", lambda e: e.matmul(p_small[:, 32:64], c.onesf[:], dta[:], start=True, stop=True),
                     r=[c.onesf.d, dta.d], w=[p_small.d])
                x.op("dve", lambda e: e.tensor_copy(acum[:], p_small[:, 0:32]), r=[p_small.d], w=[acum.d])
                x.op("dve", lambda e: e.tensor_scalar(out=nacum[:], in0=p_small[:, 0:32], scalar1=-1.0, scalar2=None,
                                                      op0=ALU.mult), r=[p_small.d], w=[nacum.d])
                x.op("dve", lambda e: e.tensor_copy(alast[:], p_small[:, 32:64]), r=[p_small.d], w=[alast.d])
                x.op("act", lambda e: e.activation(out=eac[:], in_=acum[:], func=AF.Exp), r=[acum.d], w=[eac.d])
                x.op("act", lambda e: e.activation(out=decay[:], in_=alast[:], func=AF.Exp), r=[alast.d], w=[decay.d])
                x.op("dve", lambda e: e.tensor_tensor(out=wend[:], in0=alast[:], in1=acum[:], op=ALU.subtract),
                     r=[alast.d, acum.d], w=[wend.d])
                x.op("act", lambda e: e.activation(out=wend[:], in_=wend[:], func=AF.Exp), r=[wend.d], w=[wend.d])
                x.op("dve", lambda e: e.tensor_tensor(out=wend[:], in0=wend[:], in1=dt_[:], op=ALU.mult),
                     r=[wend.d, dt_.d], w=[wend.d])
                x.op("pe", lambda e: e.matmul(p_small[0:32, 128:256], acum[:], c.identf[:], start=True, stop=True),
                     r=[acum.d, c.identf.d], w=[p_small.d])
                x.op("dve", lambda e: e.tensor_copy(acT[:], p_small[0:32, 128:256]), r=[p_small.d], w=[acT.d])
                xv = xt_[:, 0:2048].rearrange("p (h d) -> p h d", d=64)
                x.op("dve", lambda e: e.tensor_tensor(out=xdt[:].rearrange("p (h d) -> p h d", d=64), in0=xv,
                                                      in1=bc(dt_[:].unsqueeze(2), [128, NHh, 64]), op=ALU.mult),
                     r=[xt_.d, dt_.d], w=[xdt.d])
                x.op("pool", lambda e: e.tensor_tensor(out=xde[:].rearrange("p (h d) -> p h d", d=64), in0=xv,
                                                       in1=bc(wend[:].unsqueeze(2), [128, NHh, 64]), op=ALU.mult),
                     r=[xt_.d, wend.d], w=[xde.d])
                for g in range(4):
                    x.op("pe", lambda e: e.matmul(p_cb[:, g * 128:(g + 1) * 128], bc_[:, g, :], bc_[:, 4 + g, :],
                                                  start=True, stop=True), r=[bc_.d], w=[p_cb.d], inc=(g == 3))
                for g in range(4):
                    if g % 2 == 0:
                        x.op("dve", lambda e: e.tensor_copy(cbm[g][:], p_cb[:, g * 128:(g + 1) * 128]),
                             r=[p_cb.d], w=[cbm[g].d])
                    else:
                        x.op("act", lambda e: e.activation(out=cbm[g][:], in_=p_cb[:, g * 128:(g + 1) * 128], func=AF.Copy),
                             r=[p_cb.d], w=[cbm[g].d])
                for g in range(4):
                    py = p_y[g % 2]
                    x.op("pe", lambda e: e.matmul(p_i[:], bc_[:, 4 + g, :], Sb[g][:], start=True, stop=True),
                         r=[bc_.d, Sb[g].d], w=[p_i.d])
                    def hG(hl):
                        h = g * 8 + hl
                        pgt = p_G[(h % 4) // 2]
                        pgs = slice((h % 2) * 128, (h % 2) * 128 + 128)
                        x.op("pe", lambda e: e.matmul(pgt[:, pgs], sel[:, h, :], acT[:], start=True, stop=False),
                             r=[sel.d, acT.d], w=[pgt.d], inc=False)
                        x.op("pe", lambda e: e.matmul(pgt[:, pgs], c.ident[:], c.negT[:], start=False, stop=True),
                             r=[c.ident.d, c.negT.d], w=[pgt.d])
                        eh = Eh[h % 4]
                        mh = Mh[h % 4]
                        x.op("act", lambda e: e.activation(out=eh[:], in_=pgt[:, pgs], func=AF.Exp,
                                                           bias=nacum[:, h:h + 1]), r=[pgt.d, nacum.d], w=[eh.d])
                        x.op("dve", lambda e: e.tensor_tensor(out=mh[:], in0=eh[:], in1=cbm[g][:], op=ALU.mult),
                             r=[eh.d, cbm[g].d], w=[mh.d])

                    def hY(hl):
                        h = g * 8 + hl
                        mh = Mh[h % 4]
                        x.op("pe", lambda e: e.matmul(py[:, hl * 64:(hl + 1) * 64], mh[:], xdt[:, h * 64:(h + 1) * 64],
                                                      start=True, stop=True), r=[mh.d, xdt.d], w=[py.d])

                    hG(0)
                    hG(1)
                    for hl in range(8):
                        if hl + 2 < 8:
                            hG(hl + 2)
                        hY(hl)
                    gs_ = slice(g * 512, (g + 1) * 512)
                    x.op("dve", lambda e: e.tensor_tensor(out=t1[:, gs_].rearrange("p (h d) -> p h d", d=64),
                                                          in0=p_i[:].rearrange("p (h d) -> p h d", d=64),
                                                          in1=bc(eac[:, g * 8:(g + 1) * 8].unsqueeze(2), [128, 8, 64]),
                                                          op=ALU.mult), r=[p_i.d, eac.d], mw=[t1.d])
                    x.op("dve", lambda e: e.tensor_tensor(out=yf[:, gs_], in0=py[:], in1=t1[:, gs_], op=ALU.add),
                         r=[py.d, t1.d], mw=[yf.d])
                    x.op("pe", lambda e: e.matmul(p_s[:], xt_[:, 2048 + g * 128:2048 + (g + 1) * 128], xde[:, gs_],
                                                  start=True, stop=True), r=[xt_.d, xde.d], w=[p_s.d])
                    x.op("pool", lambda e: e.tensor_tensor(out=St[g][:].rearrange("p (h d) -> p h d", d=64),
                                                           in0=St[g][:].rearrange("p (h d) -> p h d", d=64),
                                                           in1=bc(decay[:, g * 8:(g + 1) * 8].unsqueeze(2), [128, 8, 64]),
                                                           op=ALU.mult), r=[St[g].d, decay.d], w=[St[g].d])
                    x.op("dve", lambda e: e.tensor_tensor(out=St[g][:], in0=p_s[:], in1=St[g][:], op=ALU.add),
                         r=[p_s.d, St[g].d], w=[St[g].d])
                    x.op("act", lambda e: e.activation(out=Sb[g][:], in_=St[g][:], func=AF.Copy),
                         r=[St[g].d], w=[Sb[g].d])
                x.op("pool", lambda e: e.tensor_tensor(out=t1[:].rearrange("p (h d) -> p h d", d=64), in0=xv,
                                                       in1=bc(dsk_b[:].unsqueeze(2), [128, NHh, 64]), op=ALU.mult),
                     r=[xt_.d, dsk_b.d, yf.d], w=[t1.d])
                x.op("dve", lambda e: e.tensor_tensor(out=yf[:], in0=yf[:], in1=t1[:], op=ALU.add),
                     r=[yf.d, t1.d], w=[yf.d])
                x.op("act", lambda e: e.activation(out=t1[:], in_=zt[i][:], func=AF.Silu), r=[zt[i].d, yf.d], w=[t1.d])
                x.op("dve", lambda e: e.tensor_tensor(out=yf[:], in0=yf[:], in1=t1[:], op=ALU.mult),
                     r=[yf.d, t1.d], w=[yf.d])
                x.op("act", lambda e: e.activation(out=sqy[:], in_=yf[:], func=AF.Square), r=[yf.d], w=[sqy.d])
                x.op("dve", lambda e: e.tensor_reduce(out=ssy[:], in_=sqy[:].rearrange("p (g d) -> p g d", g=4),
                                                      axis=AX.X, op=ALU.add), r=[sqy.d], w=[ssy.d])
                rsqrt_mean(x, c, rsy, ssy, 512, 4)
                x.op("dve", lambda e: e.tensor_tensor(out=yf[:].rearrange("p (g d) -> p g d", g=4),
                                                      in0=yf[:].rearrange("p (g d) -> p g d", g=4),
                                                      in1=bc(rsy[:].unsqueeze(2), [128, 4, 512]), op=ALU.mult),
                     r=[yf.d, rsy.d], w=[yf.d])
                x.op("pool", lambda e: e.tensor_tensor(out=yb[i][:], in0=yf[:], in1=ngb[:], op=ALU.mult),
                     r=[yf.d, ngb.d], w=[yb[i].d])
                x.dma("sp", y[t0:t0 + 128, :], yb[i][:], r=[yb[i].d], mw=[d_y])


def la_handlers_dsa(x, c, ls, dram, outs_holder, pos, dm=False):
    invf16 = dram("invf16", [128, 16], F32, "ExternalInput")
    invf8 = dram("invf8", [128, 8], F32, "ExternalInput")
    gq = dram("gq", [128], F32, "ExternalInput")
    gk = dram("gk", [128], F32, "ExternalInput")
    gi = dram("gi", [64], F32, "ExternalInput")
    qT = dram("qT", [2, 16, 128, 8, 128] if dm else [16, 128, TOK], BF16, "ExternalOutput")
    kT = dram("kT", [4, 128, TOK], BF16, "ExternalOutput")
    v = dram("v", [TOK, 512], BF16, "ExternalOutput")
    qiT = dram("qiT", [2, 8, 128, 8, 128] if dm else [8, 128, TOK], BF16, "ExternalOutput")
    kiT = dram("kiT", [64, TOK], BF16, "ExternalOutput")
    wi = dram("wi", [2, 8, 128, 16] if dm else [TOK, 16], F32, "ExternalOutput")
    outs = [x.mkdep(n) for n in ("qT", "kT", "v", "qiT", "kiT", "wi")]
    outs_holder.extend(outs)
    cos16, sin16 = emit_rope_tables(x, ls, pos, invf16, 16, NT)
    cos8, sin8 = emit_rope_tables(x, ls, pos, invf8, 8, NT)
    gqb = load_bcast(x, ls, gq, 128, "gqb")
    gkb = load_bcast(x, ls, gk, 128, "gkb")
    gib = load_bcast(x, ls, gi, 64, "gib")
    qk_tiles = alloc_qk_tiles(x, ls)
    qbf = sbt(x, [128, 512], BF16, "qbf", ls)
    stage = [sbt(x, [128, 4, TOK], BF16, "stage%d" % i, ls) for i in range(2)]
    kist = sbt(x, [64, TOK], BF16, "kist", ls)
    ptr = [pst(x, [128, 512], BF16, "ptr%d" % i, ls) for i in range(2)]
    vst = [sbt(x, [128, 512], BF16, "vst%d" % i, ls) for i in range(2)]
    wst = [sbt(x, [128, 16], F32, "wst%d" % i, ls) for i in range(2)]
    cnt = [0]
    nst = [0]
    blocks = []

    def mk_qk(col0, gdim, gain, half, cos, sin, dst_ap, dep, dmh=None):
        sg = stage[nst[0] % 2]
        nst[0] += 1

        def handler(t, ps):
            emit_qk_post(x, c, qk_tiles, ps, 512, gdim, gain, half, cos, sin, t, qbf)
            p = ptr[cnt[0] % 2]
            cnt[0] += 1
            for j in range(4):
                x.op("pe", lambda e: e.transpose(p[:, j * 128:(j + 1) * 128], qbf[:, j * 128:(j + 1) * 128], c.ident[:]),
                     r=[qbf.d, c.ident.d], w=[p.d], inc=(j == 3))
            x.op("dve", lambda e: e.tensor_copy(sg[:, :, t * 128:(t + 1) * 128],
                                                p[:, :].rearrange("p (a b) -> p a b", a=4)), r=[p.d], mw=[sg.d])
            if t == NT - 1:
                if dmh is None:
                    x.dma("sp", dst_ap.rearrange("h p t -> p h t"), sg[:], r=[sg.d], mw=[dep])
                else:
                    tens, h0 = dmh
                    sgv = sg[:].rearrange("p h (k a t) -> p h k a t", a=2, t=128)
                    for a_ in range(2):
                        for hh_ in range(4):
                            x.dma("sp", tens[a_, h0 + hh_].rearrange("d k t -> d k t"), sgv[:, hh_, :, a_, :],
                                  r=[sg.d], mw=[dep])
        blocks.append((col0, 512, "tok", handler))
    for hb in range(4):
        mk_qk(hb * 512, 128, gqb, 16, cos16, sin16, None if dm else qT[hb * 4:(hb + 1) * 4], outs[0],
              (qT, hb * 4) if dm else None)
    mk_qk(2048, 128, gkb, 16, cos16, sin16, kT[0:4], outs[1])

    def vh(t, ps):
        vs = vst[cnt[0] % 2]
        cnt[0] += 1
        x.op("act", lambda e: e.activation(out=vs[:], in_=ps[:, :], func=AF.Copy), r=[ps.d], w=[vs.d])
        x.dma("sp", v[t * 128:(t + 1) * 128, :], vs[:], r=[vs.d], mw=[outs[2]])
    blocks.append((2560, 512, "tok", vh))
    for qb in range(2):
        mk_qk(3072 + qb * 512, 64, None, 8, cos8, sin8, None if dm else qiT[qb * 4:(qb + 1) * 4], outs[3],
              (qiT, qb * 4) if dm else None)

    def kwh(t, ps):
        emit_qk_post(x, c, qk_tiles, ps, 64, 64, gib, 8, cos8, sin8, t, qbf)
        p = ptr[cnt[0] % 2]
        ws = wst[cnt[0] % 2]
        cnt[0] += 1
        x.op("pe", lambda e: e.transpose(p[0:64, 0:128], qbf[:, 0:64], c.ident[:]), r=[qbf.d, c.ident.d], w=[p.d])
        x.op("dve", lambda e: e.tensor_copy(kist[:, t * 128:(t + 1) * 128], p[0:64, 0:128]), r=[p.d], mw=[kist.d])
        x.op("act", lambda e: e.activation(out=ws[:], in_=ps[:, 64:80], func=AF.Copy, scale=0.25), r=[ps.d], w=[ws.d])
        x.dma("sp", wi[t % 2, t // 2] if dm else wi[t * 128:(t + 1) * 128, :], ws[:], r=[ws.d], mw=[outs[5]])
        if t == NT - 1:
            x.dma("sp", kiT, kist[:], r=[kist.d], mw=[outs[4]])
    blocks.append((4096, 80, "tok", kwh))
    return blocks


def emit_LB_DSA(x, c, dram):
    NS = 16
    qTs = dram("qTs", [NS, 128, 2048], BF16, "ExternalInput")
    qiTs = dram("qiTs", [NS, 128, 1024], BF16, "ExternalInput")
    wis = dram("wis", [NS, 128, 16], F32, "ExternalInput")
    kT = dram("kT", [4, 128, S], BF16, "ExternalInput")
    v = dram("v", [S, 512], BF16, "ExternalInput")
    kiT2 = dram("kiT2", [128, S], BF16, "ExternalInput")
    dmask = dram("dmask", [2, 128, 128], F32, "ExternalInput")
    gqk = dram("gqk", [2, 128], F32, "ExternalInput")
    o = dram("o", [NS * 128, 2048], BF16, "ExternalOutput")
    SCALE = 128 ** -0.5
    with Scope(x) as st:
        d_in = x.mkdep("in")
        d_o = x.mkdep("o")
        ls = st
        gt = [load_bcast(x, ls, gqk[i], 128, "gqk%d" % i) for i in range(2)]
        gm = sbt(x, [128, 2], F32, "gm")
        negC = sbt(x, [128, 1], F32, "negC")
        for i in range(2):
            x.op("dve", lambda e: e.tensor_reduce(out=gm[:, i:i + 1], in_=gt[i][:], axis=AX.X, op=ALU.max,
                                                  apply_absolute_value=True), r=[gt[i].d], w=[gm.d])
        x.op("dve", lambda e: e.scalar_tensor_tensor(out=negC[:], in0=gm[:, 0:1], scalar=-(128 ** 0.5), in1=gm[:, 1:2],
                                                     op0=ALU.mult, op1=ALU.mult), r=[gm.d], w=[negC.d])
        kts = sbt(x, [128, 4, S], BF16, "kts")
        x.dma("sp", kts[:], kT.rearrange("g p t -> p g t"), w=[kts.d])
        va = sbt(x, [128, 32, 4, 129], BF16, "va")
        x.op("pool", lambda e: e.memset(va[:, :, :, 128:129], 1.0), w=[va.d])
        for g in range(4):
            x.dma("sp", va[:, :, g, 0:128], v[:, g * 128:(g + 1) * 128].rearrange("(kb p) d -> p kb d", p=128),
                  mw=[va.d])
        ki2 = sbt(x, [128, S], BF16, "ki2")
        x.dma("sp", ki2[:], kiT2, w=[ki2.d])
        dm = sbt(x, [128, 2, 128], F32, "dm")
        x.dma("sp", dm[:], dmask.rearrange("a p k -> p a k"), w=[dm.d])
        zr = sbt(x, [128, 512], BF16, "zr")
        x.op("pool", lambda e: e.memset(zr[:], 0.0), w=[zr.d])
        acc = sbt(x, [128, S], F32, "acc")
        work = sbt(x, [128, S], F32, "work")
        nb = sbt(x, [128, S], BF16, "nb")
        nbT4 = sbt(x, [128, 32, 4, 128], BF16, "nbT4")
        qs = [sbt(x, [128, 2048], BF16, "qs%d" % i) for i in range(2)]
        qis = [sbt(x, [128, 8, 128], BF16, "qis%d" % i) for i in range(2)]
        wt = [sbt(x, [128, 16], F32, "wt%d" % i) for i in range(2)]
        aw = sbt(x, [128, 16], F32, "aw")
        sgn = sbt(x, [128, 16], F32, "sgn")
        rr = [sbt(x, [128, 512], F32, "rr%d" % i) for i in range(2)]
        m8 = sbt(x, [128, 8], F32, "m8")
        thr = sbt(x, [128, 1], F32, "thr")
        thr0 = sbt(x, [128, 1], F32, "thr0")
        x.op("pool", lambda e: e.memset(thr0[:], -1e29), w=[thr0.d])
        P = [sbt(x, [128, 512], BF16, "P%d" % i) for i in range(2)]
        R = sbt(x, [128, 4], F32, "R")
        ob = [sbt(x, [128, 2048], BF16, "ob%d" % i) for i in range(2)]
        p_ix = [pst(x, [128, 512], F32, "p_ix%d" % i) for i in range(2)]
        p_tr = [pst(x, [128, 512], BF16, "p_tr%d" % i) for i in range(2)]
        p_s = [pst(x, [128, 512], F32, "p_s%d" % i) for i in range(2)]
        p_o = pst(x, [128, 4, 256], F32, "p_o")
        p_ob = p_o[:].rearrange("p a b -> p (a b)")
        nix = 0
        ns_ = 0
        def part_A(i):
            nonlocal nix
            b2 = i % 2
            nkb = 2 * i + 2
            L = nkb * 128
            x.dma("sp", qs[b2][:], qTs[i], r=[d_in], w=[qs[b2].d])
            x.dma("sp", qis[b2][:], qiTs[i].rearrange("p (a t) -> p a t", a=8), r=[d_in], w=[qis[b2].d])
            x.dma("sp", wt[b2][:], wis[i], r=[d_in], w=[wt[b2].d])
            w_ = wt[b2]
            x.op("dve", lambda e: e.scalar_tensor_tensor(out=aw[:], in0=w_[:], scalar=-1.0, in1=w_[:],
                                                         op0=ALU.mult, op1=ALU.max), r=[w_.d], w=[aw.d])
            x.op("dve", lambda e: e.tensor_scalar(out=aw[:], in0=aw[:], scalar1=0.125, scalar2=None, op0=ALU.mult),
                 r=[aw.d], w=[aw.d])
            x.op("dve", lambda e: e.tensor_scalar(out=sgn[:], in0=w_[:], scalar1=0.0, scalar2=2.0,
                                                  op0=ALU.is_ge, op1=ALU.mult), r=[w_.d], w=[sgn.d])
            x.op("dve", lambda e: e.tensor_scalar(out=sgn[:], in0=sgn[:], scalar1=-1.0, scalar2=None, op0=ALU.add),
                 r=[sgn.d], w=[sgn.d])
            for kq in range((L + 511) // 512):
                W = min(512, L - kq * 512)
                cs = slice(kq * 512, kq * 512 + W)
                for hi in range(16):
                    ps = p_ix[nix % 2]
                    r_ = rr[nix % 2]
                    nix += 1
                    pr_ = slice((hi % 2) * 64, (hi % 2) * 64 + 64)
                    x.op("pe", lambda e: e.matmul(ps[:, 0:W], qis[b2][pr_, hi // 2, :], ki2[pr_, cs], start=True, stop=True),
                         r=[qis[b2].d, ki2.d], w=[ps.d])
                    x.op("act", lambda e: e.activation(out=r_[:, 0:W], in_=ps[:, 0:W], func=AF.Relu, scale=aw[:, hi:hi + 1]),
                         r=[ps.d, aw.d], w=[r_.d])
                    if hi == 0:
                        x.op("dve", lambda e: e.tensor_scalar(out=acc[:, cs], in0=r_[:, 0:W], scalar1=sgn[:, 0:1],
                                                              scalar2=None, op0=ALU.mult), r=[r_.d, sgn.d], w=[acc.d])
                    else:
                        x.op("dve", lambda e: e.scalar_tensor_tensor(out=acc[:, cs], in0=r_[:, 0:W], scalar=sgn[:, hi:hi + 1],
                                                                     in1=acc[:, cs], op0=ALU.mult, op1=ALU.add),
                             r=[r_.d, sgn.d, acc.d], w=[acc.d])
            for a in range(2):
                ks = slice((nkb - 2 + a) * 128, (nkb - 1 + a) * 128)
                x.op("dve", lambda e: e.tensor_tensor(out=acc[:, ks], in0=acc[:, ks], in1=dm[:, a, :], op=ALU.add),
                     r=[acc.d, dm.d], w=[acc.d])
            if i >= 1:
                x.op("pool", lambda e: e.tensor_copy(work[:, 0:L], acc[:, 0:L]), r=[acc.d], w=[work.d])
                for rd in range(32):
                    x.op("dve", lambda e: e.max(out=m8[:], in_=work[:, 0:L]), r=[work.d], w=[m8.d])
                    if rd < 31:
                        x.op("dve", lambda e: e.match_replace(out=work[:, 0:L], in_to_replace=m8[:], in_values=work[:, 0:L],
                                                              imm_value=-1e30), r=[m8.d, work.d], w=[work.d])
                x.op("dve", lambda e: e.tensor_copy(thr[:], m8[:, 7:8]), r=[m8.d], w=[thr.d])
                th = thr
            else:
                th = thr0
            x.op("dve", lambda e: e.tensor_scalar(out=nb[:, 0:L], in0=acc[:, 0:L], scalar1=th[:, 0:1], scalar2=NEG,
                                                  op0=ALU.is_lt, op1=ALU.mult), r=[acc.d, th.d], w=[nb.d])

        def part_T(i):
            b2 = i % 2
            nkb = 2 * i + 2
            L = nkb * 128
            for kg in range((nkb + 3) // 4):
                p = p_tr[kg % 2]
                nn = min(4, nkb - kg * 4)
                for j in range(nn):
                    kb = kg * 4 + j
                    x.op("pe", lambda e: e.transpose(p[:, j * 128:(j + 1) * 128], nb[:, kb * 128:(kb + 1) * 128], c.ident[:]),
                         r=[nb.d, c.ident.d], w=[p.d], inc=(j == nn - 1))
                src = p[:, 0:nn * 128].rearrange("p (a b) -> p a b", a=nn)
                x.op("act", lambda e: e.activation(out=nbT4[:, kg * 4:kg * 4 + nn, :, :],
                                                   in_=bc(src.unsqueeze(2), [128, nn, 4, 128]), func=AF.Copy),
                     r=[p.d], w=[nbT4.d])

        def part_C(i):
            b2 = i % 2
            nkb = 2 * i + 2
            L = nkb * 128
            obb = ob[b2]
            asteps = [(g, kb) for g in range(4) for kb in range(nkb)]

            def a_scores(idx):
                g, kb = asteps[idx]
                ps = p_s[idx % 2]
                pp_ = P[idx % 2]
                x.op("pe", lambda e: e.matmul(ps[:], kts[:, g, kb * 128:(kb + 1) * 128],
                                              qs[b2][:, g * 512:(g + 1) * 512], start=True, stop=False),
                     r=[kts.d, qs[b2].d], w=[ps.d], inc=False)
                x.op("pe", lambda e: e.matmul(ps[:], c.ident[:], nbT4[:, kb, :, :].rearrange("p a b -> p (a b)"),
                                              start=False, stop=True), r=[c.ident.d, nbT4.d], w=[ps.d])
                x.op("act", lambda e: e.activation(out=pp_[:], in_=ps[:], func=AF.Exp, scale=SCALE, bias=negC[:, 0:1]),
                     r=[ps.d, negC.d], w=[pp_.d])

            def a_pv(idx):
                g, kb = asteps[idx]
                pp_ = P[idx % 2]
                for r in range(4):
                    x.op("pe", lambda e: e.matmul(p_o[:, r, 0:129], pp_[:, r * 128:(r + 1) * 128], va[:, kb, g, :],
                                                  start=False, stop=(kb == nkb - 1 and r % 2 == 1)),
                         r=[pp_.d, va.d], w=[p_o.d], inc=(r == 3))

            def a_epi(g):
                x.op("dve", lambda e: e.reciprocal(out=R[:], in_=p_o[:, :, 128:129].rearrange("p a b -> p (a b)")),
                     r=[p_o.d], w=[R.d])
                for r in range(4):
                    hh = g * 4 + r
                    x.op("act", lambda e: e.activation(out=obb[:, hh * 128:(hh + 1) * 128], in_=p_o[:, r, 0:128],
                                                       func=AF.Copy, scale=R[:, r:r + 1]), r=[p_o.d, R.d], mw=[obb.d])

            a_scores(0)
            for idx, (g, kb) in enumerate(asteps):
                if kb == 0:
                    for bnk in range(2):
                        x.op("pe", lambda e: e.matmul(p_ob[:, bnk * 512:(bnk + 1) * 512], zr[:, 0:128], zr[:],
                                                      start=True, stop=False), r=[zr.d], w=[p_o.d], inc=False)
                if idx + 1 < len(asteps):
                    a_scores(idx + 1)
                a_pv(idx)
                if kb == nkb - 1:
                    a_epi(g)
            x.dma("sp", o[i * 128:(i + 1) * 128, :], obb[:], r=[obb.d], mw=[d_o])

        part_A(0)
        part_T(0)
        for i in range(NS):
            if i + 1 < NS:
                part_A(i + 1)
            part_C(i)
            if i + 1 < NS:
                part_T(i + 1)


def _standalone(emit, *args):
    nc = bass.Bass("TRN2", target_bir_lowering=False)
    dram = lambda n, s, dt, k: nc.dram_tensor(n, list(s), dt, kind=k).ap()
    with ExitStack() as st:
        x = X(nc, st)
        c = make_consts(x)
        emit(x, c, dram, *args)
        x.global_barrier()
        print(emit.__name__, args, "sems", x.nsem, "cnt", x.cnt)
    return nc


def build_LA(kind):
    return _standalone(emit_LA, kind)


def build_LB_DA(lambda_init):
    return _standalone(emit_LB_DA, lambda_init)


def build_LB_SSD():
    return _standalone(emit_LB_SSD)


def build_LB_DSA():
    return _standalone(emit_LB_DSA)


def build_LC(FO):
    return _standalone(emit_LC, FO)


def _invf_table(half):
    invf = np.power(np.float32(ROPE_THETA), -np.arange(half, dtype=np.float32) / half).astype(np.float32)
    return np.ascontiguousarray(np.broadcast_to(invf[None, :], (128, half))).astype(np.float32)


def _ca(a):
    return np.ascontiguousarray(a)


GROUPS = [[0, 1], [2, 3], [4, 5], [6, 7]]
FIN_K = {0: 6144, 1: 10304, 2: 4176}


def _mk_dram(mapping):
    def dram(n, s, dt, k):
        ap = mapping[n]
        assert [int(v) for v in ap.shape] == [int(v) for v in s], (n, ap.shape, s)
        return ap
    return dram


def build_fused(depth=DEPTH):
    nc = bass.Bass("TRN2", target_bir_lowering=False)
    ext_in = lambda n, s, dt: nc.dram_tensor(n, list(s), dt, kind="ExternalInput").ap()
    internal = lambda n, s, dt: nc.dram_tensor(n, list(s), dt, kind="Internal").ap()
    x_in = ext_in("x_in", [TOK, D], F32)
    c_in = ext_in("c_in", [D], F32)
    pos = ext_in("pos", [TOK], I32)
    rk = ext_in("rk", [1, 1], I32)
    invf8 = ext_in("invf8", [128, 8], F32)
    invf16 = ext_in("invf16", [128, 16], F32)
    dmask = ext_in("dmask", [2, 128, 128], F32)
    x_out = nc.dram_tensor("x_out", [TOK, D], F32, kind="ExternalOutput").ap()
    xres = internal("xres", [TOK, D], F32)
    xmid = internal("xmid", [TOK, D], F32)
    scratch = {}

    def scr(n, s, dt):
        if n not in scratch:
            scratch[n] = internal(n, s, dt)
        return scratch[n]

    with ExitStack() as st:
        x = X(nc, st)
        c = make_consts(x)
        reg = st.enter_context(nc.gpsimd.register("rk"))
        nc.gpsimd.reg_load(reg, rk[0:1, 0:1])
        r = nc.gpsimd.snap(reg, min_val=0, max_val=1)
        d_g = x.mkdep("xchg")
        RS = bass.ds(r, 1)
        CH = 2 * 1024 * 1024
        MAXE = {BF16: 12 * 1024 * 1024 + 4096, F32: 1024 * 1024}
        GBS = {BF16: [], F32: []}
        goff = {BF16: 0, F32: 0}

        def gather(parts, both=False):
            a0 = parts[0]
            dt = a0.dtype
            shp = [int(v) for v in a0.shape]
            rowe = int(np.prod(shp[1:]))
            c0 = max(1, min(shp[0], CH // (rowe * mybir.dt.size(dt))))
            while shp[0] % c0:
                c0 -= 1
            nch = shp[0] // c0
            ce = c0 * rowe
            assert nch * 2 * ce <= MAXE[dt], (nch, ce, dt)
            if goff[dt] >= len(GBS[dt]):
                GBS[dt].append(internal("GB%d_%d" % (mybir.dt.size(dt), len(GBS[dt])), [2, MAXE[dt]], dt))
            gb = GBS[dt][goff[dt]]
            goff[dt] += 1
            off = 0
            for d_, a in enumerate(parts):
                for k in range(nch):
                    x.op("pool", lambda e: e.collective_compute(
                        "AllGather", ALU.bypass, replica_groups=GROUPS, ins=[a[k * c0:(k + 1) * c0].opt()],
                        outs=[gb[d_, off + k * 2 * ce:off + (k + 1) * 2 * ce].opt()]), w=[d_g])
            x.global_barrier()
            gs = scr("GS%d_%d" % (mybir.dt.size(dt), goff[dt] - 1), [1, MAXE[dt]], dt)
            tot = nch * 2 * ce
            CP = 4 * 1024 * 1024
            for o_ in range(0, tot, CP):
                n_ = min(CP, tot - o_)
                x.dma("pool", gs[:, o_:o_ + n_], (gb[0:1] if both else gb[RS])[:, o_:o_ + n_], mw=[d_g])
            row = gs[:, 0:tot]
            names = ["e%d" % i_ for i_ in range(len(shp) - 1)]
            kw = {"k": nch, "s": 2, "c": c0}
            kw.update({n_: v_ for n_, v_ in zip(names, shp[1:])})
            return row.rearrange("a (k s c %s) -> (a k) s c %s" % (" ".join(names), " ".join(names)), **kw)

        def cp(dst, src):
            x.dma("pool", dst, src, mw=[d_g])

        for i in range(depth):
            kind, j = i % 3, i // 3
            xsrc = x_in if i == 0 else xres
            xdst = x_out if i == depth - 1 else xres
            sfx = "_%d" % i
            ada_i = internal("ada" + sfx, [6 * D], F32)
            mp = {"x_in": xsrc, "c_in": c_in, "pos": pos, "ada_w": ext_in("ada_w" + sfx, [D, 6 * D], F32),
                  "ada_b": ext_in("ada_b" + sfx, [6 * D], F32), "g1": ext_in("g1" + sfx, [D], F32), "ada": ada_i,
                  "w_in": ext_in("w_in" + sfx, [D, FIN_K[kind]], F32)}
            goff[BF16] = goff[F32] = 0
            if kind == 0:
                A = {"qT": scr("A_qT", [16, 128, TOK], BF16), "kT": scr("A_kT", [16, 128, TOK], BF16),
                     "v": scr("A_v", [2, TOK, 1024], BF16)}
                mp.update(A)
                mp.update({"invf": invf8, "gq": ext_in("gq" + sfx, [64], F32), "gk": ext_in("gk" + sfx, [64], F32)})
                emit_LA(x, c, _mk_dram(mp), kind, True)
                x.global_barrier()
                Gq = gather([A["qT"][0:8], A["qT"][8:16]])
                Gk = gather([A["kT"][0:8], A["kT"][8:16]])
                Gv = gather([A["v"][0], A["v"][1]])
                x.global_barrier()
                L_qT = scr("L_qT", [8, 128, S], BF16)
                L_kT = scr("L_kT", [8, 128, S], BF16)
                L_v = scr("L_v", [S, 1024], BF16)
                for s_ in range(2):
                    cs = slice(s_ * TOK, (s_ + 1) * TOK)
                    for kk in range(2):
                        cp(L_qT[kk * 4:(kk + 1) * 4, :, cs], Gq[kk, s_])
                        cp(L_kT[kk * 4:(kk + 1) * 4, :, cs], Gk[kk, s_])
                        cp(L_v[s_ * TOK + kk * 1024:s_ * TOK + (kk + 1) * 1024, :], Gv[kk, s_])
                x.global_barrier()
                B_o = scr("B_o", [S, 1024], BF16)
                li = 0.8 - 0.6 * math.exp(-0.3 * i)
                emit_LB_DA(x, c, _mk_dram({"qT": L_qT, "kT": L_kT, "v": L_v, "lam4": ext_in("lam4" + sfx, [4, 64], F32),
                                           "gqk": ext_in("gqk" + sfx, [2, 64], F32),
                                           "subg": ext_in("subg" + sfx, [128], F32), "o": B_o}), float(li))
                x.global_barrier()
                goff[BF16] = 0
                Go = gather([B_o[0:TOK], B_o[TOK:S]])
                x.global_barrier()
                FO = 2048
                L_o = scr("L_o", [TOK, 2048], BF16)
                for hh in range(2):
                    for kk in range(2):
                        cp(L_o[kk * 1024:(kk + 1) * 1024, hh * 1024:(hh + 1) * 1024], Go[kk, hh])
            elif kind == 1:
                A = {"z": scr("A_z", [2, TOK, 2048], BF16), "xbcT": scr("A_xbcT", [2, 3072, TOK], BF16),
                     "dtr": scr("A_dtr", [2, TOK, 32], F32)}
                mp.update(A)
                emit_LA(x, c, _mk_dram(mp), kind, True)
                x.global_barrier()
                Gz = gather([A["z"][0], A["z"][1]])
                Gx = gather([A["xbcT"][0], A["xbcT"][1]])
                Gd = gather([A["dtr"][0], A["dtr"][1]])
                x.global_barrier()
                L_raw = scr("L_raw", [3072, S], F32)
                L_z = scr("L_z", [S, 2048], BF16)
                L_dt = scr("L_dt", [S, 32], F32)
                for s_ in range(2):
                    cs = slice(s_ * TOK, (s_ + 1) * TOK)
                    for kk in range(6):
                        cp(L_raw[kk * 512:(kk + 1) * 512, cs], Gx[kk, s_])
                    for kk in range(4):
                        cp(L_z[s_ * TOK + kk * 512:s_ * TOK + (kk + 1) * 512, :], Gz[kk, s_])
                    cp(L_dt[cs, :], Gd[0, s_])
                x.global_barrier()
                B_y = scr("B_y", [S, 2048], BF16)
                emit_LB_SSD(x, c, _mk_dram({"raw": L_raw, "convw": ext_in("convw" + sfx, [4, 3072], F32),
                                            "convb": ext_in("convb" + sfx, [3072], F32), "dtr": L_dt,
                                            "hp": ext_in("hp" + sfx, [3, 32], F32), "z": L_z,
                                            "ng": ext_in("ng" + sfx, [2048], F32),
                                            "tokd": scr("tokd", [S, 2560], BF16), "featd": scr("featd", [1024, S], BF16),
                                            "y": B_y}))
                x.global_barrier()
                goff[BF16] = 0
                Gy = gather([B_y[0:TOK], B_y[TOK:S]])
                x.global_barrier()
                FO = 4096
                L_o = scr("L_o4", [TOK, 4096], BF16)
                for gh in range(2):
                    for kk in range(4):
                        cp(L_o[kk * 512:(kk + 1) * 512, gh * 2048:(gh + 1) * 2048], Gy[kk, gh])
            else:
                A = {"qT": scr("D_qT", [2, 16, 128, 8, 128], BF16), "kT": scr("D_kT", [4, 128, TOK], BF16),
                     "v": scr("D_v", [TOK, 512], BF16), "qiT": scr("D_qiT", [2, 8, 128, 8, 128], BF16),
                     "kiT": scr("D_kiT", [64, TOK], BF16), "wi": scr("D_wi", [2, 8, 128, 16], F32)}
                mp.update(A)
                mp.update({"invf16": invf16, "invf8": invf8, "gq": ext_in("gq" + sfx, [128], F32),
                           "gk": ext_in("gk" + sfx, [128], F32), "gi": ext_in("gi" + sfx, [64], F32)})
                emit_LA(x, c, _mk_dram(mp), kind, True)
                x.global_barrier()
                Gq = gather([A["qT"][0], A["qT"][1]])
                Gqi = gather([A["qiT"][0], A["qiT"][1]])
                Gw = gather([A["wi"][0], A["wi"][1]])
                Gk = gather([A["kT"]], both=True)
                Gv = gather([A["v"]], both=True)
                Gki = gather([A["kiT"]], both=True)
                x.global_barrier()
                L_qTs = scr("L_qTs", [16, 128, 2048], BF16)
                L_qiTs = scr("L_qiTs", [16, 128, 1024], BF16)
                L_wis = scr("L_wis", [16, 128, 16], F32)
                L_kT = scr("L_kT4", [4, 128, S], BF16)
                L_v = scr("L_v4", [S, 512], BF16)
                L_ki = scr("L_ki2", [128, S], BF16)
                with nc.allow_non_contiguous_dma(reason="tile gathers"):
                    for sl_ in range(16):
                        s_, k_ = sl_ // 8, sl_ % 8
                        for kk in range(2):
                            cp(L_qTs[sl_][:, kk * 1024:(kk + 1) * 1024].rearrange("d (h t) -> d h t", h=8),
                               Gq[kk, s_][:, :, k_, :].rearrange("h d t -> d h t"))
                        cp(L_qiTs[sl_].rearrange("d (h t) -> d h t", h=8),
                           Gqi[0, s_][:, :, k_, :].rearrange("h d t -> d h t"))
                        cp(L_wis[sl_], Gw[0, s_][k_])
                for s_ in range(2):
                    cs = slice(s_ * TOK, (s_ + 1) * TOK)
                    cp(L_kT[:, :, cs], Gk[0, s_])
                    cp(L_v[cs, :], Gv[0, s_])
                    for dup in range(2):
                        cp(L_ki[dup * 64:(dup + 1) * 64, cs], Gki[0, s_])
                x.global_barrier()
                B_o2 = scr("B_o2", [TOK, 2048], BF16)
                emit_LB_DSA(x, c, _mk_dram({"qTs": L_qTs, "qiTs": L_qiTs, "wis": L_wis, "kT": L_kT, "v": L_v,
                                            "kiT2": L_ki, "dmask": dmask,
                                            "gqk": ext_in("gqk" + sfx, [2, 128], F32), "o": B_o2}))
                x.global_barrier()
                goff[BF16] = 0
                Go2 = gather([B_o2[0:1024], B_o2[1024:2048]])
                x.global_barrier()
                FO = 2048
                L_o = scr("L_o", [TOK, 2048], BF16)
                for tl in range(16):
                    jj = tl // 2
                    cp(L_o[tl * 128:(tl + 1) * 128, :], Go2[jj // 4, tl % 2][(jj % 4) * 128:(jj % 4 + 1) * 128, :])
            x.global_barrier()
            emit_LC(x, c, _mk_dram({"x_in": xsrc, "o_in": L_o, "ada": ada_i, "g2": ext_in("g2" + sfx, [D], F32),
                                    "w_out": ext_in("w_out" + sfx, [FO, D], F32),
                                    "wgu": ext_in("wgu" + sfx, [D, 2 * FFN], F32),
                                    "wd": ext_in("wd" + sfx, [FFN, D], F32), "x_mid": xmid, "x_out": xdst}), FO)
            x.global_barrier()
        print("fused sems", x.nsem, "cnt", x.cnt)
    return nc


def fused_inputs(inp, depth=DEPTH):
    x = np.asarray(inp["x"], dtype=np.float32)
    c = np.asarray(inp["c"], dtype=np.float32)
    pos = np.asarray(inp["positions"]).astype(np.int32)
    tri = np.where(np.arange(128)[None, :] <= np.arange(128)[:, None], 0.0, -1e30).astype(np.float32)
    full_neg = np.full((128, 128), -1e30, np.float32)
    zero_m = np.zeros((128, 128), np.float32)
    f32 = lambda a: _ca(np.asarray(a, dtype=np.float32))
    maps = []
    for cc in range(8):
        b, h = cc // 2, cc % 2
        sl = slice(h * TOK, (h + 1) * TOK)
        m = {"x_in": _ca(x[b, sl]), "c_in": _ca(c[b]), "pos": _ca(pos[b, sl]), "rk": np.array([[h]], np.int32),
             "invf8": _invf_table(8), "invf16": _invf_table(16),
             "dmask": np.stack([tri, full_neg]) if h == 0 else np.stack([zero_m, tri])}
        for i in range(depth):
            kind, j = i % 3, i // 3
            sfx = "_%d" % i
            m["ada_w" + sfx] = inp["ada_w"][i]
            m["ada_b" + sfx] = inp["ada_b"][i]
            m["g1" + sfx] = inp["norm1_g"][i]
            m["g2" + sfx] = inp["norm2_g"][i]
            m["wgu" + sfx] = inp["ffn_w_gate_up"][i]
            m["wd" + sfx] = inp["ffn_w_down"][i]
            if kind == 0:
                m["w_in" + sfx] = inp["da_w_in"][j]
                m["w_out" + sfx] = inp["da_w_out"][j]
                m["gq" + sfx] = inp["da_q_norm_g"][j]
                m["gk" + sfx] = inp["da_k_norm_g"][j]
                m["lam4" + sfx] = f32(np.stack([inp["da_lambda_q1"][j], inp["da_lambda_k1"][j],
                                                inp["da_lambda_q2"][j], inp["da_lambda_k2"][j]]))
                m["gqk" + sfx] = f32(np.stack([inp["da_q_norm_g"][j], inp["da_k_norm_g"][j]]))
                m["subg" + sfx] = inp["da_subln_g"][j]
            elif kind == 1:
                m["w_in" + sfx] = inp["ssd_w_in"][j]
                m["w_out" + sfx] = inp["ssd_w_out"][j]
                ch = np.concatenate([np.arange(h * 2048, (h + 1) * 2048), np.arange(4096 + h * 512, 4096 + (h + 1) * 512),
                                     np.arange(5120 + h * 512, 5120 + (h + 1) * 512)])
                hs = slice(h * 32, (h + 1) * 32)
                m["convw" + sfx] = _ca(inp["ssd_conv_w"][j][:, ch])
                m["convb" + sfx] = _ca(inp["ssd_conv_b"][j][ch])
                m["hp" + sfx] = f32(np.stack([inp["ssd_dt_bias"][j][hs], inp["ssd_a_log"][j][hs], inp["ssd_d_skip"][j][hs]]))
                m["ng" + sfx] = _ca(inp["ssd_norm_g"][j][h * 2048:(h + 1) * 2048])
            else:
                m["w_in" + sfx] = inp["sa_w_in"][j]
                m["w_out" + sfx] = inp["sa_w_out"][j]
                m["gq" + sfx] = inp["sa_q_norm_g"][j]
                m["gk" + sfx] = inp["sa_k_norm_g"][j]
                m["gi" + sfx] = inp["sa_idx_k_norm_g"][j]
                m["gqk" + sfx] = f32(np.stack([inp["sa_q_norm_g"][j], inp["sa_k_norm_g"][j]]))
        maps.append(m)
    return maps


def kernel(**inp):
    depth = DEPTH
    nc = build_fused(depth)
    maps = fused_inputs(inp, depth)
    res = run_bass_kernel_spmd(nc, maps, core_ids=list(range(8))).results
    out = np.empty((NB, S, D), np.float32)
    for cc in range(8):
        b, h = cc // 2, cc % 2
        out[b, h * TOK:(h + 1) * TOK] = res[cc]["x_out"]
    return out
```

```python
import math
import numpy as np
from contextlib import ExitStack
import ml_dtypes
import concourse.bass as bass
import concourse.mybir as mybir
from concourse.bass_utils import run_bass_kernel_spmd

F32 = mybir.dt.float32
BF16 = mybir.dt.bfloat16
I32 = mybir.dt.int32
ALU = mybir.AluOpType
AF = mybir.ActivationFunctionType
AX = mybir.AxisListType

D = 2048
S = 4096
NB = 4
DEPTH = 4
KC = D // 128
TOK = 2048
NT = TOK // 128
FFN = 5632
EPS = 1e-6
ROPE_THETA = 500000.0
NEG = -30000.0

SAME_ENGINE_SYNC = True


class Dep:
    __slots__ = ("name", "w", "r", "dsem", "dq")

    def __init__(self, name=""):
        self.name = name
        self.w = {}
        self.r = {}
        self.dsem = None
        self.dq = None


class X:
    def __init__(self, nc, stack):
        self.nc = nc
        self.stack = stack
        self.root = stack
        self.eng = {"pe": nc.tensor, "act": nc.scalar, "dve": nc.vector,
                    "pool": nc.gpsimd, "sp": nc.sync}
        self.sem = {}
        self.cnt = {}
        self.seen = {}
        for k in self.eng:
            self.sem[k] = stack.enter_context(nc.semaphore("es_" + k))
            self.cnt[k] = 0
            self.seen[k] = {}
        self.nsem = 5
        self.semcnt = {}
        self.free_dsems = {"sp": [], "pool": [], "act": []}
        self.alltok = {}
        self.uid = 0

    def name(self, p):
        self.uid += 1
        return "%s_%d" % (p, self.uid)

    def sb(self, shape, dt, name="sb", stack=None):
        st = stack or self.stack
        return st.enter_context(self.nc.sbuf_tensor(self.name(name), list(shape), dt))

    def ps(self, shape, dt=F32, name="ps", stack=None):
        st = stack or self.stack
        return st.enter_context(self.nc.psum_tensor(self.name(name), list(shape), dt))

    def _wait(self, e, toks):
        en = self.eng[e]
        seen = self.seen[e]
        for s, v in toks.items():
            cv = self.semcnt.get(s)
            if cv is not None and cv > v:
                v = cv
            if seen.get(s, 0) >= v:
                continue
            if s is self.sem[e]:
                if e == "pe" or not SAME_ENGINE_SYNC:
                    continue
            en.wait_ge(s, v)
            seen[s] = v

    def _pre(self, e, r, w, mw=()):
        for d in r:
            self._wait(e, d.w)
        for d in w:
            self._wait(e, d.w)
            self._wait(e, d.r)
        for d in mw:
            self._wait(e, d.r)

    def _post(self, tok, r, w, mw=()):
        s, v = tok
        if self.alltok.get(s, 0) < v:
            self.alltok[s] = v
        for d in r:
            if d.r.get(s, 0) < v:
                d.r[s] = v
        for d in w:
            d.w = {s: v}
            d.r = {}
        for d in mw:
            if d.w.get(s, 0) < v:
                d.w[s] = v

    def op(self, e, fn, r=(), w=(), mw=(), inc=True):
        self._pre(e, r, w, mw)
        ins = fn(self.eng[e])
        if inc:
            self.cnt[e] += 1
            ins.then_inc(self.sem[e], 1)
            self._post((self.sem[e], self.cnt[e]), r, w, mw)
        else:
            self._post((self.sem[e], self.cnt[e] + 1), r, w, mw)
        return ins

    def dma(self, e, out, in_, r=(), w=(), mw=(), **kw):
        host = (list(w) + list(mw))[0]
        assert host.dq in (None, e), (host.name, host.dq, e)
        if host.dsem is None:
            host.dq = e
            if self.free_dsems[e]:
                host.dsem = self.free_dsems[e].pop()
            else:
                host.dsem = self.root.enter_context(self.nc.semaphore(self.name("ds")))
                self.semcnt[host.dsem] = 0
                self.nsem += 1
        self._pre(e, r, w, mw)
        ins = self.eng[e].dma_start(out=out, in_=in_, **kw)
        self.semcnt[host.dsem] += 16
        ins.then_inc(host.dsem, 16)
        self._post((host.dsem, self.semcnt[host.dsem]), r, w, mw)
        return ins

    def mkdep(self, name=""):
        d = Dep(name)
        if isinstance(self.stack, Scope):
            self.stack.deps.append(d)
        return d

    def global_barrier(self):
        for e in self.eng:
            self._wait(e, dict(self.alltok))

    def barrier(self, deps):
        toks = {}
        for d in deps:
            for src in (d.w, d.r):
                for s_, v in src.items():
                    if toks.get(s_, 0) < v:
                        toks[s_] = v
        for e in self.eng:
            self._wait(e, toks)

    def finish(self, deps, e="sp"):
        for d in deps:
            self._wait(e, d.w)


class Scope:
    def __init__(self, x):
        self.x = x
        self.st = ExitStack()
        self.deps = []

    def __enter__(self):
        self.st.__enter__()
        self.prev = self.x.stack
        self.x.stack = self
        return self

    def __exit__(self, *a):
        self.x.stack = self.prev
        if a[0] is None:
            self.x.barrier(self.deps)
            for d in self.deps:
                if d.dsem is not None:
                    self.x.free_dsems[d.dq].append(d.dsem)
                    d.dsem = None
                    d.dq = None
        return self.st.__exit__(*a)

    def enter_context(self, cm):
        return self.st.enter_context(cm)


class T:
    def __init__(self, x, shape, dt, name, psum=False, stack=None):
        stack = stack or x.stack
        self.t = x.ps(shape, dt, name, stack) if psum else x.sb(shape, dt, name, stack)
        self.d = Dep(name)
        if isinstance(stack, Scope):
            stack.deps.append(self.d)

    def __getitem__(self, k):
        return self.t[k]


def sbt(x, shape, dt, name, stack=None):
    return T(x, shape, dt, name, False, stack)


def pst(x, shape, dt, name, stack=None):
    return T(x, shape, dt, name, True, stack)


class Consts:
    pass


def make_consts(x):
    c = Consts()
    idf = sbt(x, [128, 128], F32, "idf")
    c.ident = sbt(x, [128, 128], BF16, "ident")
    x.op("pool", lambda e: e.memset(idf[:], 1.0), w=[idf.d])
    x.op("pool", lambda e: e.affine_select(out=idf[:], in_=idf[:], pattern=[[-1, 128]],
                                           compare_op=ALU.is_equal, fill=0.0, base=0,
                                           channel_multiplier=1), r=[idf.d], w=[idf.d])
    x.op("dve", lambda e: e.tensor_copy(c.ident[:], idf[:]), r=[idf.d], w=[c.ident.d])
    c.identf = idf
    ngf = sbt(x, [128, 128], F32, "ngf")
    c.negT = sbt(x, [128, 128], BF16, "negT")
    x.op("pool", lambda e: e.memset(ngf[:], 0.0), w=[ngf.d])
    x.op("pool", lambda e: e.affine_select(out=ngf[:], in_=ngf[:], pattern=[[1, 128]],
                                           compare_op=ALU.is_ge, fill=NEG, base=0,
                                           channel_multiplier=-1), r=[ngf.d], w=[ngf.d])
    x.op("dve", lambda e: e.tensor_copy(c.negT[:], ngf[:]), r=[ngf.d], w=[c.negT.d])
    c.negQ = sbt(x, [128, 128], F32, "negQ")
    x.op("pool", lambda e: e.memset(c.negQ[:], 0.0), w=[c.negQ.d])
    x.op("pool", lambda e: e.affine_select(out=c.negQ[:], in_=c.negQ[:], pattern=[[-1, 128]],
                                           compare_op=ALU.is_ge, fill=-1e30, base=0,
                                           channel_multiplier=1), r=[c.negQ.d], w=[c.negQ.d])
    trf = sbt(x, [128, 128], F32, "trf")
    c.tri = sbt(x, [128, 128], BF16, "tri")
    x.op("pool", lambda e: e.memset(trf[:], 1.0), w=[trf.d])
    x.op("pool", lambda e: e.affine_select(out=trf[:], in_=trf[:], pattern=[[1, 128]],
                                           compare_op=ALU.is_ge, fill=0.0, base=0,
                                           channel_multiplier=-1), r=[trf.d], w=[trf.d])
    x.op("dve", lambda e: e.tensor_copy(c.tri[:], trf[:]), r=[trf.d], w=[c.tri.d])
    c.trif = trf
    c.ones = sbt(x, [128, 128], BF16, "ones")
    x.op("pool", lambda e: e.memset(c.ones[:], 1.0), w=[c.ones.d])
    c.onesf = sbt(x, [128, 128], F32, "onesf")
    x.op("pool", lambda e: e.memset(c.onesf[:], 1.0), w=[c.onesf.d])
    c.nhalf = sbt(x, [128, 64], F32, "nhalf")
    x.op("pool", lambda e: e.memset(c.nhalf[:], -0.5), w=[c.nhalf.d])
    return c


def rsqrt_mean(x, c, out, ss, n, width):
    x.op("dve", lambda e: e.tensor_scalar(out=out[:, 0:width], in0=ss[:, 0:width], scalar1=1.0 / n,
                                          scalar2=EPS, op0=ALU.mult, op1=ALU.add),
         r=[ss.d], w=[out.d])
    x.op("pool", lambda e: e.tensor_tensor(out=out[:, 0:width], in0=out[:, 0:width],
                                           in1=c.nhalf[:, 0:width], op=ALU.pow),
         r=[out.d, c.nhalf.d], w=[out.d])


def bc(ap, shape):
    return ap.to_broadcast(list(shape))


def emit_ada(x, c_ap, adaw_ap, adab_ap, ada_ap, d_ada):
    with Scope(x) as ls:
        cs = sbt(x, [128, 16], F32, "c_sb", ls)
        ca = sbt(x, [128, 16], F32, "c_act", ls)
        brow = sbt(x, [1, 6 * D], F32, "brow", ls)
        arow = sbt(x, [1, 6 * D], F32, "arow", ls)
        wt = [sbt(x, [128, 16, 512], F32, "adaw%d" % i, ls) for i in range(2)]
        pa = [pst(x, [1, 512], F32, "adap%d" % i, ls) for i in range(2)]
        x.dma("sp", cs[:], c_ap.rearrange("(p k) -> p k", k=16), w=[cs.d])
        x.dma("sp", brow[:], adab_ap.rearrange("(o n) -> o n", o=1), w=[brow.d])
        x.op("act", lambda e: e.activation(out=ca[:], in_=cs[:], func=AF.Silu), r=[cs.d], w=[ca.d])
        for blk in range(24):
            i = blk % 2
            x.dma("sp", wt[i][:], adaw_ap[:, blk * 512:(blk + 1) * 512].rearrange("(p k) f -> p k f", k=16),
                  w=[wt[i].d])
            for k in range(16):
                x.op("pe", lambda e: e.matmul(pa[i][:], ca[:, k:k + 1], wt[i][:, k, :],
                                              start=(k == 0), stop=(k == 15)),
                     r=[ca.d, wt[i].d], w=[pa[i].d], inc=(k == 15))
            x.op("dve", lambda e: e.tensor_tensor(out=arow[0:1, blk * 512:(blk + 1) * 512], in0=pa[i][:],
                                                  in1=brow[0:1, blk * 512:(blk + 1) * 512], op=ALU.add),
                 r=[pa[i].d, brow.d], mw=[arow.d])
        x.dma("sp", ada_ap.rearrange("(o n) -> o n", o=1), arow[:], r=[arow.d], w=[d_ada])


def load_cols(x, ls, vec_ap, name, d_src=None):
    t = sbt(x, [128, 16], F32, name, ls)
    x.dma("sp", t[:], vec_ap.rearrange("(k p) -> p k", p=128), r=([d_src] if d_src else []), w=[t.d],
          allow_slow_non_contiguous=True)
    return t


def load_bcast(x, ls, vec_ap, n, name, d_src=None, dt=F32):
    t = sbt(x, [128, n], dt, name, ls)
    x.dma("sp", t[:], vec_ap.rearrange("(o n) -> o n", o=1).partition_broadcast(128),
          r=([d_src] if d_src else []), w=[t.d])
    return t


def emit_mod_cols(x, ls, g_ap, ada_ap, d_ada, scale_idx, shift_idx):
    g = load_cols(x, ls, g_ap, "gcol")
    sc = load_cols(x, ls, ada_ap[scale_idx * D:(scale_idx + 1) * D], "sccol", d_ada)
    sh = load_cols(x, ls, ada_ap[shift_idx * D:(shift_idx + 1) * D], "shcol", d_ada)
    sT = sbt(x, [128, 16], F32, "sT", ls)
    x.op("dve", lambda e: e.scalar_tensor_tensor(out=sT[:], in0=sc[:], scalar=1.0, in1=g[:],
                                                 op0=ALU.add, op1=ALU.mult),
         r=[sc.d, g.d], w=[sT.d])
    return sT, sh


def emit_norm_hT(x, c, x_ap, d_x, ntiles, sT, shT, hT, hT_d, tile_off=0):
    with Scope(x) as ls:
        xt = [sbt(x, [128, D], F32, "xt%d" % i, ls) for i in range(2)]
        xn = [sbt(x, [128, D], BF16, "xn%d" % i, ls) for i in range(2)]
        junk = sbt(x, [128, D], BF16, "junk", ls)
        ss = [sbt(x, [128, 1], F32, "ss%d" % i, ls) for i in range(2)]
        rstd = [sbt(x, [128, 1], F32, "rstd%d" % i, ls) for i in range(2)]
        pt = [pst(x, [128, 512], BF16, "ptn%d" % i, ls) for i in range(2)]
        for t in range(ntiles):
            i = t % 2
            x.dma("sp", xt[i][:], x_ap[t * 128:(t + 1) * 128, :], r=[d_x], w=[xt[i].d])
            x.op("act", lambda e: e.activation(out=junk[:], in_=xt[i][:], func=AF.Square,
                                               accum_out=ss[i][:, 0:1]),
                 r=[xt[i].d], w=[junk.d, ss[i].d])
            rsqrt_mean(x, c, rstd[i], ss[i], D, 1)
            x.op("act", lambda e: e.activation(out=xn[i][:], in_=xt[i][:], func=AF.Copy,
                                               scale=rstd[i][:, 0:1]),
                 r=[xt[i].d, rstd[i].d], w=[xn[i].d])
            for g in range(4):
                p = pt[g % 2]
                for j in range(4):
                    kc = g * 4 + j
                    x.op("pe", lambda e: e.transpose(p[:, j * 128:(j + 1) * 128],
                                                     xn[i][:, kc * 128:(kc + 1) * 128], c.ident[:]),
                         r=[xn[i].d, c.ident.d], w=[p.d], inc=(j == 3))
                for j in range(4):
                    kc = g * 4 + j
                    dst = hT[:, kc, (tile_off + t) * 128:(tile_off + t + 1) * 128]
                    if True:
                        x.op("dve", lambda e: e.tensor_scalar(out=dst, in0=p[:, j * 128:(j + 1) * 128],
                                                              scalar1=sT[:, kc:kc + 1], scalar2=shT[:, kc:kc + 1],
                                                              op0=ALU.mult, op1=ALU.add),
                             r=[p.d, sT.d, shT.d], mw=[hT_d[tile_off + t]])
                    else:
                        x.op("act", lambda e: e.activation(out=dst, in_=p[:, j * 128:(j + 1) * 128],
                                                           func=AF.Identity, scale=sT[:, kc:kc + 1],
                                                           bias=shT[:, kc:kc + 1]),
                             r=[p.d, sT.d, shT.d], mw=[hT_d[tile_off + t]])


def emit_rope_tables(x, ls, pos_ap, invf_ap, half, ntiles):
    n = ntiles * half
    posi = sbt(x, [128, ntiles], I32, "posi", ls)
    posf = sbt(x, [128, ntiles], F32, "posf", ls)
    invf = sbt(x, [128, half], F32, "invf", ls)
    ang = sbt(x, [128, ntiles, half], F32, "ang", ls)
    x.dma("sp", posi[:], pos_ap.rearrange("(t p) -> p t", p=128), w=[posi.d], allow_slow_non_contiguous=True)
    x.dma("sp", invf[:], invf_ap, w=[invf.d])
    x.op("dve", lambda e: e.tensor_copy(posf[:], posi[:]), r=[posi.d], w=[posf.d])
    x.op("dve", lambda e: e.tensor_tensor(out=ang[:], in0=bc(posf[:].unsqueeze(2), [128, ntiles, half]),
                                          in1=bc(invf[:].unsqueeze(1), [128, ntiles, half]), op=ALU.mult),
         r=[posf.d, invf.d], w=[ang.d])
    outs = []
    C1 = 6.28125
    C2 = 2.0 * math.pi - C1
    for nm, shift in (("cos", math.pi / 2), ("sin", 0.0)):
        a = sbt(x, [128, n], F32, nm + "_a", ls)
        ki = sbt(x, [128, n], I32, nm + "_ki", ls)
        kf = sbt(x, [128, n], F32, nm + "_kf", ls)
        m = sbt(x, [128, n], F32, nm + "_m", ls)
        res = sbt(x, [128, ntiles, half], F32, nm + "_t", ls)
        af = ang[:].rearrange("p t h -> p (t h)")
        x.op("dve", lambda e: e.tensor_scalar(out=a[:], in0=af, scalar1=shift, scalar2=None, op0=ALU.add),
             r=[ang.d], w=[a.d])
        x.op("dve", lambda e: e.tensor_scalar(out=kf[:], in0=a[:], scalar1=1.0 / (2 * math.pi), scalar2=None,
                                              op0=ALU.mult), r=[a.d], w=[kf.d])
        x.op("dve", lambda e: e.tensor_copy(ki[:], kf[:]), r=[kf.d], w=[ki.d])
        x.op("dve", lambda e: e.tensor_copy(kf[:], ki[:]), r=[ki.d], w=[kf.d])
        x.op("dve", lambda e: e.scalar_tensor_tensor(out=a[:], in0=kf[:], scalar=-C1, in1=a[:],
                                                     op0=ALU.mult, op1=ALU.add), r=[kf.d, a.d], w=[a.d])
        x.op("dve", lambda e: e.scalar_tensor_tensor(out=a[:], in0=kf[:], scalar=-C2, in1=a[:],
                                                     op0=ALU.mult, op1=ALU.add), r=[kf.d, a.d], w=[a.d])
        x.op("dve", lambda e: e.tensor_scalar(out=m[:], in0=a[:], scalar1=math.pi, scalar2=2 * math.pi,
                                              op0=ALU.is_gt, op1=ALU.mult), r=[a.d], w=[m.d])
        x.op("dve", lambda e: e.tensor_tensor(out=a[:], in0=a[:], in1=m[:], op=ALU.subtract),
             r=[a.d, m.d], w=[a.d])
        x.op("dve", lambda e: e.tensor_scalar(out=m[:], in0=a[:], scalar1=-math.pi, scalar2=2 * math.pi,
                                              op0=ALU.is_lt, op1=ALU.mult), r=[a.d], w=[m.d])
        x.op("dve", lambda e: e.tensor_tensor(out=a[:], in0=a[:], in1=m[:], op=ALU.add),
             r=[a.d, m.d], w=[a.d])
        x.op("dve", lambda e: e.tensor_scalar(out=a[:], in0=a[:], scalar1=-3.1415925, scalar2=3.1415925,
                                              op0=ALU.max, op1=ALU.min), r=[a.d], w=[a.d])
        x.op("act", lambda e: e.activation(out=res[:].rearrange("p t h -> p (t h)"), in_=a[:], func=AF.Sin),
             r=[a.d], w=[res.d])
        outs.append(res)
    return outs[0], outs[1]


def emit_qk_post(x, c, ls_tiles, ps, ncols, gdim, gain, half, cos, sin, t, out_bf):
    ng = ncols // gdim
    qn, sq, ssq, rs, t1, t2, t3, t4 = ls_tiles
    pv = ps[:, 0:ncols].rearrange("p (g d) -> p g d", d=gdim)
    qv = qn[:, 0:ncols].rearrange("p (g d) -> p g d", d=gdim)
    if gain is not None:
        x.op("act", lambda e: e.activation(out=sq[:, 0:ncols], in_=ps[:, 0:ncols], func=AF.Square),
             r=[ps.d], w=[sq.d])
        x.op("dve", lambda e: e.tensor_reduce(out=ssq[:, 0:ng], in_=sq[:, 0:ncols].rearrange("p (g d) -> p g d", d=gdim),
                                              axis=AX.X, op=ALU.add), r=[sq.d], w=[ssq.d])
        rsqrt_mean(x, c, rs, ssq, gdim, ng)
        x.op("dve", lambda e: e.tensor_tensor(out=qv, in0=pv, in1=bc(rs[:, 0:ng].unsqueeze(2), [128, ng, gdim]),
                                              op=ALU.mult), r=[ps.d, rs.d], w=[qn.d])
        x.op("pool", lambda e: e.tensor_tensor(out=qv, in0=qv, in1=bc(gain[:, 0:gdim].unsqueeze(1), [128, ng, gdim]),
                                               op=ALU.mult), r=[qn.d, gain.d], w=[qn.d])
    else:
        x.op("act", lambda e: e.activation(out=qn[:, 0:ncols], in_=ps[:, 0:ncols], func=AF.Copy),
             r=[ps.d], w=[qn.d])
    if half:
        x1 = qv[:, :, 0:half]
        x2 = qv[:, :, half:2 * half]
        cb = bc(cos[:, t, :].unsqueeze(1), [128, ng, half])
        sb_ = bc(sin[:, t, :].unsqueeze(1), [128, ng, half])
        tv = [tt[:, 0:ng * half].rearrange("p (g h) -> p g h", h=half) for tt in (t1, t2, t3, t4)]
        x.op("dve", lambda e: e.tensor_tensor(out=tv[0], in0=x1, in1=cb, op=ALU.mult), r=[qn.d, cos.d], w=[t1.d])
        x.op("dve", lambda e: e.tensor_tensor(out=tv[1], in0=x2, in1=sb_, op=ALU.mult), r=[qn.d, sin.d], w=[t2.d])
        x.op("pool", lambda e: e.tensor_tensor(out=tv[2], in0=x2, in1=cb, op=ALU.mult), r=[qn.d, cos.d], w=[t3.d])
        x.op("pool", lambda e: e.tensor_tensor(out=tv[3], in0=x1, in1=sb_, op=ALU.mult), r=[qn.d, sin.d], w=[t4.d])
        x.op("dve", lambda e: e.tensor_tensor(out=x1, in0=tv[0], in1=tv[1], op=ALU.subtract),
             r=[t1.d, t2.d], w=[qn.d])
        x.op("pool", lambda e: e.tensor_tensor(out=x2, in0=tv[2], in1=tv[3], op=ALU.add),
             r=[t3.d, t4.d, qn.d], w=[qn.d])
    x.op("act", lambda e: e.activation(out=out_bf[:, 0:ncols], in_=qn[:, 0:ncols], func=AF.Copy),
         r=[qn.d], w=[out_bf.d])


def alloc_qk_tiles(x, ls):
    qn = sbt(x, [128, 512], F32, "qn", ls)
    sq = sbt(x, [128, 512], F32, "sq", ls)
    ssq = sbt(x, [128, 8], F32, "ssq", ls)
    rs = sbt(x, [128, 8], F32, "rs", ls)
    ts = [sbt(x, [128, 128], F32, "rt%d" % i, ls) for i in range(4)]
    return (qn, sq, ssq, rs, ts[0], ts[1], ts[2], ts[3])


class ProjCtx:
    pass


def emit_proj(x, c, w_ap, hT, hT_d, ntiles, blocks):
    with Scope(x) as ls:
        wb = [sbt(x, [128, 16, 512], BF16, "wb%d" % i, ls) for i in range(2)]
        pp = [pst(x, [128, 512], F32, "pp%d" % i, ls) for i in range(2)]
        n = 0
        for bi, (col0, ncols, mode, handler) in enumerate(blocks):
            wt = wb[bi % 2]
            x.dma("pool", wt[:, :, 0:ncols], w_ap[:, col0:col0 + ncols].rearrange("(k p) f -> p k f", p=128),
                  w=[wt.d])
            if mode == "tok":
                for t in range(ntiles):
                    ps = pp[n % 2]
                    n += 1
                    for kc in range(16):
                        x.op("pe", lambda e: e.matmul(ps[:, 0:ncols], hT[:, kc, t * 128:(t + 1) * 128],
                                                      wt[:, kc, 0:ncols], start=(kc == 0), stop=(kc == 15)),
                             r=[hT_d[t], wt.d], w=[ps.d], inc=(kc == 15))
                    handler(t, ps)
            else:
                for fc in range(ncols // 128):
                    for tb in range(ntiles // 4):
                        ps = pp[n % 2]
                        n += 1
                        for kc in range(16):
                            x.op("pe", lambda e: e.matmul(ps[:, :], wt[:, kc, fc * 128:(fc + 1) * 128],
                                                          hT[:, kc, tb * 512:(tb + 1) * 512],
                                                          start=(kc == 0), stop=(kc == 15)),
                                 r=hT_d[tb * 4:tb * 4 + 4] + [wt.d], w=[ps.d], inc=(kc == 15))
                        handler(fc, tb, ps)


def emit_LA(x, c, dram, kind, dm=False):
    x_in = dram("x_in", [TOK, D], F32, "ExternalInput")
    c_in = dram("c_in", [D], F32, "ExternalInput")
    pos = dram("pos", [TOK], I32, "ExternalInput")
    adaw = dram("ada_w", [D, 6 * D], F32, "ExternalInput")
    adab = dram("ada_b", [6 * D], F32, "ExternalInput")
    g1 = dram("g1", [D], F32, "ExternalInput")
    ada = dram("ada", [6 * D], F32, "ExternalOutput")
    FIN = {0: 6144, 1: 10304, 2: 4176}[kind]
    w_in = dram("w_in", [D, FIN], F32, "ExternalInput")
    outs = []
    with Scope(x) as st:
        d_ada = x.mkdep("ada")
        d_x = x.mkdep("x")
        emit_ada(x, c_in, adaw, adab, ada, d_ada)
        hT = x.sb([128, 16, TOK], BF16, "hT")
        hT_d = [x.mkdep("hT%d" % t) for t in range(NT)]
        with Scope(x) as ls:
            sT, shT = emit_mod_cols(x, ls, g1, ada, d_ada, 1, 0)
            emit_norm_hT(x, c, x_in, d_x, NT, sT, shT, hT, hT_d)
        ls = st
        if kind == 0:
            invf = dram("invf", [128, 8], F32, "ExternalInput")
            gq = dram("gq", [64], F32, "ExternalInput")
            gk = dram("gk", [64], F32, "ExternalInput")
            qT = dram("qT", [16, 128, TOK], BF16, "ExternalOutput")
            kT = dram("kT", [16, 128, TOK], BF16, "ExternalOutput")
            v = dram("v", [2, TOK, 1024] if dm else [TOK, 2048], BF16, "ExternalOutput")
            outs = [x.mkdep("qT"), x.mkdep("kT"), x.mkdep("v")]
            cos, sin = emit_rope_tables(x, ls, pos, invf, 8, NT)
            gqb = load_bcast(x, ls, gq, 64, "gqb")
            gkb = load_bcast(x, ls, gk, 64, "gkb")
            qk_tiles = alloc_qk_tiles(x, ls)
            qbf = sbt(x, [128, 512], BF16, "qbf", ls)
            stage = [sbt(x, [128, 4, TOK], BF16, "stage%d" % i, ls) for i in range(2)]
            ptr = [pst(x, [128, 512], BF16, "ptr%d" % i, ls) for i in range(2)]
            vst = [sbt(x, [128, 512], BF16, "vst%d" % i, ls) for i in range(2)]
            blocks = []
            cnt = [0]
            for which in range(2):
                for hb in range(4):
                    def handler(t, ps, which=which, hb=hb):
                        sg = stage[(which * 4 + hb) % 2]
                        emit_qk_post(x, c, qk_tiles, ps, 512, 64, gqb if which == 0 else gkb, 8, cos, sin, t, qbf)
                        p = ptr[cnt[0] % 2]
                        cnt[0] += 1
                        for j in range(4):
                            x.op("pe", lambda e: e.transpose(p[:, j * 128:(j + 1) * 128], qbf[:, j * 128:(j + 1) * 128],
                                                             c.ident[:]),
                                 r=[qbf.d, c.ident.d], w=[p.d], inc=(j == 3))
                        x.op("dve", lambda e: e.tensor_copy(sg[:, :, t * 128:(t + 1) * 128],
                                                            p[:, :].rearrange("p (a b) -> p a b", a=4)),
                             r=[p.d], mw=[sg.d])
                        if t == NT - 1:
                            dst = (qT if which == 0 else kT)[hb * 4:(hb + 1) * 4, :, :].rearrange("h p t -> p h t")
                            x.dma("sp", dst, sg[:], r=[sg.d], mw=[outs[which]])
                    blocks.append((which * 2048 + hb * 512, 512, "tok", handler))
            for vb in range(4):
                def vhandler(t, ps, vb=vb):
                    vs = vst[cnt[0] % 2]
                    cnt[0] += 1
                    x.op("act", lambda e: e.activation(out=vs[:], in_=ps[:, :], func=AF.Copy), r=[ps.d], w=[vs.d])
                    vdst = (v[vb // 2, t * 128:(t + 1) * 128, (vb % 2) * 512:(vb % 2 + 1) * 512] if dm
                            else v[t * 128:(t + 1) * 128, vb * 512:(vb + 1) * 512])
                    x.dma("sp", vdst, vs[:], r=[vs.d], mw=[outs[2]])
                blocks.append((4096 + vb * 512, 512, "tok", vhandler))
            emit_proj(x, c, w_in, hT, hT_d, NT, blocks)
        elif kind == 1:
            blocks = la_handlers_ssd(x, c, ls, dram, outs, dm)
            emit_proj(x, c, w_in, hT, hT_d, NT, blocks)
        else:
            blocks = la_handlers_dsa(x, c, ls, dram, outs, pos, dm)
            emit_proj(x, c, w_in, hT, hT_d, NT, blocks)


def emit_LB_DA(x, c, dram, lambda_init):
    NH = 8
    qT = dram("qT", [NH, 128, S], BF16, "ExternalInput")
    kT = dram("kT", [NH, 128, S], BF16, "ExternalInput")
    v = dram("v", [S, NH * 128], BF16, "ExternalInput")
    lam4 = dram("lam4", [4, 64], F32, "ExternalInput")
    gqk = dram("gqk", [2, 64], F32, "ExternalInput")
    subg = dram("subg", [128], F32, "ExternalInput")
    o = dram("o", [S, NH * 128], BF16, "ExternalOutput")
    with Scope(x) as st:
        d_o = x.mkdep("o")
        ls = st
        lt = [load_bcast(x, ls, lam4[i], 64, "lam%d" % i) for i in range(4)]
        gt = [load_bcast(x, ls, gqk[i], 64, "gqk%d" % i) for i in range(2)]
        gs = load_bcast(x, ls, subg, 128, "gs")
        x.op("dve", lambda e: e.tensor_scalar(out=gs[:], in0=gs[:], scalar1=1.0 - lambda_init, scalar2=None,
                                              op0=ALU.mult), r=[gs.d], w=[gs.d])
        pr = sbt(x, [128, 64], F32, "pr")
        s12 = sbt(x, [128, 2], F32, "s12")
        e12 = sbt(x, [128, 2], F32, "e12")
        neglam = sbt(x, [128, 1], F32, "neglam")
        for i in range(2):
            x.op("dve", lambda e: e.tensor_tensor(out=pr[:], in0=lt[2 * i][:], in1=lt[2 * i + 1][:], op=ALU.mult),
                 r=[lt[2 * i].d, lt[2 * i + 1].d], w=[pr.d])
            x.op("dve", lambda e: e.tensor_reduce(out=s12[:, i:i + 1], in_=pr[:], axis=AX.X, op=ALU.add),
                 r=[pr.d], w=[s12.d])
        x.op("act", lambda e: e.activation(out=e12[:], in_=s12[:], func=AF.Exp), r=[s12.d], w=[e12.d])
        x.op("dve", lambda e: e.scalar_tensor_tensor(out=neglam[:], in0=e12[:, 1:2], scalar=-lambda_init,
                                                     in1=e12[:, 0:1], op0=ALU.add, op1=ALU.subtract),
             r=[e12.d], w=[neglam.d])
        gm = sbt(x, [128, 2], F32, "gm")
        negC = sbt(x, [128, 1], F32, "negC")
        for i in range(2):
            x.op("dve", lambda e: e.tensor_reduce(out=gm[:, i:i + 1], in_=gt[i][:], axis=AX.X, op=ALU.max,
                                                  apply_absolute_value=True), r=[gt[i].d], w=[gm.d])
        x.op("dve", lambda e: e.scalar_tensor_tensor(out=negC[:], in0=gm[:, 0:1], scalar=-8.0, in1=gm[:, 1:2],
                                                     op0=ALU.mult, op1=ALU.mult), r=[gm.d], w=[negC.d])
        kt = [sbt(x, [128, S], BF16, "kt%d" % i) for i in range(2)]
        qt = [sbt(x, [128, S], BF16, "qt%d" % i) for i in range(2)]
        va = [sbt(x, [128, 32, 129], BF16, "va%d" % i) for i in range(2)]
        for i in range(2):
            x.op("pool", lambda e: e.memset(va[i][:, :, 128:129], 1.0), w=[va[i].d])
        pss = [pst(x, [128, 512], F32, "pss%d" % i) for i in range(4)]
        pso = pst(x, [128, 8, 256], F32, "pso")
        pt = [sbt(x, [128, 512], BF16, "pt%d" % i) for i in range(4)]
        R = sbt(x, [128, 8], F32, "R")
        tmp = [sbt(x, [128, 128], F32, "tmp%d" % i) for i in range(2)]
        of = sbt(x, [128, 4, 128], F32, "of")
        sq = sbt(x, [128, 512], F32, "sqo")
        ssq = sbt(x, [128, 4], F32, "ssqo")
        rs = sbt(x, [128, 4], F32, "rso")
        ob = [sbt(x, [128, 4, 128], BF16, "ob%d" % i) for i in range(2)]
        zr = sbt(x, [128, 512], BF16, "zr")
        x.op("pool", lambda e: e.memset(zr[:], 0.0), w=[zr.d])
        psob = pso[:].rearrange("p a b -> p (a b)")
        n = 0
        nq = 0
        for h in range(NH):
            hi = h % 2
            x.dma("sp", kt[hi][:], kT[h], w=[kt[hi].d])
            x.dma("sp", qt[hi][:], qT[h], w=[qt[hi].d])
            x.dma("sp", va[hi][:, :, 0:128], v[:, h * 128:(h + 1) * 128].rearrange("(kb p) d -> p kb d", p=128),
                  mw=[va[hi].d])
            steps = [(qc, kb, comp) for qc in range(8) for kb in range(4 * qc + 4) for comp in range(2)]

            def scores(idx):
                qc, kb, comp = steps[idx]
                jmin = max(0, kb - 4 * qc)
                diag = kb >= 4 * qc
                N = 512 - 128 * jmin
                q0 = qc * 512 + 128 * jmin
                ps = pss[idx % 4]
                ptt = pt[idx % 4]
                pr_ = slice(comp * 64, (comp + 1) * 64)
                lhs = kt[hi][pr_, kb * 128:(kb + 1) * 128]
                if diag:
                    x.op("pe", lambda e: e.matmul(ps[:, 0:128], lhs, qt[hi][pr_, q0:q0 + 128],
                                                  start=True, stop=False),
                         r=[kt[hi].d, qt[hi].d], w=[ps.d], inc=False)
                    x.op("pe", lambda e: e.matmul(ps[:, 0:128], c.ident[:], c.negT[:],
                                                  start=False, stop=True),
                         r=[c.ident.d, c.negT.d], w=[ps.d], inc=(N == 128))
                    if N > 128:
                        x.op("pe", lambda e: e.matmul(ps[:, 128:N], lhs, qt[hi][pr_, q0 + 128:q0 + N],
                                                      start=True, stop=True),
                             r=[kt[hi].d, qt[hi].d], w=[ps.d], inc=True)
                else:
                    x.op("pe", lambda e: e.matmul(ps[:, 0:N], lhs, qt[hi][pr_, q0:q0 + N],
                                                  start=True, stop=True),
                         r=[kt[hi].d, qt[hi].d], w=[ps.d], inc=True)
                x.op("act", lambda e: e.activation(out=ptt[:, 0:N], in_=ps[:, 0:N], func=AF.Exp,
                                                   scale=0.125, bias=negC[:, 0:1]),
                     r=[ps.d, negC.d], w=[ptt.d])

            def pv(idx):
                qc, kb, comp = steps[idx]
                jmin = max(0, kb - 4 * qc)
                ptt = pt[idx % 4]
                for j in range(jmin, 4):
                    last = (kb == 4 * qc + j)
                    x.op("pe", lambda e: e.matmul(pso[:, comp * 4 + j, 0:129],
                                                  ptt[:, (j - jmin) * 128:(j - jmin + 1) * 128],
                                                  va[hi][:, kb, :], start=False,
                                                  stop=(last and j % 2 == 1)),
                         r=[ptt.d, va[hi].d], w=[pso.d], inc=(j == 3))

            def epilogue(qc):
                nonlocal nq
                x.op("dve", lambda e: e.reciprocal(out=R[:], in_=pso[:, :, 128:129].rearrange("p a b -> p (a b)")),
                     r=[pso.d], w=[R.d])
                x.op("dve", lambda e: e.tensor_scalar(out=R[:, 4:8], in0=R[:, 4:8], scalar1=neglam[:, 0:1],
                                                      scalar2=None, op0=ALU.mult), r=[R.d, neglam.d], w=[R.d])
                for j in range(4):
                    tm = tmp[j % 2]
                    x.op("act", lambda e: e.activation(out=tm[:], in_=pso[:, 4 + j, 0:128], func=AF.Copy,
                                                       scale=R[:, 4 + j:5 + j]), r=[pso.d, R.d], w=[tm.d])
                    x.op("dve", lambda e: e.scalar_tensor_tensor(out=of[:, j, :], in0=pso[:, j, 0:128],
                                                                 scalar=R[:, j:j + 1], in1=tm[:],
                                                                 op0=ALU.mult, op1=ALU.add),
                         r=[pso.d, R.d, tm.d], mw=[of.d])
                ofl = of[:].rearrange("p a b -> p (a b)")
                x.op("act", lambda e: e.activation(out=sq[:], in_=ofl, func=AF.Square), r=[of.d], w=[sq.d])
                x.op("dve", lambda e: e.tensor_reduce(out=ssq[:], in_=sq[:].rearrange("p (a b) -> p a b", a=4),
                                                      axis=AX.X, op=ALU.add), r=[sq.d], w=[ssq.d])
                rsqrt_mean(x, c, rs, ssq, 128, 4)
                x.op("dve", lambda e: e.tensor_tensor(out=of[:], in0=of[:], in1=bc(rs[:].unsqueeze(2), [128, 4, 128]),
                                                      op=ALU.mult), r=[of.d, rs.d], w=[of.d])
                obb = ob[nq % 2]
                nq += 1
                x.op("pool", lambda e: e.tensor_tensor(out=obb[:], in0=of[:], in1=bc(gs[:].unsqueeze(1), [128, 4, 128]),
                                                       op=ALU.mult), r=[of.d, gs.d], w=[obb.d])
                x.dma("sp", o[qc * 512:(qc + 1) * 512, h * 128:(h + 1) * 128].rearrange("(j p) d -> p j d", p=128),
                      obb[:], r=[obb.d], mw=[d_o])

            scores(0)
            scores(1)
            for idx, (qc, kb, comp) in enumerate(steps):
                if kb == 0 and comp == 0:
                    for bnk in range(4):
                        x.op("pe", lambda e: e.matmul(psob[:, bnk * 512:(bnk + 1) * 512], zr[:, 0:128], zr[:],
                                                      start=True, stop=False), r=[zr.d], w=[pso.d], inc=False)
                if idx + 2 < len(steps):
                    scores(idx + 2)
                pv(idx)
                if kb == 4 * qc + 3 and comp == 1:
                    epilogue(qc)


def emit_outproj(x, c, o_ap, d_o, FO, wout_ap, x_ap, d_x, gate_b, xmid_ap, d_xmid):
    KO = FO // 128
    TB = 1024
    with Scope(x) as ls:
        oT = sbt(x, [128, KO, TB], BF16, "oT", ls)
        oT_d = [x.mkdep("oT%d" % t) for t in range(TB // 128)]
        ls.deps.extend(oT_d)
        ot = [sbt(x, [128, FO], BF16, "ot%d" % i, ls) for i in range(2)]
        pt = [pst(x, [128, 512], BF16, "pto%d" % i, ls) for i in range(2)]
        wb = [sbt(x, [128, KO, 512], BF16, "wo%d" % i, ls) for i in range(2)]
        pp = [pst(x, [128, 512], F32, "ppo%d" % i, ls) for i in range(2)]
        tm = [sbt(x, [128, 512], F32, "tmo%d" % i, ls) for i in range(2)]
        xt = [sbt(x, [128, 512], F32, "xto%d" % i, ls) for i in range(2)]
        n = 0
        nw = 0
        for tb in range(TOK // TB):
            for t in range(TB // 128):
                i = t % 2
                tok0 = tb * TB + t * 128
                x.dma("sp", ot[i][:], o_ap[tok0:tok0 + 128, :], r=[d_o], w=[ot[i].d])
                for g in range(KO // 4):
                    p = pt[g % 2]
                    for j in range(4):
                        kc = g * 4 + j
                        x.op("pe", lambda e: e.transpose(p[:, j * 128:(j + 1) * 128], ot[i][:, kc * 128:(kc + 1) * 128],
                                                         c.ident[:]), r=[ot[i].d, c.ident.d], w=[p.d], inc=(j == 3))
                    dst = oT[:, g * 4:(g + 1) * 4, t * 128:(t + 1) * 128]
                    src = p[:, :].rearrange("p (a b) -> p a b", a=4)
                    if g % 2 == 0:
                        x.op("dve", lambda e: e.tensor_copy(dst, src), r=[p.d], mw=[oT_d[t]])
                    else:
                        x.op("act", lambda e: e.activation(out=dst, in_=src, func=AF.Copy), r=[p.d], mw=[oT_d[t]])
            for cb in range(4):
                wt = wb[nw % 2]
                nw += 1
                x.dma("pool", wt[:], wout_ap[:, cb * 512:(cb + 1) * 512].rearrange("(k p) f -> p k f", p=128), w=[wt.d])
                for t in range(TB // 128):
                    tok0 = tb * TB + t * 128
                    ps = pp[n % 2]
                    tmm = tm[n % 2]
                    xtt = xt[n % 2]
                    n += 1
                    x.dma("sp", xtt[:], x_ap[tok0:tok0 + 128, cb * 512:(cb + 1) * 512], r=[d_x], w=[xtt.d])
                    for kc in range(KO):
                        x.op("pe", lambda e: e.matmul(ps[:], oT[:, kc, t * 128:(t + 1) * 128], wt[:, kc, :],
                                                      start=(kc == 0), stop=(kc == KO - 1)),
                             r=[oT_d[t], wt.d], w=[ps.d], inc=(kc == KO - 1))
                    x.op("dve", lambda e: e.tensor_tensor(out=tmm[:], in0=ps[:], in1=gate_b[:, cb * 512:(cb + 1) * 512],
                                                          op=ALU.mult), r=[ps.d, gate_b.d], w=[tmm.d])
                    x.op("pool", lambda e: e.tensor_tensor(out=tmm[:], in0=tmm[:], in1=xtt[:], op=ALU.add),
                         r=[tmm.d, xtt.d], w=[tmm.d])
                    x.dma("sp", xmid_ap[tok0:tok0 + 128, cb * 512:(cb + 1) * 512], tmm[:], r=[tmm.d], mw=[d_xmid])


def emit_ffn(x, c, xmid_ap, d_xmid, sT, shT, gate_b, wgu_ap, wd_ap, xout_ap, d_xout):
    TB = 512
    NFC = FFN // 128
    with Scope(x) as ls:
        h2T = sbt(x, [128, 16, TB], BF16, "h2T", ls)
        h2_d = [x.mkdep("h2T%d" % t) for t in range(4)]
        ls.deps.extend(h2_d)
        actT = sbt(x, [128, NFC, TB], BF16, "actT", ls)
        act_d = [x.mkdep("act%d" % f) for f in range(NFC)]
        ls.deps.extend(act_d)
        wg = [sbt(x, [128, 16, 256], BF16, "wg%d" % i, ls) for i in range(2)]
        wu = [sbt(x, [128, 16, 256], BF16, "wu%d" % i, ls) for i in range(2)]
        wdA = sbt(x, [128, 22, 512], BF16, "wdA", ls)
        wdB = sbt(x, [128, 22, 512], BF16, "wdB", ls)
        psg = [pst(x, [128, 512], F32, "psg%d" % i, ls) for i in range(2)]
        psu = [pst(x, [128, 512], F32, "psu%d" % i, ls) for i in range(2)]
        psd = [pst(x, [128, 512], F32, "psd%d" % i, ls) for i in range(2)]
        sg = [sbt(x, [128, 512], F32, "sg%d" % i, ls) for i in range(2)]
        tm = [sbt(x, [128, 512], F32, "tmf%d" % i, ls) for i in range(2)]
        xt = [sbt(x, [128, 512], F32, "xtf%d" % i, ls) for i in range(2)]
        nblk = 0
        n = 0
        nd = 0
        for tb in range(TOK // TB):
            emit_norm_hT(x, c, xmid_ap[tb * TB:(tb + 1) * TB, :], d_xmid, 4, sT, shT, h2T, h2_d)
            for blk in range(FFN // 256):
                wgt = wg[nblk % 2]
                wut = wu[nblk % 2]
                nblk += 1
                x.dma("pool", wgt[:], wgu_ap[:, blk * 256:(blk + 1) * 256].rearrange("(k p) f -> p k f", p=128),
                      w=[wgt.d])
                x.dma("pool", wut[:], wgu_ap[:, FFN + blk * 256:FFN + (blk + 1) * 256].rearrange("(k p) f -> p k f", p=128),
                      w=[wut.d])
                for fl in range(2):
                    fc = blk * 2 + fl
                    pg = psg[n % 2]
                    pu = psu[n % 2]
                    sgg = sg[n % 2]
                    n += 1
                    for kc in range(16):
                        x.op("pe", lambda e: e.matmul(pg[:], wgt[:, kc, fl * 128:(fl + 1) * 128], h2T[:, kc, :],
                                                      start=(kc == 0), stop=(kc == 15)),
                             r=h2_d + [wgt.d], w=[pg.d], inc=(kc == 15))
                    for kc in range(16):
                        x.op("pe", lambda e: e.matmul(pu[:], wut[:, kc, fl * 128:(fl + 1) * 128], h2T[:, kc, :],
                                                      start=(kc == 0), stop=(kc == 15)),
                             r=h2_d + [wut.d], w=[pu.d], inc=(kc == 15))
                    x.op("act", lambda e: e.activation(out=sgg[:], in_=pg[:], func=AF.Silu), r=[pg.d], w=[sgg.d])
                    x.op("dve", lambda e: e.tensor_tensor(out=actT[:, fc, :], in0=pu[:], in1=sgg[:], op=ALU.mult),
                         r=[pu.d, sgg.d], w=[act_d[fc]])
            for cb in range(4):
                x.dma("pool", wdA[:], wd_ap[0:22 * 128, cb * 512:(cb + 1) * 512].rearrange("(k p) f -> p k f", p=128),
                      w=[wdA.d])
                x.dma("pool", wdB[:], wd_ap[22 * 128:44 * 128, cb * 512:(cb + 1) * 512].rearrange("(k p) f -> p k f", p=128),
                      w=[wdB.d])
                for t in range(4):
                    tok0 = tb * TB + t * 128
                    ps = psd[nd % 2]
                    tmm = tm[nd % 2]
                    xtt = xt[nd % 2]
                    nd += 1
                    x.dma("sp", xtt[:], xmid_ap[tok0:tok0 + 128, cb * 512:(cb + 1) * 512], r=[d_xmid], w=[xtt.d])
                    for fc in range(NFC):
                        wt = wdA if fc < 22 else wdB
                        x.op("pe", lambda e: e.matmul(ps[:], actT[:, fc, t * 128:(t + 1) * 128], wt[:, fc % 22, :],
                                                      start=(fc == 0), stop=(fc == NFC - 1)),
                             r=[act_d[fc], wt.d], w=[ps.d], inc=(fc == NFC - 1 or fc == 21))
                    x.op("dve", lambda e: e.tensor_tensor(out=tmm[:], in0=ps[:], in1=gate_b[:, cb * 512:(cb + 1) * 512],
                                                          op=ALU.mult), r=[ps.d, gate_b.d], w=[tmm.d])
                    x.op("pool", lambda e: e.tensor_tensor(out=tmm[:], in0=tmm[:], in1=xtt[:], op=ALU.add),
                         r=[tmm.d, xtt.d], w=[tmm.d])
                    x.dma("sp", xout_ap[tok0:tok0 + 128, cb * 512:(cb + 1) * 512], tmm[:], r=[tmm.d], mw=[d_xout])


def emit_LC(x, c, dram, FO):
    x_in = dram("x_in", [TOK, D], F32, "ExternalInput")
    o_in = dram("o_in", [TOK, FO], BF16, "ExternalInput")
    ada = dram("ada", [6 * D], F32, "ExternalInput")
    g2 = dram("g2", [D], F32, "ExternalInput")
    w_out = dram("w_out", [FO, D], F32, "ExternalInput")
    wgu = dram("wgu", [D, 2 * FFN], F32, "ExternalInput")
    wd = dram("wd", [FFN, D], F32, "ExternalInput")
    x_mid = dram("x_mid", [TOK, D], F32, "Internal")
    x_out = dram("x_out", [TOK, D], F32, "ExternalOutput")
    with Scope(x) as st:
        d_none = x.mkdep("in")
        d_xmid = x.mkdep("xmid")
        d_xout = x.mkdep("xout")
        with Scope(x) as ls:
            g1b = load_bcast(x, ls, ada[2 * D:3 * D], D, "g1b")
            emit_outproj(x, c, o_in, d_none, FO, w_out, x_in, d_none, g1b, x_mid, d_xmid)
        with Scope(x) as ls:
            g2b = load_bcast(x, ls, ada[5 * D:6 * D], D, "g2b")
            sT, shT = emit_mod_cols(x, ls, g2, ada, d_none, 4, 3)
            emit_ffn(x, c, x_mid, d_xmid, sT, shT, g2b, wgu, wd, x_out, d_xout)


def la_handlers_ssd(x, c, ls, dram, outs_holder, dm=False):
    z = dram("z", [2, TOK, 2048] if dm else [TOK, 4096], BF16, "ExternalOutput")
    xbcT = dram("xbcT", [2, 3072, TOK] if dm else [6144, TOK], BF16, "ExternalOutput")
    dtr = dram("dtr", [2, TOK, 32] if dm else [TOK, 64], F32, "ExternalOutput")
    outs = [x.mkdep("z"), x.mkdep("xbcT"), x.mkdep("dtr")]
    outs_holder.extend(outs)
    zst = [sbt(x, [128, 512], BF16, "zst%d" % i, ls) for i in range(2)]
    fst = [sbt(x, [128, 512], BF16, "fst%d" % i, ls) for i in range(2)]
    dst_ = [sbt(x, [128, 64], F32, "dst%d" % i, ls) for i in range(2)]
    cnt = [0]
    blocks = []
    for zb in range(8):
        def zh(t, ps, zb=zb):
            s_ = zst[cnt[0] % 2]
            cnt[0] += 1
            x.op("act", lambda e: e.activation(out=s_[:], in_=ps[:, :], func=AF.Copy), r=[ps.d], w=[s_.d])
            zdst = (z[zb // 4, t * 128:(t + 1) * 128, (zb % 4) * 512:(zb % 4 + 1) * 512] if dm
                    else z[t * 128:(t + 1) * 128, zb * 512:(zb + 1) * 512])
            x.dma("sp", zdst, s_[:], r=[s_.d], mw=[outs[0]])
        blocks.append((zb * 512, 512, "tok", zh))
    for xb in range(12):
        def xh(fc, tb, ps, xb=xb):
            s_ = fst[cnt[0] % 2]
            cnt[0] += 1
            x.op("act", lambda e: e.activation(out=s_[:], in_=ps[:, :], func=AF.Copy), r=[ps.d], w=[s_.d])
            if dm:
                dd, rb = ((xb // 4, (xb % 4) * 512) if xb < 8 else ((xb - 8) % 2, 2048 + ((xb - 8) // 2) * 512))
                xdst = xbcT[dd, rb + fc * 128:rb + fc * 128 + 128, tb * 512:(tb + 1) * 512]
            else:
                r0 = xb * 512 + fc * 128
                xdst = xbcT[r0:r0 + 128, tb * 512:(tb + 1) * 512]
            x.dma("sp", xdst, s_[:], r=[s_.d], mw=[outs[1]])
        blocks.append((4096 + xb * 512, 512, "feat", xh))

    def dh(t, ps):
        s_ = dst_[cnt[0] % 2]
        cnt[0] += 1
        x.op("act", lambda e: e.activation(out=s_[:], in_=ps[:, 0:64], func=AF.Copy), r=[ps.d], w=[s_.d])
        if dm:
            for dd in range(2):
                x.dma("sp", dtr[dd, t * 128:(t + 1) * 128, :], s_[:, dd * 32:(dd + 1) * 32], r=[s_.d], mw=[outs[2]])
        else:
            x.dma("sp", dtr[t * 128:(t + 1) * 128, :], s_[:], r=[s_.d], mw=[outs[2]])
    blocks.append((10240, 64, "tok", dh))
    return blocks


def emit_LB_SSD(x, c, dram):
    NHh = 32
    NCH = 24
    raw = dram("raw", [NCH * 128, S], F32, "ExternalInput")
    convw = dram("convw", [4, NCH * 128], F32, "ExternalInput")
    convb = dram("convb", [NCH * 128], F32, "ExternalInput")
    dtr = dram("dtr", [S, NHh], F32, "ExternalInput")
    hp = dram("hp", [3, NHh], F32, "ExternalInput")
    z = dram("z", [S, 2048], BF16, "ExternalInput")
    ng = dram("ng", [2048], F32, "ExternalInput")
    tokd = dram("tokd", [S, 2560], BF16, "Internal")
    featd = dram("featd", [1024, S], BF16, "Internal")
    y = dram("y", [S, 2048], BF16, "ExternalOutput")
    with Scope(x) as st:
        d_in = x.mkdep("in")
        d_tok = x.mkdep("tokd")
        d_feat = x.mkdep("featd")
        d_y = x.mkdep("y")
        with Scope(x) as ls:
            cw = sbt(x, [128, 4, NCH], F32, "cw", ls)
            cb_ = sbt(x, [128, NCH], F32, "cb", ls)
            for j in range(4):
                x.dma("sp", cw[:, j, :], convw[j].rearrange("(k p) -> p k", p=128), mw=[cw.d],
                      allow_slow_non_contiguous=True)
            x.dma("sp", cb_[:], convb.rearrange("(k p) -> p k", p=128), w=[cb_.d], allow_slow_non_contiguous=True)
            rw = [sbt(x, [128, S + 3], F32, "rw%d" % i, ls) for i in range(2)]
            for i in range(2):
                x.op("pool", lambda e: e.memset(rw[i][:, 0:3], 0.0), w=[rw[i].d])
            acc = sbt(x, [128, S], F32, "acc", ls)
            sil = [sbt(x, [128, S], BF16, "sil%d" % i, ls) for i in range(2)]
            ptc = [pst(x, [128, 512], BF16, "ptc%d" % i, ls) for i in range(2)]
            stg = [sbt(x, [128, 4, 128], BF16, "stg%d" % i, ls) for i in range(2)]
            n = 0
            for cc in range(NCH):
                r_ = rw[cc % 2]
                sl_ = sil[cc % 2]
                x.dma("sp", r_[:, 3:3 + S], raw[cc * 128:(cc + 1) * 128, :], r=[d_in], mw=[r_.d])
                x.op("dve", lambda e: e.tensor_scalar(out=acc[:], in0=r_[:, 3:3 + S], scalar1=cw[:, 3, cc:cc + 1],
                                                      scalar2=cb_[:, cc:cc + 1], op0=ALU.mult, op1=ALU.add),
                     r=[r_.d, cw.d, cb_.d], w=[acc.d])
                for j in range(3):
                    x.op("dve", lambda e: e.scalar_tensor_tensor(out=acc[:], in0=r_[:, j:j + S],
                                                                 scalar=cw[:, j, cc:cc + 1], in1=acc[:],
                                                                 op0=ALU.mult, op1=ALU.add),
                         r=[r_.d, cw.d, acc.d], w=[acc.d])
                x.op("act", lambda e: e.activation(out=sl_[:], in_=acc[:], func=AF.Silu), r=[acc.d], w=[sl_.d])
                if cc >= 16:
                    x.dma("sp", featd[(cc - 16) * 128:(cc - 15) * 128, :], sl_[:], r=[sl_.d], mw=[d_feat])
                if cc < 20:
                    for tg in range(8):
                        p = ptc[n % 2]
                        sg_ = stg[n % 2]
                        n += 1
                        for j in range(4):
                            tt = tg * 4 + j
                            x.op("pe", lambda e: e.transpose(p[:, j * 128:(j + 1) * 128], sl_[:, tt * 128:(tt + 1) * 128],
                                                             c.ident[:]), r=[sl_.d, c.ident.d], w=[p.d], inc=(j == 3))
                        if n % 2 == 0:
                            x.op("dve", lambda e: e.tensor_copy(sg_[:], p[:, :].rearrange("p (a b) -> p a b", a=4)),
                                 r=[p.d], w=[sg_.d])
                        else:
                            x.op("act", lambda e: e.activation(out=sg_[:], in_=p[:, :].rearrange("p (a b) -> p a b", a=4),
                                                               func=AF.Copy), r=[p.d], w=[sg_.d])
                        x.dma("sp", tokd[tg * 512:(tg + 1) * 512, cc * 128:(cc + 1) * 128].rearrange("(j p) c -> p j c", p=128),
                              sg_[:], r=[sg_.d], mw=[d_tok])
        with Scope(x) as ls:
            hb = [load_bcast(x, ls, hp[i], NHh, "hp%d" % i) for i in range(3)]
            dtb_b, alog_b, dsk_b = hb
            a_b = sbt(x, [128, NHh], F32, "a_b", ls)
            x.op("act", lambda e: e.activation(out=a_b[:], in_=alog_b[:], func=AF.Exp), r=[alog_b.d], w=[a_b.d])
            x.op("dve", lambda e: e.tensor_scalar(out=a_b[:], in0=a_b[:], scalar1=-1.0, scalar2=None, op0=ALU.mult),
                 r=[a_b.d], w=[a_b.d])
            ngb = load_bcast(x, ls, ng, 2048, "ngb")
            sel = sbt(x, [32, NHh, 128], F32, "sel", ls)
            x.op("pool", lambda e: e.memset(sel[:], 1.0), w=[sel.d])
            x.op("pool", lambda e: e.affine_select(out=sel[:], in_=sel[:], pattern=[[-1, NHh], [0, 128]],
                                                   compare_op=ALU.is_equal, fill=0.0, base=0, channel_multiplier=1),
                 r=[sel.d], w=[sel.d])
            St = [sbt(x, [128, 512], F32, "St%d" % g, ls) for g in range(4)]
            Sb = [sbt(x, [128, 512], BF16, "Sb%d" % g, ls) for g in range(4)]
            for g in range(4):
                x.op("pool", lambda e: e.memset(St[g][:], 0.0), w=[St[g].d])
                x.op("pool", lambda e: e.memset(Sb[g][:], 0.0), w=[Sb[g].d])
            xs_t = [sbt(x, [128, 2560], BF16, "xs_t%d" % i, ls) for i in range(2)]
            bct = [sbt(x, [128, 8, 128], BF16, "bct%d" % i, ls) for i in range(2)]
            dtt = [sbt(x, [128, NHh], F32, "dtt%d" % i, ls) for i in range(2)]
            zt = [sbt(x, [128, 2048], BF16, "zt%d" % i, ls) for i in range(2)]
            f = lambda nm, w_: sbt(x, [128, w_], F32, nm, ls)
            dtb, ab, ee, dt_, dta, acum, nacum, alast, eac, wend, decay = [f(nm, NHh) for nm in
                ("dtb", "ab", "ee", "dt_", "dta", "acum", "nacum", "alast", "eac", "wend", "decay")]
            acT = sbt(x, [32, 128], F32, "acT", ls)
            xdt = sbt(x, [128, 2048], BF16, "xdt", ls)
            xde = sbt(x, [128, 2048], BF16, "xde", ls)
            cbm = [sbt(x, [128, 128], F32, "cbm%d" % g, ls) for g in range(4)]
            Eh = [sbt(x, [128, 128], F32, "Eh%d" % i, ls) for i in range(4)]
            Mh = [sbt(x, [128, 128], BF16, "Mh%d" % i, ls) for i in range(4)]
            yf = sbt(x, [128, 2048], F32, "yf", ls)
            t1 = sbt(x, [128, 2048], F32, "t1", ls)
            sqy = sbt(x, [128, 2048], F32, "sqy", ls)
            ssy = sbt(x, [128, 4], F32, "ssy", ls)
            rsy = sbt(x, [128, 4], F32, "rsy", ls)
            yb = [sbt(x, [128, 2048], BF16, "yb%d" % i, ls) for i in range(2)]
            p_small = pst(x, [128, 512], F32, "p_small", ls)
            p_cb = pst(x, [128, 512], F32, "p_cb", ls)
            p_G = [pst(x, [128, 512], F32, "p_G%d" % i, ls) for i in range(2)]
            p_y = [pst(x, [128, 512], F32, "p_y%d" % i, ls) for i in range(2)]
            p_i = pst(x, [128, 512], F32, "p_i", ls)
            p_s = pst(x, [128, 512], F32, "p_s", ls)
            for ck in range(S // 128):
                i = ck % 2
                t0 = ck * 128
                xt_ = xs_t[i]
                bc_ = bct[i]
                x.dma("sp", xt_[:], tokd[t0:t0 + 128, :], r=[d_tok], w=[xt_.d])
                x.dma("sp", bc_[:], featd[:, t0:t0 + 128].rearrange("(g p) t -> p g t", p=128), r=[d_feat], w=[bc_.d])
                x.dma("sp", dtt[i][:], dtr[t0:t0 + 128, :], r=[d_in], w=[dtt[i].d])
                x.dma("sp", zt[i][:], z[t0:t0 + 128, :], r=[d_in], w=[zt[i].d])
                x.op("dve", lambda e: e.tensor_tensor(out=dtb[:], in0=dtt[i][:], in1=dtb_b[:], op=ALU.add),
                     r=[dtt[i].d, dtb_b.d], w=[dtb.d])
                x.op("dve", lambda e: e.scalar_tensor_tensor(out=ab[:], in0=dtb[:], scalar=-1.0, in1=dtb[:],
                                                             op0=ALU.mult, op1=ALU.max), r=[dtb.d], w=[ab.d])
                x.op("act", lambda e: e.activation(out=ee[:], in_=ab[:], func=AF.Exp, scale=-1.0), r=[ab.d], w=[ee.d])
                x.op("act", lambda e: e.activation(out=ee[:], in_=ee[:], func=AF.Ln, bias=1.0), r=[ee.d], w=[ee.d])
                x.op("dve", lambda e: e.scalar_tensor_tensor(out=dt_[:], in0=dtb[:], scalar=0.0, in1=ee[:],
                                                             op0=ALU.max, op1=ALU.add), r=[dtb.d, ee.d], w=[dt_.d])
                x.op("dve", lambda e: e.tensor_tensor(out=dta[:], in0=dt_[:], in1=a_b[:], op=ALU.mult),
                     r=[dt_.d, a_b.d], w=[dta.d])
                x.op("pe", lambda e: e.matmul(p_small[:, 0:32], c.trif[:], dta[:], start=True, stop=True),
                     r=[c.trif.d, dta.d], w=[p_small.d])
                x.op("pe", lambda e: e.matmul(p_small[:, 32:64], c.onesf[:], dta[:], start=True, stop=True),
                     r=[c.onesf.d, dta.d], w=[p_small.d])
                x.op("dve", lambda e: e.tensor_copy(acum[:], p_small[:, 0:32]), r=[p_small.d], w=[acum.d])
                x.op("dve", lambda e: e.tensor_scalar(out=nacum[:], in0=p_small[:, 0:32], scalar1=-1.0, scalar2=None,
                                                      op0=ALU.mult), r=[p_small.d], w=[nacum.d])
                x.op("dve", lambda e: e.tensor_copy(alast[:], p_small[:, 32:64]), r=[p_small.d], w=[alast.d])
                x.op("act", lambda e: e.activation(out=eac[:], in_=acum[:], func=AF.Exp), r=[acum.d], w=[eac.d])
                x.op("act", lambda e: e.activation(out=decay[:], in_=alast[:], func=AF.Exp), r=[alast.d], w=[decay.d])
                x.op("dve", lambda e: e.tensor_tensor(out=wend[:], in0=alast[:], in1=acum[:], op=ALU.subtract),
                     r=[alast.d, acum.d], w=[wend.d])
                x.op("act", lambda e: e.activation(out=wend[:], in_=wend[:], func=AF.Exp), r=[wend.d], w=[wend.d])
                x.op("dve", lambda e: e.tensor_tensor(out=wend[:], in0=wend[:], in1=dt_[:], op=ALU.mult),
                     r=[wend.d, dt_.d], w=[wend.d])
                x.op("pe", lambda e: e.matmul(p_small[0:32, 128:256], acum[:], c.identf[:], start=True, stop=True),
                     r=[acum.d, c.identf.d], w=[p_small.d])
                x.op("dve", lambda e: e.tensor_copy(acT[:], p_small[0:32, 128:256]), r=[p_small.d], w=[acT.d])
                xv = xt_[:, 0:2048].rearrange("p (h d) -> p h d", d=64)
                x.op("dve", lambda e: e.tensor_tensor(out=xdt[:].rearrange("p (h d) -> p h d", d=64), in0=xv,
                                                      in1=bc(dt_[:].unsqueeze(2), [128, NHh, 64]), op=ALU.mult),
                     r=[xt_.d, dt_.d], w=[xdt.d])
                x.op("pool", lambda e: e.tensor_tensor(out=xde[:].rearrange("p (h d) -> p h d", d=64), in0=xv,
                                                       in1=bc(wend[:].unsqueeze(2), [128, NHh, 64]), op=ALU.mult),
                     r=[xt_.d, wend.d], w=[xde.d])
                for g in range(4):
                    x.op("pe", lambda e: e.matmul(p_cb[:, g * 128:(g + 1) * 128], bc_[:, g, :], bc_[:, 4 + g, :],
                                                  start=True, stop=True), r=[bc_.d], w=[p_cb.d], inc=(g == 3))
                for g in range(4):
                    if g % 2 == 0:
                        x.op("dve", lambda e: e.tensor_copy(cbm[g][:], p_cb[:, g * 128:(g + 1) * 128]),
                             r=[p_cb.d], w=[cbm[g].d])
                    else:
                        x.op("act", lambda e: e.activation(out=cbm[g][:], in_=p_cb[:, g * 128:(g + 1) * 128], func=AF.Copy),
                             r=[p_cb.d], w=[cbm[g].d])
                for g in range(4):
                    py = p_y[g % 2]
                    x.op("pe", lambda e: e.matmul(p_i[:], bc_[:, 4 + g, :], Sb[g][:], start=True, stop=True),
                         r=[bc_.d, Sb[g].d], w=[p_i.d])
                    def hG(hl):
                        h = g * 8 + hl
                        pgt = p_G[(h % 4) // 2]
                        pgs = slice((h % 2) * 128, (h % 2) * 128 + 128)
                        x.op("pe", lambda e: e.matmul(pgt[:, pgs], sel[:, h, :], acT[:], start=True, stop=False),
                             r=[sel.d, acT.d], w=[pgt.d], inc=False)
                        x.op("pe", lambda e: e.matmul(pgt[:, pgs], c.ident[:], c.negT[:], start=False, stop=True),
                             r=[c.ident.d, c.negT.d], w=[pgt.d])
                        eh = Eh[h % 4]
                        mh = Mh[h % 4]
                        x.op("act", lambda e: e.activation(out=eh[:], in_=pgt[:, pgs], func=AF.Exp,
                                                           bias=nacum[:, h:h + 1]), r=[pgt.d, nacum.d], w=[eh.d])
                        x.op("dve", lambda e: e.tensor_tensor(out=mh[:], in0=eh[:], in1=cbm[g][:], op=ALU.mult),
                             r=[eh.d, cbm[g].d], w=[mh.d])

                    def hY(hl):
                        h = g * 8 + hl
                        mh = Mh[h % 4]
                        x.op("pe", lambda e: e.matmul(py[:, hl * 64:(hl + 1) * 64], mh[:], xdt[:, h * 64:(h + 1) * 64],
                                                      start=True, stop=True), r=[mh.d, xdt.d], w=[py.d])

                    hG(0)
                    hG(1)
                    for hl in range(8):
                        if hl + 2 < 8:
                            hG(hl + 2)
                        hY(hl)
                    gs_ = slice(g * 512, (g + 1) * 512)
                    x.op("dve", lambda e: e.tensor_tensor(out=t1[:, gs_].rearrange("p (h d) -> p h d", d=64),
                                                          in0=p_i[:].rearrange("p (h d) -> p h d", d=64),
                                                          in1=bc(eac[:, g * 8:(g + 1) * 8].unsqueeze(2), [128, 8, 64]),
                                                          op=ALU.mult), r=[p_i.d, eac.d], mw=[t1.d])
                    x.op("dve", lambda e: e.tensor_tensor(out=yf[:, gs_], in0=py[:], in1=t1[:, gs_], op=ALU.add),
                         r=[py.d, t1.d], mw=[yf.d])
                    x.op("pe", lambda e: e.matmul(p_s[:], xt_[:, 2048 + g * 128:2048 + (g + 1) * 128], xde[:, gs_],
                                                  start=True, stop=True), r=[xt_.d, xde.d], w=[p_s.d])
                    x.op("pool", lambda e: e.tensor_tensor(out=St[g][:].rearrange("p (h d) -> p h d", d=64),
                                                           in0=St[g][:].rearrange("p (h d) -> p h d", d=64),
                                                           in1=bc(decay[:, g * 8:(g + 1) * 8].unsqueeze(2), [128, 8, 64]),
                                                           op=ALU.mult), r=[St[g].d, decay.d], w=[St[g].d])
                    x.op("dve", lambda e: e.tensor_tensor(out=St[g][:], in0=p_s[:], in1=St[g][:], op=ALU.add),
                         r=[p_s.d, St[g].d], w=[St[g].d])
                    x.op("act", lambda e: e.activation(out=Sb[g][:], in_=St[g][:], func=AF.Copy),
                         r=[St[g].d], w=[Sb[g].d])
                x.op("pool", lambda e: e.tensor_tensor(out=t1[:].rearrange("p (h d) -> p h d", d=64), in0=xv,
                                                       in1=bc(dsk_b[:].unsqueeze(2), [128, NHh, 64]), op=ALU.mult),
                     r=[xt_.d, dsk_b.d, yf.d], w=[t1.d])
                x.op("dve", lambda e: e.tensor_tensor(out=yf[:], in0=yf[:], in1=t1[:], op=ALU.add),
                     r=[yf.d, t1.d], w=[yf.d])
                x.op("act", lambda e: e.activation(out=t1[:], in_=zt[i][:], func=AF.Silu), r=[zt[i].d, yf.d], w=[t1.d])
                x.op("dve", lambda e: e.tensor_tensor(out=yf[:], in0=yf[:], in1=t1[:], op=ALU.mult),
                     r=[yf.d, t1.d], w=[yf.d])
                x.op("act", lambda e: e.activation(out=sqy[:], in_=yf[:], func=AF.Square), r=[yf.d], w=[sqy.d])
                x.op("dve", lambda e: e.tensor_reduce(out=ssy[:], in_=sqy[:].rearrange("p (g d) -> p g d", g=4),
                                                      axis=AX.X, op=ALU.add), r=[sqy.d], w=[ssy.d])
                rsqrt_mean(x, c, rsy, ssy, 512, 4)
                x.op("dve", lambda e: e.tensor_tensor(out=yf[:].rearrange("p (g d) -> p g d", g=4),
                                                      in0=yf[:].rearrange("p (g d) -> p g d", g=4),
                                                      in1=bc(rsy[:].unsqueeze(2), [128, 4, 512]), op=ALU.mult),
                     r=[yf.d, rsy.d], w=[yf.d])
                x.op("pool", lambda e: e.tensor_tensor(out=yb[i][:], in0=yf[:], in1=ngb[:], op=ALU.mult),
                     r=[yf.d, ngb.d], w=[yb[i].d])
                x.dma("sp", y[t0:t0 + 128, :], yb[i][:], r=[yb[i].d], mw=[d_y])


def la_handlers_dsa(x, c, ls, dram, outs_holder, pos, dm=False):
    invf16 = dram("invf16", [128, 16], F32, "ExternalInput")
    invf8 = dram("invf8", [128, 8], F32, "ExternalInput")
    gq = dram("gq", [128], F32, "ExternalInput")
    gk = dram("gk", [128], F32, "ExternalInput")
    gi = dram("gi", [64], F32, "ExternalInput")
    qT = dram("qT", [2, 16, 128, 8, 128] if dm else [16, 128, TOK], BF16, "ExternalOutput")
    kT = dram("kT", [4, 128, TOK], BF16, "ExternalOutput")
    v = dram("v", [TOK, 512], BF16, "ExternalOutput")
    qiT = dram("qiT", [2, 8, 128, 8, 128] if dm else [8, 128, TOK], BF16, "ExternalOutput")
    kiT = dram("kiT", [64, TOK], BF16, "ExternalOutput")
    wi = dram("wi", [2, 8, 128, 16] if dm else [TOK, 16], F32, "ExternalOutput")
    outs = [x.mkdep(n) for n in ("qT", "kT", "v", "qiT", "kiT", "wi")]
    outs_holder.extend(outs)
    cos16, sin16 = emit_rope_tables(x, ls, pos, invf16, 16, NT)
    cos8, sin8 = emit_rope_tables(x, ls, pos, invf8, 8, NT)
    gqb = load_bcast(x, ls, gq, 128, "gqb")
    gkb = load_bcast(x, ls, gk, 128, "gkb")
    gib = load_bcast(x, ls, gi, 64, "gib")
    qk_tiles = alloc_qk_tiles(x, ls)
    qbf = sbt(x, [128, 512], BF16, "qbf", ls)
    stage = [sbt(x, [128, 4, TOK], BF16, "stage%d" % i, ls) for i in range(2)]
    kist = sbt(x, [64, TOK], BF16, "kist", ls)
    ptr = [pst(x, [128, 512], BF16, "ptr%d" % i, ls) for i in range(2)]
    vst = [sbt(x, [128, 512], BF16, "vst%d" % i, ls) for i in range(2)]
    wst = [sbt(x, [128, 16], F32, "wst%d" % i, ls) for i in range(2)]
    cnt = [0]
    nst = [0]
    blocks = []

    def mk_qk(col0, gdim, gain, half, cos, sin, dst_ap, dep, dmh=None):
        sg = stage[nst[0] % 2]
        nst[0] += 1

        def handler(t, ps):
            emit_qk_post(x, c, qk_tiles, ps, 512, gdim, gain, half, cos, sin, t, qbf)
            p = ptr[cnt[0] % 2]
            cnt[0] += 1
            for j in range(4):
                x.op("pe", lambda e: e.transpose(p[:, j * 128:(j + 1) * 128], qbf[:, j * 128:(j + 1) * 128], c.ident[:]),
                     r=[qbf.d, c.ident.d], w=[p.d], inc=(j == 3))
            x.op("dve", lambda e: e.tensor_copy(sg[:, :, t * 128:(t + 1) * 128],
                                                p[:, :].rearrange("p (a b) -> p a b", a=4)), r=[p.d], mw=[sg.d])
            if t == NT - 1:
                if dmh is None:
                    x.dma("sp", dst_ap.rearrange("h p t -> p h t"), sg[:], r=[sg.d], mw=[dep])
                else:
                    tens, h0 = dmh
                    sgv = sg[:].rearrange("p h (k a t) -> p h k a t", a=2, t=128)
                    for a_ in range(2):
                        for hh_ in range(4):
                            x.dma("sp", tens[a_, h0 + hh_].rearrange("d k t -> d k t"), sgv[:, hh_, :, a_, :],
                                  r=[sg.d], mw=[dep])
        blocks.append((col0, 512, "tok", handler))
    for hb in range(4):
        mk_qk(hb * 512, 128, gqb, 16, cos16, sin16, None if dm else qT[hb * 4:(hb + 1) * 4], outs[0],
              (qT, hb * 4) if dm else None)
    mk_qk(2048, 128, gkb, 16, cos16, sin16, kT[0:4], outs[1])

    def vh(t, ps):
        vs = vst[cnt[0] % 2]
        cnt[0] += 1
        x.op("act", lambda e: e.activation(out=vs[:], in_=ps[:, :], func=AF.Copy), r=[ps.d], w=[vs.d])
        x.dma("sp", v[t * 128:(t + 1) * 128, :], vs[:], r=[vs.d], mw=[outs[2]])
    blocks.append((2560, 512, "tok", vh))
    for qb in range(2):
        mk_qk(3072 + qb * 512, 64, None, 8, cos8, sin8, None if dm else qiT[qb * 4:(qb + 1) * 4], outs[3],
              (qiT, qb * 4) if dm else None)

    def kwh(t, ps):
        emit_qk_post(x, c, qk_tiles, ps, 64, 64, gib, 8, cos8, sin8, t, qbf)
        p = ptr[cnt[0] % 2]
        ws = wst[cnt[0] % 2]
        cnt[0] += 1
        x.op("pe", lambda e: e.transpose(p[0:64, 0:128], qbf[:, 0:64], c.ident[:]), r=[qbf.d, c.ident.d], w=[p.d])
        x.op("dve", lambda e: e.tensor_copy(kist[:, t * 128:(t + 1) * 128], p[0:64, 0:128]), r=[p.d], mw=[kist.d])
        x.op("act", lambda e: e.activation(out=ws[:], in_=ps[:, 64:80], func=AF.Copy, scale=0.25), r=[ps.d], w=[ws.d])
        x.dma("sp", wi[t % 2, t // 2] if dm else wi[t * 128:(t + 1) * 128, :], ws[:], r=[ws.d], mw=[outs[5]])
        if t == NT - 1:
            x.dma("sp", kiT, kist[:], r=[kist.d], mw=[outs[4]])
    blocks.append((4096, 80, "tok", kwh))
    return blocks


def emit_LB_DSA(x, c, dram):
    NS = 16
    qTs = dram("qTs", [NS, 128, 2048], BF16, "ExternalInput")
    qiTs = dram("qiTs", [NS, 128, 1024], BF16, "ExternalInput")
    wis = dram("wis", [NS, 128, 16], F32, "ExternalInput")
    kT = dram("kT", [4, 128, S], BF16, "ExternalInput")
    v = dram("v", [S, 512], BF16, "ExternalInput")
    kiT2 = dram("kiT2", [128, S], BF16, "ExternalInput")
    dmask = dram("dmask", [2, 128, 128], F32, "ExternalInput")
    gqk = dram("gqk", [2, 128], F32, "ExternalInput")
    o = dram("o", [NS * 128, 2048], BF16, "ExternalOutput")
    SCALE = 128 ** -0.5
    with Scope(x) as st:
        d_in = x.mkdep("in")
        d_o = x.mkdep("o")
        ls = st
        gt = [load_bcast(x, ls, gqk[i], 128, "gqk%d" % i) for i in range(2)]
        gm = sbt(x, [128, 2], F32, "gm")
        negC = sbt(x, [128, 1], F32, "negC")
        for i in range(2):
            x.op("dve", lambda e: e.tensor_reduce(out=gm[:, i:i + 1], in_=gt[i][:], axis=AX.X, op=ALU.max,
                                                  apply_absolute_value=True), r=[gt[i].d], w=[gm.d])
        x.op("dve", lambda e: e.scalar_tensor_tensor(out=negC[:], in0=gm[:, 0:1], scalar=-(128 ** 0.5), in1=gm[:, 1:2],
                                                     op0=ALU.mult, op1=ALU.mult), r=[gm.d], w=[negC.d])
        kts = sbt(x, [128, 4, S], BF16, "kts")
        x.dma("sp", kts[:], kT.rearrange("g p t -> p g t"), w=[kts.d])
        va = sbt(x, [128, 32, 4, 129], BF16, "va")
        x.op("pool", lambda e: e.memset(va[:, :, :, 128:129], 1.0), w=[va.d])
        for g in range(4):
            x.dma("sp", va[:, :, g, 0:128], v[:, g * 128:(g + 1) * 128].rearrange("(kb p) d -> p kb d", p=128),
                  mw=[va.d])
        ki2 = sbt(x, [128, S], BF16, "ki2")
        x.dma("sp", ki2[:], kiT2, w=[ki2.d])
        dm = sbt(x, [128, 2, 128], F32, "dm")
        x.dma("sp", dm[:], dmask.rearrange("a p k -> p a k"), w=[dm.d])
        zr = sbt(x, [128, 512], BF16, "zr")
        x.op("pool", lambda e: e.memset(zr[:], 0.0), w=[zr.d])
        acc = sbt(x, [128, S], F32, "acc")
        work = sbt(x, [128, S], F32, "work")
        nb = sbt(x, [128, S], BF16, "nb")
        nbT4 = sbt(x, [128, 32, 4, 128], BF16, "nbT4")
        qs = [sbt(x, [128, 2048], BF16, "qs%d" % i) for i in range(2)]
        qis = [sbt(x, [128, 8, 128], BF16, "qis%d" % i) for i in range(2)]
        wt = [sbt(x, [128, 16], F32, "wt%d" % i) for i in range(2)]
        aw = sbt(x, [128, 16], F32, "aw")
        sgn = sbt(x, [128, 16], F32, "sgn")
        rr = [sbt(x, [128, 512], F32, "rr%d" % i) for i in range(2)]
        m8 = sbt(x, [128, 8], F32, "m8")
        thr = sbt(x, [128, 1], F32, "thr")
        thr0 = sbt(x, [128, 1], F32, "thr0")
        x.op("pool", lambda e: e.memset(thr0[:], -1e29), w=[thr0.d])
        P = [sbt(x, [128, 512], BF16, "P%d" % i) for i in range(2)]
        R = sbt(x, [128, 4], F32, "R")
        ob = [sbt(x, [128, 2048], BF16, "ob%d" % i) for i in range(2)]
        p_ix = [pst(x, [128, 512], F32, "p_ix%d" % i) for i in range(2)]
        p_tr = [pst(x, [128, 512], BF16, "p_tr%d" % i) for i in range(2)]
        p_s = [pst(x, [128, 512], F32, "p_s%d" % i) for i in range(2)]
        p_o = pst(x, [128, 4, 256], F32, "p_o")
        p_ob = p_o[:].rearrange("p a b -> p (a b)")
        nix = 0
        ns_ = 0
        def part_A(i):
            nonlocal nix
            b2 = i % 2
            nkb = 2 * i + 2
            L = nkb * 128
            x.dma("sp", qs[b2][:], qTs[i], r=[d_in], w=[qs[b2].d])
            x.dma("sp", qis[b2][:], qiTs[i].rearrange("p (a t) -> p a t", a=8), r=[d_in], w=[qis[b2].d])
            x.dma("sp", wt[b2][:], wis[i], r=[d_in], w=[wt[b2].d])
            w_ = wt[b2]
            x.op("dve", lambda e: e.scalar_tensor_tensor(out=aw[:], in0=w_[:], scalar=-1.0, in1=w_[:],
                                                         op0=ALU.mult, op1=ALU.max), r=[w_.d], w=[aw.d])
            x.op("dve", lambda e: e.tensor_scalar(out=aw[:], in0=aw[:], scalar1=0.125, scalar2=None, op0=ALU.mult),
                 r=[aw.d], w=[aw.d])
            x.op("dve", lambda e: e.tensor_scalar(out=sgn[:], in0=w_[:], scalar1=0.0, scalar2=2.0,
                                                  op0=ALU.is_ge, op1=ALU.mult), r=[w_.d], w=[sgn.d])
            x.op("dve", lambda e: e.tensor_scalar(out=sgn[:], in0=sgn[:], scalar1=-1.0, scalar2=None, op0=ALU.add),
                 r=[sgn.d], w=[sgn.d])
            for kq in range((L + 511) // 512):
                W = min(512, L - kq * 512)
                cs = slice(kq * 512, kq * 512 + W)
                for hi in range(16):
                    ps = p_ix[nix % 2]
                    r_ = rr[nix % 2]
                    nix += 1
                    pr_ = slice((hi % 2) * 64, (hi % 2) * 64 + 64)
                    x.op("pe", lambda e: e.matmul(ps[:, 0:W], qis[b2][pr_, hi // 2, :], ki2[pr_, cs], start=True, stop=True),
                         r=[qis[b2].d, ki2.d], w=[ps.d])
                    x.op("act", lambda e: e.activation(out=r_[:, 0:W], in_=ps[:, 0:W], func=AF.Relu, scale=aw[:, hi:hi + 1]),
                         r=[ps.d, aw.d], w=[r_.d])
                    if hi == 0:
                        x.op("dve", lambda e: e.tensor_scalar(out=acc[:, cs], in0=r_[:, 0:W], scalar1=sgn[:, 0:1],
                                                              scalar2=None, op0=ALU.mult), r=[r_.d, sgn.d], w=[acc.d])
                    else:
                        x.op("dve", lambda e: e.scalar_tensor_tensor(out=acc[:, cs], in0=r_[:, 0:W], scalar=sgn[:, hi:hi + 1],
                                                                     in1=acc[:, cs], op0=ALU.mult, op1=ALU.add),
                             r=[r_.d, sgn.d, acc.d], w=[acc.d])
            for a in range(2):
                ks = slice((nkb - 2 + a) * 128, (nkb - 1 + a) * 128)
                x.op("dve", lambda e: e.tensor_tensor(out=acc[:, ks], in0=acc[:, ks], in1=dm[:, a, :], op=ALU.add),
                     r=[acc.d, dm.d], w=[acc.d])
            if i >= 1:
                x.op("pool", lambda e: e.tensor_copy(work[:, 0:L], acc[:, 0:L]), r=[acc.d], w=[work.d])
                for rd in range(32):
                    x.op("dve", lambda e: e.max(out=m8[:], in_=work[:, 0:L]), r=[work.d], w=[m8.d])
                    if rd < 31:
                        x.op("dve", lambda e: e.match_replace(out=work[:, 0:L], in_to_replace=m8[:], in_values=work[:, 0:L],
                                                              imm_value=-1e30), r=[m8.d, work.d], w=[work.d])
                x.op("dve", lambda e: e.tensor_copy(thr[:], m8[:, 7:8]), r=[m8.d], w=[thr.d])
                th = thr
            else:
                th = thr0
            x.op("dve", lambda e: e.tensor_scalar(out=nb[:, 0:L], in0=acc[:, 0:L], scalar1=th[:, 0:1], scalar2=NEG,
                                                  op0=ALU.is_lt, op1=ALU.mult), r=[acc.d, th.d], w=[nb.d])

        def part_T(i):
            b2 = i % 2
            nkb = 2 * i + 2
            L = nkb * 128
            for kg in range((nkb + 3) // 4):
                p = p_tr[kg % 2]
                nn = min(4, nkb - kg * 4)
                for j in range(nn):
                    kb = kg * 4 + j
                    x.op("pe", lambda e: e.transpose(p[:, j * 128:(j + 1) * 128], nb[:, kb * 128:(kb + 1) * 128], c.ident[:]),
                         r=[nb.d, c.ident.d], w=[p.d], inc=(j == nn - 1))
                src = p[:, 0:nn * 128].rearrange("p (a b) -> p a b", a=nn)
                x.op("act", lambda e: e.activation(out=nbT4[:, kg * 4:kg * 4 + nn, :, :],
                                                   in_=bc(src.unsqueeze(2), [128, nn, 4, 128]), func=AF.Copy),
                     r=[p.d], w=[nbT4.d])

        def part_C(i):
            b2 = i % 2
            nkb = 2 * i + 2
            L = nkb * 128
            obb = ob[b2]
            asteps = [(g, kb) for g in range(4) for kb in range(nkb)]

            def a_scores(idx):
                g, kb = asteps[idx]
                ps = p_s[idx % 2]
                pp_ = P[idx % 2]
                x.op("pe", lambda e: e.matmul(ps[:], kts[:, g, kb * 128:(kb + 1) * 128],
                                              qs[b2][:, g * 512:(g + 1) * 512], start=True, stop=False),
                     r=[kts.d, qs[b2].d], w=[ps.d], inc=False)
                x.op("pe", lambda e: e.matmul(ps[:], c.ident[:], nbT4[:, kb, :, :].rearrange("p a b -> p (a b)"),
                                              start=False, stop=True), r=[c.ident.d, nbT4.d], w=[ps.d])
                x.op("act", lambda e: e.activation(out=pp_[:], in_=ps[:], func=AF.Exp, scale=SCALE, bias=negC[:, 0:1]),
                     r=[ps.d, negC.d], w=[pp_.d])

            def a_pv(idx):
                g, kb = asteps[idx]
                pp_ = P[idx % 2]
                for r in range(4):
                    x.op("pe", lambda e: e.matmul(p_o[:, r, 0:129], pp_[:, r * 128:(r + 1) * 128], va[:, kb, g, :],
                                                  start=False, stop=(kb == nkb - 1 and r % 2 == 1)),
                         r=[pp_.d, va.d], w=[p_o.d], inc=(r == 3))

            def a_epi(g):
                x.op("dve", lambda e: e.reciprocal(out=R[:], in_=p_o[:, :, 128:129].rearrange("p a b -> p (a b)")),
                     r=[p_o.d], w=[R.d])
                for r in range(4):
                    hh = g * 4 + r
                    x.op("act", lambda e: e.activation(out=obb[:, hh * 128:(hh + 1) * 128], in_=p_o[:, r, 0:128],
                                                       func=AF.Copy, scale=R[:, r:r + 1]), r=[p_o.d, R.d], mw=[obb.d])

            a_scores(0)
            for idx, (g, kb) in enumerate(asteps):
                if kb == 0:
                    for bnk in range(2):
                        x.op("pe", lambda e: e.matmul(p_ob[:, bnk * 512:(bnk + 1) * 512], zr[:, 0:128], zr[:],
                                                      start=True, stop=False), r=[zr.d], w=[p_o.d], inc=False)
                if idx + 1 < len(asteps):
                    a_scores(idx + 1)
                a_pv(idx)
                if kb == nkb - 1:
                    a_epi(g)
            x.dma("sp", o[i * 128:(i + 1) * 128, :], obb[:], r=[obb.d], mw=[d_o])

        part_A(0)
        part_T(0)
        for i in range(NS):
            if i + 1 < NS:
                part_A(i + 1)
            part_C(i)
            if i + 1 < NS:
                part_T(i + 1)


def _standalone(emit, *args):
    nc = bass.Bass("TRN2", target_bir_lowering=False)
    dram = lambda n, s, dt, k: nc.dram_tensor(n, list(s), dt, kind=k).ap()
    with ExitStack() as st:
        x = X(nc, st)
        c = make_consts(x)
        emit(x, c, dram, *args)
        x.global_barrier()
        print(emit.__name__, args, "sems", x.nsem, "cnt", x.cnt)
    return nc


def build_LA(kind):
    return _standalone(emit_LA, kind)


def build_LB_DA(lambda_init):
    return _standalone(emit_LB_DA, lambda_init)


def build_LB_SSD():
    return _standalone(emit_LB_SSD)


def build_LB_DSA():
    return _standalone(emit_LB_DSA)


def build_LC(FO):
    return _standalone(emit_LC, FO)


def _invf_table(half):
    invf = np.power(np.float32(ROPE_THETA), -np.arange(half, dtype=np.float32) / half).astype(np.float32)
    return np.ascontiguousarray(np.broadcast_to(invf[None, :], (128, half))).astype(np.float32)


def _ca(a):
    return np.ascontiguousarray(a)


GROUPS = [[0, 1], [2, 3], [4, 5], [6, 7]]
FIN_K = {0: 6144, 1: 10304, 2: 4176}


def _mk_dram(mapping):
    def dram(n, s, dt, k):
        ap = mapping[n]
        assert [int(v) for v in ap.shape] == [int(v) for v in s], (n, ap.shape, s)
        return ap
    return dram


def build_fused(depth=DEPTH):
    nc = bass.Bass("TRN2", target_bir_lowering=False)
    ext_in = lambda n, s, dt: nc.dram_tensor(n, list(s), dt, kind="ExternalInput").ap()
    internal = lambda n, s, dt: nc.dram_tensor(n, list(s), dt, kind="Internal").ap()
    x_in = ext_in("x_in", [TOK, D], F32)
    c_in = ext_in("c_in", [D], F32)
    pos = ext_in("pos", [TOK], I32)
    rk = ext_in("rk", [1, 1], I32)
    invf8 = ext_in("invf8", [128, 8], F32)
    invf16 = ext_in("invf16", [128, 16], F32)
    dmask = ext_in("dmask", [2, 128, 128], F32)
    x_out = nc.dram_tensor("x_out", [TOK, D], F32, kind="ExternalOutput").ap()
    xres = internal("xres", [TOK, D], F32)
    xmid = internal("xmid", [TOK, D], F32)
    scratch = {}

    def scr(n, s, dt):
        if n not in scratch:
            scratch[n] = internal(n, s, dt)
        return scratch[n]

    with ExitStack() as st:
        x = X(nc, st)
        c = make_consts(x)
        reg = st.enter_context(nc.gpsimd.register("rk"))
        nc.gpsimd.reg_load(reg, rk[0:1, 0:1])
        r = nc.gpsimd.snap(reg, min_val=0, max_val=1)
        d_g = x.mkdep("xchg")
        RS = bass.ds(r, 1)
        CH = 2 * 1024 * 1024
        MAXE = {BF16: 12 * 1024 * 1024 + 4096, F32: 1024 * 1024}
        GBS = {BF16: [], F32: []}
        goff = {BF16: 0, F32: 0}

        def gather(parts, both=False, stage=False):
            a0 = parts[0]
            dt = a0.dtype
            shp = [int(v) for v in a0.shape]
            rowe = int(np.prod(shp[1:]))
            c0 = max(1, min(shp[0], CH // (rowe * mybir.dt.size(dt))))
            while shp[0] % c0:
                c0 -= 1
            nch = shp[0] // c0
            ce = c0 * rowe
            assert nch * 2 * ce <= MAXE[dt], (nch, ce, dt)
            if goff[dt] >= len(GBS[dt]):
                GBS[dt].append(internal("GB%d_%d" % (mybir.dt.size(dt), len(GBS[dt])), [2, MAXE[dt]], dt))
            gb = GBS[dt][goff[dt]]
            goff[dt] += 1
            off = 0
            for d_, a in enumerate(parts):
                for k in range(nch):
                    x.op("pool", lambda e: e.collective_compute(
                        "AllGather", ALU.bypass, replica_groups=GROUPS, ins=[a[k * c0:(k + 1) * c0].opt()],
                        outs=[gb[d_, off + k * 2 * ce:off + (k + 1) * 2 * ce].opt()]), w=[d_g])
            x.global_barrier()
            tot = nch * 2 * ce
            if stage:
                gs = scr("GS%d_%d" % (mybir.dt.size(dt), goff[dt] - 1), [1, MAXE[dt]], dt)
                CP = 4 * 1024 * 1024
                for o_ in range(0, tot, CP):
                    n_ = min(CP, tot - o_)
                    x.dma("pool", gs[:, o_:o_ + n_], (gb[0:1] if both else gb[RS])[:, o_:o_ + n_], mw=[d_g])
                row = gs[:, 0:tot]
            else:
                row = (gb[0:1] if both else gb[RS])[:, 0:tot]
            names = ["e%d" % i_ for i_ in range(len(shp) - 1)]
            kw = {"k": nch, "s": 2, "c": c0}
            kw.update({n_: v_ for n_, v_ in zip(names, shp[1:])})
            return row.rearrange("a (k s c %s) -> (a k) s c %s" % (" ".join(names), " ".join(names)), **kw)

        def cp(dst, src):
            x.dma("pool", dst, src, mw=[d_g])

        for i in range(depth):
            kind, j = i % 3, i // 3
            xsrc = x_in if i == 0 else xres
            xdst = x_out if i == depth - 1 else xres
            sfx = "_%d" % i
            ada_i = internal("ada" + sfx, [6 * D], F32)
            mp = {"x_in": xsrc, "c_in": c_in, "pos": pos, "ada_w": ext_in("ada_w" + sfx, [D, 6 * D], F32),
                  "ada_b": ext_in("ada_b" + sfx, [6 * D], F32), "g1": ext_in("g1" + sfx, [D], F32), "ada": ada_i,
                  "w_in": ext_in("w_in" + sfx, [D, FIN_K[kind]], F32)}
            goff[BF16] = goff[F32] = 0
            if kind == 0:
                A = {"qT": scr("A_qT", [16, 128, TOK], BF16), "kT": scr("A_kT", [16, 128, TOK], BF16),
                     "v": scr("A_v", [2, TOK, 1024], BF16)}
                mp.update(A)
                mp.update({"invf": invf8, "gq": ext_in("gq" + sfx, [64], F32), "gk": ext_in("gk" + sfx, [64], F32)})
                emit_LA(x, c, _mk_dram(mp), kind, True)
                x.global_barrier()
                Gq = gather([A["qT"][0:8], A["qT"][8:16]])
                Gk = gather([A["kT"][0:8], A["kT"][8:16]])
                Gv = gather([A["v"][0], A["v"][1]])
                x.global_barrier()
                L_qT = scr("L_qT", [8, 128, S], BF16)
                L_kT = scr("L_kT", [8, 128, S], BF16)
                L_v = scr("L_v", [S, 1024], BF16)
                for s_ in range(2):
                    cs = slice(s_ * TOK, (s_ + 1) * TOK)
                    for kk in range(2):
                        cp(L_qT[kk * 4:(kk + 1) * 4, :, cs], Gq[kk, s_])
                        cp(L_kT[kk * 4:(kk + 1) * 4, :, cs], Gk[kk, s_])
                        cp(L_v[s_ * TOK + kk * 1024:s_ * TOK + (kk + 1) * 1024, :], Gv[kk, s_])
                x.global_barrier()
                B_o = scr("B_o", [S, 1024], BF16)
                li = 0.8 - 0.6 * math.exp(-0.3 * i)
                emit_LB_DA(x, c, _mk_dram({"qT": L_qT, "kT": L_kT, "v": L_v, "lam4": ext_in("lam4" + sfx, [4, 64], F32),
                                           "gqk": ext_in("gqk" + sfx, [2, 64], F32),
                                           "subg": ext_in("subg" + sfx, [128], F32), "o": B_o}), float(li))
                x.global_barrier()
                goff[BF16] = 0
                Go = gather([B_o[0:TOK], B_o[TOK:S]])
                x.global_barrier()
                FO = 2048
                L_o = scr("L_o", [TOK, 2048], BF16)
                for hh in range(2):
                    for kk in range(2):
                        cp(L_o[kk * 1024:(kk + 1) * 1024, hh * 1024:(hh + 1) * 1024], Go[kk, hh])
            elif kind == 1:
                A = {"z": scr("A_z", [2, TOK, 2048], BF16), "xbcT": scr("A_xbcT", [2, 3072, TOK], BF16),
                     "dtr": scr("A_dtr", [2, TOK, 32], F32)}
                mp.update(A)
                emit_LA(x, c, _mk_dram(mp), kind, True)
                x.global_barrier()
                Gz = gather([A["z"][0], A["z"][1]])
                Gx = gather([A["xbcT"][0], A["xbcT"][1]])
                Gd = gather([A["dtr"][0], A["dtr"][1]])
                x.global_barrier()
                L_raw = scr("L_raw", [3072, S], F32)
                L_z = scr("L_z", [S, 2048], BF16)
                L_dt = scr("L_dt", [S, 32], F32)
                for s_ in range(2):
                    cs = slice(s_ * TOK, (s_ + 1) * TOK)
                    for kk in range(6):
                        cp(L_raw[kk * 512:(kk + 1) * 512, cs], Gx[kk, s_])
                    for kk in range(4):
                        cp(L_z[s_ * TOK + kk * 512:s_ * TOK + (kk + 1) * 512, :], Gz[kk, s_])
                    cp(L_dt[cs, :], Gd[0, s_])
                x.global_barrier()
                B_y = scr("B_y", [S, 2048], BF16)
                emit_LB_SSD(x, c, _mk_dram({"raw": L_raw, "convw": ext_in("convw" + sfx, [4, 3072], F32),
                                            "convb": ext_in("convb" + sfx, [3072], F32), "dtr": L_dt,
                                            "hp": ext_in("hp" + sfx, [3, 32], F32), "z": L_z,
                                            "ng": ext_in("ng" + sfx, [2048], F32),
                                            "tokd": scr("tokd", [S, 2560], BF16), "featd": scr("featd", [1024, S], BF16),
                                            "y": B_y}))
                x.global_barrier()
                goff[BF16] = 0
                Gy = gather([B_y[0:TOK], B_y[TOK:S]])
                x.global_barrier()
                FO = 4096
                L_o = scr("L_o4", [TOK, 4096], BF16)
                for gh in range(2):
                    for kk in range(4):
                        cp(L_o[kk * 512:(kk + 1) * 512, gh * 2048:(gh + 1) * 2048], Gy[kk, gh])
            else:
                A = {"qT": scr("D_qT", [2, 16, 128, 8, 128], BF16), "kT": scr("D_kT", [4, 128, TOK], BF16),
                     "v": scr("D_v", [TOK, 512], BF16), "qiT": scr("D_qiT", [2, 8, 128, 8, 128], BF16),
                     "kiT": scr("D_kiT", [64, TOK], BF16), "wi": scr("D_wi", [2, 8, 128, 16], F32)}
                mp.update(A)
                mp.update({"invf16": invf16, "invf8": invf8, "gq": ext_in("gq" + sfx, [128], F32),
                           "gk": ext_in("gk" + sfx, [128], F32), "gi": ext_in("gi" + sfx, [64], F32)})
                emit_LA(x, c, _mk_dram(mp), kind, True)
                x.global_barrier()
                Gq = gather([A["qT"][0], A["qT"][1]], stage=True)
                Gqi = gather([A["qiT"][0], A["qiT"][1]], stage=True)
                Gw = gather([A["wi"][0], A["wi"][1]], stage=True)
                Gk = gather([A["kT"]], both=True)
                Gv = gather([A["v"]], both=True)
                Gki = gather([A["kiT"]], both=True)
                x.global_barrier()
                L_qTs = scr("L_qTs", [16, 128, 2048], BF16)
                L_qiTs = scr("L_qiTs", [16, 128, 1024], BF16)
                L_wis = scr("L_wis", [16, 128, 16], F32)
                L_kT = scr("L_kT4", [4, 128, S], BF16)
                L_v = scr("L_v4", [S, 512], BF16)
                L_ki = scr("L_ki2", [128, S], BF16)
                with nc.allow_non_contiguous_dma(reason="tile gathers"):
                    for sl_ in range(16):
                        s_, k_ = sl_ // 8, sl_ % 8
                        for kk in range(2):
                            cp(L_qTs[sl_][:, kk * 1024:(kk + 1) * 1024].rearrange("d (h t) -> d h t", h=8),
                               Gq[kk, s_][:, :, k_, :].rearrange("h d t -> d h t"))
                        cp(L_qiTs[sl_].rearrange("d (h t) -> d h t", h=8),
                           Gqi[0, s_][:, :, k_, :].rearrange("h d t -> d h t"))
                        cp(L_wis[sl_], Gw[0, s_][k_])
                for s_ in range(2):
                    cs = slice(s_ * TOK, (s_ + 1) * TOK)
                    cp(L_kT[:, :, cs], Gk[0, s_])
                    cp(L_v[cs, :], Gv[0, s_])
                    for dup in range(2):
                        cp(L_ki[dup * 64:(dup + 1) * 64, cs], Gki[0, s_])
                x.global_barrier()
                B_o2 = scr("B_o2", [TOK, 2048], BF16)
                emit_LB_DSA(x, c, _mk_dram({"qTs": L_qTs, "qiTs": L_qiTs, "wis": L_wis, "kT": L_kT, "v": L_v,
                                            "kiT2": L_ki, "dmask": dmask,
                                            "gqk": ext_in("gqk" + sfx, [2, 128], F32), "o": B_o2}))
                x.global_barrier()
                goff[BF16] = 0
                Go2 = gather([B_o2[0:1024], B_o2[1024:2048]], stage=True)
                x.global_barrier()
                FO = 2048
                L_o = scr("L_o", [TOK, 2048], BF16)
                for tl in range(16):
                    jj = tl // 2
                    cp(L_o[tl * 128:(tl + 1) * 128, :], Go2[jj // 4, tl % 2][(jj % 4) * 128:(jj % 4 + 1) * 128, :])
            x.global_barrier()
            emit_LC(x, c, _mk_dram({"x_in": xsrc, "o_in": L_o, "ada": ada_i, "g2": ext_in("g2" + sfx, [D], F32),
                                    "w_out": ext_in("w_out" + sfx, [FO, D], F32),
                                    "wgu": ext_in("wgu" + sfx, [D, 2 * FFN], F32),
                                    "wd": ext_in("wd" + sfx, [FFN, D], F32), "x_mid": xmid, "x_out": xdst}), FO)
            x.global_barrier()
        print("fused sems", x.nsem, "cnt", x.cnt)
    return nc


def fused_inputs(inp, depth=DEPTH):
    x = np.asarray(inp["x"], dtype=np.float32)
    c = np.asarray(inp["c"], dtype=np.float32)
    pos = np.asarray(inp["positions"]).astype(np.int32)
    tri = np.where(np.arange(128)[None, :] <= np.arange(128)[:, None], 0.0, -1e30).astype(np.float32)
    full_neg = np.full((128, 128), -1e30, np.float32)
    zero_m = np.zeros((128, 128), np.float32)
    f32 = lambda a: _ca(np.asarray(a, dtype=np.float32))
    maps = []
    for cc in range(8):
        b, h = cc // 2, cc % 2
        sl = slice(h * TOK, (h + 1) * TOK)
        m = {"x_in": _ca(x[b, sl]), "c_in": _ca(c[b]), "pos": _ca(pos[b, sl]), "rk": np.array([[h]], np.int32),
             "invf8": _invf_table(8), "invf16": _invf_table(16),
             "dmask": np.stack([tri, full_neg]) if h == 0 else np.stack([zero_m, tri])}
        for i in range(depth):
            kind, j = i % 3, i // 3
            sfx = "_%d" % i
            m["ada_w" + sfx] = inp["ada_w"][i]
            m["ada_b" + sfx] = inp["ada_b"][i]
            m["g1" + sfx] = inp["norm1_g"][i]
            m["g2" + sfx] = inp["norm2_g"][i]
            m["wgu" + sfx] = inp["ffn_w_gate_up"][i]
            m["wd" + sfx] = inp["ffn_w_down"][i]
            if kind == 0:
                m["w_in" + sfx] = inp["da_w_in"][j]
                m["w_out" + sfx] = inp["da_w_out"][j]
                m["gq" + sfx] = inp["da_q_norm_g"][j]
                m["gk" + sfx] = inp["da_k_norm_g"][j]
                m["lam4" + sfx] = f32(np.stack([inp["da_lambda_q1"][j], inp["da_lambda_k1"][j],
                                                inp["da_lambda_q2"][j], inp["da_lambda_k2"][j]]))
                m["gqk" + sfx] = f32(np.stack([inp["da_q_norm_g"][j], inp["da_k_norm_g"][j]]))
                m["subg" + sfx] = inp["da_subln_g"][j]
            elif kind == 1:
                m["w_in" + sfx] = inp["ssd_w_in"][j]
                m["w_out" + sfx] = inp["ssd_w_out"][j]
                ch = np.concatenate([np.arange(h * 2048, (h + 1) * 2048), np.arange(4096 + h * 512, 4096 + (h + 1) * 512),
                                     np.arange(5120 + h * 512, 5120 + (h + 1) * 512)])
                hs = slice(h * 32, (h + 1) * 32)
                m["convw" + sfx] = _ca(inp["ssd_conv_w"][j][:, ch])
                m["convb" + sfx] = _ca(inp["ssd_conv_b"][j][ch])
                m["hp" + sfx] = f32(np.stack([inp["ssd_dt_bias"][j][hs], inp["ssd_a_log"][j][hs], inp["ssd_d_skip"][j][hs]]))
                m["ng" + sfx] = _ca(inp["ssd_norm_g"][j][h * 2048:(h + 1) * 2048])
            else:
                m["w_in" + sfx] = inp["sa_w_in"][j]
                m["w_out" + sfx] = inp["sa_w_out"][j]
                m["gq" + sfx] = inp["sa_q_norm_g"][j]
                m["gk" + sfx] = inp["sa_k_norm_g"][j]
                m["gi" + sfx] = inp["sa_idx_k_norm_g"][j]
                m["gqk" + sfx] = f32(np.stack([inp["sa_q_norm_g"][j], inp["sa_k_norm_g"][j]]))
        maps.append(m)
    return maps


def kernel(**inp):
    depth = DEPTH
    nc = build_fused(depth)
    maps = fused_inputs(inp, depth)
    res = run_bass_kernel_spmd(nc, maps, core_ids=list(range(8))).results
    out = np.empty((NB, S, D), np.float32)
    for cc in range(8):
        b, h = cc // 2, cc % 2
        out[b, h * TOK:(h + 1) * TOK] = res[cc]["x_out"]
    return out
```

```python
import math
import numpy as np
from contextlib import ExitStack
import ml_dtypes
import concourse.bass as bass
import concourse.mybir as mybir
from concourse.bass_utils import run_bass_kernel_spmd

F32 = mybir.dt.float32
BF16 = mybir.dt.bfloat16
I32 = mybir.dt.int32
ALU = mybir.AluOpType
AF = mybir.ActivationFunctionType
AX = mybir.AxisListType

D = 2048
S = 4096
NB = 4
DEPTH = 4
KC = D // 128
TOK = 2048
NT = TOK // 128
FFN = 5632
EPS = 1e-6
ROPE_THETA = 500000.0
NEG = -30000.0

SAME_ENGINE_SYNC = True


class Dep:
    __slots__ = ("name", "w", "r", "dsem", "dq")

    def __init__(self, name=""):
        self.name = name
        self.w = {}
        self.r = {}
        self.dsem = None
        self.dq = None


class X:
    def __init__(self, nc, stack):
        self.nc = nc
        self.stack = stack
        self.root = stack
        self.eng = {"pe": nc.tensor, "act": nc.scalar, "dve": nc.vector,
                    "pool": nc.gpsimd, "sp": nc.sync}
        self.sem = {}
        self.cnt = {}
        self.seen = {}
        for k in self.eng:
            self.sem[k] = stack.enter_context(nc.semaphore("es_" + k))
            self.cnt[k] = 0
            self.seen[k] = {}
        self.nsem = 5
        self.semcnt = {}
        self.free_dsems = {"sp": [], "pool": [], "act": []}
        self.alltok = {}
        self.uid = 0

    def name(self, p):
        self.uid += 1
        return "%s_%d" % (p, self.uid)

    def sb(self, shape, dt, name="sb", stack=None):
        st = stack or self.stack
        return st.enter_context(self.nc.sbuf_tensor(self.name(name), list(shape), dt))

    def ps(self, shape, dt=F32, name="ps", stack=None):
        st = stack or self.stack
        return st.enter_context(self.nc.psum_tensor(self.name(name), list(shape), dt))

    def _wait(self, e, toks):
        en = self.eng[e]
        seen = self.seen[e]
        for s, v in toks.items():
            cv = self.semcnt.get(s)
            if cv is not None and cv > v:
                v = cv
            if seen.get(s, 0) >= v:
                continue
            if s is self.sem[e]:
                if e == "pe" or not SAME_ENGINE_SYNC:
                    continue
            en.wait_ge(s, v)
            seen[s] = v

    def _pre(self, e, r, w, mw=()):
        for d in r:
            self._wait(e, d.w)
        for d in w:
            self._wait(e, d.w)
            self._wait(e, d.r)
        for d in mw:
            self._wait(e, d.r)

    def _post(self, tok, r, w, mw=()):
        s, v = tok
        if self.alltok.get(s, 0) < v:
            self.alltok[s] = v
        for d in r:
            if d.r.get(s, 0) < v:
                d.r[s] = v
        for d in w:
            d.w = {s: v}
            d.r = {}
        for d in mw:
            if d.w.get(s, 0) < v:
                d.w[s] = v

    def op(self, e, fn, r=(), w=(), mw=(), inc=True):
        self._pre(e, r, w, mw)
        ins = fn(self.eng[e])
        if inc:
            self.cnt[e] += 1
            ins.then_inc(self.sem[e], 1)
            self._post((self.sem[e], self.cnt[e]), r, w, mw)
        else:
            self._post((self.sem[e], self.cnt[e] + 1), r, w, mw)
        return ins

    def dma(self, e, out, in_, r=(), w=(), mw=(), **kw):
        host = (list(w) + list(mw))[0]
        assert host.dq in (None, e), (host.name, host.dq, e)
        if host.dsem is None:
            host.dq = e
            if self.free_dsems[e]:
                host.dsem = self.free_dsems[e].pop()
            else:
                host.dsem = self.root.enter_context(self.nc.semaphore(self.name("ds")))
                self.semcnt[host.dsem] = 0
                self.nsem += 1
        self._pre(e, r, w, mw)
        ins = self.eng[e].dma_start(out=out, in_=in_, **kw)
        self.semcnt[host.dsem] += 16
        ins.then_inc(host.dsem, 16)
        self._post((host.dsem, self.semcnt[host.dsem]), r, w, mw)
        return ins

    def mkdep(self, name=""):
        d = Dep(name)
        if isinstance(self.stack, Scope):
            self.stack.deps.append(d)
        return d

    def global_barrier(self):
        for e in self.eng:
            self._wait(e, dict(self.alltok))

    def barrier(self, deps):
        toks = {}
        for d in deps:
            for src in (d.w, d.r):
                for s_, v in src.items():
                    if toks.get(s_, 0) < v:
                        toks[s_] = v
        for e in self.eng:
            self._wait(e, toks)

    def finish(self, deps, e="sp"):
        for d in deps:
            self._wait(e, d.w)


class Scope:
    def __init__(self, x):
        self.x = x
        self.st = ExitStack()
        self.deps = []

    def __enter__(self):
        self.st.__enter__()
        self.prev = self.x.stack
        self.x.stack = self
        return self

    def __exit__(self, *a):
        self.x.stack = self.prev
        if a[0] is None:
            self.x.barrier(self.deps)
            for d in self.deps:
                if d.dsem is not None:
                    self.x.free_dsems[d.dq].append(d.dsem)
                    d.dsem = None
                    d.dq = None
        return self.st.__exit__(*a)

    def enter_context(self, cm):
        return self.st.enter_context(cm)


class T:
    def __init__(self, x, shape, dt, name, psum=False, stack=None):
        stack = stack or x.stack
        self.t = x.ps(shape, dt, name, stack) if psum else x.sb(shape, dt, name, stack)
        self.d = Dep(name)
        if isinstance(stack, Scope):
            stack.deps.append(self.d)

    def __getitem__(self, k):
        return self.t[k]


def sbt(x, shape, dt, name, stack=None):
    return T(x, shape, dt, name, False, stack)


def pst(x, shape, dt, name, stack=None):
    return T(x, shape, dt, name, True, stack)


class Consts:
    pass


def make_consts(x):
    c = Consts()
    idf = sbt(x, [128, 128], F32, "idf")
    c.ident = sbt(x, [128, 128], BF16, "ident")
    x.op("pool", lambda e: e.memset(idf[:], 1.0), w=[idf.d])
    x.op("pool", lambda e: e.affine_select(out=idf[:], in_=idf[:], pattern=[[-1, 128]],
                                           compare_op=ALU.is_equal, fill=0.0, base=0,
                                           channel_multiplier=1), r=[idf.d], w=[idf.d])
    x.op("dve", lambda e: e.tensor_copy(c.ident[:], idf[:]), r=[idf.d], w=[c.ident.d])
    c.identf = idf
    ngf = sbt(x, [128, 128], F32, "ngf")
    c.negT = sbt(x, [128, 128], BF16, "negT")
    x.op("pool", lambda e: e.memset(ngf[:], 0.0), w=[ngf.d])
    x.op("pool", lambda e: e.affine_select(out=ngf[:], in_=ngf[:], pattern=[[1, 128]],
                                           compare_op=ALU.is_ge, fill=NEG, base=0,
                                           channel_multiplier=-1), r=[ngf.d], w=[ngf.d])
    x.op("dve", lambda e: e.tensor_copy(c.negT[:], ngf[:]), r=[ngf.d], w=[c.negT.d])
    c.negQ = sbt(x, [128, 128], F32, "negQ")
    x.op("pool", lambda e: e.memset(c.negQ[:], 0.0), w=[c.negQ.d])
    x.op("pool", lambda e: e.affine_select(out=c.negQ[:], in_=c.negQ[:], pattern=[[-1, 128]],
                                           compare_op=ALU.is_ge, fill=-1e30, base=0,
                                           channel_multiplier=1), r=[c.negQ.d], w=[c.negQ.d])
    trf = sbt(x, [128, 128], F32, "trf")
    c.tri = sbt(x, [128, 128], BF16, "tri")
    x.op("pool", lambda e: e.memset(trf[:], 1.0), w=[trf.d])
    x.op("pool", lambda e: e.affine_select(out=trf[:], in_=trf[:], pattern=[[1, 128]],
                                           compare_op=ALU.is_ge, fill=0.0, base=0,
                                           channel_multiplier=-1), r=[trf.d], w=[trf.d])
    x.op("dve", lambda e: e.tensor_copy(c.tri[:], trf[:]), r=[trf.d], w=[c.tri.d])
    c.trif = trf
    c.ones = sbt(x, [128, 128], BF16, "ones")
    x.op("pool", lambda e: e.memset(c.ones[:], 1.0), w=[c.ones.d])
    c.onesf = sbt(x, [128, 128], F32, "onesf")
    x.op("pool", lambda e: e.memset(c.onesf[:], 1.0), w=[c.onesf.d])
    c.nhalf = sbt(x, [128, 64], F32, "nhalf")
    x.op("pool", lambda e: e.memset(c.nhalf[:], -0.5), w=[c.nhalf.d])
    return c


def rsqrt_mean(x, c, out, ss, n, width):
    x.op("dve", lambda e: e.tensor_scalar(out=out[:, 0:width], in0=ss[:, 0:width], scalar1=1.0 / n,
                                          scalar2=EPS, op0=ALU.mult, op1=ALU.add),
         r=[ss.d], w=[out.d])
    x.op("pool", lambda e: e.tensor_tensor(out=out[:, 0:width], in0=out[:, 0:width],
                                           in1=c.nhalf[:, 0:width], op=ALU.pow),
         r=[out.d, c.nhalf.d], w=[out.d])


def bc(ap, shape):
    return ap.to_broadcast(list(shape))


def emit_ada(x, c_ap, adaw_ap, adab_ap, ada_ap, d_ada):
    with Scope(x) as ls:
        cs = sbt(x, [128, 16], F32, "c_sb", ls)
        ca = sbt(x, [128, 16], F32, "c_act", ls)
        brow = sbt(x, [1, 6 * D], F32, "brow", ls)
        arow = sbt(x, [1, 6 * D], F32, "arow", ls)
        wt = [sbt(x, [128, 16, 512], F32, "adaw%d" % i, ls) for i in range(2)]
        pa = [pst(x, [1, 512], F32, "adap%d" % i, ls) for i in range(2)]
        x.dma("sp", cs[:], c_ap.rearrange("(p k) -> p k", k=16), w=[cs.d])
        x.dma("sp", brow[:], adab_ap.rearrange("(o n) -> o n", o=1), w=[brow.d])
        x.op("act", lambda e: e.activation(out=ca[:], in_=cs[:], func=AF.Silu), r=[cs.d], w=[ca.d])
        for blk in range(24):
            i = blk % 2
            x.dma("sp", wt[i][:], adaw_ap[:, blk * 512:(blk + 1) * 512].rearrange("(p k) f -> p k f", k=16),
                  w=[wt[i].d])
            for k in range(16):
                x.op("pe", lambda e: e.matmul(pa[i][:], ca[:, k:k + 1], wt[i][:, k, :],
                                              start=(k == 0), stop=(k == 15)),
                     r=[ca.d, wt[i].d], w=[pa[i].d], inc=(k == 15))
            x.op("dve", lambda e: e.tensor_tensor(out=arow[0:1, blk * 512:(blk + 1) * 512], in0=pa[i][:],
                                                  in1=brow[0:1, blk * 512:(blk + 1) * 512], op=ALU.add),
                 r=[pa[i].d, brow.d], mw=[arow.d])
        x.dma("sp", ada_ap.rearrange("(o n) -> o n", o=1), arow[:], r=[arow.d], w=[d_ada])


def load_cols(x, ls, vec_ap, name, d_src=None):
    t = sbt(x, [128, 16], F32, name, ls)
    x.dma("sp", t[:], vec_ap.rearrange("(k p) -> p k", p=128), r=([d_src] if d_src else []), w=[t.d],
          allow_slow_non_contiguous=True)
    return t


def load_bcast(x, ls, vec_ap, n, name, d_src=None, dt=F32):
    t = sbt(x, [128, n], dt, name, ls)
    x.dma("sp", t[:], vec_ap.rearrange("(o n) -> o n", o=1).partition_broadcast(128),
          r=([d_src] if d_src else []), w=[t.d])
    return t


def emit_mod_cols(x, ls, g_ap, ada_ap, d_ada, scale_idx, shift_idx):
    g = load_cols(x, ls, g_ap, "gcol")
    sc = load_cols(x, ls, ada_ap[scale_idx * D:(scale_idx + 1) * D], "sccol", d_ada)
    sh = load_cols(x, ls, ada_ap[shift_idx * D:(shift_idx + 1) * D], "shcol", d_ada)
    sT = sbt(x, [128, 16], F32, "sT", ls)
    x.op("dve", lambda e: e.scalar_tensor_tensor(out=sT[:], in0=sc[:], scalar=1.0, in1=g[:],
                                                 op0=ALU.add, op1=ALU.mult),
         r=[sc.d, g.d], w=[sT.d])
    return sT, sh


def emit_norm_hT(x, c, x_ap, d_x, ntiles, sT, shT, hT, hT_d, tile_off=0):
    with Scope(x) as ls:
        xt = [sbt(x, [128, D], F32, "xt%d" % i, ls) for i in range(2)]
        xn = [sbt(x, [128, D], BF16, "xn%d" % i, ls) for i in range(2)]
        junk = sbt(x, [128, D], BF16, "junk", ls)
        ss = [sbt(x, [128, 1], F32, "ss%d" % i, ls) for i in range(2)]
        rstd = [sbt(x, [128, 1], F32, "rstd%d" % i, ls) for i in range(2)]
        pt = [pst(x, [128, 512], BF16, "ptn%d" % i, ls) for i in range(2)]
        for t in range(ntiles):
            i = t % 2
            x.dma("sp", xt[i][:], x_ap[t * 128:(t + 1) * 128, :], r=[d_x], w=[xt[i].d])
            x.op("act", lambda e: e.activation(out=junk[:], in_=xt[i][:], func=AF.Square,
                                               accum_out=ss[i][:, 0:1]),
                 r=[xt[i].d], w=[junk.d, ss[i].d])
            rsqrt_mean(x, c, rstd[i], ss[i], D, 1)
            x.op("act", lambda e: e.activation(out=xn[i][:], in_=xt[i][:], func=AF.Copy,
                                               scale=rstd[i][:, 0:1]),
                 r=[xt[i].d, rstd[i].d], w=[xn[i].d])
            for g in range(4):
                p = pt[g % 2]
                for j in range(4):
                    kc = g * 4 + j
                    x.op("pe", lambda e: e.transpose(p[:, j * 128:(j + 1) * 128],
                                                     xn[i][:, kc * 128:(kc + 1) * 128], c.ident[:]),
                         r=[xn[i].d, c.ident.d], w=[p.d], inc=(j == 3))
                for j in range(4):
                    kc = g * 4 + j
                    dst = hT[:, kc, (tile_off + t) * 128:(tile_off + t + 1) * 128]
                    if True:
                        x.op("dve", lambda e: e.tensor_scalar(out=dst, in0=p[:, j * 128:(j + 1) * 128],
                                                              scalar1=sT[:, kc:kc + 1], scalar2=shT[:, kc:kc + 1],
                                                              op0=ALU.mult, op1=ALU.add),
                             r=[p.d, sT.d, shT.d], mw=[hT_d[tile_off + t]])
                    else:
                        x.op("act", lambda e: e.activation(out=dst, in_=p[:, j * 128:(j + 1) * 128],
                                                           func=AF.Identity, scale=sT[:, kc:kc + 1],
                                                           bias=shT[:, kc:kc + 1]),
                             r=[p.d, sT.d, shT.d], mw=[hT_d[tile_off + t]])


def emit_rope_tables(x, ls, pos_ap, invf_ap, half, ntiles):
    n = ntiles * half
    posi = sbt(x, [128, ntiles], I32, "posi", ls)
    posf = sbt(x, [128, ntiles], F32, "posf", ls)
    invf = sbt(x, [128, half], F32, "invf", ls)
    ang = sbt(x, [128, ntiles, half], F32, "ang", ls)
    x.dma("sp", posi[:], pos_ap.rearrange("(t p) -> p t", p=128), w=[posi.d], allow_slow_non_contiguous=True)
    x.dma("sp", invf[:], invf_ap, w=[invf.d])
    x.op("dve", lambda e: e.tensor_copy(posf[:], posi[:]), r=[posi.d], w=[posf.d])
    x.op("dve", lambda e: e.tensor_tensor(out=ang[:], in0=bc(posf[:].unsqueeze(2), [128, ntiles, half]),
                                          in1=bc(invf[:].unsqueeze(1), [128, ntiles, half]), op=ALU.mult),
         r=[posf.d, invf.d], w=[ang.d])
    outs = []
    C1 = 6.28125
    C2 = 2.0 * math.pi - C1
    for nm, shift in (("cos", math.pi / 2), ("sin", 0.0)):
        a = sbt(x, [128, n], F32, nm + "_a", ls)
        ki = sbt(x, [128, n], I32, nm + "_ki", ls)
        kf = sbt(x, [128, n], F32, nm + "_kf", ls)
        m = sbt(x, [128, n], F32, nm + "_m", ls)
        res = sbt(x, [128, ntiles, half], F32, nm + "_t", ls)
        af = ang[:].rearrange("p t h -> p (t h)")
        x.op("dve", lambda e: e.tensor_scalar(out=a[:], in0=af, scalar1=shift, scalar2=None, op0=ALU.add),
             r=[ang.d], w=[a.d])
        x.op("dve", lambda e: e.tensor_scalar(out=kf[:], in0=a[:], scalar1=1.0 / (2 * math.pi), scalar2=None,
                                              op0=ALU.mult), r=[a.d], w=[kf.d])
        x.op("dve", lambda e: e.tensor_copy(ki[:], kf[:]), r=[kf.d], w=[ki.d])
        x.op("dve", lambda e: e.tensor_copy(kf[:], ki[:]), r=[ki.d], w=[kf.d])
        x.op("dve", lambda e: e.scalar_tensor_tensor(out=a[:], in0=kf[:], scalar=-C1, in1=a[:],
                                                     op0=ALU.mult, op1=ALU.add), r=[kf.d, a.d], w=[a.d])
        x.op("dve", lambda e: e.scalar_tensor_tensor(out=a[:], in0=kf[:], scalar=-C2, in1=a[:],
                                                     op0=ALU.mult, op1=ALU.add), r=[kf.d, a.d], w=[a.d])
        x.op("dve", lambda e: e.tensor_scalar(out=m[:], in0=a[:], scalar1=math.pi, scalar2=2 * math.pi,
                                              op0=ALU.is_gt, op1=ALU.mult), r=[a.d], w=[m.d])
        x.op("dve", lambda e: e.tensor_tensor(out=a[:], in0=a[:], in1=m[:], op=ALU.subtract),
             r=[a.d, m.d], w=[a.d])
        x.op("dve", lambda e: e.tensor_scalar(out=m[:], in0=a[:], scalar1=-math.pi, scalar2=2 * math.pi,
                                              op0=ALU.is_lt, op1=ALU.mult), r=[a.d], w=[m.d])
        x.op("dve", lambda e: e.tensor_tensor(out=a[:], in0=a[:], in1=m[:], op=ALU.add),
             r=[a.d, m.d], w=[a.d])
        x.op("dve", lambda e: e.tensor_scalar(out=a[:], in0=a[:], scalar1=-3.1415925, scalar2=3.1415925,
                                              op0=ALU.max, op1=ALU.min), r=[a.d], w=[a.d])
        x.op("act", lambda e: e.activation(out=res[:].rearrange("p t h -> p (t h)"), in_=a[:], func=AF.Sin),
             r=[a.d], w=[res.d])
        outs.append(res)
    return outs[0], outs[1]


def emit_qk_post(x, c, ls_tiles, ps, ncols, gdim, gain, half, cos, sin, t, out_bf):
    ng = ncols // gdim
    qn, sq, ssq, rs, t1, t2, t3, t4 = ls_tiles
    pv = ps[:, 0:ncols].rearrange("p (g d) -> p g d", d=gdim)
    qv = qn[:, 0:ncols].rearrange("p (g d) -> p g d", d=gdim)
    if gain is not None:
        x.op("act", lambda e: e.activation(out=sq[:, 0:ncols], in_=ps[:, 0:ncols], func=AF.Square),
             r=[ps.d], w=[sq.d])
        x.op("dve", lambda e: e.tensor_reduce(out=ssq[:, 0:ng], in_=sq[:, 0:ncols].rearrange("p (g d) -> p g d", d=gdim),
                                              axis=AX.X, op=ALU.add), r=[sq.d], w=[ssq.d])
        rsqrt_mean(x, c, rs, ssq, gdim, ng)
        x.op("dve", lambda e: e.tensor_tensor(out=qv, in0=pv, in1=bc(rs[:, 0:ng].unsqueeze(2), [128, ng, gdim]),
                                              op=ALU.mult), r=[ps.d, rs.d], w=[qn.d])
        x.op("pool", lambda e: e.tensor_tensor(out=qv, in0=qv, in1=bc(gain[:, 0:gdim].unsqueeze(1), [128, ng, gdim]),
                                               op=ALU.mult), r=[qn.d, gain.d], w=[qn.d])
    else:
        x.op("act", lambda e: e.activation(out=qn[:, 0:ncols], in_=ps[:, 0:ncols], func=AF.Copy),
             r=[ps.d], w=[qn.d])
    if half:
        x1 = qv[:, :, 0:half]
        x2 = qv[:, :, half:2 * half]
        cb = bc(cos[:, t, :].unsqueeze(1), [128, ng, half])
        sb_ = bc(sin[:, t, :].unsqueeze(1), [128, ng, half])
        tv = [tt[:, 0:ng * half].rearrange("p (g h) -> p g h", h=half) for tt in (t1, t2, t3, t4)]
        x.op("dve", lambda e: e.tensor_tensor(out=tv[0], in0=x1, in1=cb, op=ALU.mult), r=[qn.d, cos.d], w=[t1.d])
        x.op("dve", lambda e: e.tensor_tensor(out=tv[1], in0=x2, in1=sb_, op=ALU.mult), r=[qn.d, sin.d], w=[t2.d])
        x.op("pool", lambda e: e.tensor_tensor(out=tv[2], in0=x2, in1=cb, op=ALU.mult), r=[qn.d, cos.d], w=[t3.d])
        x.op("pool", lambda e: e.tensor_tensor(out=tv[3], in0=x1, in1=sb_, op=ALU.mult), r=[qn.d, sin.d], w=[t4.d])
        x.op("dve", lambda e: e.tensor_tensor(out=x1, in0=tv[0], in1=tv[1], op=ALU.subtract),
             r=[t1.d, t2.d], w=[qn.d])
        x.op("pool", lambda e: e.tensor_tensor(out=x2, in0=tv[2], in1=tv[3], op=ALU.add),
             r=[t3.d, t4.d, qn.d], w=[qn.d])
    x.op("act", lambda e: e.activation(out=out_bf[:, 0:ncols], in_=qn[:, 0:ncols], func=AF.Copy),
         r=[qn.d], w=[out_bf.d])


def emit_qk_post2(x, c, ls_tiles, ps, gdim, gain, half, cos, sin, t0, out_bf):
    ng = 512 // gdim
    ng2 = 2 * ng
    qn, sq, ssq, rs, t1, t2, t3, t4 = ls_tiles
    psf = ps[:].rearrange("p a c -> p (a c)")
    pv = ps[:].rearrange("p a (g d) -> p (a g) d", d=gdim)
    qv = qn[:, 0:1024].rearrange("p (g d) -> p g d", d=gdim)
    if gain is not None:
        x.op("act", lambda e: e.activation(out=sq[:, 0:1024], in_=psf, func=AF.Square), r=[ps.d], w=[sq.d])
        x.op("dve", lambda e: e.tensor_reduce(out=ssq[:, 0:ng2], in_=sq[:, 0:1024].rearrange("p (g d) -> p g d", d=gdim),
                                              axis=AX.X, op=ALU.add), r=[sq.d], w=[ssq.d])
        rsqrt_mean(x, c, rs, ssq, gdim, ng2)
        x.op("dve", lambda e: e.tensor_tensor(out=qv, in0=pv, in1=bc(rs[:, 0:ng2].unsqueeze(2), [128, ng2, gdim]),
                                              op=ALU.mult), r=[ps.d, rs.d], w=[qn.d])
        x.op("pool", lambda e: e.tensor_tensor(out=qv, in0=qv, in1=bc(gain[:, 0:gdim].unsqueeze(1), [128, ng2, gdim]),
                                               op=ALU.mult), r=[qn.d, gain.d], w=[qn.d])
    else:
        x.op("act", lambda e: e.activation(out=qn[:, 0:1024], in_=psf, func=AF.Copy), r=[ps.d], w=[qn.d])
    q4 = qn[:, 0:1024].rearrange("p (a g d) -> p a g d", a=2, d=gdim)
    x1 = q4[:, :, :, 0:half]
    x2 = q4[:, :, :, half:2 * half]
    cb = bc(cos[:, t0:t0 + 2, :].unsqueeze(2), [128, 2, ng, half])
    sb_ = bc(sin[:, t0:t0 + 2, :].unsqueeze(2), [128, 2, ng, half])
    tv = [tt[:, 0:ng2 * half].rearrange("p (a g h) -> p a g h", a=2, h=half) for tt in (t1, t2, t3, t4)]
    x.op("dve", lambda e: e.tensor_tensor(out=tv[0], in0=x1, in1=cb, op=ALU.mult), r=[qn.d, cos.d], w=[t1.d])
    x.op("dve", lambda e: e.tensor_tensor(out=tv[1], in0=x2, in1=sb_, op=ALU.mult), r=[qn.d, sin.d], w=[t2.d])
    x.op("pool", lambda e: e.tensor_tensor(out=tv[2], in0=x2, in1=cb, op=ALU.mult), r=[qn.d, cos.d], w=[t3.d])
    x.op("pool", lambda e: e.tensor_tensor(out=tv[3], in0=x1, in1=sb_, op=ALU.mult), r=[qn.d, sin.d], w=[t4.d])
    x.op("dve", lambda e: e.tensor_tensor(out=x1, in0=tv[0], in1=tv[1], op=ALU.subtract),
         r=[t1.d, t2.d], w=[qn.d])
    x.op("pool", lambda e: e.tensor_tensor(out=x2, in0=tv[2], in1=tv[3], op=ALU.add),
         r=[t3.d, t4.d, qn.d], w=[qn.d])
    x.op("act", lambda e: e.activation(out=out_bf[:, 0:1024], in_=qn[:, 0:1024], func=AF.Copy),
         r=[qn.d], w=[out_bf.d])


def alloc_qk_tiles(x, ls):
    qn = sbt(x, [128, 1024], F32, "qn", ls)
    sq = sbt(x, [128, 1024], F32, "sq", ls)
    ssq = sbt(x, [128, 16], F32, "ssq", ls)
    rs = sbt(x, [128, 16], F32, "rs", ls)
    ts = [sbt(x, [128, 128], F32, "rt%d" % i, ls) for i in range(4)]
    return (qn, sq, ssq, rs, ts[0], ts[1], ts[2], ts[3])


class ProjCtx:
    pass


def emit_proj(x, c, w_ap, hT, hT_d, ntiles, blocks):
    with Scope(x) as ls:
        wb = [sbt(x, [128, 16, 512], BF16, "wb%d" % i, ls) for i in range(2)]
        pp = [pst(x, [128, 512], F32, "pp%d" % i, ls) for i in range(2)]
        pp2 = ([pst(x, [128, 2, 512], F32, "pp2_%d" % i, ls) for i in range(2)]
               if any(b_[2] == "tok2" for b_ in blocks) else None)
        n = 0
        for bi, (col0, ncols, mode, handler) in enumerate(blocks):
            wt = wb[bi % 2]
            x.dma("pool", wt[:, :, 0:ncols], w_ap[:, col0:col0 + ncols].rearrange("(k p) f -> p k f", p=128),
                  w=[wt.d])
            if mode == "tok":
                for t in range(ntiles):
                    ps = pp[n % 2]
                    n += 1
                    for kc in range(16):
                        x.op("pe", lambda e: e.matmul(ps[:, 0:ncols], hT[:, kc, t * 128:(t + 1) * 128],
                                                      wt[:, kc, 0:ncols], start=(kc == 0), stop=(kc == 15)),
                             r=[hT_d[t], wt.d], w=[ps.d], inc=(kc == 15))
                    handler(t, ps)
            elif mode == "tok2":
                for t in range(0, ntiles, 2):
                    ps = pp2[n % 2]
                    n += 1
                    for a_ in range(2):
                        for kc in range(16):
                            x.op("pe", lambda e: e.matmul(ps[:, a_, 0:ncols], hT[:, kc, (t + a_) * 128:(t + a_ + 1) * 128],
                                                          wt[:, kc, 0:ncols], start=(kc == 0), stop=(kc == 15)),
                                 r=[hT_d[t + a_], wt.d], w=[ps.d], inc=(kc == 15))
                    handler(t, ps)
            else:
                for fc in range(ncols // 128):
                    for tb in range(ntiles // 4):
                        ps = pp[n % 2]
                        n += 1
                        for kc in range(16):
                            x.op("pe", lambda e: e.matmul(ps[:, :], wt[:, kc, fc * 128:(fc + 1) * 128],
                                                          hT[:, kc, tb * 512:(tb + 1) * 512],
                                                          start=(kc == 0), stop=(kc == 15)),
                                 r=hT_d[tb * 4:tb * 4 + 4] + [wt.d], w=[ps.d], inc=(kc == 15))
                        handler(fc, tb, ps)


def emit_LA(x, c, dram, kind, dm=False):
    x_in = dram("x_in", [TOK, D], F32, "ExternalInput")
    c_in = dram("c_in", [D], F32, "ExternalInput")
    pos = dram("pos", [TOK], I32, "ExternalInput")
    adaw = dram("ada_w", [D, 6 * D], F32, "ExternalInput")
    adab = dram("ada_b", [6 * D], F32, "ExternalInput")
    g1 = dram("g1", [D], F32, "ExternalInput")
    ada = dram("ada", [6 * D], F32, "ExternalOutput")
    FIN = {0: 6144, 1: 10304, 2: 4176}[kind]
    w_in = dram("w_in", [D, FIN], F32, "ExternalInput")
    outs = []
    with Scope(x) as st:
        d_ada = x.mkdep("ada")
        d_x = x.mkdep("x")
        emit_ada(x, c_in, adaw, adab, ada, d_ada)
        hT = x.sb([128, 16, TOK], BF16, "hT")
        hT_d = [x.mkdep("hT%d" % t) for t in range(NT)]
        with Scope(x) as ls:
            sT, shT = emit_mod_cols(x, ls, g1, ada, d_ada, 1, 0)
            emit_norm_hT(x, c, x_in, d_x, NT, sT, shT, hT, hT_d)
        ls = st
        if kind == 0:
            invf = dram("invf", [128, 8], F32, "ExternalInput")
            gq = dram("gq", [64], F32, "ExternalInput")
            gk = dram("gk", [64], F32, "ExternalInput")
            qT = dram("qT", [16, 128, TOK], BF16, "ExternalOutput")
            kT = dram("kT", [16, 128, TOK], BF16, "ExternalOutput")
            v = dram("v", [2, TOK, 1024] if dm else [TOK, 2048], BF16, "ExternalOutput")
            outs = [x.mkdep("qT"), x.mkdep("kT"), x.mkdep("v")]
            cos, sin = emit_rope_tables(x, ls, pos, invf, 8, NT)
            gqb = load_bcast(x, ls, gq, 64, "gqb")
            gkb = load_bcast(x, ls, gk, 64, "gkb")
            qk_tiles = alloc_qk_tiles(x, ls)
            qbf = sbt(x, [128, 1024], BF16, "qbf", ls)
            stage = [sbt(x, [128, 4, TOK], BF16, "stage%d" % i, ls) for i in range(2)]
            ptr = [pst(x, [128, 512], BF16, "ptr%d" % i, ls) for i in range(2)]
            vst = [sbt(x, [128, 512], BF16, "vst%d" % i, ls) for i in range(2)]
            blocks = []
            cnt = [0]
            for which in range(2):
                for hb in range(4):
                    def handler(t, ps, which=which, hb=hb):
                        sg = stage[(which * 4 + hb) % 2]
                        emit_qk_post2(x, c, qk_tiles, ps, 64, gqb if which == 0 else gkb, 8, cos, sin, t, qbf)
                        for a_ in range(2):
                            p = ptr[cnt[0] % 2]
                            cnt[0] += 1
                            for j in range(4):
                                x.op("pe", lambda e: e.transpose(p[:, j * 128:(j + 1) * 128],
                                                                 qbf[:, a_ * 512 + j * 128:a_ * 512 + (j + 1) * 128], c.ident[:]),
                                     r=[qbf.d, c.ident.d], w=[p.d], inc=(j == 3))
                            x.op("dve", lambda e: e.tensor_copy(sg[:, :, (t + a_) * 128:(t + a_ + 1) * 128],
                                                                p[:, :].rearrange("p (a b) -> p a b", a=4)),
                                 r=[p.d], mw=[sg.d])
                        if t == NT - 2:
                            dst = (qT if which == 0 else kT)[hb * 4:(hb + 1) * 4, :, :].rearrange("h p t -> p h t")
                            x.dma("sp", dst, sg[:], r=[sg.d], mw=[outs[which]])
                    blocks.append((which * 2048 + hb * 512, 512, "tok2", handler))
            for vb in range(4):
                def vhandler(t, ps, vb=vb):
                    vs = vst[cnt[0] % 2]
                    cnt[0] += 1
                    x.op("act", lambda e: e.activation(out=vs[:], in_=ps[:, :], func=AF.Copy), r=[ps.d], w=[vs.d])
                    vdst = (v[vb // 2, t * 128:(t + 1) * 128, (vb % 2) * 512:(vb % 2 + 1) * 512] if dm
                            else v[t * 128:(t + 1) * 128, vb * 512:(vb + 1) * 512])
                    x.dma("sp", vdst, vs[:], r=[vs.d], mw=[outs[2]])
                blocks.append((4096 + vb * 512, 512, "tok", vhandler))
            emit_proj(x, c, w_in, hT, hT_d, NT, blocks)
        elif kind == 1:
            blocks = la_handlers_ssd(x, c, ls, dram, outs, dm)
            emit_proj(x, c, w_in, hT, hT_d, NT, blocks)
        else:
            blocks = la_handlers_dsa(x, c, ls, dram, outs, pos, dm)
            emit_proj(x, c, w_in, hT, hT_d, NT, blocks)


def emit_LB_DA(x, c, dram, lambda_init):
    NH = 8
    qT = dram("qT", [NH, 128, S], BF16, "ExternalInput")
    kT = dram("kT", [NH, 128, S], BF16, "ExternalInput")
    v = dram("v", [S, NH * 128], BF16, "ExternalInput")
    lam4 = dram("lam4", [4, 64], F32, "ExternalInput")
    gqk = dram("gqk", [2, 64], F32, "ExternalInput")
    subg = dram("subg", [128], F32, "ExternalInput")
    o = dram("o", [S, NH * 128], BF16, "ExternalOutput")
    with Scope(x) as st:
        d_o = x.mkdep("o")
        ls = st
        lt = [load_bcast(x, ls, lam4[i], 64, "lam%d" % i) for i in range(4)]
        gt = [load_bcast(x, ls, gqk[i], 64, "gqk%d" % i) for i in range(2)]
        gs = load_bcast(x, ls, subg, 128, "gs")
        x.op("dve", lambda e: e.tensor_scalar(out=gs[:], in0=gs[:], scalar1=1.0 - lambda_init, scalar2=None,
                                              op0=ALU.mult), r=[gs.d], w=[gs.d])
        pr = sbt(x, [128, 64], F32, "pr")
        s12 = sbt(x, [128, 2], F32, "s12")
        e12 = sbt(x, [128, 2], F32, "e12")
        neglam = sbt(x, [128, 1], F32, "neglam")
        for i in range(2):
            x.op("dve", lambda e: e.tensor_tensor(out=pr[:], in0=lt[2 * i][:], in1=lt[2 * i + 1][:], op=ALU.mult),
                 r=[lt[2 * i].d, lt[2 * i + 1].d], w=[pr.d])
            x.op("dve", lambda e: e.tensor_reduce(out=s12[:, i:i + 1], in_=pr[:], axis=AX.X, op=ALU.add),
                 r=[pr.d], w=[s12.d])
        x.op("act", lambda e: e.activation(out=e12[:], in_=s12[:], func=AF.Exp), r=[s12.d], w=[e12.d])
        x.op("dve", lambda e: e.scalar_tensor_tensor(out=neglam[:], in0=e12[:, 1:2], scalar=-lambda_init,
                                                     in1=e12[:, 0:1], op0=ALU.add, op1=ALU.subtract),
             r=[e12.d], w=[neglam.d])
        gm = sbt(x, [128, 2], F32, "gm")
        negC = sbt(x, [128, 1], F32, "negC")
        for i in range(2):
            x.op("dve", lambda e: e.tensor_reduce(out=gm[:, i:i + 1], in_=gt[i][:], axis=AX.X, op=ALU.max,
                                                  apply_absolute_value=True), r=[gt[i].d], w=[gm.d])
        x.op("dve", lambda e: e.scalar_tensor_tensor(out=negC[:], in0=gm[:, 0:1], scalar=-8.0, in1=gm[:, 1:2],
                                                     op0=ALU.mult, op1=ALU.mult), r=[gm.d], w=[negC.d])
        kt = [sbt(x, [128, S], BF16, "kt%d" % i) for i in range(2)]
        qt = [sbt(x, [128, S], BF16, "qt%d" % i) for i in range(2)]
        va = [sbt(x, [128, 32, 129], BF16, "va%d" % i) for i in range(2)]
        for i in range(2):
            x.op("pool", lambda e: e.memset(va[i][:, :, 128:129], 1.0), w=[va[i].d])
        pss = [pst(x, [128, 512], F32, "pss%d" % i) for i in range(4)]
        pso = pst(x, [128, 8, 256], F32, "pso")
        pt = [sbt(x, [128, 512], BF16, "pt%d" % i) for i in range(4)]
        R = sbt(x, [128, 8], F32, "R")
        tmp = [sbt(x, [128, 128], F32, "tmp%d" % i) for i in range(2)]
        of = sbt(x, [128, 4, 128], F32, "of")
        sq = sbt(x, [128, 512], F32, "sqo")
        ssq = sbt(x, [128, 4], F32, "ssqo")
        rs = sbt(x, [128, 4], F32, "rso")
        ob = [sbt(x, [128, 4, 128], BF16, "ob%d" % i) for i in range(2)]
        zr = sbt(x, [128, 512], BF16, "zr")
        x.op("pool", lambda e: e.memset(zr[:], 0.0), w=[zr.d])
        psob = pso[:].rearrange("p a b -> p (a b)")
        n = 0
        nq = 0
        for h in range(NH):
            hi = h % 2
            x.dma("sp", kt[hi][:], kT[h], w=[kt[hi].d])
            x.dma("sp", qt[hi][:], qT[h], w=[qt[hi].d])
            x.dma("sp", va[hi][:, :, 0:128], v[:, h * 128:(h + 1) * 128].rearrange("(kb p) d -> p kb d", p=128),
                  mw=[va[hi].d])
            steps = [(qc, kb, comp) for qc in range(8) for kb in range(4 * qc + 4) for comp in range(2)]

            def scores(idx):
                qc, kb, comp = steps[idx]
                jmin = max(0, kb - 4 * qc)
                diag = kb >= 4 * qc
                N = 512 - 128 * jmin
                q0 = qc * 512 + 128 * jmin
                ps = pss[idx % 4]
                ptt = pt[idx % 4]
                pr_ = slice(comp * 64, (comp + 1) * 64)
                lhs = kt[hi][pr_, kb * 128:(kb + 1) * 128]
                if diag:
                    x.op("pe", lambda e: e.matmul(ps[:, 0:128], lhs, qt[hi][pr_, q0:q0 + 128],
                                                  start=True, stop=False),
                         r=[kt[hi].d, qt[hi].d], w=[ps.d], inc=False)
                    x.op("pe", lambda e: e.matmul(ps[:, 0:128], c.ident[:], c.negT[:],
                                                  start=False, stop=True),
                         r=[c.ident.d, c.negT.d], w=[ps.d], inc=(N == 128))
                    if N > 128:
                        x.op("pe", lambda e: e.matmul(ps[:, 128:N], lhs, qt[hi][pr_, q0 + 128:q0 + N],
                                                      start=True, stop=True),
                             r=[kt[hi].d, qt[hi].d], w=[ps.d], inc=True)
                else:
                    x.op("pe", lambda e: e.matmul(ps[:, 0:N], lhs, qt[hi][pr_, q0:q0 + N],
                                                  start=True, stop=True),
                         r=[kt[hi].d, qt[hi].d], w=[ps.d], inc=True)
                x.op("act", lambda e: e.activation(out=ptt[:, 0:N], in_=ps[:, 0:N], func=AF.Exp,
                                                   scale=0.125, bias=negC[:, 0:1]),
                     r=[ps.d, negC.d], w=[ptt.d])

            def pv(idx):
                qc, kb, comp = steps[idx]
                jmin = max(0, kb - 4 * qc)
                ptt = pt[idx % 4]
                for j in range(jmin, 4):
                    last = (kb == 4 * qc + j)
                    x.op("pe", lambda e: e.matmul(pso[:, comp * 4 + j, 0:129],
                                                  ptt[:, (j - jmin) * 128:(j - jmin + 1) * 128],
                                                  va[hi][:, kb, :], start=False,
                                                  stop=(last and j % 2 == 1)),
                         r=[ptt.d, va[hi].d], w=[pso.d], inc=(j == 3))

            def epilogue(qc):
                nonlocal nq
                x.op("dve", lambda e: e.reciprocal(out=R[:], in_=pso[:, :, 128:129].rearrange("p a b -> p (a b)")),
                     r=[pso.d], w=[R.d])
                x.op("dve", lambda e: e.tensor_scalar(out=R[:, 4:8], in0=R[:, 4:8], scalar1=neglam[:, 0:1],
                                                      scalar2=None, op0=ALU.mult), r=[R.d, neglam.d], w=[R.d])
                for j in range(4):
                    tm = tmp[j % 2]
                    x.op("act", lambda e: e.activation(out=tm[:], in_=pso[:, 4 + j, 0:128], func=AF.Copy,
                                                       scale=R[:, 4 + j:5 + j]), r=[pso.d, R.d], w=[tm.d])
                    x.op("dve", lambda e: e.scalar_tensor_tensor(out=of[:, j, :], in0=pso[:, j, 0:128],
                                                                 scalar=R[:, j:j + 1], in1=tm[:],
                                                                 op0=ALU.mult, op1=ALU.add),
                         r=[pso.d, R.d, tm.d], mw=[of.d])
                ofl = of[:].rearrange("p a b -> p (a b)")
                x.op("act", lambda e: e.activation(out=sq[:], in_=ofl, func=AF.Square), r=[of.d], w=[sq.d])
                x.op("dve", lambda e: e.tensor_reduce(out=ssq[:], in_=sq[:].rearrange("p (a b) -> p a b", a=4),
                                                      axis=AX.X, op=ALU.add), r=[sq.d], w=[ssq.d])
                rsqrt_mean(x, c, rs, ssq, 128, 4)
                x.op("dve", lambda e: e.tensor_tensor(out=of[:], in0=of[:], in1=bc(rs[:].unsqueeze(2), [128, 4, 128]),
                                                      op=ALU.mult), r=[of.d, rs.d], w=[of.d])
                obb = ob[nq % 2]
                nq += 1
                x.op("pool", lambda e: e.tensor_tensor(out=obb[:], in0=of[:], in1=bc(gs[:].unsqueeze(1), [128, 4, 128]),
                                                       op=ALU.mult), r=[of.d, gs.d], w=[obb.d])
                x.dma("sp", o[qc * 512:(qc + 1) * 512, h * 128:(h + 1) * 128].rearrange("(j p) d -> p j d", p=128),
                      obb[:], r=[obb.d], mw=[d_o])

            scores(0)
            scores(1)
            for idx, (qc, kb, comp) in enumerate(steps):
                if kb == 0 and comp == 0:
                    for bnk in range(4):
                        x.op("pe", lambda e: e.matmul(psob[:, bnk * 512:(bnk + 1) * 512], zr[:, 0:128], zr[:],
                                                      start=True, stop=False), r=[zr.d], w=[pso.d], inc=False)
                if idx + 2 < len(steps):
                    scores(idx + 2)
                pv(idx)
                if kb == 4 * qc + 3 and comp == 1:
                    epilogue(qc)


def emit_outproj(x, c, o_ap, d_o, FO, wout_ap, x_ap, d_x, gate_b, xmid_ap, d_xmid):
    KO = FO // 128
    TB = 1024
    with Scope(x) as ls:
        oT = sbt(x, [128, KO, TB], BF16, "oT", ls)
        oT_d = [x.mkdep("oT%d" % t) for t in range(TB // 128)]
        ls.deps.extend(oT_d)
        ot = [sbt(x, [128, FO], BF16, "ot%d" % i, ls) for i in range(2)]
        pt = [pst(x, [128, 512], BF16, "pto%d" % i, ls) for i in range(2)]
        wb = [sbt(x, [128, KO, 512], BF16, "wo%d" % i, ls) for i in range(2)]
        pp = [pst(x, [128, 512], F32, "ppo%d" % i, ls) for i in range(2)]
        tm = [sbt(x, [128, 512], F32, "tmo%d" % i, ls) for i in range(2)]
        xt = [sbt(x, [128, 512], F32, "xto%d" % i, ls) for i in range(2)]
        n = 0
        nw = 0
        for tb in range(TOK // TB):
            for t in range(TB // 128):
                i = t % 2
                tok0 = tb * TB + t * 128
                x.dma("sp", ot[i][:], o_ap[tok0:tok0 + 128, :], r=[d_o], w=[ot[i].d])
                for g in range(KO // 4):
                    p = pt[g % 2]
                    for j in range(4):
                        kc = g * 4 + j
                        x.op("pe", lambda e: e.transpose(p[:, j * 128:(j + 1) * 128], ot[i][:, kc * 128:(kc + 1) * 128],
                                                         c.ident[:]), r=[ot[i].d, c.ident.d], w=[p.d], inc=(j == 3))
                    dst = oT[:, g * 4:(g + 1) * 4, t * 128:(t + 1) * 128]
                    src = p[:, :].rearrange("p (a b) -> p a b", a=4)
                    if g % 2 == 0:
                        x.op("dve", lambda e: e.tensor_copy(dst, src), r=[p.d], mw=[oT_d[t]])
                    else:
                        x.op("act", lambda e: e.activation(out=dst, in_=src, func=AF.Copy), r=[p.d], mw=[oT_d[t]])
            for cb in range(4):
                wt = wb[nw % 2]
                nw += 1
                x.dma("pool", wt[:], wout_ap[:, cb * 512:(cb + 1) * 512].rearrange("(k p) f -> p k f", p=128), w=[wt.d])
                for t in range(TB // 128):
                    tok0 = tb * TB + t * 128
                    ps = pp[n % 2]
                    tmm = tm[n % 2]
                    xtt = xt[n % 2]
                    n += 1
                    x.dma("sp", xtt[:], x_ap[tok0:tok0 + 128, cb * 512:(cb + 1) * 512], r=[d_x], w=[xtt.d])
                    for kc in range(KO):
                        x.op("pe", lambda e: e.matmul(ps[:], oT[:, kc, t * 128:(t + 1) * 128], wt[:, kc, :],
                                                      start=(kc == 0), stop=(kc == KO - 1)),
                             r=[oT_d[t], wt.d], w=[ps.d], inc=(kc == KO - 1))
                    x.op("dve", lambda e: e.tensor_tensor(out=tmm[:], in0=ps[:], in1=gate_b[:, cb * 512:(cb + 1) * 512],
                                                          op=ALU.mult), r=[ps.d, gate_b.d], w=[tmm.d])
                    x.op("pool", lambda e: e.tensor_tensor(out=tmm[:], in0=tmm[:], in1=xtt[:], op=ALU.add),
                         r=[tmm.d, xtt.d], w=[tmm.d])
                    x.dma("sp", xmid_ap[tok0:tok0 + 128, cb * 512:(cb + 1) * 512], tmm[:], r=[tmm.d], mw=[d_xmid])


def emit_ffn(x, c, xmid_ap, d_xmid, sT, shT, gate_b, wgu_ap, wd_ap, xout_ap, d_xout):
    TB = 512
    NFC = FFN // 128
    with Scope(x) as ls:
        h2T = sbt(x, [128, 16, TB], BF16, "h2T", ls)
        h2_d = [x.mkdep("h2T%d" % t) for t in range(4)]
        ls.deps.extend(h2_d)
        actT = sbt(x, [128, NFC, TB], BF16, "actT", ls)
        act_d = [x.mkdep("act%d" % f) for f in range(NFC)]
        ls.deps.extend(act_d)
        wg = [sbt(x, [128, 16, 256], BF16, "wg%d" % i, ls) for i in range(2)]
        wu = [sbt(x, [128, 16, 256], BF16, "wu%d" % i, ls) for i in range(2)]
        wdA = sbt(x, [128, 22, 512], BF16, "wdA", ls)
        wdB = sbt(x, [128, 22, 512], BF16, "wdB", ls)
        psg = [pst(x, [128, 512], F32, "psg%d" % i, ls) for i in range(2)]
        psu = [pst(x, [128, 512], F32, "psu%d" % i, ls) for i in range(2)]
        psd = [pst(x, [128, 512], F32, "psd%d" % i, ls) for i in range(2)]
        sg = [sbt(x, [128, 512], F32, "sg%d" % i, ls) for i in range(2)]
        tm = [sbt(x, [128, 512], F32, "tmf%d" % i, ls) for i in range(2)]
        xt = [sbt(x, [128, 512], F32, "xtf%d" % i, ls) for i in range(2)]
        nblk = 0
        n = 0
        nd = 0
        for tb in range(TOK // TB):
            emit_norm_hT(x, c, xmid_ap[tb * TB:(tb + 1) * TB, :], d_xmid, 4, sT, shT, h2T, h2_d)
            for blk in range(FFN // 256):
                wgt = wg[nblk % 2]
                wut = wu[nblk % 2]
                nblk += 1
                x.dma("pool", wgt[:], wgu_ap[:, blk * 256:(blk + 1) * 256].rearrange("(k p) f -> p k f", p=128),
                      w=[wgt.d])
                x.dma("pool", wut[:], wgu_ap[:, FFN + blk * 256:FFN + (blk + 1) * 256].rearrange("(k p) f -> p k f", p=128),
                      w=[wut.d])
                for fl in range(2):
                    fc = blk * 2 + fl
                    pg = psg[n % 2]
                    pu = psu[n % 2]
                    sgg = sg[n % 2]
                    n += 1
                    for kc in range(16):
                        x.op("pe", lambda e: e.matmul(pg[:], wgt[:, kc, fl * 128:(fl + 1) * 128], h2T[:, kc, :],
                                                      start=(kc == 0), stop=(kc == 15)),
                             r=h2_d + [wgt.d], w=[pg.d], inc=(kc == 15))
                    for kc in range(16):
                        x.op("pe", lambda e: e.matmul(pu[:], wut[:, kc, fl * 128:(fl + 1) * 128], h2T[:, kc, :],
                                                      start=(kc == 0), stop=(kc == 15)),
                             r=h2_d + [wut.d], w=[pu.d], inc=(kc == 15))
                    x.op("act", lambda e: e.activation(out=sgg[:], in_=pg[:], func=AF.Silu), r=[pg.d], w=[sgg.d])
                    x.op("dve", lambda e: e.tensor_tensor(out=actT[:, fc, :], in0=pu[:], in1=sgg[:], op=ALU.mult),
                         r=[pu.d, sgg.d], w=[act_d[fc]])
            for cb in range(4):
                x.dma("pool", wdA[:], wd_ap[0:22 * 128, cb * 512:(cb + 1) * 512].rearrange("(k p) f -> p k f", p=128),
                      w=[wdA.d])
                x.dma("pool", wdB[:], wd_ap[22 * 128:44 * 128, cb * 512:(cb + 1) * 512].rearrange("(k p) f -> p k f", p=128),
                      w=[wdB.d])
                for t in range(4):
                    tok0 = tb * TB + t * 128
                    ps = psd[nd % 2]
                    tmm = tm[nd % 2]
                    xtt = xt[nd % 2]
                    nd += 1
                    x.dma("sp", xtt[:], xmid_ap[tok0:tok0 + 128, cb * 512:(cb + 1) * 512], r=[d_xmid], w=[xtt.d])
                    for fc in range(NFC):
                        wt = wdA if fc < 22 else wdB
                        x.op("pe", lambda e: e.matmul(ps[:], actT[:, fc, t * 128:(t + 1) * 128], wt[:, fc % 22, :],
                                                      start=(fc == 0), stop=(fc == NFC - 1)),
                             r=[act_d[fc], wt.d], w=[ps.d], inc=(fc == NFC - 1 or fc == 21))
                    x.op("dve", lambda e: e.tensor_tensor(out=tmm[:], in0=ps[:], in1=gate_b[:, cb * 512:(cb + 1) * 512],
                                                          op=ALU.mult), r=[ps.d, gate_b.d], w=[tmm.d])
                    x.op("pool", lambda e: e.tensor_tensor(out=tmm[:], in0=tmm[:], in1=xtt[:], op=ALU.add),
                         r=[tmm.d, xtt.d], w=[tmm.d])
                    x.dma("sp", xout_ap[tok0:tok0 + 128, cb * 512:(cb + 1) * 512], tmm[:], r=[tmm.d], mw=[d_xout])


def emit_LC(x, c, dram, FO):
    x_in = dram("x_in", [TOK, D], F32, "ExternalInput")
    o_in = dram("o_in", [TOK, FO], BF16, "ExternalInput")
    ada = dram("ada", [6 * D], F32, "ExternalInput")
    g2 = dram("g2", [D], F32, "ExternalInput")
    w_out = dram("w_out", [FO, D], F32, "ExternalInput")
    wgu = dram("wgu", [D, 2 * FFN], F32, "ExternalInput")
    wd = dram("wd", [FFN, D], F32, "ExternalInput")
    x_mid = dram("x_mid", [TOK, D], F32, "Internal")
    x_out = dram("x_out", [TOK, D], F32, "ExternalOutput")
    with Scope(x) as st:
        d_none = x.mkdep("in")
        d_xmid = x.mkdep("xmid")
        d_xout = x.mkdep("xout")
        with Scope(x) as ls:
            g1b = load_bcast(x, ls, ada[2 * D:3 * D], D, "g1b")
            emit_outproj(x, c, o_in, d_none, FO, w_out, x_in, d_none, g1b, x_mid, d_xmid)
        with Scope(x) as ls:
            g2b = load_bcast(x, ls, ada[5 * D:6 * D], D, "g2b")
            sT, shT = emit_mod_cols(x, ls, g2, ada, d_none, 4, 3)
            emit_ffn(x, c, x_mid, d_xmid, sT, shT, g2b, wgu, wd, x_out, d_xout)


def la_handlers_ssd(x, c, ls, dram, outs_holder, dm=False):
    z = dram("z", [2, TOK, 2048] if dm else [TOK, 4096], BF16, "ExternalOutput")
    xbcT = dram("xbcT", [2, 3072, TOK] if dm else [6144, TOK], BF16, "ExternalOutput")
    dtr = dram("dtr", [2, TOK, 32] if dm else [TOK, 64], F32, "ExternalOutput")
    outs = [x.mkdep("z"), x.mkdep("xbcT"), x.mkdep("dtr")]
    outs_holder.extend(outs)
    zst = [sbt(x, [128, 512], BF16, "zst%d" % i, ls) for i in range(2)]
    fst = [sbt(x, [128, 512], BF16, "fst%d" % i, ls) for i in range(2)]
    dst_ = [sbt(x, [128, 64], F32, "dst%d" % i, ls) for i in range(2)]
    cnt = [0]
    blocks = []
    for zb in range(8):
        def zh(t, ps, zb=zb):
            s_ = zst[cnt[0] % 2]
            cnt[0] += 1
            x.op("act", lambda e: e.activation(out=s_[:], in_=ps[:, :], func=AF.Copy), r=[ps.d], w=[s_.d])
            zdst = (z[zb // 4, t * 128:(t + 1) * 128, (zb % 4) * 512:(zb % 4 + 1) * 512] if dm
                    else z[t * 128:(t + 1) * 128, zb * 512:(zb + 1) * 512])
            x.dma("sp", zdst, s_[:], r=[s_.d], mw=[outs[0]])
        blocks.append((zb * 512, 512, "tok", zh))
    for xb in range(12):
        def xh(fc, tb, ps, xb=xb):
            s_ = fst[cnt[0] % 2]
            cnt[0] += 1
            x.op("act", lambda e: e.activation(out=s_[:], in_=ps[:, :], func=AF.Copy), r=[ps.d], w=[s_.d])
            if dm:
                dd, rb = ((xb // 4, (xb % 4) * 512) if xb < 8 else ((xb - 8) % 2, 2048 + ((xb - 8) // 2) * 512))
                xdst = xbcT[dd, rb + fc * 128:rb + fc * 128 + 128, tb * 512:(tb + 1) * 512]
            else:
                r0 = xb * 512 + fc * 128
                xdst = xbcT[r0:r0 + 128, tb * 512:(tb + 1) * 512]
            x.dma("sp", xdst, s_[:], r=[s_.d], mw=[outs[1]])
        blocks.append((4096 + xb * 512, 512, "feat", xh))

    def dh(t, ps):
        s_ = dst_[cnt[0] % 2]
        cnt[0] += 1
        x.op("act", lambda e: e.activation(out=s_[:], in_=ps[:, 0:64], func=AF.Copy), r=[ps.d], w=[s_.d])
        if dm:
            for dd in range(2):
                x.dma("sp", dtr[dd, t * 128:(t + 1) * 128, :], s_[:, dd * 32:(dd + 1) * 32], r=[s_.d], mw=[outs[2]])
        else:
            x.dma("sp", dtr[t * 128:(t + 1) * 128, :], s_[:], r=[s_.d], mw=[outs[2]])
    blocks.append((10240, 64, "tok", dh))
    return blocks


def emit_LB_SSD(x, c, dram):
    NHh = 32
    NCH = 24
    raw = dram("raw", [NCH * 128, S], F32, "ExternalInput")
    convw = dram("convw", [4, NCH * 128], F32, "ExternalInput")
    convb = dram("convb", [NCH * 128], F32, "ExternalInput")
    dtr = dram("dtr", [S, NHh], F32, "ExternalInput")
    hp = dram("hp", [3, NHh], F32, "ExternalInput")
    z = dram("z", [S, 2048], BF16, "ExternalInput")
    ng = dram("ng", [2048], F32, "ExternalInput")
    tokd = dram("tokd", [S, 2560], BF16, "Internal")
    featd = dram("featd", [1024, S], BF16, "Internal")
    y = dram("y", [S, 2048], BF16, "ExternalOutput")
    with Scope(x) as st:
        d_in = x.mkdep("in")
        d_tok = x.mkdep("tokd")
        d_feat = x.mkdep("featd")
        d_y = x.mkdep("y")
        with Scope(x) as ls:
            cw = sbt(x, [128, 4, NCH], F32, "cw", ls)
            cb_ = sbt(x, [128, NCH], F32, "cb", ls)
            for j in range(4):
                x.dma("sp", cw[:, j, :], convw[j].rearrange("(k p) -> p k", p=128), mw=[cw.d],
                      allow_slow_non_contiguous=True)
            x.dma("sp", cb_[:], convb.rearrange("(k p) -> p k", p=128), w=[cb_.d], allow_slow_non_contiguous=True)
            rw = [sbt(x, [128, S + 3], F32, "rw%d" % i, ls) for i in range(2)]
            for i in range(2):
                x.op("pool", lambda e: e.memset(rw[i][:, 0:3], 0.0), w=[rw[i].d])
            acc = sbt(x, [128, S], F32, "acc", ls)
            sil = [sbt(x, [128, S], BF16, "sil%d" % i, ls) for i in range(2)]
            ptc = [pst(x, [128, 512], BF16, "ptc%d" % i, ls) for i in range(2)]
            stg = [sbt(x, [128, 4, 128], BF16, "stg%d" % i, ls) for i in range(2)]
            n = 0
            for cc in range(NCH):
                r_ = rw[cc % 2]
                sl_ = sil[cc % 2]
                x.dma("sp", r_[:, 3:3 + S], raw[cc * 128:(cc + 1) * 128, :], r=[d_in], mw=[r_.d])
                x.op("dve", lambda e: e.tensor_scalar(out=acc[:], in0=r_[:, 3:3 + S], scalar1=cw[:, 3, cc:cc + 1],
                                                      scalar2=cb_[:, cc:cc + 1], op0=ALU.mult, op1=ALU.add),
                     r=[r_.d, cw.d, cb_.d], w=[acc.d])
                for j in range(3):
                    x.op("dve", lambda e: e.scalar_tensor_tensor(out=acc[:], in0=r_[:, j:j + S],
                                                                 scalar=cw[:, j, cc:cc + 1], in1=acc[:],
                                                                 op0=ALU.mult, op1=ALU.add),
                         r=[r_.d, cw.d, acc.d], w=[acc.d])
                x.op("act", lambda e: e.activation(out=sl_[:], in_=acc[:], func=AF.Silu), r=[acc.d], w=[sl_.d])
                if cc >= 16:
                    x.dma("sp", featd[(cc - 16) * 128:(cc - 15) * 128, :], sl_[:], r=[sl_.d], mw=[d_feat])
                if cc < 20:
                    for tg in range(8):
                        p = ptc[n % 2]
                        sg_ = stg[n % 2]
                        n += 1
                        for j in range(4):
                            tt = tg * 4 + j
                            x.op("pe", lambda e: e.transpose(p[:, j * 128:(j + 1) * 128], sl_[:, tt * 128:(tt + 1) * 128],
                                                             c.ident[:]), r=[sl_.d, c.ident.d], w=[p.d], inc=(j == 3))
                        if n % 2 == 0:
                            x.op("dve", lambda e: e.tensor_copy(sg_[:], p[:, :].rearrange("p (a b) -> p a b", a=4)),
                                 r=[p.d], w=[sg_.d])
                        else:
                            x.op("act", lambda e: e.activation(out=sg_[:], in_=p[:, :].rearrange("p (a b) -> p a b", a=4),
                                                               func=AF.Copy), r=[p.d], w=[sg_.d])
                        x.dma("sp", tokd[tg * 512:(tg + 1) * 512, cc * 128:(cc + 1) * 128].rearrange("(j p) c -> p j c", p=128),
                              sg_[:], r=[sg_.d], mw=[d_tok])
        with Scope(x) as ls:
            hb = [load_bcast(x, ls, hp[i], NHh, "hp%d" % i) for i in range(3)]
            dtb_b, alog_b, dsk_b = hb
            a_b = sbt(x, [128, NHh], F32, "a_b", ls)
            x.op("act", lambda e: e.activation(out=a_b[:], in_=alog_b[:], func=AF.Exp), r=[alog_b.d], w=[a_b.d])
            x.op("dve", lambda e: e.tensor_scalar(out=a_b[:], in0=a_b[:], scalar1=-1.0, scalar2=None, op0=ALU.mult),
                 r=[a_b.d], w=[a_b.d])
            ngb = load_bcast(x, ls, ng, 2048, "ngb")
            sel = sbt(x, [32, NHh, 128], F32, "sel", ls)
            x.op("pool", lambda e: e.memset(sel[:], 1.0), w=[sel.d])
            x.op("pool", lambda e: e.affine_select(out=sel[:], in_=sel[:], pattern=[[-1, NHh], [0, 128]],
                                                   compare_op=ALU.is_equal, fill=0.0, base=0, channel_multiplier=1),
                 r=[sel.d], w=[sel.d])
            St = [sbt(x, [128, 512], F32, "St%d" % g, ls) for g in range(4)]
            Sb = [sbt(x, [128, 512], BF16, "Sb%d" % g, ls) for g in range(4)]
            for g in range(4):
                x.op("pool", lambda e: e.memset(St[g][:], 0.0), w=[St[g].d])
                x.op("pool", lambda e: e.memset(Sb[g][:], 0.0), w=[Sb[g].d])
            xs_t = [sbt(x, [128, 2560], BF16, "xs_t%d" % i, ls) for i in range(2)]
            bct = [sbt(x, [128, 8, 128], BF16, "bct%d" % i, ls) for i in range(2)]
            dtt = [sbt(x, [128, NHh], F32, "dtt%d" % i, ls) for i in range(2)]
            zt = [sbt(x, [128, 2048], BF16, "zt%d" % i, ls) for i in range(2)]
            f = lambda nm, w_: sbt(x, [128, w_], F32, nm, ls)
            dtb, ab, ee, dt_, dta, acum, nacum, alast, eac, wend, decay = [f(nm, NHh) for nm in
                ("dtb", "ab", "ee", "dt_", "dta", "acum", "nacum", "alast", "eac", "wend", "decay")]
            acT = sbt(x, [32, 128], F32, "acT", ls)
            xdt = sbt(x, [128, 2048], BF16, "xdt", ls)
            xde = sbt(x, [128, 2048], BF16, "xde", ls)
            cbm = [sbt(x, [128, 128], F32, "cbm%d" % g, ls) for g in range(4)]
            Eh = [sbt(x, [128, 128], F32, "Eh%d" % i, ls) for i in range(4)]
            Mh = [sbt(x, [128, 128], BF16, "Mh%d" % i, ls) for i in range(4)]
            yf = sbt(x, [128, 2048], F32, "yf", ls)
            t1 = sbt(x, [128, 2048], F32, "t1", ls)
            sqy = sbt(x, [128, 2048], F32, "sqy", ls)
            ssy = sbt(x, [128, 4], F32, "ssy", ls)
            rsy = sbt(x, [128, 4], F32, "rsy", ls)
            yb = [sbt(x, [128, 2048], BF16, "yb%d" % i, ls) for i in range(2)]
            p_small = pst(x, [128, 512], F32, "p_small", ls)
            p_cb = pst(x, [128, 512], F32, "p_cb", ls)
            p_G = [pst(x, [128, 512], F32, "p_G%d" % i, ls) for i in range(2)]
            p_y = [pst(x, [128, 512], F32, "p_y%d" % i, ls) for i in range(2)]
            p_i = pst(x, [128, 512], F32, "p_i", ls)
            p_s = pst(x, [128, 512], F32, "p_s", ls)
            for ck in range(S // 128):
                i = ck % 2
                t0 = ck * 128
                xt_ = xs_t[i]
                bc_ = bct[i]
                x.dma("sp", xt_[:], tokd[t0:t0 + 128, :], r=[d_tok], w=[xt_.d])
                x.dma("sp", bc_[:], featd[:, t0:t0 + 128].rearrange("(g p) t -> p g t", p=128), r=[d_feat], w=[bc_.d])
                x.dma("sp", dtt[i][:], dtr[t0:t0 + 128, :], r=[d_in], w=[dtt[i].d])
                x.dma("sp", zt[i][:], z[t0:t0 + 128, :], r=[d_in], w=[zt[i].d])
                x.op("dve", lambda e: e.tensor_tensor(out=dtb[:], in0=dtt[i][:], in1=dtb_b[:], op=ALU.add),
                     r=[dtt[i].d, dtb_b.d], w=[dtb.d])
                x.op("dve", lambda e: e.scalar_tensor_tensor(out=ab[:], in0=dtb[:], scalar=-1.0, in1=dtb[:],
                                                             op0=ALU.mult, op1=ALU.max), r=[dtb.d], w=[ab.d])
                x.op("act", lambda e: e.activation(out=ee[:], in_=ab[:], func=AF.Exp, scale=-1.0), r=[ab.d], w=[ee.d])
                x.op("act", lambda e: e.activation(out=ee[:], in_=ee[:], func=AF.Ln, bias=1.0), r=[ee.d], w=[ee.d])
                x.op("dve", lambda e: e.scalar_tensor_tensor(out=dt_[:], in0=dtb[:], scalar=0.0, in1=ee[:],
                                                             op0=ALU.max, op1=ALU.add), r=[dtb.d, ee.d], w=[dt_.d])
                x.op("dve", lambda e: e.tensor_tensor(out=dta[:], in0=dt_[:], in1=a_b[:], op=ALU.mult),
                     r=[dt_.d, a_b.d], w=[dta.d])
                x.op("pe", lambda e: e.matmul(p_small[:, 0:32], c.trif[:], dta[:], start=True, stop=True),
                     r=[c.trif.d, dta.d], w=[p_small.d])
                x.op("pe", lambda e: e.matmul(p_small[:, 32:64], c.onesf[:], dta[:], start=True, stop=True),
                     r=[c.onesf.d, dta.d], w=[p_small.d])
                x.op("dve", lambda e: e.tensor_copy(acum[:], p_small[:, 0:32]), r=[p_small.d], w=[acum.d])
                x.op("dve", lambda e: e.tensor_scalar(out=nacum[:], in0=p_small[:, 0:32], scalar1=-1.0, scalar2=None,
                                                      op0=ALU.mult), r=[p_small.d], w=[nacum.d])
                x.op("dve", lambda e: e.tensor_copy(alast[:], p_small[:, 32:64]), r=[p_small.d], w=[alast.d])
                x.op("act", lambda e: e.activation(out=eac[:], in_=acum[:], func=AF.Exp), r=[acum.d], w=[eac.d])
                x.op("act", lambda e: e.activation(out=decay[:], in_=alast[:], func=AF.Exp), r=[alast.d], w=[decay.d])
                x.op("dve", lambda e: e.tensor_tensor(out=wend[:], in0=alast[:], in1=acum[:], op=ALU.subtract),
                     r=[alast.d, acum.d], w=[wend.d])
                x.op("act", lambda e: e.activation(out=wend[:], in_=wend[:], func=AF.Exp), r=[wend.d], w=[wend.d])
                x.op("dve", lambda e: e.tensor_tensor(out=wend[:], in0=wend[:], in1=dt_[:], op=ALU.mult),
                     r=[wend.d, dt_.d], w=[wend.d])
                x.op("pe", lambda e: e.matmul(p_small[0:32, 128:256], acum[:], c.identf[:], start=True, stop=True),
                     r=[acum.d, c.identf.d], w=[p_small.d])
                x.op("dve", lambda e: e.tensor_copy(acT[:], p_small[0:32, 128:256]), r=[p_small.d], w=[acT.d])
                xv = xt_[:, 0:2048].rearrange("p (h d) -> p h d", d=64)
                x.op("dve", lambda e: e.tensor_tensor(out=xdt[:].rearrange("p (h d) -> p h d", d=64), in0=xv,
                                                      in1=bc(dt_[:].unsqueeze(2), [128, NHh, 64]), op=ALU.mult),
                     r=[xt_.d, dt_.d], w=[xdt.d])
                x.op("pool", lambda e: e.tensor_tensor(out=xde[:].rearrange("p (h d) -> p h d", d=64), in0=xv,
                                                       in1=bc(wend[:].unsqueeze(2), [128, NHh, 64]), op=ALU.mult),
                     r=[xt_.d, wend.d], w=[xde.d])
                for g in range(4):
                    x.op("pe", lambda e: e.matmul(p_cb[:, g * 128:(g + 1) * 128], bc_[:, g, :], bc_[:, 4 + g, :],
                                                  start=True, stop=True), r=[bc_.d], w=[p_cb.d], inc=(g == 3))
                for g in range(4):
                    if g % 2 == 0:
                        x.op("dve", lambda e: e.tensor_copy(cbm[g][:], p_cb[:, g * 128:(g + 1) * 128]),
                             r=[p_cb.d], w=[cbm[g].d])
                    else:
                        x.op("act", lambda e: e.activation(out=cbm[g][:], in_=p_cb[:, g * 128:(g + 1) * 128], func=AF.Copy),
                             r=[p_cb.d], w=[cbm[g].d])
                for g in range(4):
                    py = p_y[g % 2]
                    x.op("pe", lambda e: e.matmul(p_i[:], bc_[:, 4 + g, :], Sb[g][:], start=True, stop=True),
                         r=[bc_.d, Sb[g].d], w=[p_i.d])
                    def hG(hl):
                        h = g * 8 + hl
                        pgt = p_G[(h % 4) // 2]
                        pgs = slice((h % 2) * 128, (h % 2) * 128 + 128)
                        x.op("pe", lambda e: e.matmul(pgt[:, pgs], sel[:, h, :], acT[:], start=True, stop=False),
                             r=[sel.d, acT.d], w=[pgt.d], inc=False)
                        x.op("pe", lambda e: e.matmul(pgt[:, pgs], c.ident[:], c.negT[:], start=False, stop=True),
                             r=[c.ident.d, c.negT.d], w=[pgt.d])
                        eh = Eh[h % 4]
                        mh = Mh[h % 4]
                        x.op("act", lambda e: e.activation(out=eh[:], in_=pgt[:, pgs], func=AF.Exp,
                                                           bias=nacum[:, h:h + 1]), r=[pgt.d, nacum.d], w=[eh.d])
                        x.op("dve", lambda e: e.tensor_tensor(out=mh[:], in0=eh[:], in1=cbm[g][:], op=ALU.mult),
                             r=[eh.d, cbm[g].d], w=[mh.d])

                    def hY(hl):
                        h = g * 8 + hl
                        mh = Mh[h % 4]
                        x.op("pe", lambda e: e.matmul(py[:, hl * 64:(hl + 1) * 64], mh[:], xdt[:, h * 64:(h + 1) * 64],
                                                      start=True, stop=True), r=[mh.d, xdt.d], w=[py.d])

                    hG(0)
                    hG(1)
                    for hl in range(8):
                        if hl + 2 < 8:
                            hG(hl + 2)
                        hY(hl)
                    gs_ = slice(g * 512, (g + 1) * 512)
                    x.op("dve", lambda e: e.tensor_tensor(out=t1[:, gs_].rearrange("p (h d) -> p h d", d=64),
                                                          in0=p_i[:].rearrange("p (h d) -> p h d", d=64),
                                                          in1=bc(eac[:, g * 8:(g + 1) * 8].unsqueeze(2), [128, 8, 64]),
                                                          op=ALU.mult), r=[p_i.d, eac.d], mw=[t1.d])
                    x.op("dve", lambda e: e.tensor_tensor(out=yf[:, gs_], in0=py[:], in1=t1[:, gs_], op=ALU.add),
                         r=[py.d, t1.d], mw=[yf.d])
                    x.op("pe", lambda e: e.matmul(p_s[:], xt_[:, 2048 + g * 128:2048 + (g + 1) * 128], xde[:, gs_],
                                                  start=True, stop=True), r=[xt_.d, xde.d], w=[p_s.d])
                    x.op("pool", lambda e: e.tensor_tensor(out=St[g][:].rearrange("p (h d) -> p h d", d=64),
                                                           in0=St[g][:].rearrange("p (h d) -> p h d", d=64),
                                                           in1=bc(decay[:, g * 8:(g + 1) * 8].unsqueeze(2), [128, 8, 64]),
                                                           op=ALU.mult), r=[St[g].d, decay.d], w=[St[g].d])
                    x.op("dve", lambda e: e.tensor_tensor(out=St[g][:], in0=p_s[:], in1=St[g][:], op=ALU.add),
                         r=[p_s.d, St[g].d], w=[St[g].d])
                    x.op("act", lambda e: e.activation(out=Sb[g][:], in_=St[g][:], func=AF.Copy),
                         r=[St[g].d], w=[Sb[g].d])
                x.op("pool", lambda e: e.tensor_tensor(out=t1[:].rearrange("p (h d) -> p h d", d=64), in0=xv,
                                                       in1=bc(dsk_b[:].unsqueeze(2), [128, NHh, 64]), op=ALU.mult),
                     r=[xt_.d, dsk_b.d, yf.d], w=[t1.d])
                x.op("dve", lambda e: e.tensor_tensor(out=yf[:], in0=yf[:], in1=t1[:], op=ALU.add),
                     r=[yf.d, t1.d], w=[yf.d])
                x.op("act", lambda e: e.activation(out=t1[:], in_=zt[i][:], func=AF.Silu), r=[zt[i].d, yf.d], w=[t1.d])
                x.op("dve", lambda e: e.tensor_tensor(out=yf[:], in0=yf[:], in1=t1[:], op=ALU.mult),
                     r=[yf.d, t1.d], w=[yf.d])
                x.op("act", lambda e: e.activation(out=sqy[:], in_=yf[:], func=AF.Square), r=[yf.d], w=[sqy.d])
                x.op("dve", lambda e: e.tensor_reduce(out=ssy[:], in_=sqy[:].rearrange("p (g d) -> p g d", g=4),
                                                      axis=AX.X, op=ALU.add), r=[sqy.d], w=[ssy.d])
                rsqrt_mean(x, c, rsy, ssy, 512, 4)
                x.op("dve", lambda e: e.tensor_tensor(out=yf[:].rearrange("p (g d) -> p g d", g=4),
                                                      in0=yf[:].rearrange("p (g d) -> p g d", g=4),
                                                      in1=bc(rsy[:].unsqueeze(2), [128, 4, 512]), op=ALU.mult),
                     r=[yf.d, rsy.d], w=[yf.d])
                x.op("pool", lambda e: e.tensor_tensor(out=yb[i][:], in0=yf[:], in1=ngb[:], op=ALU.mult),
                     r=[yf.d, ngb.d], w=[yb[i].d])
                x.dma("sp", y[t0:t0 + 128, :], yb[i][:], r=[yb[i].d], mw=[d_y])


def la_handlers_dsa(x, c, ls, dram, outs_holder, pos, dm=False):
    invf16 = dram("invf16", [128, 16], F32, "ExternalInput")
    invf8 = dram("invf8", [128, 8], F32, "ExternalInput")
    gq = dram("gq", [128], F32, "ExternalInput")
    gk = dram("gk", [128], F32, "ExternalInput")
    gi = dram("gi", [64], F32, "ExternalInput")
    qT = dram("qT", [2, 16, 128, 8, 128] if dm else [16, 128, TOK], BF16, "ExternalOutput")
    kT = dram("kT", [4, 128, TOK], BF16, "ExternalOutput")
    v = dram("v", [TOK, 512], BF16, "ExternalOutput")
    qiT = dram("qiT", [2, 8, 128, 8, 128] if dm else [8, 128, TOK], BF16, "ExternalOutput")
    kiT = dram("kiT", [64, TOK], BF16, "ExternalOutput")
    wi = dram("wi", [2, 8, 128, 16] if dm else [TOK, 16], F32, "ExternalOutput")
    outs = [x.mkdep(n) for n in ("qT", "kT", "v", "qiT", "kiT", "wi")]
    outs_holder.extend(outs)
    cos16, sin16 = emit_rope_tables(x, ls, pos, invf16, 16, NT)
    cos8, sin8 = emit_rope_tables(x, ls, pos, invf8, 8, NT)
    gqb = load_bcast(x, ls, gq, 128, "gqb")
    gkb = load_bcast(x, ls, gk, 128, "gkb")
    gib = load_bcast(x, ls, gi, 64, "gib")
    qk_tiles = alloc_qk_tiles(x, ls)
    qbf = sbt(x, [128, 1024], BF16, "qbf", ls)
    stage = [sbt(x, [128, 4, TOK], BF16, "stage%d" % i, ls) for i in range(2)]
    kist = sbt(x, [64, TOK], BF16, "kist", ls)
    ptr = [pst(x, [128, 512], BF16, "ptr%d" % i, ls) for i in range(2)]
    vst = [sbt(x, [128, 512], BF16, "vst%d" % i, ls) for i in range(2)]
    wst = [sbt(x, [128, 16], F32, "wst%d" % i, ls) for i in range(2)]
    cnt = [0]
    nst = [0]
    blocks = []

    def mk_qk(col0, gdim, gain, half, cos, sin, dst_ap, dep, dmh=None):
        sg = stage[nst[0] % 2]
        nst[0] += 1

        def handler(t, ps):
            emit_qk_post2(x, c, qk_tiles, ps, gdim, gain, half, cos, sin, t, qbf)
            for a_ in range(2):
                p = ptr[cnt[0] % 2]
                cnt[0] += 1
                for j in range(4):
                    x.op("pe", lambda e: e.transpose(p[:, j * 128:(j + 1) * 128],
                                                     qbf[:, a_ * 512 + j * 128:a_ * 512 + (j + 1) * 128], c.ident[:]),
                         r=[qbf.d, c.ident.d], w=[p.d], inc=(j == 3))
                x.op("dve", lambda e: e.tensor_copy(sg[:, :, (t + a_) * 128:(t + a_ + 1) * 128],
                                                    p[:, :].rearrange("p (a b) -> p a b", a=4)), r=[p.d], mw=[sg.d])
            if t == NT - 2:
                if dmh is None:
                    x.dma("sp", dst_ap.rearrange("h p t -> p h t"), sg[:], r=[sg.d], mw=[dep])
                else:
                    tens, h0 = dmh
                    sgv = sg[:].rearrange("p h (k a t) -> p h k a t", a=2, t=128)
                    for a2 in range(2):
                        for hh_ in range(4):
                            x.dma("sp", tens[a2, h0 + hh_].rearrange("d k t -> d k t"), sgv[:, hh_, :, a2, :],
                                  r=[sg.d], mw=[dep])
        blocks.append((col0, 512, "tok2", handler))
    for hb in range(4):
        mk_qk(hb * 512, 128, gqb, 16, cos16, sin16, None if dm else qT[hb * 4:(hb + 1) * 4], outs[0],
              (qT, hb * 4) if dm else None)
    mk_qk(2048, 128, gkb, 16, cos16, sin16, kT[0:4], outs[1])

    def vh(t, ps):
        vs = vst[cnt[0] % 2]
        cnt[0] += 1
        x.op("act", lambda e: e.activation(out=vs[:], in_=ps[:, :], func=AF.Copy), r=[ps.d], w=[vs.d])
        x.dma("sp", v[t * 128:(t + 1) * 128, :], vs[:], r=[vs.d], mw=[outs[2]])
    blocks.append((2560, 512, "tok", vh))
    for qb in range(2):
        mk_qk(3072 + qb * 512, 64, None, 8, cos8, sin8, None if dm else qiT[qb * 4:(qb + 1) * 4], outs[3],
              (qiT, qb * 4) if dm else None)

    def kwh(t, ps):
        emit_qk_post(x, c, qk_tiles, ps, 64, 64, gib, 8, cos8, sin8, t, qbf)
        p = ptr[cnt[0] % 2]
        ws = wst[cnt[0] % 2]
        cnt[0] += 1
        x.op("pe", lambda e: e.transpose(p[0:64, 0:128], qbf[:, 0:64], c.ident[:]), r=[qbf.d, c.ident.d], w=[p.d])
        x.op("dve", lambda e: e.tensor_copy(kist[:, t * 128:(t + 1) * 128], p[0:64, 0:128]), r=[p.d], mw=[kist.d])
        x.op("act", lambda e: e.activation(out=ws[:], in_=ps[:, 64:80], func=AF.Copy, scale=0.25), r=[ps.d], w=[ws.d])
        x.dma("sp", wi[t % 2, t // 2] if dm else wi[t * 128:(t + 1) * 128, :], ws[:], r=[ws.d], mw=[outs[5]])
        if t == NT - 1:
            x.dma("sp", kiT, kist[:], r=[kist.d], mw=[outs[4]])
    blocks.append((4096, 80, "tok", kwh))
    return blocks


def emit_LB_DSA(x, c, dram):
    NS = 16
    qTs = dram("qTs", [NS, 128, 2048], BF16, "ExternalInput")
    qiTs = dram("qiTs", [NS, 128, 1024], BF16, "ExternalInput")
    wis = dram("wis", [NS, 128, 16], F32, "ExternalInput")
    kT = dram("kT", [4, 128, S], BF16, "ExternalInput")
    v = dram("v", [S, 512], BF16, "ExternalInput")
    kiT2 = dram("kiT2", [128, S], BF16, "ExternalInput")
    dmask = dram("dmask", [2, 128, 128], F32, "ExternalInput")
    gqk = dram("gqk", [2, 128], F32, "ExternalInput")
    o = dram("o", [NS * 128, 2048], BF16, "ExternalOutput")
    SCALE = 128 ** -0.5
    with Scope(x) as st:
        d_in = x.mkdep("in")
        d_o = x.mkdep("o")
        ls = st
        gt = [load_bcast(x, ls, gqk[i], 128, "gqk%d" % i) for i in range(2)]
        gm = sbt(x, [128, 2], F32, "gm")
        negC = sbt(x, [128, 1], F32, "negC")
        for i in range(2):
            x.op("dve", lambda e: e.tensor_reduce(out=gm[:, i:i + 1], in_=gt[i][:], axis=AX.X, op=ALU.max,
                                                  apply_absolute_value=True), r=[gt[i].d], w=[gm.d])
        x.op("dve", lambda e: e.scalar_tensor_tensor(out=negC[:], in0=gm[:, 0:1], scalar=-(128 ** 0.5), in1=gm[:, 1:2],
                                                     op0=ALU.mult, op1=ALU.mult), r=[gm.d], w=[negC.d])
        kts = sbt(x, [128, 4, S], BF16, "kts")
        x.dma("sp", kts[:], kT.rearrange("g p t -> p g t"), w=[kts.d])
        va = sbt(x, [128, 32, 4, 129], BF16, "va")
        x.op("pool", lambda e: e.memset(va[:, :, :, 128:129], 1.0), w=[va.d])
        for g in range(4):
            x.dma("sp", va[:, :, g, 0:128], v[:, g * 128:(g + 1) * 128].rearrange("(kb p) d -> p kb d", p=128),
                  mw=[va.d])
        ki2 = sbt(x, [128, S], BF16, "ki2")
        x.dma("sp", ki2[:], kiT2, w=[ki2.d])
        dm = sbt(x, [128, 2, 128], F32, "dm")
        x.dma("sp", dm[:], dmask.rearrange("a p k -> p a k"), w=[dm.d])
        zr = sbt(x, [128, 512], BF16, "zr")
        x.op("pool", lambda e: e.memset(zr[:], 0.0), w=[zr.d])
        acc = sbt(x, [128, S], F32, "acc")
        work = sbt(x, [128, S], F32, "work")
        nb = sbt(x, [128, S], BF16, "nb")
        nbT4 = sbt(x, [128, 32, 4, 128], BF16, "nbT4")
        qs = [sbt(x, [128, 2048], BF16, "qs%d" % i) for i in range(2)]
        qis = [sbt(x, [128, 8, 128], BF16, "qis%d" % i) for i in range(2)]
        wt = [sbt(x, [128, 16], F32, "wt%d" % i) for i in range(2)]
        aw = sbt(x, [128, 16], F32, "aw")
        sgn = sbt(x, [128, 16], F32, "sgn")
        rr = [sbt(x, [128, 512], F32, "rr%d" % i) for i in range(2)]
        m8 = sbt(x, [128, 8], F32, "m8")
        thr = sbt(x, [128, 1], F32, "thr")
        thr0 = sbt(x, [128, 1], F32, "thr0")
        x.op("pool", lambda e: e.memset(thr0[:], -1e29), w=[thr0.d])
        P = [sbt(x, [128, 512], BF16, "P%d" % i) for i in range(2)]
        R = sbt(x, [128, 4], F32, "R")
        ob = [sbt(x, [128, 2048], BF16, "ob%d" % i) for i in range(2)]
        p_ix = [pst(x, [128, 512], F32, "p_ix%d" % i) for i in range(2)]
        p_tr = [pst(x, [128, 512], BF16, "p_tr%d" % i) for i in range(2)]
        p_s = [pst(x, [128, 512], F32, "p_s%d" % i) for i in range(2)]
        p_o = pst(x, [128, 4, 256], F32, "p_o")
        p_ob = p_o[:].rearrange("p a b -> p (a b)")
        nix = 0
        ns_ = 0
        def part_A(i):
            nonlocal nix
            b2 = i % 2
            nkb = 2 * i + 2
            L = nkb * 128
            x.dma("sp", qs[b2][:], qTs[i], r=[d_in], w=[qs[b2].d])
            x.dma("sp", qis[b2][:], qiTs[i].rearrange("p (a t) -> p a t", a=8), r=[d_in], w=[qis[b2].d])
            x.dma("sp", wt[b2][:], wis[i], r=[d_in], w=[wt[b2].d])
            w_ = wt[b2]
            x.op("dve", lambda e: e.scalar_tensor_tensor(out=aw[:], in0=w_[:], scalar=-1.0, in1=w_[:],
                                                         op0=ALU.mult, op1=ALU.max), r=[w_.d], w=[aw.d])
            x.op("dve", lambda e: e.tensor_scalar(out=aw[:], in0=aw[:], scalar1=0.125, scalar2=None, op0=ALU.mult),
                 r=[aw.d], w=[aw.d])
            x.op("dve", lambda e: e.tensor_scalar(out=sgn[:], in0=w_[:], scalar1=0.0, scalar2=2.0,
                                                  op0=ALU.is_ge, op1=ALU.mult), r=[w_.d], w=[sgn.d])
            x.op("dve", lambda e: e.tensor_scalar(out=sgn[:], in0=sgn[:], scalar1=-1.0, scalar2=None, op0=ALU.add),
                 r=[sgn.d], w=[sgn.d])
            for kq in range((L + 511) // 512):
                W = min(512, L - kq * 512)
                cs = slice(kq * 512, kq * 512 + W)
                for hi in range(16):
                    ps = p_ix[nix % 2]
                    r_ = rr[nix % 2]
                    nix += 1
                    pr_ = slice((hi % 2) * 64, (hi % 2) * 64 + 64)
                    x.op("pe", lambda e: e.matmul(ps[:, 0:W], qis[b2][pr_, hi // 2, :], ki2[pr_, cs], start=True, stop=True),
                         r=[qis[b2].d, ki2.d], w=[ps.d])
                    x.op("act", lambda e: e.activation(out=r_[:, 0:W], in_=ps[:, 0:W], func=AF.Relu, scale=aw[:, hi:hi + 1]),
                         r=[ps.d, aw.d], w=[r_.d])
                    if hi == 0:
                        x.op("dve", lambda e: e.tensor_scalar(out=acc[:, cs], in0=r_[:, 0:W], scalar1=sgn[:, 0:1],
                                                              scalar2=None, op0=ALU.mult), r=[r_.d, sgn.d], w=[acc.d])
                    else:
                        x.op("dve", lambda e: e.scalar_tensor_tensor(out=acc[:, cs], in0=r_[:, 0:W], scalar=sgn[:, hi:hi + 1],
                                                                     in1=acc[:, cs], op0=ALU.mult, op1=ALU.add),
                             r=[r_.d, sgn.d, acc.d], w=[acc.d])
            for a in range(2):
                ks = slice((nkb - 2 + a) * 128, (nkb - 1 + a) * 128)
                x.op("dve", lambda e: e.tensor_tensor(out=acc[:, ks], in0=acc[:, ks], in1=dm[:, a, :], op=ALU.add),
                     r=[acc.d, dm.d], w=[acc.d])
            if i >= 1:
                x.op("pool", lambda e: e.tensor_copy(work[:, 0:L], acc[:, 0:L]), r=[acc.d], w=[work.d])
                for rd in range(32):
                    x.op("dve", lambda e: e.max(out=m8[:], in_=work[:, 0:L]), r=[work.d], w=[m8.d])
                    if rd < 31:
                        x.op("dve", lambda e: e.match_replace(out=work[:, 0:L], in_to_replace=m8[:], in_values=work[:, 0:L],
                                                              imm_value=-1e30), r=[m8.d, work.d], w=[work.d])
                x.op("dve", lambda e: e.tensor_copy(thr[:], m8[:, 7:8]), r=[m8.d], w=[thr.d])
                th = thr
            else:
                th = thr0
            x.op("dve", lambda e: e.tensor_scalar(out=nb[:, 0:L], in0=acc[:, 0:L], scalar1=th[:, 0:1], scalar2=NEG,
                                                  op0=ALU.is_lt, op1=ALU.mult), r=[acc.d, th.d], w=[nb.d])

        def part_T(i):
            b2 = i % 2
            nkb = 2 * i + 2
            L = nkb * 128
            for kg in range((nkb + 3) // 4):
                p = p_tr[kg % 2]
                nn = min(4, nkb - kg * 4)
                for j in range(nn):
                    kb = kg * 4 + j
                    x.op("pe", lambda e: e.transpose(p[:, j * 128:(j + 1) * 128], nb[:, kb * 128:(kb + 1) * 128], c.ident[:]),
                         r=[nb.d, c.ident.d], w=[p.d], inc=(j == nn - 1))
                src = p[:, 0:nn * 128].rearrange("p (a b) -> p a b", a=nn)
                x.op("act", lambda e: e.activation(out=nbT4[:, kg * 4:kg * 4 + nn, :, :],
                                                   in_=bc(src.unsqueeze(2), [128, nn, 4, 128]), func=AF.Copy),
                     r=[p.d], w=[nbT4.d])

        def part_C(i):
            b2 = i % 2
            nkb = 2 * i + 2
            L = nkb * 128
            obb = ob[b2]
            asteps = [(g, kb) for g in range(4) for kb in range(nkb)]

            def a_scores(idx):
                g, kb = asteps[idx]
                ps = p_s[idx % 2]
                pp_ = P[idx % 2]
                x.op("pe", lambda e: e.matmul(ps[:], kts[:, g, kb * 128:(kb + 1) * 128],
                                              qs[b2][:, g * 512:(g + 1) * 512], start=True, stop=False),
                     r=[kts.d, qs[b2].d], w=[ps.d], inc=False)
                x.op("pe", lambda e: e.matmul(ps[:], c.ident[:], nbT4[:, kb, :, :].rearrange("p a b -> p (a b)"),
                                              start=False, stop=True), r=[c.ident.d, nbT4.d], w=[ps.d])
                x.op("act", lambda e: e.activation(out=pp_[:], in_=ps[:], func=AF.Exp, scale=SCALE, bias=negC[:, 0:1]),
                     r=[ps.d, negC.d], w=[pp_.d])

            def a_pv(idx):
                g, kb = asteps[idx]
                pp_ = P[idx % 2]
                for r in range(4):
                    x.op("pe", lambda e: e.matmul(p_o[:, r, 0:129], pp_[:, r * 128:(r + 1) * 128], va[:, kb, g, :],
                                                  start=False, stop=(kb == nkb - 1 and r % 2 == 1)),
                         r=[pp_.d, va.d], w=[p_o.d], inc=(r == 3))

            def a_epi(g):
                x.op("dve", lambda e: e.reciprocal(out=R[:], in_=p_o[:, :, 128:129].rearrange("p a b -> p (a b)")),
                     r=[p_o.d], w=[R.d])
                for r in range(4):
                    hh = g * 4 + r
                    x.op("act", lambda e: e.activation(out=obb[:, hh * 128:(hh + 1) * 128], in_=p_o[:, r, 0:128],
                                                       func=AF.Copy, scale=R[:, r:r + 1]), r=[p_o.d, R.d], mw=[obb.d])

            a_scores(0)
            for idx, (g, kb) in enumerate(asteps):
                if kb == 0:
                    for bnk in range(2):
                        x.op("pe", lambda e: e.matmul(p_ob[:, bnk * 512:(bnk + 1) * 512], zr[:, 0:128], zr[:],
                                                      start=True, stop=False), r=[zr.d], w=[p_o.d], inc=False)
                if idx + 1 < len(asteps):
                    a_scores(idx + 1)
                a_pv(idx)
                if kb == nkb - 1:
                    a_epi(g)
            x.dma("sp", o[i * 128:(i + 1) * 128, :], obb[:], r=[obb.d], mw=[d_o])

        part_A(0)
        part_T(0)
        for i in range(NS):
            if i + 1 < NS:
                part_A(i + 1)
            part_C(i)
            if i + 1 < NS:
                part_T(i + 1)


def _standalone(emit, *args):
    nc = bass.Bass("TRN2", target_bir_lowering=False)
    dram = lambda n, s, dt, k: nc.dram_tensor(n, list(s), dt, kind=k).ap()
    with ExitStack() as st:
        x = X(nc, st)
        c = make_consts(x)
        emit(x, c, dram, *args)
        x.global_barrier()
        print(emit.__name__, args, "sems", x.nsem, "cnt", x.cnt)
    return nc


def build_LA(kind):
    return _standalone(emit_LA, kind)


def build_LB_DA(lambda_init):
    return _standalone(emit_LB_DA, lambda_init)


def build_LB_SSD():
    return _standalone(emit_LB_SSD)


def build_LB_DSA():
    return _standalone(emit_LB_DSA)


def build_LC(FO):
    return _standalone(emit_LC, FO)


def _invf_table(half):
    invf = np.power(np.float32(ROPE_THETA), -np.arange(half, dtype=np.float32) / half).astype(np.float32)
    return np.ascontiguousarray(np.broadcast_to(invf[None, :], (128, half))).astype(np.float32)


def _ca(a):
    return np.ascontiguousarray(a)


GROUPS = [[0, 1], [2, 3], [4, 5], [6, 7]]
FIN_K = {0: 6144, 1: 10304, 2: 4176}


def _mk_dram(mapping):
    def dram(n, s, dt, k):
        ap = mapping[n]
        assert [int(v) for v in ap.shape] == [int(v) for v in s], (n, ap.shape, s)
        return ap
    return dram


def build_fused(depth=DEPTH):
    nc = bass.Bass("TRN2", target_bir_lowering=False)
    ext_in = lambda n, s, dt: nc.dram_tensor(n, list(s), dt, kind="ExternalInput").ap()
    internal = lambda n, s, dt: nc.dram_tensor(n, list(s), dt, kind="Internal").ap()
    x_in = ext_in("x_in", [TOK, D], F32)
    c_in = ext_in("c_in", [D], F32)
    pos = ext_in("pos", [TOK], I32)
    rk = ext_in("rk", [1, 1], I32)
    invf8 = ext_in("invf8", [128, 8], F32)
    invf16 = ext_in("invf16", [128, 16], F32)
    dmask = ext_in("dmask", [2, 128, 128], F32)
    x_out = nc.dram_tensor("x_out", [TOK, D], F32, kind="ExternalOutput").ap()
    xres = internal("xres", [TOK, D], F32)
    xmid = internal("xmid", [TOK, D], F32)
    scratch = {}

    def scr(n, s, dt):
        if n not in scratch:
            scratch[n] = internal(n, s, dt)
        return scratch[n]

    with ExitStack() as st:
        x = X(nc, st)
        c = make_consts(x)
        reg = st.enter_context(nc.gpsimd.register("rk"))
        nc.gpsimd.reg_load(reg, rk[0:1, 0:1])
        r = nc.gpsimd.snap(reg, min_val=0, max_val=1)
        d_g = x.mkdep("xchg")
        RS = bass.ds(r, 1)
        CH = 2 * 1024 * 1024
        MAXE = {BF16: 12 * 1024 * 1024 + 4096, F32: 1024 * 1024}
        GBS = {BF16: [], F32: []}
        goff = {BF16: 0, F32: 0}

        def gather(parts, both=False, stage=False):
            a0 = parts[0]
            dt = a0.dtype
            shp = [int(v) for v in a0.shape]
            rowe = int(np.prod(shp[1:]))
            c0 = max(1, min(shp[0], CH // (rowe * mybir.dt.size(dt))))
            while shp[0] % c0:
                c0 -= 1
            nch = shp[0] // c0
            ce = c0 * rowe
            assert nch * 2 * ce <= MAXE[dt], (nch, ce, dt)
            if goff[dt] >= len(GBS[dt]):
                GBS[dt].append(internal("GB%d_%d" % (mybir.dt.size(dt), len(GBS[dt])), [2, MAXE[dt]], dt))
            gb = GBS[dt][goff[dt]]
            goff[dt] += 1
            off = 0
            for d_, a in enumerate(parts):
                for k in range(nch):
                    x.op("pool", lambda e: e.collective_compute(
                        "AllGather", ALU.bypass, replica_groups=GROUPS, ins=[a[k * c0:(k + 1) * c0].opt()],
                        outs=[gb[d_, off + k * 2 * ce:off + (k + 1) * 2 * ce].opt()]), w=[d_g])
            x.global_barrier()
            tot = nch * 2 * ce
            if stage:
                gs = scr("GS%d_%d" % (mybir.dt.size(dt), goff[dt] - 1), [1, MAXE[dt]], dt)
                CP = 4 * 1024 * 1024
                for o_ in range(0, tot, CP):
                    n_ = min(CP, tot - o_)
                    x.dma("pool", gs[:, o_:o_ + n_], (gb[0:1] if both else gb[RS])[:, o_:o_ + n_], mw=[d_g])
                row = gs[:, 0:tot]
            else:
                row = (gb[0:1] if both else gb[RS])[:, 0:tot]
            names = ["e%d" % i_ for i_ in range(len(shp) - 1)]
            kw = {"k": nch, "s": 2, "c": c0}
            kw.update({n_: v_ for n_, v_ in zip(names, shp[1:])})
            return row.rearrange("a (k s c %s) -> (a k) s c %s" % (" ".join(names), " ".join(names)), **kw)

        def cp(dst, src):
            x.dma("pool", dst, src, mw=[d_g])

        for i in range(depth):
            kind, j = i % 3, i // 3
            xsrc = x_in if i == 0 else xres
            xdst = x_out if i == depth - 1 else xres
            sfx = "_%d" % i
            ada_i = internal("ada" + sfx, [6 * D], F32)
            mp = {"x_in": xsrc, "c_in": c_in, "pos": pos, "ada_w": ext_in("ada_w" + sfx, [D, 6 * D], F32),
                  "ada_b": ext_in("ada_b" + sfx, [6 * D], F32), "g1": ext_in("g1" + sfx, [D], F32), "ada": ada_i,
                  "w_in": ext_in("w_in" + sfx, [D, FIN_K[kind]], F32)}
            goff[BF16] = goff[F32] = 0
            if kind == 0:
                A = {"qT": scr("A_qT", [16, 128, TOK], BF16), "kT": scr("A_kT", [16, 128, TOK], BF16),
                     "v": scr("A_v", [2, TOK, 1024], BF16)}
                mp.update(A)
                mp.update({"invf": invf8, "gq": ext_in("gq" + sfx, [64], F32), "gk": ext_in("gk" + sfx, [64], F32)})
                emit_LA(x, c, _mk_dram(mp), kind, True)
                x.global_barrier()
                Gq = gather([A["qT"][0:8], A["qT"][8:16]])
                Gk = gather([A["kT"][0:8], A["kT"][8:16]])
                Gv = gather([A["v"][0], A["v"][1]])
                x.global_barrier()
                L_qT = scr("L_qT", [8, 128, S], BF16)
                L_kT = scr("L_kT", [8, 128, S], BF16)
                L_v = scr("L_v", [S, 1024], BF16)
                for s_ in range(2):
                    cs = slice(s_ * TOK, (s_ + 1) * TOK)
                    for kk in range(2):
                        cp(L_qT[kk * 4:(kk + 1) * 4, :, cs], Gq[kk, s_])
                        cp(L_kT[kk * 4:(kk + 1) * 4, :, cs], Gk[kk, s_])
                        cp(L_v[s_ * TOK + kk * 1024:s_ * TOK + (kk + 1) * 1024, :], Gv[kk, s_])
                x.global_barrier()
                B_o = scr("B_o", [S, 1024], BF16)
                li = 0.8 - 0.6 * math.exp(-0.3 * i)
                emit_LB_DA(x, c, _mk_dram({"qT": L_qT, "kT": L_kT, "v": L_v, "lam4": ext_in("lam4" + sfx, [4, 64], F32),
                                           "gqk": ext_in("gqk" + sfx, [2, 64], F32),
                                           "subg": ext_in("subg" + sfx, [128], F32), "o": B_o}), float(li))
                x.global_barrier()
                goff[BF16] = 0
                Go = gather([B_o[0:TOK], B_o[TOK:S]])
                x.global_barrier()
                FO = 2048
                L_o = scr("L_o", [TOK, 2048], BF16)
                for hh in range(2):
                    for kk in range(2):
                        cp(L_o[kk * 1024:(kk + 1) * 1024, hh * 1024:(hh + 1) * 1024], Go[kk, hh])
            elif kind == 1:
                A = {"z": scr("A_z", [2, TOK, 2048], BF16), "xbcT": scr("A_xbcT", [2, 3072, TOK], BF16),
                     "dtr": scr("A_dtr", [2, TOK, 32], F32)}
                mp.update(A)
                emit_LA(x, c, _mk_dram(mp), kind, True)
                x.global_barrier()
                Gz = gather([A["z"][0], A["z"][1]])
                Gx = gather([A["xbcT"][0], A["xbcT"][1]])
                Gd = gather([A["dtr"][0], A["dtr"][1]])
                x.global_barrier()
                L_raw = scr("L_raw", [3072, S], F32)
                L_z = scr("L_z", [S, 2048], BF16)
                L_dt = scr("L_dt", [S, 32], F32)
                for s_ in range(2):
                    cs = slice(s_ * TOK, (s_ + 1) * TOK)
                    for kk in range(6):
                        cp(L_raw[kk * 512:(kk + 1) * 512, cs], Gx[kk, s_])
                    for kk in range(4):
                        cp(L_z[s_ * TOK + kk * 512:s_ * TOK + (kk + 1) * 512, :], Gz[kk, s_])
                    cp(L_dt[cs, :], Gd[0, s_])
                x.global_barrier()
                B_y = scr("B_y", [S, 2048], BF16)
                emit_LB_SSD(x, c, _mk_dram({"raw": L_raw, "convw": ext_in("convw" + sfx, [4, 3072], F32),
                                            "convb": ext_in("convb" + sfx, [3072], F32), "dtr": L_dt,
                                            "hp": ext_in("hp" + sfx, [3, 32], F32), "z": L_z,
                                            "ng": ext_in("ng" + sfx, [2048], F32),
                                            "tokd": scr("tokd", [S, 2560], BF16), "featd": scr("featd", [1024, S], BF16),
                                            "y": B_y}))
                x.global_barrier()
                goff[BF16] = 0
                Gy = gather([B_y[0:TOK], B_y[TOK:S]])
                x.global_barrier()
                FO = 4096
                L_o = scr("L_o4", [TOK, 4096], BF16)
                for gh in range(2):
                    for kk in range(4):
                        cp(L_o[kk * 512:(kk + 1) * 512, gh * 2048:(gh + 1) * 2048], Gy[kk, gh])
            else:
                A = {"qT": scr("D_qT", [2, 16, 128, 8, 128], BF16), "kT": scr("D_kT", [4, 128, TOK], BF16),
                     "v": scr("D_v", [TOK, 512], BF16), "qiT": scr("D_qiT", [2, 8, 128, 8, 128], BF16),
                     "kiT": scr("D_kiT", [64, TOK], BF16), "wi": scr("D_wi", [2, 8, 128, 16], F32)}
                mp.update(A)
                mp.update({"invf16": invf16, "invf8": invf8, "gq": ext_in("gq" + sfx, [128], F32),
                           "gk": ext_in("gk" + sfx, [128], F32), "gi": ext_in("gi" + sfx, [64], F32)})
                emit_LA(x, c, _mk_dram(mp), kind, True)
                x.global_barrier()
                Gq = gather([A["qT"][0], A["qT"][1]], stage=True)
                Gqi = gather([A["qiT"][0], A["qiT"][1]], stage=True)
                Gw = gather([A["wi"][0], A["wi"][1]], stage=True)
                Gk = gather([A["kT"]], both=True)
                Gv = gather([A["v"]], both=True)
                Gki = gather([A["kiT"]], both=True)
                x.global_barrier()
                L_qTs = scr("L_qTs", [16, 128, 2048], BF16)
                L_qiTs = scr("L_qiTs", [16, 128, 1024], BF16)
                L_wis = scr("L_wis", [16, 128, 16], F32)
                L_kT = scr("L_kT4", [4, 128, S], BF16)
                L_v = scr("L_v4", [S, 512], BF16)
                L_ki = scr("L_ki2", [128, S], BF16)
                with nc.allow_non_contiguous_dma(reason="tile gathers"):
                    for sl_ in range(16):
                        s_, k_ = sl_ // 8, sl_ % 8
                        for kk in range(2):
                            cp(L_qTs[sl_][:, kk * 1024:(kk + 1) * 1024].rearrange("d (h t) -> d h t", h=8),
                               Gq[kk, s_][:, :, k_, :].rearrange("h d t -> d h t"))
                        cp(L_qiTs[sl_].rearrange("d (h t) -> d h t", h=8),
                           Gqi[0, s_][:, :, k_, :].rearrange("h d t -> d h t"))
                        cp(L_wis[sl_], Gw[0, s_][k_])
                for s_ in range(2):
                    cs = slice(s_ * TOK, (s_ + 1) * TOK)
                    cp(L_kT[:, :, cs], Gk[0, s_])
                    cp(L_v[cs, :], Gv[0, s_])
                    for dup in range(2):
                        cp(L_ki[dup * 64:(dup + 1) * 64, cs], Gki[0, s_])
                x.global_barrier()
                B_o2 = scr("B_o2", [TOK, 2048], BF16)
                emit_LB_DSA(x, c, _mk_dram({"qTs": L_qTs, "qiTs": L_qiTs, "wis": L_wis, "kT": L_kT, "v": L_v,
                                            "kiT2": L_ki, "dmask": dmask,
                                            "gqk": ext_in("gqk" + sfx, [2, 128], F32), "o": B_o2}))
                x.global_barrier()
                goff[BF16] = 0
                Go2 = gather([B_o2[0:1024], B_o2[1024:2048]], stage=True)
                x.global_barrier()
                FO = 2048
                L_o = scr("L_o", [TOK, 2048], BF16)
                for tl in range(16):
                    jj = tl // 2
                    cp(L_o[tl * 128:(tl + 1) * 128, :], Go2[jj // 4, tl % 2][(jj % 4) * 128:(jj % 4 + 1) * 128, :])
            x.global_barrier()
            emit_LC(x, c, _mk_dram({"x_in": xsrc, "o_in": L_o, "ada": ada_i, "g2": ext_in("g2" + sfx, [D], F32),
                                    "w_out": ext_in("w_out" + sfx, [FO, D], F32),
                                    "wgu": ext_in("wgu" + sfx, [D, 2 * FFN], F32),
                                    "wd": ext_in("wd" + sfx, [FFN, D], F32), "x_mid": xmid, "x_out": xdst}), FO)
            x.global_barrier()
        print("fused sems", x.nsem, "cnt", x.cnt)
    return nc


def fused_inputs(inp, depth=DEPTH):
    x = np.asarray(inp["x"], dtype=np.float32)
    c = np.asarray(inp["c"], dtype=np.float32)
    pos = np.asarray(inp["positions"]).astype(np.int32)
    tri = np.where(np.arange(128)[None, :] <= np.arange(128)[:, None], 0.0, -1e30).astype(np.float32)
    full_neg = np.full((128, 128), -1e30, np.float32)
    zero_m = np.zeros((128, 128), np.float32)
    f32 = lambda a: _ca(np.asarray(a, dtype=np.float32))
    maps = []
    for cc in range(8):
        b, h = cc // 2, cc % 2
        sl = slice(h * TOK, (h + 1) * TOK)
        m = {"x_in": _ca(x[b, sl]), "c_in": _ca(c[b]), "pos": _ca(pos[b, sl]), "rk": np.array([[h]], np.int32),
             "invf8": _invf_table(8), "invf16": _invf_table(16),
             "dmask": np.stack([tri, full_neg]) if h == 0 else np.stack([zero_m, tri])}
        for i in range(depth):
            kind, j = i % 3, i // 3
            sfx = "_%d" % i
            m["ada_w" + sfx] = inp["ada_w"][i]
            m["ada_b" + sfx] = inp["ada_b"][i]
            m["g1" + sfx] = inp["norm1_g"][i]
            m["g2" + sfx] = inp["norm2_g"][i]
            m["wgu" + sfx] = inp["ffn_w_gate_up"][i]
            m["wd" + sfx] = inp["ffn_w_down"][i]
            if kind == 0:
                m["w_in" + sfx] = inp["da_w_in"][j]
                m["w_out" + sfx] = inp["da_w_out"][j]
                m["gq" + sfx] = inp["da_q_norm_g"][j]
                m["gk" + sfx] = inp["da_k_norm_g"][j]
                m["lam4" + sfx] = f32(np.stack([inp["da_lambda_q1"][j], inp["da_lambda_k1"][j],
                                                inp["da_lambda_q2"][j], inp["da_lambda_k2"][j]]))
                m["gqk" + sfx] = f32(np.stack([inp["da_q_norm_g"][j], inp["da_k_norm_g"][j]]))
                m["subg" + sfx] = inp["da_subln_g"][j]
            elif kind == 1:
                m["w_in" + sfx] = inp["ssd_w_in"][j]
                m["w_out" + sfx] = inp["ssd_w_out"][j]
                ch = np.concatenate([np.arange(h * 2048, (h + 1) * 2048), np.arange(4096 + h * 512, 4096 + (h + 1) * 512),
                                     np.arange(5120 + h * 512, 5120 + (h + 1) * 512)])
                hs = slice(h * 32, (h + 1) * 32)
                m["convw" + sfx] = _ca(inp["ssd_conv_w"][j][:, ch])
                m["convb" + sfx] = _ca(inp["ssd_conv_b"][j][ch])
                m["hp" + sfx] = f32(np.stack([inp["ssd_dt_bias"][j][hs], inp["ssd_a_log"][j][hs], inp["ssd_d_skip"][j][hs]]))
                m["ng" + sfx] = _ca(inp["ssd_norm_g"][j][h * 2048:(h + 1) * 2048])
            else:
                m["w_in" + sfx] = inp["sa_w_in"][j]
                m["w_out" + sfx] = inp["sa_w_out"][j]
                m["gq" + sfx] = inp["sa_q_norm_g"][j]
                m["gk" + sfx] = inp["sa_k_norm_g"][j]
                m["gi" + sfx] = inp["sa_idx_k_norm_g"][j]
                m["gqk" + sfx] = f32(np.stack([inp["sa_q_norm_g"][j], inp["sa_k_norm_g"][j]]))
        maps.append(m)
    return maps


def kernel(**inp):
    depth = DEPTH
    nc = build_fused(depth)
    maps = fused_inputs(inp, depth)
    res = run_bass_kernel_spmd(nc, maps, core_ids=list(range(8))).results
    out = np.empty((NB, S, D), np.float32)
    for cc in range(8):
        b, h = cc // 2, cc % 2
        out[b, h * TOK:(h + 1) * TOK] = res[cc]["x_out"]
    return out
```

```python
import math
import numpy as np
from contextlib import ExitStack
import ml_dtypes
import concourse.bass as bass
import concourse.mybir as mybir
from concourse.bass_utils import run_bass_kernel_spmd

F32 = mybir.dt.float32
BF16 = mybir.dt.bfloat16
I32 = mybir.dt.int32
ALU = mybir.AluOpType
AF = mybir.ActivationFunctionType
AX = mybir.AxisListType

D = 2048
S = 4096
NB = 4
DEPTH = 4
KC = D // 128
TOK = 2048
NT = TOK // 128
FFN = 5632
EPS = 1e-6
ROPE_THETA = 500000.0
NEG = -30000.0

SAME_ENGINE_SYNC = True


class Dep:
    __slots__ = ("name", "w", "r", "dsem", "dq")

    def __init__(self, name=""):
        self.name = name
        self.w = {}
        self.r = {}
        self.dsem = None
        self.dq = None


class X:
    def __init__(self, nc, stack):
        self.nc = nc
        self.stack = stack
        self.root = stack
        self.eng = {"pe": nc.tensor, "act": nc.scalar, "dve": nc.vector,
                    "pool": nc.gpsimd, "sp": nc.sync}
        self.sem = {}
        self.cnt = {}
        self.seen = {}
        for k in self.eng:
            self.sem[k] = stack.enter_context(nc.semaphore("es_" + k))
            self.cnt[k] = 0
            self.seen[k] = {}
        self.nsem = 5
        self.semcnt = {}
        self.free_dsems = {"sp": [], "pool": [], "act": []}
        self.alltok = {}
        self.uid = 0

    def name(self, p):
        self.uid += 1
        return "%s_%d" % (p, self.uid)

    def sb(self, shape, dt, name="sb", stack=None):
        st = stack or self.stack
        return st.enter_context(self.nc.sbuf_tensor(self.name(name), list(shape), dt))

    def ps(self, shape, dt=F32, name="ps", stack=None):
        st = stack or self.stack
        return st.enter_context(self.nc.psum_tensor(self.name(name), list(shape), dt))

    def _wait(self, e, toks):
        en = self.eng[e]
        seen = self.seen[e]
        for s, v in toks.items():
            cv = self.semcnt.get(s)
            if cv is not None and cv > v:
                v = cv
            if seen.get(s, 0) >= v:
                continue
            if s is self.sem[e]:
                if e == "pe" or not SAME_ENGINE_SYNC:
                    continue
            en.wait_ge(s, v)
            seen[s] = v

    def _pre(self, e, r, w, mw=()):
        for d in r:
            self._wait(e, d.w)
        for d in w:
            self._wait(e, d.w)
            self._wait(e, d.r)
        for d in mw:
            self._wait(e, d.r)

    def _post(self, tok, r, w, mw=()):
        s, v = tok
        if self.alltok.get(s, 0) < v:
            self.alltok[s] = v
        for d in r:
            if d.r.get(s, 0) < v:
                d.r[s] = v
        for d in w:
            d.w = {s: v}
            d.r = {}
        for d in mw:
            if d.w.get(s, 0) < v:
                d.w[s] = v

    def op(self, e, fn, r=(), w=(), mw=(), inc=True):
        self._pre(e, r, w, mw)
        ins = fn(self.eng[e])
        if inc:
            self.cnt[e] += 1
            ins.then_inc(self.sem[e], 1)
            self._post((self.sem[e], self.cnt[e]), r, w, mw)
        else:
            self._post((self.sem[e], self.cnt[e] + 1), r, w, mw)
        return ins

    def dma(self, e, out, in_, r=(), w=(), mw=(), **kw):
        host = (list(w) + list(mw))[0]
        assert host.dq in (None, e), (host.name, host.dq, e)
        if host.dsem is None:
            host.dq = e
            if self.free_dsems[e]:
                host.dsem = self.free_dsems[e].pop()
            else:
                host.dsem = self.root.enter_context(self.nc.semaphore(self.name("ds")))
                self.semcnt[host.dsem] = 0
                self.nsem += 1
        self._pre(e, r, w, mw)
        ins = self.eng[e].dma_start(out=out, in_=in_, **kw)
        self.semcnt[host.dsem] += 16
        ins.then_inc(host.dsem, 16)
        self._post((host.dsem, self.semcnt[host.dsem]), r, w, mw)
        return ins

    def mkdep(self, name=""):
        d = Dep(name)
        if isinstance(self.stack, Scope):
            self.stack.deps.append(d)
        return d

    def global_barrier(self):
        for e in self.eng:
            self._wait(e, dict(self.alltok))

    def barrier(self, deps):
        toks = {}
        for d in deps:
            for src in (d.w, d.r):
                for s_, v in src.items():
                    if toks.get(s_, 0) < v:
                        toks[s_] = v
        for e in self.eng:
            self._wait(e, toks)

    def finish(self, deps, e="sp"):
        for d in deps:
            self._wait(e, d.w)


class Scope:
    def __init__(self, x):
        self.x = x
        self.st = ExitStack()
        self.deps = []

    def __enter__(self):
        self.st.__enter__()
        self.prev = self.x.stack
        self.x.stack = self
        return self

    def __exit__(self, *a):
        self.x.stack = self.prev
        if a[0] is None:
            self.x.barrier(self.deps)
            for d in self.deps:
                if d.dsem is not None:
                    self.x.free_dsems[d.dq].append(d.dsem)
                    d.dsem = None
                    d.dq = None
        return self.st.__exit__(*a)

    def enter_context(self, cm):
        return self.st.enter_context(cm)


class T:
    def __init__(self, x, shape, dt, name, psum=False, stack=None):
        stack = stack or x.stack
        self.t = x.ps(shape, dt, name, stack) if psum else x.sb(shape, dt, name, stack)
        self.d = Dep(name)
        if isinstance(stack, Scope):
            stack.deps.append(self.d)

    def __getitem__(self, k):
        return self.t[k]


def sbt(x, shape, dt, name, stack=None):
    return T(x, shape, dt, name, False, stack)


def pst(x, shape, dt, name, stack=None):
    return T(x, shape, dt, name, True, stack)


class Consts:
    pass


def make_consts(x):
    c = Consts()
    idf = sbt(x, [128, 128], F32, "idf")
    c.ident = sbt(x, [128, 128], BF16, "ident")
    x.op("pool", lambda e: e.memset(idf[:], 1.0), w=[idf.d])
    x.op("pool", lambda e: e.affine_select(out=idf[:], in_=idf[:], pattern=[[-1, 128]],
                                           compare_op=ALU.is_equal, fill=0.0, base=0,
                                           channel_multiplier=1), r=[idf.d], w=[idf.d])
    x.op("dve", lambda e: e.tensor_copy(c.ident[:], idf[:]), r=[idf.d], w=[c.ident.d])
    c.identf = idf
    ngf = sbt(x, [128, 128], F32, "ngf")
    c.negT = sbt(x, [128, 128], BF16, "negT")
    x.op("pool", lambda e: e.memset(ngf[:], 0.0), w=[ngf.d])
    x.op("pool", lambda e: e.affine_select(out=ngf[:], in_=ngf[:], pattern=[[1, 128]],
                                           compare_op=ALU.is_ge, fill=NEG, base=0,
                                           channel_multiplier=-1), r=[ngf.d], w=[ngf.d])
    x.op("dve", lambda e: e.tensor_copy(c.negT[:], ngf[:]), r=[ngf.d], w=[c.negT.d])
    c.negQ = sbt(x, [128, 128], F32, "negQ")
    x.op("pool", lambda e: e.memset(c.negQ[:], 0.0), w=[c.negQ.d])
    x.op("pool", lambda e: e.affine_select(out=c.negQ[:], in_=c.negQ[:], pattern=[[-1, 128]],
                                           compare_op=ALU.is_ge, fill=-1e30, base=0,
                                           channel_multiplier=1), r=[c.negQ.d], w=[c.negQ.d])
    trf = sbt(x, [128, 128], F32, "trf")
    c.tri = sbt(x, [128, 128], BF16, "tri")
    x.op("pool", lambda e: e.memset(trf[:], 1.0), w=[trf.d])
    x.op("pool", lambda e: e.affine_select(out=trf[:], in_=trf[:], pattern=[[1, 128]],
                                           compare_op=ALU.is_ge, fill=0.0, base=0,
                                           channel_multiplier=-1), r=[trf.d], w=[trf.d])
    x.op("dve", lambda e: e.tensor_copy(c.tri[:], trf[:]), r=[trf.d], w=[c.tri.d])
    c.trif = trf
    c.ones = sbt(x, [128, 128], BF16, "ones")
    x.op("pool", lambda e: e.memset(c.ones[:], 1.0), w=[c.ones.d])
    c.onesf = sbt(x, [128, 128], F32, "onesf")
    x.op("pool", lambda e: e.memset(c.onesf[:], 1.0), w=[c.onesf.d])
    c.nhalf = sbt(x, [128, 64], F32, "nhalf")
    x.op("pool", lambda e: e.memset(c.nhalf[:], -0.5), w=[c.nhalf.d])
    return c


def rsqrt_mean(x, c, out, ss, n, width):
    x.op("dve", lambda e: e.tensor_scalar(out=out[:, 0:width], in0=ss[:, 0:width], scalar1=1.0 / n,
                                          scalar2=EPS, op0=ALU.mult, op1=ALU.add),
         r=[ss.d], w=[out.d])
    x.op("pool", lambda e: e.tensor_tensor(out=out[:, 0:width], in0=out[:, 0:width],
                                           in1=c.nhalf[:, 0:width], op=ALU.pow),
         r=[out.d, c.nhalf.d], w=[out.d])


def bc(ap, shape):
    return ap.to_broadcast(list(shape))


def emit_ada(x, c_ap, adaw_ap, adab_ap, ada_ap, d_ada):
    with Scope(x) as ls:
        cs = sbt(x, [128, 16], F32, "c_sb", ls)
        ca = sbt(x, [128, 16], F32, "c_act", ls)
        brow = sbt(x, [1, 6 * D], F32, "brow", ls)
        arow = sbt(x, [1, 6 * D], F32, "arow", ls)
        wt = [sbt(x, [128, 16, 512], F32, "adaw%d" % i, ls) for i in range(2)]
        pa = [pst(x, [1, 512], F32, "adap%d" % i, ls) for i in range(2)]
        x.dma("sp", cs[:], c_ap.rearrange("(p k) -> p k", k=16), w=[cs.d])
        x.dma("sp", brow[:], adab_ap.rearrange("(o n) -> o n", o=1), w=[brow.d])
        x.op("act", lambda e: e.activation(out=ca[:], in_=cs[:], func=AF.Silu), r=[cs.d], w=[ca.d])
        for blk in range(24):
            i = blk % 2
            x.dma("sp", wt[i][:], adaw_ap[:, blk * 512:(blk + 1) * 512].rearrange("(p k) f -> p k f", k=16),
                  w=[wt[i].d])
            for k in range(16):
                x.op("pe", lambda e: e.matmul(pa[i][:], ca[:, k:k + 1], wt[i][:, k, :],
                                              start=(k == 0), stop=(k == 15)),
                     r=[ca.d, wt[i].d], w=[pa[i].d], inc=(k == 15))
            x.op("dve", lambda e: e.tensor_tensor(out=arow[0:1, blk * 512:(blk + 1) * 512], in0=pa[i][:],
                                                  in1=brow[0:1, blk * 512:(blk + 1) * 512], op=ALU.add),
                 r=[pa[i].d, brow.d], mw=[arow.d])
        x.dma("sp", ada_ap.rearrange("(o n) -> o n", o=1), arow[:], r=[arow.d], w=[d_ada])


def load_cols(x, ls, vec_ap, name, d_src=None):
    t = sbt(x, [128, 16], F32, name, ls)
    x.dma("sp", t[:], vec_ap.rearrange("(k p) -> p k", p=128), r=([d_src] if d_src else []), w=[t.d],
          allow_slow_non_contiguous=True)
    return t


def load_bcast(x, ls, vec_ap, n, name, d_src=None, dt=F32):
    t = sbt(x, [128, n], dt, name, ls)
    x.dma("sp", t[:], vec_ap.rearrange("(o n) -> o n", o=1).partition_broadcast(128),
          r=([d_src] if d_src else []), w=[t.d])
    return t


def emit_mod_cols(x, ls, g_ap, ada_ap, d_ada, scale_idx, shift_idx):
    g = load_cols(x, ls, g_ap, "gcol")
    sc = load_cols(x, ls, ada_ap[scale_idx * D:(scale_idx + 1) * D], "sccol", d_ada)
    sh = load_cols(x, ls, ada_ap[shift_idx * D:(shift_idx + 1) * D], "shcol", d_ada)
    sT = sbt(x, [128, 16], F32, "sT", ls)
    x.op("dve", lambda e: e.scalar_tensor_tensor(out=sT[:], in0=sc[:], scalar=1.0, in1=g[:],
                                                 op0=ALU.add, op1=ALU.mult),
         r=[sc.d, g.d], w=[sT.d])
    return sT, sh


def emit_norm_hT(x, c, x_ap, d_x, ntiles, sT, shT, hT, hT_d, tile_off=0):
    with Scope(x) as ls:
        xt = [sbt(x, [128, D], F32, "xt%d" % i, ls) for i in range(2)]
        xn = [sbt(x, [128, D], BF16, "xn%d" % i, ls) for i in range(2)]
        junk = sbt(x, [128, D], BF16, "junk", ls)
        ss = [sbt(x, [128, 1], F32, "ss%d" % i, ls) for i in range(2)]
        rstd = [sbt(x, [128, 1], F32, "rstd%d" % i, ls) for i in range(2)]
        pt = [pst(x, [128, 512], BF16, "ptn%d" % i, ls) for i in range(2)]
        for t in range(ntiles):
            i = t % 2
            x.dma("sp", xt[i][:], x_ap[t * 128:(t + 1) * 128, :], r=[d_x], w=[xt[i].d])
            x.op("act", lambda e: e.activation(out=junk[:], in_=xt[i][:], func=AF.Square,
                                               accum_out=ss[i][:, 0:1]),
                 r=[xt[i].d], w=[junk.d, ss[i].d])
            rsqrt_mean(x, c, rstd[i], ss[i], D, 1)
            x.op("act", lambda e: e.activation(out=xn[i][:], in_=xt[i][:], func=AF.Copy,
                                               scale=rstd[i][:, 0:1]),
                 r=[xt[i].d, rstd[i].d], w=[xn[i].d])
            for g in range(4):
                p = pt[g % 2]
                for j in range(4):
                    kc = g * 4 + j
                    x.op("pe", lambda e: e.transpose(p[:, j * 128:(j + 1) * 128],
                                                     xn[i][:, kc * 128:(kc + 1) * 128], c.ident[:]),
                         r=[xn[i].d, c.ident.d], w=[p.d], inc=(j == 3))
                for j in range(4):
                    kc = g * 4 + j
                    dst = hT[:, kc, (tile_off + t) * 128:(tile_off + t + 1) * 128]
                    if True:
                        x.op("dve", lambda e: e.tensor_scalar(out=dst, in0=p[:, j * 128:(j + 1) * 128],
                                                              scalar1=sT[:, kc:kc + 1], scalar2=shT[:, kc:kc + 1],
                                                              op0=ALU.mult, op1=ALU.add),
                             r=[p.d, sT.d, shT.d], mw=[hT_d[tile_off + t]])
                    else:
                        x.op("act", lambda e: e.activation(out=dst, in_=p[:, j * 128:(j + 1) * 128],
                                                           func=AF.Identity, scale=sT[:, kc:kc + 1],
                                                           bias=shT[:, kc:kc + 1]),
                             r=[p.d, sT.d, shT.d], mw=[hT_d[tile_off + t]])


def emit_rope_tables(x, ls, pos_ap, invf_ap, half, ntiles):
    n = ntiles * half
    posi = sbt(x, [128, ntiles], I32, "posi", ls)
    posf = sbt(x, [128, ntiles], F32, "posf", ls)
    invf = sbt(x, [128, half], F32, "invf", ls)
    ang = sbt(x, [128, ntiles, half], F32, "ang", ls)
    x.dma("sp", posi[:], pos_ap.rearrange("(t p) -> p t", p=128), w=[posi.d], allow_slow_non_contiguous=True)
    x.dma("sp", invf[:], invf_ap, w=[invf.d])
    x.op("dve", lambda e: e.tensor_copy(posf[:], posi[:]), r=[posi.d], w=[posf.d])
    x.op("dve", lambda e: e.tensor_tensor(out=ang[:], in0=bc(posf[:].unsqueeze(2), [128, ntiles, half]),
                                          in1=bc(invf[:].unsqueeze(1), [128, ntiles, half]), op=ALU.mult),
         r=[posf.d, invf.d], w=[ang.d])
    outs = []
    C1 = 6.28125
    C2 = 2.0 * math.pi - C1
    for nm, shift in (("cos", math.pi / 2), ("sin", 0.0)):
        a = sbt(x, [128, n], F32, nm + "_a", ls)
        ki = sbt(x, [128, n], I32, nm + "_ki", ls)
        kf = sbt(x, [128, n], F32, nm + "_kf", ls)
        m = sbt(x, [128, n], F32, nm + "_m", ls)
        res = sbt(x, [128, ntiles, half], F32, nm + "_t", ls)
        af = ang[:].rearrange("p t h -> p (t h)")
        x.op("dve", lambda e: e.tensor_scalar(out=a[:], in0=af, scalar1=shift, scalar2=None, op0=ALU.add),
             r=[ang.d], w=[a.d])
        x.op("dve", lambda e: e.tensor_scalar(out=kf[:], in0=a[:], scalar1=1.0 / (2 * math.pi), scalar2=None,
                                              op0=ALU.mult), r=[a.d], w=[kf.d])
        x.op("dve", lambda e: e.tensor_copy(ki[:], kf[:]), r=[kf.d], w=[ki.d])
        x.op("dve", lambda e: e.tensor_copy(kf[:], ki[:]), r=[ki.d], w=[kf.d])
        x.op("dve", lambda e: e.scalar_tensor_tensor(out=a[:], in0=kf[:], scalar=-C1, in1=a[:],
                                                     op0=ALU.mult, op1=ALU.add), r=[kf.d, a.d], w=[a.d])
        x.op("dve", lambda e: e.scalar_tensor_tensor(out=a[:], in0=kf[:], scalar=-C2, in1=a[:],
                                                     op0=ALU.mult, op1=ALU.add), r=[kf.d, a.d], w=[a.d])
        x.op("dve", lambda e: e.tensor_scalar(out=m[:], in0=a[:], scalar1=math.pi, scalar2=2 * math.pi,
                                              op0=ALU.is_gt, op1=ALU.mult), r=[a.d], w=[m.d])
        x.op("dve", lambda e: e.tensor_tensor(out=a[:], in0=a[:], in1=m[:], op=ALU.subtract),
             r=[a.d, m.d], w=[a.d])
        x.op("dve", lambda e: e.tensor_scalar(out=m[:], in0=a[:], scalar1=-math.pi, scalar2=2 * math.pi,
                                              op0=ALU.is_lt, op1=ALU.mult), r=[a.d], w=[m.d])
        x.op("dve", lambda e: e.tensor_tensor(out=a[:], in0=a[:], in1=m[:], op=ALU.add),
             r=[a.d, m.d], w=[a.d])
        x.op("dve", lambda e: e.tensor_scalar(out=a[:], in0=a[:], scalar1=-3.1415925, scalar2=3.1415925,
                                              op0=ALU.max, op1=ALU.min), r=[a.d], w=[a.d])
        x.op("act", lambda e: e.activation(out=res[:].rearrange("p t h -> p (t h)"), in_=a[:], func=AF.Sin),
             r=[a.d], w=[res.d])
        outs.append(res)
    return outs[0], outs[1]


def emit_qk_post(x, c, ls_tiles, ps, ncols, gdim, gain, half, cos, sin, t, out_bf):
    ng = ncols // gdim
    qn, sq, ssq, rs, t1, t2, t3, t4 = ls_tiles
    pv = ps[:, 0:ncols].rearrange("p (g d) -> p g d", d=gdim)
    qv = qn[:, 0:ncols].rearrange("p (g d) -> p g d", d=gdim)
    if gain is not None:
        x.op("act", lambda e: e.activation(out=sq[:, 0:ncols], in_=ps[:, 0:ncols], func=AF.Square),
             r=[ps.d], w=[sq.d])
        x.op("dve", lambda e: e.tensor_reduce(out=ssq[:, 0:ng], in_=sq[:, 0:ncols].rearrange("p (g d) -> p g d", d=gdim),
                                              axis=AX.X, op=ALU.add), r=[sq.d], w=[ssq.d])
        rsqrt_mean(x, c, rs, ssq, gdim, ng)
        x.op("dve", lambda e: e.tensor_tensor(out=qv, in0=pv, in1=bc(rs[:, 0:ng].unsqueeze(2), [128, ng, gdim]),
                                              op=ALU.mult), r=[ps.d, rs.d], w=[qn.d])
        x.op("pool", lambda e: e.tensor_tensor(out=qv, in0=qv, in1=bc(gain[:, 0:gdim].unsqueeze(1), [128, ng, gdim]),
                                               op=ALU.mult), r=[qn.d, gain.d], w=[qn.d])
    else:
        x.op("act", lambda e: e.activation(out=qn[:, 0:ncols], in_=ps[:, 0:ncols], func=AF.Copy),
             r=[ps.d], w=[qn.d])
    if half:
        x1 = qv[:, :, 0:half]
        x2 = qv[:, :, half:2 * half]
        cb = bc(cos[:, t, :].unsqueeze(1), [128, ng, half])
        sb_ = bc(sin[:, t, :].unsqueeze(1), [128, ng, half])
        tv = [tt[:, 0:ng * half].rearrange("p (g h) -> p g h", h=half) for tt in (t1, t2, t3, t4)]
        x.op("dve", lambda e: e.tensor_tensor(out=tv[0], in0=x1, in1=cb, op=ALU.mult), r=[qn.d, cos.d], w=[t1.d])
        x.op("dve", lambda e: e.tensor_tensor(out=tv[1], in0=x2, in1=sb_, op=ALU.mult), r=[qn.d, sin.d], w=[t2.d])
        x.op("pool", lambda e: e.tensor_tensor(out=tv[2], in0=x2, in1=cb, op=ALU.mult), r=[qn.d, cos.d], w=[t3.d])
        x.op("pool", lambda e: e.tensor_tensor(out=tv[3], in0=x1, in1=sb_, op=ALU.mult), r=[qn.d, sin.d], w=[t4.d])
        x.op("dve", lambda e: e.tensor_tensor(out=x1, in0=tv[0], in1=tv[1], op=ALU.subtract),
             r=[t1.d, t2.d], w=[qn.d])
        x.op("pool", lambda e: e.tensor_tensor(out=x2, in0=tv[2], in1=tv[3], op=ALU.add),
             r=[t3.d, t4.d, qn.d], w=[qn.d])
    x.op("act", lambda e: e.activation(out=out_bf[:, 0:ncols], in_=qn[:, 0:ncols], func=AF.Copy),
         r=[qn.d], w=[out_bf.d])


def emit_qk_post2(x, c, ls_tiles, ps, gdim, gain, half, cos, sin, t0, out_bf):
    ng = 512 // gdim
    ng2 = 2 * ng
    qn, sq, ssq, rs, t1, t2, t3, t4 = ls_tiles
    psf = ps[:].rearrange("p a c -> p (a c)")
    pv = ps[:].rearrange("p a (g d) -> p (a g) d", d=gdim)
    qv = qn[:, 0:1024].rearrange("p (g d) -> p g d", d=gdim)
    if gain is not None:
        x.op("act", lambda e: e.activation(out=sq[:, 0:1024], in_=psf, func=AF.Square), r=[ps.d], w=[sq.d])
        x.op("dve", lambda e: e.tensor_reduce(out=ssq[:, 0:ng2], in_=sq[:, 0:1024].rearrange("p (g d) -> p g d", d=gdim),
                                              axis=AX.X, op=ALU.add), r=[sq.d], w=[ssq.d])
        rsqrt_mean(x, c, rs, ssq, gdim, ng2)
        x.op("dve", lambda e: e.tensor_tensor(out=qv, in0=pv, in1=bc(rs[:, 0:ng2].unsqueeze(2), [128, ng2, gdim]),
                                              op=ALU.mult), r=[ps.d, rs.d], w=[qn.d])
        x.op("pool", lambda e: e.tensor_tensor(out=qv, in0=qv, in1=bc(gain[:, 0:gdim].unsqueeze(1), [128, ng2, gdim]),
                                               op=ALU.mult), r=[qn.d, gain.d], w=[qn.d])
    else:
        x.op("act", lambda e: e.activation(out=qn[:, 0:1024], in_=psf, func=AF.Copy), r=[ps.d], w=[qn.d])
    q4 = qn[:, 0:1024].rearrange("p (a g d) -> p a g d", a=2, d=gdim)
    x1 = q4[:, :, :, 0:half]
    x2 = q4[:, :, :, half:2 * half]
    cb = bc(cos[:, t0:t0 + 2, :].unsqueeze(2), [128, 2, ng, half])
    sb_ = bc(sin[:, t0:t0 + 2, :].unsqueeze(2), [128, 2, ng, half])
    tv = [tt[:, 0:ng2 * half].rearrange("p (a g h) -> p a g h", a=2, h=half) for tt in (t1, t2, t3, t4)]
    x.op("dve", lambda e: e.tensor_tensor(out=tv[0], in0=x1, in1=cb, op=ALU.mult), r=[qn.d, cos.d], w=[t1.d])
    x.op("dve", lambda e: e.tensor_tensor(out=tv[1], in0=x2, in1=sb_, op=ALU.mult), r=[qn.d, sin.d], w=[t2.d])
    x.op("pool", lambda e: e.tensor_tensor(out=tv[2], in0=x2, in1=cb, op=ALU.mult), r=[qn.d, cos.d], w=[t3.d])
    x.op("pool", lambda e: e.tensor_tensor(out=tv[3], in0=x1, in1=sb_, op=ALU.mult), r=[qn.d, sin.d], w=[t4.d])
    x.op("dve", lambda e: e.tensor_tensor(out=x1, in0=tv[0], in1=tv[1], op=ALU.subtract),
         r=[t1.d, t2.d], w=[qn.d])
    x.op("pool", lambda e: e.tensor_tensor(out=x2, in0=tv[2], in1=tv[3], op=ALU.add),
         r=[t3.d, t4.d, qn.d], w=[qn.d])
    x.op("act", lambda e: e.activation(out=out_bf[:, 0:1024], in_=qn[:, 0:1024], func=AF.Copy),
         r=[qn.d], w=[out_bf.d])


def alloc_qk_tiles(x, ls):
    qn = sbt(x, [128, 1024], F32, "qn", ls)
    sq = sbt(x, [128, 1024], F32, "sq", ls)
    ssq = sbt(x, [128, 16], F32, "ssq", ls)
    rs = sbt(x, [128, 16], F32, "rs", ls)
    ts = [sbt(x, [128, 128], F32, "rt%d" % i, ls) for i in range(4)]
    return (qn, sq, ssq, rs, ts[0], ts[1], ts[2], ts[3])


class ProjCtx:
    pass


def emit_proj(x, c, w_ap, hT, hT_d, ntiles, blocks):
    with Scope(x) as ls:
        wb = [sbt(x, [128, 16, 512], BF16, "wb%d" % i, ls) for i in range(2)]
        pp = [pst(x, [128, 512], F32, "pp%d" % i, ls) for i in range(2)]
        pp2 = ([pst(x, [128, 2, 512], F32, "pp2_%d" % i, ls) for i in range(2)]
               if any(b_[2] == "tok2" for b_ in blocks) else None)
        n = 0
        for bi, (col0, ncols, mode, handler) in enumerate(blocks):
            wt = wb[bi % 2]
            x.dma("pool", wt[:, :, 0:ncols], w_ap[:, col0:col0 + ncols].rearrange("(k p) f -> p k f", p=128),
                  w=[wt.d])
            if mode == "tok":
                for t in range(ntiles):
                    ps = pp[n % 2]
                    n += 1
                    for kc in range(16):
                        x.op("pe", lambda e: e.matmul(ps[:, 0:ncols], hT[:, kc, t * 128:(t + 1) * 128],
                                                      wt[:, kc, 0:ncols], start=(kc == 0), stop=(kc == 15)),
                             r=[hT_d[t], wt.d], w=[ps.d], inc=(kc == 15))
                    handler(t, ps)
            elif mode == "tok2":
                for t in range(0, ntiles, 2):
                    ps = pp2[n % 2]
                    n += 1
                    for a_ in range(2):
                        for kc in range(16):
                            x.op("pe", lambda e: e.matmul(ps[:, a_, 0:ncols], hT[:, kc, (t + a_) * 128:(t + a_ + 1) * 128],
                                                          wt[:, kc, 0:ncols], start=(kc == 0), stop=(kc == 15)),
                                 r=[hT_d[t + a_], wt.d], w=[ps.d], inc=(kc == 15))
                    handler(t, ps)
            else:
                for fc in range(ncols // 128):
                    for tb in range(ntiles // 4):
                        ps = pp[n % 2]
                        n += 1
                        for kc in range(16):
                            x.op("pe", lambda e: e.matmul(ps[:, :], wt[:, kc, fc * 128:(fc + 1) * 128],
                                                          hT[:, kc, tb * 512:(tb + 1) * 512],
                                                          start=(kc == 0), stop=(kc == 15)),
                                 r=hT_d[tb * 4:tb * 4 + 4] + [wt.d], w=[ps.d], inc=(kc == 15))
                        handler(fc, tb, ps)


def emit_LA(x, c, dram, kind, dm=False):
    x_in = dram("x_in", [TOK, D], F32, "ExternalInput")
    c_in = dram("c_in", [D], F32, "ExternalInput")
    pos = dram("pos", [TOK], I32, "ExternalInput")
    adaw = dram("ada_w", [D, 6 * D], F32, "ExternalInput")
    adab = dram("ada_b", [6 * D], F32, "ExternalInput")
    g1 = dram("g1", [D], F32, "ExternalInput")
    ada = dram("ada", [6 * D], F32, "ExternalOutput")
    FIN = {0: 6144, 1: 10304, 2: 4176}[kind]
    w_in = dram("w_in", [D, FIN], F32, "ExternalInput")
    outs = []
    with Scope(x) as st:
        d_ada = x.mkdep("ada")
        d_x = x.mkdep("x")
        emit_ada(x, c_in, adaw, adab, ada, d_ada)
        hT = x.sb([128, 16, TOK], BF16, "hT")
        hT_d = [x.mkdep("hT%d" % t) for t in range(NT)]
        with Scope(x) as ls:
            sT, shT = emit_mod_cols(x, ls, g1, ada, d_ada, 1, 0)
            emit_norm_hT(x, c, x_in, d_x, NT, sT, shT, hT, hT_d)
        ls = st
        if kind == 0:
            invf = dram("invf", [128, 8], F32, "ExternalInput")
            gq = dram("gq", [64], F32, "ExternalInput")
            gk = dram("gk", [64], F32, "ExternalInput")
            qT = dram("qT", [16, 128, TOK], BF16, "ExternalOutput")
            kT = dram("kT", [16, 128, TOK], BF16, "ExternalOutput")
            v = dram("v", [2, TOK, 1024] if dm else [TOK, 2048], BF16, "ExternalOutput")
            outs = [x.mkdep("qT"), x.mkdep("kT"), x.mkdep("v")]
            cos, sin = emit_rope_tables(x, ls, pos, invf, 8, NT)
            gqb = load_bcast(x, ls, gq, 64, "gqb")
            gkb = load_bcast(x, ls, gk, 64, "gkb")
            qk_tiles = alloc_qk_tiles(x, ls)
            qbf = sbt(x, [128, 1024], BF16, "qbf", ls)
            stage = [sbt(x, [128, 4, TOK], BF16, "stage%d" % i, ls) for i in range(2)]
            ptr = [pst(x, [128, 512], BF16, "ptr%d" % i, ls) for i in range(2)]
            vst = [sbt(x, [128, 512], BF16, "vst%d" % i, ls) for i in range(2)]
            blocks = []
            cnt = [0]
            for which in range(2):
                for hb in range(4):
                    def handler(t, ps, which=which, hb=hb):
                        sg = stage[(which * 4 + hb) % 2]
                        emit_qk_post2(x, c, qk_tiles, ps, 64, gqb if which == 0 else gkb, 8, cos, sin, t, qbf)
                        for a_ in range(2):
                            p = ptr[cnt[0] % 2]
                            cnt[0] += 1
                            for j in range(4):
                                x.op("pe", lambda e: e.transpose(p[:, j * 128:(j + 1) * 128],
                                                                 qbf[:, a_ * 512 + j * 128:a_ * 512 + (j + 1) * 128], c.ident[:]),
                                     r=[qbf.d, c.ident.d], w=[p.d], inc=(j == 3))
                            x.op("dve", lambda e: e.tensor_copy(sg[:, :, (t + a_) * 128:(t + a_ + 1) * 128],
                                                                p[:, :].rearrange("p (a b) -> p a b", a=4)),
                                 r=[p.d], mw=[sg.d])
                        if t == NT - 2:
                            dst = (qT if which == 0 else kT)[hb * 4:(hb + 1) * 4, :, :].rearrange("h p t -> p h t")
                            x.dma("sp", dst, sg[:], r=[sg.d], mw=[outs[which]])
                    blocks.append((which * 2048 + hb * 512, 512, "tok2", handler))
            for vb in range(4):
                def vhandler(t, ps, vb=vb):
                    vs = vst[cnt[0] % 2]
                    cnt[0] += 1
                    x.op("act", lambda e: e.activation(out=vs[:], in_=ps[:, :], func=AF.Copy), r=[ps.d], w=[vs.d])
                    vdst = (v[vb // 2, t * 128:(t + 1) * 128, (vb % 2) * 512:(vb % 2 + 1) * 512] if dm
                            else v[t * 128:(t + 1) * 128, vb * 512:(vb + 1) * 512])
                    x.dma("sp", vdst, vs[:], r=[vs.d], mw=[outs[2]])
                blocks.append((4096 + vb * 512, 512, "tok", vhandler))
            emit_proj(x, c, w_in, hT, hT_d, NT, blocks)
        elif kind == 1:
            blocks = la_handlers_ssd(x, c, ls, dram, outs, dm)
            emit_proj(x, c, w_in, hT, hT_d, NT, blocks)
        else:
            blocks = la_handlers_dsa(x, c, ls, dram, outs, pos, dm)
            emit_proj(x, c, w_in, hT, hT_d, NT, blocks)


def emit_LB_DA(x, c, dram, lambda_init):
    NH = 8
    qT = dram("qT", [NH, 128, S], BF16, "ExternalInput")
    kT = dram("kT", [NH, 128, S], BF16, "ExternalInput")
    v = dram("v", [S, NH * 128], BF16, "ExternalInput")
    lam4 = dram("lam4", [4, 64], F32, "ExternalInput")
    gqk = dram("gqk", [2, 64], F32, "ExternalInput")
    subg = dram("subg", [128], F32, "ExternalInput")
    o = dram("o", [S, NH * 128], BF16, "ExternalOutput")
    with Scope(x) as st:
        d_o = x.mkdep("o")
        ls = st
        lt = [load_bcast(x, ls, lam4[i], 64, "lam%d" % i) for i in range(4)]
        gt = [load_bcast(x, ls, gqk[i], 64, "gqk%d" % i) for i in range(2)]
        gs = load_bcast(x, ls, subg, 128, "gs")
        x.op("dve", lambda e: e.tensor_scalar(out=gs[:], in0=gs[:], scalar1=1.0 - lambda_init, scalar2=None,
                                              op0=ALU.mult), r=[gs.d], w=[gs.d])
        pr = sbt(x, [128, 64], F32, "pr")
        s12 = sbt(x, [128, 2], F32, "s12")
        e12 = sbt(x, [128, 2], F32, "e12")
        neglam = sbt(x, [128, 1], F32, "neglam")
        for i in range(2):
            x.op("dve", lambda e: e.tensor_tensor(out=pr[:], in0=lt[2 * i][:], in1=lt[2 * i + 1][:], op=ALU.mult),
                 r=[lt[2 * i].d, lt[2 * i + 1].d], w=[pr.d])
            x.op("dve", lambda e: e.tensor_reduce(out=s12[:, i:i + 1], in_=pr[:], axis=AX.X, op=ALU.add),
                 r=[pr.d], w=[s12.d])
        x.op("act", lambda e: e.activation(out=e12[:], in_=s12[:], func=AF.Exp), r=[s12.d], w=[e12.d])
        x.op("dve", lambda e: e.scalar_tensor_tensor(out=neglam[:], in0=e12[:, 1:2], scalar=-lambda_init,
                                                     in1=e12[:, 0:1], op0=ALU.add, op1=ALU.subtract),
             r=[e12.d], w=[neglam.d])
        gm = sbt(x, [128, 2], F32, "gm")
        negC = sbt(x, [128, 1], F32, "negC")
        for i in range(2):
            x.op("dve", lambda e: e.tensor_reduce(out=gm[:, i:i + 1], in_=gt[i][:], axis=AX.X, op=ALU.max,
                                                  apply_absolute_value=True), r=[gt[i].d], w=[gm.d])
        x.op("dve", lambda e: e.scalar_tensor_tensor(out=negC[:], in0=gm[:, 0:1], scalar=-8.0, in1=gm[:, 1:2],
                                                     op0=ALU.mult, op1=ALU.mult), r=[gm.d], w=[negC.d])
        kt = [sbt(x, [128, S], BF16, "kt%d" % i) for i in range(2)]
        qt = [sbt(x, [128, S], BF16, "qt%d" % i) for i in range(2)]
        va = [sbt(x, [128, 32, 129], BF16, "va%d" % i) for i in range(2)]
        for i in range(2):
            x.op("pool", lambda e: e.memset(va[i][:, :, 128:129], 1.0), w=[va[i].d])
        pss = [pst(x, [128, 512], F32, "pss%d" % i) for i in range(4)]
        pso = pst(x, [128, 8, 256], F32, "pso")
        pt = [sbt(x, [128, 512], BF16, "pt%d" % i) for i in range(4)]
        R = sbt(x, [128, 8], F32, "R")
        tmp = [sbt(x, [128, 128], F32, "tmp%d" % i) for i in range(2)]
        of = sbt(x, [128, 4, 128], F32, "of")
        sq = sbt(x, [128, 512], F32, "sqo")
        ssq = sbt(x, [128, 4], F32, "ssqo")
        rs = sbt(x, [128, 4], F32, "rso")
        ob = [sbt(x, [128, 4, 128], BF16, "ob%d" % i) for i in range(2)]
        zr = sbt(x, [128, 512], BF16, "zr")
        x.op("pool", lambda e: e.memset(zr[:], 0.0), w=[zr.d])
        psob = pso[:].rearrange("p a b -> p (a b)")
        n = 0
        nq = 0
        for h in range(NH):
            hi = h % 2
            x.dma("sp", kt[hi][:], kT[h], w=[kt[hi].d])
            x.dma("sp", qt[hi][:], qT[h], w=[qt[hi].d])
            x.dma("sp", va[hi][:, :, 0:128], v[:, h * 128:(h + 1) * 128].rearrange("(kb p) d -> p kb d", p=128),
                  mw=[va[hi].d])
            steps = [(qc, kb, comp) for qc in range(8) for kb in range(4 * qc + 4) for comp in range(2)]

            def scores(idx):
                qc, kb, comp = steps[idx]
                jmin = max(0, kb - 4 * qc)
                diag = kb >= 4 * qc
                N = 512 - 128 * jmin
                q0 = qc * 512 + 128 * jmin
                ps = pss[idx % 4]
                ptt = pt[idx % 4]
                pr_ = slice(comp * 64, (comp + 1) * 64)
                lhs = kt[hi][pr_, kb * 128:(kb + 1) * 128]
                if diag:
                    x.op("pe", lambda e: e.matmul(ps[:, 0:128], lhs, qt[hi][pr_, q0:q0 + 128],
                                                  start=True, stop=False),
                         r=[kt[hi].d, qt[hi].d], w=[ps.d], inc=False)
                    x.op("pe", lambda e: e.matmul(ps[:, 0:128], c.ident[:], c.negT[:],
                                                  start=False, stop=True),
                         r=[c.ident.d, c.negT.d], w=[ps.d], inc=(N == 128))
                    if N > 128:
                        x.op("pe", lambda e: e.matmul(ps[:, 128:N], lhs, qt[hi][pr_, q0 + 128:q0 + N],
                                                      start=True, stop=True),
                             r=[kt[hi].d, qt[hi].d], w=[ps.d], inc=True)
                else:
                    x.op("pe", lambda e: e.matmul(ps[:, 0:N], lhs, qt[hi][pr_, q0:q0 + N],
                                                  start=True, stop=True),
                         r=[kt[hi].d, qt[hi].d], w=[ps.d], inc=True)
                x.op("act", lambda e: e.activation(out=ptt[:, 0:N], in_=ps[:, 0:N], func=AF.Exp,
                                                   scale=0.125, bias=negC[:, 0:1]),
                     r=[ps.d, negC.d], w=[ptt.d])

            def pv(idx):
                qc, kb, comp = steps[idx]
                jmin = max(0, kb - 4 * qc)
                ptt = pt[idx % 4]
                for j in range(jmin, 4):
                    last = (kb == 4 * qc + j)
                    x.op("pe", lambda e: e.matmul(pso[:, comp * 4 + j, 0:129],
                                                  ptt[:, (j - jmin) * 128:(j - jmin + 1) * 128],
                                                  va[hi][:, kb, :], start=False,
                                                  stop=(last and j % 2 == 1)),
                         r=[ptt.d, va[hi].d], w=[pso.d], inc=(j == 3))

            def epilogue(qc):
                nonlocal nq
                x.op("dve", lambda e: e.reciprocal(out=R[:], in_=pso[:, :, 128:129].rearrange("p a b -> p (a b)")),
                     r=[pso.d], w=[R.d])
                x.op("dve", lambda e: e.tensor_scalar(out=R[:, 4:8], in0=R[:, 4:8], scalar1=neglam[:, 0:1],
                                                      scalar2=None, op0=ALU.mult), r=[R.d, neglam.d], w=[R.d])
                for j in range(4):
                    tm = tmp[j % 2]
                    x.op("act", lambda e: e.activation(out=tm[:], in_=pso[:, 4 + j, 0:128], func=AF.Copy,
                                                       scale=R[:, 4 + j:5 + j]), r=[pso.d, R.d], w=[tm.d])
                    x.op("dve", lambda e: e.scalar_tensor_tensor(out=of[:, j, :], in0=pso[:, j, 0:128],
                                                                 scalar=R[:, j:j + 1], in1=tm[:],
                                                                 op0=ALU.mult, op1=ALU.add),
                         r=[pso.d, R.d, tm.d], mw=[of.d])
                ofl = of[:].rearrange("p a b -> p (a b)")
                x.op("act", lambda e: e.activation(out=sq[:], in_=ofl, func=AF.Square), r=[of.d], w=[sq.d])
                x.op("dve", lambda e: e.tensor_reduce(out=ssq[:], in_=sq[:].rearrange("p (a b) -> p a b", a=4),
                                                      axis=AX.X, op=ALU.add), r=[sq.d], w=[ssq.d])
                rsqrt_mean(x, c, rs, ssq, 128, 4)
                x.op("dve", lambda e: e.tensor_tensor(out=of[:], in0=of[:], in1=bc(rs[:].unsqueeze(2), [128, 4, 128]),
                                                      op=ALU.mult), r=[of.d, rs.d], w=[of.d])
                obb = ob[nq % 2]
                nq += 1
                x.op("pool", lambda e: e.tensor_tensor(out=obb[:], in0=of[:], in1=bc(gs[:].unsqueeze(1), [128, 4, 128]),
                                                       op=ALU.mult), r=[of.d, gs.d], w=[obb.d])
                x.dma("sp", o[qc * 512:(qc + 1) * 512, h * 128:(h + 1) * 128].rearrange("(j p) d -> p j d", p=128),
                      obb[:], r=[obb.d], mw=[d_o])

            scores(0)
            scores(1)
            for idx, (qc, kb, comp) in enumerate(steps):
                if kb == 0 and comp == 0:
                    for bnk in range(4):
                        x.op("pe", lambda e: e.matmul(psob[:, bnk * 512:(bnk + 1) * 512], zr[:, 0:128], zr[:],
                                                      start=True, stop=False), r=[zr.d], w=[pso.d], inc=False)
                if idx + 2 < len(steps):
                    scores(idx + 2)
                pv(idx)
                if kb == 4 * qc + 3 and comp == 1:
                    epilogue(qc)


def emit_outproj(x, c, o_ap, d_o, FO, wout_ap, x_ap, d_x, gate_b, xmid_ap, d_xmid):
    KO = FO // 128
    TB = 1024
    with Scope(x) as ls:
        oT = sbt(x, [128, KO, TB], BF16, "oT", ls)
        oT_d = [x.mkdep("oT%d" % t) for t in range(TB // 128)]
        ls.deps.extend(oT_d)
        ot = [sbt(x, [128, FO], BF16, "ot%d" % i, ls) for i in range(2)]
        pt = [pst(x, [128, 512], BF16, "pto%d" % i, ls) for i in range(2)]
        wb = [sbt(x, [128, KO, 512], BF16, "wo%d" % i, ls) for i in range(2)]
        pp = [pst(x, [128, 512], F32, "ppo%d" % i, ls) for i in range(2)]
        tm = [sbt(x, [128, 512], F32, "tmo%d" % i, ls) for i in range(2)]
        xt = [sbt(x, [128, 512], F32, "xto%d" % i, ls) for i in range(2)]
        n = 0
        nw = 0
        for tb in range(TOK // TB):
            for t in range(TB // 128):
                i = t % 2
                tok0 = tb * TB + t * 128
                x.dma("sp", ot[i][:], o_ap[tok0:tok0 + 128, :], r=[d_o], w=[ot[i].d])
                for g in range(KO // 4):
                    p = pt[g % 2]
                    for j in range(4):
                        kc = g * 4 + j
                        x.op("pe", lambda e: e.transpose(p[:, j * 128:(j + 1) * 128], ot[i][:, kc * 128:(kc + 1) * 128],
                                                         c.ident[:]), r=[ot[i].d, c.ident.d], w=[p.d], inc=(j == 3))
                    dst = oT[:, g * 4:(g + 1) * 4, t * 128:(t + 1) * 128]
                    src = p[:, :].rearrange("p (a b) -> p a b", a=4)
                    if g % 2 == 0:
                        x.op("dve", lambda e: e.tensor_copy(dst, src), r=[p.d], mw=[oT_d[t]])
                    else:
                        x.op("act", lambda e: e.activation(out=dst, in_=src, func=AF.Copy), r=[p.d], mw=[oT_d[t]])
            for cb in range(4):
                wt = wb[nw % 2]
                nw += 1
                x.dma("pool", wt[:], wout_ap[:, cb * 512:(cb + 1) * 512].rearrange("(k p) f -> p k f", p=128), w=[wt.d])
                for t in range(TB // 128):
                    tok0 = tb * TB + t * 128
                    ps = pp[n % 2]
                    tmm = tm[n % 2]
                    xtt = xt[n % 2]
                    n += 1
                    x.dma("sp", xtt[:], x_ap[tok0:tok0 + 128, cb * 512:(cb + 1) * 512], r=[d_x], w=[xtt.d])
                    for kc in range(KO):
                        x.op("pe", lambda e: e.matmul(ps[:], oT[:, kc, t * 128:(t + 1) * 128], wt[:, kc, :],
                                                      start=(kc == 0), stop=(kc == KO - 1)),
                             r=[oT_d[t], wt.d], w=[ps.d], inc=(kc == KO - 1))
                    x.op("dve", lambda e: e.tensor_tensor(out=tmm[:], in0=ps[:], in1=gate_b[:, cb * 512:(cb + 1) * 512],
                                                          op=ALU.mult), r=[ps.d, gate_b.d], w=[tmm.d])
                    x.op("pool", lambda e: e.tensor_tensor(out=tmm[:], in0=tmm[:], in1=xtt[:], op=ALU.add),
                         r=[tmm.d, xtt.d], w=[tmm.d])
                    x.dma("act", xmid_ap[tok0:tok0 + 128, cb * 512:(cb + 1) * 512], tmm[:], r=[tmm.d], mw=[d_xmid])


def emit_ffn(x, c, xmid_ap, d_xmid, sT, shT, gate_b, wgu_ap, wd_ap, xout_ap, d_xout, act_ap):
    NFC = FFN // 128
    d_act = x.mkdep("act_d")
    with Scope(x) as ls:
        h2T = sbt(x, [128, 16, TOK], BF16, "h2T", ls)
        h2_d = [x.mkdep("h2T%d" % t) for t in range(NT)]
        ls.deps.extend(h2_d)
        emit_norm_hT(x, c, xmid_ap, d_xmid, NT, sT, shT, h2T, h2_d)
        wg = [sbt(x, [128, 16, 256], BF16, "wg%d" % i, ls) for i in range(2)]
        wu = [sbt(x, [128, 16, 256], BF16, "wu%d" % i, ls) for i in range(2)]
        psg = [pst(x, [128, 512], F32, "psg%d" % i, ls) for i in range(2)]
        psu = [pst(x, [128, 512], F32, "psu%d" % i, ls) for i in range(2)]
        sg = [sbt(x, [128, 512], F32, "sg%d" % i, ls) for i in range(2)]
        ab = [sbt(x, [128, 4, 128], BF16, "ab%d" % i, ls) for i in range(3)]
        n = 0
        for blk in range(FFN // 256):
            wgt = wg[blk % 2]
            wut = wu[blk % 2]
            x.dma("pool", wgt[:], wgu_ap[:, blk * 256:(blk + 1) * 256].rearrange("(k p) f -> p k f", p=128), w=[wgt.d])
            x.dma("pool", wut[:], wgu_ap[:, FFN + blk * 256:FFN + (blk + 1) * 256].rearrange("(k p) f -> p k f", p=128),
                  w=[wut.d])
            for fl in range(2):
                fc = blk * 2 + fl
                for tb in range(TOK // 512):
                    pg = psg[n % 2]
                    pu = psu[n % 2]
                    sgg = sg[n % 2]
                    abb = ab[n % 3]
                    n += 1
                    hd = h2_d[tb * 4:tb * 4 + 4]
                    for kc in range(16):
                        x.op("pe", lambda e: e.matmul(pg[:], wgt[:, kc, fl * 128:(fl + 1) * 128],
                                                      h2T[:, kc, tb * 512:(tb + 1) * 512],
                                                      start=(kc == 0), stop=(kc == 15)),
                             r=hd + [wgt.d], w=[pg.d], inc=(kc == 15))
                    for kc in range(16):
                        x.op("pe", lambda e: e.matmul(pu[:], wut[:, kc, fl * 128:(fl + 1) * 128],
                                                      h2T[:, kc, tb * 512:(tb + 1) * 512],
                                                      start=(kc == 0), stop=(kc == 15)),
                             r=hd + [wut.d], w=[pu.d], inc=(kc == 15))
                    x.op("act", lambda e: e.activation(out=sgg[:], in_=pg[:], func=AF.Silu), r=[pg.d], w=[sgg.d])
                    x.op("dve", lambda e: e.tensor_tensor(out=abb[:].rearrange("p a b -> p (a b)"), in0=pu[:], in1=sgg[:],
                                                          op=ALU.mult), r=[pu.d, sgg.d], w=[abb.d])
                    x.dma("sp", act_ap[tb * 4:(tb + 1) * 4, :, fc, :].rearrange("t p k -> p t k"), abb[:],
                          r=[abb.d], mw=[d_act])
    with Scope(x) as ls:
        wdA = sbt(x, [128, 22, 512], BF16, "wdA", ls)
        wdB = sbt(x, [128, 22, 512], BF16, "wdB", ls)
        at = [sbt(x, [128, NFC, 128], BF16, "at%d" % i, ls) for i in range(3)]
        psd = [pst(x, [128, 512], F32, "psd%d" % i, ls) for i in range(2)]
        tm = [sbt(x, [128, 512], F32, "tmf%d" % i, ls) for i in range(2)]
        xt = [sbt(x, [128, 512], F32, "xtf%d" % i, ls) for i in range(2)]
        nd = 0
        for cb in range(4):
            x.dma("pool", wdA[:], wd_ap[0:22 * 128, cb * 512:(cb + 1) * 512].rearrange("(k p) f -> p k f", p=128),
                  w=[wdA.d])
            x.dma("pool", wdB[:], wd_ap[22 * 128:44 * 128, cb * 512:(cb + 1) * 512].rearrange("(k p) f -> p k f", p=128),
                  w=[wdB.d])
            for t in range(NT):
                tok0 = t * 128
                ps = psd[nd % 2]
                tmm = tm[nd % 2]
                xtt = xt[nd % 2]
                att = at[nd % 3]
                if nd == 0:
                    x.dma("sp", att[:], act_ap[0], r=[d_act], w=[att.d])
                if nd + 1 < 4 * NT:
                    x.dma("sp", at[(nd + 1) % 3][:], act_ap[(t + 1) % NT], r=[d_act], w=[at[(nd + 1) % 3].d])
                nd += 1
                x.dma("sp", xtt[:], xmid_ap[tok0:tok0 + 128, cb * 512:(cb + 1) * 512], r=[d_xmid], w=[xtt.d])
                for fc in range(NFC):
                    wt = wdA if fc < 22 else wdB
                    x.op("pe", lambda e: e.matmul(ps[:], att[:, fc, :], wt[:, fc % 22, :],
                                                  start=(fc == 0), stop=(fc == NFC - 1)),
                         r=[att.d, wt.d], w=[ps.d], inc=(fc == NFC - 1 or fc == 21))
                x.op("dve", lambda e: e.tensor_tensor(out=tmm[:], in0=ps[:], in1=gate_b[:, cb * 512:(cb + 1) * 512],
                                                      op=ALU.mult), r=[ps.d, gate_b.d], w=[tmm.d])
                x.op("pool", lambda e: e.tensor_tensor(out=tmm[:], in0=tmm[:], in1=xtt[:], op=ALU.add),
                     r=[tmm.d, xtt.d], w=[tmm.d])
                x.dma("act", xout_ap[tok0:tok0 + 128, cb * 512:(cb + 1) * 512], tmm[:], r=[tmm.d], mw=[d_xout])


def emit_LC(x, c, dram, FO):
    x_in = dram("x_in", [TOK, D], F32, "ExternalInput")
    o_in = dram("o_in", [TOK, FO], BF16, "ExternalInput")
    ada = dram("ada", [6 * D], F32, "ExternalInput")
    g2 = dram("g2", [D], F32, "ExternalInput")
    w_out = dram("w_out", [FO, D], F32, "ExternalInput")
    wgu = dram("wgu", [D, 2 * FFN], F32, "ExternalInput")
    wd = dram("wd", [FFN, D], F32, "ExternalInput")
    x_mid = dram("x_mid", [TOK, D], F32, "Internal")
    act_d = dram("act_d", [NT, 128, FFN // 128, 128], BF16, "Internal")
    x_out = dram("x_out", [TOK, D], F32, "ExternalOutput")
    with Scope(x) as st:
        d_none = x.mkdep("in")
        d_xmid = x.mkdep("xmid")
        d_xout = x.mkdep("xout")
        with Scope(x) as ls:
            g1b = load_bcast(x, ls, ada[2 * D:3 * D], D, "g1b")
            emit_outproj(x, c, o_in, d_none, FO, w_out, x_in, d_none, g1b, x_mid, d_xmid)
        with Scope(x) as ls:
            g2b = load_bcast(x, ls, ada[5 * D:6 * D], D, "g2b")
            sT, shT = emit_mod_cols(x, ls, g2, ada, d_none, 4, 3)
            emit_ffn(x, c, x_mid, d_xmid, sT, shT, g2b, wgu, wd, x_out, d_xout, act_d)


def la_handlers_ssd(x, c, ls, dram, outs_holder, dm=False):
    z = dram("z", [2, TOK, 2048] if dm else [TOK, 4096], BF16, "ExternalOutput")
    xbcT = dram("xbcT", [2, 3072, TOK] if dm else [6144, TOK], BF16, "ExternalOutput")
    dtr = dram("dtr", [2, TOK, 32] if dm else [TOK, 64], F32, "ExternalOutput")
    outs = [x.mkdep("z"), x.mkdep("xbcT"), x.mkdep("dtr")]
    outs_holder.extend(outs)
    zst = [sbt(x, [128, 512], BF16, "zst%d" % i, ls) for i in range(2)]
    fst = [sbt(x, [128, 512], BF16, "fst%d" % i, ls) for i in range(2)]
    dst_ = [sbt(x, [128, 64], F32, "dst%d" % i, ls) for i in range(2)]
    cnt = [0]
    blocks = []
    for zb in range(8):
        def zh(t, ps, zb=zb):
            s_ = zst[cnt[0] % 2]
            cnt[0] += 1
            x.op("act", lambda e: e.activation(out=s_[:], in_=ps[:, :], func=AF.Copy), r=[ps.d], w=[s_.d])
            zdst = (z[zb // 4, t * 128:(t + 1) * 128, (zb % 4) * 512:(zb % 4 + 1) * 512] if dm
                    else z[t * 128:(t + 1) * 128, zb * 512:(zb + 1) * 512])
            x.dma("sp", zdst, s_[:], r=[s_.d], mw=[outs[0]])
        blocks.append((zb * 512, 512, "tok", zh))
    for xb in range(12):
        def xh(fc, tb, ps, xb=xb):
            s_ = fst[cnt[0] % 2]
            cnt[0] += 1
            x.op("act", lambda e: e.activation(out=s_[:], in_=ps[:, :], func=AF.Copy), r=[ps.d], w=[s_.d])
            if dm:
                dd, rb = ((xb // 4, (xb % 4) * 512) if xb < 8 else ((xb - 8) % 2, 2048 + ((xb - 8) // 2) * 512))
                xdst = xbcT[dd, rb + fc * 128:rb + fc * 128 + 128, tb * 512:(tb + 1) * 512]
            else:
                r0 = xb * 512 + fc * 128
                xdst = xbcT[r0:r0 + 128, tb * 512:(tb + 1) * 512]
            x.dma("sp", xdst, s_[:], r=[s_.d], mw=[outs[1]])
        blocks.append((4096 + xb * 512, 512, "feat", xh))

    def dh(t, ps):
        s_ = dst_[cnt[0] % 2]
        cnt[0] += 1
        x.op("act", lambda e: e.activation(out=s_[:], in_=ps[:, 0:64], func=AF.Copy), r=[ps.d], w=[s_.d])
        if dm:
            for dd in range(2):
                x.dma("sp", dtr[dd, t * 128:(t + 1) * 128, :], s_[:, dd * 32:(dd + 1) * 32], r=[s_.d], mw=[outs[2]])
        else:
            x.dma("sp", dtr[t * 128:(t + 1) * 128, :], s_[:], r=[s_.d], mw=[outs[2]])
    blocks.append((10240, 64, "tok", dh))
    return blocks


def emit_LB_SSD(x, c, dram):
    NHh = 32
    NCH = 24
    raw = dram("raw", [NCH * 128, S], F32, "ExternalInput")
    convw = dram("convw", [4, NCH * 128], F32, "ExternalInput")
    convb = dram("convb", [NCH * 128], F32, "ExternalInput")
    dtr = dram("dtr", [S, NHh], F32, "ExternalInput")
    hp = dram("hp", [3, NHh], F32, "ExternalInput")
    z = dram("z", [S, 2048], BF16, "ExternalInput")
    ng = dram("ng", [2048], F32, "ExternalInput")
    tokd = dram("tokd", [S, 2560], BF16, "Internal")
    featd = dram("featd", [1024, S], BF16, "Internal")
    y = dram("y", [S, 2048], BF16, "ExternalOutput")
    with Scope(x) as st:
        d_in = x.mkdep("in")
        d_tok = x.mkdep("tokd")
        d_feat = x.mkdep("featd")
        d_y = x.mkdep("y")
        with Scope(x) as ls:
            cw = sbt(x, [128, 4, NCH], F32, "cw", ls)
            cb_ = sbt(x, [128, NCH], F32, "cb", ls)
            for j in range(4):
                x.dma("sp", cw[:, j, :], convw[j].rearrange("(k p) -> p k", p=128), mw=[cw.d],
                      allow_slow_non_contiguous=True)
            x.dma("sp", cb_[:], convb.rearrange("(k p) -> p k", p=128), w=[cb_.d], allow_slow_non_contiguous=True)
            rw = [sbt(x, [128, S + 3], F32, "rw%d" % i, ls) for i in range(2)]
            for i in range(2):
                x.op("pool", lambda e: e.memset(rw[i][:, 0:3], 0.0), w=[rw[i].d])
            acc = sbt(x, [128, S], F32, "acc", ls)
            sil = [sbt(x, [128, S], BF16, "sil%d" % i, ls) for i in range(2)]
            ptc = [pst(x, [128, 512], BF16, "ptc%d" % i, ls) for i in range(2)]
            stg = [sbt(x, [128, 4, 128], BF16, "stg%d" % i, ls) for i in range(2)]
            n = 0
            for cc in range(NCH):
                r_ = rw[cc % 2]
                sl_ = sil[cc % 2]
                x.dma("sp", r_[:, 3:3 + S], raw[cc * 128:(cc + 1) * 128, :], r=[d_in], mw=[r_.d])
                x.op("dve", lambda e: e.tensor_scalar(out=acc[:], in0=r_[:, 3:3 + S], scalar1=cw[:, 3, cc:cc + 1],
                                                      scalar2=cb_[:, cc:cc + 1], op0=ALU.mult, op1=ALU.add),
                     r=[r_.d, cw.d, cb_.d], w=[acc.d])
                for j in range(3):
                    x.op("dve", lambda e: e.scalar_tensor_tensor(out=acc[:], in0=r_[:, j:j + S],
                                                                 scalar=cw[:, j, cc:cc + 1], in1=acc[:],
                                                                 op0=ALU.mult, op1=ALU.add),
                         r=[r_.d, cw.d, acc.d], w=[acc.d])
                x.op("act", lambda e: e.activation(out=sl_[:], in_=acc[:], func=AF.Silu), r=[acc.d], w=[sl_.d])
                if cc >= 16:
                    x.dma("sp", featd[(cc - 16) * 128:(cc - 15) * 128, :], sl_[:], r=[sl_.d], mw=[d_feat])
                if cc < 20:
                    for tg in range(8):
                        p = ptc[n % 2]
                        sg_ = stg[n % 2]
                        n += 1
                        for j in range(4):
                            tt = tg * 4 + j
                            x.op("pe", lambda e: e.transpose(p[:, j * 128:(j + 1) * 128], sl_[:, tt * 128:(tt + 1) * 128],
                                                             c.ident[:]), r=[sl_.d, c.ident.d], w=[p.d], inc=(j == 3))
                        if n % 2 == 0:
                            x.op("dve", lambda e: e.tensor_copy(sg_[:], p[:, :].rearrange("p (a b) -> p a b", a=4)),
                                 r=[p.d], w=[sg_.d])
                        else:
                            x.op("act", lambda e: e.activation(out=sg_[:], in_=p[:, :].rearrange("p (a b) -> p a b", a=4),
                                                               func=AF.Copy), r=[p.d], w=[sg_.d])
                        x.dma("sp", tokd[tg * 512:(tg + 1) * 512, cc * 128:(cc + 1) * 128].rearrange("(j p) c -> p j c", p=128),
                              sg_[:], r=[sg_.d], mw=[d_tok])
        with Scope(x) as ls:
            hb = [load_bcast(x, ls, hp[i], NHh, "hp%d" % i) for i in range(3)]
            dtb_b, alog_b, dsk_b = hb
            a_b = sbt(x, [128, NHh], F32, "a_b", ls)
            x.op("act", lambda e: e.activation(out=a_b[:], in_=alog_b[:], func=AF.Exp), r=[alog_b.d], w=[a_b.d])
            x.op("dve", lambda e: e.tensor_scalar(out=a_b[:], in0=a_b[:], scalar1=-1.0, scalar2=None, op0=ALU.mult),
                 r=[a_b.d], w=[a_b.d])
            ngb = load_bcast(x, ls, ng, 2048, "ngb")
            sel = sbt(x, [32, NHh, 128], F32, "sel", ls)
            x.op("pool", lambda e: e.memset(sel[:], 1.0), w=[sel.d])
            x.op("pool", lambda e: e.affine_select(out=sel[:], in_=sel[:], pattern=[[-1, NHh], [0, 128]],
                                                   compare_op=ALU.is_equal, fill=0.0, base=0, channel_multiplier=1),
                 r=[sel.d], w=[sel.d])
            St = [sbt(x, [128, 512], F32, "St%d" % g, ls) for g in range(4)]
            Sb = [sbt(x, [128, 512], BF16, "Sb%d" % g, ls) for g in range(4)]
            for g in range(4):
                x.op("pool", lambda e: e.memset(St[g][:], 0.0), w=[St[g].d])
                x.op("pool", lambda e: e.memset(Sb[g][:], 0.0), w=[Sb[g].d])
            xs_t = [sbt(x, [128, 2560], BF16, "xs_t%d" % i, ls) for i in range(2)]
            bct = [sbt(x, [128, 8, 128], BF16, "bct%d" % i, ls) for i in range(2)]
            dtt = [sbt(x, [128, NHh], F32, "dtt%d" % i, ls) for i in range(2)]
            zt = [sbt(x, [128, 2048], BF16, "zt%d" % i, ls) for i in range(2)]
            f = lambda nm, w_: sbt(x, [128, w_], F32, nm, ls)
            dtb, ab, ee, dt_, dta, acum, nacum, alast, eac, wend, decay = [f(nm, NHh) for nm in
                ("dtb", "ab", "ee", "dt_", "dta", "acum", "nacum", "alast", "eac", "wend", "decay")]
            acT = sbt(x, [32, 128], F32, "acT", ls)
            xdt = sbt(x, [128, 2048], BF16, "xdt", ls)
            xde = sbt(x, [128, 2048], BF16, "xde", ls)
            cbm = [sbt(x, [128, 128], F32, "cbm%d" % g, ls) for g in range(4)]
            Eh = [sbt(x, [128, 128], F32, "Eh%d" % i, ls) for i in range(4)]
            Mh = [sbt(x, [128, 128], BF16, "Mh%d" % i, ls) for i in range(4)]
            yf = sbt(x, [128, 2048], F32, "yf", ls)
            t1 = sbt(x, [128, 2048], F32, "t1", ls)
            sqy = sbt(x, [128, 2048], F32, "sqy", ls)
            ssy = sbt(x, [128, 4], F32, "ssy", ls)
            rsy = sbt(x, [128, 4], F32, "rsy", ls)
            yb = [sbt(x, [128, 2048], BF16, "yb%d" % i, ls) for i in range(2)]
            p_small = pst(x, [128, 512], F32, "p_small", ls)
            p_cb = pst(x, [128, 512], F32, "p_cb", ls)
            p_G = [pst(x, [128, 512], F32, "p_G%d" % i, ls) for i in range(2)]
            p_y = [pst(x, [128, 512], F32, "p_y%d" % i, ls) for i in range(2)]
            p_i = pst(x, [128, 512], F32, "p_i", ls)
            p_s = pst(x, [128, 512], F32, "p_s", ls)
            for ck in range(S // 128):
                i = ck % 2
                t0 = ck * 128
                xt_ = xs_t[i]
                bc_ = bct[i]
                x.dma("sp", xt_[:], tokd[t0:t0 + 128, :], r=[d_tok], w=[xt_.d])
                x.dma("sp", bc_[:], featd[:, t0:t0 + 128].rearrange("(g p) t -> p g t", p=128), r=[d_feat], w=[bc_.d])
                x.dma("sp", dtt[i][:], dtr[t0:t0 + 128, :], r=[d_in], w=[dtt[i].d])
                x.dma("sp", zt[i][:], z[t0:t0 + 128, :], r=[d_in], w=[zt[i].d])
                x.op("dve", lambda e: e.tensor_tensor(out=dtb[:], in0=dtt[i][:], in1=dtb_b[:], op=ALU.add),
                     r=[dtt[i].d, dtb_b.d], w=[dtb.d])
                x.op("dve", lambda e: e.scalar_tensor_tensor(out=ab[:], in0=dtb[:], scalar=-1.0, in1=dtb[:],
                                                             op0=ALU.mult, op1=ALU.max), r=[dtb.d], w=[ab.d])
                x.op("act", lambda e: e.activation(out=ee[:], in_=ab[:], func=AF.Exp, scale=-1.0), r=[ab.d], w=[ee.d])
                x.op("act", lambda e: e.activation(out=ee[:], in_=ee[:], func=AF.Ln, bias=1.0), r=[ee.d], w=[ee.d])
                x.op("dve", lambda e: e.scalar_tensor_tensor(out=dt_[:], in0=dtb[:], scalar=0.0, in1=ee[:],
                                                             op0=ALU.max, op1=ALU.add), r=[dtb.d, ee.d], w=[dt_.d])
                x.op("dve", lambda e: e.tensor_tensor(out=dta[:], in0=dt_[:], in1=a_b[:], op=ALU.mult),
                     r=[dt_.d, a_b.d], w=[dta.d])
                x.op("pe", lambda e: e.matmul(p_small[:, 0:32], c.trif[:], dta[:], start=True, stop=True),
                     r=[c.trif.d, dta.d], w=[p_small.d])
                x.op("pe", lambda e: e.matmul(p_small[:, 32:64], c.onesf[:], dta[:], start=True, stop=True),
                     r=[c.onesf.d, dta.d], w=[p_small.d])
                x.op("dve", lambda e: e.tensor_copy(acum[:], p_small[:, 0:32]), r=[p_small.d], w=[acum.d])
                x.op("dve", lambda e: e.tensor_scalar(out=nacum[:], in0=p_small[:, 0:32], scalar1=-1.0, scalar2=None,
                                                      op0=ALU.mult), r=[p_small.d], w=[nacum.d])
                x.op("dve", lambda e: e.tensor_copy(alast[:], p_small[:, 32:64]), r=[p_small.d], w=[alast.d])
                x.op("act", lambda e: e.activation(out=eac[:], in_=acum[:], func=AF.Exp), r=[acum.d], w=[eac.d])
                x.op("act", lambda e: e.activation(out=decay[:], in_=alast[:], func=AF.Exp), r=[alast.d], w=[decay.d])
                x.op("dve", lambda e: e.tensor_tensor(out=wend[:], in0=alast[:], in1=acum[:], op=ALU.subtract),
                     r=[alast.d, acum.d], w=[wend.d])
                x.op("act", lambda e: e.activation(out=wend[:], in_=wend[:], func=AF.Exp), r=[wend.d], w=[wend.d])
                x.op("dve", lambda e: e.tensor_tensor(out=wend[:], in0=wend[:], in1=dt_[:], op=ALU.mult),
                     r=[wend.d, dt_.d], w=[wend.d])
                x.op("pe", lambda e: e.matmul(p_small[0:32, 128:256], acum[:], c.identf[:], start=True, stop=True),
                     r=[acum.d, c.identf.d], w=[p_small.d])
                x.op("dve", lambda e: e.tensor_copy(acT[:], p_small[0:32, 128:256]), r=[p_small.d], w=[acT.d])
                xv = xt_[:, 0:2048].rearrange("p (h d) -> p h d", d=64)
                x.op("dve", lambda e: e.tensor_tensor(out=xdt[:].rearrange("p (h d) -> p h d", d=64), in0=xv,
                                                      in1=bc(dt_[:].unsqueeze(2), [128, NHh, 64]), op=ALU.mult),
                     r=[xt_.d, dt_.d], w=[xdt.d])
                x.op("pool", lambda e: e.tensor_tensor(out=xde[:].rearrange("p (h d) -> p h d", d=64), in0=xv,
                                                       in1=bc(wend[:].unsqueeze(2), [128, NHh, 64]), op=ALU.mult),
                     r=[xt_.d, wend.d], w=[xde.d])
                for g in range(4):
                    x.op("pe", lambda e: e.matmul(p_cb[:, g * 128:(g + 1) * 128], bc_[:, g, :], bc_[:, 4 + g, :],
                                                  start=True, stop=True), r=[bc_.d], w=[p_cb.d], inc=(g == 3))
                for g in range(4):
                    if g % 2 == 0:
                        x.op("dve", lambda e: e.tensor_copy(cbm[g][:], p_cb[:, g * 128:(g + 1) * 128]),
                             r=[p_cb.d], w=[cbm[g].d])
                    else:
                        x.op("act", lambda e: e.activation(out=cbm[g][:], in_=p_cb[:, g * 128:(g + 1) * 128], func=AF.Copy),
                             r=[p_cb.d], w=[cbm[g].d])
                for g in range(4):
                    py = p_y[g % 2]
                    x.op("pe", lambda e: e.matmul(p_i[:], bc_[:, 4 + g, :], Sb[g][:], start=True, stop=True),
                         r=[bc_.d, Sb[g].d], w=[p_i.d])
                    def hG(hl):
                        h = g * 8 + hl
                        pgt = p_G[(h % 4) // 2]
                        pgs = slice((h % 2) * 128, (h % 2) * 128 + 128)
                        x.op("pe", lambda e: e.matmul(pgt[:, pgs], sel[:, h, :], acT[:], start=True, stop=False),
                             r=[sel.d, acT.d], w=[pgt.d], inc=False)
                        x.op("pe", lambda e: e.matmul(pgt[:, pgs], c.ident[:], c.negT[:], start=False, stop=True),
                             r=[c.ident.d, c.negT.d], w=[pgt.d])
                        eh = Eh[h % 4]
                        mh = Mh[h % 4]
                        x.op("act", lambda e: e.activation(out=eh[:], in_=pgt[:, pgs], func=AF.Exp,
                                                           bias=nacum[:, h:h + 1]), r=[pgt.d, nacum.d], w=[eh.d])
                        x.op("dve", lambda e: e.tensor_tensor(out=mh[:], in0=eh[:], in1=cbm[g][:], op=ALU.mult),
                             r=[eh.d, cbm[g].d], w=[mh.d])

                    def hY(hl):
                        h = g * 8 + hl
                        mh = Mh[h % 4]
                        x.op("pe", lambda e: e.matmul(py[:, hl * 64:(hl + 1) * 64], mh[:], xdt[:, h * 64:(h + 1) * 64],
                                                      start=True, stop=True), r=[mh.d, xdt.d], w=[py.d])

                    hG(0)
                    hG(1)
                    for hl in range(8):
                        if hl + 2 < 8:
                            hG(hl + 2)
                        hY(hl)
                    gs_ = slice(g * 512, (g + 1) * 512)
                    x.op("dve", lambda e: e.tensor_tensor(out=t1[:, gs_].rearrange("p (h d) -> p h d", d=64),
                                                          in0=p_i[:].rearrange("p (h d) -> p h d", d=64),
                                                          in1=bc(eac[:, g * 8:(g + 1) * 8].unsqueeze(2), [128, 8, 64]),
                                                          op=ALU.mult), r=[p_i.d, eac.d], mw=[t1.d])
                    x.op("dve", lambda e: e.tensor_tensor(out=yf[:, gs_], in0=py[:], in1=t1[:, gs_], op=ALU.add),
                         r=[py.d, t1.d], mw=[yf.d])
                    x.op("pe", lambda e: e.matmul(p_s[:], xt_[:, 2048 + g * 128:2048 + (g + 1) * 128], xde[:, gs_],
                                                  start=True, stop=True), r=[xt_.d, xde.d], w=[p_s.d])
                    x.op("pool", lambda e: e.tensor_tensor(out=St[g][:].rearrange("p (h d) -> p h d", d=64),
                                                           in0=St[g][:].rearrange("p (h d) -> p h d", d=64),
                                                           in1=bc(decay[:, g * 8:(g + 1) * 8].unsqueeze(2), [128, 8, 64]),
                                                           op=ALU.mult), r=[St[g].d, decay.d], w=[St[g].d])
                    x.op("dve", lambda e: e.tensor_tensor(out=St[g][:], in0=p_s[:], in1=St[g][:], op=ALU.add),
                         r=[p_s.d, St[g].d], w=[St[g].d])
                    x.op("act", lambda e: e.activation(out=Sb[g][:], in_=St[g][:], func=AF.Copy),
                         r=[St[g].d], w=[Sb[g].d])
                x.op("pool", lambda e: e.tensor_tensor(out=t1[:].rearrange("p (h d) -> p h d", d=64), in0=xv,
                                                       in1=bc(dsk_b[:].unsqueeze(2), [128, NHh, 64]), op=ALU.mult),
                     r=[xt_.d, dsk_b.d, yf.d], w=[t1.d])
                x.op("dve", lambda e: e.tensor_tensor(out=yf[:], in0=yf[:], in1=t1[:], op=ALU.add),
                     r=[yf.d, t1.d], w=[yf.d])
                x.op("act", lambda e: e.activation(out=t1[:], in_=zt[i][:], func=AF.Silu), r=[zt[i].d, yf.d], w=[t1.d])
                x.op("dve", lambda e: e.tensor_tensor(out=yf[:], in0=yf[:], in1=t1[:], op=ALU.mult),
                     r=[yf.d, t1.d], w=[yf.d])
                x.op("act", lambda e: e.activation(out=sqy[:], in_=yf[:], func=AF.Square), r=[yf.d], w=[sqy.d])
                x.op("dve", lambda e: e.tensor_reduce(out=ssy[:], in_=sqy[:].rearrange("p (g d) -> p g d", g=4),
                                                      axis=AX.X, op=ALU.add), r=[sqy.d], w=[ssy.d])
                rsqrt_mean(x, c, rsy, ssy, 512, 4)
                x.op("dve", lambda e: e.tensor_tensor(out=yf[:].rearrange("p (g d) -> p g d", g=4),
                                                      in0=yf[:].rearrange("p (g d) -> p g d", g=4),
                                                      in1=bc(rsy[:].unsqueeze(2), [128, 4, 512]), op=ALU.mult),
                     r=[yf.d, rsy.d], w=[yf.d])
                x.op("pool", lambda e: e.tensor_tensor(out=yb[i][:], in0=yf[:], in1=ngb[:], op=ALU.mult),
                     r=[yf.d, ngb.d], w=[yb[i].d])
                x.dma("sp", y[t0:t0 + 128, :], yb[i][:], r=[yb[i].d], mw=[d_y])


def la_handlers_dsa(x, c, ls, dram, outs_holder, pos, dm=False):
    invf16 = dram("invf16", [128, 16], F32, "ExternalInput")
    invf8 = dram("invf8", [128, 8], F32, "ExternalInput")
    gq = dram("gq", [128], F32, "ExternalInput")
    gk = dram("gk", [128], F32, "ExternalInput")
    gi = dram("gi", [64], F32, "ExternalInput")
    qT = dram("qT", [2, 16, 128, 8, 128] if dm else [16, 128, TOK], BF16, "ExternalOutput")
    kT = dram("kT", [4, 128, TOK], BF16, "ExternalOutput")
    v = dram("v", [TOK, 512], BF16, "ExternalOutput")
    qiT = dram("qiT", [2, 8, 128, 8, 128] if dm else [8, 128, TOK], BF16, "ExternalOutput")
    kiT = dram("kiT", [64, TOK], BF16, "ExternalOutput")
    wi = dram("wi", [2, 8, 128, 16] if dm else [TOK, 16], F32, "ExternalOutput")
    outs = [x.mkdep(n) for n in ("qT", "kT", "v", "qiT", "kiT", "wi")]
    outs_holder.extend(outs)
    cos16, sin16 = emit_rope_tables(x, ls, pos, invf16, 16, NT)
    cos8, sin8 = emit_rope_tables(x, ls, pos, invf8, 8, NT)
    gqb = load_bcast(x, ls, gq, 128, "gqb")
    gkb = load_bcast(x, ls, gk, 128, "gkb")
    gib = load_bcast(x, ls, gi, 64, "gib")
    qk_tiles = alloc_qk_tiles(x, ls)
    qbf = sbt(x, [128, 1024], BF16, "qbf", ls)
    stage = [sbt(x, [128, 4, TOK], BF16, "stage%d" % i, ls) for i in range(2)]
    kist = sbt(x, [64, TOK], BF16, "kist", ls)
    ptr = [pst(x, [128, 512], BF16, "ptr%d" % i, ls) for i in range(2)]
    vst = [sbt(x, [128, 512], BF16, "vst%d" % i, ls) for i in range(2)]
    wst = [sbt(x, [128, 16], F32, "wst%d" % i, ls) for i in range(2)]
    cnt = [0]
    nst = [0]
    blocks = []

    def mk_qk(col0, gdim, gain, half, cos, sin, dst_ap, dep, dmh=None):
        sg = stage[nst[0] % 2]
        nst[0] += 1

        def handler(t, ps):
            emit_qk_post2(x, c, qk_tiles, ps, gdim, gain, half, cos, sin, t, qbf)
            for a_ in range(2):
                p = ptr[cnt[0] % 2]
                cnt[0] += 1
                for j in range(4):
                    x.op("pe", lambda e: e.transpose(p[:, j * 128:(j + 1) * 128],
                                                     qbf[:, a_ * 512 + j * 128:a_ * 512 + (j + 1) * 128], c.ident[:]),
                         r=[qbf.d, c.ident.d], w=[p.d], inc=(j == 3))
                x.op("dve", lambda e: e.tensor_copy(sg[:, :, (t + a_) * 128:(t + a_ + 1) * 128],
                                                    p[:, :].rearrange("p (a b) -> p a b", a=4)), r=[p.d], mw=[sg.d])
            if t == NT - 2:
                if dmh is None:
                    x.dma("sp", dst_ap.rearrange("h p t -> p h t"), sg[:], r=[sg.d], mw=[dep])
                else:
                    tens, h0 = dmh
                    sgv = sg[:].rearrange("p h (k a t) -> p h k a t", a=2, t=128)
                    for a2 in range(2):
                        for hh_ in range(4):
                            x.dma("sp", tens[a2, h0 + hh_].rearrange("d k t -> d k t"), sgv[:, hh_, :, a2, :],
                                  r=[sg.d], mw=[dep])
        blocks.append((col0, 512, "tok2", handler))
    for hb in range(4):
        mk_qk(hb * 512, 128, gqb, 16, cos16, sin16, None if dm else qT[hb * 4:(hb + 1) * 4], outs[0],
              (qT, hb * 4) if dm else None)
    mk_qk(2048, 128, gkb, 16, cos16, sin16, kT[0:4], outs[1])

    def vh(t, ps):
        vs = vst[cnt[0] % 2]
        cnt[0] += 1
        x.op("act", lambda e: e.activation(out=vs[:], in_=ps[:, :], func=AF.Copy), r=[ps.d], w=[vs.d])
        x.dma("sp", v[t * 128:(t + 1) * 128, :], vs[:], r=[vs.d], mw=[outs[2]])
    blocks.append((2560, 512, "tok", vh))
    for qb in range(2):
        mk_qk(3072 + qb * 512, 64, None, 8, cos8, sin8, None if dm else qiT[qb * 4:(qb + 1) * 4], outs[3],
              (qiT, qb * 4) if dm else None)

    def kwh(t, ps):
        emit_qk_post(x, c, qk_tiles, ps, 64, 64, gib, 8, cos8, sin8, t, qbf)
        p = ptr[cnt[0] % 2]
        ws = wst[cnt[0] % 2]
        cnt[0] += 1
        x.op("pe", lambda e: e.transpose(p[0:64, 0:128], qbf[:, 0:64], c.ident[:]), r=[qbf.d, c.ident.d], w=[p.d])
        x.op("dve", lambda e: e.tensor_copy(kist[:, t * 128:(t + 1) * 128], p[0:64, 0:128]), r=[p.d], mw=[kist.d])
        x.op("act", lambda e: e.activation(out=ws[:], in_=ps[:, 64:80], func=AF.Copy, scale=0.25), r=[ps.d], w=[ws.d])
        x.dma("sp", wi[t % 2, t // 2] if dm else wi[t * 128:(t + 1) * 128, :], ws[:], r=[ws.d], mw=[outs[5]])
        if t == NT - 1:
            x.dma("sp", kiT, kist[:], r=[kist.d], mw=[outs[4]])
    blocks.append((4096, 80, "tok", kwh))
    return blocks


def emit_LB_DSA(x, c, dram):
    NS = 16
    qTs = dram("qTs", [NS, 128, 2048], BF16, "ExternalInput")
    qiTs = dram("qiTs", [NS, 128, 1024], BF16, "ExternalInput")
    wis = dram("wis", [NS, 128, 16], F32, "ExternalInput")
    kT = dram("kT", [4, 128, S], BF16, "ExternalInput")
    v = dram("v", [S, 512], BF16, "ExternalInput")
    kiT2 = dram("kiT2", [128, S], BF16, "ExternalInput")
    dmask = dram("dmask", [2, 128, 128], F32, "ExternalInput")
    gqk = dram("gqk", [2, 128], F32, "ExternalInput")
    o = dram("o", [NS * 128, 2048], BF16, "ExternalOutput")
    SCALE = 128 ** -0.5
    with Scope(x) as st:
        d_in = x.mkdep("in")
        d_o = x.mkdep("o")
        ls = st
        gt = [load_bcast(x, ls, gqk[i], 128, "gqk%d" % i) for i in range(2)]
        gm = sbt(x, [128, 2], F32, "gm")
        negC = sbt(x, [128, 1], F32, "negC")
        for i in range(2):
            x.op("dve", lambda e: e.tensor_reduce(out=gm[:, i:i + 1], in_=gt[i][:], axis=AX.X, op=ALU.max,
                                                  apply_absolute_value=True), r=[gt[i].d], w=[gm.d])
        x.op("dve", lambda e: e.scalar_tensor_tensor(out=negC[:], in0=gm[:, 0:1], scalar=-(128 ** 0.5), in1=gm[:, 1:2],
                                                     op0=ALU.mult, op1=ALU.mult), r=[gm.d], w=[negC.d])
        kts = sbt(x, [128, 4, S], BF16, "kts")
        x.dma("sp", kts[:], kT.rearrange("g p t -> p g t"), w=[kts.d])
        va = sbt(x, [128, 32, 4, 129], BF16, "va")
        x.op("pool", lambda e: e.memset(va[:, :, :, 128:129], 1.0), w=[va.d])
        for g in range(4):
            x.dma("sp", va[:, :, g, 0:128], v[:, g * 128:(g + 1) * 128].rearrange("(kb p) d -> p kb d", p=128),
                  mw=[va.d])
        ki2 = sbt(x, [128, S], BF16, "ki2")
        x.dma("sp", ki2[:], kiT2, w=[ki2.d])
        dm = sbt(x, [128, 2, 128], F32, "dm")
        x.dma("sp", dm[:], dmask.rearrange("a p k -> p a k"), w=[dm.d])
        zr = sbt(x, [128, 512], BF16, "zr")
        x.op("pool", lambda e: e.memset(zr[:], 0.0), w=[zr.d])
        acc = sbt(x, [128, S], F32, "acc")
        work = sbt(x, [128, S], F32, "work")
        nb = sbt(x, [128, S], BF16, "nb")
        nbT4 = sbt(x, [128, 32, 4, 128], BF16, "nbT4")
        qs = [sbt(x, [128, 2048], BF16, "qs%d" % i) for i in range(2)]
        qis = [sbt(x, [128, 8, 128], BF16, "qis%d" % i) for i in range(2)]
        wt = [sbt(x, [128, 16], F32, "wt%d" % i) for i in range(2)]
        aw = sbt(x, [128, 16], F32, "aw")
        sgn = sbt(x, [128, 16], F32, "sgn")
        rr = [sbt(x, [128, 512], F32, "rr%d" % i) for i in range(2)]
        m8 = sbt(x, [128, 8], F32, "m8")
        thr = sbt(x, [128, 1], F32, "thr")
        thr0 = sbt(x, [128, 1], F32, "thr0")
        x.op("pool", lambda e: e.memset(thr0[:], -1e29), w=[thr0.d])
        P = [sbt(x, [128, 512], BF16, "P%d" % i) for i in range(2)]
        R = sbt(x, [128, 4], F32, "R")
        ob = [sbt(x, [128, 2048], BF16, "ob%d" % i) for i in range(2)]
        p_ix = [pst(x, [128, 512], F32, "p_ix%d" % i) for i in range(2)]
        p_tr = [pst(x, [128, 512], BF16, "p_tr%d" % i) for i in range(2)]
        p_s = [pst(x, [128, 512], F32, "p_s%d" % i) for i in range(2)]
        p_o = pst(x, [128, 4, 256], F32, "p_o")
        p_ob = p_o[:].rearrange("p a b -> p (a b)")
        nix = 0
        ns_ = 0
        def part_A(i):
            nonlocal nix
            b2 = i % 2
            nkb = 2 * i + 2
            L = nkb * 128
            x.dma("sp", qs[b2][:], qTs[i], r=[d_in], w=[qs[b2].d])
            x.dma("sp", qis[b2][:], qiTs[i].rearrange("p (a t) -> p a t", a=8), r=[d_in], w=[qis[b2].d])
            x.dma("sp", wt[b2][:], wis[i], r=[d_in], w=[wt[b2].d])
            w_ = wt[b2]
            x.op("dve", lambda e: e.scalar_tensor_tensor(out=aw[:], in0=w_[:], scalar=-1.0, in1=w_[:],
                                                         op0=ALU.mult, op1=ALU.max), r=[w_.d], w=[aw.d])
            x.op("dve", lambda e: e.tensor_scalar(out=aw[:], in0=aw[:], scalar1=0.125, scalar2=None, op0=ALU.mult),
                 r=[aw.d], w=[aw.d])
            x.op("dve", lambda e: e.tensor_scalar(out=sgn[:], in0=w_[:], scalar1=0.0, scalar2=2.0,
                                                  op0=ALU.is_ge, op1=ALU.mult), r=[w_.d], w=[sgn.d])
            x.op("dve", lambda e: e.tensor_scalar(out=sgn[:], in0=sgn[:], scalar1=-1.0, scalar2=None, op0=ALU.add),
                 r=[sgn.d], w=[sgn.d])
            for kq in range((L + 511) // 512):
                W = min(512, L - kq * 512)
                cs = slice(kq * 512, kq * 512 + W)
                for hi in range(16):
                    ps = p_ix[nix % 2]
                    r_ = rr[nix % 2]
                    nix += 1
                    pr_ = slice((hi % 2) * 64, (hi % 2) * 64 + 64)
                    x.op("pe", lambda e: e.matmul(ps[:, 0:W], qis[b2][pr_, hi // 2, :], ki2[pr_, cs], start=True, stop=True),
                         r=[qis[b2].d, ki2.d], w=[ps.d])
                    x.op("act", lambda e: e.activation(out=r_[:, 0:W], in_=ps[:, 0:W], func=AF.Relu, scale=aw[:, hi:hi + 1]),
                         r=[ps.d, aw.d], w=[r_.d])
                    if hi == 0:
                        x.op("dve", lambda e: e.tensor_scalar(out=acc[:, cs], in0=r_[:, 0:W], scalar1=sgn[:, 0:1],
                                                              scalar2=None, op0=ALU.mult), r=[r_.d, sgn.d], w=[acc.d])
                    else:
                        x.op("dve", lambda e: e.scalar_tensor_tensor(out=acc[:, cs], in0=r_[:, 0:W], scalar=sgn[:, hi:hi + 1],
                                                                     in1=acc[:, cs], op0=ALU.mult, op1=ALU.add),
                             r=[r_.d, sgn.d, acc.d], w=[acc.d])
            for a in range(2):
                ks = slice((nkb - 2 + a) * 128, (nkb - 1 + a) * 128)
                x.op("dve", lambda e: e.tensor_tensor(out=acc[:, ks], in0=acc[:, ks], in1=dm[:, a, :], op=ALU.add),
                     r=[acc.d, dm.d], w=[acc.d])
            if i >= 1:
                x.op("pool", lambda e: e.tensor_copy(work[:, 0:L], acc[:, 0:L]), r=[acc.d], w=[work.d])
                for rd in range(32):
                    x.op("dve", lambda e: e.max(out=m8[:], in_=work[:, 0:L]), r=[work.d], w=[m8.d])
                    if rd < 31:
                        x.op("dve", lambda e: e.match_replace(out=work[:, 0:L], in_to_replace=m8[:], in_values=work[:, 0:L],
                                                              imm_value=-1e30), r=[m8.d, work.d], w=[work.d])
                x.op("dve", lambda e: e.tensor_copy(thr[:], m8[:, 7:8]), r=[m8.d], w=[thr.d])
                th = thr
            else:
                th = thr0
            x.op("dve", lambda e: e.tensor_scalar(out=nb[:, 0:L], in0=acc[:, 0:L], scalar1=th[:, 0:1], scalar2=NEG,
                                                  op0=ALU.is_lt, op1=ALU.mult), r=[acc.d, th.d], w=[nb.d])

        def part_T(i):
            b2 = i % 2
            nkb = 2 * i + 2
            L = nkb * 128
            for kg in range((nkb + 3) // 4):
                p = p_tr[kg % 2]
                nn = min(4, nkb - kg * 4)
                for j in range(nn):
                    kb = kg * 4 + j
                    x.op("pe", lambda e: e.transpose(p[:, j * 128:(j + 1) * 128], nb[:, kb * 128:(kb + 1) * 128], c.ident[:]),
                         r=[nb.d, c.ident.d], w=[p.d], inc=(j == nn - 1))
                src = p[:, 0:nn * 128].rearrange("p (a b) -> p a b", a=nn)
                x.op("act", lambda e: e.activation(out=nbT4[:, kg * 4:kg * 4 + nn, :, :],
                                                   in_=bc(src.unsqueeze(2), [128, nn, 4, 128]), func=AF.Copy),
                     r=[p.d], w=[nbT4.d])

        def part_C(i):
            b2 = i % 2
            nkb = 2 * i + 2
            L = nkb * 128
            obb = ob[b2]
            asteps = [(g, kb) for g in range(4) for kb in range(nkb)]

            def a_scores(idx):
                g, kb = asteps[idx]
                ps = p_s[idx % 2]
                pp_ = P[idx % 2]
                x.op("pe", lambda e: e.matmul(ps[:], kts[:, g, kb * 128:(kb + 1) * 128],
                                              qs[b2][:, g * 512:(g + 1) * 512], start=True, stop=False),
                     r=[kts.d, qs[b2].d], w=[ps.d], inc=False)
                x.op("pe", lambda e: e.matmul(ps[:], c.ident[:], nbT4[:, kb, :, :].rearrange("p a b -> p (a b)"),
                                              start=False, stop=True), r=[c.ident.d, nbT4.d], w=[ps.d])
                x.op("act", lambda e: e.activation(out=pp_[:], in_=ps[:], func=AF.Exp, scale=SCALE, bias=negC[:, 0:1]),
                     r=[ps.d, negC.d], w=[pp_.d])

            def a_pv(idx):
                g, kb = asteps[idx]
                pp_ = P[idx % 2]
                for r in range(4):
                    x.op("pe", lambda e: e.matmul(p_o[:, r, 0:129], pp_[:, r * 128:(r + 1) * 128], va[:, kb, g, :],
                                                  start=False, stop=(kb == nkb - 1 and r % 2 == 1)),
                         r=[pp_.d, va.d], w=[p_o.d], inc=(r == 3))

            def a_epi(g):
                x.op("dve", lambda e: e.reciprocal(out=R[:], in_=p_o[:, :, 128:129].rearrange("p a b -> p (a b)")),
                     r=[p_o.d], w=[R.d])
                for r in range(4):
                    hh = g * 4 + r
                    x.op("act", lambda e: e.activation(out=obb[:, hh * 128:(hh + 1) * 128], in_=p_o[:, r, 0:128],
                                                       func=AF.Copy, scale=R[:, r:r + 1]), r=[p_o.d, R.d], mw=[obb.d])

            a_scores(0)
            for idx, (g, kb) in enumerate(asteps):
                if kb == 0:
                    for bnk in range(2):
                        x.op("pe", lambda e: e.matmul(p_ob[:, bnk * 512:(bnk + 1) * 512], zr[:, 0:128], zr[:],
                                                      start=True, stop=False), r=[zr.d], w=[p_o.d], inc=False)
                if idx + 1 < len(asteps):
                    a_scores(idx + 1)
                a_pv(idx)
                if kb == nkb - 1:
                    a_epi(g)
            x.dma("sp", o[i * 128:(i + 1) * 128, :], obb[:], r=[obb.d], mw=[d_o])

        part_A(0)
        part_T(0)
        for i in range(NS):
            if i + 1 < NS:
                part_A(i + 1)
            part_C(i)
            if i + 1 < NS:
                part_T(i + 1)


def _standalone(emit, *args):
    nc = bass.Bass("TRN2", target_bir_lowering=False)
    dram = lambda n, s, dt, k: nc.dram_tensor(n, list(s), dt, kind=k).ap()
    with ExitStack() as st:
        x = X(nc, st)
        c = make_consts(x)
        emit(x, c, dram, *args)
        x.global_barrier()
        print(emit.__name__, args, "sems", x.nsem, "cnt", x.cnt)
    return nc


def build_LA(kind):
    return _standalone(emit_LA, kind)


def build_LB_DA(lambda_init):
    return _standalone(emit_LB_DA, lambda_init)


def build_LB_SSD():
    return _standalone(emit_LB_SSD)


def build_LB_DSA():
    return _standalone(emit_LB_DSA)


def build_LC(FO):
    return _standalone(emit_LC, FO)


def _invf_table(half):
    invf = np.power(np.float32(ROPE_THETA), -np.arange(half, dtype=np.float32) / half).astype(np.float32)
    return np.ascontiguousarray(np.broadcast_to(invf[None, :], (128, half))).astype(np.float32)


def _ca(a):
    return np.ascontiguousarray(a)


GROUPS = [[0, 1], [2, 3], [4, 5], [6, 7]]
FIN_K = {0: 6144, 1: 10304, 2: 4176}


def _mk_dram(mapping):
    def dram(n, s, dt, k):
        ap = mapping[n]
        assert [int(v) for v in ap.shape] == [int(v) for v in s], (n, ap.shape, s)
        return ap
    return dram


def build_fused(depth=DEPTH):
    nc = bass.Bass("TRN2", target_bir_lowering=False)
    ext_in = lambda n, s, dt: nc.dram_tensor(n, list(s), dt, kind="ExternalInput").ap()
    internal = lambda n, s, dt: nc.dram_tensor(n, list(s), dt, kind="Internal").ap()
    x_in = ext_in("x_in", [TOK, D], F32)
    c_in = ext_in("c_in", [D], F32)
    pos = ext_in("pos", [TOK], I32)
    rk = ext_in("rk", [1, 1], I32)
    invf8 = ext_in("invf8", [128, 8], F32)
    invf16 = ext_in("invf16", [128, 16], F32)
    dmask = ext_in("dmask", [2, 128, 128], F32)
    x_out = nc.dram_tensor("x_out", [TOK, D], F32, kind="ExternalOutput").ap()
    xres = internal("xres", [TOK, D], F32)
    xmid = internal("xmid", [TOK, D], F32)
    scratch = {}

    def scr(n, s, dt):
        if n not in scratch:
            scratch[n] = internal(n, s, dt)
        return scratch[n]

    with ExitStack() as st:
        x = X(nc, st)
        c = make_consts(x)
        reg = st.enter_context(nc.gpsimd.register("rk"))
        nc.gpsimd.reg_load(reg, rk[0:1, 0:1])
        r = nc.gpsimd.snap(reg, min_val=0, max_val=1)
        d_g = x.mkdep("xchg")
        RS = bass.ds(r, 1)
        CH = 2 * 1024 * 1024
        MAXE = {BF16: 12 * 1024 * 1024 + 4096, F32: 1024 * 1024}
        GBS = {BF16: [], F32: []}
        goff = {BF16: 0, F32: 0}

        def gather(parts, both=False, stage=False):
            a0 = parts[0]
            dt = a0.dtype
            shp = [int(v) for v in a0.shape]
            rowe = int(np.prod(shp[1:]))
            c0 = max(1, min(shp[0], CH // (rowe * mybir.dt.size(dt))))
            while shp[0] % c0:
                c0 -= 1
            nch = shp[0] // c0
            ce = c0 * rowe
            assert nch * 2 * ce <= MAXE[dt], (nch, ce, dt)
            if goff[dt] >= len(GBS[dt]):
                GBS[dt].append(internal("GB%d_%d" % (mybir.dt.size(dt), len(GBS[dt])), [2, MAXE[dt]], dt))
            gb = GBS[dt][goff[dt]]
            goff[dt] += 1
            off = 0
            for d_, a in enumerate(parts):
                for k in range(nch):
                    x.op("pool", lambda e: e.collective_compute(
                        "AllGather", ALU.bypass, replica_groups=GROUPS, ins=[a[k * c0:(k + 1) * c0].opt()],
                        outs=[gb[d_, off + k * 2 * ce:off + (k + 1) * 2 * ce].opt()]), w=[d_g])
            x.global_barrier()
            tot = nch * 2 * ce
            if stage:
                gs = scr("GS%d_%d" % (mybir.dt.size(dt), goff[dt] - 1), [1, MAXE[dt]], dt)
                CP = 4 * 1024 * 1024
                for o_ in range(0, tot, CP):
                    n_ = min(CP, tot - o_)
                    x.dma("pool", gs[:, o_:o_ + n_], (gb[0:1] if both else gb[RS])[:, o_:o_ + n_], mw=[d_g])
                row = gs[:, 0:tot]
            else:
                row = (gb[0:1] if both else gb[RS])[:, 0:tot]
            names = ["e%d" % i_ for i_ in range(len(shp) - 1)]
            kw = {"k": nch, "s": 2, "c": c0}
            kw.update({n_: v_ for n_, v_ in zip(names, shp[1:])})
            return row.rearrange("a (k s c %s) -> (a k) s c %s" % (" ".join(names), " ".join(names)), **kw)

        def cp(dst, src):
            x.dma("pool", dst, src, mw=[d_g])

        for i in range(depth):
            kind, j = i % 3, i // 3
            xsrc = x_in if i == 0 else xres
            xdst = x_out if i == depth - 1 else xres
            sfx = "_%d" % i
            ada_i = internal("ada" + sfx, [6 * D], F32)
            mp = {"x_in": xsrc, "c_in": c_in, "pos": pos, "ada_w": ext_in("ada_w" + sfx, [D, 6 * D], F32),
                  "ada_b": ext_in("ada_b" + sfx, [6 * D], F32), "g1": ext_in("g1" + sfx, [D], F32), "ada": ada_i,
                  "w_in": ext_in("w_in" + sfx, [D, FIN_K[kind]], F32)}
            goff[BF16] = goff[F32] = 0
            if kind == 0:
                A = {"qT": scr("A_qT", [16, 128, TOK], BF16), "kT": scr("A_kT", [16, 128, TOK], BF16),
                     "v": scr("A_v", [2, TOK, 1024], BF16)}
                mp.update(A)
                mp.update({"invf": invf8, "gq": ext_in("gq" + sfx, [64], F32), "gk": ext_in("gk" + sfx, [64], F32)})
                emit_LA(x, c, _mk_dram(mp), kind, True)
                x.global_barrier()
                Gq = gather([A["qT"][0:8], A["qT"][8:16]])
                Gk = gather([A["kT"][0:8], A["kT"][8:16]])
                Gv = gather([A["v"][0], A["v"][1]])
                x.global_barrier()
                L_qT = scr("L_qT", [8, 128, S], BF16)
                L_kT = scr("L_kT", [8, 128, S], BF16)
                L_v = scr("L_v", [S, 1024], BF16)
                for s_ in range(2):
                    cs = slice(s_ * TOK, (s_ + 1) * TOK)
                    for kk in range(2):
                        cp(L_qT[kk * 4:(kk + 1) * 4, :, cs], Gq[kk, s_])
                        cp(L_kT[kk * 4:(kk + 1) * 4, :, cs], Gk[kk, s_])
                        cp(L_v[s_ * TOK + kk * 1024:s_ * TOK + (kk + 1) * 1024, :], Gv[kk, s_])
                x.global_barrier()
                B_o = scr("B_o", [S, 1024], BF16)
                li = 0.8 - 0.6 * math.exp(-0.3 * i)
                emit_LB_DA(x, c, _mk_dram({"qT": L_qT, "kT": L_kT, "v": L_v, "lam4": ext_in("lam4" + sfx, [4, 64], F32),
                                           "gqk": ext_in("gqk" + sfx, [2, 64], F32),
                                           "subg": ext_in("subg" + sfx, [128], F32), "o": B_o}), float(li))
                x.global_barrier()
                goff[BF16] = 0
                Go = gather([B_o[0:TOK], B_o[TOK:S]])
                x.global_barrier()
                FO = 2048
                L_o = scr("L_o", [TOK, 2048], BF16)
                for hh in range(2):
                    for kk in range(2):
                        cp(L_o[kk * 1024:(kk + 1) * 1024, hh * 1024:(hh + 1) * 1024], Go[kk, hh])
            elif kind == 1:
                A = {"z": scr("A_z", [2, TOK, 2048], BF16), "xbcT": scr("A_xbcT", [2, 3072, TOK], BF16),
                     "dtr": scr("A_dtr", [2, TOK, 32], F32)}
                mp.update(A)
                emit_LA(x, c, _mk_dram(mp), kind, True)
                x.global_barrier()
                Gz = gather([A["z"][0], A["z"][1]])
                Gx = gather([A["xbcT"][0], A["xbcT"][1]])
                Gd = gather([A["dtr"][0], A["dtr"][1]])
                x.global_barrier()
                L_raw = scr("L_raw", [3072, S], F32)
                L_z = scr("L_z", [S, 2048], BF16)
                L_dt = scr("L_dt", [S, 32], F32)
                for s_ in range(2):
                    cs = slice(s_ * TOK, (s_ + 1) * TOK)
                    for kk in range(6):
                        cp(L_raw[kk * 512:(kk + 1) * 512, cs], Gx[kk, s_])
                    for kk in range(4):
                        cp(L_z[s_ * TOK + kk * 512:s_ * TOK + (kk + 1) * 512, :], Gz[kk, s_])
                    cp(L_dt[cs, :], Gd[0, s_])
                x.global_barrier()
                B_y = scr("B_y", [S, 2048], BF16)
                emit_LB_SSD(x, c, _mk_dram({"raw": L_raw, "convw": ext_in("convw" + sfx, [4, 3072], F32),
                                            "convb": ext_in("convb" + sfx, [3072], F32), "dtr": L_dt,
                                            "hp": ext_in("hp" + sfx, [3, 32], F32), "z": L_z,
                                            "ng": ext_in("ng" + sfx, [2048], F32),
                                            "tokd": scr("tokd", [S, 2560], BF16), "featd": scr("featd", [1024, S], BF16),
                                            "y": B_y}))
                x.global_barrier()
                goff[BF16] = 0
                Gy = gather([B_y[0:TOK], B_y[TOK:S]])
                x.global_barrier()
                FO = 4096
                L_o = scr("L_o4", [TOK, 4096], BF16)
                for gh in range(2):
                    for kk in range(4):
                        cp(L_o[kk * 512:(kk + 1) * 512, gh * 2048:(gh + 1) * 2048], Gy[kk, gh])
            else:
                A = {"qT": scr("D_qT", [2, 16, 128, 8, 128], BF16), "kT": scr("D_kT", [4, 128, TOK], BF16),
                     "v": scr("D_v", [TOK, 512], BF16), "qiT": scr("D_qiT", [2, 8, 128, 8, 128], BF16),
                     "kiT": scr("D_kiT", [64, TOK], BF16), "wi": scr("D_wi", [2, 8, 128, 16], F32)}
                mp.update(A)
                mp.update({"invf16": invf16, "invf8": invf8, "gq": ext_in("gq" + sfx, [128], F32),
                           "gk": ext_in("gk" + sfx, [128], F32), "gi": ext_in("gi" + sfx, [64], F32)})
                emit_LA(x, c, _mk_dram(mp), kind, True)
                x.global_barrier()
                Gq = gather([A["qT"][0], A["qT"][1]], stage=True)
                Gqi = gather([A["qiT"][0], A["qiT"][1]], stage=True)
                Gw = gather([A["wi"][0], A["wi"][1]], stage=True)
                Gk = gather([A["kT"]], both=True)
                Gv = gather([A["v"]], both=True)
                Gki = gather([A["kiT"]], both=True)
                x.global_barrier()
                L_qTs = scr("L_qTs", [16, 128, 2048], BF16)
                L_qiTs = scr("L_qiTs", [16, 128, 1024], BF16)
                L_wis = scr("L_wis", [16, 128, 16], F32)
                L_kT = scr("L_kT4", [4, 128, S], BF16)
                L_v = scr("L_v4", [S, 512], BF16)
                L_ki = scr("L_ki2", [128, S], BF16)
                with nc.allow_non_contiguous_dma(reason="tile gathers"):
                    for sl_ in range(16):
                        s_, k_ = sl_ // 8, sl_ % 8
                        for kk in range(2):
                            cp(L_qTs[sl_][:, kk * 1024:(kk + 1) * 1024].rearrange("d (h t) -> d h t", h=8),
                               Gq[kk, s_][:, :, k_, :].rearrange("h d t -> d h t"))
                        cp(L_qiTs[sl_].rearrange("d (h t) -> d h t", h=8),
                           Gqi[0, s_][:, :, k_, :].rearrange("h d t -> d h t"))
                        cp(L_wis[sl_], Gw[0, s_][k_])
                for s_ in range(2):
                    cs = slice(s_ * TOK, (s_ + 1) * TOK)
                    cp(L_kT[:, :, cs], Gk[0, s_])
                    cp(L_v[cs, :], Gv[0, s_])
                    for dup in range(2):
                        cp(L_ki[dup * 64:(dup + 1) * 64, cs], Gki[0, s_])
                x.global_barrier()
                B_o2 = scr("B_o2", [TOK, 2048], BF16)
                emit_LB_DSA(x, c, _mk_dram({"qTs": L_qTs, "qiTs": L_qiTs, "wis": L_wis, "kT": L_kT, "v": L_v,
                                            "kiT2": L_ki, "dmask": dmask,
                                            "gqk": ext_in("gqk" + sfx, [2, 128], F32), "o": B_o2}))
                x.global_barrier()
                goff[BF16] = 0
                Go2 = gather([B_o2[0:1024], B_o2[1024:2048]], stage=True)
                x.global_barrier()
                FO = 2048
                L_o = scr("L_o", [TOK, 2048], BF16)
                for tl in range(16):
                    jj = tl // 2
                    cp(L_o[tl * 128:(tl + 1) * 128, :], Go2[jj // 4, tl % 2][(jj % 4) * 128:(jj % 4 + 1) * 128, :])
            x.global_barrier()
            emit_LC(x, c, _mk_dram({"x_in": xsrc, "o_in": L_o, "ada": ada_i, "g2": ext_in("g2" + sfx, [D], F32),
                                    "w_out": ext_in("w_out" + sfx, [FO, D], F32),
                                    "wgu": ext_in("wgu" + sfx, [D, 2 * FFN], F32),
                                    "wd": ext_in("wd" + sfx, [FFN, D], F32), "x_mid": xmid, "x_out": xdst,
                                    "act_d": scr("act_d", [NT, 128, FFN // 128, 128], BF16)}), FO)
            x.global_barrier()
        print("fused sems", x.nsem, "cnt", x.cnt)
    return nc


def fused_inputs(inp, depth=DEPTH):
    x = np.asarray(inp["x"], dtype=np.float32)
    c = np.asarray(inp["c"], dtype=np.float32)
    pos = np.asarray(inp["positions"]).astype(np.int32)
    tri = np.where(np.arange(128)[None, :] <= np.arange(128)[:, None], 0.0, -1e30).astype(np.float32)
    full_neg = np.full((128, 128), -1e30, np.float32)
    zero_m = np.zeros((128, 128), np.float32)
    f32 = lambda a: _ca(np.asarray(a, dtype=np.float32))
    maps = []
    for cc in range(8):
        b, h = cc // 2, cc % 2
        sl = slice(h * TOK, (h + 1) * TOK)
        m = {"x_in": _ca(x[b, sl]), "c_in": _ca(c[b]), "pos": _ca(pos[b, sl]), "rk": np.array([[h]], np.int32),
             "invf8": _invf_table(8), "invf16": _invf_table(16),
             "dmask": np.stack([tri, full_neg]) if h == 0 else np.stack([zero_m, tri])}
        for i in range(depth):
            kind, j = i % 3, i // 3
            sfx = "_%d" % i
            m["ada_w" + sfx] = inp["ada_w"][i]
            m["ada_b" + sfx] = inp["ada_b"][i]
            m["g1" + sfx] = inp["norm1_g"][i]
            m["g2" + sfx] = inp["norm2_g"][i]
            m["wgu" + sfx] = inp["ffn_w_gate_up"][i]
            m["wd" + sfx] = inp["ffn_w_down"][i]
            if kind == 0:
                m["w_in" + sfx] = inp["da_w_in"][j]
                m["w_out" + sfx] = inp["da_w_out"][j]
                m["gq" + sfx] = inp["da_q_norm_g"][j]
                m["gk" + sfx] = inp["da_k_norm_g"][j]
                m["lam4" + sfx] = f32(np.stack([inp["da_lambda_q1"][j], inp["da_lambda_k1"][j],
                                                inp["da_lambda_q2"][j], inp["da_lambda_k2"][j]]))
                m["gqk" + sfx] = f32(np.stack([inp["da_q_norm_g"][j], inp["da_k_norm_g"][j]]))
                m["subg" + sfx] = inp["da_subln_g"][j]
            elif kind == 1:
                m["w_in" + sfx] = inp["ssd_w_in"][j]
                m["w_out" + sfx] = inp["ssd_w_out"][j]
                ch = np.concatenate([np.arange(h * 2048, (h + 1) * 2048), np.arange(4096 + h * 512, 4096 + (h + 1) * 512),
                                     np.arange(5120 + h * 512, 5120 + (h + 1) * 512)])
                hs = slice(h * 32, (h + 1) * 32)
                m["convw" + sfx] = _ca(inp["ssd_conv_w"][j][:, ch])
                m["convb" + sfx] = _ca(inp["ssd_conv_b"][j][ch])
                m["hp" + sfx] = f32(np.stack([inp["ssd_dt_bias"][j][hs], inp["ssd_a_log"][j][hs], inp["ssd_d_skip"][j][hs]]))
                m["ng" + sfx] = _ca(inp["ssd_norm_g"][j][h * 2048:(h + 1) * 2048])
            else:
                m["w_in" + sfx] = inp["sa_w_in"][j]
                m["w_out" + sfx] = inp["sa_w_out"][j]
                m["gq" + sfx] = inp["sa_q_norm_g"][j]
                m["gk" + sfx] = inp["sa_k_norm_g"][j]
                m["gi" + sfx] = inp["sa_idx_k_norm_g"][j]
                m["gqk" + sfx] = f32(np.stack([inp["sa_q_norm_g"][j], inp["sa_k_norm_g"][j]]))
        maps.append(m)
    return maps


def kernel(**inp):
    depth = DEPTH
    nc = build_fused(depth)
    maps = fused_inputs(inp, depth)
    res = run_bass_kernel_spmd(nc, maps, core_ids=list(range(8))).results
    out = np.empty((NB, S, D), np.float32)
    for cc in range(8):
        b, h = cc // 2, cc % 2
        out[b, h * TOK:(h + 1) * TOK] = res[cc]["x_out"]
    return out
```

```python
import math
import numpy as np
from contextlib import ExitStack
import ml_dtypes
import concourse.bass as bass
import concourse.mybir as mybir
from concourse.bass_utils import run_bass_kernel_spmd

F32 = mybir.dt.float32
BF16 = mybir.dt.bfloat16
I32 = mybir.dt.int32
ALU = mybir.AluOpType
AF = mybir.ActivationFunctionType
AX = mybir.AxisListType

D = 2048
S = 4096
NB = 4
DEPTH = 4
KC = D // 128
TOK = 2048
NT = TOK // 128
FFN = 5632
EPS = 1e-6
ROPE_THETA = 500000.0
NEG = -30000.0

SAME_ENGINE_SYNC = True


class Dep:
    __slots__ = ("name", "w", "r", "dsem", "dq")

    def __init__(self, name=""):
        self.name = name
        self.w = {}
        self.r = {}
        self.dsem = None
        self.dq = None


class X:
    def __init__(self, nc, stack):
        self.nc = nc
        self.stack = stack
        self.root = stack
        self.eng = {"pe": nc.tensor, "act": nc.scalar, "dve": nc.vector,
                    "pool": nc.gpsimd, "sp": nc.sync}
        self.sem = {}
        self.cnt = {}
        self.seen = {}
        for k in self.eng:
            self.sem[k] = stack.enter_context(nc.semaphore("es_" + k))
            self.cnt[k] = 0
            self.seen[k] = {}
        self.nsem = 5
        self.semcnt = {}
        self.free_dsems = {"sp": [], "pool": [], "act": []}
        self.alltok = {}
        self.uid = 0

    def name(self, p):
        self.uid += 1
        return "%s_%d" % (p, self.uid)

    def sb(self, shape, dt, name="sb", stack=None):
        st = stack or self.stack
        return st.enter_context(self.nc.sbuf_tensor(self.name(name), list(shape), dt))

    def ps(self, shape, dt=F32, name="ps", stack=None):
        st = stack or self.stack
        return st.enter_context(self.nc.psum_tensor(self.name(name), list(shape), dt))

    def _wait(self, e, toks):
        en = self.eng[e]
        seen = self.seen[e]
        for s, v in toks.items():
            cv = self.semcnt.get(s)
            if cv is not None and cv > v:
                v = cv
            if seen.get(s, 0) >= v:
                continue
            if s is self.sem[e]:
                if e == "pe" or not SAME_ENGINE_SYNC:
                    continue
            en.wait_ge(s, v)
            seen[s] = v

    def _pre(self, e, r, w, mw=()):
        for d in r:
            self._wait(e, d.w)
        for d in w:
            self._wait(e, d.w)
            self._wait(e, d.r)
        for d in mw:
            self._wait(e, d.r)

    def _post(self, tok, r, w, mw=()):
        s, v = tok
        if self.alltok.get(s, 0) < v:
            self.alltok[s] = v
        for d in r:
            if d.r.get(s, 0) < v:
                d.r[s] = v
        for d in w:
            d.w = {s: v}
            d.r = {}
        for d in mw:
            if d.w.get(s, 0) < v:
                d.w[s] = v

    def op(self, e, fn, r=(), w=(), mw=(), inc=True):
        self._pre(e, r, w, mw)
        ins = fn(self.eng[e])
        if inc:
            self.cnt[e] += 1
            ins.then_inc(self.sem[e], 1)
            self._post((self.sem[e], self.cnt[e]), r, w, mw)
        else:
            self._post((self.sem[e], self.cnt[e] + 1), r, w, mw)
        return ins

    def dma(self, e, out, in_, r=(), w=(), mw=(), **kw):
        host = (list(w) + list(mw))[0]
        assert host.dq in (None, e), (host.name, host.dq, e)
        if host.dsem is None:
            host.dq = e
            if self.free_dsems[e]:
                host.dsem = self.free_dsems[e].pop()
            else:
                host.dsem = self.root.enter_context(self.nc.semaphore(self.name("ds")))
                self.semcnt[host.dsem] = 0
                self.nsem += 1
        self._pre(e, r, w, mw)
        ins = self.eng[e].dma_start(out=out, in_=in_, **kw)
        self.semcnt[host.dsem] += 16
        ins.then_inc(host.dsem, 16)
        self._post((host.dsem, self.semcnt[host.dsem]), r, w, mw)
        return ins

    def mkdep(self, name=""):
        d = Dep(name)
        if isinstance(self.stack, Scope):
            self.stack.deps.append(d)
        return d

    def global_barrier(self):
        for e in self.eng:
            self._wait(e, dict(self.alltok))

    def barrier(self, deps):
        toks = {}
        for d in deps:
            for src in (d.w, d.r):
                for s_, v in src.items():
                    if toks.get(s_, 0) < v:
                        toks[s_] = v
        for e in self.eng:
            self._wait(e, toks)

    def finish(self, deps, e="sp"):
        for d in deps:
            self._wait(e, d.w)


class Scope:
    def __init__(self, x):
        self.x = x
        self.st = ExitStack()
        self.deps = []

    def __enter__(self):
        self.st.__enter__()
        self.prev = self.x.stack
        self.x.stack = self
        return self

    def __exit__(self, *a):
        self.x.stack = self.prev
        if a[0] is None:
            self.x.barrier(self.deps)
            for d in self.deps:
                if d.dsem is not None:
                    self.x.free_dsems[d.dq].append(d.dsem)
                    d.dsem = None
                    d.dq = None
        return self.st.__exit__(*a)

    def enter_context(self, cm):
        return self.st.enter_context(cm)


class T:
    def __init__(self, x, shape, dt, name, psum=False, stack=None):
        stack = stack or x.stack
        self.t = x.ps(shape, dt, name, stack) if psum else x.sb(shape, dt, name, stack)
        self.d = Dep(name)
        if isinstance(stack, Scope):
            stack.deps.append(self.d)

    def __getitem__(self, k):
        return self.t[k]


def sbt(x, shape, dt, name, stack=None):
    return T(x, shape, dt, name, False, stack)


def pst(x, shape, dt, name, stack=None):
    return T(x, shape, dt, name, True, stack)


class Consts:
    pass


def make_consts(x):
    c = Consts()
    idf = sbt(x, [128, 128], F32, "idf")
    c.ident = sbt(x, [128, 128], BF16, "ident")
    x.op("pool", lambda e: e.memset(idf[:], 1.0), w=[idf.d])
    x.op("pool", lambda e: e.affine_select(out=idf[:], in_=idf[:], pattern=[[-1, 128]],
                                           compare_op=ALU.is_equal, fill=0.0, base=0,
                                           channel_multiplier=1), r=[idf.d], w=[idf.d])
    x.op("dve", lambda e: e.tensor_copy(c.ident[:], idf[:]), r=[idf.d], w=[c.ident.d])
    c.identf = idf
    ngf = sbt(x, [128, 128], F32, "ngf")
    c.negT = sbt(x, [128, 128], BF16, "negT")
    x.op("pool", lambda e: e.memset(ngf[:], 0.0), w=[ngf.d])
    x.op("pool", lambda e: e.affine_select(out=ngf[:], in_=ngf[:], pattern=[[1, 128]],
                                           compare_op=ALU.is_ge, fill=NEG, base=0,
                                           channel_multiplier=-1), r=[ngf.d], w=[ngf.d])
    x.op("dve", lambda e: e.tensor_copy(c.negT[:], ngf[:]), r=[ngf.d], w=[c.negT.d])
    c.negQ = sbt(x, [128, 128], F32, "negQ")
    x.op("pool", lambda e: e.memset(c.negQ[:], 0.0), w=[c.negQ.d])
    x.op("pool", lambda e: e.affine_select(out=c.negQ[:], in_=c.negQ[:], pattern=[[-1, 128]],
                                           compare_op=ALU.is_ge, fill=-1e30, base=0,
                                           channel_multiplier=1), r=[c.negQ.d], w=[c.negQ.d])
    trf = sbt(x, [128, 128], F32, "trf")
    c.tri = sbt(x, [128, 128], BF16, "tri")
    x.op("pool", lambda e: e.memset(trf[:], 1.0), w=[trf.d])
    x.op("pool", lambda e: e.affine_select(out=trf[:], in_=trf[:], pattern=[[1, 128]],
                                           compare_op=ALU.is_ge, fill=0.0, base=0,
                                           channel_multiplier=-1), r=[trf.d], w=[trf.d])
    x.op("dve", lambda e: e.tensor_copy(c.tri[:], trf[:]), r=[trf.d], w=[c.tri.d])
    c.trif = trf
    c.ones = sbt(x, [128, 128], BF16, "ones")
    x.op("pool", lambda e: e.memset(c.ones[:], 1.0), w=[c.ones.d])
    c.onesf = sbt(x, [128, 128], F32, "onesf")
    x.op("pool", lambda e: e.memset(c.onesf[:], 1.0), w=[c.onesf.d])
    c.nhalf = sbt(x, [128, 64], F32, "nhalf")
    x.op("pool", lambda e: e.memset(c.nhalf[:], -0.5), w=[c.nhalf.d])
    return c


def rsqrt_mean(x, c, out, ss, n, width):
    x.op("dve", lambda e: e.tensor_scalar(out=out[:, 0:width], in0=ss[:, 0:width], scalar1=1.0 / n,
                                          scalar2=EPS, op0=ALU.mult, op1=ALU.add),
         r=[ss.d], w=[out.d])
    x.op("pool", lambda e: e.tensor_tensor(out=out[:, 0:width], in0=out[:, 0:width],
                                           in1=c.nhalf[:, 0:width], op=ALU.pow),
         r=[out.d, c.nhalf.d], w=[out.d])


def bc(ap, shape):
    return ap.to_broadcast(list(shape))


def emit_ada(x, c_ap, adaw_ap, adab_ap, ada_ap, d_ada):
    with Scope(x) as ls:
        cs = sbt(x, [128, 16], F32, "c_sb", ls)
        ca = sbt(x, [128, 16], F32, "c_act", ls)
        brow = sbt(x, [1, 6 * D], F32, "brow", ls)
        arow = sbt(x, [1, 6 * D], F32, "arow", ls)
        wt = [sbt(x, [128, 16, 512], F32, "adaw%d" % i, ls) for i in range(2)]
        pa = [pst(x, [1, 512], F32, "adap%d" % i, ls) for i in range(2)]
        x.dma("sp", cs[:], c_ap.rearrange("(p k) -> p k", k=16), w=[cs.d])
        x.dma("sp", brow[:], adab_ap.rearrange("(o n) -> o n", o=1), w=[brow.d])
        x.op("act", lambda e: e.activation(out=ca[:], in_=cs[:], func=AF.Silu), r=[cs.d], w=[ca.d])
        for blk in range(24):
            i = blk % 2
            x.dma("sp", wt[i][:], adaw_ap[:, blk * 512:(blk + 1) * 512].rearrange("(p k) f -> p k f", k=16),
                  w=[wt[i].d])
            for k in range(16):
                x.op("pe", lambda e: e.matmul(pa[i][:], ca[:, k:k + 1], wt[i][:, k, :],
                                              start=(k == 0), stop=(k == 15)),
                     r=[ca.d, wt[i].d], w=[pa[i].d], inc=(k == 15))
            x.op("dve", lambda e: e.tensor_tensor(out=arow[0:1, blk * 512:(blk + 1) * 512], in0=pa[i][:],
                                                  in1=brow[0:1, blk * 512:(blk + 1) * 512], op=ALU.add),
                 r=[pa[i].d, brow.d], mw=[arow.d])
        x.dma("sp", ada_ap.rearrange("(o n) -> o n", o=1), arow[:], r=[arow.d], w=[d_ada])


def load_cols(x, ls, vec_ap, name, d_src=None):
    t = sbt(x, [128, 16], F32, name, ls)
    x.dma("sp", t[:], vec_ap.rearrange("(k p) -> p k", p=128), r=([d_src] if d_src else []), w=[t.d],
          allow_slow_non_contiguous=True)
    return t


def load_bcast(x, ls, vec_ap, n, name, d_src=None, dt=F32):
    t = sbt(x, [128, n], dt, name, ls)
    x.dma("sp", t[:], vec_ap.rearrange("(o n) -> o n", o=1).partition_broadcast(128),
          r=([d_src] if d_src else []), w=[t.d])
    return t


def emit_mod_cols(x, ls, g_ap, ada_ap, d_ada, scale_idx, shift_idx):
    g = load_cols(x, ls, g_ap, "gcol")
    sc = load_cols(x, ls, ada_ap[scale_idx * D:(scale_idx + 1) * D], "sccol", d_ada)
    sh = load_cols(x, ls, ada_ap[shift_idx * D:(shift_idx + 1) * D], "shcol", d_ada)
    sT = sbt(x, [128, 16], F32, "sT", ls)
    x.op("dve", lambda e: e.scalar_tensor_tensor(out=sT[:], in0=sc[:], scalar=1.0, in1=g[:],
                                                 op0=ALU.add, op1=ALU.mult),
         r=[sc.d, g.d], w=[sT.d])
    return sT, sh


def emit_norm_hT(x, c, x_ap, d_x, ntiles, sT, shT, hT, hT_d, tile_off=0):
    with Scope(x) as ls:
        xt = [sbt(x, [128, D], F32, "xt%d" % i, ls) for i in range(2)]
        xn = [sbt(x, [128, D], BF16, "xn%d" % i, ls) for i in range(2)]
        junk = sbt(x, [128, D], BF16, "junk", ls)
        ss = [sbt(x, [128, 1], F32, "ss%d" % i, ls) for i in range(2)]
        rstd = [sbt(x, [128, 1], F32, "rstd%d" % i, ls) for i in range(2)]
        pt = [pst(x, [128, 512], BF16, "ptn%d" % i, ls) for i in range(2)]
        for t in range(ntiles):
            i = t % 2
            x.dma("sp", xt[i][:], x_ap[t * 128:(t + 1) * 128, :], r=[d_x], w=[xt[i].d])
            x.op("act", lambda e: e.activation(out=junk[:], in_=xt[i][:], func=AF.Square,
                                               accum_out=ss[i][:, 0:1]),
                 r=[xt[i].d], w=[junk.d, ss[i].d])
            rsqrt_mean(x, c, rstd[i], ss[i], D, 1)
            x.op("act", lambda e: e.activation(out=xn[i][:], in_=xt[i][:], func=AF.Copy,
                                               scale=rstd[i][:, 0:1]),
                 r=[xt[i].d, rstd[i].d], w=[xn[i].d])
            for g in range(4):
                p = pt[g % 2]
                for j in range(4):
                    kc = g * 4 + j
                    x.op("pe", lambda e: e.transpose(p[:, j * 128:(j + 1) * 128],
                                                     xn[i][:, kc * 128:(kc + 1) * 128], c.ident[:]),
                         r=[xn[i].d, c.ident.d], w=[p.d], inc=(j == 3))
                for j in range(4):
                    kc = g * 4 + j
                    dst = hT[:, kc, (tile_off + t) * 128:(tile_off + t + 1) * 128]
                    if True:
                        x.op("dve", lambda e: e.tensor_scalar(out=dst, in0=p[:, j * 128:(j + 1) * 128],
                                                              scalar1=sT[:, kc:kc + 1], scalar2=shT[:, kc:kc + 1],
                                                              op0=ALU.mult, op1=ALU.add),
                             r=[p.d, sT.d, shT.d], mw=[hT_d[tile_off + t]])
                    else:
                        x.op("act", lambda e: e.activation(out=dst, in_=p[:, j * 128:(j + 1) * 128],
                                                           func=AF.Identity, scale=sT[:, kc:kc + 1],
                                                           bias=shT[:, kc:kc + 1]),
                             r=[p.d, sT.d, shT.d], mw=[hT_d[tile_off + t]])


def emit_rope_tables(x, ls, pos_ap, invf_ap, half, ntiles):
    n = ntiles * half
    posi = sbt(x, [128, ntiles], I32, "posi", ls)
    posf = sbt(x, [128, ntiles], F32, "posf", ls)
    invf = sbt(x, [128, half], F32, "invf", ls)
    ang = sbt(x, [128, ntiles, half], F32, "ang", ls)
    x.dma("sp", posi[:], pos_ap.rearrange("(t p) -> p t", p=128), w=[posi.d], allow_slow_non_contiguous=True)
    x.dma("sp", invf[:], invf_ap, w=[invf.d])
    x.op("dve", lambda e: e.tensor_copy(posf[:], posi[:]), r=[posi.d], w=[posf.d])
    x.op("dve", lambda e: e.tensor_tensor(out=ang[:], in0=bc(posf[:].unsqueeze(2), [128, ntiles, half]),
                                          in1=bc(invf[:].unsqueeze(1), [128, ntiles, half]), op=ALU.mult),
         r=[posf.d, invf.d], w=[ang.d])
    outs = []
    C1 = 6.28125
    C2 = 2.0 * math.pi - C1
    for nm, shift in (("cos", math.pi / 2), ("sin", 0.0)):
        a = sbt(x, [128, n], F32, nm + "_a", ls)
        ki = sbt(x, [128, n], I32, nm + "_ki", ls)
        kf = sbt(x, [128, n], F32, nm + "_kf", ls)
        m = sbt(x, [128, n], F32, nm + "_m", ls)
        res = sbt(x, [128, ntiles, half], F32, nm + "_t", ls)
        af = ang[:].rearrange("p t h -> p (t h)")
        x.op("dve", lambda e: e.tensor_scalar(out=a[:], in0=af, scalar1=shift, scalar2=None, op0=ALU.add),
             r=[ang.d], w=[a.d])
        x.op("dve", lambda e: e.tensor_scalar(out=kf[:], in0=a[:], scalar1=1.0 / (2 * math.pi), scalar2=None,
                                              op0=ALU.mult), r=[a.d], w=[kf.d])
        x.op("dve", lambda e: e.tensor_copy(ki[:], kf[:]), r=[kf.d], w=[ki.d])
        x.op("dve", lambda e: e.tensor_copy(kf[:], ki[:]), r=[ki.d], w=[kf.d])
        x.op("dve", lambda e: e.scalar_tensor_tensor(out=a[:], in0=kf[:], scalar=-C1, in1=a[:],
                                                     op0=ALU.mult, op1=ALU.add), r=[kf.d, a.d], w=[a.d])
        x.op("dve", lambda e: e.scalar_tensor_tensor(out=a[:], in0=kf[:], scalar=-C2, in1=a[:],
                                                     op0=ALU.mult, op1=ALU.add), r=[kf.d, a.d], w=[a.d])
        x.op("dve", lambda e: e.tensor_scalar(out=m[:], in0=a[:], scalar1=math.pi, scalar2=2 * math.pi,
                                              op0=ALU.is_gt, op1=ALU.mult), r=[a.d], w=[m.d])
        x.op("dve", lambda e: e.tensor_tensor(out=a[:], in0=a[:], in1=m[:], op=ALU.subtract),
             r=[a.d, m.d], w=[a.d])
        x.op("dve", lambda e: e.tensor_scalar(out=m[:], in0=a[:], scalar1=-math.pi, scalar2=2 * math.pi,
                                              op0=ALU.is_lt, op1=ALU.mult), r=[a.d], w=[m.d])
        x.op("dve", lambda e: e.tensor_tensor(out=a[:], in0=a[:], in1=m[:], op=ALU.add),
             r=[a.d, m.d], w=[a.d])
        x.op("dve", lambda e: e.tensor_scalar(out=a[:], in0=a[:], scalar1=-3.1415925, scalar2=3.1415925,
                                              op0=ALU.max, op1=ALU.min), r=[a.d], w=[a.d])
        x.op("act", lambda e: e.activation(out=res[:].rearrange("p t h -> p (t h)"), in_=a[:], func=AF.Sin),
             r=[a.d], w=[res.d])
        outs.append(res)
    return outs[0], outs[1]


def emit_qk_post(x, c, ls_tiles, ps, ncols, gdim, gain, half, cos, sin, t, out_bf):
    ng = ncols // gdim
    qn, sq, ssq, rs, t1, t2, t3, t4 = ls_tiles
    pv = ps[:, 0:ncols].rearrange("p (g d) -> p g d", d=gdim)
    qv = qn[:, 0:ncols].rearrange("p (g d) -> p g d", d=gdim)
    if gain is not None:
        x.op("act", lambda e: e.activation(out=sq[:, 0:ncols], in_=ps[:, 0:ncols], func=AF.Square),
             r=[ps.d], w=[sq.d])
        x.op("dve", lambda e: e.tensor_reduce(out=ssq[:, 0:ng], in_=sq[:, 0:ncols].rearrange("p (g d) -> p g d", d=gdim),
                                              axis=AX.X, op=ALU.add), r=[sq.d], w=[ssq.d])
        rsqrt_mean(x, c, rs, ssq, gdim, ng)
        x.op("dve", lambda e: e.tensor_tensor(out=qv, in0=pv, in1=bc(rs[:, 0:ng].unsqueeze(2), [128, ng, gdim]),
                                              op=ALU.mult), r=[ps.d, rs.d], w=[qn.d])
        x.op("pool", lambda e: e.tensor_tensor(out=qv, in0=qv, in1=bc(gain[:, 0:gdim].unsqueeze(1), [128, ng, gdim]),
                                               op=ALU.mult), r=[qn.d, gain.d], w=[qn.d])
    else:
        x.op("act", lambda e: e.activation(out=qn[:, 0:ncols], in_=ps[:, 0:ncols], func=AF.Copy),
             r=[ps.d], w=[qn.d])
    if half:
        x1 = qv[:, :, 0:half]
        x2 = qv[:, :, half:2 * half]
        cb = bc(cos[:, t, :].unsqueeze(1), [128, ng, half])
        sb_ = bc(sin[:, t, :].unsqueeze(1), [128, ng, half])
        tv = [tt[:, 0:ng * half].rearrange("p (g h) -> p g h", h=half) for tt in (t1, t2, t3, t4)]
        x.op("dve", lambda e: e.tensor_tensor(out=tv[0], in0=x1, in1=cb, op=ALU.mult), r=[qn.d, cos.d], w=[t1.d])
        x.op("dve", lambda e: e.tensor_tensor(out=tv[1], in0=x2, in1=sb_, op=ALU.mult), r=[qn.d, sin.d], w=[t2.d])
        x.op("pool", lambda e: e.tensor_tensor(out=tv[2], in0=x2, in1=cb, op=ALU.mult), r=[qn.d, cos.d], w=[t3.d])
        x.op("pool", lambda e: e.tensor_tensor(out=tv[3], in0=x1, in1=sb_, op=ALU.mult), r=[qn.d, sin.d], w=[t4.d])
        x.op("dve", lambda e: e.tensor_tensor(out=x1, in0=tv[0], in1=tv[1], op=ALU.subtract),
             r=[t1.d, t2.d], w=[qn.d])
        x.op("pool", lambda e: e.tensor_tensor(out=x2, in0=tv[2], in1=tv[3], op=ALU.add),
             r=[t3.d, t4.d, qn.d], w=[qn.d])
    x.op("act", lambda e: e.activation(out=out_bf[:, 0:ncols], in_=qn[:, 0:ncols], func=AF.Copy),
         r=[qn.d], w=[out_bf.d])


def emit_qk_post2(x, c, ls_tiles, ps, gdim, gain, half, cos, sin, t0, out_bf):
    ng = 512 // gdim
    ng2 = 2 * ng
    qn, sq, ssq, rs, t1, t2, t3, t4 = ls_tiles
    psf = ps[:].rearrange("p a c -> p (a c)")
    pv = ps[:].rearrange("p a (g d) -> p (a g) d", d=gdim)
    qv = qn[:, 0:1024].rearrange("p (g d) -> p g d", d=gdim)
    if gain is not None:
        x.op("act", lambda e: e.activation(out=sq[:, 0:1024], in_=psf, func=AF.Square), r=[ps.d], w=[sq.d])
        x.op("dve", lambda e: e.tensor_reduce(out=ssq[:, 0:ng2], in_=sq[:, 0:1024].rearrange("p (g d) -> p g d", d=gdim),
                                              axis=AX.X, op=ALU.add), r=[sq.d], w=[ssq.d])
        rsqrt_mean(x, c, rs, ssq, gdim, ng2)
        x.op("dve", lambda e: e.tensor_tensor(out=qv, in0=pv, in1=bc(rs[:, 0:ng2].unsqueeze(2), [128, ng2, gdim]),
                                              op=ALU.mult), r=[ps.d, rs.d], w=[qn.d])
        x.op("pool", lambda e: e.tensor_tensor(out=qv, in0=qv, in1=bc(gain[:, 0:gdim].unsqueeze(1), [128, ng2, gdim]),
                                               op=ALU.mult), r=[qn.d, gain.d], w=[qn.d])
    else:
        x.op("act", lambda e: e.activation(out=qn[:, 0:1024], in_=psf, func=AF.Copy), r=[ps.d], w=[qn.d])
    q4 = qn[:, 0:1024].rearrange("p (a g d) -> p a g d", a=2, d=gdim)
    x1 = q4[:, :, :, 0:half]
    x2 = q4[:, :, :, half:2 * half]
    cb = bc(cos[:, t0:t0 + 2, :].unsqueeze(2), [128, 2, ng, half])
    sb_ = bc(sin[:, t0:t0 + 2, :].unsqueeze(2), [128, 2, ng, half])
    tv = [tt[:, 0:ng2 * half].rearrange("p (a g h) -> p a g h", a=2, h=half) for tt in (t1, t2, t3, t4)]
    x.op("dve", lambda e: e.tensor_tensor(out=tv[0], in0=x1, in1=cb, op=ALU.mult), r=[qn.d, cos.d], w=[t1.d])
    x.op("dve", lambda e: e.tensor_tensor(out=tv[1], in0=x2, in1=sb_, op=ALU.mult), r=[qn.d, sin.d], w=[t2.d])
    x.op("pool", lambda e: e.tensor_tensor(out=tv[2], in0=x2, in1=cb, op=ALU.mult), r=[qn.d, cos.d], w=[t3.d])
    x.op("pool", lambda e: e.tensor_tensor(out=tv[3], in0=x1, in1=sb_, op=ALU.mult), r=[qn.d, sin.d], w=[t4.d])
    x.op("dve", lambda e: e.tensor_tensor(out=x1, in0=tv[0], in1=tv[1], op=ALU.subtract),
         r=[t1.d, t2.d], w=[qn.d])
    x.op("pool", lambda e: e.tensor_tensor(out=x2, in0=tv[2], in1=tv[3], op=ALU.add),
         r=[t3.d, t4.d, qn.d], w=[qn.d])
    x.op("act", lambda e: e.activation(out=out_bf[:, 0:1024], in_=qn[:, 0:1024], func=AF.Copy),
         r=[qn.d], w=[out_bf.d])


def alloc_qk_tiles(x, ls):
    qn = sbt(x, [128, 1024], F32, "qn", ls)
    sq = sbt(x, [128, 1024], F32, "sq", ls)
    ssq = sbt(x, [128, 16], F32, "ssq", ls)
    rs = sbt(x, [128, 16], F32, "rs", ls)
    ts = [sbt(x, [128, 128], F32, "rt%d" % i, ls) for i in range(4)]
    return (qn, sq, ssq, rs, ts[0], ts[1], ts[2], ts[3])


class ProjCtx:
    pass


def emit_proj(x, c, w_ap, hT, hT_d, ntiles, blocks):
    with Scope(x) as ls:
        wb = [sbt(x, [128, 16, 512], BF16, "wb%d" % i, ls) for i in range(2)]
        pp = [pst(x, [128, 512], F32, "pp%d" % i, ls) for i in range(2)]
        pp2 = ([pst(x, [128, 2, 512], F32, "pp2_%d" % i, ls) for i in range(2)]
               if any(b_[2] == "tok2" for b_ in blocks) else None)
        n = 0
        for bi, (col0, ncols, mode, handler) in enumerate(blocks):
            wt = wb[bi % 2]
            x.dma("pool", wt[:, :, 0:ncols], w_ap[:, col0:col0 + ncols].rearrange("(k p) f -> p k f", p=128),
                  w=[wt.d])
            if mode == "tok":
                for t in range(ntiles):
                    ps = pp[n % 2]
                    n += 1
                    for kc in range(16):
                        x.op("pe", lambda e: e.matmul(ps[:, 0:ncols], hT[:, kc, t * 128:(t + 1) * 128],
                                                      wt[:, kc, 0:ncols], start=(kc == 0), stop=(kc == 15)),
                             r=[hT_d[t], wt.d], w=[ps.d], inc=(kc == 15))
                    handler(t, ps)
            elif mode == "tok2":
                for t in range(0, ntiles, 2):
                    ps = pp2[n % 2]
                    n += 1
                    for a_ in range(2):
                        for kc in range(16):
                            x.op("pe", lambda e: e.matmul(ps[:, a_, 0:ncols], hT[:, kc, (t + a_) * 128:(t + a_ + 1) * 128],
                                                          wt[:, kc, 0:ncols], start=(kc == 0), stop=(kc == 15)),
                                 r=[hT_d[t + a_], wt.d], w=[ps.d], inc=(kc == 15))
                    handler(t, ps)
            else:
                for fc in range(ncols // 128):
                    for tb in range(ntiles // 4):
                        ps = pp[n % 2]
                        n += 1
                        for kc in range(16):
                            x.op("pe", lambda e: e.matmul(ps[:, :], wt[:, kc, fc * 128:(fc + 1) * 128],
                                                          hT[:, kc, tb * 512:(tb + 1) * 512],
                                                          start=(kc == 0), stop=(kc == 15)),
                                 r=hT_d[tb * 4:tb * 4 + 4] + [wt.d], w=[ps.d], inc=(kc == 15))
                        handler(fc, tb, ps)


def emit_LA(x, c, dram, kind, dm=False):
    x_in = dram("x_in", [TOK, D], F32, "ExternalInput")
    c_in = dram("c_in", [D], F32, "ExternalInput")
    pos = dram("pos", [TOK], I32, "ExternalInput")
    adaw = dram("ada_w", [D, 6 * D], F32, "ExternalInput")
    adab = dram("ada_b", [6 * D], F32, "ExternalInput")
    g1 = dram("g1", [D], F32, "ExternalInput")
    ada = dram("ada", [6 * D], F32, "ExternalOutput")
    FIN = {0: 6144, 1: 10304, 2: 4176}[kind]
    w_in = dram("w_in", [D, FIN], F32, "ExternalInput")
    outs = []
    with Scope(x) as st:
        d_ada = x.mkdep("ada")
        d_x = x.mkdep("x")
        emit_ada(x, c_in, adaw, adab, ada, d_ada)
        hT = x.sb([128, 16, TOK], BF16, "hT")
        hT_d = [x.mkdep("hT%d" % t) for t in range(NT)]
        with Scope(x) as ls:
            sT, shT = emit_mod_cols(x, ls, g1, ada, d_ada, 1, 0)
            emit_norm_hT(x, c, x_in, d_x, NT, sT, shT, hT, hT_d)
        ls = st
        if kind == 0:
            invf = dram("invf", [128, 8], F32, "ExternalInput")
            gq = dram("gq", [64], F32, "ExternalInput")
            gk = dram("gk", [64], F32, "ExternalInput")
            qT = dram("qT", [16, 128, TOK], BF16, "ExternalOutput")
            kT = dram("kT", [16, 128, TOK], BF16, "ExternalOutput")
            v = dram("v", [2, TOK, 1024] if dm else [TOK, 2048], BF16, "ExternalOutput")
            outs = [x.mkdep("qT"), x.mkdep("kT"), x.mkdep("v")]
            cos, sin = emit_rope_tables(x, ls, pos, invf, 8, NT)
            gqb = load_bcast(x, ls, gq, 64, "gqb")
            gkb = load_bcast(x, ls, gk, 64, "gkb")
            qk_tiles = alloc_qk_tiles(x, ls)
            qbf = sbt(x, [128, 1024], BF16, "qbf", ls)
            stage = [sbt(x, [128, 4, TOK], BF16, "stage%d" % i, ls) for i in range(2)]
            ptr = [pst(x, [128, 512], BF16, "ptr%d" % i, ls) for i in range(2)]
            vst = [sbt(x, [128, 512], BF16, "vst%d" % i, ls) for i in range(2)]
            blocks = []
            cnt = [0]
            for which in range(2):
                for hb in range(4):
                    def handler(t, ps, which=which, hb=hb):
                        sg = stage[(which * 4 + hb) % 2]
                        emit_qk_post2(x, c, qk_tiles, ps, 64, gqb if which == 0 else gkb, 8, cos, sin, t, qbf)
                        for a_ in range(2):
                            p = ptr[cnt[0] % 2]
                            cnt[0] += 1
                            for j in range(4):
                                x.op("pe", lambda e: e.transpose(p[:, j * 128:(j + 1) * 128],
                                                                 qbf[:, a_ * 512 + j * 128:a_ * 512 + (j + 1) * 128], c.ident[:]),
                                     r=[qbf.d, c.ident.d], w=[p.d], inc=(j == 3))
                            x.op("dve", lambda e: e.tensor_copy(sg[:, :, (t + a_) * 128:(t + a_ + 1) * 128],
                                                                p[:, :].rearrange("p (a b) -> p a b", a=4)),
                                 r=[p.d], mw=[sg.d])
                        if t == NT - 2:
                            dst = (qT if which == 0 else kT)[hb * 4:(hb + 1) * 4, :, :].rearrange("h p t -> p h t")
                            x.dma("sp", dst, sg[:], r=[sg.d], mw=[outs[which]])
                    blocks.append((which * 2048 + hb * 512, 512, "tok2", handler))
            for vb in range(4):
                def vhandler(t, ps, vb=vb):
                    vs = vst[cnt[0] % 2]
                    cnt[0] += 1
                    x.op("act", lambda e: e.activation(out=vs[:], in_=ps[:, :], func=AF.Copy), r=[ps.d], w=[vs.d])
                    vdst = (v[vb // 2, t * 128:(t + 1) * 128, (vb % 2) * 512:(vb % 2 + 1) * 512] if dm
                            else v[t * 128:(t + 1) * 128, vb * 512:(vb + 1) * 512])
                    x.dma("sp", vdst, vs[:], r=[vs.d], mw=[outs[2]])
                blocks.append((4096 + vb * 512, 512, "tok", vhandler))
            emit_proj(x, c, w_in, hT, hT_d, NT, blocks)
        elif kind == 1:
            blocks = la_handlers_ssd(x, c, ls, dram, outs, dm)
            emit_proj(x, c, w_in, hT, hT_d, NT, blocks)
        else:
            blocks = la_handlers_dsa(x, c, ls, dram, outs, pos, dm)
            emit_proj(x, c, w_in, hT, hT_d, NT, blocks)


def emit_LB_DA(x, c, dram, lambda_init):
    NH = 8
    qT = dram("qT", [NH, 128, S], BF16, "ExternalInput")
    kT = dram("kT", [NH, 128, S], BF16, "ExternalInput")
    v = dram("v", [S, NH * 128], BF16, "ExternalInput")
    lam4 = dram("lam4", [4, 64], F32, "ExternalInput")
    gqk = dram("gqk", [2, 64], F32, "ExternalInput")
    subg = dram("subg", [128], F32, "ExternalInput")
    o = dram("o", [S, NH * 128], BF16, "ExternalOutput")
    with Scope(x) as st:
        d_o = x.mkdep("o")
        ls = st
        lt = [load_bcast(x, ls, lam4[i], 64, "lam%d" % i) for i in range(4)]
        gt = [load_bcast(x, ls, gqk[i], 64, "gqk%d" % i) for i in range(2)]
        gs = load_bcast(x, ls, subg, 128, "gs")
        x.op("dve", lambda e: e.tensor_scalar(out=gs[:], in0=gs[:], scalar1=1.0 - lambda_init, scalar2=None,
                                              op0=ALU.mult), r=[gs.d], w=[gs.d])
        pr = sbt(x, [128, 64], F32, "pr")
        s12 = sbt(x, [128, 2], F32, "s12")
        e12 = sbt(x, [128, 2], F32, "e12")
        neglam = sbt(x, [128, 1], F32, "neglam")
        for i in range(2):
            x.op("dve", lambda e: e.tensor_tensor(out=pr[:], in0=lt[2 * i][:], in1=lt[2 * i + 1][:], op=ALU.mult),
                 r=[lt[2 * i].d, lt[2 * i + 1].d], w=[pr.d])
            x.op("dve", lambda e: e.tensor_reduce(out=s12[:, i:i + 1], in_=pr[:], axis=AX.X, op=ALU.add),
                 r=[pr.d], w=[s12.d])
        x.op("act", lambda e: e.activation(out=e12[:], in_=s12[:], func=AF.Exp), r=[s12.d], w=[e12.d])
        x.op("dve", lambda e: e.scalar_tensor_tensor(out=neglam[:], in0=e12[:, 1:2], scalar=-lambda_init,
                                                     in1=e12[:, 0:1], op0=ALU.add, op1=ALU.subtract),
             r=[e12.d], w=[neglam.d])
        gm = sbt(x, [128, 2], F32, "gm")
        negC = sbt(x, [128, 1], F32, "negC")
        for i in range(2):
            x.op("dve", lambda e: e.tensor_reduce(out=gm[:, i:i + 1], in_=gt[i][:], axis=AX.X, op=ALU.max,
                                                  apply_absolute_value=True), r=[gt[i].d], w=[gm.d])
        x.op("dve", lambda e: e.scalar_tensor_tensor(out=negC[:], in0=gm[:, 0:1], scalar=-8.0, in1=gm[:, 1:2],
                                                     op0=ALU.mult, op1=ALU.mult), r=[gm.d], w=[negC.d])
        kt = [sbt(x, [128, S], BF16, "kt%d" % i) for i in range(2)]
        qt = [sbt(x, [128, S], BF16, "qt%d" % i) for i in range(2)]
        va = [sbt(x, [128, 32, 129], BF16, "va%d" % i) for i in range(2)]
        for i in range(2):
            x.op("pool", lambda e: e.memset(va[i][:, :, 128:129], 1.0), w=[va[i].d])
        pss = [pst(x, [128, 512], F32, "pss%d" % i) for i in range(4)]
        pso = pst(x, [128, 8, 256], F32, "pso")
        pt = [sbt(x, [128, 512], BF16, "pt%d" % i) for i in range(4)]
        R = sbt(x, [128, 8], F32, "R")
        osbs = [sbt(x, [128, 8, 129], F32, "osb%d" % i) for i in range(2)]
        tmp = [sbt(x, [128, 128], F32, "tmp%d" % i) for i in range(2)]
        of = sbt(x, [128, 4, 128], F32, "of")
        sq = sbt(x, [128, 512], F32, "sqo")
        ssq = sbt(x, [128, 4], F32, "ssqo")
        rs = sbt(x, [128, 4], F32, "rso")
        ob = [sbt(x, [128, 4, 128], BF16, "ob%d" % i) for i in range(2)]
        zr = sbt(x, [128, 512], BF16, "zr")
        x.op("pool", lambda e: e.memset(zr[:], 0.0), w=[zr.d])
        psob = pso[:].rearrange("p a b -> p (a b)")
        n = 0
        nq = 0
        def load_head(h):
            hi_ = h % 2
            x.dma("sp", kt[hi_][:], kT[h], w=[kt[hi_].d])
            x.dma("sp", qt[hi_][:], qT[h], w=[qt[hi_].d])
            x.dma("sp", va[hi_][:, :, 0:128], v[:, h * 128:(h + 1) * 128].rearrange("(kb p) d -> p kb d", p=128),
                  mw=[va[hi_].d])

        load_head(0)
        for h in range(NH):
            hi = h % 2
            if h + 1 < NH:
                load_head(h + 1)
            steps = [(qc, kb, comp) for qc in range(8) for kb in range(4 * qc + 4) for comp in range(2)]

            def scores(idx):
                qc, kb, comp = steps[idx]
                jmin = max(0, kb - 4 * qc)
                diag = kb >= 4 * qc
                N = 512 - 128 * jmin
                q0 = qc * 512 + 128 * jmin
                ps = pss[idx % 4]
                ptt = pt[idx % 4]
                pr_ = slice(comp * 64, (comp + 1) * 64)
                lhs = kt[hi][pr_, kb * 128:(kb + 1) * 128]
                if diag:
                    x.op("pe", lambda e: e.matmul(ps[:, 0:128], lhs, qt[hi][pr_, q0:q0 + 128],
                                                  start=True, stop=False),
                         r=[kt[hi].d, qt[hi].d], w=[ps.d], inc=False)
                    x.op("pe", lambda e: e.matmul(ps[:, 0:128], c.ident[:], c.negT[:],
                                                  start=False, stop=True),
                         r=[c.ident.d, c.negT.d], w=[ps.d], inc=(N == 128))
                    if N > 128:
                        x.op("pe", lambda e: e.matmul(ps[:, 128:N], lhs, qt[hi][pr_, q0 + 128:q0 + N],
                                                      start=True, stop=True),
                             r=[kt[hi].d, qt[hi].d], w=[ps.d], inc=True)
                else:
                    x.op("pe", lambda e: e.matmul(ps[:, 0:N], lhs, qt[hi][pr_, q0:q0 + N],
                                                  start=True, stop=True),
                         r=[kt[hi].d, qt[hi].d], w=[ps.d], inc=True)
                x.op("act", lambda e: e.activation(out=ptt[:, 0:N], in_=ps[:, 0:N], func=AF.Exp,
                                                   scale=0.125, bias=negC[:, 0:1]),
                     r=[ps.d, negC.d], w=[ptt.d])

            def pv(idx):
                qc, kb, comp = steps[idx]
                jmin = max(0, kb - 4 * qc)
                ptt = pt[idx % 4]
                for j in range(jmin, 4):
                    last = (kb == 4 * qc + j)
                    x.op("pe", lambda e: e.matmul(pso[:, comp * 4 + j, 0:129],
                                                  ptt[:, (j - jmin) * 128:(j - jmin + 1) * 128],
                                                  va[hi][:, kb, :], start=False,
                                                  stop=(last and j % 2 == 1)),
                         r=[ptt.d, va[hi].d], w=[pso.d], inc=(j == 3))

            def epilogue(qc):
                nonlocal nq
                osb = osbs[nq % 2]
                x.op("act", lambda e: e.activation(out=osb[:], in_=pso[:, :, 0:129], func=AF.Copy), r=[pso.d], w=[osb.d])
                x.op("dve", lambda e: e.reciprocal(out=R[:], in_=osb[:, :, 128:129].rearrange("p a b -> p (a b)")),
                     r=[osb.d], w=[R.d])
                x.op("dve", lambda e: e.tensor_scalar(out=R[:, 4:8], in0=R[:, 4:8], scalar1=neglam[:, 0:1],
                                                      scalar2=None, op0=ALU.mult), r=[R.d, neglam.d], w=[R.d])
                for j in range(4):
                    tm = tmp[j % 2]
                    x.op("act", lambda e: e.activation(out=tm[:], in_=osb[:, 4 + j, 0:128], func=AF.Copy,
                                                       scale=R[:, 4 + j:5 + j]), r=[osb.d, R.d], w=[tm.d])
                    x.op("dve", lambda e: e.scalar_tensor_tensor(out=of[:, j, :], in0=osb[:, j, 0:128],
                                                                 scalar=R[:, j:j + 1], in1=tm[:],
                                                                 op0=ALU.mult, op1=ALU.add),
                         r=[osb.d, R.d, tm.d], mw=[of.d])
                ofl = of[:].rearrange("p a b -> p (a b)")
                x.op("act", lambda e: e.activation(out=sq[:], in_=ofl, func=AF.Square), r=[of.d], w=[sq.d])
                x.op("dve", lambda e: e.tensor_reduce(out=ssq[:], in_=sq[:].rearrange("p (a b) -> p a b", a=4),
                                                      axis=AX.X, op=ALU.add), r=[sq.d], w=[ssq.d])
                rsqrt_mean(x, c, rs, ssq, 128, 4)
                x.op("dve", lambda e: e.tensor_tensor(out=of[:], in0=of[:], in1=bc(rs[:].unsqueeze(2), [128, 4, 128]),
                                                      op=ALU.mult), r=[of.d, rs.d], w=[of.d])
                obb = ob[nq % 2]
                nq += 1
                x.op("pool", lambda e: e.tensor_tensor(out=obb[:], in0=of[:], in1=bc(gs[:].unsqueeze(1), [128, 4, 128]),
                                                       op=ALU.mult), r=[of.d, gs.d], w=[obb.d])
                x.dma("sp", o[qc * 512:(qc + 1) * 512, h * 128:(h + 1) * 128].rearrange("(j p) d -> p j d", p=128),
                      obb[:], r=[obb.d], mw=[d_o])

            scores(0)
            scores(1)
            for idx, (qc, kb, comp) in enumerate(steps):
                if kb == 0 and comp == 0:
                    for bnk in range(4):
                        x.op("pe", lambda e: e.matmul(psob[:, bnk * 512:(bnk + 1) * 512], zr[:, 0:128], zr[:],
                                                      start=True, stop=False), r=[zr.d], w=[pso.d], inc=False)
                if idx + 2 < len(steps):
                    scores(idx + 2)
                pv(idx)
                if kb == 4 * qc + 3 and comp == 1:
                    epilogue(qc)


def emit_outproj(x, c, o_ap, d_o, FO, wout_ap, x_ap, d_x, gate_b, xmid_ap, d_xmid):
    KO = FO // 128
    TB = 1024
    with Scope(x) as ls:
        oT = sbt(x, [128, KO, TB], BF16, "oT", ls)
        oT_d = [x.mkdep("oT%d" % t) for t in range(TB // 128)]
        ls.deps.extend(oT_d)
        ot = [sbt(x, [128, FO], BF16, "ot%d" % i, ls) for i in range(2)]
        pt = [pst(x, [128, 512], BF16, "pto%d" % i, ls) for i in range(2)]
        wb = [sbt(x, [128, KO, 512], BF16, "wo%d" % i, ls) for i in range(2)]
        pp = [pst(x, [128, 512], F32, "ppo%d" % i, ls) for i in range(2)]
        tm = [sbt(x, [128, 512], F32, "tmo%d" % i, ls) for i in range(2)]
        xt = [sbt(x, [128, 512], F32, "xto%d" % i, ls) for i in range(2)]
        n = 0
        nw = 0
        for tb in range(TOK // TB):
            for t in range(TB // 128):
                i = t % 2
                tok0 = tb * TB + t * 128
                x.dma("sp", ot[i][:], o_ap[tok0:tok0 + 128, :], r=[d_o], w=[ot[i].d])
                for g in range(KO // 4):
                    p = pt[g % 2]
                    for j in range(4):
                        kc = g * 4 + j
                        x.op("pe", lambda e: e.transpose(p[:, j * 128:(j + 1) * 128], ot[i][:, kc * 128:(kc + 1) * 128],
                                                         c.ident[:]), r=[ot[i].d, c.ident.d], w=[p.d], inc=(j == 3))
                    dst = oT[:, g * 4:(g + 1) * 4, t * 128:(t + 1) * 128]
                    src = p[:, :].rearrange("p (a b) -> p a b", a=4)
                    if g % 2 == 0:
                        x.op("dve", lambda e: e.tensor_copy(dst, src), r=[p.d], mw=[oT_d[t]])
                    else:
                        x.op("act", lambda e: e.activation(out=dst, in_=src, func=AF.Copy), r=[p.d], mw=[oT_d[t]])
            for cb in range(4):
                wt = wb[nw % 2]
                nw += 1
                x.dma("pool", wt[:], wout_ap[:, cb * 512:(cb + 1) * 512].rearrange("(k p) f -> p k f", p=128), w=[wt.d])
                for t in range(TB // 128):
                    tok0 = tb * TB + t * 128
                    ps = pp[n % 2]
                    tmm = tm[n % 2]
                    xtt = xt[n % 2]
                    n += 1
                    x.dma("sp", xtt[:], x_ap[tok0:tok0 + 128, cb * 512:(cb + 1) * 512], r=[d_x], w=[xtt.d])
                    for kc in range(KO):
                        x.op("pe", lambda e: e.matmul(ps[:], oT[:, kc, t * 128:(t + 1) * 128], wt[:, kc, :],
                                                      start=(kc == 0), stop=(kc == KO - 1)),
                             r=[oT_d[t], wt.d], w=[ps.d], inc=(kc == KO - 1))
                    x.op("dve", lambda e: e.tensor_tensor(out=tmm[:], in0=ps[:], in1=gate_b[:, cb * 512:(cb + 1) * 512],
                                                          op=ALU.mult), r=[ps.d, gate_b.d], w=[tmm.d])
                    x.op("pool", lambda e: e.tensor_tensor(out=tmm[:], in0=tmm[:], in1=xtt[:], op=ALU.add),
                         r=[tmm.d, xtt.d], w=[tmm.d])
                    x.dma("act", xmid_ap[tok0:tok0 + 128, cb * 512:(cb + 1) * 512], tmm[:], r=[tmm.d], mw=[d_xmid])


def emit_ffn(x, c, xmid_ap, d_xmid, sT, shT, gate_b, wgu_ap, wd_ap, xout_ap, d_xout, act_ap):
    NFC = FFN // 128
    d_act = x.mkdep("act_d")
    with Scope(x) as ls:
        h2T = sbt(x, [128, 16, TOK], BF16, "h2T", ls)
        h2_d = [x.mkdep("h2T%d" % t) for t in range(NT)]
        ls.deps.extend(h2_d)
        emit_norm_hT(x, c, xmid_ap, d_xmid, NT, sT, shT, h2T, h2_d)
        wg = [sbt(x, [128, 16, 256], BF16, "wg%d" % i, ls) for i in range(2)]
        wu = [sbt(x, [128, 16, 256], BF16, "wu%d" % i, ls) for i in range(2)]
        psg = [pst(x, [128, 512], F32, "psg%d" % i, ls) for i in range(2)]
        psu = [pst(x, [128, 512], F32, "psu%d" % i, ls) for i in range(2)]
        sg = [sbt(x, [128, 512], F32, "sg%d" % i, ls) for i in range(2)]
        ab = [sbt(x, [128, 4, 128], BF16, "ab%d" % i, ls) for i in range(3)]
        n = 0
        for blk in range(FFN // 256):
            wgt = wg[blk % 2]
            wut = wu[blk % 2]
            x.dma("pool", wgt[:], wgu_ap[:, blk * 256:(blk + 1) * 256].rearrange("(k p) f -> p k f", p=128), w=[wgt.d])
            x.dma("pool", wut[:], wgu_ap[:, FFN + blk * 256:FFN + (blk + 1) * 256].rearrange("(k p) f -> p k f", p=128),
                  w=[wut.d])
            for fl in range(2):
                fc = blk * 2 + fl
                for tb in range(TOK // 512):
                    pg = psg[n % 2]
                    pu = psu[n % 2]
                    sgg = sg[n % 2]
                    abb = ab[n % 3]
                    n += 1
                    hd = h2_d[tb * 4:tb * 4 + 4]
                    for kc in range(16):
                        x.op("pe", lambda e: e.matmul(pg[:], wgt[:, kc, fl * 128:(fl + 1) * 128],
                                                      h2T[:, kc, tb * 512:(tb + 1) * 512],
                                                      start=(kc == 0), stop=(kc == 15)),
                             r=hd + [wgt.d], w=[pg.d], inc=(kc == 15))
                    for kc in range(16):
                        x.op("pe", lambda e: e.matmul(pu[:], wut[:, kc, fl * 128:(fl + 1) * 128],
                                                      h2T[:, kc, tb * 512:(tb + 1) * 512],
                                                      start=(kc == 0), stop=(kc == 15)),
                             r=hd + [wut.d], w=[pu.d], inc=(kc == 15))
                    x.op("act", lambda e: e.activation(out=sgg[:], in_=pg[:], func=AF.Silu), r=[pg.d], w=[sgg.d])
                    x.op("dve", lambda e: e.tensor_tensor(out=abb[:].rearrange("p a b -> p (a b)"), in0=pu[:], in1=sgg[:],
                                                          op=ALU.mult), r=[pu.d, sgg.d], w=[abb.d])
                    x.dma("sp", act_ap[tb * 4:(tb + 1) * 4, :, fc, :].rearrange("t p k -> p t k"), abb[:],
                          r=[abb.d], mw=[d_act])
    with Scope(x) as ls:
        wdA = sbt(x, [128, 22, 512], BF16, "wdA", ls)
        wdB = sbt(x, [128, 22, 512], BF16, "wdB", ls)
        at = [sbt(x, [128, NFC, 128], BF16, "at%d" % i, ls) for i in range(3)]
        psd = [pst(x, [128, 512], F32, "psd%d" % i, ls) for i in range(2)]
        tm = [sbt(x, [128, 512], F32, "tmf%d" % i, ls) for i in range(2)]
        xt = [sbt(x, [128, 512], F32, "xtf%d" % i, ls) for i in range(2)]
        nd = 0
        for cb in range(4):
            x.dma("pool", wdA[:], wd_ap[0:22 * 128, cb * 512:(cb + 1) * 512].rearrange("(k p) f -> p k f", p=128),
                  w=[wdA.d])
            x.dma("pool", wdB[:], wd_ap[22 * 128:44 * 128, cb * 512:(cb + 1) * 512].rearrange("(k p) f -> p k f", p=128),
                  w=[wdB.d])
            for t in range(NT):
                tok0 = t * 128
                ps = psd[nd % 2]
                tmm = tm[nd % 2]
                xtt = xt[nd % 2]
                att = at[nd % 3]
                if nd == 0:
                    x.dma("sp", att[:], act_ap[0], r=[d_act], w=[att.d])
                if nd + 1 < 4 * NT:
                    x.dma("sp", at[(nd + 1) % 3][:], act_ap[(t + 1) % NT], r=[d_act], w=[at[(nd + 1) % 3].d])
                nd += 1
                x.dma("sp", xtt[:], xmid_ap[tok0:tok0 + 128, cb * 512:(cb + 1) * 512], r=[d_xmid], w=[xtt.d])
                for fc in range(NFC):
                    wt = wdA if fc < 22 else wdB
                    x.op("pe", lambda e: e.matmul(ps[:], att[:, fc, :], wt[:, fc % 22, :],
                                                  start=(fc == 0), stop=(fc == NFC - 1)),
                         r=[att.d, wt.d], w=[ps.d], inc=(fc == NFC - 1 or fc == 21))
                x.op("dve", lambda e: e.tensor_tensor(out=tmm[:], in0=ps[:], in1=gate_b[:, cb * 512:(cb + 1) * 512],
                                                      op=ALU.mult), r=[ps.d, gate_b.d], w=[tmm.d])
                x.op("pool", lambda e: e.tensor_tensor(out=tmm[:], in0=tmm[:], in1=xtt[:], op=ALU.add),
                     r=[tmm.d, xtt.d], w=[tmm.d])
                x.dma("act", xout_ap[tok0:tok0 + 128, cb * 512:(cb + 1) * 512], tmm[:], r=[tmm.d], mw=[d_xout])


def emit_LC(x, c, dram, FO):
    x_in = dram("x_in", [TOK, D], F32, "ExternalInput")
    o_in = dram("o_in", [TOK, FO], BF16, "ExternalInput")
    ada = dram("ada", [6 * D], F32, "ExternalInput")
    g2 = dram("g2", [D], F32, "ExternalInput")
    w_out = dram("w_out", [FO, D], F32, "ExternalInput")
    wgu = dram("wgu", [D, 2 * FFN], F32, "ExternalInput")
    wd = dram("wd", [FFN, D], F32, "ExternalInput")
    x_mid = dram("x_mid", [TOK, D], F32, "Internal")
    act_d = dram("act_d", [NT, 128, FFN // 128, 128], BF16, "Internal")
    x_out = dram("x_out", [TOK, D], F32, "ExternalOutput")
    with Scope(x) as st:
        d_none = x.mkdep("in")
        d_xmid = x.mkdep("xmid")
        d_xout = x.mkdep("xout")
        with Scope(x) as ls:
            g1b = load_bcast(x, ls, ada[2 * D:3 * D], D, "g1b")
            emit_outproj(x, c, o_in, d_none, FO, w_out, x_in, d_none, g1b, x_mid, d_xmid)
        with Scope(x) as ls:
            g2b = load_bcast(x, ls, ada[5 * D:6 * D], D, "g2b")
            sT, shT = emit_mod_cols(x, ls, g2, ada, d_none, 4, 3)
            emit_ffn(x, c, x_mid, d_xmid, sT, shT, g2b, wgu, wd, x_out, d_xout, act_d)


def la_handlers_ssd(x, c, ls, dram, outs_holder, dm=False):
    z = dram("z", [2, TOK, 2048] if dm else [TOK, 4096], BF16, "ExternalOutput")
    xbcT = dram("xbcT", [2, 3072, TOK] if dm else [6144, TOK], BF16, "ExternalOutput")
    dtr = dram("dtr", [2, TOK, 32] if dm else [TOK, 64], F32, "ExternalOutput")
    outs = [x.mkdep("z"), x.mkdep("xbcT"), x.mkdep("dtr")]
    outs_holder.extend(outs)
    zst = [sbt(x, [128, 512], BF16, "zst%d" % i, ls) for i in range(2)]
    fst = [sbt(x, [128, 512], BF16, "fst%d" % i, ls) for i in range(2)]
    dst_ = [sbt(x, [128, 64], F32, "dst%d" % i, ls) for i in range(2)]
    cnt = [0]
    blocks = []
    for zb in range(8):
        def zh(t, ps, zb=zb):
            s_ = zst[cnt[0] % 2]
            cnt[0] += 1
            x.op("act", lambda e: e.activation(out=s_[:], in_=ps[:, :], func=AF.Copy), r=[ps.d], w=[s_.d])
            zdst = (z[zb // 4, t * 128:(t + 1) * 128, (zb % 4) * 512:(zb % 4 + 1) * 512] if dm
                    else z[t * 128:(t + 1) * 128, zb * 512:(zb + 1) * 512])
            x.dma("sp", zdst, s_[:], r=[s_.d], mw=[outs[0]])
        blocks.append((zb * 512, 512, "tok", zh))
    for xb in range(12):
        def xh(fc, tb, ps, xb=xb):
            s_ = fst[cnt[0] % 2]
            cnt[0] += 1
            x.op("act", lambda e: e.activation(out=s_[:], in_=ps[:, :], func=AF.Copy), r=[ps.d], w=[s_.d])
            if dm:
                dd, rb = ((xb // 4, (xb % 4) * 512) if xb < 8 else ((xb - 8) % 2, 2048 + ((xb - 8) // 2) * 512))
                xdst = xbcT[dd, rb + fc * 128:rb + fc * 128 + 128, tb * 512:(tb + 1) * 512]
            else:
                r0 = xb * 512 + fc * 128
                xdst = xbcT[r0:r0 + 128, tb * 512:(tb + 1) * 512]
            x.dma("sp", xdst, s_[:], r=[s_.d], mw=[outs[1]])
        blocks.append((4096 + xb * 512, 512, "feat", xh))

    def dh(t, ps):
        s_ = dst_[cnt[0] % 2]
        cnt[0] += 1
        x.op("act", lambda e: e.activation(out=s_[:], in_=ps[:, 0:64], func=AF.Copy), r=[ps.d], w=[s_.d])
        if dm:
            for dd in range(2):
                x.dma("sp", dtr[dd, t * 128:(t + 1) * 128, :], s_[:, dd * 32:(dd + 1) * 32], r=[s_.d], mw=[outs[2]])
        else:
            x.dma("sp", dtr[t * 128:(t + 1) * 128, :], s_[:], r=[s_.d], mw=[outs[2]])
    blocks.append((10240, 64, "tok", dh))
    return blocks


def emit_LB_SSD(x, c, dram):
    NHh = 32
    NCH = 24
    raw = dram("raw", [NCH * 128, S], F32, "ExternalInput")
    convw = dram("convw", [4, NCH * 128], F32, "ExternalInput")
    convb = dram("convb", [NCH * 128], F32, "ExternalInput")
    dtr = dram("dtr", [S, NHh], F32, "ExternalInput")
    hp = dram("hp", [3, NHh], F32, "ExternalInput")
    z = dram("z", [S, 2048], BF16, "ExternalInput")
    ng = dram("ng", [2048], F32, "ExternalInput")
    tokd = dram("tokd", [S, 2560], BF16, "Internal")
    featd = dram("featd", [1024, S], BF16, "Internal")
    y = dram("y", [S, 2048], BF16, "ExternalOutput")
    with Scope(x) as st:
        d_in = x.mkdep("in")
        d_tok = x.mkdep("tokd")
        d_feat = x.mkdep("featd")
        d_y = x.mkdep("y")
        with Scope(x) as ls:
            cw = sbt(x, [128, 4, NCH], F32, "cw", ls)
            cb_ = sbt(x, [128, NCH], F32, "cb", ls)
            for j in range(4):
                x.dma("sp", cw[:, j, :], convw[j].rearrange("(k p) -> p k", p=128), mw=[cw.d],
                      allow_slow_non_contiguous=True)
            x.dma("sp", cb_[:], convb.rearrange("(k p) -> p k", p=128), w=[cb_.d], allow_slow_non_contiguous=True)
            rw = [sbt(x, [128, S + 3], F32, "rw%d" % i, ls) for i in range(2)]
            for i in range(2):
                x.op("pool", lambda e: e.memset(rw[i][:, 0:3], 0.0), w=[rw[i].d])
            acc = sbt(x, [128, S], F32, "acc", ls)
            sil = [sbt(x, [128, S], BF16, "sil%d" % i, ls) for i in range(2)]
            ptc = [pst(x, [128, 512], BF16, "ptc%d" % i, ls) for i in range(2)]
            stg = [sbt(x, [128, 4, 128], BF16, "stg%d" % i, ls) for i in range(2)]
            n = 0
            x.dma("sp", rw[0][:, 3:3 + S], raw[0:128, :], r=[d_in], mw=[rw[0].d])
            for cc in range(NCH):
                r_ = rw[cc % 2]
                sl_ = sil[cc % 2]
                if cc + 1 < NCH:
                    x.dma("sp", rw[(cc + 1) % 2][:, 3:3 + S], raw[(cc + 1) * 128:(cc + 2) * 128, :], r=[d_in],
                          mw=[rw[(cc + 1) % 2].d])
                x.op("dve", lambda e: e.tensor_scalar(out=acc[:], in0=r_[:, 3:3 + S], scalar1=cw[:, 3, cc:cc + 1],
                                                      scalar2=cb_[:, cc:cc + 1], op0=ALU.mult, op1=ALU.add),
                     r=[r_.d, cw.d, cb_.d], w=[acc.d])
                for j in range(3):
                    x.op("dve", lambda e: e.scalar_tensor_tensor(out=acc[:], in0=r_[:, j:j + S],
                                                                 scalar=cw[:, j, cc:cc + 1], in1=acc[:],
                                                                 op0=ALU.mult, op1=ALU.add),
                         r=[r_.d, cw.d, acc.d], w=[acc.d])
                x.op("act", lambda e: e.activation(out=sl_[:], in_=acc[:], func=AF.Silu), r=[acc.d], w=[sl_.d])
                if cc >= 16:
                    x.dma("sp", featd[(cc - 16) * 128:(cc - 15) * 128, :], sl_[:], r=[sl_.d], mw=[d_feat])
                if cc < 20:
                    for tg in range(8):
                        p = ptc[n % 2]
                        sg_ = stg[n % 2]
                        n += 1
                        for j in range(4):
                            tt = tg * 4 + j
                            x.op("pe", lambda e: e.transpose(p[:, j * 128:(j + 1) * 128], sl_[:, tt * 128:(tt + 1) * 128],
                                                             c.ident[:]), r=[sl_.d, c.ident.d], w=[p.d], inc=(j == 3))
                        if n % 2 == 0:
                            x.op("dve", lambda e: e.tensor_copy(sg_[:], p[:, :].rearrange("p (a b) -> p a b", a=4)),
                                 r=[p.d], w=[sg_.d])
                        else:
                            x.op("act", lambda e: e.activation(out=sg_[:], in_=p[:, :].rearrange("p (a b) -> p a b", a=4),
                                                               func=AF.Copy), r=[p.d], w=[sg_.d])
                        x.dma("sp", tokd[tg * 512:(tg + 1) * 512, cc * 128:(cc + 1) * 128].rearrange("(j p) c -> p j c", p=128),
                              sg_[:], r=[sg_.d], mw=[d_tok])
        with Scope(x) as ls:
            hb = [load_bcast(x, ls, hp[i], NHh, "hp%d" % i) for i in range(3)]
            dtb_b, alog_b, dsk_b = hb
            a_b = sbt(x, [128, NHh], F32, "a_b", ls)
            x.op("act", lambda e: e.activation(out=a_b[:], in_=alog_b[:], func=AF.Exp), r=[alog_b.d], w=[a_b.d])
            x.op("dve", lambda e: e.tensor_scalar(out=a_b[:], in0=a_b[:], scalar1=-1.0, scalar2=None, op0=ALU.mult),
                 r=[a_b.d], w=[a_b.d])
            ngb = load_bcast(x, ls, ng, 2048, "ngb")
            sel = sbt(x, [32, NHh, 128], F32, "sel", ls)
            x.op("pool", lambda e: e.memset(sel[:], 1.0), w=[sel.d])
            x.op("pool", lambda e: e.affine_select(out=sel[:], in_=sel[:], pattern=[[-1, NHh], [0, 128]],
                                                   compare_op=ALU.is_equal, fill=0.0, base=0, channel_multiplier=1),
                 r=[sel.d], w=[sel.d])
            St = [sbt(x, [128, 512], F32, "St%d" % g, ls) for g in range(4)]
            Sb = [sbt(x, [128, 512], BF16, "Sb%d" % g, ls) for g in range(4)]
            for g in range(4):
                x.op("pool", lambda e: e.memset(St[g][:], 0.0), w=[St[g].d])
                x.op("pool", lambda e: e.memset(Sb[g][:], 0.0), w=[Sb[g].d])
            xs_t = [sbt(x, [128, 2560], BF16, "xs_t%d" % i, ls) for i in range(2)]
            bct = [sbt(x, [128, 8, 128], BF16, "bct%d" % i, ls) for i in range(2)]
            dtt = [sbt(x, [128, NHh], F32, "dtt%d" % i, ls) for i in range(2)]
            zt = [sbt(x, [128, 2048], BF16, "zt%d" % i, ls) for i in range(2)]
            f = lambda nm, w_: sbt(x, [128, w_], F32, nm, ls)
            dtb, ab, ee, dt_, dta, acum, nacum, alast, eac, wend, decay = [f(nm, NHh) for nm in
                ("dtb", "ab", "ee", "dt_", "dta", "acum", "nacum", "alast", "eac", "wend", "decay")]
            acT = sbt(x, [32, 128], F32, "acT", ls)
            xdt = sbt(x, [128, 2048], BF16, "xdt", ls)
            xde = sbt(x, [128, 2048], BF16, "xde", ls)
            cbm = [sbt(x, [128, 128], F32, "cbm%d" % g, ls) for g in range(4)]
            Eh = [sbt(x, [128, 128], F32, "Eh%d" % i, ls) for i in range(4)]
            Mh = [sbt(x, [128, 128], BF16, "Mh%d" % i, ls) for i in range(4)]
            yf = sbt(x, [128, 2048], F32, "yf", ls)
            t1 = sbt(x, [128, 2048], F32, "t1", ls)
            sqy = sbt(x, [128, 2048], F32, "sqy", ls)
            ssy = sbt(x, [128, 4], F32, "ssy", ls)
            rsy = sbt(x, [128, 4], F32, "rsy", ls)
            yb = [sbt(x, [128, 2048], BF16, "yb%d" % i, ls) for i in range(2)]
            p_small = pst(x, [128, 512], F32, "p_small", ls)
            p_cb = pst(x, [128, 512], F32, "p_cb", ls)
            p_G = [pst(x, [128, 512], F32, "p_G%d" % i, ls) for i in range(2)]
            p_y = [pst(x, [128, 512], F32, "p_y%d" % i, ls) for i in range(2)]
            p_i = pst(x, [128, 512], F32, "p_i", ls)
            p_s = pst(x, [128, 512], F32, "p_s", ls)
            def load_chunk(ck_):
                i_ = ck_ % 2
                t0_ = ck_ * 128
                x.dma("sp", xs_t[i_][:], tokd[t0_:t0_ + 128, :], r=[d_tok], w=[xs_t[i_].d])
                x.dma("sp", bct[i_][:], featd[:, t0_:t0_ + 128].rearrange("(g p) t -> p g t", p=128), r=[d_feat],
                      w=[bct[i_].d])
                x.dma("sp", dtt[i_][:], dtr[t0_:t0_ + 128, :], r=[d_in], w=[dtt[i_].d])
                x.dma("sp", zt[i_][:], z[t0_:t0_ + 128, :], r=[d_in], w=[zt[i_].d])

            load_chunk(0)
            for ck in range(S // 128):
                i = ck % 2
                t0 = ck * 128
                xt_ = xs_t[i]
                bc_ = bct[i]
                if ck + 1 < S // 128:
                    load_chunk(ck + 1)
                x.op("dve", lambda e: e.tensor_tensor(out=dtb[:], in0=dtt[i][:], in1=dtb_b[:], op=ALU.add),
                     r=[dtt[i].d, dtb_b.d], w=[dtb.d])
                x.op("dve", lambda e: e.scalar_tensor_tensor(out=ab[:], in0=dtb[:], scalar=-1.0, in1=dtb[:],
                                                             op0=ALU.mult, op1=ALU.max), r=[dtb.d], w=[ab.d])
                x.op("act", lambda e: e.activation(out=ee[:], in_=ab[:], func=AF.Exp, scale=-1.0), r=[ab.d], w=[ee.d])
                x.op("act", lambda e: e.activation(out=ee[:], in_=ee[:], func=AF.Ln, bias=1.0), r=[ee.d], w=[ee.d])
                x.op("dve", lambda e: e.scalar_tensor_tensor(out=dt_[:], in0=dtb[:], scalar=0.0, in1=ee[:],
                                                             op0=ALU.max, op1=ALU.add), r=[dtb.d, ee.d], w=[dt_.d])
                x.op("dve", lambda e: e.tensor_tensor(out=dta[:], in0=dt_[:], in1=a_b[:], op=ALU.mult),
                     r=[dt_.d, a_b.d], w=[dta.d])
                x.op("pe", lambda e: e.matmul(p_small[:, 0:32], c.trif[:], dta[:], start=True, stop=True),
                     r=[c.trif.d, dta.d], w=[p_small.d])
                x.op("pe", lambda e: e.matmul(p_small[:, 32:64], c.onesf[:], dta[:], start=True, stop=True),
                     r=[c.onesf.d, dta.d], w=[p_small.d])
                x.op("dve", lambda e: e.tensor_copy(acum[:], p_small[:, 0:32]), r=[p_small.d], w=[acum.d])
                x.op("dve", lambda e: e.tensor_scalar(out=nacum[:], in0=p_small[:, 0:32], scalar1=-1.0, scalar2=None,
                                                      op0=ALU.mult), r=[p_small.d], w=[nacum.d])
                x.op("dve", lambda e: e.tensor_copy(alast[:], p_small[:, 32:64]), r=[p_small.d], w=[alast.d])
                x.op("act", lambda e: e.activation(out=eac[:], in_=acum[:], func=AF.Exp), r=[acum.d], w=[eac.d])
                x.op("act", lambda e: e.activation(out=decay[:], in_=alast[:], func=AF.Exp), r=[alast.d], w=[decay.d])
                x.op("dve", lambda e: e.tensor_tensor(out=wend[:], in0=alast[:], in1=acum[:], op=ALU.subtract),
                     r=[alast.d, acum.d], w=[wend.d])
                x.op("act", lambda e: e.activation(out=wend[:], in_=wend[:], func=AF.Exp), r=[wend.d], w=[wend.d])
                x.op("dve", lambda e: e.tensor_tensor(out=wend[:], in0=wend[:], in1=dt_[:], op=ALU.mult),
                     r=[wend.d, dt_.d], w=[wend.d])
                x.op("pe", lambda e: e.matmul(p_small[0:32, 128:256], acum[:], c.identf[:], start=True, stop=True),
                     r=[acum.d, c.identf.d], w=[p_small.d])
                x.op("dve", lambda e: e.tensor_copy(acT[:], p_small[0:32, 128:256]), r=[p_small.d], w=[acT.d])
                xv = xt_[:, 0:2048].rearrange("p (h d) -> p h d", d=64)
                x.op("dve", lambda e: e.tensor_tensor(out=xdt[:].rearrange("p (h d) -> p h d", d=64), in0=xv,
                                                      in1=bc(dt_[:].unsqueeze(2), [128, NHh, 64]), op=ALU.mult),
                     r=[xt_.d, dt_.d], w=[xdt.d])
                x.op("pool", lambda e: e.tensor_tensor(out=xde[:].rearrange("p (h d) -> p h d", d=64), in0=xv,
                                                       in1=bc(wend[:].unsqueeze(2), [128, NHh, 64]), op=ALU.mult),
                     r=[xt_.d, wend.d], w=[xde.d])
                for g in range(4):
                    x.op("pe", lambda e: e.matmul(p_cb[:, g * 128:(g + 1) * 128], bc_[:, g, :], bc_[:, 4 + g, :],
                                                  start=True, stop=True), r=[bc_.d], w=[p_cb.d], inc=(g == 3))
                for g in range(4):
                    if g % 2 == 0:
                        x.op("dve", lambda e: e.tensor_copy(cbm[g][:], p_cb[:, g * 128:(g + 1) * 128]),
                             r=[p_cb.d], w=[cbm[g].d])
                    else:
                        x.op("act", lambda e: e.activation(out=cbm[g][:], in_=p_cb[:, g * 128:(g + 1) * 128], func=AF.Copy),
                             r=[p_cb.d], w=[cbm[g].d])
                for g in range(4):
                    py = p_y[g % 2]
                    x.op("pe", lambda e: e.matmul(p_i[:], bc_[:, 4 + g, :], Sb[g][:], start=True, stop=True),
                         r=[bc_.d, Sb[g].d], w=[p_i.d])
                    def hG(hl):
                        h = g * 8 + hl
                        pgt = p_G[(h % 4) // 2]
                        pgs = slice((h % 2) * 128, (h % 2) * 128 + 128)
                        x.op("pe", lambda e: e.matmul(pgt[:, pgs], sel[:, h, :], acT[:], start=True, stop=False),
                             r=[sel.d, acT.d], w=[pgt.d], inc=False)
                        x.op("pe", lambda e: e.matmul(pgt[:, pgs], c.ident[:], c.negT[:], start=False, stop=True),
                             r=[c.ident.d, c.negT.d], w=[pgt.d])
                        eh = Eh[h % 4]
                        mh = Mh[h % 4]
                        x.op("act", lambda e: e.activation(out=eh[:], in_=pgt[:, pgs], func=AF.Exp,
                                                           bias=nacum[:, h:h + 1]), r=[pgt.d, nacum.d], w=[eh.d])
                        x.op("dve", lambda e: e.tensor_tensor(out=mh[:], in0=eh[:], in1=cbm[g][:], op=ALU.mult),
                             r=[eh.d, cbm[g].d], w=[mh.d])

                    def hY(hl):
                        h = g * 8 + hl
                        mh = Mh[h % 4]
                        x.op("pe", lambda e: e.matmul(py[:, hl * 64:(hl + 1) * 64], mh[:], xdt[:, h * 64:(h + 1) * 64],
                                                      start=True, stop=True), r=[mh.d, xdt.d], w=[py.d])

                    hG(0)
                    hG(1)
                    for hl in range(8):
                        if hl + 2 < 8:
                            hG(hl + 2)
                        hY(hl)
                    gs_ = slice(g * 512, (g + 1) * 512)
                    x.op("dve", lambda e: e.tensor_tensor(out=t1[:, gs_].rearrange("p (h d) -> p h d", d=64),
                                                          in0=p_i[:].rearrange("p (h d) -> p h d", d=64),
                                                          in1=bc(eac[:, g * 8:(g + 1) * 8].unsqueeze(2), [128, 8, 64]),
                                                          op=ALU.mult), r=[p_i.d, eac.d], mw=[t1.d])
                    x.op("dve", lambda e: e.tensor_tensor(out=yf[:, gs_], in0=py[:], in1=t1[:, gs_], op=ALU.add),
                         r=[py.d, t1.d], mw=[yf.d])
                    x.op("pe", lambda e: e.matmul(p_s[:], xt_[:, 2048 + g * 128:2048 + (g + 1) * 128], xde[:, gs_],
                                                  start=True, stop=True), r=[xt_.d, xde.d], w=[p_s.d])
                    x.op("pool", lambda e: e.tensor_tensor(out=St[g][:].rearrange("p (h d) -> p h d", d=64),
                                                           in0=St[g][:].rearrange("p (h d) -> p h d", d=64),
                                                           in1=bc(decay[:, g * 8:(g + 1) * 8].unsqueeze(2), [128, 8, 64]),
                                                           op=ALU.mult), r=[St[g].d, decay.d], w=[St[g].d])
                    x.op("dve", lambda e: e.tensor_tensor(out=St[g][:], in0=p_s[:], in1=St[g][:], op=ALU.add),
                         r=[p_s.d, St[g].d], w=[St[g].d])
                    x.op("act", lambda e: e.activation(out=Sb[g][:], in_=St[g][:], func=AF.Copy),
                         r=[St[g].d], w=[Sb[g].d])
                x.op("pool", lambda e: e.tensor_tensor(out=t1[:].rearrange("p (h d) -> p h d", d=64), in0=xv,
                                                       in1=bc(dsk_b[:].unsqueeze(2), [128, NHh, 64]), op=ALU.mult),
                     r=[xt_.d, dsk_b.d, yf.d], w=[t1.d])
                x.op("dve", lambda e: e.tensor_tensor(out=yf[:], in0=yf[:], in1=t1[:], op=ALU.add),
                     r=[yf.d, t1.d], w=[yf.d])
                x.op("act", lambda e: e.activation(out=t1[:], in_=zt[i][:], func=AF.Silu), r=[zt[i].d, yf.d], w=[t1.d])
                x.op("dve", lambda e: e.tensor_tensor(out=yf[:], in0=yf[:], in1=t1[:], op=ALU.mult),
                     r=[yf.d, t1.d], w=[yf.d])
                x.op("act", lambda e: e.activation(out=sqy[:], in_=yf[:], func=AF.Square), r=[yf.d], w=[sqy.d])
                x.op("dve", lambda e: e.tensor_reduce(out=ssy[:], in_=sqy[:].rearrange("p (g d) -> p g d", g=4),
                                                      axis=AX.X, op=ALU.add), r=[sqy.d], w=[ssy.d])
                rsqrt_mean(x, c, rsy, ssy, 512, 4)
                x.op("dve", lambda e: e.tensor_tensor(out=yf[:].rearrange("p (g d) -> p g d", g=4),
                                                      in0=yf[:].rearrange("p (g d) -> p g d", g=4),
                                                      in1=bc(rsy[:].unsqueeze(2), [128, 4, 512]), op=ALU.mult),
                     r=[yf.d, rsy.d], w=[yf.d])
                x.op("pool", lambda e: e.tensor_tensor(out=yb[i][:], in0=yf[:], in1=ngb[:], op=ALU.mult),
                     r=[yf.d, ngb.d], w=[yb[i].d])
                x.dma("sp", y[t0:t0 + 128, :], yb[i][:], r=[yb[i].d], mw=[d_y])


def la_handlers_dsa(x, c, ls, dram, outs_holder, pos, dm=False):
    invf16 = dram("invf16", [128, 16], F32, "ExternalInput")
    invf8 = dram("invf8", [128, 8], F32, "ExternalInput")
    gq = dram("gq", [128], F32, "ExternalInput")
    gk = dram("gk", [128], F32, "ExternalInput")
    gi = dram("gi", [64], F32, "ExternalInput")
    qT = dram("qT", [2, 16, 128, 8, 128] if dm else [16, 128, TOK], BF16, "ExternalOutput")
    kT = dram("kT", [4, 128, TOK], BF16, "ExternalOutput")
    v = dram("v", [TOK, 512], BF16, "ExternalOutput")
    qiT = dram("qiT", [2, 8, 128, 8, 128] if dm else [8, 128, TOK], BF16, "ExternalOutput")
    kiT = dram("kiT", [64, TOK], BF16, "ExternalOutput")
    wi = dram("wi", [2, 8, 128, 16] if dm else [TOK, 16], F32, "ExternalOutput")
    outs = [x.mkdep(n) for n in ("qT", "kT", "v", "qiT", "kiT", "wi")]
    outs_holder.extend(outs)
    cos16, sin16 = emit_rope_tables(x, ls, pos, invf16, 16, NT)
    cos8, sin8 = emit_rope_tables(x, ls, pos, invf8, 8, NT)
    gqb = load_bcast(x, ls, gq, 128, "gqb")
    gkb = load_bcast(x, ls, gk, 128, "gkb")
    gib = load_bcast(x, ls, gi, 64, "gib")
    qk_tiles = alloc_qk_tiles(x, ls)
    qbf = sbt(x, [128, 1024], BF16, "qbf", ls)
    stage = [sbt(x, [128, 4, TOK], BF16, "stage%d" % i, ls) for i in range(2)]
    kist = sbt(x, [64, TOK], BF16, "kist", ls)
    ptr = [pst(x, [128, 512], BF16, "ptr%d" % i, ls) for i in range(2)]
    vst = [sbt(x, [128, 512], BF16, "vst%d" % i, ls) for i in range(2)]
    wst = [sbt(x, [128, 16], F32, "wst%d" % i, ls) for i in range(2)]
    cnt = [0]
    nst = [0]
    blocks = []

    def mk_qk(col0, gdim, gain, half, cos, sin, dst_ap, dep, dmh=None):
        sg = stage[nst[0] % 2]
        nst[0] += 1

        def handler(t, ps):
            emit_qk_post2(x, c, qk_tiles, ps, gdim, gain, half, cos, sin, t, qbf)
            for a_ in range(2):
                p = ptr[cnt[0] % 2]
                cnt[0] += 1
                for j in range(4):
                    x.op("pe", lambda e: e.transpose(p[:, j * 128:(j + 1) * 128],
                                                     qbf[:, a_ * 512 + j * 128:a_ * 512 + (j + 1) * 128], c.ident[:]),
                         r=[qbf.d, c.ident.d], w=[p.d], inc=(j == 3))
                x.op("dve", lambda e: e.tensor_copy(sg[:, :, (t + a_) * 128:(t + a_ + 1) * 128],
                                                    p[:, :].rearrange("p (a b) -> p a b", a=4)), r=[p.d], mw=[sg.d])
            if t == NT - 2:
                if dmh is None:
                    x.dma("sp", dst_ap.rearrange("h p t -> p h t"), sg[:], r=[sg.d], mw=[dep])
                else:
                    tens, h0 = dmh
                    sgv = sg[:].rearrange("p h (k a t) -> p h k a t", a=2, t=128)
                    for a2 in range(2):
                        for hh_ in range(4):
                            x.dma("sp", tens[a2, h0 + hh_].rearrange("d k t -> d k t"), sgv[:, hh_, :, a2, :],
                                  r=[sg.d], mw=[dep])
        blocks.append((col0, 512, "tok2", handler))
    for hb in range(4):
        mk_qk(hb * 512, 128, gqb, 16, cos16, sin16, None if dm else qT[hb * 4:(hb + 1) * 4], outs[0],
              (qT, hb * 4) if dm else None)
    mk_qk(2048, 128, gkb, 16, cos16, sin16, kT[0:4], outs[1])

    def vh(t, ps):
        vs = vst[cnt[0] % 2]
        cnt[0] += 1
        x.op("act", lambda e: e.activation(out=vs[:], in_=ps[:, :], func=AF.Copy), r=[ps.d], w=[vs.d])
        x.dma("sp", v[t * 128:(t + 1) * 128, :], vs[:], r=[vs.d], mw=[outs[2]])
    blocks.append((2560, 512, "tok", vh))
    for qb in range(2):
        mk_qk(3072 + qb * 512, 64, None, 8, cos8, sin8, None if dm else qiT[qb * 4:(qb + 1) * 4], outs[3],
              (qiT, qb * 4) if dm else None)

    def kwh(t, ps):
        emit_qk_post(x, c, qk_tiles, ps, 64, 64, gib, 8, cos8, sin8, t, qbf)
        p = ptr[cnt[0] % 2]
        ws = wst[cnt[0] % 2]
        cnt[0] += 1
        x.op("pe", lambda e: e.transpose(p[0:64, 0:128], qbf[:, 0:64], c.ident[:]), r=[qbf.d, c.ident.d], w=[p.d])
        x.op("dve", lambda e: e.tensor_copy(kist[:, t * 128:(t + 1) * 128], p[0:64, 0:128]), r=[p.d], mw=[kist.d])
        x.op("act", lambda e: e.activation(out=ws[:], in_=ps[:, 64:80], func=AF.Copy, scale=0.25), r=[ps.d], w=[ws.d])
        x.dma("sp", wi[t % 2, t // 2] if dm else wi[t * 128:(t + 1) * 128, :], ws[:], r=[ws.d], mw=[outs[5]])
        if t == NT - 1:
            x.dma("sp", kiT, kist[:], r=[kist.d], mw=[outs[4]])
    blocks.append((4096, 80, "tok", kwh))
    return blocks


def emit_LB_DSA(x, c, dram):
    NS = 16
    qTs = dram("qTs", [NS, 128, 2048], BF16, "ExternalInput")
    qiTs = dram("qiTs", [NS, 128, 1024], BF16, "ExternalInput")
    wis = dram("wis", [NS, 128, 16], F32, "ExternalInput")
    kT = dram("kT", [4, 128, S], BF16, "ExternalInput")
    v = dram("v", [S, 512], BF16, "ExternalInput")
    kiT2 = dram("kiT2", [128, S], BF16, "ExternalInput")
    dmask = dram("dmask", [2, 128, 128], F32, "ExternalInput")
    gqk = dram("gqk", [2, 128], F32, "ExternalInput")
    o = dram("o", [NS * 128, 2048], BF16, "ExternalOutput")
    SCALE = 128 ** -0.5
    with Scope(x) as st:
        d_in = x.mkdep("in")
        d_o = x.mkdep("o")
        ls = st
        gt = [load_bcast(x, ls, gqk[i], 128, "gqk%d" % i) for i in range(2)]
        gm = sbt(x, [128, 2], F32, "gm")
        negC = sbt(x, [128, 1], F32, "negC")
        for i in range(2):
            x.op("dve", lambda e: e.tensor_reduce(out=gm[:, i:i + 1], in_=gt[i][:], axis=AX.X, op=ALU.max,
                                                  apply_absolute_value=True), r=[gt[i].d], w=[gm.d])
        x.op("dve", lambda e: e.scalar_tensor_tensor(out=negC[:], in0=gm[:, 0:1], scalar=-(128 ** 0.5), in1=gm[:, 1:2],
                                                     op0=ALU.mult, op1=ALU.mult), r=[gm.d], w=[negC.d])
        kts = sbt(x, [128, 4, S], BF16, "kts")
        x.dma("sp", kts[:], kT.rearrange("g p t -> p g t"), w=[kts.d])
        va = sbt(x, [128, 32, 4, 129], BF16, "va")
        x.op("pool", lambda e: e.memset(va[:, :, :, 128:129], 1.0), w=[va.d])
        for g in range(4):
            x.dma("sp", va[:, :, g, 0:128], v[:, g * 128:(g + 1) * 128].rearrange("(kb p) d -> p kb d", p=128),
                  mw=[va.d])
        ki2 = sbt(x, [128, S], BF16, "ki2")
        x.dma("sp", ki2[:], kiT2, w=[ki2.d])
        dm = sbt(x, [128, 2, 128], F32, "dm")
        x.dma("sp", dm[:], dmask.rearrange("a p k -> p a k"), w=[dm.d])
        zr = sbt(x, [128, 512], BF16, "zr")
        x.op("pool", lambda e: e.memset(zr[:], 0.0), w=[zr.d])
        acc = sbt(x, [128, S], F32, "acc")
        work = sbt(x, [128, S], F32, "work")
        nb = sbt(x, [128, S], BF16, "nb")
        nbT4 = sbt(x, [128, 32, 4, 128], BF16, "nbT4")
        qs = [sbt(x, [128, 2048], BF16, "qs%d" % i) for i in range(2)]
        qis = [sbt(x, [128, 8, 128], BF16, "qis%d" % i) for i in range(2)]
        wt = [sbt(x, [128, 16], F32, "wt%d" % i) for i in range(2)]
        aw = sbt(x, [128, 16], F32, "aw")
        sgn = sbt(x, [128, 16], F32, "sgn")
        rr = [sbt(x, [128, 512], F32, "rr%d" % i) for i in range(2)]
        m8 = sbt(x, [128, 8], F32, "m8")
        thr = sbt(x, [128, 1], F32, "thr")
        thr0 = sbt(x, [128, 1], F32, "thr0")
        x.op("pool", lambda e: e.memset(thr0[:], -1e29), w=[thr0.d])
        P = [sbt(x, [128, 512], BF16, "P%d" % i) for i in range(2)]
        R = sbt(x, [128, 4], F32, "R")
        ob = [sbt(x, [128, 2048], BF16, "ob%d" % i) for i in range(2)]
        p_ix = [pst(x, [128, 512], F32, "p_ix%d" % i) for i in range(2)]
        p_tr = [pst(x, [128, 512], BF16, "p_tr%d" % i) for i in range(2)]
        p_s = [pst(x, [128, 512], F32, "p_s%d" % i) for i in range(2)]
        p_o = pst(x, [128, 4, 256], F32, "p_o")
        p_ob = p_o[:].rearrange("p a b -> p (a b)")
        nix = 0
        ns_ = 0
        def part_A(i):
            nonlocal nix
            b2 = i % 2
            nkb = 2 * i + 2
            L = nkb * 128
            x.dma("sp", qs[b2][:], qTs[i], r=[d_in], w=[qs[b2].d])
            x.dma("sp", qis[b2][:], qiTs[i].rearrange("p (a t) -> p a t", a=8), r=[d_in], w=[qis[b2].d])
            x.dma("sp", wt[b2][:], wis[i], r=[d_in], w=[wt[b2].d])
            w_ = wt[b2]
            x.op("dve", lambda e: e.scalar_tensor_tensor(out=aw[:], in0=w_[:], scalar=-1.0, in1=w_[:],
                                                         op0=ALU.mult, op1=ALU.max), r=[w_.d], w=[aw.d])
            x.op("dve", lambda e: e.tensor_scalar(out=aw[:], in0=aw[:], scalar1=0.125, scalar2=None, op0=ALU.mult),
                 r=[aw.d], w=[aw.d])
            x.op("dve", lambda e: e.tensor_scalar(out=sgn[:], in0=w_[:], scalar1=0.0, scalar2=2.0,
                                                  op0=ALU.is_ge, op1=ALU.mult), r=[w_.d], w=[sgn.d])
            x.op("dve", lambda e: e.tensor_scalar(out=sgn[:], in0=sgn[:], scalar1=-1.0, scalar2=None, op0=ALU.add),
                 r=[sgn.d], w=[sgn.d])
            for kq in range((L + 511) // 512):
                W = min(512, L - kq * 512)
                cs = slice(kq * 512, kq * 512 + W)
                for hi in range(16):
                    ps = p_ix[nix % 2]
                    r_ = rr[nix % 2]
                    nix += 1
                    pr_ = slice((hi % 2) * 64, (hi % 2) * 64 + 64)
                    x.op("pe", lambda e: e.matmul(ps[:, 0:W], qis[b2][pr_, hi // 2, :], ki2[pr_, cs], start=True, stop=True),
                         r=[qis[b2].d, ki2.d], w=[ps.d])
                    x.op("act", lambda e: e.activation(out=r_[:, 0:W], in_=ps[:, 0:W], func=AF.Relu, scale=aw[:, hi:hi + 1]),
                         r=[ps.d, aw.d], w=[r_.d])
                    if hi == 0:
                        x.op("dve", lambda e: e.tensor_scalar(out=acc[:, cs], in0=r_[:, 0:W], scalar1=sgn[:, 0:1],
                                                              scalar2=None, op0=ALU.mult), r=[r_.d, sgn.d], w=[acc.d])
                    else:
                        x.op("dve", lambda e: e.scalar_tensor_tensor(out=acc[:, cs], in0=r_[:, 0:W], scalar=sgn[:, hi:hi + 1],
                                                                     in1=acc[:, cs], op0=ALU.mult, op1=ALU.add),
                             r=[r_.d, sgn.d, acc.d], w=[acc.d])
            for a in range(2):
                ks = slice((nkb - 2 + a) * 128, (nkb - 1 + a) * 128)
                x.op("dve", lambda e: e.tensor_tensor(out=acc[:, ks], in0=acc[:, ks], in1=dm[:, a, :], op=ALU.add),
                     r=[acc.d, dm.d], w=[acc.d])
            if i >= 1:
                x.op("pool", lambda e: e.tensor_copy(work[:, 0:L], acc[:, 0:L]), r=[acc.d], w=[work.d])
                for rd in range(32):
                    x.op("dve", lambda e: e.max(out=m8[:], in_=work[:, 0:L]), r=[work.d], w=[m8.d])
                    if rd < 31:
                        x.op("dve", lambda e: e.match_replace(out=work[:, 0:L], in_to_replace=m8[:], in_values=work[:, 0:L],
                                                              imm_value=-1e30), r=[m8.d, work.d], w=[work.d])
                x.op("dve", lambda e: e.tensor_copy(thr[:], m8[:, 7:8]), r=[m8.d], w=[thr.d])
                th = thr
            else:
                th = thr0
            x.op("dve", lambda e: e.tensor_scalar(out=nb[:, 0:L], in0=acc[:, 0:L], scalar1=th[:, 0:1], scalar2=NEG,
                                                  op0=ALU.is_lt, op1=ALU.mult), r=[acc.d, th.d], w=[nb.d])

        def part_T(i):
            b2 = i % 2
            nkb = 2 * i + 2
            L = nkb * 128
            for kg in range((nkb + 3) // 4):
                p = p_tr[kg % 2]
                nn = min(4, nkb - kg * 4)
                for j in range(nn):
                    kb = kg * 4 + j
                    x.op("pe", lambda e: e.transpose(p[:, j * 128:(j + 1) * 128], nb[:, kb * 128:(kb + 1) * 128], c.ident[:]),
                         r=[nb.d, c.ident.d], w=[p.d], inc=(j == nn - 1))
                src = p[:, 0:nn * 128].rearrange("p (a b) -> p a b", a=nn)
                x.op("act", lambda e: e.activation(out=nbT4[:, kg * 4:kg * 4 + nn, :, :],
                                                   in_=bc(src.unsqueeze(2), [128, nn, 4, 128]), func=AF.Copy),
                     r=[p.d], w=[nbT4.d])

        def part_C(i):
            b2 = i % 2
            nkb = 2 * i + 2
            L = nkb * 128
            obb = ob[b2]
            asteps = [(g, kb) for g in range(4) for kb in range(nkb)]

            def a_scores(idx):
                g, kb = asteps[idx]
                ps = p_s[idx % 2]
                pp_ = P[idx % 2]
                x.op("pe", lambda e: e.matmul(ps[:], kts[:, g, kb * 128:(kb + 1) * 128],
                                              qs[b2][:, g * 512:(g + 1) * 512], start=True, stop=False),
                     r=[kts.d, qs[b2].d], w=[ps.d], inc=False)
                x.op("pe", lambda e: e.matmul(ps[:], c.ident[:], nbT4[:, kb, :, :].rearrange("p a b -> p (a b)"),
                                              start=False, stop=True), r=[c.ident.d, nbT4.d], w=[ps.d])
                x.op("act", lambda e: e.activation(out=pp_[:], in_=ps[:], func=AF.Exp, scale=SCALE, bias=negC[:, 0:1]),
                     r=[ps.d, negC.d], w=[pp_.d])

            def a_pv(idx):
                g, kb = asteps[idx]
                pp_ = P[idx % 2]
                for r in range(4):
                    x.op("pe", lambda e: e.matmul(p_o[:, r, 0:129], pp_[:, r * 128:(r + 1) * 128], va[:, kb, g, :],
                                                  start=False, stop=(kb == nkb - 1 and r % 2 == 1)),
                         r=[pp_.d, va.d], w=[p_o.d], inc=(r == 3))

            def a_epi(g):
                x.op("dve", lambda e: e.reciprocal(out=R[:], in_=p_o[:, :, 128:129].rearrange("p a b -> p (a b)")),
                     r=[p_o.d], w=[R.d])
                for r in range(4):
                    hh = g * 4 + r
                    x.op("act", lambda e: e.activation(out=obb[:, hh * 128:(hh + 1) * 128], in_=p_o[:, r, 0:128],
                                                       func=AF.Copy, scale=R[:, r:r + 1]), r=[p_o.d, R.d], mw=[obb.d])

            a_scores(0)
            for idx, (g, kb) in enumerate(asteps):
                if kb == 0:
                    for bnk in range(2):
                        x.op("pe", lambda e: e.matmul(p_ob[:, bnk * 512:(bnk + 1) * 512], zr[:, 0:128], zr[:],
                                                      start=True, stop=False), r=[zr.d], w=[p_o.d], inc=False)
                if idx + 1 < len(asteps):
                    a_scores(idx + 1)
                a_pv(idx)
                if kb == nkb - 1:
                    a_epi(g)
            x.dma("sp", o[i * 128:(i + 1) * 128, :], obb[:], r=[obb.d], mw=[d_o])

        part_A(0)
        part_T(0)
        for i in range(NS):
            if i + 1 < NS:
                part_A(i + 1)
            part_C(i)
            if i + 1 < NS:
                part_T(i + 1)


def _standalone(emit, *args):
    nc = bass.Bass("TRN2", target_bir_lowering=False)
    dram = lambda n, s, dt, k: nc.dram_tensor(n, list(s), dt, kind=k).ap()
    with ExitStack() as st:
        x = X(nc, st)
        c = make_consts(x)
        emit(x, c, dram, *args)
        x.global_barrier()
        print(emit.__name__, args, "sems", x.nsem, "cnt", x.cnt)
    return nc


def build_LA(kind):
    return _standalone(emit_LA, kind)


def build_LB_DA(lambda_init):
    return _standalone(emit_LB_DA, lambda_init)


def build_LB_SSD():
    return _standalone(emit_LB_SSD)


def build_LB_DSA():
    return _standalone(emit_LB_DSA)


def build_LC(FO):
    return _standalone(emit_LC, FO)


def _invf_table(half):
    invf = np.power(np.float32(ROPE_THETA), -np.arange(half, dtype=np.float32) / half).astype(np.float32)
    return np.ascontiguousarray(np.broadcast_to(invf[None, :], (128, half))).astype(np.float32)


def _ca(a):
    return np.ascontiguousarray(a)


GROUPS = [[0, 1], [2, 3], [4, 5], [6, 7]]
FIN_K = {0: 6144, 1: 10304, 2: 4176}


def _mk_dram(mapping):
    def dram(n, s, dt, k):
        ap = mapping[n]
        assert [int(v) for v in ap.shape] == [int(v) for v in s], (n, ap.shape, s)
        return ap
    return dram


def build_fused(depth=DEPTH):
    nc = bass.Bass("TRN2", target_bir_lowering=False)
    ext_in = lambda n, s, dt: nc.dram_tensor(n, list(s), dt, kind="ExternalInput").ap()
    internal = lambda n, s, dt: nc.dram_tensor(n, list(s), dt, kind="Internal").ap()
    x_in = ext_in("x_in", [TOK, D], F32)
    c_in = ext_in("c_in", [D], F32)
    pos = ext_in("pos", [TOK], I32)
    rk = ext_in("rk", [1, 1], I32)
    invf8 = ext_in("invf8", [128, 8], F32)
    invf16 = ext_in("invf16", [128, 16], F32)
    dmask = ext_in("dmask", [2, 128, 128], F32)
    x_out = nc.dram_tensor("x_out", [TOK, D], F32, kind="ExternalOutput").ap()
    xres = internal("xres", [TOK, D], F32)
    xmid = internal("xmid", [TOK, D], F32)
    scratch = {}

    def scr(n, s, dt):
        if n not in scratch:
            scratch[n] = internal(n, s, dt)
        return scratch[n]

    with ExitStack() as st:
        x = X(nc, st)
        c = make_consts(x)
        reg = st.enter_context(nc.gpsimd.register("rk"))
        nc.gpsimd.reg_load(reg, rk[0:1, 0:1])
        r = nc.gpsimd.snap(reg, min_val=0, max_val=1)
        d_g = x.mkdep("xchg")
        RS = bass.ds(r, 1)
        CH = 2 * 1024 * 1024
        MAXE = {BF16: 12 * 1024 * 1024 + 4096, F32: 1024 * 1024}
        GBS = {BF16: [], F32: []}
        goff = {BF16: 0, F32: 0}

        def gather(parts, both=False, stage=False):
            a0 = parts[0]
            dt = a0.dtype
            shp = [int(v) for v in a0.shape]
            rowe = int(np.prod(shp[1:]))
            c0 = max(1, min(shp[0], CH // (rowe * mybir.dt.size(dt))))
            while shp[0] % c0:
                c0 -= 1
            nch = shp[0] // c0
            ce = c0 * rowe
            assert nch * 2 * ce <= MAXE[dt], (nch, ce, dt)
            if goff[dt] >= len(GBS[dt]):
                GBS[dt].append(internal("GB%d_%d" % (mybir.dt.size(dt), len(GBS[dt])), [2, MAXE[dt]], dt))
            gb = GBS[dt][goff[dt]]
            goff[dt] += 1
            off = 0
            for d_, a in enumerate(parts):
                for k in range(nch):
                    x.op("pool", lambda e: e.collective_compute(
                        "AllGather", ALU.bypass, replica_groups=GROUPS, ins=[a[k * c0:(k + 1) * c0].opt()],
                        outs=[gb[d_, off + k * 2 * ce:off + (k + 1) * 2 * ce].opt()]), w=[d_g])
            x.global_barrier()
            tot = nch * 2 * ce
            if stage:
                gs = scr("GS%d_%d" % (mybir.dt.size(dt), goff[dt] - 1), [1, MAXE[dt]], dt)
                CP = 4 * 1024 * 1024
                for o_ in range(0, tot, CP):
                    n_ = min(CP, tot - o_)
                    x.dma("pool", gs[:, o_:o_ + n_], (gb[0:1] if both else gb[RS])[:, o_:o_ + n_], mw=[d_g])
                row = gs[:, 0:tot]
            else:
                row = (gb[0:1] if both else gb[RS])[:, 0:tot]
            names = ["e%d" % i_ for i_ in range(len(shp) - 1)]
            kw = {"k": nch, "s": 2, "c": c0}
            kw.update({n_: v_ for n_, v_ in zip(names, shp[1:])})
            return row.rearrange("a (k s c %s) -> (a k) s c %s" % (" ".join(names), " ".join(names)), **kw)

        def cp(dst, src):
            x.dma("pool", dst, src, mw=[d_g])

        for i in range(depth):
            kind, j = i % 3, i // 3
            xsrc = x_in if i == 0 else xres
            xdst = x_out if i == depth - 1 else xres
            sfx = "_%d" % i
            ada_i = internal("ada" + sfx, [6 * D], F32)
            mp = {"x_in": xsrc, "c_in": c_in, "pos": pos, "ada_w": ext_in("ada_w" + sfx, [D, 6 * D], F32),
                  "ada_b": ext_in("ada_b" + sfx, [6 * D], F32), "g1": ext_in("g1" + sfx, [D], F32), "ada": ada_i,
                  "w_in": ext_in("w_in" + sfx, [D, FIN_K[kind]], F32)}
            goff[BF16] = goff[F32] = 0
            if kind == 0:
                A = {"qT": scr("A_qT", [16, 128, TOK], BF16), "kT": scr("A_kT", [16, 128, TOK], BF16),
                     "v": scr("A_v", [2, TOK, 1024], BF16)}
                mp.update(A)
                mp.update({"invf": invf8, "gq": ext_in("gq" + sfx, [64], F32), "gk": ext_in("gk" + sfx, [64], F32)})
                emit_LA(x, c, _mk_dram(mp), kind, True)
                x.global_barrier()
                Gq = gather([A["qT"][0:8], A["qT"][8:16]])
                Gk = gather([A["kT"][0:8], A["kT"][8:16]])
                Gv = gather([A["v"][0], A["v"][1]])
                x.global_barrier()
                L_qT = scr("L_qT", [8, 128, S], BF16)
                L_kT = scr("L_kT", [8, 128, S], BF16)
                L_v = scr("L_v", [S, 1024], BF16)
                for s_ in range(2):
                    cs = slice(s_ * TOK, (s_ + 1) * TOK)
                    for kk in range(2):
                        cp(L_qT[kk * 4:(kk + 1) * 4, :, cs], Gq[kk, s_])
                        cp(L_kT[kk * 4:(kk + 1) * 4, :, cs], Gk[kk, s_])
                        cp(L_v[s_ * TOK + kk * 1024:s_ * TOK + (kk + 1) * 1024, :], Gv[kk, s_])
                x.global_barrier()
                B_o = scr("B_o", [S, 1024], BF16)
                li = 0.8 - 0.6 * math.exp(-0.3 * i)
                emit_LB_DA(x, c, _mk_dram({"qT": L_qT, "kT": L_kT, "v": L_v, "lam4": ext_in("lam4" + sfx, [4, 64], F32),
                                           "gqk": ext_in("gqk" + sfx, [2, 64], F32),
                                           "subg": ext_in("subg" + sfx, [128], F32), "o": B_o}), float(li))
                x.global_barrier()
                goff[BF16] = 0
                Go = gather([B_o[0:TOK], B_o[TOK:S]])
                x.global_barrier()
                FO = 2048
                L_o = scr("L_o", [TOK, 2048], BF16)
                for hh in range(2):
                    for kk in range(2):
                        cp(L_o[kk * 1024:(kk + 1) * 1024, hh * 1024:(hh + 1) * 1024], Go[kk, hh])
            elif kind == 1:
                A = {"z": scr("A_z", [2, TOK, 2048], BF16), "xbcT": scr("A_xbcT", [2, 3072, TOK], BF16),
                     "dtr": scr("A_dtr", [2, TOK, 32], F32)}
                mp.update(A)
                emit_LA(x, c, _mk_dram(mp), kind, True)
                x.global_barrier()
                Gz = gather([A["z"][0], A["z"][1]])
                Gx = gather([A["xbcT"][0], A["xbcT"][1]])
                Gd = gather([A["dtr"][0], A["dtr"][1]])
                x.global_barrier()
                L_raw = scr("L_raw", [3072, S], F32)
                L_z = scr("L_z", [S, 2048], BF16)
                L_dt = scr("L_dt", [S, 32], F32)
                for s_ in range(2):
                    cs = slice(s_ * TOK, (s_ + 1) * TOK)
                    for kk in range(6):
                        cp(L_raw[kk * 512:(kk + 1) * 512, cs], Gx[kk, s_])
                    for kk in range(4):
                        cp(L_z[s_ * TOK + kk * 512:s_ * TOK + (kk + 1) * 512, :], Gz[kk, s_])
                    cp(L_dt[cs, :], Gd[0, s_])
                x.global_barrier()
                B_y = scr("B_y", [S, 2048], BF16)
                emit_LB_SSD(x, c, _mk_dram({"raw": L_raw, "convw": ext_in("convw" + sfx, [4, 3072], F32),
                                            "convb": ext_in("convb" + sfx, [3072], F32), "dtr": L_dt,
                                            "hp": ext_in("hp" + sfx, [3, 32], F32), "z": L_z,
                                            "ng": ext_in("ng" + sfx, [2048], F32),
                                            "tokd": scr("tokd", [S, 2560], BF16), "featd": scr("featd", [1024, S], BF16),
                                            "y": B_y}))
                x.global_barrier()
                goff[BF16] = 0
                Gy = gather([B_y[0:TOK], B_y[TOK:S]])
                x.global_barrier()
                FO = 4096
                L_o = scr("L_o4", [TOK, 4096], BF16)
                for gh in range(2):
                    for kk in range(4):
                        cp(L_o[kk * 512:(kk + 1) * 512, gh * 2048:(gh + 1) * 2048], Gy[kk, gh])
            else:
                A = {"qT": scr("D_qT", [2, 16, 128, 8, 128], BF16), "kT": scr("D_kT", [4, 128, TOK], BF16),
                     "v": scr("D_v", [TOK, 512], BF16), "qiT": scr("D_qiT", [2, 8, 128, 8, 128], BF16),
                     "kiT": scr("D_kiT", [64, TOK], BF16), "wi": scr("D_wi", [2, 8, 128, 16], F32)}
                mp.update(A)
                mp.update({"invf16": invf16, "invf8": invf8, "gq": ext_in("gq" + sfx, [128], F32),
                           "gk": ext_in("gk" + sfx, [128], F32), "gi": ext_in("gi" + sfx, [64], F32)})
                emit_LA(x, c, _mk_dram(mp), kind, True)
                x.global_barrier()
                Gq = gather([A["qT"][0], A["qT"][1]], stage=True)
                Gqi = gather([A["qiT"][0], A["qiT"][1]], stage=True)
                Gw = gather([A["wi"][0], A["wi"][1]], stage=True)
                Gk = gather([A["kT"]], both=True)
                Gv = gather([A["v"]], both=True)
                Gki = gather([A["kiT"]], both=True)
                x.global_barrier()
                L_qTs = scr("L_qTs", [16, 128, 2048], BF16)
                L_qiTs = scr("L_qiTs", [16, 128, 1024], BF16)
                L_wis = scr("L_wis", [16, 128, 16], F32)
                L_kT = scr("L_kT4", [4, 128, S], BF16)
                L_v = scr("L_v4", [S, 512], BF16)
                L_ki = scr("L_ki2", [128, S], BF16)
                with nc.allow_non_contiguous_dma(reason="tile gathers"):
                    for sl_ in range(16):
                        s_, k_ = sl_ // 8, sl_ % 8
                        for kk in range(2):
                            cp(L_qTs[sl_][:, kk * 1024:(kk + 1) * 1024].rearrange("d (h t) -> d h t", h=8),
                               Gq[kk, s_][:, :, k_, :].rearrange("h d t -> d h t"))
                        cp(L_qiTs[sl_].rearrange("d (h t) -> d h t", h=8),
                           Gqi[0, s_][:, :, k_, :].rearrange("h d t -> d h t"))
                        cp(L_wis[sl_], Gw[0, s_][k_])
                for s_ in range(2):
                    cs = slice(s_ * TOK, (s_ + 1) * TOK)
                    cp(L_kT[:, :, cs], Gk[0, s_])
                    cp(L_v[cs, :], Gv[0, s_])
                    for dup in range(2):
                        cp(L_ki[dup * 64:(dup + 1) * 64, cs], Gki[0, s_])
                x.global_barrier()
                B_o2 = scr("B_o2", [TOK, 2048], BF16)
                emit_LB_DSA(x, c, _mk_dram({"qTs": L_qTs, "qiTs": L_qiTs, "wis": L_wis, "kT": L_kT, "v": L_v,
                                            "kiT2": L_ki, "dmask": dmask,
                                            "gqk": ext_in("gqk" + sfx, [2, 128], F32), "o": B_o2}))
                x.global_barrier()
                goff[BF16] = 0
                Go2 = gather([B_o2[0:1024], B_o2[1024:2048]], stage=True)
                x.global_barrier()
                FO = 2048
                L_o = scr("L_o", [TOK, 2048], BF16)
                for tl in range(16):
                    jj = tl // 2
                    cp(L_o[tl * 128:(tl + 1) * 128, :], Go2[jj // 4, tl % 2][(jj % 4) * 128:(jj % 4 + 1) * 128, :])
            x.global_barrier()
            emit_LC(x, c, _mk_dram({"x_in": xsrc, "o_in": L_o, "ada": ada_i, "g2": ext_in("g2" + sfx, [D], F32),
                                    "w_out": ext_in("w_out" + sfx, [FO, D], F32),
                                    "wgu": ext_in("wgu" + sfx, [D, 2 * FFN], F32),
                                    "wd": ext_in("wd" + sfx, [FFN, D], F32), "x_mid": xmid, "x_out": xdst,
                                    "act_d": scr("act_d", [NT, 128, FFN // 128, 128], BF16)}), FO)
            x.global_barrier()
        print("fused sems", x.nsem, "cnt", x.cnt)
    return nc


def fused_inputs(inp, depth=DEPTH):
    x = np.asarray(inp["x"], dtype=np.float32)
    c = np.asarray(inp["c"], dtype=np.float32)
    pos = np.asarray(inp["positions"]).astype(np.int32)
    tri = np.where(np.arange(128)[None, :] <= np.arange(128)[:, None], 0.0, -1e30).astype(np.float32)
    full_neg = np.full((128, 128), -1e30, np.float32)
    zero_m = np.zeros((128, 128), np.float32)
    f32 = lambda a: _ca(np.asarray(a, dtype=np.float32))
    maps = []
    for cc in range(8):
        b, h = cc // 2, cc % 2
        sl = slice(h * TOK, (h + 1) * TOK)
        m = {"x_in": _ca(x[b, sl]), "c_in": _ca(c[b]), "pos": _ca(pos[b, sl]), "rk": np.array([[h]], np.int32),
             "invf8": _invf_table(8), "invf16": _invf_table(16),
             "dmask": np.stack([tri, full_neg]) if h == 0 else np.stack([zero_m, tri])}
        for i in range(depth):
            kind, j = i % 3, i // 3
            sfx = "_%d" % i
            m["ada_w" + sfx] = inp["ada_w"][i]
            m["ada_b" + sfx] = inp["ada_b"][i]
            m["g1" + sfx] = inp["norm1_g"][i]
            m["g2" + sfx] = inp["norm2_g"][i]
            m["wgu" + sfx] = inp["ffn_w_gate_up"][i]
            m["wd" + sfx] = inp["ffn_w_down"][i]
            if kind == 0:
                m["w_in" + sfx] = inp["da_w_in"][j]
                m["w_out" + sfx] = inp["da_w_out"][j]
                m["gq" + sfx] = inp["da_q_norm_g"][j]
                m["gk" + sfx] = inp["da_k_norm_g"][j]
                m["lam4" + sfx] = f32(np.stack([inp["da_lambda_q1"][j], inp["da_lambda_k1"][j],
                                                inp["da_lambda_q2"][j], inp["da_lambda_k2"][j]]))
                m["gqk" + sfx] = f32(np.stack([inp["da_q_norm_g"][j], inp["da_k_norm_g"][j]]))
                m["subg" + sfx] = inp["da_subln_g"][j]
            elif kind == 1:
                m["w_in" + sfx] = inp["ssd_w_in"][j]
                m["w_out" + sfx] = inp["ssd_w_out"][j]
                ch = np.concatenate([np.arange(h * 2048, (h + 1) * 2048), np.arange(4096 + h * 512, 4096 + (h + 1) * 512),
                                     np.arange(5120 + h * 512, 5120 + (h + 1) * 512)])
                hs = slice(h * 32, (h + 1) * 32)
                m["convw" + sfx] = _ca(inp["ssd_conv_w"][j][:, ch])
                m["convb" + sfx] = _ca(inp["ssd_conv_b"][j][ch])
                m["hp" + sfx] = f32(np.stack([inp["ssd_dt_bias"][j][hs], inp["ssd_a_log"][j][hs], inp["ssd_d_skip"][j][hs]]))
                m["ng" + sfx] = _ca(inp["ssd_norm_g"][j][h * 2048:(h + 1) * 2048])
            else:
                m["w_in" + sfx] = inp["sa_w_in"][j]
                m["w_out" + sfx] = inp["sa_w_out"][j]
                m["gq" + sfx] = inp["sa_q_norm_g"][j]
                m["gk" + sfx] = inp["sa_k_norm_g"][j]
                m["gi" + sfx] = inp["sa_idx_k_norm_g"][j]
                m["gqk" + sfx] = f32(np.stack([inp["sa_q_norm_g"][j], inp["sa_k_norm_g"][j]]))
        maps.append(m)
    return maps


def kernel(**inp):
    depth = DEPTH
    nc = build_fused(depth)
    maps = fused_inputs(inp, depth)
    res = run_bass_kernel_spmd(nc, maps, core_ids=list(range(8))).results
    out = np.empty((NB, S, D), np.float32)
    for cc in range(8):
        b, h = cc // 2, cc % 2
        out[b, h * TOK:(h + 1) * TOK] = res[cc]["x_out"]
    return out
```

```python
import math
import numpy as np
from contextlib import ExitStack
import ml_dtypes
import concourse.bass as bass
import concourse.mybir as mybir
from concourse.bass_utils import run_bass_kernel_spmd

F32 = mybir.dt.float32
BF16 = mybir.dt.bfloat16
I32 = mybir.dt.int32
ALU = mybir.AluOpType
AF = mybir.ActivationFunctionType
AX = mybir.AxisListType

D = 2048
S = 4096
NB = 4
DEPTH = 4
KC = D // 128
TOK = 2048
NT = TOK // 128
FFN = 5632
EPS = 1e-6
ROPE_THETA = 500000.0
NEG = -30000.0

SAME_ENGINE_SYNC = True


class Dep:
    __slots__ = ("name", "w", "r", "dsem", "dq")

    def __init__(self, name=""):
        self.name = name
        self.w = {}
        self.r = {}
        self.dsem = None
        self.dq = None


class X:
    def __init__(self, nc, stack):
        self.nc = nc
        self.stack = stack
        self.root = stack
        self.eng = {"pe": nc.tensor, "act": nc.scalar, "dve": nc.vector,
                    "pool": nc.gpsimd, "sp": nc.sync}
        self.sem = {}
        self.cnt = {}
        self.seen = {}
        for k in self.eng:
            self.sem[k] = stack.enter_context(nc.semaphore("es_" + k))
            self.cnt[k] = 0
            self.seen[k] = {}
        self.nsem = 5
        self.semcnt = {}
        self.free_dsems = {"sp": [], "pool": [], "act": []}
        self.alltok = {}
        self.uid = 0

    def name(self, p):
        self.uid += 1
        return "%s_%d" % (p, self.uid)

    def sb(self, shape, dt, name="sb", stack=None):
        st = stack or self.stack
        return st.enter_context(self.nc.sbuf_tensor(self.name(name), list(shape), dt))

    def ps(self, shape, dt=F32, name="ps", stack=None):
        st = stack or self.stack
        return st.enter_context(self.nc.psum_tensor(self.name(name), list(shape), dt))

    def _wait(self, e, toks):
        en = self.eng[e]
        seen = self.seen[e]
        for s, v in toks.items():
            cv = self.semcnt.get(s)
            if cv is not None and cv > v:
                v = cv
            if seen.get(s, 0) >= v:
                continue
            if s is self.sem[e]:
                if e == "pe" or not SAME_ENGINE_SYNC:
                    continue
            en.wait_ge(s, v)
            seen[s] = v

    def _pre(self, e, r, w, mw=()):
        for d in r:
            self._wait(e, d.w)
        for d in w:
            self._wait(e, d.w)
            self._wait(e, d.r)
        for d in mw:
            self._wait(e, d.r)

    def _post(self, tok, r, w, mw=()):
        s, v = tok
        if self.alltok.get(s, 0) < v:
            self.alltok[s] = v
        for d in r:
            if d.r.get(s, 0) < v:
                d.r[s] = v
        for d in w:
            d.w = {s: v}
            d.r = {}
        for d in mw:
            if d.w.get(s, 0) < v:
                d.w[s] = v

    def op(self, e, fn, r=(), w=(), mw=(), inc=True):
        self._pre(e, r, w, mw)
        ins = fn(self.eng[e])
        if inc:
            self.cnt[e] += 1
            ins.then_inc(self.sem[e], 1)
            self._post((self.sem[e], self.cnt[e]), r, w, mw)
        else:
            self._post((self.sem[e], self.cnt[e] + 1), r, w, mw)
        return ins

    def dma(self, e, out, in_, r=(), w=(), mw=(), **kw):
        host = (list(w) + list(mw))[0]
        assert host.dq in (None, e), (host.name, host.dq, e)
        if host.dsem is None:
            host.dq = e
            if self.free_dsems[e]:
                host.dsem = self.free_dsems[e].pop()
            else:
                host.dsem = self.root.enter_context(self.nc.semaphore(self.name("ds")))
                self.semcnt[host.dsem] = 0
                self.nsem += 1
        self._pre(e, r, w, mw)
        ins = self.eng[e].dma_start(out=out, in_=in_, **kw)
        self.semcnt[host.dsem] += 16
        ins.then_inc(host.dsem, 16)
        self._post((host.dsem, self.semcnt[host.dsem]), r, w, mw)
        return ins

    def mkdep(self, name=""):
        d = Dep(name)
        if isinstance(self.stack, Scope):
            self.stack.deps.append(d)
        return d

    def global_barrier(self):
        for e in self.eng:
            self._wait(e, dict(self.alltok))

    def barrier(self, deps):
        toks = {}
        for d in deps:
            for src in (d.w, d.r):
                for s_, v in src.items():
                    if toks.get(s_, 0) < v:
                        toks[s_] = v
        for e in self.eng:
            self._wait(e, toks)

    def finish(self, deps, e="sp"):
        for d in deps:
            self._wait(e, d.w)


class Scope:
    def __init__(self, x):
        self.x = x
        self.st = ExitStack()
        self.deps = []

    def __enter__(self):
        self.st.__enter__()
        self.prev = self.x.stack
        self.x.stack = self
        return self

    def __exit__(self, *a):
        self.x.stack = self.prev
        if a[0] is None:
            self.x.barrier(self.deps)
            for d in self.deps:
                if d.dsem is not None:
                    self.x.free_dsems[d.dq].append(d.dsem)
                    d.dsem = None
                    d.dq = None
        return self.st.__exit__(*a)

    def enter_context(self, cm):
        return self.st.enter_context(cm)


class T:
    def __init__(self, x, shape, dt, name, psum=False, stack=None):
        stack = stack or x.stack
        self.t = x.ps(shape, dt, name, stack) if psum else x.sb(shape, dt, name, stack)
        self.d = Dep(name)
        if isinstance(stack, Scope):
            stack.deps.append(self.d)

    def __getitem__(self, k):
        return self.t[k]


def sbt(x, shape, dt, name, stack=None):
    return T(x, shape, dt, name, False, stack)


def pst(x, shape, dt, name, stack=None):
    return T(x, shape, dt, name, True, stack)


class Consts:
    pass


def make_consts(x):
    c = Consts()
    idf = sbt(x, [128, 128], F32, "idf")
    c.ident = sbt(x, [128, 128], BF16, "ident")
    x.op("pool", lambda e: e.memset(idf[:], 1.0), w=[idf.d])
    x.op("pool", lambda e: e.affine_select(out=idf[:], in_=idf[:], pattern=[[-1, 128]],
                                           compare_op=ALU.is_equal, fill=0.0, base=0,
                                           channel_multiplier=1), r=[idf.d], w=[idf.d])
    x.op("dve", lambda e: e.tensor_copy(c.ident[:], idf[:]), r=[idf.d], w=[c.ident.d])
    c.identf = idf
    ngf = sbt(x, [128, 128], F32, "ngf")
    c.negT = sbt(x, [128, 128], BF16, "negT")
    x.op("pool", lambda e: e.memset(ngf[:], 0.0), w=[ngf.d])
    x.op("pool", lambda e: e.affine_select(out=ngf[:], in_=ngf[:], pattern=[[1, 128]],
                                           compare_op=ALU.is_ge, fill=NEG, base=0,
                                           channel_multiplier=-1), r=[ngf.d], w=[ngf.d])
    x.op("dve", lambda e: e.tensor_copy(c.negT[:], ngf[:]), r=[ngf.d], w=[c.negT.d])
    c.negQ = sbt(x, [128, 128], F32, "negQ")
    x.op("pool", lambda e: e.memset(c.negQ[:], 0.0), w=[c.negQ.d])
    x.op("pool", lambda e: e.affine_select(out=c.negQ[:], in_=c.negQ[:], pattern=[[-1, 128]],
                                           compare_op=ALU.is_ge, fill=-1e30, base=0,
                                           channel_multiplier=1), r=[c.negQ.d], w=[c.negQ.d])
    trf = sbt(x, [128, 128], F32, "trf")
    c.tri = sbt(x, [128, 128], BF16, "tri")
    x.op("pool", lambda e: e.memset(trf[:], 1.0), w=[trf.d])
    x.op("pool", lambda e: e.affine_select(out=trf[:], in_=trf[:], pattern=[[1, 128]],
                                           compare_op=ALU.is_ge, fill=0.0, base=0,
                                           channel_multiplier=-1), r=[trf.d], w=[trf.d])
    x.op("dve", lambda e: e.tensor_copy(c.tri[:], trf[:]), r=[trf.d], w=[c.tri.d])
    c.trif = trf
    c.ones = sbt(x, [128, 128], BF16, "ones")
    x.op("pool", lambda e: e.memset(c.ones[:], 1.0), w=[c.ones.d])
    c.onesf = sbt(x, [128, 128], F32, "onesf")
    x.op("pool", lambda e: e.memset(c.onesf[:], 1.0), w=[c.onesf.d])
    c.nhalf = sbt(x, [128, 64], F32, "nhalf")
    x.op("pool", lambda e: e.memset(c.nhalf[:], -0.5), w=[c.nhalf.d])
    return c


def rsqrt_mean(x, c, out, ss, n, width):
    x.op("dve", lambda e: e.tensor_scalar(out=out[:, 0:width], in0=ss[:, 0:width], scalar1=1.0 / n,
                                          scalar2=EPS, op0=ALU.mult, op1=ALU.add),
         r=[ss.d], w=[out.d])
    x.op("pool", lambda e: e.tensor_tensor(out=out[:, 0:width], in0=out[:, 0:width],
                                           in1=c.nhalf[:, 0:width], op=ALU.pow),
         r=[out.d, c.nhalf.d], w=[out.d])


def bc(ap, shape):
    return ap.to_broadcast(list(shape))


def emit_ada_split(x, c_ap, adaw_ap, adab_ap, ada_ap, d_ada, split):
    RS, H, G = split
    with Scope(x) as ls:
        cs = sbt(x, [128, 16], F32, "c_sb", ls)
        ca = sbt(x, [128, 16], F32, "c_act", ls)
        brow = sbt(x, [1, 6 * D], F32, "brow", ls)
        arow = sbt(x, [1, 6 * D], F32, "arow", ls)
        hrow = sbt(x, [1, 3 * D], F32, "hrow", ls)
        wt = [sbt(x, [128, 16, 512], F32, "adaw%d" % i, ls) for i in range(2)]
        pa = [pst(x, [1, 512], F32, "adap%d" % i, ls) for i in range(2)]
        d_h = x.mkdep("adaH")
        x.dma("sp", cs[:], c_ap.rearrange("(p k) -> p k", k=16), w=[cs.d])
        x.dma("sp", brow[:], adab_ap.rearrange("(o n) -> o n", o=1), w=[brow.d])
        x.op("act", lambda e: e.activation(out=ca[:], in_=cs[:], func=AF.Silu), r=[cs.d], w=[ca.d])
        for blk in range(12):
            i = blk % 2
            x.dma("sp", wt[i][:], adaw_ap[:, blk * 512:(blk + 1) * 512].rearrange("(p k) f -> p k f", k=16),
                  w=[wt[i].d])
            for k in range(16):
                x.op("pe", lambda e: e.matmul(pa[i][:], ca[:, k:k + 1], wt[i][:, k, :],
                                              start=(k == 0), stop=(k == 15)),
                     r=[ca.d, wt[i].d], w=[pa[i].d], inc=(k == 15))
            x.op("dve", lambda e: e.tensor_copy(hrow[0:1, blk * 512:(blk + 1) * 512], pa[i][:]),
                 r=[pa[i].d], mw=[hrow.d])
        x.dma("sp", H.rearrange("(o n) -> o n", o=1), hrow[:], r=[hrow.d], w=[d_h])
        x.global_barrier()
        x.op("pool", lambda e: e.collective_compute("AllGather", ALU.bypass, replica_groups=GROUPS,
                                                    ins=[H.opt()], outs=[G.opt()]), w=[d_h])
        x.global_barrier()
        x.dma("sp", arow[:], G.rearrange("s n -> (s n)").rearrange("(o n) -> o n", o=1), r=[d_h], w=[arow.d])
        x.op("dve", lambda e: e.tensor_tensor(out=arow[:], in0=arow[:], in1=brow[:], op=ALU.add),
             r=[arow.d, brow.d], w=[arow.d])
        x.dma("sp", ada_ap.rearrange("(o n) -> o n", o=1), arow[:], r=[arow.d], w=[d_ada])


def emit_ada(x, c_ap, adaw_ap, adab_ap, ada_ap, d_ada):
    with Scope(x) as ls:
        cs = sbt(x, [128, 16], F32, "c_sb", ls)
        ca = sbt(x, [128, 16], F32, "c_act", ls)
        brow = sbt(x, [1, 6 * D], F32, "brow", ls)
        arow = sbt(x, [1, 6 * D], F32, "arow", ls)
        wt = [sbt(x, [128, 16, 512], F32, "adaw%d" % i, ls) for i in range(2)]
        pa = [pst(x, [1, 512], F32, "adap%d" % i, ls) for i in range(2)]
        x.dma("sp", cs[:], c_ap.rearrange("(p k) -> p k", k=16), w=[cs.d])
        x.dma("sp", brow[:], adab_ap.rearrange("(o n) -> o n", o=1), w=[brow.d])
        x.op("act", lambda e: e.activation(out=ca[:], in_=cs[:], func=AF.Silu), r=[cs.d], w=[ca.d])
        for blk in range(24):
            i = blk % 2
            x.dma("sp", wt[i][:], adaw_ap[:, blk * 512:(blk + 1) * 512].rearrange("(p k) f -> p k f", k=16),
                  w=[wt[i].d])
            for k in range(16):
                x.op("pe", lambda e: e.matmul(pa[i][:], ca[:, k:k + 1], wt[i][:, k, :],
                                              start=(k == 0), stop=(k == 15)),
                     r=[ca.d, wt[i].d], w=[pa[i].d], inc=(k == 15))
            x.op("dve", lambda e: e.tensor_tensor(out=arow[0:1, blk * 512:(blk + 1) * 512], in0=pa[i][:],
                                                  in1=brow[0:1, blk * 512:(blk + 1) * 512], op=ALU.add),
                 r=[pa[i].d, brow.d], mw=[arow.d])
        x.dma("sp", ada_ap.rearrange("(o n) -> o n", o=1), arow[:], r=[arow.d], w=[d_ada])


def load_cols(x, ls, vec_ap, name, d_src=None):
    t = sbt(x, [128, 16], F32, name, ls)
    x.dma("sp", t[:], vec_ap.rearrange("(k p) -> p k", p=128), r=([d_src] if d_src else []), w=[t.d],
          allow_slow_non_contiguous=True)
    return t


def load_bcast(x, ls, vec_ap, n, name, d_src=None, dt=F32):
    t = sbt(x, [128, n], dt, name, ls)
    x.dma("sp", t[:], vec_ap.rearrange("(o n) -> o n", o=1).partition_broadcast(128),
          r=([d_src] if d_src else []), w=[t.d])
    return t


def emit_mod_cols(x, ls, g_ap, ada_ap, d_ada, scale_idx, shift_idx):
    g = load_cols(x, ls, g_ap, "gcol")
    sc = load_cols(x, ls, ada_ap[scale_idx * D:(scale_idx + 1) * D], "sccol", d_ada)
    sh = load_cols(x, ls, ada_ap[shift_idx * D:(shift_idx + 1) * D], "shcol", d_ada)
    sT = sbt(x, [128, 16], F32, "sT", ls)
    x.op("dve", lambda e: e.scalar_tensor_tensor(out=sT[:], in0=sc[:], scalar=1.0, in1=g[:],
                                                 op0=ALU.add, op1=ALU.mult),
         r=[sc.d, g.d], w=[sT.d])
    return sT, sh


def emit_norm_hT(x, c, x_ap, d_x, ntiles, sT, shT, hT, hT_d, tile_off=0):
    with Scope(x) as ls:
        xt = [sbt(x, [128, D], F32, "xt%d" % i, ls) for i in range(2)]
        xn = [sbt(x, [128, D], BF16, "xn%d" % i, ls) for i in range(2)]
        junk = sbt(x, [128, D], BF16, "junk", ls)
        ss = [sbt(x, [128, 1], F32, "ss%d" % i, ls) for i in range(2)]
        rstd = [sbt(x, [128, 1], F32, "rstd%d" % i, ls) for i in range(2)]
        pt = [pst(x, [128, 512], BF16, "ptn%d" % i, ls) for i in range(2)]
        for t in range(ntiles):
            i = t % 2
            x.dma("sp", xt[i][:], x_ap[t * 128:(t + 1) * 128, :], r=[d_x], w=[xt[i].d])
            x.op("act", lambda e: e.activation(out=junk[:], in_=xt[i][:], func=AF.Square,
                                               accum_out=ss[i][:, 0:1]),
                 r=[xt[i].d], w=[junk.d, ss[i].d])
            rsqrt_mean(x, c, rstd[i], ss[i], D, 1)
            x.op("act", lambda e: e.activation(out=xn[i][:], in_=xt[i][:], func=AF.Copy,
                                               scale=rstd[i][:, 0:1]),
                 r=[xt[i].d, rstd[i].d], w=[xn[i].d])
            for g in range(4):
                p = pt[g % 2]
                for j in range(4):
                    kc = g * 4 + j
                    x.op("pe", lambda e: e.transpose(p[:, j * 128:(j + 1) * 128],
                                                     xn[i][:, kc * 128:(kc + 1) * 128], c.ident[:]),
                         r=[xn[i].d, c.ident.d], w=[p.d], inc=(j == 3))
                for j in range(4):
                    kc = g * 4 + j
                    dst = hT[:, kc, (tile_off + t) * 128:(tile_off + t + 1) * 128]
                    if True:
                        x.op("dve", lambda e: e.tensor_scalar(out=dst, in0=p[:, j * 128:(j + 1) * 128],
                                                              scalar1=sT[:, kc:kc + 1], scalar2=shT[:, kc:kc + 1],
                                                              op0=ALU.mult, op1=ALU.add),
                             r=[p.d, sT.d, shT.d], mw=[hT_d[tile_off + t]])
                    else:
                        x.op("act", lambda e: e.activation(out=dst, in_=p[:, j * 128:(j + 1) * 128],
                                                           func=AF.Identity, scale=sT[:, kc:kc + 1],
                                                           bias=shT[:, kc:kc + 1]),
                             r=[p.d, sT.d, shT.d], mw=[hT_d[tile_off + t]])


def emit_rope_tables(x, ls, pos_ap, invf_ap, half, ntiles):
    n = ntiles * half
    posi = sbt(x, [128, ntiles], I32, "posi", ls)
    posf = sbt(x, [128, ntiles], F32, "posf", ls)
    invf = sbt(x, [128, half], F32, "invf", ls)
    ang = sbt(x, [128, ntiles, half], F32, "ang", ls)
    x.dma("sp", posi[:], pos_ap.rearrange("(t p) -> p t", p=128), w=[posi.d], allow_slow_non_contiguous=True)
    x.dma("sp", invf[:], invf_ap, w=[invf.d])
    x.op("dve", lambda e: e.tensor_copy(posf[:], posi[:]), r=[posi.d], w=[posf.d])
    x.op("dve", lambda e: e.tensor_tensor(out=ang[:], in0=bc(posf[:].unsqueeze(2), [128, ntiles, half]),
                                          in1=bc(invf[:].unsqueeze(1), [128, ntiles, half]), op=ALU.mult),
         r=[posf.d, invf.d], w=[ang.d])
    outs = []
    C1 = 6.28125
    C2 = 2.0 * math.pi - C1
    for nm, shift in (("cos", math.pi / 2), ("sin", 0.0)):
        a = sbt(x, [128, n], F32, nm + "_a", ls)
        ki = sbt(x, [128, n], I32, nm + "_ki", ls)
        kf = sbt(x, [128, n], F32, nm + "_kf", ls)
        m = sbt(x, [128, n], F32, nm + "_m", ls)
        res = sbt(x, [128, ntiles, half], F32, nm + "_t", ls)
        af = ang[:].rearrange("p t h -> p (t h)")
        x.op("dve", lambda e: e.tensor_scalar(out=a[:], in0=af, scalar1=shift, scalar2=None, op0=ALU.add),
             r=[ang.d], w=[a.d])
        x.op("dve", lambda e: e.tensor_scalar(out=kf[:], in0=a[:], scalar1=1.0 / (2 * math.pi), scalar2=None,
                                              op0=ALU.mult), r=[a.d], w=[kf.d])
        x.op("dve", lambda e: e.tensor_copy(ki[:], kf[:]), r=[kf.d], w=[ki.d])
        x.op("dve", lambda e: e.tensor_copy(kf[:], ki[:]), r=[ki.d], w=[kf.d])
        x.op("dve", lambda e: e.scalar_tensor_tensor(out=a[:], in0=kf[:], scalar=-C1, in1=a[:],
                                                     op0=ALU.mult, op1=ALU.add), r=[kf.d, a.d], w=[a.d])
        x.op("dve", lambda e: e.scalar_tensor_tensor(out=a[:], in0=kf[:], scalar=-C2, in1=a[:],
                                                     op0=ALU.mult, op1=ALU.add), r=[kf.d, a.d], w=[a.d])
        x.op("dve", lambda e: e.tensor_scalar(out=m[:], in0=a[:], scalar1=math.pi, scalar2=2 * math.pi,
                                              op0=ALU.is_gt, op1=ALU.mult), r=[a.d], w=[m.d])
        x.op("dve", lambda e: e.tensor_tensor(out=a[:], in0=a[:], in1=m[:], op=ALU.subtract),
             r=[a.d, m.d], w=[a.d])
        x.op("dve", lambda e: e.tensor_scalar(out=m[:], in0=a[:], scalar1=-math.pi, scalar2=2 * math.pi,
                                              op0=ALU.is_lt, op1=ALU.mult), r=[a.d], w=[m.d])
        x.op("dve", lambda e: e.tensor_tensor(out=a[:], in0=a[:], in1=m[:], op=ALU.add),
             r=[a.d, m.d], w=[a.d])
        x.op("dve", lambda e: e.tensor_scalar(out=a[:], in0=a[:], scalar1=-3.1415925, scalar2=3.1415925,
                                              op0=ALU.max, op1=ALU.min), r=[a.d], w=[a.d])
        x.op("act", lambda e: e.activation(out=res[:].rearrange("p t h -> p (t h)"), in_=a[:], func=AF.Sin),
             r=[a.d], w=[res.d])
        outs.append(res)
    return outs[0], outs[1]


def emit_qk_post(x, c, ls_tiles, ps, ncols, gdim, gain, half, cos, sin, t, out_bf):
    ng = ncols // gdim
    qn, sq, ssq, rs, t1, t2, t3, t4 = ls_tiles
    pv = ps[:, 0:ncols].rearrange("p (g d) -> p g d", d=gdim)
    qv = qn[:, 0:ncols].rearrange("p (g d) -> p g d", d=gdim)
    if gain is not None:
        x.op("act", lambda e: e.activation(out=sq[:, 0:ncols], in_=ps[:, 0:ncols], func=AF.Square),
             r=[ps.d], w=[sq.d])
        x.op("dve", lambda e: e.tensor_reduce(out=ssq[:, 0:ng], in_=sq[:, 0:ncols].rearrange("p (g d) -> p g d", d=gdim),
                                              axis=AX.X, op=ALU.add), r=[sq.d], w=[ssq.d])
        rsqrt_mean(x, c, rs, ssq, gdim, ng)
        x.op("dve", lambda e: e.tensor_tensor(out=qv, in0=pv, in1=bc(rs[:, 0:ng].unsqueeze(2), [128, ng, gdim]),
                                              op=ALU.mult), r=[ps.d, rs.d], w=[qn.d])
        x.op("pool", lambda e: e.tensor_tensor(out=qv, in0=qv, in1=bc(gain[:, 0:gdim].unsqueeze(1), [128, ng, gdim]),
                                               op=ALU.mult), r=[qn.d, gain.d], w=[qn.d])
    else:
        x.op("act", lambda e: e.activation(out=qn[:, 0:ncols], in_=ps[:, 0:ncols], func=AF.Copy),
             r=[ps.d], w=[qn.d])
    if half:
        x1 = qv[:, :, 0:half]
        x2 = qv[:, :, half:2 * half]
        cb = bc(cos[:, t, :].unsqueeze(1), [128, ng, half])
        sb_ = bc(sin[:, t, :].unsqueeze(1), [128, ng, half])
        tv = [tt[:, 0:ng * half].rearrange("p (g h) -> p g h", h=half) for tt in (t1, t2, t3, t4)]
        x.op("dve", lambda e: e.tensor_tensor(out=tv[0], in0=x1, in1=cb, op=ALU.mult), r=[qn.d, cos.d], w=[t1.d])
        x.op("dve", lambda e: e.tensor_tensor(out=tv[1], in0=x2, in1=sb_, op=ALU.mult), r=[qn.d, sin.d], w=[t2.d])
        x.op("pool", lambda e: e.tensor_tensor(out=tv[2], in0=x2, in1=cb, op=ALU.mult), r=[qn.d, cos.d], w=[t3.d])
        x.op("pool", lambda e: e.tensor_tensor(out=tv[3], in0=x1, in1=sb_, op=ALU.mult), r=[qn.d, sin.d], w=[t4.d])
        x.op("dve", lambda e: e.tensor_tensor(out=x1, in0=tv[0], in1=tv[1], op=ALU.subtract),
             r=[t1.d, t2.d], w=[qn.d])
        x.op("pool", lambda e: e.tensor_tensor(out=x2, in0=tv[2], in1=tv[3], op=ALU.add),
             r=[t3.d, t4.d, qn.d], w=[qn.d])
    x.op("act", lambda e: e.activation(out=out_bf[:, 0:ncols], in_=qn[:, 0:ncols], func=AF.Copy),
         r=[qn.d], w=[out_bf.d])


def emit_qk_post2(x, c, ls_tiles, ps, gdim, gain, half, cos, sin, t0, out_bf):
    ng = 512 // gdim
    ng2 = 2 * ng
    qn, sq, ssq, rs, t1, t2, t3, t4 = ls_tiles
    psf = ps[:].rearrange("p a c -> p (a c)")
    pv = ps[:].rearrange("p a (g d) -> p (a g) d", d=gdim)
    qv = qn[:, 0:1024].rearrange("p (g d) -> p g d", d=gdim)
    if gain is not None:
        x.op("act", lambda e: e.activation(out=sq[:, 0:1024], in_=psf, func=AF.Square), r=[ps.d], w=[sq.d])
        x.op("dve", lambda e: e.tensor_reduce(out=ssq[:, 0:ng2], in_=sq[:, 0:1024].rearrange("p (g d) -> p g d", d=gdim),
                                              axis=AX.X, op=ALU.add), r=[sq.d], w=[ssq.d])
        rsqrt_mean(x, c, rs, ssq, gdim, ng2)
        x.op("dve", lambda e: e.tensor_tensor(out=qv, in0=pv, in1=bc(rs[:, 0:ng2].unsqueeze(2), [128, ng2, gdim]),
                                              op=ALU.mult), r=[ps.d, rs.d], w=[qn.d])
        x.op("pool", lambda e: e.tensor_tensor(out=qv, in0=qv, in1=bc(gain[:, 0:gdim].unsqueeze(1), [128, ng2, gdim]),
                                               op=ALU.mult), r=[qn.d, gain.d], w=[qn.d])
    else:
        x.op("act", lambda e: e.activation(out=qn[:, 0:1024], in_=psf, func=AF.Copy), r=[ps.d], w=[qn.d])
    q4 = qn[:, 0:1024].rearrange("p (a g d) -> p a g d", a=2, d=gdim)
    x1 = q4[:, :, :, 0:half]
    x2 = q4[:, :, :, half:2 * half]
    cb = bc(cos[:, t0:t0 + 2, :].unsqueeze(2), [128, 2, ng, half])
    sb_ = bc(sin[:, t0:t0 + 2, :].unsqueeze(2), [128, 2, ng, half])
    tv = [tt[:, 0:ng2 * half].rearrange("p (a g h) -> p a g h", a=2, h=half) for tt in (t1, t2, t3, t4)]
    x.op("dve", lambda e: e.tensor_tensor(out=tv[0], in0=x1, in1=cb, op=ALU.mult), r=[qn.d, cos.d], w=[t1.d])
    x.op("dve", lambda e: e.tensor_tensor(out=tv[1], in0=x2, in1=sb_, op=ALU.mult), r=[qn.d, sin.d], w=[t2.d])
    x.op("pool", lambda e: e.tensor_tensor(out=tv[2], in0=x2, in1=cb, op=ALU.mult), r=[qn.d, cos.d], w=[t3.d])
    x.op("pool", lambda e: e.tensor_tensor(out=tv[3], in0=x1, in1=sb_, op=ALU.mult), r=[qn.d, sin.d], w=[t4.d])
    x.op("dve", lambda e: e.tensor_tensor(out=x1, in0=tv[0], in1=tv[1], op=ALU.subtract),
         r=[t1.d, t2.d], w=[qn.d])
    x.op("pool", lambda e: e.tensor_tensor(out=x2, in0=tv[2], in1=tv[3], op=ALU.add),
         r=[t3.d, t4.d, qn.d], w=[qn.d])
    x.op("act", lambda e: e.activation(out=out_bf[:, 0:1024], in_=qn[:, 0:1024], func=AF.Copy),
         r=[qn.d], w=[out_bf.d])


def alloc_qk_tiles(x, ls):
    qn = sbt(x, [128, 1024], F32, "qn", ls)
    sq = sbt(x, [128, 1024], F32, "sq", ls)
    ssq = sbt(x, [128, 16], F32, "ssq", ls)
    rs = sbt(x, [128, 16], F32, "rs", ls)
    ts = [sbt(x, [128, 128], F32, "rt%d" % i, ls) for i in range(4)]
    return (qn, sq, ssq, rs, ts[0], ts[1], ts[2], ts[3])


class ProjCtx:
    pass


def emit_proj(x, c, w_ap, hT, hT_d, ntiles, blocks):
    with Scope(x) as ls:
        wb = [sbt(x, [128, 16, 512], BF16, "wb%d" % i, ls) for i in range(2)]
        pp = [pst(x, [128, 512], F32, "pp%d" % i, ls) for i in range(2)]
        pp2 = ([pst(x, [128, 2, 512], F32, "pp2_%d" % i, ls) for i in range(2)]
               if any(b_[2] == "tok2" for b_ in blocks) else None)
        n = 0
        for bi, (col0, ncols, mode, handler) in enumerate(blocks):
            wt = wb[bi % 2]
            x.dma("pool", wt[:, :, 0:ncols], w_ap[:, col0:col0 + ncols].rearrange("(k p) f -> p k f", p=128),
                  w=[wt.d])
            if mode == "tok":
                for t in range(ntiles):
                    ps = pp[n % 2]
                    n += 1
                    for kc in range(16):
                        x.op("pe", lambda e: e.matmul(ps[:, 0:ncols], hT[:, kc, t * 128:(t + 1) * 128],
                                                      wt[:, kc, 0:ncols], start=(kc == 0), stop=(kc == 15)),
                             r=[hT_d[t], wt.d], w=[ps.d], inc=(kc == 15))
                    handler(t, ps)
            elif mode == "tok2":
                for t in range(0, ntiles, 2):
                    ps = pp2[n % 2]
                    n += 1
                    for a_ in range(2):
                        for kc in range(16):
                            x.op("pe", lambda e: e.matmul(ps[:, a_, 0:ncols], hT[:, kc, (t + a_) * 128:(t + a_ + 1) * 128],
                                                          wt[:, kc, 0:ncols], start=(kc == 0), stop=(kc == 15)),
                                 r=[hT_d[t + a_], wt.d], w=[ps.d], inc=(kc == 15))
                    handler(t, ps)
            else:
                for fc in range(ncols // 128):
                    for tb in range(ntiles // 4):
                        ps = pp[n % 2]
                        n += 1
                        for kc in range(16):
                            x.op("pe", lambda e: e.matmul(ps[:, :], wt[:, kc, fc * 128:(fc + 1) * 128],
                                                          hT[:, kc, tb * 512:(tb + 1) * 512],
                                                          start=(kc == 0), stop=(kc == 15)),
                                 r=hT_d[tb * 4:tb * 4 + 4] + [wt.d], w=[ps.d], inc=(kc == 15))
                        handler(fc, tb, ps)


def emit_LA(x, c, dram, kind, dm=False, ada_split=None):
    x_in = dram("x_in", [TOK, D], F32, "ExternalInput")
    c_in = dram("c_in", [D], F32, "ExternalInput")
    pos = dram("pos", [TOK], I32, "ExternalInput")
    adaw = dram("ada_w", [D, 3 * D if ada_split is not None else 6 * D], F32, "ExternalInput")
    adab = dram("ada_b", [6 * D], F32, "ExternalInput")
    g1 = dram("g1", [D], F32, "ExternalInput")
    ada = dram("ada", [6 * D], F32, "ExternalOutput")
    FIN = {0: 6144, 1: 10304, 2: 4176}[kind]
    w_in = dram("w_in", [D, FIN], F32, "ExternalInput")
    outs = []
    with Scope(x) as st:
        d_ada = x.mkdep("ada")
        d_x = x.mkdep("x")
        if ada_split is None:
            emit_ada(x, c_in, adaw, adab, ada, d_ada)
        else:
            emit_ada_split(x, c_in, adaw, adab, ada, d_ada, ada_split)
        hT = x.sb([128, 16, TOK], BF16, "hT")
        hT_d = [x.mkdep("hT%d" % t) for t in range(NT)]
        with Scope(x) as ls:
            sT, shT = emit_mod_cols(x, ls, g1, ada, d_ada, 1, 0)
            emit_norm_hT(x, c, x_in, d_x, NT, sT, shT, hT, hT_d)
        ls = st
        if kind == 0:
            invf = dram("invf", [128, 8], F32, "ExternalInput")
            gq = dram("gq", [64], F32, "ExternalInput")
            gk = dram("gk", [64], F32, "ExternalInput")
            qT = dram("qT", [16, 128, TOK], BF16, "ExternalOutput")
            kT = dram("kT", [16, 128, TOK], BF16, "ExternalOutput")
            v = dram("v", [2, TOK, 1024] if dm else [TOK, 2048], BF16, "ExternalOutput")
            outs = [x.mkdep("qT"), x.mkdep("kT"), x.mkdep("v")]
            cos, sin = emit_rope_tables(x, ls, pos, invf, 8, NT)
            gqb = load_bcast(x, ls, gq, 64, "gqb")
            gkb = load_bcast(x, ls, gk, 64, "gkb")
            qk_tiles = alloc_qk_tiles(x, ls)
            qbf = sbt(x, [128, 1024], BF16, "qbf", ls)
            stage = [sbt(x, [128, 4, TOK], BF16, "stage%d" % i, ls) for i in range(2)]
            ptr = [pst(x, [128, 512], BF16, "ptr%d" % i, ls) for i in range(2)]
            vst = [sbt(x, [128, 512], BF16, "vst%d" % i, ls) for i in range(2)]
            blocks = []
            cnt = [0]
            for which in range(2):
                for hb in range(4):
                    def handler(t, ps, which=which, hb=hb):
                        sg = stage[(which * 4 + hb) % 2]
                        emit_qk_post2(x, c, qk_tiles, ps, 64, gqb if which == 0 else gkb, 8, cos, sin, t, qbf)
                        for a_ in range(2):
                            p = ptr[cnt[0] % 2]
                            cnt[0] += 1
                            for j in range(4):
                                x.op("pe", lambda e: e.transpose(p[:, j * 128:(j + 1) * 128],
                                                                 qbf[:, a_ * 512 + j * 128:a_ * 512 + (j + 1) * 128], c.ident[:]),
                                     r=[qbf.d, c.ident.d], w=[p.d], inc=(j == 3))
                            x.op("dve", lambda e: e.tensor_copy(sg[:, :, (t + a_) * 128:(t + a_ + 1) * 128],
                                                                p[:, :].rearrange("p (a b) -> p a b", a=4)),
                                 r=[p.d], mw=[sg.d])
                        if t == NT - 2:
                            dst = (qT if which == 0 else kT)[hb * 4:(hb + 1) * 4, :, :].rearrange("h p t -> p h t")
                            x.dma("sp", dst, sg[:], r=[sg.d], mw=[outs[which]])
                    blocks.append((which * 2048 + hb * 512, 512, "tok2", handler))
            for vb in range(4):
                def vhandler(t, ps, vb=vb):
                    vs = vst[cnt[0] % 2]
                    cnt[0] += 1
                    x.op("act", lambda e: e.activation(out=vs[:], in_=ps[:, :], func=AF.Copy), r=[ps.d], w=[vs.d])
                    vdst = (v[vb // 2, t * 128:(t + 1) * 128, (vb % 2) * 512:(vb % 2 + 1) * 512] if dm
                            else v[t * 128:(t + 1) * 128, vb * 512:(vb + 1) * 512])
                    x.dma("sp", vdst, vs[:], r=[vs.d], mw=[outs[2]])
                blocks.append((4096 + vb * 512, 512, "tok", vhandler))
            emit_proj(x, c, w_in, hT, hT_d, NT, blocks)
        elif kind == 1:
            blocks = la_handlers_ssd(x, c, ls, dram, outs, dm)
            emit_proj(x, c, w_in, hT, hT_d, NT, blocks)
        else:
            blocks = la_handlers_dsa(x, c, ls, dram, outs, pos, dm)
            emit_proj(x, c, w_in, hT, hT_d, NT, blocks)


def emit_LB_DA(x, c, dram, lambda_init):
    NH = 8
    qT = dram("qT", [NH, 128, S], BF16, "ExternalInput")
    kT = dram("kT", [NH, 128, S], BF16, "ExternalInput")
    v = dram("v", [S, NH * 128], BF16, "ExternalInput")
    lam4 = dram("lam4", [4, 64], F32, "ExternalInput")
    gqk = dram("gqk", [2, 64], F32, "ExternalInput")
    subg = dram("subg", [128], F32, "ExternalInput")
    o = dram("o", [S, NH * 128], BF16, "ExternalOutput")
    with Scope(x) as st:
        d_o = x.mkdep("o")
        ls = st
        lt = [load_bcast(x, ls, lam4[i], 64, "lam%d" % i) for i in range(4)]
        gt = [load_bcast(x, ls, gqk[i], 64, "gqk%d" % i) for i in range(2)]
        gs = load_bcast(x, ls, subg, 128, "gs")
        x.op("dve", lambda e: e.tensor_scalar(out=gs[:], in0=gs[:], scalar1=1.0 - lambda_init, scalar2=None,
                                              op0=ALU.mult), r=[gs.d], w=[gs.d])
        pr = sbt(x, [128, 64], F32, "pr")
        s12 = sbt(x, [128, 2], F32, "s12")
        e12 = sbt(x, [128, 2], F32, "e12")
        neglam = sbt(x, [128, 1], F32, "neglam")
        for i in range(2):
            x.op("dve", lambda e: e.tensor_tensor(out=pr[:], in0=lt[2 * i][:], in1=lt[2 * i + 1][:], op=ALU.mult),
                 r=[lt[2 * i].d, lt[2 * i + 1].d], w=[pr.d])
            x.op("dve", lambda e: e.tensor_reduce(out=s12[:, i:i + 1], in_=pr[:], axis=AX.X, op=ALU.add),
                 r=[pr.d], w=[s12.d])
        x.op("act", lambda e: e.activation(out=e12[:], in_=s12[:], func=AF.Exp), r=[s12.d], w=[e12.d])
        x.op("dve", lambda e: e.scalar_tensor_tensor(out=neglam[:], in0=e12[:, 1:2], scalar=-lambda_init,
                                                     in1=e12[:, 0:1], op0=ALU.add, op1=ALU.subtract),
             r=[e12.d], w=[neglam.d])
        gm = sbt(x, [128, 2], F32, "gm")
        negC = sbt(x, [128, 1], F32, "negC")
        for i in range(2):
            x.op("dve", lambda e: e.tensor_reduce(out=gm[:, i:i + 1], in_=gt[i][:], axis=AX.X, op=ALU.max,
                                                  apply_absolute_value=True), r=[gt[i].d], w=[gm.d])
        x.op("dve", lambda e: e.scalar_tensor_tensor(out=negC[:], in0=gm[:, 0:1], scalar=-8.0, in1=gm[:, 1:2],
                                                     op0=ALU.mult, op1=ALU.mult), r=[gm.d], w=[negC.d])
        kt = [sbt(x, [128, S], BF16, "kt%d" % i) for i in range(2)]
        qt = [sbt(x, [128, S], BF16, "qt%d" % i) for i in range(2)]
        va = [sbt(x, [128, 32, 129], BF16, "va%d" % i) for i in range(2)]
        for i in range(2):
            x.op("pool", lambda e: e.memset(va[i][:, :, 128:129], 1.0), w=[va[i].d])
        pss = [pst(x, [128, 512], F32, "pss%d" % i) for i in range(4)]
        pso = pst(x, [128, 8, 256], F32, "pso")
        pt = [sbt(x, [128, 512], BF16, "pt%d" % i) for i in range(4)]
        R = sbt(x, [128, 8], F32, "R")
        osbs = [sbt(x, [128, 8, 129], F32, "osb%d" % i) for i in range(2)]
        tmp = [sbt(x, [128, 128], F32, "tmp%d" % i) for i in range(2)]
        of = sbt(x, [128, 4, 128], F32, "of")
        sq = sbt(x, [128, 512], F32, "sqo")
        ssq = sbt(x, [128, 4], F32, "ssqo")
        rs = sbt(x, [128, 4], F32, "rso")
        ob = [sbt(x, [128, 4, 128], BF16, "ob%d" % i) for i in range(2)]
        zr = sbt(x, [128, 512], BF16, "zr")
        x.op("pool", lambda e: e.memset(zr[:], 0.0), w=[zr.d])
        psob = pso[:].rearrange("p a b -> p (a b)")
        n = 0
        nq = 0
        def load_head(h):
            hi_ = h % 2
            x.dma("sp", kt[hi_][:], kT[h], w=[kt[hi_].d])
            x.dma("sp", qt[hi_][:], qT[h], w=[qt[hi_].d])
            x.dma("sp", va[hi_][:, :, 0:128], v[:, h * 128:(h + 1) * 128].rearrange("(kb p) d -> p kb d", p=128),
                  mw=[va[hi_].d])

        load_head(0)
        for h in range(NH):
            hi = h % 2
            if h + 1 < NH:
                load_head(h + 1)
            steps = [(qc, kb, comp) for qc in range(8) for kb in range(4 * qc + 4) for comp in range(2)]

            def scores(idx):
                qc, kb, comp = steps[idx]
                jmin = max(0, kb - 4 * qc)
                diag = kb >= 4 * qc
                N = 512 - 128 * jmin
                q0 = qc * 512 + 128 * jmin
                ps = pss[idx % 4]
                ptt = pt[idx % 4]
                pr_ = slice(comp * 64, (comp + 1) * 64)
                lhs = kt[hi][pr_, kb * 128:(kb + 1) * 128]
                if diag:
                    x.op("pe", lambda e: e.matmul(ps[:, 0:128], lhs, qt[hi][pr_, q0:q0 + 128],
                                                  start=True, stop=False),
                         r=[kt[hi].d, qt[hi].d], w=[ps.d], inc=False)
                    x.op("pe", lambda e: e.matmul(ps[:, 0:128], c.ident[:], c.negT[:],
                                                  start=False, stop=True),
                         r=[c.ident.d, c.negT.d], w=[ps.d], inc=(N == 128))
                    if N > 128:
                        x.op("pe", lambda e: e.matmul(ps[:, 128:N], lhs, qt[hi][pr_, q0 + 128:q0 + N],
                                                      start=True, stop=True),
                             r=[kt[hi].d, qt[hi].d], w=[ps.d], inc=True)
                else:
                    x.op("pe", lambda e: e.matmul(ps[:, 0:N], lhs, qt[hi][pr_, q0:q0 + N],
                                                  start=True, stop=True),
                         r=[kt[hi].d, qt[hi].d], w=[ps.d], inc=True)
                x.op("act", lambda e: e.activation(out=ptt[:, 0:N], in_=ps[:, 0:N], func=AF.Exp,
                                                   scale=0.125, bias=negC[:, 0:1]),
                     r=[ps.d, negC.d], w=[ptt.d])

            def pv(idx):
                qc, kb, comp = steps[idx]
                jmin = max(0, kb - 4 * qc)
                ptt = pt[idx % 4]
                for j in range(jmin, 4):
                    last = (kb == 4 * qc + j)
                    x.op("pe", lambda e: e.matmul(pso[:, comp * 4 + j, 0:129],
                                                  ptt[:, (j - jmin) * 128:(j - jmin + 1) * 128],
                                                  va[hi][:, kb, :], start=False,
                                                  stop=(last and j % 2 == 1)),
                         r=[ptt.d, va[hi].d], w=[pso.d], inc=(j == 3))

            def epilogue(qc):
                nonlocal nq
                osb = osbs[nq % 2]
                x.op("act", lambda e: e.activation(out=osb[:], in_=pso[:, :, 0:129], func=AF.Copy), r=[pso.d], w=[osb.d])
                x.op("dve", lambda e: e.reciprocal(out=R[:], in_=osb[:, :, 128:129].rearrange("p a b -> p (a b)")),
                     r=[osb.d], w=[R.d])
                x.op("dve", lambda e: e.tensor_scalar(out=R[:, 4:8], in0=R[:, 4:8], scalar1=neglam[:, 0:1],
                                                      scalar2=None, op0=ALU.mult), r=[R.d, neglam.d], w=[R.d])
                for j in range(4):
                    tm = tmp[j % 2]
                    x.op("act", lambda e: e.activation(out=tm[:], in_=osb[:, 4 + j, 0:128], func=AF.Copy,
                                                       scale=R[:, 4 + j:5 + j]), r=[osb.d, R.d], w=[tm.d])
                    x.op("dve", lambda e: e.scalar_tensor_tensor(out=of[:, j, :], in0=osb[:, j, 0:128],
                                                                 scalar=R[:, j:j + 1], in1=tm[:],
                                                                 op0=ALU.mult, op1=ALU.add),
                         r=[osb.d, R.d, tm.d], mw=[of.d])
                ofl = of[:].rearrange("p a b -> p (a b)")
                x.op("act", lambda e: e.activation(out=sq[:], in_=ofl, func=AF.Square), r=[of.d], w=[sq.d])
                x.op("dve", lambda e: e.tensor_reduce(out=ssq[:], in_=sq[:].rearrange("p (a b) -> p a b", a=4),
                                                      axis=AX.X, op=ALU.add), r=[sq.d], w=[ssq.d])
                rsqrt_mean(x, c, rs, ssq, 128, 4)
                x.op("dve", lambda e: e.tensor_tensor(out=of[:], in0=of[:], in1=bc(rs[:].unsqueeze(2), [128, 4, 128]),
                                                      op=ALU.mult), r=[of.d, rs.d], w=[of.d])
                obb = ob[nq % 2]
                nq += 1
                x.op("pool", lambda e: e.tensor_tensor(out=obb[:], in0=of[:], in1=bc(gs[:].unsqueeze(1), [128, 4, 128]),
                                                       op=ALU.mult), r=[of.d, gs.d], w=[obb.d])
                x.dma("sp", o[qc * 512:(qc + 1) * 512, h * 128:(h + 1) * 128].rearrange("(j p) d -> p j d", p=128),
                      obb[:], r=[obb.d], mw=[d_o])

            scores(0)
            scores(1)
            for idx, (qc, kb, comp) in enumerate(steps):
                if kb == 0 and comp == 0:
                    for bnk in range(4):
                        x.op("pe", lambda e: e.matmul(psob[:, bnk * 512:(bnk + 1) * 512], zr[:, 0:128], zr[:],
                                                      start=True, stop=False), r=[zr.d], w=[pso.d], inc=False)
                if idx + 2 < len(steps):
                    scores(idx + 2)
                pv(idx)
                if kb == 4 * qc + 3 and comp == 1:
                    epilogue(qc)


def emit_outproj(x, c, o_ap, d_o, FO, wout_ap, x_ap, d_x, gate_b, xmid_ap, d_xmid):
    KO = FO // 128
    TB = 1024
    with Scope(x) as ls:
        oT = sbt(x, [128, KO, TB], BF16, "oT", ls)
        oT_d = [x.mkdep("oT%d" % t) for t in range(TB // 128)]
        ls.deps.extend(oT_d)
        ot = [sbt(x, [128, FO], BF16, "ot%d" % i, ls) for i in range(2)]
        pt = [pst(x, [128, 512], BF16, "pto%d" % i, ls) for i in range(2)]
        wb = [sbt(x, [128, KO, 512], BF16, "wo%d" % i, ls) for i in range(2)]
        pp = [pst(x, [128, 512], F32, "ppo%d" % i, ls) for i in range(2)]
        tm = [sbt(x, [128, 512], F32, "tmo%d" % i, ls) for i in range(2)]
        xt = [sbt(x, [128, 512], F32, "xto%d" % i, ls) for i in range(2)]
        n = 0
        nw = 0
        for tb in range(TOK // TB):
            for t in range(TB // 128):
                i = t % 2
                tok0 = tb * TB + t * 128
                x.dma("sp", ot[i][:], o_ap[tok0:tok0 + 128, :], r=[d_o], w=[ot[i].d])
                for g in range(KO // 4):
                    p = pt[g % 2]
                    for j in range(4):
                        kc = g * 4 + j
                        x.op("pe", lambda e: e.transpose(p[:, j * 128:(j + 1) * 128], ot[i][:, kc * 128:(kc + 1) * 128],
                                                         c.ident[:]), r=[ot[i].d, c.ident.d], w=[p.d], inc=(j == 3))
                    dst = oT[:, g * 4:(g + 1) * 4, t * 128:(t + 1) * 128]
                    src = p[:, :].rearrange("p (a b) -> p a b", a=4)
                    if g % 2 == 0:
                        x.op("dve", lambda e: e.tensor_copy(dst, src), r=[p.d], mw=[oT_d[t]])
                    else:
                        x.op("act", lambda e: e.activation(out=dst, in_=src, func=AF.Copy), r=[p.d], mw=[oT_d[t]])
            for cb in range(4):
                wt = wb[nw % 2]
                nw += 1
                x.dma("pool", wt[:], wout_ap[:, cb * 512:(cb + 1) * 512].rearrange("(k p) f -> p k f", p=128), w=[wt.d])
                for t in range(TB // 128):
                    tok0 = tb * TB + t * 128
                    ps = pp[n % 2]
                    tmm = tm[n % 2]
                    xtt = xt[n % 2]
                    n += 1
                    x.dma("sp", xtt[:], x_ap[tok0:tok0 + 128, cb * 512:(cb + 1) * 512], r=[d_x], w=[xtt.d])
                    for kc in range(KO):
                        x.op("pe", lambda e: e.matmul(ps[:], oT[:, kc, t * 128:(t + 1) * 128], wt[:, kc, :],
                                                      start=(kc == 0), stop=(kc == KO - 1)),
                             r=[oT_d[t], wt.d], w=[ps.d], inc=(kc == KO - 1))
                    x.op("dve", lambda e: e.tensor_tensor(out=tmm[:], in0=ps[:], in1=gate_b[:, cb * 512:(cb + 1) * 512],
                                                          op=ALU.mult), r=[ps.d, gate_b.d], w=[tmm.d])
                    x.op("pool", lambda e: e.tensor_tensor(out=tmm[:], in0=tmm[:], in1=xtt[:], op=ALU.add),
                         r=[tmm.d, xtt.d], w=[tmm.d])
                    x.dma("act", xmid_ap[tok0:tok0 + 128, cb * 512:(cb + 1) * 512], tmm[:], r=[tmm.d], mw=[d_xmid])


def emit_ffn(x, c, xmid_ap, d_xmid, sT, shT, gate_b, wgu_ap, wd_ap, xout_ap, d_xout, act_ap):
    NFC = FFN // 128
    d_act = x.mkdep("act_d")
    with Scope(x) as ls:
        h2T = sbt(x, [128, 16, TOK], BF16, "h2T", ls)
        h2_d = [x.mkdep("h2T%d" % t) for t in range(NT)]
        ls.deps.extend(h2_d)
        emit_norm_hT(x, c, xmid_ap, d_xmid, NT, sT, shT, h2T, h2_d)
        wg = [sbt(x, [128, 16, 256], BF16, "wg%d" % i, ls) for i in range(2)]
        wu = [sbt(x, [128, 16, 256], BF16, "wu%d" % i, ls) for i in range(2)]
        psg = [pst(x, [128, 512], F32, "psg%d" % i, ls) for i in range(2)]
        psu = [pst(x, [128, 512], F32, "psu%d" % i, ls) for i in range(2)]
        sg = [sbt(x, [128, 512], F32, "sg%d" % i, ls) for i in range(2)]
        ab = [sbt(x, [128, 4, 128], BF16, "ab%d" % i, ls) for i in range(3)]
        n = 0
        for blk in range(FFN // 256):
            wgt = wg[blk % 2]
            wut = wu[blk % 2]
            x.dma("pool", wgt[:], wgu_ap[:, blk * 256:(blk + 1) * 256].rearrange("(k p) f -> p k f", p=128), w=[wgt.d])
            x.dma("pool", wut[:], wgu_ap[:, FFN + blk * 256:FFN + (blk + 1) * 256].rearrange("(k p) f -> p k f", p=128),
                  w=[wut.d])
            for fl in range(2):
                fc = blk * 2 + fl
                for tb in range(TOK // 512):
                    pg = psg[n % 2]
                    pu = psu[n % 2]
                    sgg = sg[n % 2]
                    abb = ab[n % 3]
                    n += 1
                    hd = h2_d[tb * 4:tb * 4 + 4]
                    for kc in range(16):
                        x.op("pe", lambda e: e.matmul(pg[:], wgt[:, kc, fl * 128:(fl + 1) * 128],
                                                      h2T[:, kc, tb * 512:(tb + 1) * 512],
                                                      start=(kc == 0), stop=(kc == 15)),
                             r=hd + [wgt.d], w=[pg.d], inc=(kc == 15))
                    for kc in range(16):
                        x.op("pe", lambda e: e.matmul(pu[:], wut[:, kc, fl * 128:(fl + 1) * 128],
                                                      h2T[:, kc, tb * 512:(tb + 1) * 512],
                                                      start=(kc == 0), stop=(kc == 15)),
                             r=hd + [wut.d], w=[pu.d], inc=(kc == 15))
                    x.op("act", lambda e: e.activation(out=sgg[:], in_=pg[:], func=AF.Silu), r=[pg.d], w=[sgg.d])
                    x.op("dve", lambda e: e.tensor_tensor(out=abb[:].rearrange("p a b -> p (a b)"), in0=pu[:], in1=sgg[:],
                                                          op=ALU.mult), r=[pu.d, sgg.d], w=[abb.d])
                    x.dma("sp", act_ap[tb * 4:(tb + 1) * 4, :, fc, :].rearrange("t p k -> p t k"), abb[:],
                          r=[abb.d], mw=[d_act])
    with Scope(x) as ls:
        wdA = sbt(x, [128, 22, 512], BF16, "wdA", ls)
        wdB = sbt(x, [128, 22, 512], BF16, "wdB", ls)
        at = [sbt(x, [128, NFC, 128], BF16, "at%d" % i, ls) for i in range(3)]
        psd = [pst(x, [128, 512], F32, "psd%d" % i, ls) for i in range(2)]
        tm = [sbt(x, [128, 512], F32, "tmf%d" % i, ls) for i in range(2)]
        xt = [sbt(x, [128, 512], F32, "xtf%d" % i, ls) for i in range(2)]
        nd = 0
        for cb in range(4):
            x.dma("pool", wdA[:], wd_ap[0:22 * 128, cb * 512:(cb + 1) * 512].rearrange("(k p) f -> p k f", p=128),
                  w=[wdA.d])
            x.dma("pool", wdB[:], wd_ap[22 * 128:44 * 128, cb * 512:(cb + 1) * 512].rearrange("(k p) f -> p k f", p=128),
                  w=[wdB.d])
            for t in range(NT):
                tok0 = t * 128
                ps = psd[nd % 2]
                tmm = tm[nd % 2]
                xtt = xt[nd % 2]
                att = at[nd % 3]
                if nd == 0:
                    x.dma("sp", att[:], act_ap[0], r=[d_act], w=[att.d])
                if nd + 1 < 4 * NT:
                    x.dma("sp", at[(nd + 1) % 3][:], act_ap[(t + 1) % NT], r=[d_act], w=[at[(nd + 1) % 3].d])
                nd += 1
                x.dma("sp", xtt[:], xmid_ap[tok0:tok0 + 128, cb * 512:(cb + 1) * 512], r=[d_xmid], w=[xtt.d])
                for fc in range(NFC):
                    wt = wdA if fc < 22 else wdB
                    x.op("pe", lambda e: e.matmul(ps[:], att[:, fc, :], wt[:, fc % 22, :],
                                                  start=(fc == 0), stop=(fc == NFC - 1)),
                         r=[att.d, wt.d], w=[ps.d], inc=(fc == NFC - 1 or fc == 21))
                x.op("dve", lambda e: e.tensor_tensor(out=tmm[:], in0=ps[:], in1=gate_b[:, cb * 512:(cb + 1) * 512],
                                                      op=ALU.mult), r=[ps.d, gate_b.d], w=[tmm.d])
                x.op("pool", lambda e: e.tensor_tensor(out=tmm[:], in0=tmm[:], in1=xtt[:], op=ALU.add),
                     r=[tmm.d, xtt.d], w=[tmm.d])
                x.dma("act", xout_ap[tok0:tok0 + 128, cb * 512:(cb + 1) * 512], tmm[:], r=[tmm.d], mw=[d_xout])


def emit_LC(x, c, dram, FO):
    x_in = dram("x_in", [TOK, D], F32, "ExternalInput")
    o_in = dram("o_in", [TOK, FO], BF16, "ExternalInput")
    ada = dram("ada", [6 * D], F32, "ExternalInput")
    g2 = dram("g2", [D], F32, "ExternalInput")
    w_out = dram("w_out", [FO, D], F32, "ExternalInput")
    wgu = dram("wgu", [D, 2 * FFN], F32, "ExternalInput")
    wd = dram("wd", [FFN, D], F32, "ExternalInput")
    x_mid = dram("x_mid", [TOK, D], F32, "Internal")
    act_d = dram("act_d", [NT, 128, FFN // 128, 128], BF16, "Internal")
    x_out = dram("x_out", [TOK, D], F32, "ExternalOutput")
    with Scope(x) as st:
        d_none = x.mkdep("in")
        d_xmid = x.mkdep("xmid")
        d_xout = x.mkdep("xout")
        with Scope(x) as ls:
            g1b = load_bcast(x, ls, ada[2 * D:3 * D], D, "g1b")
            emit_outproj(x, c, o_in, d_none, FO, w_out, x_in, d_none, g1b, x_mid, d_xmid)
        with Scope(x) as ls:
            g2b = load_bcast(x, ls, ada[5 * D:6 * D], D, "g2b")
            sT, shT = emit_mod_cols(x, ls, g2, ada, d_none, 4, 3)
            emit_ffn(x, c, x_mid, d_xmid, sT, shT, g2b, wgu, wd, x_out, d_xout, act_d)


def la_handlers_ssd(x, c, ls, dram, outs_holder, dm=False):
    z = dram("z", [2, TOK, 2048] if dm else [TOK, 4096], BF16, "ExternalOutput")
    xbcT = dram("xbcT", [2, 3072, TOK] if dm else [6144, TOK], BF16, "ExternalOutput")
    dtr = dram("dtr", [2, TOK, 32] if dm else [TOK, 64], F32, "ExternalOutput")
    outs = [x.mkdep("z"), x.mkdep("xbcT"), x.mkdep("dtr")]
    outs_holder.extend(outs)
    zst = [sbt(x, [128, 512], BF16, "zst%d" % i, ls) for i in range(2)]
    fst = [sbt(x, [128, 512], BF16, "fst%d" % i, ls) for i in range(2)]
    dst_ = [sbt(x, [128, 64], F32, "dst%d" % i, ls) for i in range(2)]
    cnt = [0]
    blocks = []
    for zb in range(8):
        def zh(t, ps, zb=zb):
            s_ = zst[cnt[0] % 2]
            cnt[0] += 1
            x.op("act", lambda e: e.activation(out=s_[:], in_=ps[:, :], func=AF.Copy), r=[ps.d], w=[s_.d])
            zdst = (z[zb // 4, t * 128:(t + 1) * 128, (zb % 4) * 512:(zb % 4 + 1) * 512] if dm
                    else z[t * 128:(t + 1) * 128, zb * 512:(zb + 1) * 512])
            x.dma("sp", zdst, s_[:], r=[s_.d], mw=[outs[0]])
        blocks.append((zb * 512, 512, "tok", zh))
    for xb in range(12):
        def xh(fc, tb, ps, xb=xb):
            s_ = fst[cnt[0] % 2]
            cnt[0] += 1
            x.op("act", lambda e: e.activation(out=s_[:], in_=ps[:, :], func=AF.Copy), r=[ps.d], w=[s_.d])
            if dm:
                dd, rb = ((xb // 4, (xb % 4) * 512) if xb < 8 else ((xb - 8) % 2, 2048 + ((xb - 8) // 2) * 512))
                xdst = xbcT[dd, rb + fc * 128:rb + fc * 128 + 128, tb * 512:(tb + 1) * 512]
            else:
                r0 = xb * 512 + fc * 128
                xdst = xbcT[r0:r0 + 128, tb * 512:(tb + 1) * 512]
            x.dma("sp", xdst, s_[:], r=[s_.d], mw=[outs[1]])
        blocks.append((4096 + xb * 512, 512, "feat", xh))

    def dh(t, ps):
        s_ = dst_[cnt[0] % 2]
        cnt[0] += 1
        x.op("act", lambda e: e.activation(out=s_[:], in_=ps[:, 0:64], func=AF.Copy), r=[ps.d], w=[s_.d])
        if dm:
            for dd in range(2):
                x.dma("sp", dtr[dd, t * 128:(t + 1) * 128, :], s_[:, dd * 32:(dd + 1) * 32], r=[s_.d], mw=[outs[2]])
        else:
            x.dma("sp", dtr[t * 128:(t + 1) * 128, :], s_[:], r=[s_.d], mw=[outs[2]])
    blocks.append((10240, 64, "tok", dh))
    return blocks


def emit_LB_SSD(x, c, dram):
    NHh = 32
    NCH = 24
    raw = dram("raw", [NCH * 128, S], F32, "ExternalInput")
    convw = dram("convw", [4, NCH * 128], F32, "ExternalInput")
    convb = dram("convb", [NCH * 128], F32, "ExternalInput")
    dtr = dram("dtr", [S, NHh], F32, "ExternalInput")
    hp = dram("hp", [3, NHh], F32, "ExternalInput")
    z = dram("z", [S, 2048], BF16, "ExternalInput")
    ng = dram("ng", [2048], F32, "ExternalInput")
    tokd = dram("tokd", [S, 2560], BF16, "Internal")
    featd = dram("featd", [1024, S], BF16, "Internal")
    y = dram("y", [S, 2048], BF16, "ExternalOutput")
    with Scope(x) as st:
        d_in = x.mkdep("in")
        d_tok = x.mkdep("tokd")
        d_feat = x.mkdep("featd")
        d_y = x.mkdep("y")
        with Scope(x) as ls:
            cw = sbt(x, [128, 4, NCH], F32, "cw", ls)
            cb_ = sbt(x, [128, NCH], F32, "cb", ls)
            for j in range(4):
                x.dma("sp", cw[:, j, :], convw[j].rearrange("(k p) -> p k", p=128), mw=[cw.d],
                      allow_slow_non_contiguous=True)
            x.dma("sp", cb_[:], convb.rearrange("(k p) -> p k", p=128), w=[cb_.d], allow_slow_non_contiguous=True)
            rw = [sbt(x, [128, S + 3], F32, "rw%d" % i, ls) for i in range(2)]
            for i in range(2):
                x.op("pool", lambda e: e.memset(rw[i][:, 0:3], 0.0), w=[rw[i].d])
            acc = sbt(x, [128, S], F32, "acc", ls)
            sil = [sbt(x, [128, S], BF16, "sil%d" % i, ls) for i in range(2)]
            ptc = [pst(x, [128, 512], BF16, "ptc%d" % i, ls) for i in range(2)]
            stg = [sbt(x, [128, 4, 128], BF16, "stg%d" % i, ls) for i in range(2)]
            n = 0
            x.dma("sp", rw[0][:, 3:3 + S], raw[0:128, :], r=[d_in], mw=[rw[0].d])
            for cc in range(NCH):
                r_ = rw[cc % 2]
                sl_ = sil[cc % 2]
                if cc + 1 < NCH:
                    x.dma("sp", rw[(cc + 1) % 2][:, 3:3 + S], raw[(cc + 1) * 128:(cc + 2) * 128, :], r=[d_in],
                          mw=[rw[(cc + 1) % 2].d])
                x.op("dve", lambda e: e.tensor_scalar(out=acc[:], in0=r_[:, 3:3 + S], scalar1=cw[:, 3, cc:cc + 1],
                                                      scalar2=cb_[:, cc:cc + 1], op0=ALU.mult, op1=ALU.add),
                     r=[r_.d, cw.d, cb_.d], w=[acc.d])
                for j in range(3):
                    x.op("dve", lambda e: e.scalar_tensor_tensor(out=acc[:], in0=r_[:, j:j + S],
                                                                 scalar=cw[:, j, cc:cc + 1], in1=acc[:],
                                                                 op0=ALU.mult, op1=ALU.add),
                         r=[r_.d, cw.d, acc.d], w=[acc.d])
                x.op("act", lambda e: e.activation(out=sl_[:], in_=acc[:], func=AF.Silu), r=[acc.d], w=[sl_.d])
                if cc >= 16:
                    x.dma("sp", featd[(cc - 16) * 128:(cc - 15) * 128, :], sl_[:], r=[sl_.d], mw=[d_feat])
                if cc < 20:
                    for tg in range(8):
                        p = ptc[n % 2]
                        sg_ = stg[n % 2]
                        n += 1
                        for j in range(4):
                            tt = tg * 4 + j
                            x.op("pe", lambda e: e.transpose(p[:, j * 128:(j + 1) * 128], sl_[:, tt * 128:(tt + 1) * 128],
                                                             c.ident[:]), r=[sl_.d, c.ident.d], w=[p.d], inc=(j == 3))
                        if n % 2 == 0:
                            x.op("dve", lambda e: e.tensor_copy(sg_[:], p[:, :].rearrange("p (a b) -> p a b", a=4)),
                                 r=[p.d], w=[sg_.d])
                        else:
                            x.op("act", lambda e: e.activation(out=sg_[:], in_=p[:, :].rearrange("p (a b) -> p a b", a=4),
                                                               func=AF.Copy), r=[p.d], w=[sg_.d])
                        x.dma("sp", tokd[tg * 512:(tg + 1) * 512, cc * 128:(cc + 1) * 128].rearrange("(j p) c -> p j c", p=128),
                              sg_[:], r=[sg_.d], mw=[d_tok])
        with Scope(x) as ls:
            hb = [load_bcast(x, ls, hp[i], NHh, "hp%d" % i) for i in range(3)]
            dtb_b, alog_b, dsk_b = hb
            a_b = sbt(x, [128, NHh], F32, "a_b", ls)
            x.op("act", lambda e: e.activation(out=a_b[:], in_=alog_b[:], func=AF.Exp), r=[alog_b.d], w=[a_b.d])
            x.op("dve", lambda e: e.tensor_scalar(out=a_b[:], in0=a_b[:], scalar1=-1.0, scalar2=None, op0=ALU.mult),
                 r=[a_b.d], w=[a_b.d])
            ngb = load_bcast(x, ls, ng, 2048, "ngb")
            sel = sbt(x, [32, NHh, 128], F32, "sel", ls)
            x.op("pool", lambda e: e.memset(sel[:], 1.0), w=[sel.d])
            x.op("pool", lambda e: e.affine_select(out=sel[:], in_=sel[:], pattern=[[-1, NHh], [0, 128]],
                                                   compare_op=ALU.is_equal, fill=0.0, base=0, channel_multiplier=1),
                 r=[sel.d], w=[sel.d])
            St = [sbt(x, [128, 512], F32, "St%d" % g, ls) for g in range(4)]
            Sb = [sbt(x, [128, 512], BF16, "Sb%d" % g, ls) for g in range(4)]
            for g in range(4):
                x.op("pool", lambda e: e.memset(St[g][:], 0.0), w=[St[g].d])
                x.op("pool", lambda e: e.memset(Sb[g][:], 0.0), w=[Sb[g].d])
            xs_t = [sbt(x, [128, 2560], BF16, "xs_t%d" % i, ls) for i in range(2)]
            bct = [sbt(x, [128, 8, 128], BF16, "bct%d" % i, ls) for i in range(2)]
            dtt = [sbt(x, [128, NHh], F32, "dtt%d" % i, ls) for i in range(2)]
            zt = [sbt(x, [128, 2048], BF16, "zt%d" % i, ls) for i in range(2)]
            f = lambda nm, w_: sbt(x, [128, w_], F32, nm, ls)
            dtb, ab, ee, dt_, dta, acum, nacum, alast, eac, wend, decay = [f(nm, NHh) for nm in
                ("dtb", "ab", "ee", "dt_", "dta", "acum", "nacum", "alast", "eac", "wend", "decay")]
            acT = sbt(x, [32, 128], F32, "acT", ls)
            xdt = sbt(x, [128, 2048], BF16, "xdt", ls)
            xde = sbt(x, [128, 2048], BF16, "xde", ls)
            cbm = [sbt(x, [128, 128], F32, "cbm%d" % g, ls) for g in range(4)]
            Eh = [sbt(x, [128, 128], F32, "Eh%d" % i, ls) for i in range(4)]
            Mh = [sbt(x, [128, 128], BF16, "Mh%d" % i, ls) for i in range(4)]
            yf = sbt(x, [128, 2048], F32, "yf", ls)
            t1 = sbt(x, [128, 2048], F32, "t1", ls)
            sqy = sbt(x, [128, 2048], F32, "sqy", ls)
            ssy = sbt(x, [128, 4], F32, "ssy", ls)
            rsy = sbt(x, [128, 4], F32, "rsy", ls)
            yb = [sbt(x, [128, 2048], BF16, "yb%d" % i, ls) for i in range(2)]
            p_small = pst(x, [128, 512], F32, "p_small", ls)
            p_cb = pst(x, [128, 512], F32, "p_cb", ls)
            p_G = [pst(x, [128, 512], F32, "p_G%d" % i, ls) for i in range(2)]
            p_y = [pst(x, [128, 512], F32, "p_y%d" % i, ls) for i in range(2)]
            p_i = pst(x, [128, 512], F32, "p_i", ls)
            p_s = pst(x, [128, 512], F32, "p_s", ls)
            def load_chunk(ck_):
                i_ = ck_ % 2
                t0_ = ck_ * 128
                x.dma("sp", xs_t[i_][:], tokd[t0_:t0_ + 128, :], r=[d_tok], w=[xs_t[i_].d])
                x.dma("sp", bct[i_][:], featd[:, t0_:t0_ + 128].rearrange("(g p) t -> p g t", p=128), r=[d_feat],
                      w=[bct[i_].d])
                x.dma("sp", dtt[i_][:], dtr[t0_:t0_ + 128, :], r=[d_in], w=[dtt[i_].d])
                x.dma("sp", zt[i_][:], z[t0_:t0_ + 128, :], r=[d_in], w=[zt[i_].d])

            load_chunk(0)
            for ck in range(S // 128):
                i = ck % 2
                t0 = ck * 128
                xt_ = xs_t[i]
                bc_ = bct[i]
                if ck + 1 < S // 128:
                    load_chunk(ck + 1)
                x.op("dve", lambda e: e.tensor_tensor(out=dtb[:], in0=dtt[i][:], in1=dtb_b[:], op=ALU.add),
                     r=[dtt[i].d, dtb_b.d], w=[dtb.d])
                x.op("dve", lambda e: e.scalar_tensor_tensor(out=ab[:], in0=dtb[:], scalar=-1.0, in1=dtb[:],
                                                             op0=ALU.mult, op1=ALU.max), r=[dtb.d], w=[ab.d])
                x.op("act", lambda e: e.activation(out=ee[:], in_=ab[:], func=AF.Exp, scale=-1.0), r=[ab.d], w=[ee.d])
                x.op("act", lambda e: e.activation(out=ee[:], in_=ee[:], func=AF.Ln, bias=1.0), r=[ee.d], w=[ee.d])
                x.op("dve", lambda e: e.scalar_tensor_tensor(out=dt_[:], in0=dtb[:], scalar=0.0, in1=ee[:],
                                                             op0=ALU.max, op1=ALU.add), r=[dtb.d, ee.d], w=[dt_.d])
                x.op("dve", lambda e: e.tensor_tensor(out=dta[:], in0=dt_[:], in1=a_b[:], op=ALU.mult),
                     r=[dt_.d, a_b.d], w=[dta.d])
                x.op("pe", lambda e: e.matmul(p_small[:, 0:32], c.trif[:], dta[:], start=True, stop=True),
                     r=[c.trif.d, dta.d], w=[p_small.d])
                x.op("pe", lambda e: e.matmul(p_small[:, 32:64], c.onesf[:], dta[:], start=True, stop=True),
                     r=[c.onesf.d, dta.d], w=[p_small.d])
                x.op("dve", lambda e: e.tensor_copy(acum[:], p_small[:, 0:32]), r=[p_small.d], w=[acum.d])
                x.op("dve", lambda e: e.tensor_scalar(out=nacum[:], in0=p_small[:, 0:32], scalar1=-1.0, scalar2=None,
                                                      op0=ALU.mult), r=[p_small.d], w=[nacum.d])
                x.op("dve", lambda e: e.tensor_copy(alast[:], p_small[:, 32:64]), r=[p_small.d], w=[alast.d])
                x.op("act", lambda e: e.activation(out=eac[:], in_=acum[:], func=AF.Exp), r=[acum.d], w=[eac.d])
                x.op("act", lambda e: e.activation(out=decay[:], in_=alast[:], func=AF.Exp), r=[alast.d], w=[decay.d])
                x.op("dve", lambda e: e.tensor_tensor(out=wend[:], in0=alast[:], in1=acum[:], op=ALU.subtract),
                     r=[alast.d, acum.d], w=[wend.d])
                x.op("act", lambda e: e.activation(out=wend[:], in_=wend[:], func=AF.Exp), r=[wend.d], w=[wend.d])
                x.op("dve", lambda e: e.tensor_tensor(out=wend[:], in0=wend[:], in1=dt_[:], op=ALU.mult),
                     r=[wend.d, dt_.d], w=[wend.d])
                x.op("pe", lambda e: e.matmul(p_small[0:32, 128:256], acum[:], c.identf[:], start=True, stop=True),
                     r=[acum.d, c.identf.d], w=[p_small.d])
                x.op("dve", lambda e: e.tensor_copy(acT[:], p_small[0:32, 128:256]), r=[p_small.d], w=[acT.d])
                xv = xt_[:, 0:2048].rearrange("p (h d) -> p h d", d=64)
                x.op("dve", lambda e: e.tensor_tensor(out=xdt[:].rearrange("p (h d) -> p h d", d=64), in0=xv,
                                                      in1=bc(dt_[:].unsqueeze(2), [128, NHh, 64]), op=ALU.mult),
                     r=[xt_.d, dt_.d], w=[xdt.d])
                x.op("pool", lambda e: e.tensor_tensor(out=xde[:].rearrange("p (h d) -> p h d", d=64), in0=xv,
                                                       in1=bc(wend[:].unsqueeze(2), [128, NHh, 64]), op=ALU.mult),
                     r=[xt_.d, wend.d], w=[xde.d])
                for g in range(4):
                    x.op("pe", lambda e: e.matmul(p_cb[:, g * 128:(g + 1) * 128], bc_[:, g, :], bc_[:, 4 + g, :],
                                                  start=True, stop=True), r=[bc_.d], w=[p_cb.d], inc=(g == 3))
                for g in range(4):
                    if g % 2 == 0:
                        x.op("dve", lambda e: e.tensor_copy(cbm[g][:], p_cb[:, g * 128:(g + 1) * 128]),
                             r=[p_cb.d], w=[cbm[g].d])
                    else:
                        x.op("act", lambda e: e.activation(out=cbm[g][:], in_=p_cb[:, g * 128:(g + 1) * 128], func=AF.Copy),
                             r=[p_cb.d], w=[cbm[g].d])
                for g in range(4):
                    py = p_y[g % 2]
                    x.op("pe", lambda e: e.matmul(p_i[:], bc_[:, 4 + g, :], Sb[g][:], start=True, stop=True),
                         r=[bc_.d, Sb[g].d], w=[p_i.d])
                    def hG(hl):
                        h = g * 8 + hl
                        pgt = p_G[(h % 4) // 2]
                        pgs = slice((h % 2) * 128, (h % 2) * 128 + 128)
                        x.op("pe", lambda e: e.matmul(pgt[:, pgs], sel[:, h, :], acT[:], start=True, stop=False),
                             r=[sel.d, acT.d], w=[pgt.d], inc=False)
                        x.op("pe", lambda e: e.matmul(pgt[:, pgs], c.ident[:], c.negT[:], start=False, stop=True),
                             r=[c.ident.d, c.negT.d], w=[pgt.d])
                        eh = Eh[h % 4]
                        mh = Mh[h % 4]
                        x.op("act", lambda e: e.activation(out=eh[:], in_=pgt[:, pgs], func=AF.Exp,
                                                           bias=nacum[:, h:h + 1]), r=[pgt.d, nacum.d], w=[eh.d])
                        x.op("dve", lambda e: e.tensor_tensor(out=mh[:], in0=eh[:], in1=cbm[g][:], op=ALU.mult),
                             r=[eh.d, cbm[g].d], w=[mh.d])

                    def hY(hl):
                        h = g * 8 + hl
                        mh = Mh[h % 4]
                        x.op("pe", lambda e: e.matmul(py[:, hl * 64:(hl + 1) * 64], mh[:], xdt[:, h * 64:(h + 1) * 64],
                                                      start=True, stop=True), r=[mh.d, xdt.d], w=[py.d])

                    hG(0)
                    hG(1)
                    for hl in range(8):
                        if hl + 2 < 8:
                            hG(hl + 2)
                        hY(hl)
                    gs_ = slice(g * 512, (g + 1) * 512)
                    x.op("dve", lambda e: e.tensor_tensor(out=t1[:, gs_].rearrange("p (h d) -> p h d", d=64),
                                                          in0=p_i[:].rearrange("p (h d) -> p h d", d=64),
                                                          in1=bc(eac[:, g * 8:(g + 1) * 8].unsqueeze(2), [128, 8, 64]),
                                                          op=ALU.mult), r=[p_i.d, eac.d], mw=[t1.d])
                    x.op("dve", lambda e: e.tensor_tensor(out=yf[:, gs_], in0=py[:], in1=t1[:, gs_], op=ALU.add),
                         r=[py.d, t1.d], mw=[yf.d])
                    x.op("pe", lambda e: e.matmul(p_s[:], xt_[:, 2048 + g * 128:2048 + (g + 1) * 128], xde[:, gs_],
                                                  start=True, stop=True), r=[xt_.d, xde.d], w=[p_s.d])
                    x.op("pool", lambda e: e.tensor_tensor(out=St[g][:].rearrange("p (h d) -> p h d", d=64),
                                                           in0=St[g][:].rearrange("p (h d) -> p h d", d=64),
                                                           in1=bc(decay[:, g * 8:(g + 1) * 8].unsqueeze(2), [128, 8, 64]),
                                                           op=ALU.mult), r=[St[g].d, decay.d], w=[St[g].d])
                    x.op("dve", lambda e: e.tensor_tensor(out=St[g][:], in0=p_s[:], in1=St[g][:], op=ALU.add),
                         r=[p_s.d, St[g].d], w=[St[g].d])
                    x.op("act", lambda e: e.activation(out=Sb[g][:], in_=St[g][:], func=AF.Copy),
                         r=[St[g].d], w=[Sb[g].d])
                x.op("pool", lambda e: e.tensor_tensor(out=t1[:].rearrange("p (h d) -> p h d", d=64), in0=xv,
                                                       in1=bc(dsk_b[:].unsqueeze(2), [128, NHh, 64]), op=ALU.mult),
                     r=[xt_.d, dsk_b.d, yf.d], w=[t1.d])
                x.op("dve", lambda e: e.tensor_tensor(out=yf[:], in0=yf[:], in1=t1[:], op=ALU.add),
                     r=[yf.d, t1.d], w=[yf.d])
                x.op("act", lambda e: e.activation(out=t1[:], in_=zt[i][:], func=AF.Silu), r=[zt[i].d, yf.d], w=[t1.d])
                x.op("dve", lambda e: e.tensor_tensor(out=yf[:], in0=yf[:], in1=t1[:], op=ALU.mult),
                     r=[yf.d, t1.d], w=[yf.d])
                x.op("act", lambda e: e.activation(out=sqy[:], in_=yf[:], func=AF.Square), r=[yf.d], w=[sqy.d])
                x.op("dve", lambda e: e.tensor_reduce(out=ssy[:], in_=sqy[:].rearrange("p (g d) -> p g d", g=4),
                                                      axis=AX.X, op=ALU.add), r=[sqy.d], w=[ssy.d])
                rsqrt_mean(x, c, rsy, ssy, 512, 4)
                x.op("dve", lambda e: e.tensor_tensor(out=yf[:].rearrange("p (g d) -> p g d", g=4),
                                                      in0=yf[:].rearrange("p (g d) -> p g d", g=4),
                                                      in1=bc(rsy[:].unsqueeze(2), [128, 4, 512]), op=ALU.mult),
                     r=[yf.d, rsy.d], w=[yf.d])
                x.op("pool", lambda e: e.tensor_tensor(out=yb[i][:], in0=yf[:], in1=ngb[:], op=ALU.mult),
                     r=[yf.d, ngb.d], w=[yb[i].d])
                x.dma("sp", y[t0:t0 + 128, :], yb[i][:], r=[yb[i].d], mw=[d_y])


def la_handlers_dsa(x, c, ls, dram, outs_holder, pos, dm=False):
    invf16 = dram("invf16", [128, 16], F32, "ExternalInput")
    invf8 = dram("invf8", [128, 8], F32, "ExternalInput")
    gq = dram("gq", [128], F32, "ExternalInput")
    gk = dram("gk", [128], F32, "ExternalInput")
    gi = dram("gi", [64], F32, "ExternalInput")
    qT = dram("qT", [2, 16, 128, 8, 128] if dm else [16, 128, TOK], BF16, "ExternalOutput")
    kT = dram("kT", [4, 128, TOK], BF16, "ExternalOutput")
    v = dram("v", [TOK, 512], BF16, "ExternalOutput")
    qiT = dram("qiT", [2, 8, 128, 8, 128] if dm else [8, 128, TOK], BF16, "ExternalOutput")
    kiT = dram("kiT", [64, TOK], BF16, "ExternalOutput")
    wi = dram("wi", [2, 8, 128, 16] if dm else [TOK, 16], F32, "ExternalOutput")
    outs = [x.mkdep(n) for n in ("qT", "kT", "v", "qiT", "kiT", "wi")]
    outs_holder.extend(outs)
    cos16, sin16 = emit_rope_tables(x, ls, pos, invf16, 16, NT)
    cos8, sin8 = emit_rope_tables(x, ls, pos, invf8, 8, NT)
    gqb = load_bcast(x, ls, gq, 128, "gqb")
    gkb = load_bcast(x, ls, gk, 128, "gkb")
    gib = load_bcast(x, ls, gi, 64, "gib")
    qk_tiles = alloc_qk_tiles(x, ls)
    qbf = sbt(x, [128, 1024], BF16, "qbf", ls)
    stage = [sbt(x, [128, 4, TOK], BF16, "stage%d" % i, ls) for i in range(2)]
    kist = sbt(x, [64, TOK], BF16, "kist", ls)
    ptr = [pst(x, [128, 512], BF16, "ptr%d" % i, ls) for i in range(2)]
    vst = [sbt(x, [128, 512], BF16, "vst%d" % i, ls) for i in range(2)]
    wst = [sbt(x, [128, 16], F32, "wst%d" % i, ls) for i in range(2)]
    cnt = [0]
    nst = [0]
    blocks = []

    def mk_qk(col0, gdim, gain, half, cos, sin, dst_ap, dep, dmh=None):
        sg = stage[nst[0] % 2]
        nst[0] += 1

        def handler(t, ps):
            emit_qk_post2(x, c, qk_tiles, ps, gdim, gain, half, cos, sin, t, qbf)
            for a_ in range(2):
                p = ptr[cnt[0] % 2]
                cnt[0] += 1
                for j in range(4):
                    x.op("pe", lambda e: e.transpose(p[:, j * 128:(j + 1) * 128],
                                                     qbf[:, a_ * 512 + j * 128:a_ * 512 + (j + 1) * 128], c.ident[:]),
                         r=[qbf.d, c.ident.d], w=[p.d], inc=(j == 3))
                x.op("dve", lambda e: e.tensor_copy(sg[:, :, (t + a_) * 128:(t + a_ + 1) * 128],
                                                    p[:, :].rearrange("p (a b) -> p a b", a=4)), r=[p.d], mw=[sg.d])
            if t == NT - 2:
                if dmh is None:
                    x.dma("sp", dst_ap.rearrange("h p t -> p h t"), sg[:], r=[sg.d], mw=[dep])
                else:
                    tens, h0 = dmh
                    sgv = sg[:].rearrange("p h (k a t) -> p h k a t", a=2, t=128)
                    for a2 in range(2):
                        for hh_ in range(4):
                            x.dma("sp", tens[a2, h0 + hh_].rearrange("d k t -> d k t"), sgv[:, hh_, :, a2, :],
                                  r=[sg.d], mw=[dep])
        blocks.append((col0, 512, "tok2", handler))
    for hb in range(4):
        mk_qk(hb * 512, 128, gqb, 16, cos16, sin16, None if dm else qT[hb * 4:(hb + 1) * 4], outs[0],
              (qT, hb * 4) if dm else None)
    mk_qk(2048, 128, gkb, 16, cos16, sin16, kT[0:4], outs[1])

    def vh(t, ps):
        vs = vst[cnt[0] % 2]
        cnt[0] += 1
        x.op("act", lambda e: e.activation(out=vs[:], in_=ps[:, :], func=AF.Copy), r=[ps.d], w=[vs.d])
        x.dma("sp", v[t * 128:(t + 1) * 128, :], vs[:], r=[vs.d], mw=[outs[2]])
    blocks.append((2560, 512, "tok", vh))
    for qb in range(2):
        mk_qk(3072 + qb * 512, 64, None, 8, cos8, sin8, None if dm else qiT[qb * 4:(qb + 1) * 4], outs[3],
              (qiT, qb * 4) if dm else None)

    def kwh(t, ps):
        emit_qk_post(x, c, qk_tiles, ps, 64, 64, gib, 8, cos8, sin8, t, qbf)
        p = ptr[cnt[0] % 2]
        ws = wst[cnt[0] % 2]
        cnt[0] += 1
        x.op("pe", lambda e: e.transpose(p[0:64, 0:128], qbf[:, 0:64], c.ident[:]), r=[qbf.d, c.ident.d], w=[p.d])
        x.op("dve", lambda e: e.tensor_copy(kist[:, t * 128:(t + 1) * 128], p[0:64, 0:128]), r=[p.d], mw=[kist.d])
        x.op("act", lambda e: e.activation(out=ws[:], in_=ps[:, 64:80], func=AF.Copy, scale=0.25), r=[ps.d], w=[ws.d])
        x.dma("sp", wi[t % 2, t // 2] if dm else wi[t * 128:(t + 1) * 128, :], ws[:], r=[ws.d], mw=[outs[5]])
        if t == NT - 1:
            x.dma("sp", kiT, kist[:], r=[kist.d], mw=[outs[4]])
    blocks.append((4096, 80, "tok", kwh))
    return blocks


def emit_LB_DSA(x, c, dram):
    NS = 16
    qTs = dram("qTs", [NS, 128, 2048], BF16, "ExternalInput")
    qiTs = dram("qiTs", [NS, 128, 1024], BF16, "ExternalInput")
    wis = dram("wis", [NS, 128, 16], F32, "ExternalInput")
    kT = dram("kT", [4, 128, S], BF16, "ExternalInput")
    v = dram("v", [S, 512], BF16, "ExternalInput")
    kiT2 = dram("kiT2", [128, S], BF16, "ExternalInput")
    dmask = dram("dmask", [2, 128, 128], F32, "ExternalInput")
    gqk = dram("gqk", [2, 128], F32, "ExternalInput")
    o = dram("o", [NS * 128, 2048], BF16, "ExternalOutput")
    SCALE = 128 ** -0.5
    with Scope(x) as st:
        d_in = x.mkdep("in")
        d_o = x.mkdep("o")
        ls = st
        gt = [load_bcast(x, ls, gqk[i], 128, "gqk%d" % i) for i in range(2)]
        gm = sbt(x, [128, 2], F32, "gm")
        negC = sbt(x, [128, 1], F32, "negC")
        for i in range(2):
            x.op("dve", lambda e: e.tensor_reduce(out=gm[:, i:i + 1], in_=gt[i][:], axis=AX.X, op=ALU.max,
                                                  apply_absolute_value=True), r=[gt[i].d], w=[gm.d])
        x.op("dve", lambda e: e.scalar_tensor_tensor(out=negC[:], in0=gm[:, 0:1], scalar=-(128 ** 0.5), in1=gm[:, 1:2],
                                                     op0=ALU.mult, op1=ALU.mult), r=[gm.d], w=[negC.d])
        kts = sbt(x, [128, 4, S], BF16, "kts")
        x.dma("sp", kts[:], kT.rearrange("g p t -> p g t"), w=[kts.d])
        va = sbt(x, [128, 32, 4, 129], BF16, "va")
        x.op("pool", lambda e: e.memset(va[:, :, :, 128:129], 1.0), w=[va.d])
        for g in range(4):
            x.dma("sp", va[:, :, g, 0:128], v[:, g * 128:(g + 1) * 128].rearrange("(kb p) d -> p kb d", p=128),
                  mw=[va.d])
        ki2 = sbt(x, [128, S], BF16, "ki2")
        x.dma("sp", ki2[:], kiT2, w=[ki2.d])
        dm = sbt(x, [128, 2, 128], F32, "dm")
        x.dma("sp", dm[:], dmask.rearrange("a p k -> p a k"), w=[dm.d])
        zr = sbt(x, [128, 512], BF16, "zr")
        x.op("pool", lambda e: e.memset(zr[:], 0.0), w=[zr.d])
        acc = sbt(x, [128, S], F32, "acc")
        work = sbt(x, [128, S], F32, "work")
        nb = sbt(x, [128, S], BF16, "nb")
        nbT4 = sbt(x, [128, 32, 4, 128], BF16, "nbT4")
        qs = [sbt(x, [128, 2048], BF16, "qs%d" % i) for i in range(2)]
        qis = [sbt(x, [128, 8, 128], BF16, "qis%d" % i) for i in range(2)]
        wt = [sbt(x, [128, 16], F32, "wt%d" % i) for i in range(2)]
        aw = sbt(x, [128, 16], F32, "aw")
        sgn = sbt(x, [128, 16], F32, "sgn")
        rr = [sbt(x, [128, 512], F32, "rr%d" % i) for i in range(2)]
        m8 = sbt(x, [128, 8], F32, "m8")
        thr = sbt(x, [128, 1], F32, "thr")
        thr0 = sbt(x, [128, 1], F32, "thr0")
        x.op("pool", lambda e: e.memset(thr0[:], -1e29), w=[thr0.d])
        P = [sbt(x, [128, 512], BF16, "P%d" % i) for i in range(2)]
        R = sbt(x, [128, 4], F32, "R")
        ob = [sbt(x, [128, 2048], BF16, "ob%d" % i) for i in range(2)]
        p_ix = [pst(x, [128, 512], F32, "p_ix%d" % i) for i in range(2)]
        p_tr = [pst(x, [128, 512], BF16, "p_tr%d" % i) for i in range(2)]
        p_s = [pst(x, [128, 512], F32, "p_s%d" % i) for i in range(2)]
        p_o = pst(x, [128, 4, 256], F32, "p_o")
        p_ob = p_o[:].rearrange("p a b -> p (a b)")
        nix = 0
        ns_ = 0
        def part_A(i):
            nonlocal nix
            b2 = i % 2
            nkb = 2 * i + 2
            L = nkb * 128
            x.dma("sp", qs[b2][:], qTs[i], r=[d_in], w=[qs[b2].d])
            x.dma("sp", qis[b2][:], qiTs[i].rearrange("p (a t) -> p a t", a=8), r=[d_in], w=[qis[b2].d])
            x.dma("sp", wt[b2][:], wis[i], r=[d_in], w=[wt[b2].d])
            w_ = wt[b2]
            x.op("dve", lambda e: e.scalar_tensor_tensor(out=aw[:], in0=w_[:], scalar=-1.0, in1=w_[:],
                                                         op0=ALU.mult, op1=ALU.max), r=[w_.d], w=[aw.d])
            x.op("dve", lambda e: e.tensor_scalar(out=aw[:], in0=aw[:], scalar1=0.125, scalar2=None, op0=ALU.mult),
                 r=[aw.d], w=[aw.d])
            x.op("dve", lambda e: e.tensor_scalar(out=sgn[:], in0=w_[:], scalar1=0.0, scalar2=2.0,
                                                  op0=ALU.is_ge, op1=ALU.mult), r=[w_.d], w=[sgn.d])
            x.op("dve", lambda e: e.tensor_scalar(out=sgn[:], in0=sgn[:], scalar1=-1.0, scalar2=None, op0=ALU.add),
                 r=[sgn.d], w=[sgn.d])
            for kq in range((L + 511) // 512):
                W = min(512, L - kq * 512)
                cs = slice(kq * 512, kq * 512 + W)
                for hi in range(16):
                    ps = p_ix[nix % 2]
                    r_ = rr[nix % 2]
                    nix += 1
                    pr_ = slice((hi % 2) * 64, (hi % 2) * 64 + 64)
                    x.op("pe", lambda e: e.matmul(ps[:, 0:W], qis[b2][pr_, hi // 2, :], ki2[pr_, cs], start=True, stop=True),
                         r=[qis[b2].d, ki2.d], w=[ps.d])
                    x.op("act", lambda e: e.activation(out=r_[:, 0:W], in_=ps[:, 0:W], func=AF.Relu, scale=aw[:, hi:hi + 1]),
                         r=[ps.d, aw.d], w=[r_.d])
                    if hi == 0:
                        x.op("dve", lambda e: e.tensor_scalar(out=acc[:, cs], in0=r_[:, 0:W], scalar1=sgn[:, 0:1],
                                                              scalar2=None, op0=ALU.mult), r=[r_.d, sgn.d], w=[acc.d])
                    else:
                        x.op("dve", lambda e: e.scalar_tensor_tensor(out=acc[:, cs], in0=r_[:, 0:W], scalar=sgn[:, hi:hi + 1],
                                                                     in1=acc[:, cs], op0=ALU.mult, op1=ALU.add),
                             r=[r_.d, sgn.d, acc.d], w=[acc.d])
            for a in range(2):
                ks = slice((nkb - 2 + a) * 128, (nkb - 1 + a) * 128)
                x.op("dve", lambda e: e.tensor_tensor(out=acc[:, ks], in0=acc[:, ks], in1=dm[:, a, :], op=ALU.add),
                     r=[acc.d, dm.d], w=[acc.d])
            if i >= 1:
                x.op("pool", lambda e: e.tensor_copy(work[:, 0:L], acc[:, 0:L]), r=[acc.d], w=[work.d])
                for rd in range(32):
                    x.op("dve", lambda e: e.max(out=m8[:], in_=work[:, 0:L]), r=[work.d], w=[m8.d])
                    if rd < 31:
                        x.op("dve", lambda e: e.match_replace(out=work[:, 0:L], in_to_replace=m8[:], in_values=work[:, 0:L],
                                                              imm_value=-1e30), r=[m8.d, work.d], w=[work.d])
                x.op("dve", lambda e: e.tensor_copy(thr[:], m8[:, 7:8]), r=[m8.d], w=[thr.d])
                th = thr
            else:
                th = thr0
            x.op("dve", lambda e: e.tensor_scalar(out=nb[:, 0:L], in0=acc[:, 0:L], scalar1=th[:, 0:1], scalar2=NEG,
                                                  op0=ALU.is_lt, op1=ALU.mult), r=[acc.d, th.d], w=[nb.d])

        def part_T(i):
            b2 = i % 2
            nkb = 2 * i + 2
            L = nkb * 128
            for kg in range((nkb + 3) // 4):
                p = p_tr[kg % 2]
                nn = min(4, nkb - kg * 4)
                for j in range(nn):
                    kb = kg * 4 + j
                    x.op("pe", lambda e: e.transpose(p[:, j * 128:(j + 1) * 128], nb[:, kb * 128:(kb + 1) * 128], c.ident[:]),
                         r=[nb.d, c.ident.d], w=[p.d], inc=(j == nn - 1))
                src = p[:, 0:nn * 128].rearrange("p (a b) -> p a b", a=nn)
                x.op("act", lambda e: e.activation(out=nbT4[:, kg * 4:kg * 4 + nn, :, :],
                                                   in_=bc(src.unsqueeze(2), [128, nn, 4, 128]), func=AF.Copy),
                     r=[p.d], w=[nbT4.d])

        def part_C(i):
            b2 = i % 2
            nkb = 2 * i + 2
            L = nkb * 128
            obb = ob[b2]
            asteps = [(g, kb) for g in range(4) for kb in range(nkb)]

            def a_scores(idx):
                g, kb = asteps[idx]
                ps = p_s[idx % 2]
                pp_ = P[idx % 2]
                x.op("pe", lambda e: e.matmul(ps[:], kts[:, g, kb * 128:(kb + 1) * 128],
                                              qs[b2][:, g * 512:(g + 1) * 512], start=True, stop=False),
                     r=[kts.d, qs[b2].d], w=[ps.d], inc=False)
                x.op("pe", lambda e: e.matmul(ps[:], c.ident[:], nbT4[:, kb, :, :].rearrange("p a b -> p (a b)"),
                                              start=False, stop=True), r=[c.ident.d, nbT4.d], w=[ps.d])
                x.op("act", lambda e: e.activation(out=pp_[:], in_=ps[:], func=AF.Exp, scale=SCALE, bias=negC[:, 0:1]),
                     r=[ps.d, negC.d], w=[pp_.d])

            def a_pv(idx):
                g, kb = asteps[idx]
                pp_ = P[idx % 2]
                for r in range(4):
                    x.op("pe", lambda e: e.matmul(p_o[:, r, 0:129], pp_[:, r * 128:(r + 1) * 128], va[:, kb, g, :],
                                                  start=False, stop=(kb == nkb - 1 and r % 2 == 1)),
                         r=[pp_.d, va.d], w=[p_o.d], inc=(r == 3))

            def a_epi(g):
                x.op("dve", lambda e: e.reciprocal(out=R[:], in_=p_o[:, :, 128:129].rearrange("p a b -> p (a b)")),
                     r=[p_o.d], w=[R.d])
                for r in range(4):
                    hh = g * 4 + r
                    x.op("act", lambda e: e.activation(out=obb[:, hh * 128:(hh + 1) * 128], in_=p_o[:, r, 0:128],
                                                       func=AF.Copy, scale=R[:, r:r + 1]), r=[p_o.d, R.d], mw=[obb.d])

            a_scores(0)
            for idx, (g, kb) in enumerate(asteps):
                if kb == 0:
                    for bnk in range(2):
                        x.op("pe", lambda e: e.matmul(p_ob[:, bnk * 512:(bnk + 1) * 512], zr[:, 0:128], zr[:],
                                                      start=True, stop=False), r=[zr.d], w=[p_o.d], inc=False)
                if idx + 1 < len(asteps):
                    a_scores(idx + 1)
                a_pv(idx)
                if kb == nkb - 1:
                    a_epi(g)
            x.dma("sp", o[i * 128:(i + 1) * 128, :], obb[:], r=[obb.d], mw=[d_o])

        part_A(0)
        part_T(0)
        for i in range(NS):
            if i + 1 < NS:
                part_A(i + 1)
            part_C(i)
            if i + 1 < NS:
                part_T(i + 1)


def _standalone(emit, *args):
    nc = bass.Bass("TRN2", target_bir_lowering=False)
    dram = lambda n, s, dt, k: nc.dram_tensor(n, list(s), dt, kind=k).ap()
    with ExitStack() as st:
        x = X(nc, st)
        c = make_consts(x)
        emit(x, c, dram, *args)
        x.global_barrier()
        print(emit.__name__, args, "sems", x.nsem, "cnt", x.cnt)
    return nc


def build_LA(kind):
    return _standalone(emit_LA, kind)


def build_LB_DA(lambda_init):
    return _standalone(emit_LB_DA, lambda_init)


def build_LB_SSD():
    return _standalone(emit_LB_SSD)


def build_LB_DSA():
    return _standalone(emit_LB_DSA)


def build_LC(FO):
    return _standalone(emit_LC, FO)


def _invf_table(half):
    invf = np.power(np.float32(ROPE_THETA), -np.arange(half, dtype=np.float32) / half).astype(np.float32)
    return np.ascontiguousarray(np.broadcast_to(invf[None, :], (128, half))).astype(np.float32)


def _ca(a):
    return np.ascontiguousarray(a)


GROUPS = [[0, 1], [2, 3], [4, 5], [6, 7]]
FIN_K = {0: 6144, 1: 10304, 2: 4176}


def _mk_dram(mapping):
    def dram(n, s, dt, k):
        ap = mapping[n]
        assert [int(v) for v in ap.shape] == [int(v) for v in s], (n, ap.shape, s)
        return ap
    return dram


def build_fused(depth=DEPTH):
    nc = bass.Bass("TRN2", target_bir_lowering=False)
    ext_in = lambda n, s, dt: nc.dram_tensor(n, list(s), dt, kind="ExternalInput").ap()
    internal = lambda n, s, dt: nc.dram_tensor(n, list(s), dt, kind="Internal").ap()
    x_in = ext_in("x_in", [TOK, D], F32)
    c_in = ext_in("c_in", [D], F32)
    pos = ext_in("pos", [TOK], I32)
    rk = ext_in("rk", [1, 1], I32)
    invf8 = ext_in("invf8", [128, 8], F32)
    invf16 = ext_in("invf16", [128, 16], F32)
    dmask = ext_in("dmask", [2, 128, 128], F32)
    x_out = nc.dram_tensor("x_out", [TOK, D], F32, kind="ExternalOutput").ap()
    xres = internal("xres", [TOK, D], F32)
    xmid = internal("xmid", [TOK, D], F32)
    scratch = {}

    def scr(n, s, dt):
        if n not in scratch:
            scratch[n] = internal(n, s, dt)
        return scratch[n]

    with ExitStack() as st:
        x = X(nc, st)
        c = make_consts(x)
        reg = st.enter_context(nc.gpsimd.register("rk"))
        nc.gpsimd.reg_load(reg, rk[0:1, 0:1])
        r = nc.gpsimd.snap(reg, min_val=0, max_val=1)
        d_g = x.mkdep("xchg")
        RS = bass.ds(r, 1)
        ADA_SPLIT = (RS, internal("adaH", [3 * D], F32), internal("adaG", [2, 3 * D], F32))
        CH = 2 * 1024 * 1024
        MAXE = {BF16: 12 * 1024 * 1024 + 4096, F32: 1024 * 1024}
        GBS = {BF16: [], F32: []}
        goff = {BF16: 0, F32: 0}

        def gather(parts, both=False, stage=False):
            a0 = parts[0]
            dt = a0.dtype
            shp = [int(v) for v in a0.shape]
            rowe = int(np.prod(shp[1:]))
            c0 = max(1, min(shp[0], CH // (rowe * mybir.dt.size(dt))))
            while shp[0] % c0:
                c0 -= 1
            nch = shp[0] // c0
            ce = c0 * rowe
            assert nch * 2 * ce <= MAXE[dt], (nch, ce, dt)
            if goff[dt] >= len(GBS[dt]):
                GBS[dt].append(internal("GB%d_%d" % (mybir.dt.size(dt), len(GBS[dt])), [2, MAXE[dt]], dt))
            gb = GBS[dt][goff[dt]]
            goff[dt] += 1
            off = 0
            for d_, a in enumerate(parts):
                for k in range(nch):
                    x.op("pool", lambda e: e.collective_compute(
                        "AllGather", ALU.bypass, replica_groups=GROUPS, ins=[a[k * c0:(k + 1) * c0].opt()],
                        outs=[gb[d_, off + k * 2 * ce:off + (k + 1) * 2 * ce].opt()]), w=[d_g])
            x.global_barrier()
            tot = nch * 2 * ce
            if stage:
                gs = scr("GS%d_%d" % (mybir.dt.size(dt), goff[dt] - 1), [1, MAXE[dt]], dt)
                CP = 4 * 1024 * 1024
                for o_ in range(0, tot, CP):
                    n_ = min(CP, tot - o_)
                    x.dma("pool", gs[:, o_:o_ + n_], (gb[0:1] if both else gb[RS])[:, o_:o_ + n_], mw=[d_g])
                row = gs[:, 0:tot]
            else:
                row = (gb[0:1] if both else gb[RS])[:, 0:tot]
            names = ["e%d" % i_ for i_ in range(len(shp) - 1)]
            kw = {"k": nch, "s": 2, "c": c0}
            kw.update({n_: v_ for n_, v_ in zip(names, shp[1:])})
            return row.rearrange("a (k s c %s) -> (a k) s c %s" % (" ".join(names), " ".join(names)), **kw)

        def cp(dst, src):
            x.dma("pool", dst, src, mw=[d_g])

        for i in range(depth):
            kind, j = i % 3, i // 3
            xsrc = x_in if i == 0 else xres
            xdst = x_out if i == depth - 1 else xres
            sfx = "_%d" % i
            ada_i = internal("ada" + sfx, [6 * D], F32)
            mp = {"x_in": xsrc, "c_in": c_in, "pos": pos, "ada_w": ext_in("ada_w" + sfx, [D, 3 * D], F32),
                  "ada_b": ext_in("ada_b" + sfx, [6 * D], F32), "g1": ext_in("g1" + sfx, [D], F32), "ada": ada_i,
                  "w_in": ext_in("w_in" + sfx, [D, FIN_K[kind]], F32)}
            goff[BF16] = goff[F32] = 0
            if kind == 0:
                A = {"qT": scr("A_qT", [16, 128, TOK], BF16), "kT": scr("A_kT", [16, 128, TOK], BF16),
                     "v": scr("A_v", [2, TOK, 1024], BF16)}
                mp.update(A)
                mp.update({"invf": invf8, "gq": ext_in("gq" + sfx, [64], F32), "gk": ext_in("gk" + sfx, [64], F32)})
                emit_LA(x, c, _mk_dram(mp), kind, True, ADA_SPLIT)
                x.global_barrier()
                Gq = gather([A["qT"][0:8], A["qT"][8:16]])
                Gk = gather([A["kT"][0:8], A["kT"][8:16]])
                Gv = gather([A["v"][0], A["v"][1]])
                x.global_barrier()
                L_qT = scr("L_qT", [8, 128, S], BF16)
                L_kT = scr("L_kT", [8, 128, S], BF16)
                L_v = scr("L_v", [S, 1024], BF16)
                for s_ in range(2):
                    cs = slice(s_ * TOK, (s_ + 1) * TOK)
                    for kk in range(2):
                        cp(L_qT[kk * 4:(kk + 1) * 4, :, cs], Gq[kk, s_])
                        cp(L_kT[kk * 4:(kk + 1) * 4, :, cs], Gk[kk, s_])
                        cp(L_v[s_ * TOK + kk * 1024:s_ * TOK + (kk + 1) * 1024, :], Gv[kk, s_])
                x.global_barrier()
                B_o = scr("B_o", [S, 1024], BF16)
                li = 0.8 - 0.6 * math.exp(-0.3 * i)
                emit_LB_DA(x, c, _mk_dram({"qT": L_qT, "kT": L_kT, "v": L_v, "lam4": ext_in("lam4" + sfx, [4, 64], F32),
                                           "gqk": ext_in("gqk" + sfx, [2, 64], F32),
                                           "subg": ext_in("subg" + sfx, [128], F32), "o": B_o}), float(li))
                x.global_barrier()
                goff[BF16] = 0
                Go = gather([B_o[0:TOK], B_o[TOK:S]])
                x.global_barrier()
                FO = 2048
                L_o = scr("L_o", [TOK, 2048], BF16)
                for hh in range(2):
                    for kk in range(2):
                        cp(L_o[kk * 1024:(kk + 1) * 1024, hh * 1024:(hh + 1) * 1024], Go[kk, hh])
            elif kind == 1:
                A = {"z": scr("A_z", [2, TOK, 2048], BF16), "xbcT": scr("A_xbcT", [2, 3072, TOK], BF16),
                     "dtr": scr("A_dtr", [2, TOK, 32], F32)}
                mp.update(A)
                emit_LA(x, c, _mk_dram(mp), kind, True, ADA_SPLIT)
                x.global_barrier()
                Gz = gather([A["z"][0], A["z"][1]])
                Gx = gather([A["xbcT"][0], A["xbcT"][1]])
                Gd = gather([A["dtr"][0], A["dtr"][1]])
                x.global_barrier()
                L_raw = scr("L_raw", [3072, S], F32)
                L_z = scr("L_z", [S, 2048], BF16)
                L_dt = scr("L_dt", [S, 32], F32)
                for s_ in range(2):
                    cs = slice(s_ * TOK, (s_ + 1) * TOK)
                    for kk in range(6):
                        cp(L_raw[kk * 512:(kk + 1) * 512, cs], Gx[kk, s_])
                    for kk in range(4):
                        cp(L_z[s_ * TOK + kk * 512:s_ * TOK + (kk + 1) * 512, :], Gz[kk, s_])
                    cp(L_dt[cs, :], Gd[0, s_])
                x.global_barrier()
                B_y = scr("B_y", [S, 2048], BF16)
                emit_LB_SSD(x, c, _mk_dram({"raw": L_raw, "convw": ext_in("convw" + sfx, [4, 3072], F32),
                                            "convb": ext_in("convb" + sfx, [3072], F32), "dtr": L_dt,
                                            "hp": ext_in("hp" + sfx, [3, 32], F32), "z": L_z,
                                            "ng": ext_in("ng" + sfx, [2048], F32),
                                            "tokd": scr("tokd", [S, 2560], BF16), "featd": scr("featd", [1024, S], BF16),
                                            "y": B_y}))
                x.global_barrier()
                goff[BF16] = 0
                Gy = gather([B_y[0:TOK], B_y[TOK:S]])
                x.global_barrier()
                FO = 4096
                L_o = scr("L_o4", [TOK, 4096], BF16)
                for gh in range(2):
                    for kk in range(4):
                        cp(L_o[kk * 512:(kk + 1) * 512, gh * 2048:(gh + 1) * 2048], Gy[kk, gh])
            else:
                A = {"qT": scr("D_qT", [2, 16, 128, 8, 128], BF16), "kT": scr("D_kT", [4, 128, TOK], BF16),
                     "v": scr("D_v", [TOK, 512], BF16), "qiT": scr("D_qiT", [2, 8, 128, 8, 128], BF16),
                     "kiT": scr("D_kiT", [64, TOK], BF16), "wi": scr("D_wi", [2, 8, 128, 16], F32)}
                mp.update(A)
                mp.update({"invf16": invf16, "invf8": invf8, "gq": ext_in("gq" + sfx, [128], F32),
                           "gk": ext_in("gk" + sfx, [128], F32), "gi": ext_in("gi" + sfx, [64], F32)})
                emit_LA(x, c, _mk_dram(mp), kind, True, ADA_SPLIT)
                x.global_barrier()
                Gq = gather([A["qT"][0], A["qT"][1]], stage=True)
                Gqi = gather([A["qiT"][0], A["qiT"][1]], stage=True)
                Gw = gather([A["wi"][0], A["wi"][1]], stage=True)
                Gk = gather([A["kT"]], both=True)
                Gv = gather([A["v"]], both=True)
                Gki = gather([A["kiT"]], both=True)
                x.global_barrier()
                L_qTs = scr("L_qTs", [16, 128, 2048], BF16)
                L_qiTs = scr("L_qiTs", [16, 128, 1024], BF16)
                L_wis = scr("L_wis", [16, 128, 16], F32)
                L_kT = scr("L_kT4", [4, 128, S], BF16)
                L_v = scr("L_v4", [S, 512], BF16)
                L_ki = scr("L_ki2", [128, S], BF16)
                with nc.allow_non_contiguous_dma(reason="tile gathers"):
                    for sl_ in range(16):
                        s_, k_ = sl_ // 8, sl_ % 8
                        for kk in range(2):
                            cp(L_qTs[sl_][:, kk * 1024:(kk + 1) * 1024].rearrange("d (h t) -> d h t", h=8),
                               Gq[kk, s_][:, :, k_, :].rearrange("h d t -> d h t"))
                        cp(L_qiTs[sl_].rearrange("d (h t) -> d h t", h=8),
                           Gqi[0, s_][:, :, k_, :].rearrange("h d t -> d h t"))
                        cp(L_wis[sl_], Gw[0, s_][k_])
                for s_ in range(2):
                    cs = slice(s_ * TOK, (s_ + 1) * TOK)
                    cp(L_kT[:, :, cs], Gk[0, s_])
                    cp(L_v[cs, :], Gv[0, s_])
                    for dup in range(2):
                        cp(L_ki[dup * 64:(dup + 1) * 64, cs], Gki[0, s_])
                x.global_barrier()
                B_o2 = scr("B_o2", [TOK, 2048], BF16)
                emit_LB_DSA(x, c, _mk_dram({"qTs": L_qTs, "qiTs": L_qiTs, "wis": L_wis, "kT": L_kT, "v": L_v,
                                            "kiT2": L_ki, "dmask": dmask,
                                            "gqk": ext_in("gqk" + sfx, [2, 128], F32), "o": B_o2}))
                x.global_barrier()
                goff[BF16] = 0
                Go2 = gather([B_o2[0:1024], B_o2[1024:2048]], stage=True)
                x.global_barrier()
                FO = 2048
                L_o = scr("L_o", [TOK, 2048], BF16)
                for tl in range(16):
                    jj = tl // 2
                    cp(L_o[tl * 128:(tl + 1) * 128, :], Go2[jj // 4, tl % 2][(jj % 4) * 128:(jj % 4 + 1) * 128, :])
            x.global_barrier()
            emit_LC(x, c, _mk_dram({"x_in": xsrc, "o_in": L_o, "ada": ada_i, "g2": ext_in("g2" + sfx, [D], F32),
                                    "w_out": ext_in("w_out" + sfx, [FO, D], F32),
                                    "wgu": ext_in("wgu" + sfx, [D, 2 * FFN], F32),
                                    "wd": ext_in("wd" + sfx, [FFN, D], F32), "x_mid": xmid, "x_out": xdst,
                                    "act_d": scr("act_d", [NT, 128, FFN // 128, 128], BF16)}), FO)
            x.global_barrier()
        print("fused sems", x.nsem, "cnt", x.cnt)
    return nc


def fused_inputs(inp, depth=DEPTH):
    x = np.asarray(inp["x"], dtype=np.float32)
    c = np.asarray(inp["c"], dtype=np.float32)
    pos = np.asarray(inp["positions"]).astype(np.int32)
    tri = np.where(np.arange(128)[None, :] <= np.arange(128)[:, None], 0.0, -1e30).astype(np.float32)
    full_neg = np.full((128, 128), -1e30, np.float32)
    zero_m = np.zeros((128, 128), np.float32)
    f32 = lambda a: _ca(np.asarray(a, dtype=np.float32))
    maps = []
    for cc in range(8):
        b, h = cc // 2, cc % 2
        sl = slice(h * TOK, (h + 1) * TOK)
        m = {"x_in": _ca(x[b, sl]), "c_in": _ca(c[b]), "pos": _ca(pos[b, sl]), "rk": np.array([[h]], np.int32),
             "invf8": _invf_table(8), "invf16": _invf_table(16),
             "dmask": np.stack([tri, full_neg]) if h == 0 else np.stack([zero_m, tri])}
        for i in range(depth):
            kind, j = i % 3, i // 3
            sfx = "_%d" % i
            m["ada_w" + sfx] = _ca(inp["ada_w"][i][:, h * 3 * D:(h + 1) * 3 * D])
            m["ada_b" + sfx] = inp["ada_b"][i]
            m["g1" + sfx] = inp["norm1_g"][i]
            m["g2" + sfx] = inp["norm2_g"][i]
            m["wgu" + sfx] = inp["ffn_w_gate_up"][i]
            m["wd" + sfx] = inp["ffn_w_down"][i]
            if kind == 0:
                m["w_in" + sfx] = inp["da_w_in"][j]
                m["w_out" + sfx] = inp["da_w_out"][j]
                m["gq" + sfx] = inp["da_q_norm_g"][j]
                m["gk" + sfx] = inp["da_k_norm_g"][j]
                m["lam4" + sfx] = f32(np.stack([inp["da_lambda_q1"][j], inp["da_lambda_k1"][j],
                                                inp["da_lambda_q2"][j], inp["da_lambda_k2"][j]]))
                m["gqk" + sfx] = f32(np.stack([inp["da_q_norm_g"][j], inp["da_k_norm_g"][j]]))
                m["subg" + sfx] = inp["da_subln_g"][j]
            elif kind == 1:
                m["w_in" + sfx] = inp["ssd_w_in"][j]
                m["w_out" + sfx] = inp["ssd_w_out"][j]
                ch = np.concatenate([np.arange(h * 2048, (h + 1) * 2048), np.arange(4096 + h * 512, 4096 + (h + 1) * 512),
                                     np.arange(5120 + h * 512, 5120 + (h + 1) * 512)])
                hs = slice(h * 32, (h + 1) * 32)
                m["convw" + sfx] = _ca(inp["ssd_conv_w"][j][:, ch])
                m["convb" + sfx] = _ca(inp["ssd_conv_b"][j][ch])
                m["hp" + sfx] = f32(np.stack([inp["ssd_dt_bias"][j][hs], inp["ssd_a_log"][j][hs], inp["ssd_d_skip"][j][hs]]))
                m["ng" + sfx] = _ca(inp["ssd_norm_g"][j][h * 2048:(h + 1) * 2048])
            else:
                m["w_in" + sfx] = inp["sa_w_in"][j]
                m["w_out" + sfx] = inp["sa_w_out"][j]
                m["gq" + sfx] = inp["sa_q_norm_g"][j]
                m["gk" + sfx] = inp["sa_k_norm_g"][j]
                m["gi" + sfx] = inp["sa_idx_k_norm_g"][j]
                m["gqk" + sfx] = f32(np.stack([inp["sa_q_norm_g"][j], inp["sa_k_norm_g"][j]]))
        maps.append(m)
    return maps


def kernel(**inp):
    depth = DEPTH
    nc = build_fused(depth)
    maps = fused_inputs(inp, depth)
    res = run_bass_kernel_spmd(nc, maps, core_ids=list(range(8))).results
    out = np.empty((NB, S, D), np.float32)
    for cc in range(8):
        b, h = cc // 2, cc % 2
        out[b, h * TOK:(h + 1) * TOK] = res[cc]["x_out"]
    return out
```

```python
import math
import numpy as np
from contextlib import ExitStack
import ml_dtypes
import concourse.bass as bass
import concourse.mybir as mybir
from concourse.bass_utils import run_bass_kernel_spmd

F32 = mybir.dt.float32
BF16 = mybir.dt.bfloat16
I32 = mybir.dt.int32
ALU = mybir.AluOpType
AF = mybir.ActivationFunctionType
AX = mybir.AxisListType

D = 2048
S = 4096
NB = 4
DEPTH = 4
KC = D // 128
TOK = 2048
NT = TOK // 128
FFN = 5632
EPS = 1e-6
ROPE_THETA = 500000.0
NEG = -30000.0

SAME_ENGINE_SYNC = True


class Dep:
    __slots__ = ("name", "w", "r", "dsem", "dq")

    def __init__(self, name=""):
        self.name = name
        self.w = {}
        self.r = {}
        self.dsem = None
        self.dq = None


class X:
    def __init__(self, nc, stack):
        self.nc = nc
        self.stack = stack
        self.root = stack
        self.eng = {"pe": nc.tensor, "act": nc.scalar, "dve": nc.vector,
                    "pool": nc.gpsimd, "sp": nc.sync}
        self.sem = {}
        self.cnt = {}
        self.seen = {}
        for k in self.eng:
            self.sem[k] = stack.enter_context(nc.semaphore("es_" + k))
            self.cnt[k] = 0
            self.seen[k] = {}
        self.nsem = 5
        self.semcnt = {}
        self.free_dsems = {"sp": [], "pool": [], "act": []}
        self.alltok = {}
        self.uid = 0

    def name(self, p):
        self.uid += 1
        return "%s_%d" % (p, self.uid)

    def sb(self, shape, dt, name="sb", stack=None):
        st = stack or self.stack
        return st.enter_context(self.nc.sbuf_tensor(self.name(name), list(shape), dt))

    def ps(self, shape, dt=F32, name="ps", stack=None):
        st = stack or self.stack
        return st.enter_context(self.nc.psum_tensor(self.name(name), list(shape), dt))

    def _wait(self, e, toks):
        en = self.eng[e]
        seen = self.seen[e]
        for s, v in toks.items():
            cv = self.semcnt.get(s)
            if cv is not None and cv > v:
                v = cv
            if seen.get(s, 0) >= v:
                continue
            if s is self.sem[e]:
                if e == "pe" or not SAME_ENGINE_SYNC:
                    continue
            en.wait_ge(s, v)
            seen[s] = v

    def _pre(self, e, r, w, mw=()):
        for d in r:
            self._wait(e, d.w)
        for d in w:
            self._wait(e, d.w)
            self._wait(e, d.r)
        for d in mw:
            self._wait(e, d.r)

    def _post(self, tok, r, w, mw=()):
        s, v = tok
        if self.alltok.get(s, 0) < v:
            self.alltok[s] = v
        for d in r:
            if d.r.get(s, 0) < v:
                d.r[s] = v
        for d in w:
            d.w = {s: v}
            d.r = {}
        for d in mw:
            if d.w.get(s, 0) < v:
                d.w[s] = v

    def op(self, e, fn, r=(), w=(), mw=(), inc=True):
        self._pre(e, r, w, mw)
        ins = fn(self.eng[e])
        if inc:
            self.cnt[e] += 1
            ins.then_inc(self.sem[e], 1)
            self._post((self.sem[e], self.cnt[e]), r, w, mw)
        else:
            self._post((self.sem[e], self.cnt[e] + 1), r, w, mw)
        return ins

    def dma(self, e, out, in_, r=(), w=(), mw=(), **kw):
        host = (list(w) + list(mw))[0]
        assert host.dq in (None, e), (host.name, host.dq, e)
        if host.dsem is None:
            host.dq = e
            if self.free_dsems[e]:
                host.dsem = self.free_dsems[e].pop()
            else:
                host.dsem = self.root.enter_context(self.nc.semaphore(self.name("ds")))
                self.semcnt[host.dsem] = 0
                self.nsem += 1
        self._pre(e, r, w, mw)
        ins = self.eng[e].dma_start(out=out, in_=in_, **kw)
        self.semcnt[host.dsem] += 16
        ins.then_inc(host.dsem, 16)
        self._post((host.dsem, self.semcnt[host.dsem]), r, w, mw)
        return ins

    def mkdep(self, name=""):
        d = Dep(name)
        if isinstance(self.stack, Scope):
            self.stack.deps.append(d)
        return d

    def global_barrier(self):
        for e in self.eng:
            self._wait(e, dict(self.alltok))

    def barrier(self, deps):
        toks = {}
        for d in deps:
            for src in (d.w, d.r):
                for s_, v in src.items():
                    if toks.get(s_, 0) < v:
                        toks[s_] = v
        for e in self.eng:
            self._wait(e, toks)

    def finish(self, deps, e="sp"):
        for d in deps:
            self._wait(e, d.w)


class Scope:
    def __init__(self, x):
        self.x = x
        self.st = ExitStack()
        self.deps = []

    def __enter__(self):
        self.st.__enter__()
        self.prev = self.x.stack
        self.x.stack = self
        return self

    def __exit__(self, *a):
        self.x.stack = self.prev
        if a[0] is None:
            self.x.barrier(self.deps)
            for d in self.deps:
                if d.dsem is not None:
                    self.x.free_dsems[d.dq].append(d.dsem)
                    d.dsem = None
                    d.dq = None
        return self.st.__exit__(*a)

    def enter_context(self, cm):
        return self.st.enter_context(cm)


class T:
    def __init__(self, x, shape, dt, name, psum=False, stack=None):
        stack = stack or x.stack
        self.t = x.ps(shape, dt, name, stack) if psum else x.sb(shape, dt, name, stack)
        self.d = Dep(name)
        if isinstance(stack, Scope):
            stack.deps.append(self.d)

    def __getitem__(self, k):
        return self.t[k]


def sbt(x, shape, dt, name, stack=None):
    return T(x, shape, dt, name, False, stack)


def pst(x, shape, dt, name, stack=None):
    return T(x, shape, dt, name, True, stack)


class Consts:
    pass


def make_consts(x):
    c = Consts()
    idf = sbt(x, [128, 128], F32, "idf")
    c.ident = sbt(x, [128, 128], BF16, "ident")
    x.op("pool", lambda e: e.memset(idf[:], 1.0), w=[idf.d])
    x.op("pool", lambda e: e.affine_select(out=idf[:], in_=idf[:], pattern=[[-1, 128]],
                                           compare_op=ALU.is_equal, fill=0.0, base=0,
                                           channel_multiplier=1), r=[idf.d], w=[idf.d])
    x.op("dve", lambda e: e.tensor_copy(c.ident[:], idf[:]), r=[idf.d], w=[c.ident.d])
    c.identf = idf
    ngf = sbt(x, [128, 128], F32, "ngf")
    c.negT = sbt(x, [128, 128], BF16, "negT")
    x.op("pool", lambda e: e.memset(ngf[:], 0.0), w=[ngf.d])
    x.op("pool", lambda e: e.affine_select(out=ngf[:], in_=ngf[:], pattern=[[1, 128]],
                                           compare_op=ALU.is_ge, fill=NEG, base=0,
                                           channel_multiplier=-1), r=[ngf.d], w=[ngf.d])
    x.op("dve", lambda e: e.tensor_copy(c.negT[:], ngf[:]), r=[ngf.d], w=[c.negT.d])
    c.negQ = sbt(x, [128, 128], F32, "negQ")
    x.op("pool", lambda e: e.memset(c.negQ[:], 0.0), w=[c.negQ.d])
    x.op("pool", lambda e: e.affine_select(out=c.negQ[:], in_=c.negQ[:], pattern=[[-1, 128]],
                                           compare_op=ALU.is_ge, fill=-1e30, base=0,
                                           channel_multiplier=1), r=[c.negQ.d], w=[c.negQ.d])
    trf = sbt(x, [128, 128], F32, "trf")
    c.tri = sbt(x, [128, 128], BF16, "tri")
    x.op("pool", lambda e: e.memset(trf[:], 1.0), w=[trf.d])
    x.op("pool", lambda e: e.affine_select(out=trf[:], in_=trf[:], pattern=[[1, 128]],
                                           compare_op=ALU.is_ge, fill=0.0, base=0,
                                           channel_multiplier=-1), r=[trf.d], w=[trf.d])
    x.op("dve", lambda e: e.tensor_copy(c.tri[:], trf[:]), r=[trf.d], w=[c.tri.d])
    c.trif = trf
    c.ones = sbt(x, [128, 128], BF16, "ones")
    x.op("pool", lambda e: e.memset(c.ones[:], 1.0), w=[c.ones.d])
    c.onesf = sbt(x, [128, 128], F32, "onesf")
    x.op("pool", lambda e: e.memset(c.onesf[:], 1.0), w=[c.onesf.d])
    c.nhalf = sbt(x, [128, 64], F32, "nhalf")
    x.op("pool", lambda e: e.memset(c.nhalf[:], -0.5), w=[c.nhalf.d])
    return c


def rsqrt_mean(x, c, out, ss, n, width):
    x.op("dve", lambda e: e.tensor_scalar(out=out[:, 0:width], in0=ss[:, 0:width], scalar1=1.0 / n,
                                          scalar2=EPS, op0=ALU.mult, op1=ALU.add),
         r=[ss.d], w=[out.d])
    x.op("pool", lambda e: e.tensor_tensor(out=out[:, 0:width], in0=out[:, 0:width],
                                           in1=c.nhalf[:, 0:width], op=ALU.pow),
         r=[out.d, c.nhalf.d], w=[out.d])


def bc(ap, shape):
    return ap.to_broadcast(list(shape))


def emit_ada_split(x, c_ap, adaw_ap, adab_ap, ada_ap, d_ada, split):
    RS, H, G = split
    with Scope(x) as ls:
        cs = sbt(x, [128, 16], F32, "c_sb", ls)
        ca = sbt(x, [128, 16], F32, "c_act", ls)
        brow = sbt(x, [1, 6 * D], F32, "brow", ls)
        arow = sbt(x, [1, 6 * D], F32, "arow", ls)
        hrow = sbt(x, [1, 3 * D], F32, "hrow", ls)
        wt = [sbt(x, [128, 16, 512], F32, "adaw%d" % i, ls) for i in range(2)]
        pa = [pst(x, [1, 512], F32, "adap%d" % i, ls) for i in range(2)]
        d_h = x.mkdep("adaH")
        x.dma("sp", cs[:], c_ap.rearrange("(p k) -> p k", k=16), w=[cs.d])
        x.dma("sp", brow[:], adab_ap.rearrange("(o n) -> o n", o=1), w=[brow.d])
        x.op("act", lambda e: e.activation(out=ca[:], in_=cs[:], func=AF.Silu), r=[cs.d], w=[ca.d])
        for blk in range(12):
            i = blk % 2
            x.dma("sp", wt[i][:], adaw_ap[:, blk * 512:(blk + 1) * 512].rearrange("(p k) f -> p k f", k=16),
                  w=[wt[i].d])
            for k in range(16):
                x.op("pe", lambda e: e.matmul(pa[i][:], ca[:, k:k + 1], wt[i][:, k, :],
                                              start=(k == 0), stop=(k == 15)),
                     r=[ca.d, wt[i].d], w=[pa[i].d], inc=(k == 15))
            x.op("dve", lambda e: e.tensor_copy(hrow[0:1, blk * 512:(blk + 1) * 512], pa[i][:]),
                 r=[pa[i].d], mw=[hrow.d])
        x.dma("sp", H.rearrange("(o n) -> o n", o=1), hrow[:], r=[hrow.d], w=[d_h])
        x.global_barrier()
        x.op("pool", lambda e: e.collective_compute("AllGather", ALU.bypass, replica_groups=GROUPS,
                                                    ins=[H.opt()], outs=[G.opt()]), w=[d_h])
        x.global_barrier()
        x.dma("sp", arow[:], G.rearrange("s n -> (s n)").rearrange("(o n) -> o n", o=1), r=[d_h], w=[arow.d])
        x.op("dve", lambda e: e.tensor_tensor(out=arow[:], in0=arow[:], in1=brow[:], op=ALU.add),
             r=[arow.d, brow.d], w=[arow.d])
        x.dma("sp", ada_ap.rearrange("(o n) -> o n", o=1), arow[:], r=[arow.d], w=[d_ada])


def emit_ada(x, c_ap, adaw_ap, adab_ap, ada_ap, d_ada):
    with Scope(x) as ls:
        cs = sbt(x, [128, 16], F32, "c_sb", ls)
        ca = sbt(x, [128, 16], F32, "c_act", ls)
        brow = sbt(x, [1, 6 * D], F32, "brow", ls)
        arow = sbt(x, [1, 6 * D], F32, "arow", ls)
        wt = [sbt(x, [128, 16, 512], F32, "adaw%d" % i, ls) for i in range(2)]
        pa = [pst(x, [1, 512], F32, "adap%d" % i, ls) for i in range(2)]
        x.dma("sp", cs[:], c_ap.rearrange("(p k) -> p k", k=16), w=[cs.d])
        x.dma("sp", brow[:], adab_ap.rearrange("(o n) -> o n", o=1), w=[brow.d])
        x.op("act", lambda e: e.activation(out=ca[:], in_=cs[:], func=AF.Silu), r=[cs.d], w=[ca.d])
        for blk in range(24):
            i = blk % 2
            x.dma("sp", wt[i][:], adaw_ap[:, blk * 512:(blk + 1) * 512].rearrange("(p k) f -> p k f", k=16),
                  w=[wt[i].d])
            for k in range(16):
                x.op("pe", lambda e: e.matmul(pa[i][:], ca[:, k:k + 1], wt[i][:, k, :],
                                              start=(k == 0), stop=(k == 15)),
                     r=[ca.d, wt[i].d], w=[pa[i].d], inc=(k == 15))
            x.op("dve", lambda e: e.tensor_tensor(out=arow[0:1, blk * 512:(blk + 1) * 512], in0=pa[i][:],
                                                  in1=brow[0:1, blk * 512:(blk + 1) * 512], op=ALU.add),
                 r=[pa[i].d, brow.d], mw=[arow.d])
        x.dma("sp", ada_ap.rearrange("(o n) -> o n", o=1), arow[:], r=[arow.d], w=[d_ada])


def load_cols(x, ls, vec_ap, name, d_src=None):
    t = sbt(x, [128, 16], F32, name, ls)
    x.dma("sp", t[:], vec_ap.rearrange("(k p) -> p k", p=128), r=([d_src] if d_src else []), w=[t.d],
          allow_slow_non_contiguous=True)
    return t


def load_bcast(x, ls, vec_ap, n, name, d_src=None, dt=F32):
    t = sbt(x, [128, n], dt, name, ls)
    x.dma("sp", t[:], vec_ap.rearrange("(o n) -> o n", o=1).partition_broadcast(128),
          r=([d_src] if d_src else []), w=[t.d])
    return t


def emit_mod_cols(x, ls, g_ap, ada_ap, d_ada, scale_idx, shift_idx):
    g = load_cols(x, ls, g_ap, "gcol")
    sc = load_cols(x, ls, ada_ap[scale_idx * D:(scale_idx + 1) * D], "sccol", d_ada)
    sh = load_cols(x, ls, ada_ap[shift_idx * D:(shift_idx + 1) * D], "shcol", d_ada)
    sT = sbt(x, [128, 16], F32, "sT", ls)
    x.op("dve", lambda e: e.scalar_tensor_tensor(out=sT[:], in0=sc[:], scalar=1.0, in1=g[:],
                                                 op0=ALU.add, op1=ALU.mult),
         r=[sc.d, g.d], w=[sT.d])
    return sT, sh


def emit_norm_hT(x, c, x_ap, d_x, ntiles, sT, shT, hT, hT_d, tile_off=0):
    with Scope(x) as ls:
        xt = [sbt(x, [128, D], F32, "xt%d" % i, ls) for i in range(2)]
        xn = [sbt(x, [128, D], BF16, "xn%d" % i, ls) for i in range(2)]
        junk = sbt(x, [128, D], BF16, "junk", ls)
        ss = [sbt(x, [128, 1], F32, "ss%d" % i, ls) for i in range(2)]
        rstd = [sbt(x, [128, 1], F32, "rstd%d" % i, ls) for i in range(2)]
        pt = [pst(x, [128, 512], BF16, "ptn%d" % i, ls) for i in range(2)]
        for t in range(ntiles):
            i = t % 2
            x.dma("sp", xt[i][:], x_ap[t * 128:(t + 1) * 128, :], r=[d_x], w=[xt[i].d])
            x.op("act", lambda e: e.activation(out=junk[:], in_=xt[i][:], func=AF.Square,
                                               accum_out=ss[i][:, 0:1]),
                 r=[xt[i].d], w=[junk.d, ss[i].d])
            rsqrt_mean(x, c, rstd[i], ss[i], D, 1)
            x.op("act", lambda e: e.activation(out=xn[i][:], in_=xt[i][:], func=AF.Copy,
                                               scale=rstd[i][:, 0:1]),
                 r=[xt[i].d, rstd[i].d], w=[xn[i].d])
            for g in range(4):
                p = pt[g % 2]
                for j in range(4):
                    kc = g * 4 + j
                    x.op("pe", lambda e: e.transpose(p[:, j * 128:(j + 1) * 128],
                                                     xn[i][:, kc * 128:(kc + 1) * 128], c.ident[:]),
                         r=[xn[i].d, c.ident.d], w=[p.d], inc=(j == 3))
                for j in range(4):
                    kc = g * 4 + j
                    dst = hT[:, kc, (tile_off + t) * 128:(tile_off + t + 1) * 128]
                    if True:
                        x.op("dve", lambda e: e.tensor_scalar(out=dst, in0=p[:, j * 128:(j + 1) * 128],
                                                              scalar1=sT[:, kc:kc + 1], scalar2=shT[:, kc:kc + 1],
                                                              op0=ALU.mult, op1=ALU.add),
                             r=[p.d, sT.d, shT.d], mw=[hT_d[tile_off + t]])
                    else:
                        x.op("act", lambda e: e.activation(out=dst, in_=p[:, j * 128:(j + 1) * 128],
                                                           func=AF.Identity, scale=sT[:, kc:kc + 1],
                                                           bias=shT[:, kc:kc + 1]),
                             r=[p.d, sT.d, shT.d], mw=[hT_d[tile_off + t]])


def emit_rope_tables(x, ls, pos_ap, invf_ap, half, ntiles):
    n = ntiles * half
    posi = sbt(x, [128, ntiles], I32, "posi", ls)
    posf = sbt(x, [128, ntiles], F32, "posf", ls)
    invf = sbt(x, [128, half], F32, "invf", ls)
    ang = sbt(x, [128, ntiles, half], F32, "ang", ls)
    x.dma("sp", posi[:], pos_ap.rearrange("(t p) -> p t", p=128), w=[posi.d], allow_slow_non_contiguous=True)
    x.dma("sp", invf[:], invf_ap, w=[invf.d])
    x.op("dve", lambda e: e.tensor_copy(posf[:], posi[:]), r=[posi.d], w=[posf.d])
    x.op("dve", lambda e: e.tensor_tensor(out=ang[:], in0=bc(posf[:].unsqueeze(2), [128, ntiles, half]),
                                          in1=bc(invf[:].unsqueeze(1), [128, ntiles, half]), op=ALU.mult),
         r=[posf.d, invf.d], w=[ang.d])
    outs = []
    C1 = 6.28125
    C2 = 2.0 * math.pi - C1
    for nm, shift in (("cos", math.pi / 2), ("sin", 0.0)):
        a = sbt(x, [128, n], F32, nm + "_a", ls)
        ki = sbt(x, [128, n], I32, nm + "_ki", ls)
        kf = sbt(x, [128, n], F32, nm + "_kf", ls)
        m = sbt(x, [128, n], F32, nm + "_m", ls)
        res = sbt(x, [128, ntiles, half], F32, nm + "_t", ls)
        af = ang[:].rearrange("p t h -> p (t h)")
        x.op("dve", lambda e: e.tensor_scalar(out=a[:], in0=af, scalar1=shift, scalar2=None, op0=ALU.add),
             r=[ang.d], w=[a.d])
        x.op("dve", lambda e: e.tensor_scalar(out=kf[:], in0=a[:], scalar1=1.0 / (2 * math.pi), scalar2=None,
                                              op0=ALU.mult), r=[a.d], w=[kf.d])
        x.op("dve", lambda e: e.tensor_copy(ki[:], kf[:]), r=[kf.d], w=[ki.d])
        x.op("dve", lambda e: e.tensor_copy(kf[:], ki[:]), r=[ki.d], w=[kf.d])
        x.op("dve", lambda e: e.scalar_tensor_tensor(out=a[:], in0=kf[:], scalar=-C1, in1=a[:],
                                                     op0=ALU.mult, op1=ALU.add), r=[kf.d, a.d], w=[a.d])
        x.op("dve", lambda e: e.scalar_tensor_tensor(out=a[:], in0=kf[:], scalar=-C2, in1=a[:],
                                                     op0=ALU.mult, op1=ALU.add), r=[kf.d, a.d], w=[a.d])
        x.op("dve", lambda e: e.tensor_scalar(out=m[:], in0=a[:], scalar1=math.pi, scalar2=2 * math.pi,
                                              op0=ALU.is_gt, op1=ALU.mult), r=[a.d], w=[m.d])
        x.op("dve", lambda e: e.tensor_tensor(out=a[:], in0=a[:], in1=m[:], op=ALU.subtract),
             r=[a.d, m.d], w=[a.d])
        x.op("dve", lambda e: e.tensor_scalar(out=m[:], in0=a[:], scalar1=-math.pi, scalar2=2 * math.pi,
                                              op0=ALU.is_lt, op1=ALU.mult), r=[a.d], w=[m.d])
        x.op("dve", lambda e: e.tensor_tensor(out=a[:], in0=a[:], in1=m[:], op=ALU.add),
             r=[a.d, m.d], w=[a.d])
        x.op("dve", lambda e: e.tensor_scalar(out=a[:], in0=a[:], scalar1=-3.1415925, scalar2=3.1415925,
                                              op0=ALU.max, op1=ALU.min), r=[a.d], w=[a.d])
        x.op("act", lambda e: e.activation(out=res[:].rearrange("p t h -> p (t h)"), in_=a[:], func=AF.Sin),
             r=[a.d], w=[res.d])
        outs.append(res)
    return outs[0], outs[1]


def emit_qk_post(x, c, ls_tiles, ps, ncols, gdim, gain, half, cos, sin, t, out_bf):
    ng = ncols // gdim
    qn, sq, ssq, rs, t1, t2, t3, t4 = ls_tiles
    pv = ps[:, 0:ncols].rearrange("p (g d) -> p g d", d=gdim)
    qv = qn[:, 0:ncols].rearrange("p (g d) -> p g d", d=gdim)
    if gain is not None:
        x.op("act", lambda e: e.activation(out=sq[:, 0:ncols], in_=ps[:, 0:ncols], func=AF.Square),
             r=[ps.d], w=[sq.d])
        x.op("dve", lambda e: e.tensor_reduce(out=ssq[:, 0:ng], in_=sq[:, 0:ncols].rearrange("p (g d) -> p g d", d=gdim),
                                              axis=AX.X, op=ALU.add), r=[sq.d], w=[ssq.d])
        rsqrt_mean(x, c, rs, ssq, gdim, ng)
        x.op("dve", lambda e: e.tensor_tensor(out=qv, in0=pv, in1=bc(rs[:, 0:ng].unsqueeze(2), [128, ng, gdim]),
                                              op=ALU.mult), r=[ps.d, rs.d], w=[qn.d])
        x.op("pool", lambda e: e.tensor_tensor(out=qv, in0=qv, in1=bc(gain[:, 0:gdim].unsqueeze(1), [128, ng, gdim]),
                                               op=ALU.mult), r=[qn.d, gain.d], w=[qn.d])
    else:
        x.op("act", lambda e: e.activation(out=qn[:, 0:ncols], in_=ps[:, 0:ncols], func=AF.Copy),
             r=[ps.d], w=[qn.d])
    if half:
        x1 = qv[:, :, 0:half]
        x2 = qv[:, :, half:2 * half]
        cb = bc(cos[:, t, :].unsqueeze(1), [128, ng, half])
        sb_ = bc(sin[:, t, :].unsqueeze(1), [128, ng, half])
        tv = [tt[:, 0:ng * half].rearrange("p (g h) -> p g h", h=half) for tt in (t1, t2, t3, t4)]
        x.op("dve", lambda e: e.tensor_tensor(out=tv[0], in0=x1, in1=cb, op=ALU.mult), r=[qn.d, cos.d], w=[t1.d])
        x.op("dve", lambda e: e.tensor_tensor(out=tv[1], in0=x2, in1=sb_, op=ALU.mult), r=[qn.d, sin.d], w=[t2.d])
        x.op("pool", lambda e: e.tensor_tensor(out=tv[2], in0=x2, in1=cb, op=ALU.mult), r=[qn.d, cos.d], w=[t3.d])
        x.op("pool", lambda e: e.tensor_tensor(out=tv[3], in0=x1, in1=sb_, op=ALU.mult), r=[qn.d, sin.d], w=[t4.d])
        x.op("dve", lambda e: e.tensor_tensor(out=x1, in0=tv[0], in1=tv[1], op=ALU.subtract),
             r=[t1.d, t2.d], w=[qn.d])
        x.op("pool", lambda e: e.tensor_tensor(out=x2, in0=tv[2], in1=tv[3], op=ALU.add),
             r=[t3.d, t4.d, qn.d], w=[qn.d])
    x.op("act", lambda e: e.activation(out=out_bf[:, 0:ncols], in_=qn[:, 0:ncols], func=AF.Copy),
         r=[qn.d], w=[out_bf.d])


def emit_qk_post2(x, c, ls_tiles, ps, gdim, gain, half, cos, sin, t0, out_bf):
    ng = 512 // gdim
    ng2 = 2 * ng
    qn, sq, ssq, rs, t1, t2, t3, t4 = ls_tiles
    psf = ps[:].rearrange("p a c -> p (a c)")
    pv = ps[:].rearrange("p a (g d) -> p (a g) d", d=gdim)
    qv = qn[:, 0:1024].rearrange("p (g d) -> p g d", d=gdim)
    if gain is not None:
        x.op("act", lambda e: e.activation(out=sq[:, 0:1024], in_=psf, func=AF.Square), r=[ps.d], w=[sq.d])
        x.op("dve", lambda e: e.tensor_reduce(out=ssq[:, 0:ng2], in_=sq[:, 0:1024].rearrange("p (g d) -> p g d", d=gdim),
                                              axis=AX.X, op=ALU.add), r=[sq.d], w=[ssq.d])
        rsqrt_mean(x, c, rs, ssq, gdim, ng2)
        x.op("dve", lambda e: e.tensor_tensor(out=qv, in0=pv, in1=bc(rs[:, 0:ng2].unsqueeze(2), [128, ng2, gdim]),
                                              op=ALU.mult), r=[ps.d, rs.d], w=[qn.d])
        x.op("pool", lambda e: e.tensor_tensor(out=qv, in0=qv, in1=bc(gain[:, 0:gdim].unsqueeze(1), [128, ng2, gdim]),
                                               op=ALU.mult), r=[qn.d, gain.d], w=[qn.d])
    else:
        x.op("act", lambda e: e.activation(out=qn[:, 0:1024], in_=psf, func=AF.Copy), r=[ps.d], w=[qn.d])
    q4 = qn[:, 0:1024].rearrange("p (a g d) -> p a g d", a=2, d=gdim)
    x1 = q4[:, :, :, 0:half]
    x2 = q4[:, :, :, half:2 * half]
    cb = bc(cos[:, t0:t0 + 2, :].unsqueeze(2), [128, 2, ng, half])
    sb_ = bc(sin[:, t0:t0 + 2, :].unsqueeze(2), [128, 2, ng, half])
    tv = [tt[:, 0:ng2 * half].rearrange("p (a g h) -> p a g h", a=2, h=half) for tt in (t1, t2, t3, t4)]
    x.op("dve", lambda e: e.tensor_tensor(out=tv[0], in0=x1, in1=cb, op=ALU.mult), r=[qn.d, cos.d], w=[t1.d])
    x.op("dve", lambda e: e.tensor_tensor(out=tv[1], in0=x2, in1=sb_, op=ALU.mult), r=[qn.d, sin.d], w=[t2.d])
    x.op("pool", lambda e: e.tensor_tensor(out=tv[2], in0=x2, in1=cb, op=ALU.mult), r=[qn.d, cos.d], w=[t3.d])
    x.op("pool", lambda e: e.tensor_tensor(out=tv[3], in0=x1, in1=sb_, op=ALU.mult), r=[qn.d, sin.d], w=[t4.d])
    x.op("dve", lambda e: e.tensor_tensor(out=x1, in0=tv[0], in1=tv[1], op=ALU.subtract),
         r=[t1.d, t2.d], w=[qn.d])
    x.op("pool", lambda e: e.tensor_tensor(out=x2, in0=tv[2], in1=tv[3], op=ALU.add),
         r=[t3.d, t4.d, qn.d], w=[qn.d])
    x.op("act", lambda e: e.activation(out=out_bf[:, 0:1024], in_=qn[:, 0:1024], func=AF.Copy),
         r=[qn.d], w=[out_bf.d])


def alloc_qk_tiles(x, ls):
    qn = sbt(x, [128, 1024], F32, "qn", ls)
    sq = sbt(x, [128, 1024], F32, "sq", ls)
    ssq = sbt(x, [128, 16], F32, "ssq", ls)
    rs = sbt(x, [128, 16], F32, "rs", ls)
    ts = [sbt(x, [128, 128], F32, "rt%d" % i, ls) for i in range(4)]
    return (qn, sq, ssq, rs, ts[0], ts[1], ts[2], ts[3])


class ProjCtx:
    pass


def emit_proj(x, c, w_ap, hT, hT_d, ntiles, blocks):
    with Scope(x) as ls:
        wb = [sbt(x, [128, 16, 512], BF16, "wb%d" % i, ls) for i in range(2)]
        pp = [pst(x, [128, 512], F32, "pp%d" % i, ls) for i in range(2)]
        pp2 = ([pst(x, [128, 2, 512], F32, "pp2_%d" % i, ls) for i in range(2)]
               if any(b_[2] == "tok2" for b_ in blocks) else None)
        n = 0
        for bi, (col0, ncols, mode, handler) in enumerate(blocks):
            wt = wb[bi % 2]
            x.dma("pool", wt[:, :, 0:ncols], w_ap[:, col0:col0 + ncols].rearrange("(k p) f -> p k f", p=128),
                  w=[wt.d])
            if mode == "tok":
                for t in range(ntiles):
                    ps = pp[n % 2]
                    n += 1
                    for kc in range(16):
                        x.op("pe", lambda e: e.matmul(ps[:, 0:ncols], hT[:, kc, t * 128:(t + 1) * 128],
                                                      wt[:, kc, 0:ncols], start=(kc == 0), stop=(kc == 15)),
                             r=[hT_d[t], wt.d], w=[ps.d], inc=(kc == 15))
                    handler(t, ps)
            elif mode == "tok2":
                for t in range(0, ntiles, 2):
                    ps = pp2[n % 2]
                    n += 1
                    for a_ in range(2):
                        for kc in range(16):
                            x.op("pe", lambda e: e.matmul(ps[:, a_, 0:ncols], hT[:, kc, (t + a_) * 128:(t + a_ + 1) * 128],
                                                          wt[:, kc, 0:ncols], start=(kc == 0), stop=(kc == 15)),
                                 r=[hT_d[t + a_], wt.d], w=[ps.d], inc=(kc == 15))
                    handler(t, ps)
            else:
                for fc in range(ncols // 128):
                    for tb in range(ntiles // 4):
                        ps = pp[n % 2]
                        n += 1
                        for kc in range(16):
                            x.op("pe", lambda e: e.matmul(ps[:, :], wt[:, kc, fc * 128:(fc + 1) * 128],
                                                          hT[:, kc, tb * 512:(tb + 1) * 512],
                                                          start=(kc == 0), stop=(kc == 15)),
                                 r=hT_d[tb * 4:tb * 4 + 4] + [wt.d], w=[ps.d], inc=(kc == 15))
                        handler(fc, tb, ps)


def emit_LA(x, c, dram, kind, dm=False, ada_split=None):
    x_in = dram("x_in", [TOK, D], F32, "ExternalInput")
    c_in = dram("c_in", [D], F32, "ExternalInput")
    pos = dram("pos", [TOK], I32, "ExternalInput")
    adaw = dram("ada_w", [D, 3 * D if ada_split is not None else 6 * D], F32, "ExternalInput")
    adab = dram("ada_b", [6 * D], F32, "ExternalInput")
    g1 = dram("g1", [D], F32, "ExternalInput")
    ada = dram("ada", [6 * D], F32, "ExternalOutput")
    FIN = {0: 6144, 1: 10304, 2: 4176}[kind]
    w_in = dram("w_in", [D, FIN], F32, "ExternalInput")
    outs = []
    with Scope(x) as st:
        d_ada = x.mkdep("ada")
        d_x = x.mkdep("x")
        if ada_split is None:
            emit_ada(x, c_in, adaw, adab, ada, d_ada)
        else:
            emit_ada_split(x, c_in, adaw, adab, ada, d_ada, ada_split)
        hT = x.sb([128, 16, TOK], BF16, "hT")
        hT_d = [x.mkdep("hT%d" % t) for t in range(NT)]
        with Scope(x) as ls:
            sT, shT = emit_mod_cols(x, ls, g1, ada, d_ada, 1, 0)
            emit_norm_hT(x, c, x_in, d_x, NT, sT, shT, hT, hT_d)
        ls = st
        if kind == 0:
            invf = dram("invf", [128, 8], F32, "ExternalInput")
            gq = dram("gq", [64], F32, "ExternalInput")
            gk = dram("gk", [64], F32, "ExternalInput")
            qT = dram("qT", [16, 128, TOK], BF16, "ExternalOutput")
            kT = dram("kT", [16, 128, TOK], BF16, "ExternalOutput")
            v = dram("v", [2, TOK, 1024] if dm else [TOK, 2048], BF16, "ExternalOutput")
            outs = [x.mkdep("qT"), x.mkdep("kT"), x.mkdep("v")]
            cos, sin = emit_rope_tables(x, ls, pos, invf, 8, NT)
            gqb = load_bcast(x, ls, gq, 64, "gqb")
            gkb = load_bcast(x, ls, gk, 64, "gkb")
            qk_tiles = alloc_qk_tiles(x, ls)
            qbf = sbt(x, [128, 1024], BF16, "qbf", ls)
            stage = [sbt(x, [128, 4, TOK], BF16, "stage%d" % i, ls) for i in range(2)]
            ptr = [pst(x, [128, 512], BF16, "ptr%d" % i, ls) for i in range(2)]
            vst = [sbt(x, [128, 512], BF16, "vst%d" % i, ls) for i in range(2)]
            blocks = []
            cnt = [0]
            for which in range(2):
                for hb in range(4):
                    def handler(t, ps, which=which, hb=hb):
                        sg = stage[(which * 4 + hb) % 2]
                        emit_qk_post2(x, c, qk_tiles, ps, 64, gqb if which == 0 else gkb, 8, cos, sin, t, qbf)
                        for a_ in range(2):
                            p = ptr[cnt[0] % 2]
                            cnt[0] += 1
                            for j in range(4):
                                x.op("pe", lambda e: e.transpose(p[:, j * 128:(j + 1) * 128],
                                                                 qbf[:, a_ * 512 + j * 128:a_ * 512 + (j + 1) * 128], c.ident[:]),
                                     r=[qbf.d, c.ident.d], w=[p.d], inc=(j == 3))
                            x.op("dve", lambda e: e.tensor_copy(sg[:, :, (t + a_) * 128:(t + a_ + 1) * 128],
                                                                p[:, :].rearrange("p (a b) -> p a b", a=4)),
                                 r=[p.d], mw=[sg.d])
                        if t == NT - 2:
                            dst = (qT if which == 0 else kT)[hb * 4:(hb + 1) * 4, :, :].rearrange("h p t -> p h t")
                            x.dma("sp", dst, sg[:], r=[sg.d], mw=[outs[which]])
                    blocks.append((which * 2048 + hb * 512, 512, "tok2", handler))
            for vb in range(4):
                def vhandler(t, ps, vb=vb):
                    vs = vst[cnt[0] % 2]
                    cnt[0] += 1
                    x.op("act", lambda e: e.activation(out=vs[:], in_=ps[:, :], func=AF.Copy), r=[ps.d], w=[vs.d])
                    vdst = (v[vb // 2, t * 128:(t + 1) * 128, (vb % 2) * 512:(vb % 2 + 1) * 512] if dm
                            else v[t * 128:(t + 1) * 128, vb * 512:(vb + 1) * 512])
                    x.dma("sp", vdst, vs[:], r=[vs.d], mw=[outs[2]])
                blocks.append((4096 + vb * 512, 512, "tok", vhandler))
            emit_proj(x, c, w_in, hT, hT_d, NT, blocks)
        elif kind == 1:
            blocks = la_handlers_ssd(x, c, ls, dram, outs, dm)
            emit_proj(x, c, w_in, hT, hT_d, NT, blocks)
        else:
            blocks = la_handlers_dsa(x, c, ls, dram, outs, pos, dm)
            emit_proj(x, c, w_in, hT, hT_d, NT, blocks)


def emit_LB_DA(x, c, dram, lambda_init):
    NH = 8
    qT = dram("qT", [NH, 128, S], BF16, "ExternalInput")
    kT = dram("kT", [NH, 128, S], BF16, "ExternalInput")
    v = dram("v", [S, NH * 128], BF16, "ExternalInput")
    lam4 = dram("lam4", [4, 64], F32, "ExternalInput")
    gqk = dram("gqk", [2, 64], F32, "ExternalInput")
    subg = dram("subg", [128], F32, "ExternalInput")
    o = dram("o", [S, NH * 128], BF16, "ExternalOutput")
    with Scope(x) as st:
        d_o = x.mkdep("o")
        ls = st
        lt = [load_bcast(x, ls, lam4[i], 64, "lam%d" % i) for i in range(4)]
        gt = [load_bcast(x, ls, gqk[i], 64, "gqk%d" % i) for i in range(2)]
        gs = load_bcast(x, ls, subg, 128, "gs")
        x.op("dve", lambda e: e.tensor_scalar(out=gs[:], in0=gs[:], scalar1=1.0 - lambda_init, scalar2=None,
                                              op0=ALU.mult), r=[gs.d], w=[gs.d])
        pr = sbt(x, [128, 64], F32, "pr")
        s12 = sbt(x, [128, 2], F32, "s12")
        e12 = sbt(x, [128, 2], F32, "e12")
        neglam = sbt(x, [128, 1], F32, "neglam")
        for i in range(2):
            x.op("dve", lambda e: e.tensor_tensor(out=pr[:], in0=lt[2 * i][:], in1=lt[2 * i + 1][:], op=ALU.mult),
                 r=[lt[2 * i].d, lt[2 * i + 1].d], w=[pr.d])
            x.op("dve", lambda e: e.tensor_reduce(out=s12[:, i:i + 1], in_=pr[:], axis=AX.X, op=ALU.add),
                 r=[pr.d], w=[s12.d])
        x.op("act", lambda e: e.activation(out=e12[:], in_=s12[:], func=AF.Exp), r=[s12.d], w=[e12.d])
        x.op("dve", lambda e: e.scalar_tensor_tensor(out=neglam[:], in0=e12[:, 1:2], scalar=-lambda_init,
                                                     in1=e12[:, 0:1], op0=ALU.add, op1=ALU.subtract),
             r=[e12.d], w=[neglam.d])
        gm = sbt(x, [128, 2], F32, "gm")
        negC = sbt(x, [128, 1], F32, "negC")
        for i in range(2):
            x.op("dve", lambda e: e.tensor_reduce(out=gm[:, i:i + 1], in_=gt[i][:], axis=AX.X, op=ALU.max,
                                                  apply_absolute_value=True), r=[gt[i].d], w=[gm.d])
        x.op("dve", lambda e: e.scalar_tensor_tensor(out=negC[:], in0=gm[:, 0:1], scalar=-8.0, in1=gm[:, 1:2],
                                                     op0=ALU.mult, op1=ALU.mult), r=[gm.d], w=[negC.d])
        kt = [sbt(x, [128, S], BF16, "kt%d" % i) for i in range(2)]
        qt = [sbt(x, [128, S], BF16, "qt%d" % i) for i in range(2)]
        va = [sbt(x, [128, 32, 129], BF16, "va%d" % i) for i in range(2)]
        for i in range(2):
            x.op("pool", lambda e: e.memset(va[i][:, :, 128:129], 1.0), w=[va[i].d])
        pss = [pst(x, [128, 512], F32, "pss%d" % i) for i in range(4)]
        pso = pst(x, [128, 8, 256], F32, "pso")
        pt = [sbt(x, [128, 512], BF16, "pt%d" % i) for i in range(4)]
        R = sbt(x, [128, 8], F32, "R")
        osbs = [sbt(x, [128, 8, 129], F32, "osb%d" % i) for i in range(2)]
        tmp = [sbt(x, [128, 128], F32, "tmp%d" % i) for i in range(2)]
        of = sbt(x, [128, 4, 128], F32, "of")
        sq = sbt(x, [128, 512], F32, "sqo")
        ssq = sbt(x, [128, 4], F32, "ssqo")
        rs = sbt(x, [128, 4], F32, "rso")
        ob = [sbt(x, [128, 4, 128], BF16, "ob%d" % i) for i in range(2)]
        zr = sbt(x, [128, 512], BF16, "zr")
        x.op("pool", lambda e: e.memset(zr[:], 0.0), w=[zr.d])
        psob = pso[:].rearrange("p a b -> p (a b)")
        n = 0
        nq = 0
        def load_head(h):
            hi_ = h % 2
            x.dma("sp", kt[hi_][:], kT[h], w=[kt[hi_].d])
            x.dma("sp", qt[hi_][:], qT[h], w=[qt[hi_].d])
            x.dma("sp", va[hi_][:, :, 0:128], v[:, h * 128:(h + 1) * 128].rearrange("(kb p) d -> p kb d", p=128),
                  mw=[va[hi_].d])

        load_head(0)
        for h in range(NH):
            hi = h % 2
            if h + 1 < NH:
                load_head(h + 1)
            steps = [(qc, kb, comp) for qc in range(8) for kb in range(4 * qc + 4) for comp in range(2)]

            def scores(idx):
                qc, kb, comp = steps[idx]
                jmin = max(0, kb - 4 * qc)
                diag = kb >= 4 * qc
                N = 512 - 128 * jmin
                q0 = qc * 512 + 128 * jmin
                ps = pss[idx % 4]
                ptt = pt[idx % 4]
                pr_ = slice(comp * 64, (comp + 1) * 64)
                lhs = kt[hi][pr_, kb * 128:(kb + 1) * 128]
                if diag:
                    x.op("pe", lambda e: e.matmul(ps[:, 0:128], lhs, qt[hi][pr_, q0:q0 + 128],
                                                  start=True, stop=False),
                         r=[kt[hi].d, qt[hi].d], w=[ps.d], inc=False)
                    x.op("pe", lambda e: e.matmul(ps[:, 0:128], c.ident[:], c.negT[:],
                                                  start=False, stop=True),
                         r=[c.ident.d, c.negT.d], w=[ps.d], inc=(N == 128))
                    if N > 128:
                        x.op("pe", lambda e: e.matmul(ps[:, 128:N], lhs, qt[hi][pr_, q0 + 128:q0 + N],
                                                      start=True, stop=True),
                             r=[kt[hi].d, qt[hi].d], w=[ps.d], inc=True)
                else:
                    x.op("pe", lambda e: e.matmul(ps[:, 0:N], lhs, qt[hi][pr_, q0:q0 + N],
                                                  start=True, stop=True),
                         r=[kt[hi].d, qt[hi].d], w=[ps.d], inc=True)
                x.op("act", lambda e: e.activation(out=ptt[:, 0:N], in_=ps[:, 0:N], func=AF.Exp,
                                                   scale=0.125, bias=negC[:, 0:1]),
                     r=[ps.d, negC.d], w=[ptt.d])

            def pv(idx):
                qc, kb, comp = steps[idx]
                jmin = max(0, kb - 4 * qc)
                ptt = pt[idx % 4]
                for j in range(jmin, 4):
                    last = (kb == 4 * qc + j)
                    x.op("pe", lambda e: e.matmul(pso[:, comp * 4 + j, 0:129],
                                                  ptt[:, (j - jmin) * 128:(j - jmin + 1) * 128],
                                                  va[hi][:, kb, :], start=False,
                                                  stop=(last and j % 2 == 1)),
                         r=[ptt.d, va[hi].d], w=[pso.d], inc=(j == 3))

            def epilogue(qc):
                nonlocal nq
                osb = osbs[nq % 2]
                x.op("act", lambda e: e.activation(out=osb[:], in_=pso[:, :, 0:129], func=AF.Copy), r=[pso.d], w=[osb.d])
                x.op("dve", lambda e: e.reciprocal(out=R[:], in_=osb[:, :, 128:129].rearrange("p a b -> p (a b)")),
                     r=[osb.d], w=[R.d])
                x.op("dve", lambda e: e.tensor_scalar(out=R[:, 4:8], in0=R[:, 4:8], scalar1=neglam[:, 0:1],
                                                      scalar2=None, op0=ALU.mult), r=[R.d, neglam.d], w=[R.d])
                for j in range(4):
                    tm = tmp[j % 2]
                    x.op("act", lambda e: e.activation(out=tm[:], in_=osb[:, 4 + j, 0:128], func=AF.Copy,
                                                       scale=R[:, 4 + j:5 + j]), r=[osb.d, R.d], w=[tm.d])
                    x.op("dve", lambda e: e.scalar_tensor_tensor(out=of[:, j, :], in0=osb[:, j, 0:128],
                                                                 scalar=R[:, j:j + 1], in1=tm[:],
                                                                 op0=ALU.mult, op1=ALU.add),
                         r=[osb.d, R.d, tm.d], mw=[of.d])
                ofl = of[:].rearrange("p a b -> p (a b)")
                x.op("act", lambda e: e.activation(out=sq[:], in_=ofl, func=AF.Square), r=[of.d], w=[sq.d])
                x.op("dve", lambda e: e.tensor_reduce(out=ssq[:], in_=sq[:].rearrange("p (a b) -> p a b", a=4),
                                                      axis=AX.X, op=ALU.add), r=[sq.d], w=[ssq.d])
                rsqrt_mean(x, c, rs, ssq, 128, 4)
                x.op("dve", lambda e: e.tensor_tensor(out=of[:], in0=of[:], in1=bc(rs[:].unsqueeze(2), [128, 4, 128]),
                                                      op=ALU.mult), r=[of.d, rs.d], w=[of.d])
                obb = ob[nq % 2]
                nq += 1
                x.op("pool", lambda e: e.tensor_tensor(out=obb[:], in0=of[:], in1=bc(gs[:].unsqueeze(1), [128, 4, 128]),
                                                       op=ALU.mult), r=[of.d, gs.d], w=[obb.d])
                x.dma("sp", o[qc * 512:(qc + 1) * 512, h * 128:(h + 1) * 128].rearrange("(j p) d -> p j d", p=128),
                      obb[:], r=[obb.d], mw=[d_o])

            scores(0)
            scores(1)
            for idx, (qc, kb, comp) in enumerate(steps):
                if kb == 0 and comp == 0:
                    for bnk in range(4):
                        x.op("pe", lambda e: e.matmul(psob[:, bnk * 512:(bnk + 1) * 512], zr[:, 0:128], zr[:],
                                                      start=True, stop=False), r=[zr.d], w=[pso.d], inc=False)
                if idx + 2 < len(steps):
                    scores(idx + 2)
                pv(idx)
                if kb == 4 * qc + 3 and comp == 1:
                    epilogue(qc)


def emit_outproj(x, c, o_ap, d_o, FO, wout_ap, x_ap, d_x, gate_b, xmid_ap, d_xmid):
    KO = FO // 128
    TB = 1024
    with Scope(x) as ls:
        oT = sbt(x, [128, KO, TB], BF16, "oT", ls)
        oT_d = [x.mkdep("oT%d" % t) for t in range(TB // 128)]
        ls.deps.extend(oT_d)
        ot = [sbt(x, [128, FO], BF16, "ot%d" % i, ls) for i in range(2)]
        pt = [pst(x, [128, 512], BF16, "pto%d" % i, ls) for i in range(2)]
        wb = [sbt(x, [128, KO, 512], BF16, "wo%d" % i, ls) for i in range(2)]
        pp = [pst(x, [128, 512], F32, "ppo%d" % i, ls) for i in range(2)]
        tm = [sbt(x, [128, 512], F32, "tmo%d" % i, ls) for i in range(2)]
        xt = [sbt(x, [128, 512], F32, "xto%d" % i, ls) for i in range(2)]
        n = 0
        nw = 0
        for tb in range(TOK // TB):
            for t in range(TB // 128):
                i = t % 2
                tok0 = tb * TB + t * 128
                x.dma("sp", ot[i][:], o_ap[tok0:tok0 + 128, :], r=[d_o], w=[ot[i].d])
                for g in range(KO // 4):
                    p = pt[g % 2]
                    for j in range(4):
                        kc = g * 4 + j
                        x.op("pe", lambda e: e.transpose(p[:, j * 128:(j + 1) * 128], ot[i][:, kc * 128:(kc + 1) * 128],
                                                         c.ident[:]), r=[ot[i].d, c.ident.d], w=[p.d], inc=(j == 3))
                    dst = oT[:, g * 4:(g + 1) * 4, t * 128:(t + 1) * 128]
                    src = p[:, :].rearrange("p (a b) -> p a b", a=4)
                    if g % 2 == 0:
                        x.op("dve", lambda e: e.tensor_copy(dst, src), r=[p.d], mw=[oT_d[t]])
                    else:
                        x.op("act", lambda e: e.activation(out=dst, in_=src, func=AF.Copy), r=[p.d], mw=[oT_d[t]])
            for cb in range(4):
                wt = wb[nw % 2]
                nw += 1
                x.dma("pool", wt[:], wout_ap[:, cb * 512:(cb + 1) * 512].rearrange("(k p) f -> p k f", p=128), w=[wt.d])
                for t in range(TB // 128):
                    tok0 = tb * TB + t * 128
                    ps = pp[n % 2]
                    tmm = tm[n % 2]
                    xtt = xt[n % 2]
                    n += 1
                    x.dma("sp", xtt[:], x_ap[tok0:tok0 + 128, cb * 512:(cb + 1) * 512], r=[d_x], w=[xtt.d])
                    for kc in range(KO):
                        x.op("pe", lambda e: e.matmul(ps[:], oT[:, kc, t * 128:(t + 1) * 128], wt[:, kc, :],
                                                      start=(kc == 0), stop=(kc == KO - 1)),
                             r=[oT_d[t], wt.d], w=[ps.d], inc=(kc == KO - 1))
                    x.op("dve", lambda e: e.tensor_tensor(out=tmm[:], in0=ps[:], in1=gate_b[:, cb * 512:(cb + 1) * 512],
                                                          op=ALU.mult), r=[ps.d, gate_b.d], w=[tmm.d])
                    x.op("pool", lambda e: e.tensor_tensor(out=tmm[:], in0=tmm[:], in1=xtt[:], op=ALU.add),
                         r=[tmm.d, xtt.d], w=[tmm.d])
                    x.dma("act", xmid_ap[tok0:tok0 + 128, cb * 512:(cb + 1) * 512], tmm[:], r=[tmm.d], mw=[d_xmid])


def emit_ffn(x, c, xmid_ap, d_xmid, sT, shT, gate_b, wgu_ap, wd_ap, xout_ap, d_xout, act_ap):
    NFC = FFN // 128
    d_act = x.mkdep("act_d")
    with Scope(x) as ls:
        h2T = sbt(x, [128, 16, TOK], BF16, "h2T", ls)
        h2_d = [x.mkdep("h2T%d" % t) for t in range(NT)]
        ls.deps.extend(h2_d)
        emit_norm_hT(x, c, xmid_ap, d_xmid, NT, sT, shT, h2T, h2_d)
        wg = [sbt(x, [128, 16, 256], BF16, "wg%d" % i, ls) for i in range(2)]
        wu = [sbt(x, [128, 16, 256], BF16, "wu%d" % i, ls) for i in range(2)]
        psg = [pst(x, [128, 512], F32, "psg%d" % i, ls) for i in range(2)]
        psu = [pst(x, [128, 512], F32, "psu%d" % i, ls) for i in range(2)]
        sg = [sbt(x, [128, 512], F32, "sg%d" % i, ls) for i in range(2)]
        ab = [sbt(x, [128, 4, 128], BF16, "ab%d" % i, ls) for i in range(3)]
        n = 0
        for blk in range(FFN // 256):
            wgt = wg[blk % 2]
            wut = wu[blk % 2]
            x.dma("pool", wgt[:], wgu_ap[:, blk * 256:(blk + 1) * 256].rearrange("(k p) f -> p k f", p=128), w=[wgt.d])
            x.dma("pool", wut[:], wgu_ap[:, FFN + blk * 256:FFN + (blk + 1) * 256].rearrange("(k p) f -> p k f", p=128),
                  w=[wut.d])
            for fl in range(2):
                fc = blk * 2 + fl
                for tb in range(TOK // 512):
                    pg = psg[n % 2]
                    pu = psu[n % 2]
                    sgg = sg[n % 2]
                    abb = ab[n % 3]
                    n += 1
                    hd = h2_d[tb * 4:tb * 4 + 4]
                    for kc in range(16):
                        x.op("pe", lambda e: e.matmul(pg[:], wgt[:, kc, fl * 128:(fl + 1) * 128],
                                                      h2T[:, kc, tb * 512:(tb + 1) * 512],
                                                      start=(kc == 0), stop=(kc == 15)),
                             r=hd + [wgt.d], w=[pg.d], inc=(kc == 15))
                    for kc in range(16):
                        x.op("pe", lambda e: e.matmul(pu[:], wut[:, kc, fl * 128:(fl + 1) * 128],
                                                      h2T[:, kc, tb * 512:(tb + 1) * 512],
                                                      start=(kc == 0), stop=(kc == 15)),
                             r=hd + [wut.d], w=[pu.d], inc=(kc == 15))
                    x.op("act", lambda e: e.activation(out=sgg[:], in_=pg[:], func=AF.Silu), r=[pg.d], w=[sgg.d])
                    x.op("dve", lambda e: e.tensor_tensor(out=abb[:].rearrange("p a b -> p (a b)"), in0=pu[:], in1=sgg[:],
                                                          op=ALU.mult), r=[pu.d, sgg.d], w=[abb.d])
                    x.dma("sp", act_ap[tb * 4:(tb + 1) * 4, :, fc, :].rearrange("t p k -> p t k"), abb[:],
                          r=[abb.d], mw=[d_act])
    with Scope(x) as ls:
        wdA = sbt(x, [128, 22, 512], BF16, "wdA", ls)
        wdB = sbt(x, [128, 22, 512], BF16, "wdB", ls)
        at = [sbt(x, [128, NFC, 128], BF16, "at%d" % i, ls) for i in range(3)]
        psd = [pst(x, [128, 512], F32, "psd%d" % i, ls) for i in range(2)]
        tm = [sbt(x, [128, 512], F32, "tmf%d" % i, ls) for i in range(2)]
        xt = [sbt(x, [128, 512], F32, "xtf%d" % i, ls) for i in range(2)]
        nd = 0
        for cb in range(4):
            x.dma("pool", wdA[:], wd_ap[0:22 * 128, cb * 512:(cb + 1) * 512].rearrange("(k p) f -> p k f", p=128),
                  w=[wdA.d])
            x.dma("pool", wdB[:], wd_ap[22 * 128:44 * 128, cb * 512:(cb + 1) * 512].rearrange("(k p) f -> p k f", p=128),
                  w=[wdB.d])
            for t in range(NT):
                tok0 = t * 128
                ps = psd[nd % 2]
                tmm = tm[nd % 2]
                xtt = xt[nd % 2]
                att = at[nd % 3]
                if nd == 0:
                    x.dma("sp", att[:], act_ap[0], r=[d_act], w=[att.d])
                if nd + 1 < 4 * NT:
                    x.dma("sp", at[(nd + 1) % 3][:], act_ap[(t + 1) % NT], r=[d_act], w=[at[(nd + 1) % 3].d])
                nd += 1
                x.dma("sp", xtt[:], xmid_ap[tok0:tok0 + 128, cb * 512:(cb + 1) * 512], r=[d_xmid], w=[xtt.d])
                for fc in range(NFC):
                    wt = wdA if fc < 22 else wdB
                    x.op("pe", lambda e: e.matmul(ps[:], att[:, fc, :], wt[:, fc % 22, :],
                                                  start=(fc == 0), stop=(fc == NFC - 1)),
                         r=[att.d, wt.d], w=[ps.d], inc=(fc == NFC - 1 or fc == 21))
                x.op("dve", lambda e: e.tensor_tensor(out=tmm[:], in0=ps[:], in1=gate_b[:, cb * 512:(cb + 1) * 512],
                                                      op=ALU.mult), r=[ps.d, gate_b.d], w=[tmm.d])
                x.op("pool", lambda e: e.tensor_tensor(out=tmm[:], in0=tmm[:], in1=xtt[:], op=ALU.add),
                     r=[tmm.d, xtt.d], w=[tmm.d])
                x.dma("act", xout_ap[tok0:tok0 + 128, cb * 512:(cb + 1) * 512], tmm[:], r=[tmm.d], mw=[d_xout])


def emit_LC(x, c, dram, FO):
    x_in = dram("x_in", [TOK, D], F32, "ExternalInput")
    o_in = dram("o_in", [TOK, FO], BF16, "ExternalInput")
    ada = dram("ada", [6 * D], F32, "ExternalInput")
    g2 = dram("g2", [D], F32, "ExternalInput")
    w_out = dram("w_out", [FO, D], F32, "ExternalInput")
    wgu = dram("wgu", [D, 2 * FFN], F32, "ExternalInput")
    wd = dram("wd", [FFN, D], F32, "ExternalInput")
    x_mid = dram("x_mid", [TOK, D], F32, "Internal")
    act_d = dram("act_d", [NT, 128, FFN // 128, 128], BF16, "Internal")
    x_out = dram("x_out", [TOK, D], F32, "ExternalOutput")
    with Scope(x) as st:
        d_none = x.mkdep("in")
        d_xmid = x.mkdep("xmid")
        d_xout = x.mkdep("xout")
        with Scope(x) as ls:
            g1b = load_bcast(x, ls, ada[2 * D:3 * D], D, "g1b")
            emit_outproj(x, c, o_in, d_none, FO, w_out, x_in, d_none, g1b, x_mid, d_xmid)
        with Scope(x) as ls:
            g2b = load_bcast(x, ls, ada[5 * D:6 * D], D, "g2b")
            sT, shT = emit_mod_cols(x, ls, g2, ada, d_none, 4, 3)
            emit_ffn(x, c, x_mid, d_xmid, sT, shT, g2b, wgu, wd, x_out, d_xout, act_d)


def la_handlers_ssd(x, c, ls, dram, outs_holder, dm=False):
    z = dram("z", [2, TOK, 2048] if dm else [TOK, 4096], BF16, "ExternalOutput")
    xbcT = dram("xbcT", [2, 3072, TOK] if dm else [6144, TOK], BF16, "ExternalOutput")
    dtr = dram("dtr", [2, TOK, 32] if dm else [TOK, 64], F32, "ExternalOutput")
    outs = [x.mkdep("z"), x.mkdep("xbcT"), x.mkdep("dtr")]
    outs_holder.extend(outs)
    zst = [sbt(x, [128, 512], BF16, "zst%d" % i, ls) for i in range(2)]
    fst = [sbt(x, [128, 512], BF16, "fst%d" % i, ls) for i in range(2)]
    dst_ = [sbt(x, [128, 64], F32, "dst%d" % i, ls) for i in range(2)]
    cnt = [0]
    blocks = []
    for zb in range(8):
        def zh(t, ps, zb=zb):
            s_ = zst[cnt[0] % 2]
            cnt[0] += 1
            x.op("act", lambda e: e.activation(out=s_[:], in_=ps[:, :], func=AF.Copy), r=[ps.d], w=[s_.d])
            zdst = (z[zb // 4, t * 128:(t + 1) * 128, (zb % 4) * 512:(zb % 4 + 1) * 512] if dm
                    else z[t * 128:(t + 1) * 128, zb * 512:(zb + 1) * 512])
            x.dma("sp", zdst, s_[:], r=[s_.d], mw=[outs[0]])
        blocks.append((zb * 512, 512, "tok", zh))
    for xb in range(12):
        def xh(fc, tb, ps, xb=xb):
            s_ = fst[cnt[0] % 2]
            cnt[0] += 1
            x.op("act", lambda e: e.activation(out=s_[:], in_=ps[:, :], func=AF.Copy), r=[ps.d], w=[s_.d])
            if dm:
                dd, rb = ((xb // 4, (xb % 4) * 512) if xb < 8 else ((xb - 8) % 2, 2048 + ((xb - 8) // 2) * 512))
                xdst = xbcT[dd, rb + fc * 128:rb + fc * 128 + 128, tb * 512:(tb + 1) * 512]
            else:
                r0 = xb * 512 + fc * 128
                xdst = xbcT[r0:r0 + 128, tb * 512:(tb + 1) * 512]
            x.dma("sp", xdst, s_[:], r=[s_.d], mw=[outs[1]])
        blocks.append((4096 + xb * 512, 512, "feat", xh))

    def dh(t, ps):
        s_ = dst_[cnt[0] % 2]
        cnt[0] += 1
        x.op("act", lambda e: e.activation(out=s_[:], in_=ps[:, 0:64], func=AF.Copy), r=[ps.d], w=[s_.d])
        if dm:
            for dd in range(2):
                x.dma("sp", dtr[dd, t * 128:(t + 1) * 128, :], s_[:, dd * 32:(dd + 1) * 32], r=[s_.d], mw=[outs[2]])
        else:
            x.dma("sp", dtr[t * 128:(t + 1) * 128, :], s_[:], r=[s_.d], mw=[outs[2]])
    blocks.append((10240, 64, "tok", dh))
    return blocks


def emit_LB_SSD(x, c, dram):
    NHh = 32
    NCH = 24
    raw = dram("raw", [NCH * 128, S], F32, "ExternalInput")
    convw = dram("convw", [4, NCH * 128], F32, "ExternalInput")
    convb = dram("convb", [NCH * 128], F32, "ExternalInput")
    dtr = dram("dtr", [S, NHh], F32, "ExternalInput")
    hp = dram("hp", [3, NHh], F32, "ExternalInput")
    z = dram("z", [S, 2048], BF16, "ExternalInput")
    ng = dram("ng", [2048], F32, "ExternalInput")
    tokd = dram("tokd", [S, 2560], BF16, "Internal")
    featd = dram("featd", [1024, S], BF16, "Internal")
    y = dram("y", [S, 2048], BF16, "ExternalOutput")
    with Scope(x) as st:
        d_in = x.mkdep("in")
        d_tok = x.mkdep("tokd")
        d_feat = x.mkdep("featd")
        d_y = x.mkdep("y")
        with Scope(x) as ls:
            cw = sbt(x, [128, 4, NCH], F32, "cw", ls)
            cb_ = sbt(x, [128, NCH], F32, "cb", ls)
            for j in range(4):
                x.dma("sp", cw[:, j, :], convw[j].rearrange("(k p) -> p k", p=128), mw=[cw.d],
                      allow_slow_non_contiguous=True)
            x.dma("sp", cb_[:], convb.rearrange("(k p) -> p k", p=128), w=[cb_.d], allow_slow_non_contiguous=True)
            rw = [sbt(x, [128, S + 3], F32, "rw%d" % i, ls) for i in range(2)]
            for i in range(2):
                x.op("pool", lambda e: e.memset(rw[i][:, 0:3], 0.0), w=[rw[i].d])
            acc = sbt(x, [128, S], F32, "acc", ls)
            sil = [sbt(x, [128, S], BF16, "sil%d" % i, ls) for i in range(2)]
            ptc = [pst(x, [128, 512], BF16, "ptc%d" % i, ls) for i in range(2)]
            stg = [sbt(x, [128, 4, 128], BF16, "stg%d" % i, ls) for i in range(2)]
            n = 0
            x.dma("sp", rw[0][:, 3:3 + S], raw[0:128, :], r=[d_in], mw=[rw[0].d])
            for cc in range(NCH):
                r_ = rw[cc % 2]
                sl_ = sil[cc % 2]
                if cc + 1 < NCH:
                    x.dma("sp", rw[(cc + 1) % 2][:, 3:3 + S], raw[(cc + 1) * 128:(cc + 2) * 128, :], r=[d_in],
                          mw=[rw[(cc + 1) % 2].d])
                x.op("dve", lambda e: e.tensor_scalar(out=acc[:], in0=r_[:, 3:3 + S], scalar1=cw[:, 3, cc:cc + 1],
                                                      scalar2=cb_[:, cc:cc + 1], op0=ALU.mult, op1=ALU.add),
                     r=[r_.d, cw.d, cb_.d], w=[acc.d])
                for j in range(3):
                    x.op("dve", lambda e: e.scalar_tensor_tensor(out=acc[:], in0=r_[:, j:j + S],
                                                                 scalar=cw[:, j, cc:cc + 1], in1=acc[:],
                                                                 op0=ALU.mult, op1=ALU.add),
                         r=[r_.d, cw.d, acc.d], w=[acc.d])
                x.op("act", lambda e: e.activation(out=sl_[:], in_=acc[:], func=AF.Silu), r=[acc.d], w=[sl_.d])
                if cc >= 16:
                    x.dma("sp", featd[(cc - 16) * 128:(cc - 15) * 128, :], sl_[:], r=[sl_.d], mw=[d_feat])
                if cc < 20:
                    for tg in range(8):
                        p = ptc[n % 2]
                        sg_ = stg[n % 2]
                        n += 1
                        for j in range(4):
                            tt = tg * 4 + j
                            x.op("pe", lambda e: e.transpose(p[:, j * 128:(j + 1) * 128], sl_[:, tt * 128:(tt + 1) * 128],
                                                             c.ident[:]), r=[sl_.d, c.ident.d], w=[p.d], inc=(j == 3))
                        if n % 2 == 0:
                            x.op("dve", lambda e: e.tensor_copy(sg_[:], p[:, :].rearrange("p (a b) -> p a b", a=4)),
                                 r=[p.d], w=[sg_.d])
                        else:
                            x.op("act", lambda e: e.activation(out=sg_[:], in_=p[:, :].rearrange("p (a b) -> p a b", a=4),
                                                               func=AF.Copy), r=[p.d], w=[sg_.d])
                        x.dma("sp", tokd[tg * 512:(tg + 1) * 512, cc * 128:(cc + 1) * 128].rearrange("(j p) c -> p j c", p=128),
                              sg_[:], r=[sg_.d], mw=[d_tok])
        with Scope(x) as ls:
            hb = [load_bcast(x, ls, hp[i], NHh, "hp%d" % i) for i in range(3)]
            dtb_b, alog_b, dsk_b = hb
            a_b = sbt(x, [128, NHh], F32, "a_b", ls)
            x.op("act", lambda e: e.activation(out=a_b[:], in_=alog_b[:], func=AF.Exp), r=[alog_b.d], w=[a_b.d])
            x.op("dve", lambda e: e.tensor_scalar(out=a_b[:], in0=a_b[:], scalar1=-1.0, scalar2=None, op0=ALU.mult),
                 r=[a_b.d], w=[a_b.d])
            ngb = load_bcast(x, ls, ng, 2048, "ngb")
            sel = sbt(x, [32, NHh, 128], F32, "sel", ls)
            x.op("pool", lambda e: e.memset(sel[:], 1.0), w=[sel.d])
            x.op("pool", lambda e: e.affine_select(out=sel[:], in_=sel[:], pattern=[[-1, NHh], [0, 128]],
                                                   compare_op=ALU.is_equal, fill=0.0, base=0, channel_multiplier=1),
                 r=[sel.d], w=[sel.d])
            St = [sbt(x, [128, 512], F32, "St%d" % g, ls) for g in range(4)]
            Sb = [sbt(x, [128, 512], BF16, "Sb%d" % g, ls) for g in range(4)]
            for g in range(4):
                x.op("pool", lambda e: e.memset(St[g][:], 0.0), w=[St[g].d])
                x.op("pool", lambda e: e.memset(Sb[g][:], 0.0), w=[Sb[g].d])
            xs_t = [sbt(x, [128, 2560], BF16, "xs_t%d" % i, ls) for i in range(2)]
            bct = [sbt(x, [128, 8, 128], BF16, "bct%d" % i, ls) for i in range(2)]
            dtt = [sbt(x, [128, NHh], F32, "dtt%d" % i, ls) for i in range(2)]
            zt = [sbt(x, [128, 2048], BF16, "zt%d" % i, ls) for i in range(2)]
            f = lambda nm, w_: sbt(x, [128, w_], F32, nm, ls)
            dtb, ab, ee, dt_, dta, acum, nacum, alast, eac, wend, decay = [f(nm, NHh) for nm in
                ("dtb", "ab", "ee", "dt_", "dta", "acum", "nacum", "alast", "eac", "wend", "decay")]
            acT = sbt(x, [32, 128], F32, "acT", ls)
            xdt = sbt(x, [128, 2048], BF16, "xdt", ls)
            xde = sbt(x, [128, 2048], BF16, "xde", ls)
            cbm = [sbt(x, [128, 128], F32, "cbm%d" % g, ls) for g in range(4)]
            Eh = [sbt(x, [128, 128], F32, "Eh%d" % i, ls) for i in range(4)]
            Mh = [sbt(x, [128, 128], BF16, "Mh%d" % i, ls) for i in range(4)]
            yf = sbt(x, [128, 2048], F32, "yf", ls)
            t1 = sbt(x, [128, 2048], F32, "t1", ls)
            sqy = sbt(x, [128, 2048], F32, "sqy", ls)
            ssy = sbt(x, [128, 4], F32, "ssy", ls)
            rsy = sbt(x, [128, 4], F32, "rsy", ls)
            yb = [sbt(x, [128, 2048], BF16, "yb%d" % i, ls) for i in range(2)]
            p_small = pst(x, [128, 512], F32, "p_small", ls)
            p_cb = pst(x, [128, 512], F32, "p_cb", ls)
            p_G = [pst(x, [128, 512], F32, "p_G%d" % i, ls) for i in range(2)]
            p_y = [pst(x, [128, 512], F32, "p_y%d" % i, ls) for i in range(2)]
            p_i = pst(x, [128, 512], F32, "p_i", ls)
            p_s = pst(x, [128, 512], F32, "p_s", ls)
            def load_chunk(ck_):
                i_ = ck_ % 2
                t0_ = ck_ * 128
                x.dma("sp", xs_t[i_][:], tokd[t0_:t0_ + 128, :], r=[d_tok], w=[xs_t[i_].d])
                x.dma("sp", bct[i_][:], featd[:, t0_:t0_ + 128].rearrange("(g p) t -> p g t", p=128), r=[d_feat],
                      w=[bct[i_].d])
                x.dma("sp", dtt[i_][:], dtr[t0_:t0_ + 128, :], r=[d_in], w=[dtt[i_].d])
                x.dma("sp", zt[i_][:], z[t0_:t0_ + 128, :], r=[d_in], w=[zt[i_].d])

            load_chunk(0)
            for ck in range(S // 128):
                i = ck % 2
                t0 = ck * 128
                xt_ = xs_t[i]
                bc_ = bct[i]
                if ck + 1 < S // 128:
                    load_chunk(ck + 1)
                x.op("dve", lambda e: e.tensor_tensor(out=dtb[:], in0=dtt[i][:], in1=dtb_b[:], op=ALU.add),
                     r=[dtt[i].d, dtb_b.d], w=[dtb.d])
                x.op("dve", lambda e: e.scalar_tensor_tensor(out=ab[:], in0=dtb[:], scalar=-1.0, in1=dtb[:],
                                                             op0=ALU.mult, op1=ALU.max), r=[dtb.d], w=[ab.d])
                x.op("act", lambda e: e.activation(out=ee[:], in_=ab[:], func=AF.Exp, scale=-1.0), r=[ab.d], w=[ee.d])
                x.op("act", lambda e: e.activation(out=ee[:], in_=ee[:], func=AF.Ln, bias=1.0), r=[ee.d], w=[ee.d])
                x.op("dve", lambda e: e.scalar_tensor_tensor(out=dt_[:], in0=dtb[:], scalar=0.0, in1=ee[:],
                                                             op0=ALU.max, op1=ALU.add), r=[dtb.d, ee.d], w=[dt_.d])
                x.op("dve", lambda e: e.tensor_tensor(out=dta[:], in0=dt_[:], in1=a_b[:], op=ALU.mult),
                     r=[dt_.d, a_b.d], w=[dta.d])
                x.op("pe", lambda e: e.matmul(p_small[:, 0:32], c.trif[:], dta[:], start=True, stop=True),
                     r=[c.trif.d, dta.d], w=[p_small.d])
                x.op("pe", lambda e: e.matmul(p_small[:, 32:64], c.onesf[:], dta[:], start=True, stop=True),
                     r=[c.onesf.d, dta.d], w=[p_small.d])
                x.op("dve", lambda e: e.tensor_copy(acum[:], p_small[:, 0:32]), r=[p_small.d], w=[acum.d])
                x.op("dve", lambda e: e.tensor_scalar(out=nacum[:], in0=p_small[:, 0:32], scalar1=-1.0, scalar2=None,
                                                      op0=ALU.mult), r=[p_small.d], w=[nacum.d])
                x.op("dve", lambda e: e.tensor_copy(alast[:], p_small[:, 32:64]), r=[p_small.d], w=[alast.d])
                x.op("act", lambda e: e.activation(out=eac[:], in_=acum[:], func=AF.Exp), r=[acum.d], w=[eac.d])
                x.op("act", lambda e: e.activation(out=decay[:], in_=alast[:], func=AF.Exp), r=[alast.d], w=[decay.d])
                x.op("dve", lambda e: e.tensor_tensor(out=wend[:], in0=alast[:], in1=acum[:], op=ALU.subtract),
                     r=[alast.d, acum.d], w=[wend.d])
                x.op("act", lambda e: e.activation(out=wend[:], in_=wend[:], func=AF.Exp), r=[wend.d], w=[wend.d])
                x.op("dve", lambda e: e.tensor_tensor(out=wend[:], in0=wend[:], in1=dt_[:], op=ALU.mult),
                     r=[wend.d, dt_.d], w=[wend.d])
                x.op("pe", lambda e: e.matmul(p_small[0:32, 128:256], acum[:], c.identf[:], start=True, stop=True),
                     r=[acum.d, c.identf.d], w=[p_small.d])
                x.op("dve", lambda e: e.tensor_copy(acT[:], p_small[0:32, 128:256]), r=[p_small.d], w=[acT.d])
                xv = xt_[:, 0:2048].rearrange("p (h d) -> p h d", d=64)
                x.op("dve", lambda e: e.tensor_tensor(out=xdt[:].rearrange("p (h d) -> p h d", d=64), in0=xv,
                                                      in1=bc(dt_[:].unsqueeze(2), [128, NHh, 64]), op=ALU.mult),
                     r=[xt_.d, dt_.d], w=[xdt.d])
                x.op("pool", lambda e: e.tensor_tensor(out=xde[:].rearrange("p (h d) -> p h d", d=64), in0=xv,
                                                       in1=bc(wend[:].unsqueeze(2), [128, NHh, 64]), op=ALU.mult),
                     r=[xt_.d, wend.d], w=[xde.d])
                for g in range(4):
                    x.op("pe", lambda e: e.matmul(p_cb[:, g * 128:(g + 1) * 128], bc_[:, g, :], bc_[:, 4 + g, :],
                                                  start=True, stop=True), r=[bc_.d], w=[p_cb.d], inc=(g == 3))
                for g in range(4):
                    if g % 2 == 0:
                        x.op("dve", lambda e: e.tensor_copy(cbm[g][:], p_cb[:, g * 128:(g + 1) * 128]),
                             r=[p_cb.d], w=[cbm[g].d])
                    else:
                        x.op("act", lambda e: e.activation(out=cbm[g][:], in_=p_cb[:, g * 128:(g + 1) * 128], func=AF.Copy),
                             r=[p_cb.d], w=[cbm[g].d])
                for g in range(4):
                    py = p_y[g % 2]
                    x.op("pe", lambda e: e.matmul(p_i[:], bc_[:, 4 + g, :], Sb[g][:], start=True, stop=True),
                         r=[bc_.d, Sb[g].d], w=[p_i.d])
                    def hG(hl):
                        h = g * 8 + hl
                        pgt = p_G[(h % 4) // 2]
                        pgs = slice((h % 2) * 128, (h % 2) * 128 + 128)
                        x.op("pe", lambda e: e.matmul(pgt[:, pgs], sel[:, h, :], acT[:], start=True, stop=False),
                             r=[sel.d, acT.d], w=[pgt.d], inc=False)
                        x.op("pe", lambda e: e.matmul(pgt[:, pgs], c.ident[:], c.negT[:], start=False, stop=True),
                             r=[c.ident.d, c.negT.d], w=[pgt.d])
                        eh = Eh[h % 4]
                        mh = Mh[h % 4]
                        x.op("act", lambda e: e.activation(out=eh[:], in_=pgt[:, pgs], func=AF.Exp,
                                                           bias=nacum[:, h:h + 1]), r=[pgt.d, nacum.d], w=[eh.d])
                        x.op("dve", lambda e: e.tensor_tensor(out=mh[:], in0=eh[:], in1=cbm[g][:], op=ALU.mult),
                             r=[eh.d, cbm[g].d], w=[mh.d])

                    def hY(hl):
                        h = g * 8 + hl
                        mh = Mh[h % 4]
                        x.op("pe", lambda e: e.matmul(py[:, hl * 64:(hl + 1) * 64], mh[:], xdt[:, h * 64:(h + 1) * 64],
                                                      start=True, stop=True), r=[mh.d, xdt.d], w=[py.d])

                    hG(0)
                    hG(1)
                    for hl in range(8):
                        if hl + 2 < 8:
                            hG(hl + 2)
                        hY(hl)
                    gs_ = slice(g * 512, (g + 1) * 512)
                    x.op("dve", lambda e: e.tensor_tensor(out=t1[:, gs_].rearrange("p (h d) -> p h d", d=64),
                                                          in0=p_i[:].rearrange("p (h d) -> p h d", d=64),
                                                          in1=bc(eac[:, g * 8:(g + 1) * 8].unsqueeze(2), [128, 8, 64]),
                                                          op=ALU.mult), r=[p_i.d, eac.d], mw=[t1.d])
                    x.op("dve", lambda e: e.tensor_tensor(out=yf[:, gs_], in0=py[:], in1=t1[:, gs_], op=ALU.add),
                         r=[py.d, t1.d], mw=[yf.d])
                    x.op("pe", lambda e: e.matmul(p_s[:], xt_[:, 2048 + g * 128:2048 + (g + 1) * 128], xde[:, gs_],
                                                  start=True, stop=True), r=[xt_.d, xde.d], w=[p_s.d])
                    x.op("pool", lambda e: e.tensor_tensor(out=St[g][:].rearrange("p (h d) -> p h d", d=64),
                                                           in0=St[g][:].rearrange("p (h d) -> p h d", d=64),
                                                           in1=bc(decay[:, g * 8:(g + 1) * 8].unsqueeze(2), [128, 8, 64]),
                                                           op=ALU.mult), r=[St[g].d, decay.d], w=[St[g].d])
                    x.op("dve", lambda e: e.tensor_tensor(out=St[g][:], in0=p_s[:], in1=St[g][:], op=ALU.add),
                         r=[p_s.d, St[g].d], w=[St[g].d])
                    x.op("act", lambda e: e.activation(out=Sb[g][:], in_=St[g][:], func=AF.Copy),
                         r=[St[g].d], w=[Sb[g].d])
                x.op("pool", lambda e: e.tensor_tensor(out=t1[:].rearrange("p (h d) -> p h d", d=64), in0=xv,
                                                       in1=bc(dsk_b[:].unsqueeze(2), [128, NHh, 64]), op=ALU.mult),
                     r=[xt_.d, dsk_b.d, yf.d], w=[t1.d])
                x.op("dve", lambda e: e.tensor_tensor(out=yf[:], in0=yf[:], in1=t1[:], op=ALU.add),
                     r=[yf.d, t1.d], w=[yf.d])
                x.op("act", lambda e: e.activation(out=t1[:], in_=zt[i][:], func=AF.Silu), r=[zt[i].d, yf.d], w=[t1.d])
                x.op("dve", lambda e: e.tensor_tensor(out=yf[:], in0=yf[:], in1=t1[:], op=ALU.mult),
                     r=[yf.d, t1.d], w=[yf.d])
                x.op("act", lambda e: e.activation(out=sqy[:], in_=yf[:], func=AF.Square), r=[yf.d], w=[sqy.d])
                x.op("dve", lambda e: e.tensor_reduce(out=ssy[:], in_=sqy[:].rearrange("p (g d) -> p g d", g=4),
                                                      axis=AX.X, op=ALU.add), r=[sqy.d], w=[ssy.d])
                rsqrt_mean(x, c, rsy, ssy, 512, 4)
                x.op("dve", lambda e: e.tensor_tensor(out=yf[:].rearrange("p (g d) -> p g d", g=4),
                                                      in0=yf[:].rearrange("p (g d) -> p g d", g=4),
                                                      in1=bc(rsy[:].unsqueeze(2), [128, 4, 512]), op=ALU.mult),
                     r=[yf.d, rsy.d], w=[yf.d])
                x.op("pool", lambda e: e.tensor_tensor(out=yb[i][:], in0=yf[:], in1=ngb[:], op=ALU.mult),
                     r=[yf.d, ngb.d], w=[yb[i].d])
                x.dma("sp", y[t0:t0 + 128, :], yb[i][:], r=[yb[i].d], mw=[d_y])


def la_handlers_dsa(x, c, ls, dram, outs_holder, pos, dm=False):
    invf16 = dram("invf16", [128, 16], F32, "ExternalInput")
    invf8 = dram("invf8", [128, 8], F32, "ExternalInput")
    gq = dram("gq", [128], F32, "ExternalInput")
    gk = dram("gk", [128], F32, "ExternalInput")
    gi = dram("gi", [64], F32, "ExternalInput")
    qT = dram("qT", [2, 16, 128, 8, 128] if dm else [16, 128, TOK], BF16, "ExternalOutput")
    kT = dram("kT", [4, 128, TOK], BF16, "ExternalOutput")
    v = dram("v", [TOK, 512], BF16, "ExternalOutput")
    qiT = dram("qiT", [2, 8, 128, 8, 128] if dm else [8, 128, TOK], BF16, "ExternalOutput")
    kiT = dram("kiT", [64, TOK], BF16, "ExternalOutput")
    wi = dram("wi", [2, 8, 128, 16] if dm else [TOK, 16], F32, "ExternalOutput")
    outs = [x.mkdep(n) for n in ("qT", "kT", "v", "qiT", "kiT", "wi")]
    outs_holder.extend(outs)
    cos16, sin16 = emit_rope_tables(x, ls, pos, invf16, 16, NT)
    cos8, sin8 = emit_rope_tables(x, ls, pos, invf8, 8, NT)
    gqb = load_bcast(x, ls, gq, 128, "gqb")
    gkb = load_bcast(x, ls, gk, 128, "gkb")
    gib = load_bcast(x, ls, gi, 64, "gib")
    qk_tiles = alloc_qk_tiles(x, ls)
    qbf = sbt(x, [128, 1024], BF16, "qbf", ls)
    stage = [sbt(x, [128, 4, TOK], BF16, "stage%d" % i, ls) for i in range(2)]
    kist = sbt(x, [64, TOK], BF16, "kist", ls)
    ptr = [pst(x, [128, 512], BF16, "ptr%d" % i, ls) for i in range(2)]
    vst = [sbt(x, [128, 512], BF16, "vst%d" % i, ls) for i in range(2)]
    wst = [sbt(x, [128, 16], F32, "wst%d" % i, ls) for i in range(2)]
    cnt = [0]
    nst = [0]
    blocks = []

    def mk_qk(col0, gdim, gain, half, cos, sin, dst_ap, dep, dmh=None):
        sg = stage[nst[0] % 2]
        nst[0] += 1

        def handler(t, ps):
            emit_qk_post2(x, c, qk_tiles, ps, gdim, gain, half, cos, sin, t, qbf)
            for a_ in range(2):
                p = ptr[cnt[0] % 2]
                cnt[0] += 1
                for j in range(4):
                    x.op("pe", lambda e: e.transpose(p[:, j * 128:(j + 1) * 128],
                                                     qbf[:, a_ * 512 + j * 128:a_ * 512 + (j + 1) * 128], c.ident[:]),
                         r=[qbf.d, c.ident.d], w=[p.d], inc=(j == 3))
                x.op("dve", lambda e: e.tensor_copy(sg[:, :, (t + a_) * 128:(t + a_ + 1) * 128],
                                                    p[:, :].rearrange("p (a b) -> p a b", a=4)), r=[p.d], mw=[sg.d])
            if t == NT - 2:
                if dmh is None:
                    x.dma("sp", dst_ap.rearrange("h p t -> p h t"), sg[:], r=[sg.d], mw=[dep])
                else:
                    tens, h0 = dmh
                    sgv = sg[:].rearrange("p h (k a t) -> p h k a t", a=2, t=128)
                    for a2 in range(2):
                        for hh_ in range(4):
                            x.dma("sp", tens[a2, h0 + hh_].rearrange("d k t -> d k t"), sgv[:, hh_, :, a2, :],
                                  r=[sg.d], mw=[dep])
        blocks.append((col0, 512, "tok2", handler))
    for hb in range(4):
        mk_qk(hb * 512, 128, gqb, 16, cos16, sin16, None if dm else qT[hb * 4:(hb + 1) * 4], outs[0],
              (qT, hb * 4) if dm else None)
    mk_qk(2048, 128, gkb, 16, cos16, sin16, kT[0:4], outs[1])

    def vh(t, ps):
        vs = vst[cnt[0] % 2]
        cnt[0] += 1
        x.op("act", lambda e: e.activation(out=vs[:], in_=ps[:, :], func=AF.Copy), r=[ps.d], w=[vs.d])
        x.dma("sp", v[t * 128:(t + 1) * 128, :], vs[:], r=[vs.d], mw=[outs[2]])
    blocks.append((2560, 512, "tok", vh))
    for qb in range(2):
        mk_qk(3072 + qb * 512, 64, None, 8, cos8, sin8, None if dm else qiT[qb * 4:(qb + 1) * 4], outs[3],
              (qiT, qb * 4) if dm else None)

    def kwh(t, ps):
        emit_qk_post(x, c, qk_tiles, ps, 64, 64, gib, 8, cos8, sin8, t, qbf)
        p = ptr[cnt[0] % 2]
        ws = wst[cnt[0] % 2]
        cnt[0] += 1
        x.op("pe", lambda e: e.transpose(p[0:64, 0:128], qbf[:, 0:64], c.ident[:]), r=[qbf.d, c.ident.d], w=[p.d])
        x.op("dve", lambda e: e.tensor_copy(kist[:, t * 128:(t + 1) * 128], p[0:64, 0:128]), r=[p.d], mw=[kist.d])
        x.op("act", lambda e: e.activation(out=ws[:], in_=ps[:, 64:80], func=AF.Copy, scale=0.25), r=[ps.d], w=[ws.d])
        x.dma("sp", wi[t % 2, t // 2] if dm else wi[t * 128:(t + 1) * 128, :], ws[:], r=[ws.d], mw=[outs[5]])
        if t == NT - 1:
            x.dma("sp", kiT, kist[:], r=[kist.d], mw=[outs[4]])
    blocks.append((4096, 80, "tok", kwh))
    return blocks


def emit_LB_DSA(x, c, dram):
    NS = 16
    qTs = dram("qTs", [NS, 128, 2048], BF16, "ExternalInput")
    qiTs = dram("qiTs", [NS, 128, 1024], BF16, "ExternalInput")
    wis = dram("wis", [NS, 128, 16], F32, "ExternalInput")
    kT = dram("kT", [4, 128, S], BF16, "ExternalInput")
    v = dram("v", [S, 512], BF16, "ExternalInput")
    kiT2 = dram("kiT2", [128, S], BF16, "ExternalInput")
    dmask = dram("dmask", [2, 128, 128], F32, "ExternalInput")
    gqk = dram("gqk", [2, 128], F32, "ExternalInput")
    o = dram("o", [NS * 128, 2048], BF16, "ExternalOutput")
    SCALE = 128 ** -0.5
    with Scope(x) as st:
        d_in = x.mkdep("in")
        d_o = x.mkdep("o")
        ls = st
        gt = [load_bcast(x, ls, gqk[i], 128, "gqk%d" % i) for i in range(2)]
        gm = sbt(x, [128, 2], F32, "gm")
        negC = sbt(x, [128, 1], F32, "negC")
        for i in range(2):
            x.op("dve", lambda e: e.tensor_reduce(out=gm[:, i:i + 1], in_=gt[i][:], axis=AX.X, op=ALU.max,
                                                  apply_absolute_value=True), r=[gt[i].d], w=[gm.d])
        x.op("dve", lambda e: e.scalar_tensor_tensor(out=negC[:], in0=gm[:, 0:1], scalar=-(128 ** 0.5), in1=gm[:, 1:2],
                                                     op0=ALU.mult, op1=ALU.mult), r=[gm.d], w=[negC.d])
        kts = sbt(x, [128, 4, S], BF16, "kts")
        x.dma("sp", kts[:], kT.rearrange("g p t -> p g t"), w=[kts.d])
        va = sbt(x, [128, 32, 4, 129], BF16, "va")
        x.op("pool", lambda e: e.memset(va[:, :, :, 128:129], 1.0), w=[va.d])
        for g in range(4):
            x.dma("sp", va[:, :, g, 0:128], v[:, g * 128:(g + 1) * 128].rearrange("(kb p) d -> p kb d", p=128),
                  mw=[va.d])
        ki2 = sbt(x, [128, S], BF16, "ki2")
        x.dma("sp", ki2[:], kiT2, w=[ki2.d])
        dm = sbt(x, [128, 2, 128], F32, "dm")
        x.dma("sp", dm[:], dmask.rearrange("a p k -> p a k"), w=[dm.d])
        zr = sbt(x, [128, 512], BF16, "zr")
        x.op("pool", lambda e: e.memset(zr[:], 0.0), w=[zr.d])
        acc = sbt(x, [128, S], F32, "acc")
        work = sbt(x, [128, S], F32, "work")
        nb = sbt(x, [128, S], BF16, "nb")
        nbT4 = sbt(x, [128, 32, 4, 128], BF16, "nbT4")
        qs = [sbt(x, [128, 2048], BF16, "qs%d" % i) for i in range(2)]
        qis = [sbt(x, [128, 8, 128], BF16, "qis%d" % i) for i in range(2)]
        wt = [sbt(x, [128, 16], F32, "wt%d" % i) for i in range(2)]
        aw = sbt(x, [128, 16], F32, "aw")
        sgn = sbt(x, [128, 16], F32, "sgn")
        rr = [sbt(x, [128, 512], F32, "rr%d" % i) for i in range(2)]
        m8 = sbt(x, [128, 8], F32, "m8")
        thr = sbt(x, [128, 1], F32, "thr")
        thr0 = sbt(x, [128, 1], F32, "thr0")
        x.op("pool", lambda e: e.memset(thr0[:], -1e29), w=[thr0.d])
        P = [sbt(x, [128, 512], BF16, "P%d" % i) for i in range(2)]
        Rs = [sbt(x, [128, 4], F32, "R%d" % i) for i in range(4)]
        osbs = [sbt(x, [128, 4, 129], F32, "osbd%d" % i) for i in range(4)]
        ob = [sbt(x, [128, 2048], BF16, "ob%d" % i) for i in range(2)]
        p_ix = [pst(x, [128, 512], F32, "p_ix%d" % i) for i in range(2)]
        p_tr = [pst(x, [128, 512], BF16, "p_tr%d" % i) for i in range(2)]
        p_s = [pst(x, [128, 512], F32, "p_s%d" % i) for i in range(2)]
        p_o = pst(x, [128, 4, 256], F32, "p_o")
        p_ob = p_o[:].rearrange("p a b -> p (a b)")
        nix = 0
        ns_ = 0
        def part_A(i):
            nonlocal nix
            b2 = i % 2
            nkb = 2 * i + 2
            L = nkb * 128
            x.dma("sp", qs[b2][:], qTs[i], r=[d_in], w=[qs[b2].d])
            x.dma("sp", qis[b2][:], qiTs[i].rearrange("p (a t) -> p a t", a=8), r=[d_in], w=[qis[b2].d])
            x.dma("sp", wt[b2][:], wis[i], r=[d_in], w=[wt[b2].d])
            w_ = wt[b2]
            x.op("dve", lambda e: e.scalar_tensor_tensor(out=aw[:], in0=w_[:], scalar=-1.0, in1=w_[:],
                                                         op0=ALU.mult, op1=ALU.max), r=[w_.d], w=[aw.d])
            x.op("dve", lambda e: e.tensor_scalar(out=aw[:], in0=aw[:], scalar1=0.125, scalar2=None, op0=ALU.mult),
                 r=[aw.d], w=[aw.d])
            x.op("dve", lambda e: e.tensor_scalar(out=sgn[:], in0=w_[:], scalar1=0.0, scalar2=2.0,
                                                  op0=ALU.is_ge, op1=ALU.mult), r=[w_.d], w=[sgn.d])
            x.op("dve", lambda e: e.tensor_scalar(out=sgn[:], in0=sgn[:], scalar1=-1.0, scalar2=None, op0=ALU.add),
                 r=[sgn.d], w=[sgn.d])
            for kq in range((L + 511) // 512):
                W = min(512, L - kq * 512)
                cs = slice(kq * 512, kq * 512 + W)
                for hi in range(16):
                    ps = p_ix[nix % 2]
                    r_ = rr[nix % 2]
                    nix += 1
                    pr_ = slice((hi % 2) * 64, (hi % 2) * 64 + 64)
                    x.op("pe", lambda e: e.matmul(ps[:, 0:W], qis[b2][pr_, hi // 2, :], ki2[pr_, cs], start=True, stop=True),
                         r=[qis[b2].d, ki2.d], w=[ps.d])
                    x.op("act", lambda e: e.activation(out=r_[:, 0:W], in_=ps[:, 0:W], func=AF.Relu, scale=aw[:, hi:hi + 1]),
                         r=[ps.d, aw.d], w=[r_.d])
                    if hi == 0:
                        x.op("dve", lambda e: e.tensor_scalar(out=acc[:, cs], in0=r_[:, 0:W], scalar1=sgn[:, 0:1],
                                                              scalar2=None, op0=ALU.mult), r=[r_.d, sgn.d], w=[acc.d])
                    else:
                        x.op("dve", lambda e: e.scalar_tensor_tensor(out=acc[:, cs], in0=r_[:, 0:W], scalar=sgn[:, hi:hi + 1],
                                                                     in1=acc[:, cs], op0=ALU.mult, op1=ALU.add),
                             r=[r_.d, sgn.d, acc.d], w=[acc.d])
            for a in range(2):
                ks = slice((nkb - 2 + a) * 128, (nkb - 1 + a) * 128)
                x.op("dve", lambda e: e.tensor_tensor(out=acc[:, ks], in0=acc[:, ks], in1=dm[:, a, :], op=ALU.add),
                     r=[acc.d, dm.d], w=[acc.d])
            if i >= 1:
                x.op("pool", lambda e: e.tensor_copy(work[:, 0:L], acc[:, 0:L]), r=[acc.d], w=[work.d])
                for rd in range(32):
                    x.op("dve", lambda e: e.max(out=m8[:], in_=work[:, 0:L]), r=[work.d], w=[m8.d])
                    if rd < 31:
                        x.op("dve", lambda e: e.match_replace(out=work[:, 0:L], in_to_replace=m8[:], in_values=work[:, 0:L],
                                                              imm_value=-1e30), r=[m8.d, work.d], w=[work.d])
                x.op("dve", lambda e: e.tensor_copy(thr[:], m8[:, 7:8]), r=[m8.d], w=[thr.d])
                th = thr
            else:
                th = thr0
            x.op("dve", lambda e: e.tensor_scalar(out=nb[:, 0:L], in0=acc[:, 0:L], scalar1=th[:, 0:1], scalar2=NEG,
                                                  op0=ALU.is_lt, op1=ALU.mult), r=[acc.d, th.d], w=[nb.d])

        def part_T(i):
            b2 = i % 2
            nkb = 2 * i + 2
            L = nkb * 128
            for kg in range((nkb + 3) // 4):
                p = p_tr[kg % 2]
                nn = min(4, nkb - kg * 4)
                for j in range(nn):
                    kb = kg * 4 + j
                    x.op("pe", lambda e: e.transpose(p[:, j * 128:(j + 1) * 128], nb[:, kb * 128:(kb + 1) * 128], c.ident[:]),
                         r=[nb.d, c.ident.d], w=[p.d], inc=(j == nn - 1))
                src = p[:, 0:nn * 128].rearrange("p (a b) -> p a b", a=nn)
                x.op("act", lambda e: e.activation(out=nbT4[:, kg * 4:kg * 4 + nn, :, :],
                                                   in_=bc(src.unsqueeze(2), [128, nn, 4, 128]), func=AF.Copy),
                     r=[p.d], w=[nbT4.d])

        def part_C(i):
            b2 = i % 2
            nkb = 2 * i + 2
            L = nkb * 128
            obb = ob[b2]
            asteps = [(g, kb) for g in range(4) for kb in range(nkb)]

            def a_scores(idx):
                g, kb = asteps[idx]
                ps = p_s[idx % 2]
                pp_ = P[idx % 2]
                x.op("pe", lambda e: e.matmul(ps[:], kts[:, g, kb * 128:(kb + 1) * 128],
                                              qs[b2][:, g * 512:(g + 1) * 512], start=True, stop=False),
                     r=[kts.d, qs[b2].d], w=[ps.d], inc=False)
                x.op("pe", lambda e: e.matmul(ps[:], c.ident[:], nbT4[:, kb, :, :].rearrange("p a b -> p (a b)"),
                                              start=False, stop=True), r=[c.ident.d, nbT4.d], w=[ps.d])
                x.op("act", lambda e: e.activation(out=pp_[:], in_=ps[:], func=AF.Exp, scale=SCALE, bias=negC[:, 0:1]),
                     r=[ps.d, negC.d], w=[pp_.d])

            def a_pv(idx):
                g, kb = asteps[idx]
                pp_ = P[idx % 2]
                for r in range(4):
                    x.op("pe", lambda e: e.matmul(p_o[:, r, 0:129], pp_[:, r * 128:(r + 1) * 128], va[:, kb, g, :],
                                                  start=False, stop=(kb == nkb - 1 and r % 2 == 1)),
                         r=[pp_.d, va.d], w=[p_o.d], inc=(r == 3))

            def a_epi(g):
                osb = osbs[g]
                x.op("act", lambda e: e.activation(out=osb[:], in_=p_o[:, :, 0:129], func=AF.Copy), r=[p_o.d], w=[osb.d])
                Rg = Rs[g]
                x.op("dve", lambda e: e.reciprocal(out=Rg[:], in_=osb[:, :, 128:129].rearrange("p a b -> p (a b)")),
                     r=[osb.d], w=[Rg.d])
                for r in range(4):
                    hh = g * 4 + r
                    x.op("pool", lambda e: e.tensor_scalar(out=obb[:, hh * 128:(hh + 1) * 128], in0=osb[:, r, 0:128],
                                                           scalar1=Rg[:, r:r + 1], scalar2=None, op0=ALU.mult),
                         r=[osb.d, Rg.d], mw=[obb.d])

            a_scores(0)
            for idx, (g, kb) in enumerate(asteps):
                if kb == 0:
                    for bnk in range(2):
                        x.op("pe", lambda e: e.matmul(p_ob[:, bnk * 512:(bnk + 1) * 512], zr[:, 0:128], zr[:],
                                                      start=True, stop=False), r=[zr.d], w=[p_o.d], inc=False)
                if idx + 1 < len(asteps):
                    a_scores(idx + 1)
                a_pv(idx)
                if kb == nkb - 1:
                    a_epi(g)
            x.dma("sp", o[i * 128:(i + 1) * 128, :], obb[:], r=[obb.d], mw=[d_o])

        part_A(0)
        part_T(0)
        for i in range(NS):
            if i + 1 < NS:
                part_A(i + 1)
            part_C(i)
            if i + 1 < NS:
                part_T(i + 1)


def _standalone(emit, *args):
    nc = bass.Bass("TRN2", target_bir_lowering=False)
    dram = lambda n, s, dt, k: nc.dram_tensor(n, list(s), dt, kind=k).ap()
    with ExitStack() as st:
        x = X(nc, st)
        c = make_consts(x)
        emit(x, c, dram, *args)
        x.global_barrier()
        print(emit.__name__, args, "sems", x.nsem, "cnt", x.cnt)
    return nc


def build_LA(kind):
    return _standalone(emit_LA, kind)


def build_LB_DA(lambda_init):
    return _standalone(emit_LB_DA, lambda_init)


def build_LB_SSD():
    return _standalone(emit_LB_SSD)


def build_LB_DSA():
    return _standalone(emit_LB_DSA)


def build_LC(FO):
    return _standalone(emit_LC, FO)


def _invf_table(half):
    invf = np.power(np.float32(ROPE_THETA), -np.arange(half, dtype=np.float32) / half).astype(np.float32)
    return np.ascontiguousarray(np.broadcast_to(invf[None, :], (128, half))).astype(np.float32)


def _ca(a):
    return np.ascontiguousarray(a)


GROUPS = [[0, 1], [2, 3], [4, 5], [6, 7]]
FIN_K = {0: 6144, 1: 10304, 2: 4176}


def _mk_dram(mapping):
    def dram(n, s, dt, k):
        ap = mapping[n]
        assert [int(v) for v in ap.shape] == [int(v) for v in s], (n, ap.shape, s)
        return ap
    return dram


def build_fused(depth=DEPTH):
    nc = bass.Bass("TRN2", target_bir_lowering=False)
    ext_in = lambda n, s, dt: nc.dram_tensor(n, list(s), dt, kind="ExternalInput").ap()
    internal = lambda n, s, dt: nc.dram_tensor(n, list(s), dt, kind="Internal").ap()
    x_in = ext_in("x_in", [TOK, D], F32)
    c_in = ext_in("c_in", [D], F32)
    pos = ext_in("pos", [TOK], I32)
    rk = ext_in("rk", [1, 1], I32)
    invf8 = ext_in("invf8", [128, 8], F32)
    invf16 = ext_in("invf16", [128, 16], F32)
    dmask = ext_in("dmask", [2, 128, 128], F32)
    x_out = nc.dram_tensor("x_out", [TOK, D], F32, kind="ExternalOutput").ap()
    xres = internal("xres", [TOK, D], F32)
    xmid = internal("xmid", [TOK, D], F32)
    scratch = {}

    def scr(n, s, dt):
        if n not in scratch:
            scratch[n] = internal(n, s, dt)
        return scratch[n]

    with ExitStack() as st:
        x = X(nc, st)
        c = make_consts(x)
        reg = st.enter_context(nc.gpsimd.register("rk"))
        nc.gpsimd.reg_load(reg, rk[0:1, 0:1])
        r = nc.gpsimd.snap(reg, min_val=0, max_val=1)
        d_g = x.mkdep("xchg")
        RS = bass.ds(r, 1)
        ADA_SPLIT = (RS, internal("adaH", [3 * D], F32), internal("adaG", [2, 3 * D], F32))
        CH = 2 * 1024 * 1024
        MAXE = {BF16: 12 * 1024 * 1024 + 4096, F32: 1024 * 1024}
        GBS = {BF16: [], F32: []}
        goff = {BF16: 0, F32: 0}

        def gather(parts, both=False, stage=False):
            a0 = parts[0]
            dt = a0.dtype
            shp = [int(v) for v in a0.shape]
            rowe = int(np.prod(shp[1:]))
            c0 = max(1, min(shp[0], CH // (rowe * mybir.dt.size(dt))))
            while shp[0] % c0:
                c0 -= 1
            nch = shp[0] // c0
            ce = c0 * rowe
            assert nch * 2 * ce <= MAXE[dt], (nch, ce, dt)
            if goff[dt] >= len(GBS[dt]):
                GBS[dt].append(internal("GB%d_%d" % (mybir.dt.size(dt), len(GBS[dt])), [2, MAXE[dt]], dt))
            gb = GBS[dt][goff[dt]]
            goff[dt] += 1
            off = 0
            for d_, a in enumerate(parts):
                for k in range(nch):
                    x.op("pool", lambda e: e.collective_compute(
                        "AllGather", ALU.bypass, replica_groups=GROUPS, ins=[a[k * c0:(k + 1) * c0].opt()],
                        outs=[gb[d_, off + k * 2 * ce:off + (k + 1) * 2 * ce].opt()]), w=[d_g])
            x.global_barrier()
            tot = nch * 2 * ce
            if stage:
                gs = scr("GS%d_%d" % (mybir.dt.size(dt), goff[dt] - 1), [1, MAXE[dt]], dt)
                CP = 4 * 1024 * 1024
                for o_ in range(0, tot, CP):
                    n_ = min(CP, tot - o_)
                    x.dma("pool", gs[:, o_:o_ + n_], (gb[0:1] if both else gb[RS])[:, o_:o_ + n_], mw=[d_g])
                row = gs[:, 0:tot]
            else:
                row = (gb[0:1] if both else gb[RS])[:, 0:tot]
            names = ["e%d" % i_ for i_ in range(len(shp) - 1)]
            kw = {"k": nch, "s": 2, "c": c0}
            kw.update({n_: v_ for n_, v_ in zip(names, shp[1:])})
            return row.rearrange("a (k s c %s) -> (a k) s c %s" % (" ".join(names), " ".join(names)), **kw)

        def cp(dst, src):
            x.dma("pool", dst, src, mw=[d_g])

        for i in range(depth):
            kind, j = i % 3, i // 3
            xsrc = x_in if i == 0 else xres
            xdst = x_out if i == depth - 1 else xres
            sfx = "_%d" % i
            ada_i = internal("ada" + sfx, [6 * D], F32)
            mp = {"x_in": xsrc, "c_in": c_in, "pos": pos, "ada_w": ext_in("ada_w" + sfx, [D, 3 * D], F32),
                  "ada_b": ext_in("ada_b" + sfx, [6 * D], F32), "g1": ext_in("g1" + sfx, [D], F32), "ada": ada_i,
                  "w_in": ext_in("w_in" + sfx, [D, FIN_K[kind]], F32)}
            goff[BF16] = goff[F32] = 0
            if kind == 0:
                A = {"qT": scr("A_qT", [16, 128, TOK], BF16), "kT": scr("A_kT", [16, 128, TOK], BF16),
                     "v": scr("A_v", [2, TOK, 1024], BF16)}
                mp.update(A)
                mp.update({"invf": invf8, "gq": ext_in("gq" + sfx, [64], F32), "gk": ext_in("gk" + sfx, [64], F32)})
                emit_LA(x, c, _mk_dram(mp), kind, True, ADA_SPLIT)
                x.global_barrier()
                Gq = gather([A["qT"][0:8], A["qT"][8:16]])
                Gk = gather([A["kT"][0:8], A["kT"][8:16]])
                Gv = gather([A["v"][0], A["v"][1]])
                x.global_barrier()
                L_qT = scr("L_qT", [8, 128, S], BF16)
                L_kT = scr("L_kT", [8, 128, S], BF16)
                L_v = scr("L_v", [S, 1024], BF16)
                for s_ in range(2):
                    cs = slice(s_ * TOK, (s_ + 1) * TOK)
                    for kk in range(2):
                        cp(L_qT[kk * 4:(kk + 1) * 4, :, cs], Gq[kk, s_])
                        cp(L_kT[kk * 4:(kk + 1) * 4, :, cs], Gk[kk, s_])
                        cp(L_v[s_ * TOK + kk * 1024:s_ * TOK + (kk + 1) * 1024, :], Gv[kk, s_])
                x.global_barrier()
                B_o = scr("B_o", [S, 1024], BF16)
                li = 0.8 - 0.6 * math.exp(-0.3 * i)
                emit_LB_DA(x, c, _mk_dram({"qT": L_qT, "kT": L_kT, "v": L_v, "lam4": ext_in("lam4" + sfx, [4, 64], F32),
                                           "gqk": ext_in("gqk" + sfx, [2, 64], F32),
                                           "subg": ext_in("subg" + sfx, [128], F32), "o": B_o}), float(li))
                x.global_barrier()
                goff[BF16] = 0
                Go = gather([B_o[0:TOK], B_o[TOK:S]])
                x.global_barrier()
                FO = 2048
                L_o = scr("L_o", [TOK, 2048], BF16)
                for hh in range(2):
                    for kk in range(2):
                        cp(L_o[kk * 1024:(kk + 1) * 1024, hh * 1024:(hh + 1) * 1024], Go[kk, hh])
            elif kind == 1:
                A = {"z": scr("A_z", [2, TOK, 2048], BF16), "xbcT": scr("A_xbcT", [2, 3072, TOK], BF16),
                     "dtr": scr("A_dtr", [2, TOK, 32], F32)}
                mp.update(A)
                emit_LA(x, c, _mk_dram(mp), kind, True, ADA_SPLIT)
                x.global_barrier()
                Gz = gather([A["z"][0], A["z"][1]])
                Gx = gather([A["xbcT"][0], A["xbcT"][1]])
                Gd = gather([A["dtr"][0], A["dtr"][1]])
                x.global_barrier()
                L_raw = scr("L_raw", [3072, S], F32)
                L_z = scr("L_z", [S, 2048], BF16)
                L_dt = scr("L_dt", [S, 32], F32)
                for s_ in range(2):
                    cs = slice(s_ * TOK, (s_ + 1) * TOK)
                    for kk in range(6):
                        cp(L_raw[kk * 512:(kk + 1) * 512, cs], Gx[kk, s_])
                    for kk in range(4):
                        cp(L_z[s_ * TOK + kk * 512:s_ * TOK + (kk + 1) * 512, :], Gz[kk, s_])
                    cp(L_dt[cs, :], Gd[0, s_])
                x.global_barrier()
                B_y = scr("B_y", [S, 2048], BF16)
                emit_LB_SSD(x, c, _mk_dram({"raw": L_raw, "convw": ext_in("convw" + sfx, [4, 3072], F32),
                                            "convb": ext_in("convb" + sfx, [3072], F32), "dtr": L_dt,
                                            "hp": ext_in("hp" + sfx, [3, 32], F32), "z": L_z,
                                            "ng": ext_in("ng" + sfx, [2048], F32),
                                            "tokd": scr("tokd", [S, 2560], BF16), "featd": scr("featd", [1024, S], BF16),
                                            "y": B_y}))
                x.global_barrier()
                goff[BF16] = 0
                Gy = gather([B_y[0:TOK], B_y[TOK:S]])
                x.global_barrier()
                FO = 4096
                L_o = scr("L_o4", [TOK, 4096], BF16)
                for gh in range(2):
                    for kk in range(4):
                        cp(L_o[kk * 512:(kk + 1) * 512, gh * 2048:(gh + 1) * 2048], Gy[kk, gh])
            else:
                A = {"qT": scr("D_qT", [2, 16, 128, 8, 128], BF16), "kT": scr("D_kT", [4, 128, TOK], BF16),
                     "v": scr("D_v", [TOK, 512], BF16), "qiT": scr("D_qiT", [2, 8, 128, 8, 128], BF16),
                     "kiT": scr("D_kiT", [64, TOK], BF16), "wi": scr("D_wi", [2, 8, 128, 16], F32)}
                mp.update(A)
                mp.update({"invf16": invf16, "invf8": invf8, "gq": ext_in("gq" + sfx, [128], F32),
                           "gk": ext_in("gk" + sfx, [128], F32), "gi": ext_in("gi" + sfx, [64], F32)})
                emit_LA(x, c, _mk_dram(mp), kind, True, ADA_SPLIT)
                x.global_barrier()
                Gq = gather([A["qT"][0], A["qT"][1]], stage=True)
                Gqi = gather([A["qiT"][0], A["qiT"][1]], stage=True)
                Gw = gather([A["wi"][0], A["wi"][1]], stage=True)
                Gk = gather([A["kT"]], both=True)
                Gv = gather([A["v"]], both=True)
                Gki = gather([A["kiT"]], both=True)
                x.global_barrier()
                L_qTs = scr("L_qTs", [16, 128, 2048], BF16)
                L_qiTs = scr("L_qiTs", [16, 128, 1024], BF16)
                L_wis = scr("L_wis", [16, 128, 16], F32)
                L_kT = scr("L_kT4", [4, 128, S], BF16)
                L_v = scr("L_v4", [S, 512], BF16)
                L_ki = scr("L_ki2", [128, S], BF16)
                with nc.allow_non_contiguous_dma(reason="tile gathers"):
                    for sl_ in range(16):
                        s_, k_ = sl_ // 8, sl_ % 8
                        for kk in range(2):
                            cp(L_qTs[sl_][:, kk * 1024:(kk + 1) * 1024].rearrange("d (h t) -> d h t", h=8),
                               Gq[kk, s_][:, :, k_, :].rearrange("h d t -> d h t"))
                        cp(L_qiTs[sl_].rearrange("d (h t) -> d h t", h=8),
                           Gqi[0, s_][:, :, k_, :].rearrange("h d t -> d h t"))
                        cp(L_wis[sl_], Gw[0, s_][k_])
                for s_ in range(2):
                    cs = slice(s_ * TOK, (s_ + 1) * TOK)
                    cp(L_kT[:, :, cs], Gk[0, s_])
                    cp(L_v[cs, :], Gv[0, s_])
                    for dup in range(2):
                        cp(L_ki[dup * 64:(dup + 1) * 64, cs], Gki[0, s_])
                x.global_barrier()
                B_o2 = scr("B_o2", [TOK, 2048], BF16)
                emit_LB_DSA(x, c, _mk_dram({"qTs": L_qTs, "qiTs": L_qiTs, "wis": L_wis, "kT": L_kT, "v": L_v,
                                            "kiT2": L_ki, "dmask": dmask,
                                            "gqk": ext_in("gqk" + sfx, [2, 128], F32), "o": B_o2}))
                x.global_barrier()
                goff[BF16] = 0
                Go2 = gather([B_o2[0:1024], B_o2[1024:2048]], stage=True)
                x.global_barrier()
                FO = 2048
                L_o = scr("L_o", [TOK, 2048], BF16)
                for tl in range(16):
                    jj = tl // 2
                    cp(L_o[tl * 128:(tl + 1) * 128, :], Go2[jj // 4, tl % 2][(jj % 4) * 128:(jj % 4 + 1) * 128, :])
            x.global_barrier()
            emit_LC(x, c, _mk_dram({"x_in": xsrc, "o_in": L_o, "ada": ada_i, "g2": ext_in("g2" + sfx, [D], F32),
                                    "w_out": ext_in("w_out" + sfx, [FO, D], F32),
                                    "wgu": ext_in("wgu" + sfx, [D, 2 * FFN], F32),
                                    "wd": ext_in("wd" + sfx, [FFN, D], F32), "x_mid": xmid, "x_out": xdst,
                                    "act_d": scr("act_d", [NT, 128, FFN // 128, 128], BF16)}), FO)
            x.global_barrier()
        print("fused sems", x.nsem, "cnt", x.cnt)
    return nc


def fused_inputs(inp, depth=DEPTH):
    x = np.asarray(inp["x"], dtype=np.float32)
    c = np.asarray(inp["c"], dtype=np.float32)
    pos = np.asarray(inp["positions"]).astype(np.int32)
    tri = np.where(np.arange(128)[None, :] <= np.arange(128)[:, None], 0.0, -1e30).astype(np.float32)
    full_neg = np.full((128, 128), -1e30, np.float32)
    zero_m = np.zeros((128, 128), np.float32)
    f32 = lambda a: _ca(np.asarray(a, dtype=np.float32))
    maps = []
    for cc in range(8):
        b, h = cc // 2, cc % 2
        sl = slice(h * TOK, (h + 1) * TOK)
        m = {"x_in": _ca(x[b, sl]), "c_in": _ca(c[b]), "pos": _ca(pos[b, sl]), "rk": np.array([[h]], np.int32),
             "invf8": _invf_table(8), "invf16": _invf_table(16),
             "dmask": np.stack([tri, full_neg]) if h == 0 else np.stack([zero_m, tri])}
        for i in range(depth):
            kind, j = i % 3, i // 3
            sfx = "_%d" % i
            m["ada_w" + sfx] = _ca(inp["ada_w"][i][:, h * 3 * D:(h + 1) * 3 * D])
            m["ada_b" + sfx] = inp["ada_b"][i]
            m["g1" + sfx] = inp["norm1_g"][i]
            m["g2" + sfx] = inp["norm2_g"][i]
            m["wgu" + sfx] = inp["ffn_w_gate_up"][i]
            m["wd" + sfx] = inp["ffn_w_down"][i]
            if kind == 0:
                m["w_in" + sfx] = inp["da_w_in"][j]
                m["w_out" + sfx] = inp["da_w_out"][j]
                m["gq" + sfx] = inp["da_q_norm_g"][j]
                m["gk" + sfx] = inp["da_k_norm_g"][j]
                m["lam4" + sfx] = f32(np.stack([inp["da_lambda_q1"][j], inp["da_lambda_k1"][j],
                                                inp["da_lambda_q2"][j], inp["da_lambda_k2"][j]]))
                m["gqk" + sfx] = f32(np.stack([inp["da_q_norm_g"][j], inp["da_k_norm_g"][j]]))
                m["subg" + sfx] = inp["da_subln_g"][j]
            elif kind == 1:
                m["w_in" + sfx] = inp["ssd_w_in"][j]
                m["w_out" + sfx] = inp["ssd_w_out"][j]
                ch = np.concatenate([np.arange(h * 2048, (h + 1) * 2048), np.arange(4096 + h * 512, 4096 + (h + 1) * 512),
                                     np.arange(5120 + h * 512, 5120 + (h + 1) * 512)])
                hs = slice(h * 32, (h + 1) * 32)
                m["convw" + sfx] = _ca(inp["ssd_conv_w"][j][:, ch])
                m["convb" + sfx] = _ca(inp["ssd_conv_b"][j][ch])
                m["hp" + sfx] = f32(np.stack([inp["ssd_dt_bias"][j][hs], inp["ssd_a_log"][j][hs], inp["ssd_d_skip"][j][hs]]))
                m["ng" + sfx] = _ca(inp["ssd_norm_g"][j][h * 2048:(h + 1) * 2048])
            else:
                m["w_in" + sfx] = inp["sa_w_in"][j]
                m["w_out" + sfx] = inp["sa_w_out"][j]
                m["gq" + sfx] = inp["sa_q_norm_g"][j]
                m["gk" + sfx] = inp["sa_k_norm_g"][j]
                m["gi" + sfx] = inp["sa_idx_k_norm_g"][j]
                m["gqk" + sfx] = f32(np.stack([inp["sa_q_norm_g"][j], inp["sa_k_norm_g"][j]]))
        maps.append(m)
    return maps


def kernel(**inp):
    depth = DEPTH
    nc = build_fused(depth)
    maps = fused_inputs(inp, depth)
    res = run_bass_kernel_spmd(nc, maps, core_ids=list(range(8))).results
    out = np.empty((NB, S, D), np.float32)
    for cc in range(8):
        b, h = cc // 2, cc % 2
        out[b, h * TOK:(h + 1) * TOK] = res[cc]["x_out"]
    return out
```
